# Optimizing a Trainium2 kernel written in Bass

```python
import jax, jax.numpy as jnp
from jax import lax
import numpy as np

D_MODEL = 1024
BATCH = 2
SEQ = 8192
DEPTH = 2
DEC_BATCH = 128
DEC_SEQ = 4
PAST_LEN = 16384
PAGE_SIZE = 128

HEAD_DIM = 64
RWKV_HEADS = 8
RWKV_DIM = RWKV_HEADS * HEAD_DIM
D_DECAY_LORA = 64
D_AAA_LORA = 64
D_GATE_LORA = 128
RWKV_PROJ = 3 * RWKV_DIM + D_DECAY_LORA + D_AAA_LORA + D_GATE_LORA
ATT_Q_HEADS = 8
ATT_KV_HEADS = 2
ATT_GROUP = ATT_Q_HEADS // ATT_KV_HEADS
ATT_DIM = ATT_Q_HEADS * HEAD_DIM
ATT_KV_DIM = ATT_KV_HEADS * HEAD_DIM
WINDOW = 128
BLOCK = 128
IN_PROJ = RWKV_PROJ + ATT_DIM + 2 * ATT_KV_DIM + 2 * D_MODEL
D_FF = 2816
CONV_W = 3
ROPE_THETA = 10000.0
RMS_EPS = 1e-6
GN_EPS = 64e-5

kernel_name = 'rwkv7_swa_sink_gated_hybrid_step'


def rmsnorm(x, g):
    xf = x.astype(jnp.float32)
    y = xf * lax.rsqrt(jnp.mean(xf * xf, axis=-1, keepdims=True) + RMS_EPS)
    return (y * g.astype(jnp.float32)).astype(x.dtype)


def rope(x, pos):
    inv = ROPE_THETA ** (-jnp.arange(0, HEAD_DIM, 2, dtype=jnp.float32) / HEAD_DIM)
    ang = pos.astype(jnp.float32)[:, None] * inv[None, :]
    cos = jnp.cos(ang)[None, :, None, :]
    sin = jnp.sin(ang)[None, :, None, :]
    xf = x.astype(jnp.float32)
    x1, x2 = xf[..., :HEAD_DIM // 2], xf[..., HEAD_DIM // 2:]
    return jnp.concatenate([x1 * cos - x2 * sin, x2 * cos + x1 * sin], axis=-1).astype(x.dtype)


def rwkv_branch(z, shift_prev, wkv0, mu, w0, w2, a0, a2, g2, k_k, k_a, r_k, ln_g, ln_b):
    B, T, _ = z.shape
    f32 = jnp.float32
    z_prev = jnp.concatenate([shift_prev[:, None].astype(z.dtype), z[:, :-1]], axis=1)
    zs = z + mu * (z_prev - z)
    cuts = [RWKV_DIM, 2 * RWKV_DIM, 3 * RWKV_DIM, 3 * RWKV_DIM + D_DECAY_LORA,
            3 * RWKV_DIM + D_DECAY_LORA + D_AAA_LORA]
    r, k, v, zw, za, zg = jnp.split(zs, cuts, axis=-1)
    w_log = -jax.nn.softplus(-(w0 + jnp.tanh(zw) @ w2).astype(f32)) - 0.5
    decay = jnp.exp(-jnp.exp(w_log))
    a = jax.nn.sigmoid((a0 + za @ a2).astype(f32))
    g = jax.nn.sigmoid(zg) @ g2
    hs = lambda t: t.reshape(B, T, RWKV_HEADS, HEAD_DIM)
    kk = hs(k.astype(f32) * k_k)
    kk = kk / jnp.maximum(jnp.linalg.norm(kk, axis=-1, keepdims=True), 1e-12)
    kf = hs(k.astype(f32) * (1.0 + (a - 1.0) * k_a))
    rf, vf, a_h, w_h = hs(r.astype(f32)), hs(v.astype(f32)), hs(a), hs(decay)

    def step(S, inp):
        r_t, w_t, k_t, v_t, kk_t, a_t = inp
        sk = jnp.einsum('bhvk,bhk->bhv', S, kk_t)
        S = (S * w_t[:, :, None, :]
             - sk[..., None] * (kk_t * a_t)[:, :, None, :]
             + v_t[..., None] * k_t[:, :, None, :])
        return S, jnp.einsum('bhvk,bhk->bhv', S, r_t)

    tm = lambda t: jnp.swapaxes(t, 0, 1)
    S_T, y = lax.scan(step, wkv0.astype(f32), (tm(rf), tm(w_h), tm(kf), tm(vf), tm(kk), tm(a_h)))
    y = tm(y)
    y_mu = jnp.mean(y, axis=-1, keepdims=True)
    y_var = jnp.mean(jnp.square(y - y_mu), axis=-1, keepdims=True)
    yn = ((y - y_mu) * lax.rsqrt(y_var + GN_EPS)).reshape(B, T, RWKV_DIM) * ln_g + ln_b
    bonus = (jnp.sum(rf * kf * r_k, axis=-1, keepdims=True) * vf).reshape(B, T, RWKV_DIM)
    out = ((yn + bonus) * g.astype(f32)).astype(z.dtype)
    return out, z[:, -1], S_T.astype(z.dtype)


def sink_attention(q, k, v, mask, sinks):
    s = jnp.einsum('bnqkgd,bnskd->bnkgqs', q, k, preferred_element_type=jnp.float32) * (HEAD_DIM ** -0.5)
    s = jnp.where(mask[None, :, None, None], s, -jnp.inf)
    sink = sinks.astype(jnp.float32).reshape(1, 1, ATT_KV_HEADS, ATT_GROUP, 1, 1)
    m = jnp.maximum(jnp.max(s, axis=-1, keepdims=True), sink)
    e = jnp.exp(s - m)
    p = e / (jnp.sum(e, axis=-1, keepdims=True) + jnp.exp(sink - m))
    return jnp.einsum('bnkgqs,bnskd->bnqkgd', p.astype(v.dtype), v)


def swa_prompt(q, k, v, sinks):
    B, T = q.shape[:2]
    nb = T // BLOCK
    qb = q.reshape(B, nb, BLOCK, ATT_KV_HEADS, ATT_GROUP, HEAD_DIM)

    def band(t):
        tb = t.reshape(B, nb, BLOCK, ATT_KV_HEADS, HEAD_DIM)
        prev = jnp.concatenate([jnp.zeros_like(tb[:, :1]), tb[:, :-1]], axis=1)
        return jnp.concatenate([prev, tb], axis=2)

    i = jnp.arange(BLOCK)[:, None]
    j = jnp.arange(2 * BLOCK)[None, :]
    diff = BLOCK + i - j
    band_ok = (diff >= 0) & (diff < WINDOW)
    kpos = (jnp.arange(nb)[:, None, None] - 1) * BLOCK + j[None]
    mask = band_ok[None] & (kpos >= 0)
    o = sink_attention(qb, band(k), band(v), mask, sinks)
    return o.reshape(B, T, ATT_DIM)


def swa_sample(q, k, v, k_cache, v_cache, sinks):
    B, T = q.shape[:2]
    k_all = jnp.concatenate([k_cache.astype(k.dtype), k], axis=1)
    v_all = jnp.concatenate([v_cache.astype(v.dtype), v], axis=1)
    i = jnp.arange(T)[:, None]
    j = jnp.arange(WINDOW + T)[None, :]
    diff = WINDOW + i - j
    mask = ((diff >= 0) & (diff < WINDOW))[None]
    o = sink_attention(q.reshape(B, 1, T, ATT_KV_HEADS, ATT_GROUP, HEAD_DIM),
                       k_all[:, None], v_all[:, None], mask, sinks)
    return o.reshape(B, T, ATT_DIM), k_all[:, -WINDOW:], v_all[:, -WINDOW:]


def conv_ffn(h, conv_prev, w_in, conv_w, conv_b, w_down):
    T = h.shape[1]
    c, up = jnp.split(h @ w_in, 2, axis=-1)
    c_ext = jnp.concatenate([conv_prev.astype(c.dtype), c], axis=1)
    conv = conv_b
    for j in range(CONV_W):
        conv = conv + c_ext[:, j:j + T] * conv_w[j]
    a = jax.nn.gelu(conv, approximate=False) * up
    return a @ w_down, c_ext[:, -(CONV_W - 1):]


def layer(x, pos, shift_prev, wkv0, k_cache, v_cache, conv_prev, P):
    B, T, _ = x.shape
    h = rmsnorm(x, P['norm_mix_g'])
    zin = h @ P['w_in']
    cuts = [RWKV_PROJ, RWKV_PROJ + ATT_DIM, RWKV_PROJ + ATT_DIM + ATT_KV_DIM,
            RWKV_PROJ + ATT_DIM + 2 * ATT_KV_DIM, RWKV_PROJ + ATT_DIM + 2 * ATT_KV_DIM + D_MODEL]
    z_rwkv, q, k, v, g_r, g_a = jnp.split(zin, cuts, axis=-1)
    o_r, shift_new, wkv_new = rwkv_branch(
        z_rwkv, shift_prev, wkv0, P['rwkv_mu'], P['rwkv_w0'], P['rwkv_w2'], P['rwkv_a0'], P['rwkv_a2'],
        P['rwkv_g2'], P['rwkv_k_k'], P['rwkv_k_a'], P['rwkv_r_k'], P['rwkv_ln_g'], P['rwkv_ln_b'])
    q = rope(q.reshape(B, T, ATT_Q_HEADS, HEAD_DIM), pos)
    k = rope(k.reshape(B, T, ATT_KV_HEADS, HEAD_DIM), pos)
    v = v.reshape(B, T, ATT_KV_HEADS, HEAD_DIM)
    if k_cache is None:
        o_a = swa_prompt(q, k, v, P['attn_sinks'])
        k_new, v_new = k[:, -WINDOW:], v[:, -WINDOW:]
    else:
        o_a, k_new, v_new = swa_sample(q, k, v, k_cache, v_cache, P['attn_sinks'])
    merged = (jax.nn.sigmoid(g_r) * (o_r @ P['w_br_rwkv'])
              + jax.nn.sigmoid(g_a) * (o_a @ P['w_br_attn']))
    x = x + merged @ P['w_out']
    f, conv_new = conv_ffn(rmsnorm(x, P['norm_ffn_g']), conv_prev, P['ffn_w_in'],
                           P['ffn_conv_w'], P['ffn_conv_b'], P['ffn_w_down'])
    x = x + f
    return x, (shift_new, wkv_new, k_new, v_new, conv_new)


def setup_inputs(seed: int = 0) -> dict:
    key = jax.random.key(seed)
    ks = jax.random.split(key, 32)
    f32 = jnp.float32
    nrm = lambda i, shape, scale: jax.random.normal(ks[i], shape, f32) * scale
    L = DEPTH
    return {
        'x_prompt': nrm(0, (BATCH, SEQ, D_MODEL), 1.0),
        'x_sample': nrm(1, (DEC_BATCH, DEC_SEQ, D_MODEL), 1.0),
        'state_rwkv_shift': nrm(2, (L, DEC_BATCH, RWKV_PROJ), 1.0),
        'state_rwkv_wkv': nrm(3, (L, DEC_BATCH, RWKV_HEADS, HEAD_DIM, HEAD_DIM), 0.5),
        'cache_swa_k': nrm(4, (L, DEC_BATCH, WINDOW, ATT_KV_HEADS, HEAD_DIM), 1.0),
        'cache_swa_v': nrm(5, (L, DEC_BATCH, WINDOW, ATT_KV_HEADS, HEAD_DIM), 1.0),
        'state_ffn_conv': nrm(6, (L, DEC_BATCH, CONV_W - 1, D_FF), 1.0),
        'norm_mix_g': 1.0 + nrm(7, (L, D_MODEL), 0.02),
        'w_in': nrm(8, (L, D_MODEL, IN_PROJ), D_MODEL ** -0.5),
        'rwkv_mu': jax.random.uniform(ks[9], (L, RWKV_PROJ), f32),
        'rwkv_w0': jax.random.uniform(ks[10], (L, RWKV_DIM), f32, minval=-6.0, maxval=-1.0),
        'rwkv_w2': nrm(11, (L, D_DECAY_LORA, RWKV_DIM), 0.5 * D_DECAY_LORA ** -0.5),
        'rwkv_a0': nrm(12, (L, RWKV_DIM), 0.1),
        'rwkv_a2': nrm(13, (L, D_AAA_LORA, RWKV_DIM), 0.5 * D_AAA_LORA ** -0.5),
        'rwkv_g2': nrm(14, (L, D_GATE_LORA, RWKV_DIM), D_GATE_LORA ** -0.5),
        'rwkv_k_k': 0.85 + nrm(15, (L, RWKV_DIM), 0.02),
        'rwkv_k_a': 1.0 + nrm(16, (L, RWKV_DIM), 0.02),
        'rwkv_r_k': nrm(17, (L, RWKV_HEADS, HEAD_DIM), 0.1),
        'rwkv_ln_g': 1.0 + nrm(18, (L, RWKV_DIM), 0.02),
        'rwkv_ln_b': nrm(19, (L, RWKV_DIM), 0.01),
        'attn_sinks': nrm(20, (L, ATT_Q_HEADS), 0.5),
        'w_br_rwkv': nrm(21, (L, RWKV_DIM, D_MODEL), RWKV_DIM ** -0.5),
        'w_br_attn': nrm(22, (L, ATT_DIM, D_MODEL), ATT_DIM ** -0.5),
        'w_out': nrm(23, (L, D_MODEL, D_MODEL), D_MODEL ** -0.5),
        'norm_ffn_g': 1.0 + nrm(24, (L, D_MODEL), 0.02),
        'ffn_w_in': nrm(25, (L, D_MODEL, 2 * D_FF), D_MODEL ** -0.5),
        'ffn_conv_w': nrm(26, (L, CONV_W, D_FF), CONV_W ** -0.5),
        'ffn_conv_b': nrm(27, (L, D_FF), 0.01),
        'ffn_w_down': nrm(28, (L, D_FF, D_MODEL), D_FF ** -0.5),
        'norm_final_g': 1.0 + nrm(29, (D_MODEL,), 0.02),
    }


def reference(x_prompt, x_sample, state_rwkv_shift, state_rwkv_wkv, cache_swa_k, cache_swa_v,
              state_ffn_conv, norm_mix_g, w_in, rwkv_mu, rwkv_w0, rwkv_w2, rwkv_a0, rwkv_a2, rwkv_g2,
              rwkv_k_k, rwkv_k_a, rwkv_r_k, rwkv_ln_g, rwkv_ln_b, attn_sinks, w_br_rwkv, w_br_attn,
              w_out, norm_ffn_g, ffn_w_in, ffn_conv_w, ffn_conv_b, ffn_w_down, norm_final_g):
    Bp, Tp, _ = x_prompt.shape
    pos_p = jnp.arange(Tp, dtype=jnp.int32)
    pos_s = PAST_LEN + jnp.arange(x_sample.shape[1], dtype=jnp.int32)
    xp, xs = x_prompt, x_sample
    outs_p, outs_s = [], []
    for l in range(DEPTH):
        P = {
            'norm_mix_g': norm_mix_g[l], 'w_in': w_in[l], 'rwkv_mu': rwkv_mu[l],
            'rwkv_w0': rwkv_w0[l], 'rwkv_w2': rwkv_w2[l], 'rwkv_a0': rwkv_a0[l], 'rwkv_a2': rwkv_a2[l],
            'rwkv_g2': rwkv_g2[l], 'rwkv_k_k': rwkv_k_k[l], 'rwkv_k_a': rwkv_k_a[l],
            'rwkv_r_k': rwkv_r_k[l], 'rwkv_ln_g': rwkv_ln_g[l], 'rwkv_ln_b': rwkv_ln_b[l],
            'attn_sinks': attn_sinks[l], 'w_br_rwkv': w_br_rwkv[l], 'w_br_attn': w_br_attn[l],
            'w_out': w_out[l], 'norm_ffn_g': norm_ffn_g[l], 'ffn_w_in': ffn_w_in[l],
            'ffn_conv_w': ffn_conv_w[l], 'ffn_conv_b': ffn_conv_b[l], 'ffn_w_down': ffn_w_down[l],
        }
        xp, sp = layer(xp, pos_p,
                       jnp.zeros((Bp, RWKV_PROJ), xp.dtype),
                       jnp.zeros((Bp, RWKV_HEADS, HEAD_DIM, HEAD_DIM), jnp.float32),
                       None, None,
                       jnp.zeros((Bp, CONV_W - 1, D_FF), xp.dtype), P)
        xs, ss = layer(xs, pos_s, state_rwkv_shift[l], state_rwkv_wkv[l], cache_swa_k[l],
                       cache_swa_v[l], state_ffn_conv[l], P)
        outs_p.append(sp)
        outs_s.append(ss)
    y_prompt = rmsnorm(xp, norm_final_g)
    y_sample = rmsnorm(xs, norm_final_g)
    p_shift = jnp.stack([o[0] for o in outs_p])
    p_wkv = jnp.stack([o[1] for o in outs_p])
    p_k = jnp.stack([o[2] for o in outs_p])
    p_v = jnp.stack([o[3] for o in outs_p])
    p_conv = jnp.stack([o[4] for o in outs_p])
    s_shift = jnp.stack([o[0] for o in outs_s])
    s_wkv = jnp.stack([o[1] for o in outs_s])
    s_k = jnp.stack([o[2] for o in outs_s])
    s_v = jnp.stack([o[3] for o in outs_s])
    s_conv = jnp.stack([o[4] for o in outs_s])
    return (y_prompt, y_sample, p_shift, p_wkv, p_k, p_v, p_conv, s_shift, s_wkv, s_k, s_v, s_conv)
```

```python
import math
from contextlib import ExitStack

import numpy as np
import concourse.bass as bass
import concourse.mybir as mybir
from concourse.bass_utils import run_bass_kernel_spmd

F32 = mybir.dt.float32
BF = mybir.dt.bfloat16
AF = mybir.ActivationFunctionType
ALU = mybir.AluOpType
AX = mybir.AxisListType

ENGS = ["sp", "pe", "act", "dve", "pool"]
DEBUG_WHERE = True

D = 1024
HD = 64
NH = 8
RD = 512
RP = 1792
INP = 4608
DFF = 2816
NFC = 22
NS = 16
MS = 64
PAST = 16384
CDEC = -math.exp(-0.5)
NEG = -30000.0


class FW:
    def __init__(self, nc, es):
        self.nc = nc
        self.es = es
        self.ops = {e: [] for e in ENGS}
        self.lastw = {}
        self.readers = {}
        self.dma_count = {}
        self.inc = {}

    def sb(self, name, shape, dt=F32):
        return self.es.enter_context(self.nc.sbuf_tensor(name, list(shape), dt))

    def ps(self, name, shape, dt=F32):
        return self.es.enter_context(self.nc.psum_tensor(name, list(shape), dt))

    def capture(self, f):
        self.cap = []
        f()
        log, self.cap = self.cap, None
        return log

    def replay(self, logs, chunk=3):
        logs = [list(lg) for lg in logs if lg]
        pos = [0] * len(logs)
        while any(p < len(lg) for p, lg in zip(pos, logs)):
            for k, lg in enumerate(logs):
                for _ in range(chunk):
                    if pos[k] < len(lg):
                        self.op(*lg[pos[k]])
                        pos[k] += 1

    def op(self, eng, fn, r=(), w=(), dma=None):
        if getattr(self, "cap", None) is not None:
            self.cap.append((eng, fn, tuple(r), tuple(w), dma))
            return
        ops = self.ops[eng]
        idx = len(ops)
        deps = set()
        pr = [k for k in r if isinstance(k, str) and k[:2] in ("ps", "pb") and k[2:].isdigit()]
        if pr:
            r = [k for k in r if k not in pr]
            w = list(w) + pr
        for k in r:
            t = self.lastw.get(k)
            if t is not None:
                deps.add(t)
        for k in w:
            t = self.lastw.get(k)
            if t is not None:
                deps.add(t)
            for t2 in self.readers.get(k, {}).values():
                deps.add(t2)
        if dma is not None:
            c = self.dma_count.get(dma, 0) + 1
            self.dma_count[dma] = c
            tok = ("d", dma, c)
        else:
            tok = ("c", eng, idx)
        if eng == "pe":
            deps = {d for d in deps if not (d[0] == "c" and d[1] == "pe")}
        deps.discard(tok)
        rec = dict(fn=fn, deps=deps, tok=tok, signal=False)
        if DEBUG_WHERE:
            import sys as _s
            f_ = _s._getframe(1)
            wh = []
            while f_ is not None and len(wh) < 4:
                wh.append(f_.f_lineno)
                f_ = f_.f_back
            rec["where"] = wh
        ops.append(rec)
        for d in deps:
            if d[0] == "c":
                self.ops[d[1]][d[2]]["signal"] = True
        for k in w:
            self.lastw[k] = tok
            self.readers[k] = {}
        for k in r:
            rk = ("d", tok[1]) if tok[0] == "d" else tok[1]
            self.readers.setdefault(k, {})[rk] = tok
        return tok

    def fence(self):
        toks = set()
        for e in ENGS:
            for rec in reversed(self.ops[e]):
                if rec["tok"][0] == "c" and rec["fn"] is not None:
                    toks.add(rec["tok"])
                    rec["signal"] = True
                    break
        for k, c in self.dma_count.items():
            toks.add(("d", k, c))
        for e in ENGS:
            self.ops[e].append(dict(fn=None, deps=set(toks), tok=("c", e, len(self.ops[e])), signal=False))

    def dma(self, out, in_, r=(), w=(), key=None, eng="sp", **kw):
        self.op(eng, lambda e: e.dma_start(out=out, in_=in_, **kw), r=r, w=w, dma=key)

    def mm(self, out, lhsT, rhs, start, stop, r=(), w=()):
        self.op("pe", lambda e: e.matmul(out, lhsT, rhs, start=start, stop=stop), r=r, w=w)

    def tr(self, out, in_, ident, r=(), w=()):
        self.op("pe", lambda e: e.transpose(out, in_, ident), r=r, w=w)

    def act(self, out, in_, func, r=(), w=(), **kw):
        self.op("act", lambda e: e.activation(out, in_, func, **kw), r=r, w=w)

    def emit(self):
        nc = self.nc
        sems = {e: self.es.enter_context(nc.semaphore("s_" + e)) for e in ENGS}
        dsems = {}
        for i, k in enumerate(self.dma_count):
            dsems[k] = self.es.enter_context(nc.semaphore("d%d" % i))
        for e in ENGS:
            c = 0
            for rec in self.ops[e]:
                if rec["signal"] and rec["tok"][0] == "c":
                    c += 1
                rec["sigval"] = c
        final_counts = dict(self.dma_count)

        def run(engname, eng):
            waited = {}
            for rec in self.ops[engname]:
                need = {}
                for d in rec["deps"]:
                    if d[0] == "c":
                        s = ("c", d[1])
                        v = self.ops[d[1]][d[2]]["sigval"]
                    else:
                        s = ("d", d[1])
                        v = self.inc.get(d[1], 16) * d[2]
                    if need.get(s, 0) < v:
                        need[s] = v
                for s, v in need.items():
                    if waited.get(s, 0) >= v:
                        continue
                    waited[s] = v
                    eng.wait_ge(sems[s[1]] if s[0] == "c" else dsems[s[1]], v)
                if rec["fn"] is None:
                    continue
                try:
                    ins = rec["fn"](eng)
                except Exception:
                    print("EMIT FAILURE at lines", rec.get("where"), "engine", engname)
                    raise
                if rec["tok"][0] == "d":
                    ins.then_inc(dsems[rec["tok"][1]], self.inc.get(rec["tok"][1], 16))
                elif rec["signal"]:
                    ins.then_inc(sems[engname], 1)
            if engname == "sp":
                for k, c in final_counts.items():
                    v = self.inc.get(k, 16) * c
                    if waited.get(("d", k), 0) < v:
                        eng.wait_ge(dsems[k], v)

        with nc.Block() as block:
            @block.sync
            def _(e):
                run("sp", e)

            @block.tensor
            def _(e):
                run("pe", e)

            @block.scalar
            def _(e):
                run("act", e)

            @block.vector
            def _(e):
                run("dve", e)

            @block.gpsimd
            def _(e):
                run("pool", e)


def bc3(ap2, n):
    s = list(ap2.shape)
    return ap2.unsqueeze(2).to_broadcast([s[0], s[1], n])


def h3(ap2, h=NH):
    return ap2.rearrange("p (h d) -> p h d", h=h)


class Builder:
    def __init__(self, TP, taps=False):
        self.TP = TP
        self.NT = TP // 128
        self.taps = taps
        self.nc = bass.Bass("TRN2", target_bir_lowering=False)
        self.I = {}
        self.O = {}
        self.psi = 0
        self.pbi = 0
        self.tapnames = []
        self.pool = None
        self.pcnt = {}

    def din(self, n, s):
        self.I[n] = self.nc.dram_tensor(n, list(s), F32, kind="ExternalInput").ap()

    def dout(self, n, s):
        self.O[n] = self.nc.dram_tensor(n, list(s), F32, kind="ExternalOutput").ap()

    def declare(self):
        TP = self.TP
        for n, s in [("xp", (TP, D)), ("xs", (MS, D)), ("st_shift", (2, NS, RP)), ("st_wkv", (2, 128, 4096)),
                     ("ck", (2, NS, 128, 128)), ("cv", (2, NS, 128, 128)), ("st_conv", (2, 2 * NS, DFF)),
                     ("norm_mix_g", (2, D)), ("w_in", (2, D, INP)), ("rwkv_mu", (2, RP)), ("rwkv_w0", (2, RD)),
                     ("rwkv_w2", (2, 64, RD)), ("rwkv_a0", (2, RD)), ("rwkv_a2", (2, 64, RD)),
                     ("rwkv_g2", (2, 128, RD)), ("rwkv_k_k", (2, RD)), ("rwkv_k_a", (2, RD)),
                     ("rwkv_r_k", (2, RD)), ("rwkv_ln_g", (2, RD)), ("rwkv_ln_b", (2, RD)),
                     ("attn_sinks", (2, NH)), ("w_br_rwkv", (2, RD, D)), ("w_br_attn", (2, RD, D)),
                     ("w_out", (2, D, D)), ("norm_ffn_g", (2, D)), ("ffn_w_in", (2, D, 2 * DFF)),
                     ("ffn_conv_w", (2, 3, DFF)), ("ffn_conv_b", (2, DFF)), ("ffn_w_down", (2, DFF, D)),
                     ("norm_final_g", (D,)),
                     ("c_ident", (128, 128)), ("c_cosp", (TP, 32)), ("c_sinp", (TP, 32)),
                     ("c_coss", (MS, 32)), ("c_sins", (MS, 32)), ("c_tri", (128, 256)),
                     ("c_mask2", (128, 256)), ("c_maskL", (128, 128)), ("c_amask", (128, 768)),
                     ("c_smask", (32, 132)), ("c_last", (128, 1)),
                     ("xh0", (128, D)), ("c_cosh", (128, 32)), ("c_sinh", (128, 32)), ("c_amask0", (128, 256)), ("c_sel", (128, 8))]:
            self.din(n, s)
        for n, s in [("yp", (TP, D)), ("ys", (MS, D)), ("p_shift", (2, RP)), ("p_wkv", (2, NH, 64, 64)),
                     ("p_k", (2, 128, 128)), ("p_v", (2, 128, 128)), ("p_conv", (2, 2, DFF)),
                     ("s_shift", (2, NS, RP)), ("s_wkv", (2, 128, 4096)), ("s_k", (2, NS, 128, 128)),
                     ("s_v", (2, NS, 128, 128)), ("s_conv", (2, 2 * NS, DFF))]:
            self.dout(n, s)
        nc = self.nc
        self.xbuf = nc.dram_tensor("xbuf", [TP, D], F32).ap()
        self.xsbuf = nc.dram_tensor("xsbuf", [MS, D], F32).ap()
        self.mrbuf = nc.dram_tensor("mrbuf", [self.NT + 1, 128, 1024], BF).ap()
        self.xh_dram = nc.dram_tensor("xh_dram", [128, D], F32).ap()
        self.sq = nc.dram_tensor("sq", [6, MS, RD], F32).ap()
        self.sy = nc.dram_tensor("sy", [MS, RD], F32).ap()

    def alloc(self, name, shape, dt=F32):
        shape = list(shape)
        n = 1
        for d_ in shape[1:]:
            n *= d_
        nbytes = n * (4 if dt == F32 else 2)
        nw = (nbytes + 31) // 32 * 8
        off = self.aoff
        self.aoff += nw
        self.apeak = max(self.apeak, self.aoff)
        assert self.aoff <= self.ASZ, "SBUF arena overflow: %s needs %d words (limit %d)" % (name, self.aoff, self.ASZ)
        ap = self.arena[0:shape[0], off:off + nw]
        if dt != F32:
            ap = ap.bitcast(dt)
        ap = ap[:, 0:n]
        if len(shape) > 2:
            names = ["d%d" % i for i in range(len(shape) - 1)]
            pat = "p (%s) -> p %s" % (" ".join(names), " ".join(names))
            ap = ap.rearrange(pat, **{names[i]: shape[i + 1] for i in range(len(names))})
        return ap

    def release(self, mark):
        self.fw.fence()
        self.aoff = mark

    def pf(self):
        ids = {None: [0, 1, 2, 3, 4, 5], 0: [0, 1, 2], 1: [3, 4, 5]}[self.pool]
        c = self.pcnt.setdefault(("f", self.pool), 0)
        self.pcnt[("f", self.pool)] = c + 1
        k = ids[c % len(ids)]
        return self.PS[k], "ps%d" % k

    def pb(self):
        ids = {None: [0, 1], 0: [0], 1: [1]}[self.pool]
        c = self.pcnt.setdefault(("b", self.pool), 0)
        self.pcnt[("b", self.pool)] = c + 1
        k = ids[c % len(ids)]
        return self.PBK[k], "pb%d" % k

    def tap(self, name, ap, rkeys, dt=F32):
        if not self.taps:
            return
        shp = list(ap.shape)
        t = self.nc.dram_tensor("tap_" + name, shp, dt, kind="ExternalOutput").ap()
        self.tapnames.append("tap_" + name)
        self.fw.dma(t, ap, r=rkeys, key="tap_" + name)

    def V(self, fn, r=(), w=()):
        self.fw.op("dve", fn, r, w)

    def P(self, fn, r=(), w=()):
        self.fw.op("pool", fn, r, w)

    def col_load(self, dst, dkey, vec, n):
        fw = self.fw
        st = self.cstage
        fw.dma(st[0:n, :], vec.rearrange("(c p) -> c p", p=128), w=["cstage"], key="cstage")
        ps, pk = self.pf()
        fw.tr(ps[:, 0:n], st[0:n, :], self.identf[0:n, 0:n], r=["cstage", "identf"], w=[pk])
        fw.act(dst, ps[:, 0:n], AF.Copy, r=[pk], w=[dkey])

    def gather_select(self, src_ap, src_keys, n, ag_in, ag_out, name):
        fw = self.fw
        fw.dma(ag_in, src_ap, r=src_keys, w=[name + "_in"], key=name + "_st")
        self.gi = getattr(self, "gi", 0)
        ck = name + "_cc"
        fw.inc[ck] = 1
        fw.op("pool", lambda e: e.collective_compute("AllGather", ALU.bypass, replica_groups=[list(range(8))], ins=[ag_in], outs=[ag_out]),
              r=[name + "_in"], w=[name + "_out"], dma=ck)
        for r_ in range(8):
            st, sk = self.xt[r_ % 2], "xt%d" % (r_ % 2)
            fw.dma(st[:, 0:n], ag_out[r_ * 128:(r_ + 1) * 128, :], r=[name + "_out"], w=[sk], key=sk)
            if r_ == 0:
                self.V(lambda e, st=st: e.tensor_scalar(src_ap, st[:, 0:n], self.sel[:, 0:1], None, ALU.mult), r=[sk, "sel"], w=src_keys)
            else:
                self.V(lambda e, st=st, r_=r_: e.scalar_tensor_tensor(src_ap, st[:, 0:n], self.sel[:, r_:r_ + 1], src_ap, ALU.mult, ALU.add),
                       r=[sk, "sel"] + list(src_keys), w=src_keys)

    def bcast_load(self, dst, dkey, vec):
        self.fw.dma(dst, vec.partition_broadcast(dst.shape[0]), w=[dkey], key=dkey)

    def prep_w(self, nchunks, ncols, src, dst, dkey, mode, scale=None, mul=None, mulkey=None, sview=None):
        fw = self.fw
        for c in range(nchunks):
            for s0 in range(0, ncols, 2048):
                n = min(2048, ncols - s0)
                k = self.wst_i % 2
                self.wst_i += 1
                st = self.wstage[k]
                sk = "wst%d" % k
                fw.dma(st[:, 0:n], src(c, s0, n), w=[sk], key=sk)
                o = dst(c, s0, n)
                dk = dkey(c)
                if sview is not None:
                    sv_ = sview(st[:, 0:n])
                    sc = scale(c)
                    self.V(lambda eg, o=o, sv_=sv_, sc=sc: eg.tensor_scalar(o, sv_, sc, None, ALU.mult), r=[sk, "gcol"], w=[dk])
                    continue
                if mode == "plain":
                    e = ["dve", "pool", "act"][self.wst_i % 3]
                    if e == "act":
                        fw.act(o, st[:, 0:n], AF.Copy, r=[sk], w=[dk])
                    else:
                        fw.op(e, lambda eg, o=o, st=st, n=n: eg.tensor_copy(o, st[:, 0:n]), r=[sk], w=[dk])
                elif mode == "col":
                    sc = scale(c)
                    e = ["dve", "pool"][self.wst_i % 2]
                    fw.op(e, lambda eg, o=o, st=st, n=n, sc=sc: eg.tensor_scalar(o, st[:, 0:n], sc, None, ALU.mult),
                          r=[sk, "gcol"], w=[dk])
                else:
                    sc = scale(c)
                    m = mul(s0, n)
                    self.V(lambda eg, o=o, st=st, n=n, sc=sc, m=m: eg.scalar_tensor_tensor(
                        o, st[:, 0:n], sc, m, ALU.mult, ALU.mult), r=[sk, "gcol", mulkey], w=[dk])

    def norm_hT(self, xt, xk, M, hdst, hkey, identb):
        fw = self.fw
        xn, ss, t1 = self.xn, self.ss, self.t1
        fw.act(xn[0:M, :], xt[0:M, :], AF.Square, r=[xk], w=["xn", "ss"], accum_out=ss[0:M, :])
        self.V(lambda e: e.tensor_scalar(t1[0:M, :], ss[0:M, :], 1.0 / D, 1e-6, ALU.mult, ALU.add), r=["ss"], w=["t1"])
        fw.act(t1[0:M, :], t1[0:M, :], AF.Sqrt, r=["t1"], w=["t1"])
        self.V(lambda e: e.reciprocal(t1[0:M, :], t1[0:M, :]), r=["t1"], w=["t1"])
        self.V(lambda e: e.tensor_scalar(xn[0:M, :], xt[0:M, :], t1[0:M, 0:1], None, ALU.mult), r=[xk, "t1"], w=["xn"])
        pbk, pk = self.pb()
        for c in range(8):
            fw.tr(pbk[:, c * M:(c + 1) * M], xn[0:M, c * 128:(c + 1) * 128], identb[0:M, 0:M], r=["xn", "identb"], w=[pk])
        fw.act(hdst, pbk[:, 0:8 * M].rearrange("p (c t) -> p c t", c=8), AF.Copy, r=[pk], w=[hkey])

    def build(self):
        self.declare()
        nc = self.nc
        with ExitStack() as es:
            self.fw = fw = FW(nc, es)
            self.PS = [fw.ps("ps%d" % i, [128, 512], F32) for i in range(6)]
            self.PBK = [fw.ps("pb%d" % i, [128, 1024], BF) for i in range(2)]
            self.ASZ = 52224
            self.arena = fw.sb("arena", [128, self.ASZ])
            self.aoff = 0
            self.apeak = 0
            self.identf = self.alloc("identf", [128, 128])
            self.identb = self.alloc("identb", [128, 128], BF)
            self.cstage = self.alloc("cstage", [32, 128])
            self.wst_i = 0
            self.xn = self.alloc("xn", [128, D], BF)
            self.ss = self.alloc("ss", [128, 1])
            self.t1 = self.alloc("t1", [128, 1])
            self.gcol = self.alloc("gcol", [128, 8])
            self.xt = [self.alloc("xt%d" % i, [128, D]) for i in range(2)]
            self.sel = self.alloc("sel", [128, 8])
            fw.dma(self.sel[:], self.I["c_sel"], w=["sel"], key="sel")
            fw.dma(self.identf[:], self.I["c_ident"], w=["identf"], key="identf")
            self.V(lambda e: e.tensor_copy(self.identb[:], self.identf[:]), r=["identf"], w=["identb"])
            for l in range(2):
                for p_ in (self.pass_rwkv, self.pass_attn, self.pass_ffn):
                    mk_ = self.aoff
                    p_(l, None)
                    self.release(mk_)
            print("arena peak words", self.apeak, "of", self.ASZ)
            fw.emit()
        return nc

    def sbl(self, es2, name, shape, dt=F32):
        return self.alloc(name, shape, dt)

    def xsrc(self, l, i):
        if i < self.NT:
            src = self.I["xp"] if l == 0 else self.xbuf
            return src[i * 128:(i + 1) * 128, :], ("xb", i)
        src = self.I["xs"] if l == 0 else self.xsbuf
        return src, ("xb", i)

    def pass_rwkv(self, l, es2):
        fw, I, O, NT = self.fw, self.I, self.O, self.NT
        sbl = lambda n, s, dt=F32: self.sbl(es2, "r%d_" % l + n, s, dt)
        identb, identf = self.identb, self.identf
        W1 = sbl("W1", [128, 8, RP], BF)
        W2 = sbl("W2", [128, 8, RP], BF)
        Wg = sbl("Wg", [128, 8, D], BF)
        Wr = sbl("Wr", [128, 4, D], BF)
        lw2 = sbl("lw2", [128, RD], BF)
        lg2 = sbl("lg2", [128, RD], BF)
        bcs = {}
        for n in ["rwkv_w0", "rwkv_a0", "rwkv_k_k", "rwkv_k_a", "rwkv_r_k", "rwkv_ln_g", "rwkv_ln_b"]:
            bcs[n] = sbl(n, [128, RD])
            self.bcast_load(bcs[n][:], n + "_bc", I[n][l])
        mucol = sbl("mucol", [128, 2])
        tri = sbl("tri", [128, 256])
        mask2 = sbl("mask2", [128, 256])
        maskL = sbl("maskL", [128, 128])
        clast = sbl("clast", [128, 1])
        fw.dma(tri[:], I["c_tri"], w=["tri"], key="tri")
        fw.dma(mask2[:], I["c_mask2"], w=["mask2"], key="mask2")
        fw.dma(maskL[:], I["c_maskL"], w=["maskL"], key="maskL")
        fw.dma(clast[:], I["c_last"], w=["clast"], key="clast")
        self.col_load(self.gcol[:], "gcol", I["norm_mix_g"][l], 8)
        self.col_load(mucol[:], "mucol", I["rwkv_mu"][l, 1536:1792], 2)
        m0 = self.aoff
        self.wstage = [sbl("wst%d" % i_, [128, 2048]) for i_ in range(2)]
        mu_bc = sbl("mu_bc", [128, RP])
        omm_bc = sbl("omm_bc", [128, RP])
        self.bcast_load(mu_bc[:], "mu_bc", I["rwkv_mu"][l])
        self.V(lambda e: e.tensor_scalar(omm_bc[:], mu_bc[:], -1.0, 1.0, ALU.mult, ALU.add), r=["mu_bc"], w=["omm_bc"])
        win = I["w_in"][l]
        gsc = lambda c: self.gcol[:, c:c + 1]
        self.prep_w(8, RP, lambda c, s0, n: win[c * 128:(c + 1) * 128, s0:s0 + n],
                    lambda c, s0, n: W1[:, c, s0:s0 + n], lambda c: "W1_%d" % c, "colmul", gsc,
                    lambda s0, n: omm_bc[:, s0:s0 + n], "omm_bc")
        self.prep_w(8, RP, lambda c, s0, n: win[c * 128:(c + 1) * 128, s0:s0 + n],
                    lambda c, s0, n: W2[:, c, s0:s0 + n], lambda c: "W2_%d" % c, "colmul", gsc,
                    lambda s0, n: mu_bc[:, s0:s0 + n], "mu_bc")
        self.prep_w(8, D, lambda c, s0, n: win[c * 128:(c + 1) * 128, 2560 + s0:2560 + s0 + n],
                    lambda c, s0, n: Wg[:, c, s0:s0 + n], lambda c: "Wg_%d" % c, "col", gsc)
        wbr = I["w_br_rwkv"][l]
        self.prep_w(4, D, lambda c, s0, n: wbr[c * 128:(c + 1) * 128, s0:s0 + n],
                    lambda c, s0, n: Wr[:, c, s0:s0 + n], lambda c: "Wr_%d" % c, "plain")
        for (nm, p0, dk_) in [("rwkv_w2", 0, "lw2a"), ("rwkv_a2", 64, "lw2b")]:
            k = self.wst_i % 2
            self.wst_i += 1
            wsk = self.wstage[k]
            fw.dma(wsk[p0:p0 + 64, 0:RD], I[nm][l], w=["wst%d" % k], key="wst%d" % k)
            self.P(lambda e, wsk=wsk, p0=p0: e.tensor_copy(lw2[p0:p0 + 64, :], wsk[p0:p0 + 64, 0:RD]), r=["wst%d" % k], w=[dk_])
        self.prep_w(1, RD, lambda c, s0, n: I["rwkv_g2"][l], lambda c, s0, n: lg2[:, :], lambda c: "lg2", "plain")
        WK1 = ["W1_%d" % c for c in range(8)]
        WK2 = ["W2_%d" % c for c in range(8)]
        self.release(m0)
        class NSP:
            pass
        zr, zk = sbl("zr", [128, RD]), sbl("zk", [128, RD])
        lact = sbl("lact", [128, 128], BF)
        T = [sbl("tmp%d" % i_, [128, RD]) for i_ in range(8)]
        sm = sbl("sm", [128, 64])
        orT = sbl("orT", [128, 4, 128], BF)
        sgr = sbl("sgr", [128, 8, 128], BF)
        mrT0_ = sbl("mrT0", [128, 8, 128], BF)
        mrT = [mrT0_, mrT0_]
        TP_ = [sbl("tpost%d" % i_, [128, RD]) for i_ in range(2)]
        m1 = self.aoff
        NRB = 9864

        def mkrec(k):
            R = NSP()
            rb = sbl("RB%d" % k, [128, NRB], BF)
            rf = sbl("RF%d" % k, [128, 528])
            R.rb, R.rf, R.k = rb, rf, k
            R.RKT = rb[:, 0:1024].rearrange("p (j a t) -> p j a t", j=4, a=2)
            R.G4 = [rb[:, 1024 + j * 1280:1024 + (j + 1) * 1280].rearrange("p (h c) -> p h c", h=2) for j in range(4)]
            R.ZF = [rb[:, 6144 + j * 256:6144 + (j + 1) * 256].rearrange("p (h c) -> p h c", h=2) for j in range(4)]
            R.vb, R.ktt, R.bnt = rb[:, 7168:7680], rb[:, 7680:8192], rb[:, 8192:8704]
            R.sgT = rb[:, 8704:8832]
            R.hT = rb[:, 8832:9864].rearrange("p (c t) -> p c t", c=8)
            R.zv, R.WC, R.bon = rf[:, 0:512], rf[:, 512:516], rf[:, 516:524]
            R.K = (lambda k_: (lambda n: "%s#%d" % (n, k_)))(k)
            return R
        R0 = mkrec(0)
        U0b = [sbl("U0b%d" % j, [128, 2, 64], BF) for j in range(4)]
        Ub = sbl("Ub", [128, RD], BF)
        Nst = sbl("Nst", [128, 4, 128])
        Nb = sbl("Nb", [128, 4, 128], BF)
        self.V(lambda e: e.memset(Nst[:], 0.0), w=["Nst"])
        self.V(lambda e: e.memset(Nb[:], 0.0), w=["Nb"])
        m2 = self.aoff
        rt, kat = sbl("rt", [128, RD], BF), sbl("kat", [128, RD], BF)
        KT = sbl("KT", [128, 4, 128], BF)
        BT = sbl("BT", [128, 4, 128], BF)
        for j in range(4):
            self.P(lambda e, j=j: e.tensor_copy(R0.G4[j][:, :, 512:640], identb[:, :].unsqueeze(1).to_broadcast([128, 2, 128])),
                   r=["identb"], w=["G4_%d" % j])
        EZ = [[sbl("EZ%d_%d" % (j, a), [128, 2, 2, 128], BF) for a in range(2)] for j in range(4)]
        FFa = [sbl("FFa%d" % a, [128, 4, 2, 128], BF) for a in range(2)]
        FF = [[FFa[a][:, j] for a in range(2)] for j in range(4)]

        def tok_proj(M, hcur, hprev, hk, g0, dstkey):
            ps, pk = self.pf()
            n = 0
            for c in range(8):
                fw.mm(ps[0:M, :], hcur(c), W1[:, c, g0:g0 + 512], n == 0, False, r=[hk, WK1[c]], w=[pk])
                n += 1
            for c in range(8):
                fw.mm(ps[0:M, :], hprev(c), W2[:, c, g0:g0 + 512], False, c == 7, r=[hk, WK2[c]], w=[pk])
            return ps, pk

        def feat_proj(M, hcur, hprev, hk, g0):
            ps, pk = self.pf()
            for c in range(8):
                fw.mm(ps[:, 0:M], W1[:, c, g0:g0 + 128], hcur(c), c == 0, False, r=[hk, WK1[c]], w=[pk])
            for c in range(8):
                fw.mm(ps[:, 0:M], W2[:, c, g0:g0 + 128], hprev(c), False, c == 7, r=[hk, WK2[c]], w=[pk])
            return ps, pk

        def raw_last(hl, hk, M, dst):
            for gi, g0 in enumerate(range(0, RP, 512)):
                n = min(512, RP - g0)
                ps, pk = self.pf()
                for c in range(8):
                    fw.mm(ps[0:M, 0:n], hl(c), W1[:, c, g0:g0 + n], c == 0, False, r=[hk, WK1[c]], w=[pk])
                for c in range(8):
                    fw.mm(ps[0:M, 0:n], hl(c), W2[:, c, g0:g0 + n], False, c == 7, r=[hk, WK2[c]], w=[pk])
                fw.act(T[gi][0:M, 0:n], ps[0:M, 0:n], AF.Copy, r=[pk], w=["T%d" % gi])
                fw.dma(dst[:, g0:g0 + n], T[gi][0:M, 0:n], r=["T%d" % gi], key="zl%d" % gi)

        def prep(M, sample, R):
            K = R.K
            w0, a0 = bcs["rwkv_w0"], bcs["rwkv_a0"]
            kkb, kab, rkb = bcs["rwkv_k_k"], bcs["rwkv_k_a"], bcs["rwkv_r_k"]
            pw, pwk = self.pf()
            fw.mm(pw[0:M, :], lact[0:64, 0:M], lw2[0:64, :], True, True, r=["lact", "lw2a"], w=[pwk])
            pa, pak = self.pf()
            fw.mm(pa[0:M, :], lact[64:128, 0:M], lw2[64:128, :], True, True, r=["lact", "lw2b"], w=[pak])
            sg, a_, kk, t3, kf, be = T[0], T[1], T[2], T[3], T[4], T[5]
            self.V(lambda e: e.tensor_tensor(sg[0:M, :], pw[0:M, :], w0[0:M, :], ALU.add), r=[pwk, "rwkv_w0_bc"], w=["T0"])
            fw.act(sg[0:M, :], sg[0:M, :], AF.Sigmoid, r=["T0"], w=["T0"])
            self.V(lambda e: e.tensor_tensor(a_[0:M, :], pa[0:M, :], a0[0:M, :], ALU.add), r=[pak, "rwkv_a0_bc"], w=["T1"])
            fw.act(a_[0:M, :], a_[0:M, :], AF.Sigmoid, r=["T1"], w=["T1"])
            self.P(lambda e: e.tensor_tensor(kk[0:M, :], zk[0:M, :], kkb[0:M, :], ALU.mult), r=["zk", "rwkv_k_k_bc"], w=["T2"])
            self.P(lambda e: e.tensor_tensor(t3[0:M, :], kk[0:M, :], kk[0:M, :], ALU.mult), r=["T2"], w=["T3"])
            self.V(lambda e: e.tensor_reduce(sm[0:M, 0:8], h3(t3[0:M, :]), AX.X, ALU.add), r=["T3"], w=["sm0"])
            fw.act(sm[0:M, 0:8], sm[0:M, 0:8], AF.Sqrt, r=["sm0"], w=["sm0"])
            self.V(lambda e: e.tensor_scalar(sm[0:M, 0:8], sm[0:M, 0:8], 1e-12, None, ALU.max), r=["sm0"], w=["sm0"])
            self.V(lambda e: e.reciprocal(sm[0:M, 0:8], sm[0:M, 0:8]), r=["sm0"], w=["sm0"])
            self.V(lambda e: e.tensor_tensor(h3(kk[0:M, :]), h3(kk[0:M, :]), bc3(sm[0:M, 0:8], 64), ALU.mult),
                   r=["T2", "sm0"], w=["T2"])
            self.V(lambda e: e.scalar_tensor_tensor(t3[0:M, :], a_[0:M, :], -1.0, kab[0:M, :], ALU.add, ALU.mult),
                   r=["T1", "rwkv_k_a_bc"], w=["T3"])
            self.V(lambda e: e.scalar_tensor_tensor(kf[0:M, :], t3[0:M, :], 1.0, zk[0:M, :], ALU.add, ALU.mult),
                   r=["T3", "zk"], w=["T4"])
            self.P(lambda e: e.tensor_tensor(be[0:M, :], kk[0:M, :], a_[0:M, :], ALU.mult), r=["T2", "T1"], w=["T5"])
            self.P(lambda e: e.tensor_tensor(t3[0:M, :], zr[0:M, :], kf[0:M, :], ALU.mult), r=["zr", "T4"], w=["T3"])
            self.P(lambda e: e.tensor_tensor(t3[0:M, :], t3[0:M, :], rkb[0:M, :], ALU.mult), r=["T3", "rwkv_r_k_bc"], w=["T3"])
            self.V(lambda e, R=R: e.tensor_reduce(R.bon[0:M, :], h3(t3[0:M, :]), AX.X, ALU.add), r=["T3"], w=[K("bon")])
            if sample:
                fw.act(T[6][0:M, :], sg[0:M, :], AF.Exp, r=["T0"], w=["T6"], scale=CDEC)
                for x, (tl, tk) in enumerate([(zr, "zr"), (T[6], "T6"), (kf, "T4"), (R.zv, K("zv")), (kk, "T2"), (be, "T5")]):
                    fw.dma(self.sq[x], tl[0:M, :], r=[tk], w=[("sq", x)], key="sqw%d" % x)
                return
            pli, plik = self.pf()
            fw.mm(pli[:, :], tri[:, 0:128], sg[:, :], True, True, r=["tri", "T0"], w=[plik])
            ple, plek = self.pf()
            fw.mm(ple[:, :], tri[:, 128:256], sg[:, :], True, True, r=["tri", "T0"], w=[plek])
            eL, eLm, enL = T[6], T[7], T[3]
            fw.act(eL[:, :], pli[:, :], AF.Exp, r=[plik], w=["T6"])
            fw.act(eLm[:, :], ple[:, :], AF.Exp, r=[plek], w=["T7"])
            fw.act(enL[:, :], pli[:, :], AF.Exp, r=[plik], w=["T3"], scale=-1.0)
            self.V(lambda e: e.tensor_tensor(rt[:, :], zr[:, :], eL[:, :], ALU.mult), r=["zr", "T6"], w=["rt"])
            self.V(lambda e: e.tensor_tensor(kat[:, :], kk[:, :], eLm[:, :], ALU.mult), r=["T2", "T7"], w=["kat"])
            self.P(lambda e, R=R: e.tensor_tensor(R.ktt[:, :], kf[:, :], enL[:, :], ALU.mult), r=["T4", "T3"], w=[K("ktt")])
            self.V(lambda e, R=R: e.scalar_tensor_tensor(R.bnt[:, :], be[:, :], -1.0, enL[:, :], ALU.mult, ALU.mult),
                   r=["T5", "T3"], w=[K("bnt")])
            fw.act(R.vb[:, :], R.zv[:, :], AF.Copy, r=[K("zv")], w=[K("vb")])
            pwc, pwck = self.pf()
            for j in range(4):
                fw.mm(pwc[:, j:j + 1], eL[:, j * 128:(j + 1) * 128], clast[:, :], True, True, r=["T6", "clast"], w=[pwck])
            fw.act(R.WC[:, :], pwc[:, 0:4], AF.Copy, r=[pwck], w=[K("WC")])
            for (src, skey, dstf, dk) in [(rt, "rt", None, "RKT"), (kat, "kat", None, "RKT"),
                                          (R.ktt, K("ktt"), None, "KT"), (R.bnt, K("bnt"), None, "BT")]:
                pbk, pk = self.pb()
                for j in range(4):
                    fw.tr(pbk[:, j * 128:(j + 1) * 128], src[:, j * 128:(j + 1) * 128], identb[:, :], r=[skey, "identb"], w=[pk])
                if dk == "RKT":
                    which = 0 if skey == "rt" else 1
                    fw.act(R.RKT[:, :, which, :], pbk[:, 0:512].rearrange("p (j t) -> p j t", j=4), AF.Copy, r=[pk], w=["RKT%d" % which])
                else:
                    dst = KT if dk == "KT" else BT
                    self.V(lambda e, dst=dst, pbk=pbk: e.tensor_copy(dst[:, :, :], pbk[:, 0:512].rearrange("p (j t) -> p j t", j=4)),
                           r=[pk], w=[dk])

        def stageAB(R):
            K = R.K
            RK = [K("RKT0"), K("RKT1")]
            RKT, G4, ZF = R.RKT, R.G4, R.ZF
            zb = [self.pf(), self.pf()]
            for j in range(4):
                for hh in range(2):
                    o = hh * 64
                    pZ, pzk = zb[hh]
                    fw.mm(pZ[:, j * 128:(j + 1) * 128], RKT[o:o + 64, j, 1, :], BT[o:o + 64, j, :], True, True, r=["BT", K("RKT1")], w=[pzk])
            mlb = maskL[:, :].unsqueeze(1).to_broadcast([128, 4, 128])
            for hh in range(2):
                pZ, pzk = zb[hh]
                self.V(lambda e, pZ=pZ, hh=hh: e.tensor_tensor(FFa[0][:, :, hh, :], pZ[:, :].rearrange("p (j c) -> p j c", j=4), mlb, ALU.mult),
                       r=[pzk, "maskL"], w=["FF%d_0" % j for j in range(4)])
            for j in range(4):
                bk = [self.pf(), self.pf()]
                for hh in range(2):
                    o = hh * 64
                    ps, pk = bk[hh]
                    rhs = RKT[o:o + 64, j, :, :].rearrange("p a t -> p (a t)")
                    fw.mm(ps[:, 0:256], KT[o:o + 64, j, :], rhs, True, True, r=["KT"] + RK, w=[pk])
                    fw.mm(ps[:, 256:512], BT[o:o + 64, j, :], rhs, True, True, r=["BT"] + RK, w=[pk])
                for hh in range(2):
                    ps, pk = bk[hh]
                    self.V(lambda e, j=j, hh=hh, ps=ps, G4=G4: e.tensor_tensor(
                        G4[j][:, hh, 0:512].rearrange("p (a c) -> p a c", a=2), ps[:, :].rearrange("p (a c) -> p a c", a=2),
                        mask2[:, :].unsqueeze(1).to_broadcast([128, 2, 256]), ALU.mult), r=[pk, "mask2"], w=[K("G4_%d" % j)])
            for lev in range(7):
                a, b = lev % 2, (lev + 1) % 2
                for j in range(4):
                    fk, fn_ = "FF%d_%d" % (j, a), "FF%d_%d" % (j, b)
                    ezn = "EZ%d_%d" % (j, b)
                    if lev == 0:
                        ezk = K("G4_%d" % j)
                        EZs = lambda hh, j=j, G4=G4: G4[j][:, hh, 384:640]
                        Es = lambda hh, j=j, G4=G4: G4[j][:, hh, 384:512]
                        Zs = lambda j=j, G4=G4: G4[j][:, :, 512:640]
                    else:
                        ezk = "EZ%d_%d" % (j, a)
                        EZs = lambda hh, j=j, a=a: EZ[j][a][:, hh, :, :].rearrange("p a t -> p (a t)")
                        Es = lambda hh, j=j, a=a: EZ[j][a][:, hh, 0, :]
                        Zs = lambda j=j, a=a: EZ[j][a][:, :, 1, :]
                    if lev < 6:
                        pL, plk = self.pf()
                        for hh in range(2):
                            fw.mm(pL[:, hh * 256:(hh + 1) * 256], FF[j][a][:, hh, :], EZs(hh), True, True, r=[ezk, fk], w=[plk])
                        pF, pfk = self.pf()
                        for hh in range(2):
                            fw.mm(pF[:, hh * 128:(hh + 1) * 128], Es(hh), FF[j][a][:, hh, :], True, True, r=[ezk, fk], w=[pfk])
                        l3 = pL[:, :].rearrange("p (h c) -> p h c", h=2)
                        fw.act(EZ[j][b][:, :, 0, :], l3[:, :, 0:128], AF.Copy, r=[plk], w=[ezn])
                        self.V(lambda e, j=j, b=b, l3=l3, Zs=Zs: e.tensor_tensor(EZ[j][b][:, :, 1, :], l3[:, :, 128:256], Zs(), ALU.add),
                               r=[plk, ezk], w=[ezn])
                        fw.act(FF[j][b][:, :, :], pF[:, 0:256].rearrange("p (h c) -> p h c", h=2), AF.Copy, r=[pfk], w=[fn_])
                    else:
                        pL, plk = self.pf()
                        for hh in range(2):
                            fw.mm(pL[:, hh * 128:(hh + 1) * 128], FF[j][a][:, hh, :], EZ[j][a][:, hh, 1, :], True, True, r=[ezk, fk], w=[plk])
                        self.V(lambda e, j=j, a=a, pL=pL, ZF=ZF: e.tensor_tensor(ZF[j][:, :, :], pL[:, 0:256].rearrange("p (h c) -> p h c", h=2),
                                                                      EZ[j][a][:, :, 1, :], ALU.add), r=[plk, ezk], w=[K("ZF%d" % j)])

        def stageC(R):
            K = R.K
            RKT, G4, ZF, vb = R.RKT, R.G4, R.ZF, R.vb
            for j in range(4):
                pU, puk = self.pf()
                for hh in range(2):
                    o, h = hh * 64, 2 * j + hh
                    fw.mm(pU[:, hh * 64:(hh + 1) * 64], RKT[o:o + 64, j, 1, :], Nb[o:o + 64, j, o:o + 64], True, False, r=[K("RKT1"), "Nb"], w=[puk])
                    fw.mm(pU[:, hh * 64:(hh + 1) * 64], G4[j][:, hh, 128:256], vb[:, h * 64:(h + 1) * 64], False, True, r=[K("G4_%d" % j), K("vb")], w=[puk])
                fw.act(U0b[j][:, :, :], pU[:, 0:128].rearrange("p (h c) -> p h c", h=2), AF.Copy, r=[puk], w=["U0b%d" % j])
            for j in range(4):
                pU, puk = self.pf()
                for hh in range(2):
                    fw.mm(pU[:, hh * 64:(hh + 1) * 64], ZF[j][:, hh, :], U0b[j][:, hh, :], True, True, r=[K("ZF%d" % j), "U0b%d" % j], w=[puk])
                fw.act(Ub[:, j * 128:(j + 1) * 128], pU[:, 0:128], AF.Copy, r=[puk], w=["Ub%d" % j])

        def stageD(R):
            K = R.K
            RKT, G4, vb = R.RKT, R.G4, R.vb
            psY, pyk = self.pf()
            for j in range(4):
                for hh in range(2):
                    o, h = hh * 64, 2 * j + hh
                    fw.mm(psY[:, h * 64:(h + 1) * 64], RKT[o:o + 64, j, 0, :], Nb[o:o + 64, j, o:o + 64], True, False, r=[K("RKT0"), "Nb"], w=[pyk])
                    fw.mm(psY[:, h * 64:(h + 1) * 64], G4[j][:, hh, 0:128], vb[:, h * 64:(h + 1) * 64], False, False, r=[K("G4_%d" % j), K("vb")], w=[pyk])
                    fw.mm(psY[:, h * 64:(h + 1) * 64], G4[j][:, hh, 256:384], Ub[:, h * 64:(h + 1) * 64], False, True, r=[K("G4_%d" % j), "Ub%d" % j], w=[pyk])
            return psY, pyk

        def n_update(R):
            K = R.K
            ktt, bnt, vb, WC = R.ktt, R.bnt, R.vb, R.WC
            pN, pnk = self.pf()
            for j in range(4):
                fw.mm(pN[:, j * 128:(j + 1) * 128], ktt[:, j * 128:(j + 1) * 128], vb[:, j * 128:(j + 1) * 128], True, False, r=[K("ktt"), K("vb")], w=[pnk])
                fw.mm(pN[:, j * 128:(j + 1) * 128], bnt[:, j * 128:(j + 1) * 128], Ub[:, j * 128:(j + 1) * 128], False, True, r=[K("bnt"), "Ub%d" % j], w=[pnk])
            n2 = Nst[:, :, :].rearrange("p j c -> p (j c)")
            self.V(lambda e: e.tensor_tensor(n2, pN[:, :], n2, ALU.add), r=[pnk, "Nst"], w=["Nst"])
            self.V(lambda e, WC=WC: e.tensor_tensor(Nst[:, :, :], Nst[:, :, :], bc3(WC[:, :], 128), ALU.mult), r=["Nst", K("WC")], w=["Nst"])
            fw.act(Nb[:, :, :], Nst[:, :, :], AF.Copy, r=["Nst"], w=["Nb"])


        def post(M, yap, ykeys, pg, pgk, R):
            K = R.K
            lng, lnb = bcs["rwkv_ln_g"], bcs["rwkv_ln_b"]
            y2, yc = TP_[0], TP_[1]
            ob = TP_[0].bitcast(BF)[:, 0:RD]
            self.V(lambda e: e.tensor_reduce(sm[0:M, 16:24], h3(yap), AX.X, ALU.add), r=ykeys, w=["sm2"])
            fw.act(y2[0:M, :], yap, AF.Square, r=ykeys, w=["TP0"])
            self.V(lambda e: e.tensor_reduce(sm[0:M, 24:32], h3(y2[0:M, :]), AX.X, ALU.add), r=["TP0"], w=["sm3"])
            mean, var = sm[0:M, 16:24], sm[0:M, 24:32]
            self.V(lambda e: e.tensor_scalar(mean, mean, 1.0 / 64, None, ALU.mult), r=["sm2"], w=["sm2"])
            self.V(lambda e: e.tensor_tensor(sm[0:M, 32:40], mean, mean, ALU.mult), r=["sm2"], w=["sm4"])
            self.V(lambda e: e.scalar_tensor_tensor(var, var, 1.0 / 64, sm[0:M, 32:40], ALU.mult, ALU.subtract), r=["sm3", "sm4"], w=["sm3"])
            self.V(lambda e: e.tensor_scalar(var, var, 64e-5, None, ALU.add), r=["sm3"], w=["sm3"])
            fw.act(var, var, AF.Sqrt, r=["sm3"], w=["sm3"])
            self.V(lambda e: e.reciprocal(var, var), r=["sm3"], w=["sm3"])
            self.V(lambda e: e.tensor_tensor(h3(yc[0:M, :]), h3(yap), bc3(mean, 64), ALU.subtract), r=list(ykeys) + ["sm2"], w=["TP1"])
            self.V(lambda e: e.tensor_tensor(h3(yc[0:M, :]), h3(yc[0:M, :]), bc3(var, 64), ALU.mult), r=["TP1", "sm3"], w=["TP1"])
            self.P(lambda e: e.tensor_tensor(yc[0:M, :], yc[0:M, :], lng[0:M, :], ALU.mult), r=["TP1", "rwkv_ln_g_bc"], w=["TP1"])
            self.P(lambda e: e.tensor_tensor(yc[0:M, :], yc[0:M, :], lnb[0:M, :], ALU.add), r=["TP1", "rwkv_ln_b_bc"], w=["TP1"])
            self.P(lambda e, R=R: e.tensor_tensor(h3(y2[0:M, :]), h3(R.zv[0:M, :]), bc3(R.bon[0:M, :], 64), ALU.mult), r=[K("zv"), K("bon")], w=["TP0"])
            self.V(lambda e: e.tensor_tensor(yc[0:M, :], yc[0:M, :], y2[0:M, :], ALU.add), r=["TP1", "TP0"], w=["TP1"])
            self.V(lambda e: e.tensor_tensor(ob[0:M, :], yc[0:M, :], pg[0:M, :], ALU.mult), r=["TP1", pgk], w=["TP0"])
            pbk, pk = self.pb()
            for j in range(4):
                fw.tr(pbk[:, j * M:(j + 1) * M], ob[0:M, j * 128:(j + 1) * 128], identb[0:M, 0:M], r=["TP0", "identb"], w=[pk])
            fw.act(orT[:, :, 0:M], pbk[:, 0:4 * M].rearrange("p (j t) -> p j t", j=4), AF.Copy, r=[pk], w=["orT"])

        def gate_branch(M, hcur, hk, mdst, mkey):
            for half in range(2):
                pg, pgk = self.pf()
                for q in range(4):
                    dc = half * 4 + q
                    for c in range(8):
                        fw.mm(pg[:, q * M:(q + 1) * M], Wg[:, c, dc * 128:(dc + 1) * 128], hcur(c), c == 0, c == 7, r=[hk, "Wg_%d" % c], w=[pgk])
                fw.act(sgr[:, half * 4:(half + 1) * 4, 0:M], pg[:, 0:4 * M].rearrange("p (q t) -> p q t", q=4), AF.Sigmoid, r=[pgk], w=["sgr%d" % half])
                pbr, pbk_ = self.pf()
                for q in range(4):
                    dc = half * 4 + q
                    for j in range(4):
                        fw.mm(pbr[:, q * M:(q + 1) * M], Wr[:, j, dc * 128:(dc + 1) * 128], orT[:, j, 0:M], j == 0, j == 3, r=["orT", "Wr_%d" % j], w=[pbk_])
                self.V(lambda e, half=half, pbr=pbr: e.tensor_tensor(mdst[:, half * 4:(half + 1) * 4, 0:M], sgr[:, half * 4:(half + 1) * 4, 0:M],
                                                                 pbr[:, 0:4 * M].rearrange("p (q t) -> p q t", q=4), ALU.mult),
                       r=["sgr%d" % half, pbk_], w=[mkey])

        R1 = mkrec(1)
        for j in range(4):
            self.P(lambda e, j=j: e.tensor_copy(R1.G4[j][:, :, 512:640], identb[:, :].unsqueeze(1).to_broadcast([128, 2, 128])),
                   r=["identb"], w=[R1.K("G4_%d" % j)])
        RR = [R0, R1]

        def H1(i):
            R, Rp = RR[i % 2], RR[(i + 1) % 2]
            K = R.K
            hT = R.hT
            xt, xk = self.xt[i % 2], "xt%d" % (i % 2)
            src, _ = self.xsrc(l, i)
            fw.dma(xt[:], src, r=[("xb", i)], w=[xk], key=xk)
            hk = K("hTr")
            if i == 0:
                self.V(lambda e, hT=hT: e.memset(hT[:, :, 0:1], 0.0), w=[hk])
            else:
                self.P(lambda e, hT=hT, hp=Rp.hT: e.tensor_copy(hT[:, :, 0:1], hp[:, :, 128:129]), r=[Rp.K("hTr")], w=[hk])
            self.norm_hT(xt, xk, 128, hT[:, :, 1:129], hk, identb)
            hcur = lambda c, hT=hT: hT[:, c, 1:129]
            hprev = lambda c, hT=hT: hT[:, c, 0:128]
            for g0, dst, dk in [(0, zr, "zr"), (512, zk, "zk"), (1024, R.zv, K("zv"))]:
                ps, pk = tok_proj(128, hcur, hprev, hk, g0, dk)
                fw.act(dst[:, :], ps[:, :], AF.Copy, r=[pk], w=[dk])
            ps, pk = feat_proj(128, hcur, hprev, hk, 1536)
            fw.act(lact[0:64, :], ps[0:64, 0:128], AF.Tanh, r=[pk], w=["lact"])
            fw.act(lact[64:128, :], ps[64:128, 0:128], AF.Copy, r=[pk], w=["lact"])
            ps, pk = feat_proj(128, hcur, hprev, hk, 1664)
            fw.act(R.sgT[:, :], ps[:, 0:128], AF.Sigmoid, r=[pk], w=[K("sgT")])
            if i == NT - 1:
                raw_last(lambda c, hT=hT: hT[:, c, 128:129], hk, 1, O["p_shift"][l:l + 1, :])
            prep(128, False, R)
            stageAB(R)

        def H2(i):
            R = RR[i % 2]
            K = R.K
            stageC(R)
            psY, pyk = stageD(R)
            n_update(R)
            pg, pgk = self.pf()
            fw.mm(pg[:, :], R.sgT[:, :], lg2[:, :], True, True, r=[K("sgT"), "lg2"], w=[pgk])
            post(128, psY[:, :], [pyk], pg, pgk, R)
            m, mk = mrT[0], "mrT0"
            gate_branch(128, lambda c, R=R: R.hT[:, c, 1:129], K("hTr"), m, mk)
            fw.dma(self.mrbuf[i].rearrange("p (c t) -> p c t", c=8), m[:, :, :], r=[mk], w=[("mr", i)], key=mk)

        self.pool = 0
        fw.replay([fw.capture(lambda: H1(0))])
        for i in range(NT):
            logs = []
            if i + 1 < NT:
                self.pool = 0
                logs.append(fw.capture(lambda: H1(i + 1)))
            self.pool = 1
            logs.append(fw.capture(lambda: H2(i)))
            fw.replay(logs, chunk=3)
        self.pool = None
        for j in range(4):
            ps, pk = self.pf()
            fw.tr(ps[:, 0:128], Nst[:, j, :], identf[:, :], r=["Nst", "identf"], w=[pk])
            fw.act(T[0][:, j * 128:(j + 1) * 128], ps[:, 0:128], AF.Copy, r=[pk], w=["T0"])
        for h_ in range(8):
            j, o = h_ // 2, (h_ % 2) * 64
            fw.dma(O["p_wkv"][l, h_], T[0][o:o + 64, j * 128 + o:j * 128 + o + 64], r=["T0"], key="T0")

        self.release(m1)
        RS = NSP()
        RS.zv = sbl("zv_s", [128, RD])
        RS.sgT = sbl("sgT_s", [128, 128], BF)
        RS.bon = sbl("bon_s", [128, 8])
        RS.K = lambda n: n + "#s"
        hTs = sbl("hTs", [128, 8, 80], BF)
        sadd = sbl("sadd", [16, RP])
        stT = sbl("stT", [128, 2, 16])
        zf = sbl("zf", [128, 2, 64])
        QH = sbl("QH", [128, 6, 4, 64])
        Sst = sbl("Sst", [128, 64, 64])
        Stmp = sbl("Stmp", [128, 64, 64])
        sk = sbl("sk", [128, 64])
        yh = sbl("yh", [128, 4, 64])
        ytm = T[7]
        self.V(lambda e: e.memset(hTs[:], 0.0), w=["hTs"])
        i = NT
        xt, xk = self.xt[i % 2], "xt%d" % (i % 2)
        src, _ = self.xsrc(l, i)
        fw.dma(xt[0:MS, :], src, r=[("xb", i)], w=[xk], key=xk)
        self.norm_hT(xt, xk, MS, hTs[:, :, 16:80], "hTs", identb)
        hcur = lambda c: hTs[:, c, 16:80]
        hprev = lambda c: hTs[:, c, 0:64]
        fw.dma(sadd[:, :], I["st_shift"][l], w=["sadd"], key="sadd")
        for q in range(2):
            ps, pk = self.pf()
            fw.tr(ps[:, 0:16], sadd[0:16, 1536 + q * 128:1536 + (q + 1) * 128], identf[0:16, 0:16], r=["sadd", "identf"], w=[pk])
            self.V(lambda e, q=q, ps=ps: e.tensor_scalar(stT[:, q, :], ps[:, 0:16], mucol[:, q:q + 1], None, ALU.mult), r=[pk, "mucol"], w=["stT"])
        for gi, g0 in enumerate(range(0, RP, 512)):
            n = min(512, RP - g0)
            self.bcast_load(T[4 + gi][0:16, 0:n], "T%d" % (4 + gi), I["rwkv_mu"][l, g0:g0 + n])
            self.V(lambda e, gi=gi, g0=g0, n=n: e.tensor_tensor(sadd[:, g0:g0 + n], sadd[:, g0:g0 + n], T[4 + gi][0:16, 0:n], ALU.mult),
                   r=["sadd", "T%d" % (4 + gi)], w=["sadd"])
        zv, sgT = RS.zv, RS.sgT
        for g0, dst, dk in [(0, zr, "zr"), (512, zk, "zk"), (1024, zv, RS.K("zv"))]:
            ps, pk = tok_proj(MS, hcur, hprev, "hTs", g0, dk)
            fw.act(dst[0:MS, :], ps[0:MS, :], AF.Copy, r=[pk], w=[dk])
            self.V(lambda e, dst=dst, g0=g0: e.tensor_tensor(dst[0:16, :], dst[0:16, :], sadd[0:16, g0:g0 + 512], ALU.add), r=[dk, "sadd"], w=[dk])
        for q, g0 in enumerate([1536, 1664]):
            ps, pk = feat_proj(MS, hcur, hprev, "hTs", g0)
            fw.act(zf[:, q, :], ps[:, 0:MS], AF.Copy, r=[pk], w=["zf"])
            self.V(lambda e, q=q: e.tensor_tensor(zf[:, q, 0:16], zf[:, q, 0:16], stT[:, q, :], ALU.add), r=["zf", "stT"], w=["zf"])
        fw.act(lact[0:64, 0:MS], zf[0:64, 0, :], AF.Tanh, r=["zf"], w=["lact"])
        fw.act(lact[64:128, 0:MS], zf[64:128, 0, :], AF.Copy, r=["zf"], w=["lact"])
        fw.act(sgT[:, 0:MS], zf[:, 1, :], AF.Sigmoid, r=["zf"], w=[RS.K("sgT")])
        prep(MS, True, RS)
        if l == 0:
            for nm, ap, k in [("s_zr", zr, "zr"), ("s_zk", zk, "zk"), ("s_zv", zv, "zv"), ("s_dec", T[6], "T6"), ("s_kk", T[2], "T2"),
                              ("s_kf", T[4], "T4"), ("s_be", T[5], "T5"), ("s_a", T[1], "T1")]:
                self.tap(nm, ap[0:MS, :], [k])
        sqv = self.sq.rearrange("x (t q) (h d) -> (q h) x t d", t=4, h=NH)
        for x in range(6):
            fw.dma(QH[:, x, :, :], sqv[:, x, :, :], r=[("sq", x)], w=["QH"], key="QH")
        fw.dma(Sst[:, :, :].rearrange("p v k -> p (v k)"), I["st_wkv"][l], w=["Sst"], key="Sst")
        for t in range(4):
            r_, w_, k_, v_, kk_, b_ = (QH[:, x, t, :] for x in range(6))
            rowb = lambda a: a.unsqueeze(1).to_broadcast([128, 64, 64])
            colb = lambda a: a.unsqueeze(2).to_broadcast([128, 64, 64])
            self.V(lambda e, kk_=kk_: e.tensor_tensor(Stmp[:, :, :], Sst[:, :, :], rowb(kk_), ALU.mult), r=["Sst", "QH"], w=["Stmp"])
            self.V(lambda e: e.tensor_reduce(sk[:, :], Stmp[:, :, :], AX.X, ALU.add), r=["Stmp"], w=["sk"])
            self.P(lambda e, w_=w_: e.tensor_tensor(Sst[:, :, :], Sst[:, :, :], rowb(w_), ALU.mult), r=["Sst", "QH", "Stmp"], w=["Sst"])
            self.V(lambda e, b_=b_: e.tensor_tensor(Stmp[:, :, :], colb(sk[:, :]), rowb(b_), ALU.mult), r=["sk", "QH"], w=["Stmp"])
            self.V(lambda e: e.tensor_tensor(Sst[:, :, :], Sst[:, :, :], Stmp[:, :, :], ALU.subtract), r=["Sst", "Stmp"], w=["Sst"])
            self.P(lambda e, v_=v_, k_=k_: e.tensor_tensor(Stmp[:, :, :], colb(v_), rowb(k_), ALU.mult), r=["QH", "Sst"], w=["Stmp"])
            self.V(lambda e: e.tensor_tensor(Sst[:, :, :], Sst[:, :, :], Stmp[:, :, :], ALU.add), r=["Sst", "Stmp"], w=["Sst"])
            self.P(lambda e, r_=r_: e.tensor_tensor(Stmp[:, :, :], Sst[:, :, :], rowb(r_), ALU.mult), r=["Sst", "QH"], w=["Stmp"])
            self.V(lambda e, t=t: e.tensor_reduce(yh[:, t, :], Stmp[:, :, :], AX.X, ALU.add), r=["Stmp"], w=["yh"])
        fw.dma(O["s_wkv"][l], Sst[:, :, :].rearrange("p v k -> p (v k)"), r=["Sst"], key="Sst")
        if l == 0:
            self.tap("s_QH", QH, ["QH"])
            self.tap("s_yh", yh, ["yh"])
        fw.dma(self.sy.rearrange("(t q) (h d) -> (q h) t d", t=4, h=NH), yh[:, :, :], r=["yh"], w=["sy"], key="yh")
        fw.dma(ytm[0:MS, :], self.sy, r=["sy"], w=["T7"], key="ytm")
        pg, pgk = self.pf()
        fw.mm(pg[0:MS, :], sgT[:, 0:MS], lg2[:, :], True, True, r=[RS.K("sgT"), "lg2"], w=[pgk])
        post(MS, ytm[0:MS, :], ["T7"], pg, pgk, RS)
        m, mk = mrT[0], "mrT0"
        gate_branch(MS, hcur, "hTs", m, mk)
        fw.dma(self.mrbuf[NT].rearrange("p (c t) -> p c t", c=8)[:, :, 0:MS], m[:, :, 0:MS], r=[mk], w=[("mr", NT)], key=mk)
        raw_last(lambda c: hTs[:, c, 64:80], "hTs", 16, O["s_shift"][l])

    def pass_attn(self, l, es2):
        fw, I, O, NT = self.fw, self.I, self.O, self.NT
        sbl = lambda n, s, dt=F32: self.sbl(es2, "a%d_" % l + n, s, dt)
        identb, identf = self.identb, self.identf
        Wq = sbl("Wq", [128, 8, 768], BF)
        Wg = sbl("Wg", [128, 8, D], BF)
        Wa = sbl("Wa", [128, 4, D], BF)
        Wo = sbl("Wo", [128, 8, D], BF)
        self.col_load(self.gcol[:], "gcol", I["norm_mix_g"][l], 8)
        m0 = self.aoff
        self.wstage = [sbl("wst%d" % i_, [128, 2048]) for i_ in range(2)]
        win = I["w_in"][l]
        gsc = lambda c: self.gcol[:, c:c + 1]
        self.prep_w(8, 512, lambda c, s0, n: win[c * 128:(c + 1) * 128, RP:RP + 512],
                    lambda c, s0, n: Wq[:, c, 0:512].rearrange("p (j g d) -> p g j d", j=4, g=2), lambda c: "Wq_%d" % c, "col", gsc,
                    sview=lambda a: a.rearrange("p (g j d) -> p g j d", g=2, j=4))
        self.prep_w(8, 256, lambda c, s0, n: win[c * 128:(c + 1) * 128, RP + 512:RP + 768],
                    lambda c, s0, n: Wq[:, c, 512:768], lambda c: "Wq_%d" % c, "col", gsc)
        self.prep_w(8, D, lambda c, s0, n: win[c * 128:(c + 1) * 128, 3584 + s0:3584 + s0 + n],
                    lambda c, s0, n: Wg[:, c, s0:s0 + n], lambda c: "Wga_%d" % c, "col", gsc)
        wbr = I["w_br_attn"][l]
        self.prep_w(4, D, lambda c, s0, n: wbr[c * 128:(c + 1) * 128, s0:s0 + n],
                    lambda c, s0, n: Wa[:, c, s0:s0 + n], lambda c: "Wa_%d" % c, "plain")
        wo = I["w_out"][l]
        self.prep_w(8, D, lambda c, s0, n: wo[c * 128:(c + 1) * 128, s0:s0 + n],
                    lambda c, s0, n: Wo[:, c, s0:s0 + n], lambda c: "Wo_%d" % c, "plain")
        self.release(m0)
        amask = sbl("amask", [128, 1024])
        fw.dma(amask[:, 0:768], I["c_amask"], w=["amask"], key="amask")
        fw.dma(amask[:, 768:1024], I["c_amask0"], w=["amask"], key="amask")
        smask = sbl("smask", [32, 132])
        fw.dma(smask[:], I["c_smask"], w=["smask"], key="smask")
        sinks = sbl("sinks", [128, NH])
        self.bcast_load(sinks[:], "sinks", I["attn_sinks"][l])
        hT = sbl("hT", [128, 8, 128], BF)
        qkv = sbl("qkv", [128, 768])
        rot = sbl("rot", [128, 640])
        rtmp = [sbl("rtmp%d" % i, [128, 320]) for i in range(2)]
        rotb = sbl("rotb", [128, 640], BF)
        cs = [sbl("cs%d" % i, [128, 64]) for i in range(2)]
        qT = sbl("qT", [128, 4, 128], BF)
        KTr = sbl("KTr", [128, 2, 128], BF)
        Vp = sbl("Vp", [128, 2, 2, 2, 128], BF)
        sc = sbl("sc", [128, 4, 256])
        st = sbl("st", [128, 16])
        pbf = sbl("pbf", [128, 4, 256], BF)
        pT = sbl("pT", [128, 4, 2, 128], BF)
        oT = sbl("oT", [128, 4, 128], BF)
        sga = sbl("sga", [128, 8, 128])
        mrl = [sbl("mrl%d" % i, [128, 8, 128], BF) for i in range(2)]
        mg = sbl("mg", [128, 8, 128], BF)
        xo = [sbl("xo%d" % i, [128, D]) for i in range(2)]
        KA = sbl("KA", [128, NS, 128])
        VA = sbl("VA", [128, NS, 128])
        VAb = sbl("VAb", [128, NS, 128], BF)
        KB = sbl("KB", [4, NS, 128])
        VBt = sbl("VB", [4, NS, 128])
        VBb = sbl("VBb", [4, NS, 128], BF)
        KAT = sbl("KAT", [128, NS, 128], BF)
        KBT = sbl("KBT", [128, NS, 4], BF)
        qbd = sbl("qbd", [128, NS, 32], BF)
        ssc = sbl("ssc", [32, NS, 132])
        sst = sbl("sst", [32, 4 * NS])
        spb = sbl("spb", [32, NS, 132], BF)
        spT = sbl("spT", [128, NS, 32], BF)
        spTB = sbl("spTB", [4, NS, 32], BF)
        oTs = sbl("oTs", [128, 4, MS], BF)

        self.V(lambda e: e.memset(Vp[:], 0.0), w=["Vp0", "Vp1"])
        self.V(lambda e: e.memset(KTr[:], 0.0), w=["KTr0", "KTr1"])
        self.V(lambda e: e.memset(qbd[:], 0.0), w=["qbd"])

        def proj_rope(M, hcur, hk, cosap, sinap, cskey):
            for g0, n in [(0, 512), (512, 256)]:
                ps, pk = self.pf()
                for c in range(8):
                    fw.mm(ps[0:M, 0:n], hcur(c), Wq[:, c, g0:g0 + n], c == 0, c == 7, r=[hk, "Wq_%d" % c], w=[pk])
                fw.act(qkv[0:M, g0:g0 + n], ps[0:M, 0:n], AF.Copy, r=[pk], w=["qkv%d" % (g0 // 512)])
            qk3 = qkv[0:M, 0:640].rearrange("p (h d) -> p h d", h=10)
            r3 = rot[0:M, :].rearrange("p (h d) -> p h d", h=10)
            x1, x2 = qk3[:, :, 0:32], qk3[:, :, 32:64]
            cb = cosap.unsqueeze(1).to_broadcast([M, 10, 32])
            sb_ = sinap.unsqueeze(1).to_broadcast([M, 10, 32])
            ta = rtmp[0][0:M, :].rearrange("p (h d) -> p h d", h=10)
            tb = rtmp[1][0:M, :].rearrange("p (h d) -> p h d", h=10)
            rk = ["qkv0", "qkv1", cskey]
            self.V(lambda e: e.tensor_tensor(ta, x1, cb, ALU.mult), r=rk, w=["rtmp0"])
            self.P(lambda e: e.tensor_tensor(tb, x2, sb_, ALU.mult), r=rk, w=["rtmp1"])
            self.V(lambda e: e.tensor_tensor(r3[:, :, 0:32], ta, tb, ALU.subtract), r=["rtmp0", "rtmp1"], w=["rot"])
            self.V(lambda e: e.tensor_tensor(ta, x2, cb, ALU.mult), r=rk + ["rot"], w=["rtmp0"])
            self.P(lambda e: e.tensor_tensor(tb, x1, sb_, ALU.mult), r=rk + ["rot"], w=["rtmp1"])
            self.V(lambda e: e.tensor_tensor(r3[:, :, 32:64], ta, tb, ALU.add), r=["rtmp0", "rtmp1"], w=["rot"])
            fw.act(rotb[0:M, :], rot[0:M, :], AF.Copy, r=["rot"], w=["rotb"])

        def q_transposes(M, dst, dkey):
            pbk, pk = self.pb()
            for jj in range(4):
                fw.tr(pbk[:, jj * M:(jj + 1) * M], rotb[0:M, jj * 128:(jj + 1) * 128], identb[0:M, 0:M], r=["rotb", "identb"], w=[pk])
            fw.act(dst, pbk[:, 0:4 * M].rearrange("p (j t) -> p j t", j=4), AF.Copy, r=[pk], w=[dkey])

        def gate_out(M, hcur, hk, oTt, okey, mr, mrk, xt, xk, xo_, xok):
            for half in range(2):
                pg, pgk = self.pf()
                for q in range(4):
                    dc = half * 4 + q
                    for c in range(8):
                        fw.mm(pg[:, q * M:(q + 1) * M], Wg[:, c, dc * 128:(dc + 1) * 128], hcur(c), c == 0, c == 7, r=[hk, "Wga_%d" % c], w=[pgk])
                fw.act(sga[:, half * 4:(half + 1) * 4, 0:M], pg[:, 0:4 * M].rearrange("p (q t) -> p q t", q=4), AF.Sigmoid, r=[pgk], w=["sga%d" % half])
                pbr, pbk_ = self.pf()
                for q in range(4):
                    dc = half * 4 + q
                    for cc in range(4):
                        fw.mm(pbr[:, q * M:(q + 1) * M], Wa[:, cc, dc * 128:(dc + 1) * 128], oTt[:, cc, 0:M], cc == 0, cc == 3, r=[okey, "Wa_%d" % cc], w=[pbk_])
                hs = slice(half * 4, (half + 1) * 4)
                self.V(lambda e, hs=hs, pbr=pbr: e.tensor_tensor(sga[:, hs, 0:M], sga[:, hs, 0:M], pbr[:, 0:4 * M].rearrange("p (q t) -> p q t", q=4), ALU.mult),
                       r=["sga%d" % half, pbk_], w=["sga%d" % half])
                self.V(lambda e, hs=hs: e.tensor_tensor(mg[:, hs, 0:M], sga[:, hs, 0:M], mr[:, hs, 0:M], ALU.add), r=["sga%d" % half, mrk], w=["mg%d" % half])
            for grp in range(2):
                px, pxk = self.pf()
                for dc in range(8):
                    fw.mm(px[0:M, :], mg[:, dc, 0:M], Wo[:, dc, grp * 512:(grp + 1) * 512], dc == 0, dc == 7, r=["mg%d" % (dc // 4), "Wo_%d" % dc], w=[pxk])
                self.V(lambda e, grp=grp, px=px: e.tensor_tensor(xo_[0:M, grp * 512:(grp + 1) * 512], xt[0:M, grp * 512:(grp + 1) * 512], px[0:M, :], ALU.add),
                       r=[xk, pxk], w=[xok])

        def put_kv(slot):
            pbk, pk = self.pb()
            fw.tr(pbk[:, 0:128], rotb[:, 512:640], identb[:, :], r=["rotb", "identb"], w=[pk])
            self.V(lambda e, pbk=pbk, slot=slot: e.tensor_copy(KTr[:, slot, :], pbk[:, 0:128]), r=[pk], w=["KTr%d" % slot])
            for g in range(2):
                vsrc = qkv[:, 640 + g * 64:640 + (g + 1) * 64]
                fw.act(Vp[:, slot, g, 0, 0:64], vsrc, AF.Copy, r=["qkv1"], w=["Vp%d" % slot])
                self.P(lambda e, g=g, vsrc=vsrc, slot=slot: e.tensor_copy(Vp[:, slot, g, 1, 64:128], vsrc), r=["qkv1"], w=["Vp%d" % slot])

        xt, xk = self.xt[1], "xt1"
        fw.dma(xt[:], (I["xh0"] if (l == 0 or NSEG == 1) else self.xh_dram), r=["xh_dram"], w=[xk], key=xk)
        fw.dma(cs[1][:, 0:32], I["c_cosh"], w=["cs1"], key="cs1")
        fw.dma(cs[1][:, 32:64], I["c_sinh"], w=["cs1"], key="cs1")
        self.norm_hT(xt, xk, 128, hT[:, :, :], "hT", identb)
        proj_rope(128, lambda c: hT[:, c, :], "hT", cs[1][:, 0:32], cs[1][:, 32:64], "cs1")
        put_kv(1)
        for i in range(NT):
            xt, xk = self.xt[i % 2], "xt%d" % (i % 2)
            src, _ = self.xsrc(l, i)
            fw.dma(xt[:], src, r=[("xb", i)], w=[xk], key=xk)
            mr, mrk = mrl[i % 2], "mrl%d" % (i % 2)
            fw.dma(mr[:, :, :], self.mrbuf[i].rearrange("p (c t) -> p c t", c=8), r=[("mr", i)], w=[mrk], key=mrk)
            ck_ = "cs%d" % (i % 2)
            fw.dma(cs[i % 2][:, 0:32], I["c_cosp"][i * 128:(i + 1) * 128, :], w=[ck_], key=ck_)
            fw.dma(cs[i % 2][:, 32:64], I["c_sinp"][i * 128:(i + 1) * 128, :], w=[ck_], key=ck_)
            self.norm_hT(xt, xk, 128, hT[:, :, :], "hT", identb)
            hcur = lambda c: hT[:, c, :]
            proj_rope(128, hcur, "hT", cs[i % 2][:, 0:32], cs[i % 2][:, 32:64], ck_)
            slot = i % 2
            if i == NT - 1:
                fw.dma(O["p_k"][l], rot[:, 512:640], r=["rot"], key="rot")
                fw.dma(O["p_v"][l], qkv[:, 640:768], r=["qkv1"], key="qkv1")
            q_transposes(128, qT[:, :, :], "qT")
            put_kv(slot)
            mvar = 3 if i == 0 else slot
            msk = amask[:, mvar * 256:(mvar + 1) * 256].unsqueeze(1).to_broadcast([128, 4, 256])
            for g in range(2):
                o = g * 64
                pS = []
                for jj in range(4):
                    if jj % 2 == 0:
                        ps, pk = self.pf()
                        pS.append((ps, pk))
                    fw.mm(ps[:, (jj % 2) * 256:(jj % 2 + 1) * 256], qT[o:o + 64, jj, :], KTr[o:o + 64, :, :].rearrange("p s t -> p (s t)"),
                          True, True, r=["qT", "KTr0", "KTr1"], w=[pk])
                for half, (ps, pk) in enumerate(pS):
                    self.V(lambda e, ps=ps, half=half, msk=msk: e.scalar_tensor_tensor(
                        sc[:, half * 2:(half + 1) * 2, :], ps[:, :].rearrange("p (j c) -> p j c", j=2), 0.125,
                        msk[:, 0:2, :], ALU.mult, ALU.add), r=[pk, "amask"], w=["sc%d" % half])
                sck = ["sc0", "sc1"]
                self.V(lambda e: e.tensor_reduce(st[:, 0:4], sc[:, :, :], AX.X, ALU.max), r=sck, w=["st"])
                self.V(lambda e, g=g: e.tensor_tensor(st[:, 0:4], st[:, 0:4], sinks[:, g * 4:(g + 1) * 4], ALU.max), r=["st", "sinks"], w=["st"])
                self.V(lambda e: e.tensor_tensor(sc[:, :, :], sc[:, :, :], bc3(st[:, 0:4], 256), ALU.subtract), r=sck + ["st"], w=sck)
                fw.act(sc[:, :, :], sc[:, :, :], AF.Exp, r=sck, w=sck)
                self.V(lambda e: e.tensor_reduce(st[:, 4:8], sc[:, :, :], AX.X, ALU.add), r=sck, w=["st2"])
                self.V(lambda e, g=g: e.tensor_tensor(st[:, 8:12], sinks[:, g * 4:(g + 1) * 4], st[:, 0:4], ALU.subtract), r=["st", "sinks"], w=["st3"])
                fw.act(st[:, 8:12], st[:, 8:12], AF.Exp, r=["st3"], w=["st3"])
                self.V(lambda e: e.tensor_tensor(st[:, 4:8], st[:, 4:8], st[:, 8:12], ALU.add), r=["st2", "st3"], w=["st2"])
                self.V(lambda e: e.reciprocal(st[:, 4:8], st[:, 4:8]), r=["st2"], w=["st2"])
                self.V(lambda e: e.tensor_tensor(pbf[:, :, :], sc[:, :, :], bc3(st[:, 4:8], 256), ALU.mult), r=sck + ["st2"], w=["pbf"])
                pbk, pk = self.pb()
                for jj in range(4):
                    for s_ in range(2):
                        fw.tr(pbk[:, (jj * 2 + s_) * 128:(jj * 2 + s_ + 1) * 128], pbf[:, jj, s_ * 128:(s_ + 1) * 128], identb[:, :], r=["pbf", "identb"], w=[pk])
                fw.act(pT[:, :, :, :], pbk[:, :].rearrange("p (j s t) -> p j s t", j=4, s=2), AF.Copy, r=[pk], w=["pT"])
                if g == 0:
                    pO, pok = self.pf()
                for c2 in range(2):
                    cc = g * 2 + c2
                    n = 0
                    for par in range(2):
                        jj = c2 * 2 + par
                        for s_ in range(2):
                            fw.mm(pO[:, cc * 128:(cc + 1) * 128], Vp[:, s_, g, par, :], pT[:, jj, s_, :], n == 0, n == 3,
                                  r=["Vp0", "Vp1", "pT"], w=[pok])
                            n += 1
            fw.act(oT[:, :, :], pO[:, :].rearrange("p (c t) -> p c t", c=4), AF.Copy, r=[pok], w=["oT"])
            xo_, xok = xo[i % 2], "xo%d" % (i % 2)
            gate_out(128, hcur, "hT", oT, "oT", mr, mrk, xt, xk, xo_, xok)
            fw.dma(self.xbuf[i * 128:(i + 1) * 128, :], xo_[:, :], r=[xok], w=[("xb", i)], key=xok)
        if NSEG > 1:
            self.gather_select(xo_[:, :], [xok], D, self.agX_in, self.agX_out, "agX")
            fw.dma(self.xh_dram, xo_[:, :], r=[xok], w=["xh_dram"], key="xhst")

        i = NT
        xt, xk = self.xt[i % 2], "xt%d" % (i % 2)
        src, _ = self.xsrc(l, i)
        fw.dma(xt[0:MS, :], src, r=[("xb", i)], w=[xk], key=xk)
        mr, mrk = mrl[i % 2], "mrl%d" % (i % 2)
        fw.dma(mr[:, :, 0:MS], self.mrbuf[NT].rearrange("p (c t) -> p c t", c=8)[:, :, 0:MS], r=[("mr", NT)], w=[mrk], key=mrk)
        ck_ = "cs%d" % (i % 2)
        fw.dma(cs[i % 2][0:MS, 0:32], I["c_coss"], w=[ck_], key=ck_)
        fw.dma(cs[i % 2][0:MS, 32:64], I["c_sins"], w=[ck_], key=ck_)
        self.norm_hT(xt, xk, MS, hT[:, :, 0:MS], "hT", identb)
        hcur = lambda c: hT[:, c, 0:MS]
        proj_rope(MS, hcur, "hT", cs[i % 2][0:MS, 0:32], cs[i % 2][0:MS, 32:64], ck_)
        for (cin, cout, srcap, srck, dkey) in [("ck", "s_k", rot[:, 512:640], "rot", "sk"), ("cv", "s_v", qkv[:, 640:768], "qkv1", "sv")]:
            fw.dma(O[cout][l, :, 0:124, :], I[cin][l, :, 4:128, :], w=[dkey], key=dkey + "c")
            for t in range(4):
                fw.dma(O[cout][l, :, 124 + t, :], srcap[t * 16:(t + 1) * 16, :], r=[srck], w=[dkey], key=dkey + "n")
        fw.dma(KA[:, :, :], O["s_k"][l].rearrange("q p c -> p q c"), r=["sk"], w=["KA"], key="KA")
        fw.dma(VA[:, :, :], O["s_v"][l].rearrange("q p c -> p q c"), r=["sv"], w=["VA"], key="VA")
        fw.dma(KB[:, :, :], I["ck"][l, :, 0:4, :].rearrange("q p c -> p q c"), w=["KB"], key="KB")
        fw.dma(VBt[:, :, :], I["cv"][l, :, 0:4, :].rearrange("q p c -> p q c"), w=["VB"], key="VB")
        self.P(lambda e: e.tensor_copy(VAb[:, :, :], VA[:, :, :]), r=["VA"], w=["VAb"])
        self.P(lambda e: e.tensor_copy(VBb[:, :, :], VBt[:, :, :]), r=["VB"], w=["VBb"])
        for q4 in range(4):
            ps, pk = self.pf()
            for qq in range(4):
                q = q4 * 4 + qq
                fw.tr(ps[:, qq * 128:(qq + 1) * 128], KA[:, q, :], identf[:, :], r=["KA", "identf"], w=[pk])
            fw.act(KAT[:, q4 * 4:(q4 + 1) * 4, :], ps[:, :].rearrange("p (q t) -> p q t", q=4), AF.Copy, r=[pk], w=["KAT"])
        ps, pk = self.pf()
        for q in range(NS):
            fw.tr(ps[:, q * 4:(q + 1) * 4], KB[0:4, q, :], identf[0:4, 0:4], r=["KB", "identf"], w=[pk])
        fw.act(KBT[:, :, :], ps[:, 0:64].rearrange("p (q t) -> p q t", q=NS), AF.Copy, r=[pk], w=["KBT"])
        q_transposes(MS, qT[:, :, 0:MS], "qT")
        for g in range(2):
            for jj in range(4):
                o = g * 64
                dst = qbd[o:o + 64, :, g * 16 + jj * 4:g * 16 + (jj + 1) * 4]
                srcq = qT[o:o + 64, jj, 0:MS].rearrange("p (t q) -> p q t", t=4)
                self.V(lambda e, dst=dst, srcq=srcq: e.tensor_copy(dst, srcq), r=["qT"], w=["qbd"])
        pSA = []
        for q4 in range(4):
            ps, pk = self.pf()
            pSA.append((ps, pk))
            for qq in range(4):
                q = q4 * 4 + qq
                fw.mm(ps[0:32, qq * 128:(qq + 1) * 128], qbd[:, q, :], KAT[:, q, :], True, True, r=["qbd", "KAT"], w=[pk])
        psB, pkB = self.pf()
        for q in range(NS):
            fw.mm(psB[0:32, q * 4:(q + 1) * 4], qbd[:, q, :], KBT[:, q, :], True, True, r=["qbd", "KBT"], w=[pkB])
        for q4, (ps, pk) in enumerate(pSA):
            self.V(lambda e, q4=q4, ps=ps: e.scalar_tensor_tensor(
                ssc[:, q4 * 4:(q4 + 1) * 4, 0:128], ps[0:32, :].rearrange("p (q c) -> p q c", q=4), 0.125,
                smask[:, 0:128].unsqueeze(1).to_broadcast([32, 4, 128]), ALU.mult, ALU.add), r=[pk, "smask"], w=["ssc"])
        self.V(lambda e: e.scalar_tensor_tensor(
            ssc[:, :, 128:132], psB[0:32, 0:64].rearrange("p (q c) -> p q c", q=NS), 0.125,
            smask[:, 128:132].unsqueeze(1).to_broadcast([32, NS, 4]), ALU.mult, ALU.add), r=[pkB, "smask"], w=["ssc"])
        sinkc = sbl("sinkc", [32, 1])
        for g in range(2):
            for jj in range(4):
                p0 = g * 16 + jj * 4
                fw.dma(sinkc[p0:p0 + 4, :], I["attn_sinks"][l, g * 4 + jj:g * 4 + jj + 1].partition_broadcast(4), w=["sinkc"], key="sinkc")
        self.V(lambda e: e.tensor_reduce(sst[:, 0:NS], ssc[:, :, :], AX.X, ALU.max), r=["ssc"], w=["sst"])
        self.V(lambda e: e.tensor_scalar(sst[:, 0:NS], sst[:, 0:NS], sinkc[:, 0:1], None, ALU.max), r=["sst", "sinkc"], w=["sst"])
        self.V(lambda e: e.tensor_tensor(ssc[:, :, :], ssc[:, :, :], bc3(sst[:, 0:NS], 132), ALU.subtract), r=["ssc", "sst"], w=["ssc"])
        fw.act(ssc[:, :, :], ssc[:, :, :], AF.Exp, r=["ssc"], w=["ssc"])
        self.V(lambda e: e.tensor_reduce(sst[:, NS:2 * NS], ssc[:, :, :], AX.X, ALU.add), r=["ssc"], w=["sst2"])
        self.V(lambda e: e.tensor_scalar(sst[:, 2 * NS:3 * NS], sst[:, 0:NS], sinkc[:, 0:1], None, ALU.subtract), r=["sst", "sinkc"], w=["sst3"])
        fw.act(sst[:, 2 * NS:3 * NS], sst[:, 2 * NS:3 * NS], AF.Exp, r=["sst3"], w=["sst3"], scale=-1.0)
        self.V(lambda e: e.tensor_tensor(sst[:, NS:2 * NS], sst[:, NS:2 * NS], sst[:, 2 * NS:3 * NS], ALU.add), r=["sst2", "sst3"], w=["sst2"])
        self.V(lambda e: e.reciprocal(sst[:, NS:2 * NS], sst[:, NS:2 * NS]), r=["sst2"], w=["sst2"])
        self.V(lambda e: e.tensor_tensor(spb[:, :, :], ssc[:, :, :], bc3(sst[:, NS:2 * NS], 132), ALU.mult), r=["ssc", "sst2"], w=["spb"])
        identb32 = identb[0:32, 0:32]
        for q8 in range(2):
            pbk, pk = self.pb()
            for qq in range(8):
                q = q8 * 8 + qq
                fw.tr(pbk[:, qq * 32:(qq + 1) * 32], spb[:, q, 0:128], identb32, r=["spb", "identb"], w=[pk])
            fw.act(spT[:, q8 * 8:(q8 + 1) * 8, :], pbk[:, 0:256].rearrange("p (q c) -> p q c", q=8), AF.Copy, r=[pk], w=["spT"])
        pbk, pk = self.pb()
        for q in range(NS):
            fw.tr(pbk[0:4, q * 32:(q + 1) * 32], spb[:, q, 128:132], identb32, r=["spb", "identb"], w=[pk])
        fw.act(spTB[:, :, :], pbk[0:4, 0:512].rearrange("p (q c) -> p q c", q=NS), AF.Copy, r=[pk], w=["spTB"])
        pO, pok = self.pf()
        for q in range(NS):
            fw.mm(pO[:, q * 32:(q + 1) * 32], VAb[:, q, :], spT[:, q, :], True, False, r=["VAb", "spT"], w=[pok])
            fw.mm(pO[:, q * 32:(q + 1) * 32], VBb[0:4, q, :], spTB[0:4, q, :], False, True, r=["VBb", "spTB"], w=[pok])
        oraw = sbl("oraw", [128, 32, NS], BF)
        fw.act(oraw.rearrange("p c q -> p q c"), pO[:, :].rearrange("p (q c) -> p q c", q=NS), AF.Copy, r=[pok], w=["oraw"])
        for g in range(2):
            for jj in range(4):
                cc, par = g * 2 + jj // 2, jj % 2
                c0 = g * 16 + jj * 4
                srco = oraw[g * 64:(g + 1) * 64, c0:c0 + 4, :].rearrange("p t q -> p (t q)")
                fw.dma(oTs[par * 64:(par + 1) * 64, cc, :], srco, r=["oraw"], w=["oTs"], key="oTs")
        xo_, xok = xo[i % 2], "xo%d" % (i % 2)
        gate_out(MS, hcur, "hT", oTs, "oTs", mr, mrk, xt, xk, xo_, xok)
        fw.dma(self.xsbuf, xo_[0:MS, :], r=[xok], w=[("xb", NT)], key=xok)

    def pass_ffn(self, l, es2):
        fw, I, O, NT = self.fw, self.I, self.O, self.NT
        sbl = lambda n, s, dt=F32: self.sbl(es2, "f%d_" % l + n, s, dt)
        identb, identf = self.identb, self.identf
        Wc = sbl("Wc", [128, 8, DFF], BF)
        Wu = sbl("Wu", [128, 8, DFF], BF)
        Wd = sbl("Wd", [128, NFC, D], BF)
        self.col_load(self.gcol[:], "gcol", I["norm_ffn_g"][l], 8)
        cw = sbl("cw", [128, 4, NFC])
        for j in range(3):
            self.col_load(cw[:, j, :], "cw", I["ffn_conv_w"][l, j], NFC)
        self.col_load(cw[:, 3, :], "cw", I["ffn_conv_b"][l], NFC)
        m0 = self.aoff
        self.wstage = [sbl("wst%d" % i_, [128, 2048]) for i_ in range(2)]
        wi = I["ffn_w_in"][l]
        gsc = lambda c: self.gcol[:, c:c + 1]
        self.prep_w(8, DFF, lambda c, s0, n: wi[c * 128:(c + 1) * 128, s0:s0 + n],
                    lambda c, s0, n: Wc[:, c, s0:s0 + n], lambda c: "Wc_%d" % c, "col", gsc)
        self.prep_w(8, DFF, lambda c, s0, n: wi[c * 128:(c + 1) * 128, DFF + s0:DFF + s0 + n],
                    lambda c, s0, n: Wu[:, c, s0:s0 + n], lambda c: "Wu_%d" % c, "col", gsc)
        wd = I["ffn_w_down"][l]
        self.prep_w(NFC, D, lambda c, s0, n: wd[c * 128:(c + 1) * 128, s0:s0 + n],
                    lambda c, s0, n: Wd[:, c, s0:s0 + n], lambda c: "Wd_%d" % c, "plain")
        self.release(m0)
        last = (l == 1)
        if last:
            gf = sbl("gf", [128, D])
            self.bcast_load(gf[:], "gf", I["norm_final_g"])
        hT = sbl("hT", [128, 8, 128], BF)
        cxf = sbl("cx", [128, NFC * 130])
        cx1 = cxf.rearrange("p (f t) -> p f t", f=NFC)
        cxs = cxf[:, 0:NFC * NS * 6].rearrange("p (f q j) -> p f q j", f=NFC, q=NS)
        acc = [sbl("acc%d" % i_, [128, 4, 128]) for i_ in range(2)]
        aT = sbl("aT", [128, NFC, 128], BF)
        xo = [sbl("xo%d" % i_, [128, D]) for i_ in range(2)]
        ctok = sbl("ctok", [128, DFF])
        cst = ctok
        jk = self.xn

        def finish(M, xt, xk, xo_, xok, dst_final, dst_x, dkey):
            for grp in range(2):
                px, pxk = self.pf()
                for fc in range(NFC):
                    fw.mm(px[0:M, :], aT[:, fc, 0:M], Wd[:, fc, grp * 512:(grp + 1) * 512], fc == 0, fc == NFC - 1, r=["aT", "Wd_%d" % fc], w=[pxk])
                self.V(lambda e, grp=grp, px=px: e.tensor_tensor(xo_[0:M, grp * 512:(grp + 1) * 512], xt[0:M, grp * 512:(grp + 1) * 512], px[0:M, :], ALU.add),
                       r=[xk, pxk], w=[xok])
            if not last:
                fw.dma(dst_x, xo_[0:M, :], r=[xok], w=[dkey], key=xok)
                return
            ss, t1 = self.ss, self.t1
            fw.act(jk[0:M, :], xo_[0:M, :], AF.Square, r=[xok], w=["xn", "ss"], accum_out=ss[0:M, :])
            self.V(lambda e: e.tensor_scalar(t1[0:M, :], ss[0:M, :], 1.0 / D, 1e-6, ALU.mult, ALU.add), r=["ss"], w=["t1"])
            fw.act(t1[0:M, :], t1[0:M, :], AF.Sqrt, r=["t1"], w=["t1"])
            self.V(lambda e: e.reciprocal(t1[0:M, :], t1[0:M, :]), r=["t1"], w=["t1"])
            self.V(lambda e: e.scalar_tensor_tensor(xo_[0:M, :], xo_[0:M, :], t1[0:M, 0:1], gf[0:M, :], ALU.mult, ALU.mult),
                   r=[xok, "t1", "gf"], w=[xok])
            fw.dma(dst_final, xo_[0:M, :], r=[xok], key=xok)

        def ffn_core(M, hcur, hk, cview, ckey, sample):
            for b0 in range(0, NFC, 4):
                nb = min(4, NFC - b0)
                pc, pck = self.pf()
                for q in range(nb):
                    fc = b0 + q
                    for c in range(8):
                        fw.mm(pc[:, q * M:(q + 1) * M], Wc[:, c, fc * 128:(fc + 1) * 128], hcur(c), c == 0, c == 7, r=[hk, "Wc_%d" % c], w=[pck])
                pu, puk = self.pf()
                for q in range(nb):
                    fc = b0 + q
                    for c in range(8):
                        fw.mm(pu[:, q * M:(q + 1) * M], Wu[:, c, fc * 128:(fc + 1) * 128], hcur(c), c == 0, c == 7, r=[hk, "Wu_%d" % c], w=[puk])
                if sample:
                    fw.act(cview[:, b0:b0 + nb, :, 2:6], pc[:, 0:nb * M].rearrange("p (f t q) -> p f q t", f=nb, t=4), AF.Copy, r=[pck], w=[ckey])
                else:
                    fw.act(cview[:, b0:b0 + nb, 2:130], pc[:, 0:nb * M].rearrange("p (f t) -> p f t", f=nb), AF.Copy, r=[pck], w=[ckey])
                a_ = acc[(b0 // 4) % 2]
                ak = "acc%d" % ((b0 // 4) % 2)
                for q in range(nb):
                    fc = b0 + q
                    if sample:
                        c0, c1, c2 = (cview[:, fc, :, s_:s_ + 4] for s_ in range(3))
                        av = a_[:, q, 0:M].rearrange("p (t q) -> p q t", t=4)
                    else:
                        c0, c1, c2 = (cview[:, fc, s_:s_ + 128] for s_ in range(3))
                        av = a_[:, q, :]
                    self.P(lambda e, av=av, c0=c0, fc=fc: e.tensor_scalar(av, c0, cw[:, 0, fc:fc + 1], cw[:, 3, fc:fc + 1], ALU.mult, ALU.add),
                           r=[ckey, "cw"], w=[ak])
                    self.V(lambda e, av=av, c1=c1, fc=fc: e.scalar_tensor_tensor(av, c1, cw[:, 1, fc:fc + 1], av, ALU.mult, ALU.add),
                           r=[ckey, "cw", ak], w=[ak])
                    self.V(lambda e, av=av, c2=c2, fc=fc: e.scalar_tensor_tensor(av, c2, cw[:, 2, fc:fc + 1], av, ALU.mult, ALU.add),
                           r=[ckey, "cw", ak], w=[ak])
                fw.act(a_[:, 0:nb, 0:M], a_[:, 0:nb, 0:M], AF.Gelu, r=[ak], w=[ak])
                self.V(lambda e, a_=a_, pu=pu, nb=nb, b0=b0: e.tensor_tensor(aT[:, b0:b0 + nb, 0:M], a_[:, 0:nb, 0:M],
                                                                       pu[:, 0:nb * M].rearrange("p (f t) -> p f t", f=nb), ALU.mult),
                       r=[ak, puk], w=["aT"])

        def c_token_major(M, hcur, hk, rows, dsts):
            for g0 in range(0, DFF, 512):
                n = min(512, DFF - g0)
                ps, pk = self.pf()
                for c in range(8):
                    fw.mm(ps[0:M, 0:n], hcur(c), Wc[:, c, g0:g0 + n], c == 0, c == 7, r=[hk, "Wc_%d" % c], w=[pk])
                fw.act(ctok[0:M, g0:g0 + n], ps[0:M, 0:n], AF.Copy, r=[pk], w=["ctok"])
            for (r0, r1), dst in zip(rows, dsts):
                fw.dma(dst, ctok[r0:r1, :], r=["ctok"], key="ctok")

        xt, xk = self.xt[1], "xt1"
        fw.dma(xt[:], (I["xh0"] if NSEG == 1 else self.xh_dram), r=["xh_dram"], w=[xk], key=xk)
        self.norm_hT(xt, xk, 128, hT[:, :, :], "hT", identb)
        pc, pck = self.pf()
        for fc in range(NFC):
            for c in range(8):
                fw.mm(pc[:, fc * 2:(fc + 1) * 2], Wc[:, c, fc * 128:(fc + 1) * 128], hT[:, c, 126:128], c == 0, c == 7, r=["hT", "Wc_%d" % c], w=[pck])
        fw.act(cx1[:, :, 0:2], pc[:, 0:2 * NFC].rearrange("p (f t) -> p f t", f=NFC), AF.Copy, r=[pck], w=["cx"])
        for i in range(NT):
            xt, xk = self.xt[i % 2], "xt%d" % (i % 2)
            fw.dma(xt[:], self.xbuf[i * 128:(i + 1) * 128, :], r=[("xb", i)], w=[xk], key=xk)
            self.norm_hT(xt, xk, 128, hT[:, :, :], "hT", identb)
            hcur = lambda c: hT[:, c, :]
            cv_, ckey = cx1, "cx"
            if i > 0:
                self.P(lambda e: e.tensor_copy(acc[0][:, 0, 0:2 * NFC].rearrange("p (f t) -> p f t", f=NFC), cx1[:, :, 128:130]), r=[ckey], w=["acc0"])
                self.P(lambda e: e.tensor_copy(cx1[:, :, 0:2], acc[0][:, 0, 0:2 * NFC].rearrange("p (f t) -> p f t", f=NFC)), r=["acc0"], w=[ckey])
            ffn_core(128, hcur, "hT", cv_, ckey, False)
            if i == NT - 1:
                c_token_major(128, hcur, "hT", [(126, 128)], [O["p_conv"][l]])
            xo_, xok = xo[i % 2], "xo%d" % (i % 2)
            finish(128, xt, xk, xo_, xok, O["yp"][i * 128:(i + 1) * 128, :], self.xbuf[i * 128:(i + 1) * 128, :], ("xb", i))
        if not last and NSEG > 1:
            self.gather_select(xo_[:, :], [xok], D, self.agX_in, self.agX_out, "agX")
            fw.dma(self.xh_dram, xo_[:, :], r=[xok], w=["xh_dram"], key="xhst")

        i = NT
        xt, xk = self.xt[i % 2], "xt%d" % (i % 2)
        fw.dma(xt[0:MS, :], self.xsbuf, r=[("xb", i)], w=[xk], key=xk)
        self.norm_hT(xt, xk, MS, hT[:, :, 0:MS], "hT", identb)
        hcur = lambda c: hT[:, c, 0:MS]
        fw.dma(cst[0:32, :], I["st_conv"][l], w=["ctok"], key="cst")
        for b0 in range(0, NFC, 4):
            nb = min(4, NFC - b0)
            ps, pk = self.pf()
            for q in range(nb):
                fc = b0 + q
                fw.tr(ps[:, q * 32:(q + 1) * 32], cst[0:32, fc * 128:(fc + 1) * 128], identf[0:32, 0:32], r=["ctok", "identf"], w=[pk])
            fw.act(cxs[:, b0:b0 + nb, :, 0:2], ps[:, 0:nb * 32].rearrange("p (f q j) -> p f q j", f=nb, j=2), AF.Copy, r=[pk], w=["cx"])
        ffn_core(MS, hcur, "hT", cxs, "cx", True)
        sc_ = O["s_conv"][l].rearrange("(q j) f -> j q f", j=2)
        c_token_major(MS, hcur, "hT", [(32, 48), (48, 64)], [sc_[0], sc_[1]])
        xo_, xok = xo[i % 2], "xo%d" % (i % 2)
        finish(MS, xt, xk, xo_, xok, O["ys"], self.xsbuf, ("xb", NT))


NSEG = 1


def _consts_shared():
    c = {}
    c["c_ident"] = np.eye(128, dtype=np.float32)
    inv = (10000.0 ** (-np.arange(0, HD, 2, dtype=np.float32) / HD)).astype(np.float32)
    pos_s = (PAST + np.repeat(np.arange(4), NS)).astype(np.float32)
    ang_s = pos_s[:, None] * inv[None, :]
    c["c_coss"] = np.cos(ang_s).astype(np.float32)
    c["c_sins"] = np.sin(ang_s).astype(np.float32)
    s = np.arange(128)[:, None]
    t = np.arange(128)[None, :]
    incl = (s <= t).astype(np.float32)
    strict = (s < t).astype(np.float32)
    c["c_tri"] = np.concatenate([incl * CDEC, strict * CDEC], 1).astype(np.float32)
    c["c_mask2"] = np.concatenate([incl, strict], 1).astype(np.float32)
    c["c_maskL"] = (s > t).astype(np.float32)
    i_ = np.arange(128)[:, None]
    j_ = np.arange(128)[None, :]
    cur = np.where(j_ <= i_, 0.0, NEG)
    prev = np.where(j_ > i_, 0.0, NEG)
    dead = np.full((128, 128), NEG)
    c["c_amask"] = np.concatenate([cur, prev, prev, cur, cur, dead], 1).astype(np.float32)
    c["_am_first"] = np.concatenate([cur, dead], 1).astype(np.float32)
    c["_am_mid"] = np.concatenate([cur, prev], 1).astype(np.float32)
    tt = (np.arange(32) % 4)[:, None]
    ia = np.arange(128)[None, :]
    ma = np.where(ia <= 124 + tt, 0.0, NEG)
    rb = np.arange(4)[None, :]
    mb = np.where(rb > tt, 0.0, NEG)
    c["c_smask"] = np.concatenate([ma, mb], 1).astype(np.float32)
    last = np.zeros((128, 1), np.float32)
    last[127, 0] = 1.0
    c["c_last"] = last
    c["_inv"] = inv
    return c


def _rope_tab(pos, inv):
    ang = pos.astype(np.float32)[:, None] * inv[None, :]
    return np.cos(ang).astype(np.float32), np.sin(ang).astype(np.float32)


_CACHE = {}
TAPS = False
TAP_OUT = {}


def kernel(**inp):
    inp = {k: np.asarray(v) for k, v in inp.items()}
    xp_all = inp["x_prompt"].astype(np.float32)
    B, SEQ_, _ = xp_all.shape
    TPC = SEQ_ // NSEG
    if TPC not in _CACHE:
        b_ = Builder(TPC, taps=TAPS)
        _CACHE[TPC] = (b_.build(), b_.tapnames)
    nc, tapnames = _CACHE[TPC]
    consts = _consts_shared()
    inv = consts.pop("_inv")
    am_first, am_mid = consts.pop("_am_first"), consts.pop("_am_mid")
    wnames = ["norm_mix_g", "w_in", "rwkv_mu", "rwkv_w0", "rwkv_w2", "rwkv_a0", "rwkv_a2", "rwkv_g2", "rwkv_k_k",
              "rwkv_k_a", "rwkv_ln_g", "rwkv_ln_b", "attn_sinks", "w_br_rwkv", "w_br_attn", "w_out", "norm_ffn_g",
              "ffn_w_in", "ffn_conv_w", "ffn_conv_b", "ffn_w_down", "norm_final_g"]
    shared = {n: np.ascontiguousarray(inp[n], dtype=np.float32) for n in wnames}
    shared["rwkv_r_k"] = np.ascontiguousarray(inp["rwkv_r_k"], dtype=np.float32).reshape(2, RD)
    shared.update(consts)
    in_maps = []
    ncores = 8
    for c in range(ncores):
        b, seg = (c // NSEG) % B, c % NSEG
        sl = slice(c * NS, (c + 1) * NS)
        m = dict(shared)
        t0 = seg * TPC
        m["xp"] = np.ascontiguousarray(xp_all[b, t0:t0 + TPC])
        m["xh0"] = np.ascontiguousarray(xp_all[b, t0 - 128:t0]) if seg > 0 else np.zeros((128, D), np.float32)
        m["c_cosp"], m["c_sinp"] = _rope_tab(t0 + np.arange(TPC), inv)
        m["c_cosh"], m["c_sinh"] = _rope_tab(np.maximum(t0 - 128 + np.arange(128), 0), inv)
        m["c_amask0"] = am_mid if seg > 0 else am_first
        sel = np.zeros((128, 8), np.float32)
        if seg > 0:
            sel[:, c - 1] = 1.0
        m["c_sel"] = sel
        m["xs"] = np.ascontiguousarray(inp["x_sample"][sl].transpose(1, 0, 2).reshape(MS, D))
        m["st_shift"] = np.ascontiguousarray(inp["state_rwkv_shift"][:, sl])
        m["st_wkv"] = np.ascontiguousarray(inp["state_rwkv_wkv"][:, sl]).reshape(2, 128, 4096)
        m["ck"] = np.ascontiguousarray(inp["cache_swa_k"][:, sl]).reshape(2, NS, 128, 128)
        m["cv"] = np.ascontiguousarray(inp["cache_swa_v"][:, sl]).reshape(2, NS, 128, 128)
        m["st_conv"] = np.ascontiguousarray(inp["state_ffn_conv"][:, sl]).reshape(2, 2 * NS, DFF)
        in_maps.append(m)
    res = run_bass_kernel_spmd(nc, in_maps, core_ids=list(range(ncores)))
    R = res.results
    for tn in tapnames:
        TAP_OUT[tn] = [np.asarray(R[c][tn]) for c in range(ncores)]
    f = np.float32
    lastc = [b * NSEG + NSEG - 1 for b in range(B)]
    y_prompt = np.stack([np.concatenate([R[b * NSEG + sg]["yp"] for sg in range(NSEG)], 0) for b in range(B)]).astype(f)
    y_sample = np.concatenate([R[c]["ys"].reshape(4, NS, D).transpose(1, 0, 2) for c in range(ncores)], 0).astype(f)
    p_shift = np.stack([R[c]["p_shift"] for c in lastc], 1).astype(f)
    p_wkv = np.stack([R[c]["p_wkv"] for c in lastc], 1).astype(f)
    p_k = np.stack([R[c]["p_k"] for c in lastc], 1).reshape(2, B, 128, 2, 64).astype(f)
    p_v = np.stack([R[c]["p_v"] for c in lastc], 1).reshape(2, B, 128, 2, 64).astype(f)
    p_conv = np.stack([R[c]["p_conv"] for c in lastc], 1).astype(f)
    s_shift = np.concatenate([R[c]["s_shift"] for c in range(ncores)], 1).astype(f)
    s_wkv = np.concatenate([R[c]["s_wkv"].reshape(2, NS, NH, 64, 64) for c in range(ncores)], 1).astype(f)
    s_k = np.concatenate([R[c]["s_k"].reshape(2, NS, 128, 2, 64) for c in range(ncores)], 1).astype(f)
    s_v = np.concatenate([R[c]["s_v"].reshape(2, NS, 128, 2, 64) for c in range(ncores)], 1).astype(f)
    s_conv = np.concatenate([R[c]["s_conv"].reshape(2, NS, 2, DFF) for c in range(ncores)], 1).astype(f)
    return (y_prompt, y_sample, p_shift, p_wkv, p_k, p_v, p_conv, s_shift, s_wkv, s_k, s_v, s_conv)
```

```python
import math
from contextlib import ExitStack

import numpy as np
import concourse.bass as bass
import concourse.mybir as mybir
from concourse.bass_utils import run_bass_kernel_spmd

F32 = mybir.dt.float32
BF = mybir.dt.bfloat16
AF = mybir.ActivationFunctionType
ALU = mybir.AluOpType
AX = mybir.AxisListType

ENGS = ["sp", "pe", "act", "dve", "pool"]
DEBUG_WHERE = True

D = 1024
HD = 64
NH = 8
RD = 512
RP = 1792
INP = 4608
DFF = 2816
NFC = 22
NS = 16
MS = 64
PAST = 16384
CDEC = -math.exp(-0.5)
NEG = -30000.0


class FW:
    def __init__(self, nc, es):
        self.nc = nc
        self.es = es
        self.ops = {e: [] for e in ENGS}
        self.lastw = {}
        self.readers = {}
        self.dma_count = {}
        self.inc = {}

    def sb(self, name, shape, dt=F32):
        return self.es.enter_context(self.nc.sbuf_tensor(name, list(shape), dt))

    def ps(self, name, shape, dt=F32):
        return self.es.enter_context(self.nc.psum_tensor(name, list(shape), dt))

    def capture(self, f):
        self.cap = []
        f()
        log, self.cap = self.cap, None
        return log

    def replay(self, logs, chunk=2):
        logs = [list(lg) for lg in logs if lg]
        if not logs:
            return
        mn = min(len(lg) for lg in logs)
        per = [max(1, int(round(chunk * len(lg) / mn))) for lg in logs]
        pos = [0] * len(logs)
        while any(p < len(lg) for p, lg in zip(pos, logs)):
            for k, lg in enumerate(logs):
                for _ in range(per[k]):
                    if pos[k] < len(lg):
                        self.op(*lg[pos[k]])
                        pos[k] += 1

    def op(self, eng, fn, r=(), w=(), dma=None):
        if getattr(self, "cap", None) is not None:
            self.cap.append((eng, fn, tuple(r), tuple(w), dma))
            return
        ops = self.ops[eng]
        idx = len(ops)
        deps = set()
        pr = [k for k in r if isinstance(k, str) and k[:2] in ("ps", "pb") and k[2:].isdigit()]
        if pr:
            r = [k for k in r if k not in pr]
            w = list(w) + pr
        for k in r:
            t = self.lastw.get(k)
            if t is not None:
                deps.add(t)
        for k in w:
            t = self.lastw.get(k)
            if t is not None:
                deps.add(t)
            for t2 in self.readers.get(k, {}).values():
                deps.add(t2)
        if dma is not None:
            c = self.dma_count.get(dma, 0) + 1
            self.dma_count[dma] = c
            tok = ("d", dma, c)
        else:
            tok = ("c", eng, idx)
        if eng == "pe":
            deps = {d for d in deps if not (d[0] == "c" and d[1] == "pe")}
        deps.discard(tok)
        rec = dict(fn=fn, deps=deps, tok=tok, signal=False)
        if DEBUG_WHERE:
            import sys as _s
            f_ = _s._getframe(1)
            wh = []
            while f_ is not None and len(wh) < 4:
                wh.append(f_.f_lineno)
                f_ = f_.f_back
            rec["where"] = wh
        ops.append(rec)
        for d in deps:
            if d[0] == "c":
                self.ops[d[1]][d[2]]["signal"] = True
        for k in w:
            self.lastw[k] = tok
            self.readers[k] = {}
        for k in r:
            rk = ("d", tok[1]) if tok[0] == "d" else tok[1]
            self.readers.setdefault(k, {})[rk] = tok
        return tok

    def fence(self):
        toks = set()
        for e in ENGS:
            for rec in reversed(self.ops[e]):
                if rec["tok"][0] == "c" and rec["fn"] is not None:
                    toks.add(rec["tok"])
                    rec["signal"] = True
                    break
        for k, c in self.dma_count.items():
            toks.add(("d", k, c))
        for e in ENGS:
            self.ops[e].append(dict(fn=None, deps=set(toks), tok=("c", e, len(self.ops[e])), signal=False))

    def dma(self, out, in_, r=(), w=(), key=None, eng="sp", **kw):
        self.op(eng, lambda e: e.dma_start(out=out, in_=in_, **kw), r=r, w=w, dma=key)

    def mm(self, out, lhsT, rhs, start, stop, r=(), w=()):
        self.op("pe", lambda e: e.matmul(out, lhsT, rhs, start=start, stop=stop), r=r, w=w)

    def tr(self, out, in_, ident, r=(), w=()):
        self.op("pe", lambda e: e.transpose(out, in_, ident), r=r, w=w)

    def act(self, out, in_, func, r=(), w=(), **kw):
        self.op("act", lambda e: e.activation(out, in_, func, **kw), r=r, w=w)

    def emit(self):
        nc = self.nc
        sems = {e: self.es.enter_context(nc.semaphore("s_" + e)) for e in ENGS}
        dsems = {}
        for i, k in enumerate(self.dma_count):
            dsems[k] = self.es.enter_context(nc.semaphore("d%d" % i))
        for e in ENGS:
            c = 0
            for rec in self.ops[e]:
                if rec["signal"] and rec["tok"][0] == "c":
                    c += 1
                rec["sigval"] = c
        final_counts = dict(self.dma_count)

        def run(engname, eng):
            waited = {}
            for rec in self.ops[engname]:
                need = {}
                for d in rec["deps"]:
                    if d[0] == "c":
                        s = ("c", d[1])
                        v = self.ops[d[1]][d[2]]["sigval"]
                    else:
                        s = ("d", d[1])
                        v = self.inc.get(d[1], 16) * d[2]
                    if need.get(s, 0) < v:
                        need[s] = v
                for s, v in need.items():
                    if waited.get(s, 0) >= v:
                        continue
                    waited[s] = v
                    eng.wait_ge(sems[s[1]] if s[0] == "c" else dsems[s[1]], v)
                if rec["fn"] is None:
                    continue
                try:
                    ins = rec["fn"](eng)
                except Exception:
                    print("EMIT FAILURE at lines", rec.get("where"), "engine", engname)
                    raise
                if rec["tok"][0] == "d":
                    ins.then_inc(dsems[rec["tok"][1]], self.inc.get(rec["tok"][1], 16))
                elif rec["signal"]:
                    ins.then_inc(sems[engname], 1)
            if engname == "sp":
                for k, c in final_counts.items():
                    v = self.inc.get(k, 16) * c
                    if waited.get(("d", k), 0) < v:
                        eng.wait_ge(dsems[k], v)

        with nc.Block() as block:
            @block.sync
            def _(e):
                run("sp", e)

            @block.tensor
            def _(e):
                run("pe", e)

            @block.scalar
            def _(e):
                run("act", e)

            @block.vector
            def _(e):
                run("dve", e)

            @block.gpsimd
            def _(e):
                run("pool", e)


def bc3(ap2, n):
    s = list(ap2.shape)
    return ap2.unsqueeze(2).to_broadcast([s[0], s[1], n])


def h3(ap2, h=NH):
    return ap2.rearrange("p (h d) -> p h d", h=h)


class Builder:
    def __init__(self, TP, taps=False):
        self.TP = TP
        self.NT = TP // 128
        self.taps = taps
        self.nc = bass.Bass("TRN2", target_bir_lowering=False)
        self.I = {}
        self.O = {}
        self.psi = 0
        self.pbi = 0
        self.tapnames = []
        self.pool = None
        self.pcnt = {}

    def din(self, n, s):
        self.I[n] = self.nc.dram_tensor(n, list(s), F32, kind="ExternalInput").ap()

    def dout(self, n, s):
        self.O[n] = self.nc.dram_tensor(n, list(s), F32, kind="ExternalOutput").ap()

    def declare(self):
        TP = self.TP
        for n, s in [("xp", (TP, D)), ("xs", (MS, D)), ("st_shift", (2, NS, RP)), ("st_wkv", (2, 128, 4096)),
                     ("ck", (2, NS, 128, 128)), ("cv", (2, NS, 128, 128)), ("st_conv", (2, 2 * NS, DFF)),
                     ("norm_mix_g", (2, D)), ("w_in", (2, D, INP)), ("rwkv_mu", (2, RP)), ("rwkv_w0", (2, RD)),
                     ("rwkv_w2", (2, 64, RD)), ("rwkv_a0", (2, RD)), ("rwkv_a2", (2, 64, RD)),
                     ("rwkv_g2", (2, 128, RD)), ("rwkv_k_k", (2, RD)), ("rwkv_k_a", (2, RD)),
                     ("rwkv_r_k", (2, RD)), ("rwkv_ln_g", (2, RD)), ("rwkv_ln_b", (2, RD)),
                     ("attn_sinks", (2, NH)), ("w_br_rwkv", (2, RD, D)), ("w_br_attn", (2, RD, D)),
                     ("w_out", (2, D, D)), ("norm_ffn_g", (2, D)), ("ffn_w_in", (2, D, 2 * DFF)),
                     ("ffn_conv_w", (2, 3, DFF)), ("ffn_conv_b", (2, DFF)), ("ffn_w_down", (2, DFF, D)),
                     ("norm_final_g", (D,)),
                     ("c_ident", (128, 128)), ("c_cosp", (TP, 32)), ("c_sinp", (TP, 32)),
                     ("c_coss", (MS, 32)), ("c_sins", (MS, 32)), ("c_tri", (128, 256)),
                     ("c_mask2", (128, 256)), ("c_maskL", (128, 128)), ("c_amask", (128, 768)),
                     ("c_smask", (32, 132)), ("c_last", (128, 1)),
                     ("xh0", (128, D)), ("c_cosh", (128, 32)), ("c_sinh", (128, 32)), ("c_amask0", (128, 256)), ("c_sel", (128, 8))]:
            self.din(n, s)
        for n, s in [("yp", (TP, D)), ("ys", (MS, D)), ("p_shift", (2, RP)), ("p_wkv", (2, NH, 64, 64)),
                     ("p_k", (2, 128, 128)), ("p_v", (2, 128, 128)), ("p_conv", (2, 2, DFF)),
                     ("s_shift", (2, NS, RP)), ("s_wkv", (2, 128, 4096)), ("s_k", (2, NS, 128, 128)),
                     ("s_v", (2, NS, 128, 128)), ("s_conv", (2, 2 * NS, DFF))]:
            self.dout(n, s)
        nc = self.nc
        self.xbuf = nc.dram_tensor("xbuf", [TP, D], F32).ap()
        self.xsbuf = nc.dram_tensor("xsbuf", [MS, D], F32).ap()
        self.mrbuf = nc.dram_tensor("mrbuf", [self.NT + 1, 128, 1024], BF).ap()
        self.xh_dram = nc.dram_tensor("xh_dram", [128, D], F32).ap()
        self.sq = nc.dram_tensor("sq", [6, MS, RD], F32).ap()
        self.sy = nc.dram_tensor("sy", [MS, RD], F32).ap()

    def alloc(self, name, shape, dt=F32):
        shape = list(shape)
        n = 1
        for d_ in shape[1:]:
            n *= d_
        nbytes = n * (4 if dt == F32 else 2)
        nw = (nbytes + 31) // 32 * 8
        off = self.aoff
        self.aoff += nw
        self.apeak = max(self.apeak, self.aoff)
        assert self.aoff <= self.ASZ, "SBUF arena overflow: %s needs %d words (limit %d)" % (name, self.aoff, self.ASZ)
        ap = self.arena[0:shape[0], off:off + nw]
        if dt != F32:
            ap = ap.bitcast(dt)
        ap = ap[:, 0:n]
        if len(shape) > 2:
            names = ["d%d" % i for i in range(len(shape) - 1)]
            pat = "p (%s) -> p %s" % (" ".join(names), " ".join(names))
            ap = ap.rearrange(pat, **{names[i]: shape[i + 1] for i in range(len(names))})
        return ap

    def release(self, mark):
        self.fw.fence()
        self.aoff = mark

    def pf(self):
        ids = {None: [0, 1, 2, 3, 4, 5], 0: [0, 1, 2], 1: [3, 4, 5]}[self.pool]
        c = self.pcnt.setdefault(("f", self.pool), 0)
        self.pcnt[("f", self.pool)] = c + 1
        k = ids[c % len(ids)]
        return self.PS[k], "ps%d" % k

    def pb(self):
        ids = {None: [0, 1], 0: [0], 1: [1]}[self.pool]
        c = self.pcnt.setdefault(("b", self.pool), 0)
        self.pcnt[("b", self.pool)] = c + 1
        k = ids[c % len(ids)]
        return self.PBK[k], "pb%d" % k

    def tap(self, name, ap, rkeys, dt=F32):
        if not self.taps:
            return
        shp = list(ap.shape)
        t = self.nc.dram_tensor("tap_" + name, shp, dt, kind="ExternalOutput").ap()
        self.tapnames.append("tap_" + name)
        self.fw.dma(t, ap, r=rkeys, key="tap_" + name)

    def V(self, fn, r=(), w=()):
        self.fw.op("dve", fn, r, w)

    def P(self, fn, r=(), w=()):
        self.fw.op("pool", fn, r, w)

    def col_load(self, dst, dkey, vec, n):
        fw = self.fw
        st = self.cstage
        fw.dma(st[0:n, :], vec.rearrange("(c p) -> c p", p=128), w=["cstage"], key="cstage")
        ps, pk = self.pf()
        fw.tr(ps[:, 0:n], st[0:n, :], self.identf[0:n, 0:n], r=["cstage", "identf"], w=[pk])
        fw.act(dst, ps[:, 0:n], AF.Copy, r=[pk], w=[dkey])

    def gather_select(self, src_ap, src_keys, n, ag_in, ag_out, name):
        fw = self.fw
        fw.dma(ag_in, src_ap, r=src_keys, w=[name + "_in"], key=name + "_st")
        self.gi = getattr(self, "gi", 0)
        ck = name + "_cc"
        fw.inc[ck] = 1
        fw.op("pool", lambda e: e.collective_compute("AllGather", ALU.bypass, replica_groups=[list(range(8))], ins=[ag_in], outs=[ag_out]),
              r=[name + "_in"], w=[name + "_out"], dma=ck)
        for r_ in range(8):
            st, sk = self.xt[r_ % 2], "xt%d" % (r_ % 2)
            fw.dma(st[:, 0:n], ag_out[r_ * 128:(r_ + 1) * 128, :], r=[name + "_out"], w=[sk], key=sk)
            if r_ == 0:
                self.V(lambda e, st=st: e.tensor_scalar(src_ap, st[:, 0:n], self.sel[:, 0:1], None, ALU.mult), r=[sk, "sel"], w=src_keys)
            else:
                self.V(lambda e, st=st, r_=r_: e.scalar_tensor_tensor(src_ap, st[:, 0:n], self.sel[:, r_:r_ + 1], src_ap, ALU.mult, ALU.add),
                       r=[sk, "sel"] + list(src_keys), w=src_keys)

    def bcast_load(self, dst, dkey, vec):
        self.fw.dma(dst, vec.partition_broadcast(dst.shape[0]), w=[dkey], key=dkey)

    def prep_w(self, nchunks, ncols, src, dst, dkey, mode, scale=None, mul=None, mulkey=None, sview=None):
        fw = self.fw
        for c in range(nchunks):
            for s0 in range(0, ncols, 2048):
                n = min(2048, ncols - s0)
                k = self.wst_i % 2
                self.wst_i += 1
                st = self.wstage[k]
                sk = "wst%d" % k
                fw.dma(st[:, 0:n], src(c, s0, n), w=[sk], key=sk)
                o = dst(c, s0, n)
                dk = dkey(c)
                if sview is not None:
                    sv_ = sview(st[:, 0:n])
                    sc = scale(c)
                    self.V(lambda eg, o=o, sv_=sv_, sc=sc: eg.tensor_scalar(o, sv_, sc, None, ALU.mult), r=[sk, "gcol"], w=[dk])
                    continue
                if mode == "plain":
                    e = ["dve", "pool", "act"][self.wst_i % 3]
                    if e == "act":
                        fw.act(o, st[:, 0:n], AF.Copy, r=[sk], w=[dk])
                    else:
                        fw.op(e, lambda eg, o=o, st=st, n=n: eg.tensor_copy(o, st[:, 0:n]), r=[sk], w=[dk])
                elif mode == "col":
                    sc = scale(c)
                    e = ["dve", "pool"][self.wst_i % 2]
                    fw.op(e, lambda eg, o=o, st=st, n=n, sc=sc: eg.tensor_scalar(o, st[:, 0:n], sc, None, ALU.mult),
                          r=[sk, "gcol"], w=[dk])
                else:
                    sc = scale(c)
                    m = mul(s0, n)
                    self.V(lambda eg, o=o, st=st, n=n, sc=sc, m=m: eg.scalar_tensor_tensor(
                        o, st[:, 0:n], sc, m, ALU.mult, ALU.mult), r=[sk, "gcol", mulkey], w=[dk])

    def norm_hT(self, xt, xk, M, hdst, hkey, identb):
        fw = self.fw
        xn, ss, t1 = self.xn, self.ss, self.t1
        fw.act(xn[0:M, :], xt[0:M, :], AF.Square, r=[xk], w=["xn", "ss"], accum_out=ss[0:M, :])
        self.V(lambda e: e.tensor_scalar(t1[0:M, :], ss[0:M, :], 1.0 / D, 1e-6, ALU.mult, ALU.add), r=["ss"], w=["t1"])
        fw.act(t1[0:M, :], t1[0:M, :], AF.Sqrt, r=["t1"], w=["t1"])
        self.V(lambda e: e.reciprocal(t1[0:M, :], t1[0:M, :]), r=["t1"], w=["t1"])
        self.V(lambda e: e.tensor_scalar(xn[0:M, :], xt[0:M, :], t1[0:M, 0:1], None, ALU.mult), r=[xk, "t1"], w=["xn"])
        pbk, pk = self.pb()
        for c in range(8):
            fw.tr(pbk[:, c * M:(c + 1) * M], xn[0:M, c * 128:(c + 1) * 128], identb[0:M, 0:M], r=["xn", "identb"], w=[pk])
        fw.act(hdst, pbk[:, 0:8 * M].rearrange("p (c t) -> p c t", c=8), AF.Copy, r=[pk], w=[hkey])

    def build(self):
        self.declare()
        nc = self.nc
        with ExitStack() as es:
            self.fw = fw = FW(nc, es)
            self.PS = [fw.ps("ps%d" % i, [128, 512], F32) for i in range(6)]
            self.PBK = [fw.ps("pb%d" % i, [128, 1024], BF) for i in range(2)]
            self.ASZ = 52224
            self.arena = fw.sb("arena", [128, self.ASZ])
            self.aoff = 0
            self.apeak = 0
            self.identf = self.alloc("identf", [128, 128])
            self.identb = self.alloc("identb", [128, 128], BF)
            self.cstage = self.alloc("cstage", [32, 128])
            self.wst_i = 0
            self.xn = self.alloc("xn", [128, D], BF)
            self.ss = self.alloc("ss", [128, 1])
            self.t1 = self.alloc("t1", [128, 1])
            self.gcol = self.alloc("gcol", [128, 8])
            self.xt = [self.alloc("xt%d" % i, [128, D]) for i in range(2)]
            self.sel = self.alloc("sel", [128, 8])
            fw.dma(self.sel[:], self.I["c_sel"], w=["sel"], key="sel")
            fw.dma(self.identf[:], self.I["c_ident"], w=["identf"], key="identf")
            self.V(lambda e: e.tensor_copy(self.identb[:], self.identf[:]), r=["identf"], w=["identb"])
            for l in range(2):
                for p_ in (self.pass_rwkv, self.pass_attn, self.pass_ffn):
                    mk_ = self.aoff
                    p_(l, None)
                    self.release(mk_)
            print("arena peak words", self.apeak, "of", self.ASZ)
            fw.emit()
        return nc

    def sbl(self, es2, name, shape, dt=F32):
        return self.alloc(name, shape, dt)

    def xsrc(self, l, i):
        if i < self.NT:
            src = self.I["xp"] if l == 0 else self.xbuf
            return src[i * 128:(i + 1) * 128, :], ("xb", i)
        src = self.I["xs"] if l == 0 else self.xsbuf
        return src, ("xb", i)

    def pass_rwkv(self, l, es2):
        fw, I, O, NT = self.fw, self.I, self.O, self.NT
        sbl = lambda n, s, dt=F32: self.sbl(es2, "r%d_" % l + n, s, dt)
        identb, identf = self.identb, self.identf
        W1 = sbl("W1", [128, 8, RP], BF)
        W2 = sbl("W2", [128, 8, RP], BF)
        Wg = sbl("Wg", [128, 8, D], BF)
        Wr = sbl("Wr", [128, 4, D], BF)
        lw2 = sbl("lw2", [128, RD], BF)
        lg2 = sbl("lg2", [128, RD], BF)
        bcs = {}
        for n in ["rwkv_w0", "rwkv_a0", "rwkv_k_k", "rwkv_k_a", "rwkv_r_k", "rwkv_ln_g", "rwkv_ln_b"]:
            bcs[n] = sbl(n, [128, RD])
            self.bcast_load(bcs[n][:], n + "_bc", I[n][l])
        mucol = sbl("mucol", [128, 2])
        tri = sbl("tri", [128, 256])
        mask2 = sbl("mask2", [128, 256])
        maskL = sbl("maskL", [128, 128])
        clast = sbl("clast", [128, 1])
        fw.dma(tri[:], I["c_tri"], w=["tri"], key="tri")
        fw.dma(mask2[:], I["c_mask2"], w=["mask2"], key="mask2")
        fw.dma(maskL[:], I["c_maskL"], w=["maskL"], key="maskL")
        fw.dma(clast[:], I["c_last"], w=["clast"], key="clast")
        self.col_load(self.gcol[:], "gcol", I["norm_mix_g"][l], 8)
        self.col_load(mucol[:], "mucol", I["rwkv_mu"][l, 1536:1792], 2)
        m0 = self.aoff
        self.wstage = [sbl("wst%d" % i_, [128, 2048]) for i_ in range(2)]
        mu_bc = sbl("mu_bc", [128, RP])
        omm_bc = sbl("omm_bc", [128, RP])
        self.bcast_load(mu_bc[:], "mu_bc", I["rwkv_mu"][l])
        self.V(lambda e: e.tensor_scalar(omm_bc[:], mu_bc[:], -1.0, 1.0, ALU.mult, ALU.add), r=["mu_bc"], w=["omm_bc"])
        win = I["w_in"][l]
        gsc = lambda c: self.gcol[:, c:c + 1]
        self.prep_w(8, RP, lambda c, s0, n: win[c * 128:(c + 1) * 128, s0:s0 + n],
                    lambda c, s0, n: W1[:, c, s0:s0 + n], lambda c: "W1_%d" % c, "colmul", gsc,
                    lambda s0, n: omm_bc[:, s0:s0 + n], "omm_bc")
        self.prep_w(8, RP, lambda c, s0, n: win[c * 128:(c + 1) * 128, s0:s0 + n],
                    lambda c, s0, n: W2[:, c, s0:s0 + n], lambda c: "W2_%d" % c, "colmul", gsc,
                    lambda s0, n: mu_bc[:, s0:s0 + n], "mu_bc")
        self.prep_w(8, D, lambda c, s0, n: win[c * 128:(c + 1) * 128, 2560 + s0:2560 + s0 + n],
                    lambda c, s0, n: Wg[:, c, s0:s0 + n], lambda c: "Wg_%d" % c, "col", gsc)
        wbr = I["w_br_rwkv"][l]
        self.prep_w(4, D, lambda c, s0, n: wbr[c * 128:(c + 1) * 128, s0:s0 + n],
                    lambda c, s0, n: Wr[:, c, s0:s0 + n], lambda c: "Wr_%d" % c, "plain")
        for (nm, p0, dk_) in [("rwkv_w2", 0, "lw2a"), ("rwkv_a2", 64, "lw2b")]:
            k = self.wst_i % 2
            self.wst_i += 1
            wsk = self.wstage[k]
            fw.dma(wsk[p0:p0 + 64, 0:RD], I[nm][l], w=["wst%d" % k], key="wst%d" % k)
            self.P(lambda e, wsk=wsk, p0=p0: e.tensor_copy(lw2[p0:p0 + 64, :], wsk[p0:p0 + 64, 0:RD]), r=["wst%d" % k], w=[dk_])
        self.prep_w(1, RD, lambda c, s0, n: I["rwkv_g2"][l], lambda c, s0, n: lg2[:, :], lambda c: "lg2", "plain")
        WK1 = ["W1_%d" % c for c in range(8)]
        WK2 = ["W2_%d" % c for c in range(8)]
        self.release(m0)
        class NSP:
            pass
        zr, zk = sbl("zr", [128, RD]), sbl("zk", [128, RD])
        lact = sbl("lact", [128, 128], BF)
        T = [sbl("tmp%d" % i_, [128, RD]) for i_ in range(8)]
        sm = sbl("sm", [128, 64])
        orT = sbl("orT", [128, 4, 128], BF)
        sgr = sbl("sgr", [128, 8, 128], BF)
        mrT0_ = sbl("mrT0", [128, 8, 128], BF)
        mrT = [mrT0_, mrT0_]
        TP_ = [sbl("tpost%d" % i_, [128, RD]) for i_ in range(2)]
        m1 = self.aoff
        NRB = 9864

        def mkrec(k):
            R = NSP()
            rb = sbl("RB%d" % k, [128, NRB], BF)
            rf = sbl("RF%d" % k, [128, 528])
            R.rb, R.rf, R.k = rb, rf, k
            R.RKT = rb[:, 0:1024].rearrange("p (j a t) -> p j a t", j=4, a=2)
            R.G4 = [rb[:, 1024 + j * 1280:1024 + (j + 1) * 1280].rearrange("p (h c) -> p h c", h=2) for j in range(4)]
            R.ZF = [rb[:, 6144 + j * 256:6144 + (j + 1) * 256].rearrange("p (h c) -> p h c", h=2) for j in range(4)]
            R.vb, R.ktt, R.bnt = rb[:, 7168:7680], rb[:, 7680:8192], rb[:, 8192:8704]
            R.sgT = rb[:, 8704:8832]
            R.hT = rb[:, 8832:9864].rearrange("p (c t) -> p c t", c=8)
            R.zv, R.WC, R.bon = rf[:, 0:512], rf[:, 512:516], rf[:, 516:524]
            R.K = (lambda k_: (lambda n: "%s#%d" % (n, k_)))(k)
            return R
        R0 = mkrec(0)
        U0b = [sbl("U0b%d" % j, [128, 2, 64], BF) for j in range(4)]
        Ub = sbl("Ub", [128, RD], BF)
        Nst = sbl("Nst", [128, 4, 128])
        Nb = sbl("Nb", [128, 4, 128], BF)
        self.V(lambda e: e.memset(Nst[:], 0.0), w=["Nst"])
        self.V(lambda e: e.memset(Nb[:], 0.0), w=["Nb"])
        m2 = self.aoff
        rt, kat = sbl("rt", [128, RD], BF), sbl("kat", [128, RD], BF)
        KT = sbl("KT", [128, 4, 128], BF)
        BT = sbl("BT", [128, 4, 128], BF)
        for j in range(4):
            self.P(lambda e, j=j: e.tensor_copy(R0.G4[j][:, :, 512:640], identb[:, :].unsqueeze(1).to_broadcast([128, 2, 128])),
                   r=["identb"], w=["G4_%d" % j])
        EZ = [[sbl("EZ%d_%d" % (j, a), [128, 2, 2, 128], BF) for a in range(2)] for j in range(4)]
        FFa = [sbl("FFa%d" % a, [128, 4, 2, 128], BF) for a in range(2)]
        FF = [[FFa[a][:, j] for a in range(2)] for j in range(4)]

        def tok_proj(M, hcur, hprev, hk, g0, dstkey):
            ps, pk = self.pf()
            n = 0
            for c in range(8):
                fw.mm(ps[0:M, :], hcur(c), W1[:, c, g0:g0 + 512], n == 0, False, r=[hk, WK1[c]], w=[pk])
                n += 1
            for c in range(8):
                fw.mm(ps[0:M, :], hprev(c), W2[:, c, g0:g0 + 512], False, c == 7, r=[hk, WK2[c]], w=[pk])
            return ps, pk

        def feat_proj(M, hcur, hprev, hk, g0):
            ps, pk = self.pf()
            for c in range(8):
                fw.mm(ps[:, 0:M], W1[:, c, g0:g0 + 128], hcur(c), c == 0, False, r=[hk, WK1[c]], w=[pk])
            for c in range(8):
                fw.mm(ps[:, 0:M], W2[:, c, g0:g0 + 128], hprev(c), False, c == 7, r=[hk, WK2[c]], w=[pk])
            return ps, pk

        def raw_last(hl, hk, M, dst):
            for gi, g0 in enumerate(range(0, RP, 512)):
                n = min(512, RP - g0)
                ps, pk = self.pf()
                for c in range(8):
                    fw.mm(ps[0:M, 0:n], hl(c), W1[:, c, g0:g0 + n], c == 0, False, r=[hk, WK1[c]], w=[pk])
                for c in range(8):
                    fw.mm(ps[0:M, 0:n], hl(c), W2[:, c, g0:g0 + n], False, c == 7, r=[hk, WK2[c]], w=[pk])
                fw.act(T[gi][0:M, 0:n], ps[0:M, 0:n], AF.Copy, r=[pk], w=["T%d" % gi])
                fw.dma(dst[:, g0:g0 + n], T[gi][0:M, 0:n], r=["T%d" % gi], key="zl%d" % gi)

        def prep(M, sample, R):
            K = R.K
            w0, a0 = bcs["rwkv_w0"], bcs["rwkv_a0"]
            kkb, kab, rkb = bcs["rwkv_k_k"], bcs["rwkv_k_a"], bcs["rwkv_r_k"]
            pw, pwk = self.pf()
            fw.mm(pw[0:M, :], lact[0:64, 0:M], lw2[0:64, :], True, True, r=["lact", "lw2a"], w=[pwk])
            pa, pak = self.pf()
            fw.mm(pa[0:M, :], lact[64:128, 0:M], lw2[64:128, :], True, True, r=["lact", "lw2b"], w=[pak])
            sg, a_, kk, t3, kf, be = T[0], T[1], T[2], T[3], T[4], T[5]
            self.V(lambda e: e.tensor_tensor(sg[0:M, :], pw[0:M, :], w0[0:M, :], ALU.add), r=[pwk, "rwkv_w0_bc"], w=["T0"])
            fw.act(sg[0:M, :], sg[0:M, :], AF.Sigmoid, r=["T0"], w=["T0"])
            self.V(lambda e: e.tensor_tensor(a_[0:M, :], pa[0:M, :], a0[0:M, :], ALU.add), r=[pak, "rwkv_a0_bc"], w=["T1"])
            fw.act(a_[0:M, :], a_[0:M, :], AF.Sigmoid, r=["T1"], w=["T1"])
            self.P(lambda e: e.tensor_tensor(kk[0:M, :], zk[0:M, :], kkb[0:M, :], ALU.mult), r=["zk", "rwkv_k_k_bc"], w=["T2"])
            self.P(lambda e: e.tensor_tensor(t3[0:M, :], kk[0:M, :], kk[0:M, :], ALU.mult), r=["T2"], w=["T3"])
            self.V(lambda e: e.tensor_reduce(sm[0:M, 0:8], h3(t3[0:M, :]), AX.X, ALU.add), r=["T3"], w=["sm0"])
            fw.act(sm[0:M, 0:8], sm[0:M, 0:8], AF.Sqrt, r=["sm0"], w=["sm0"])
            self.V(lambda e: e.tensor_scalar(sm[0:M, 0:8], sm[0:M, 0:8], 1e-12, None, ALU.max), r=["sm0"], w=["sm0"])
            self.V(lambda e: e.reciprocal(sm[0:M, 0:8], sm[0:M, 0:8]), r=["sm0"], w=["sm0"])
            self.V(lambda e: e.tensor_tensor(h3(kk[0:M, :]), h3(kk[0:M, :]), bc3(sm[0:M, 0:8], 64), ALU.mult),
                   r=["T2", "sm0"], w=["T2"])
            self.V(lambda e: e.scalar_tensor_tensor(t3[0:M, :], a_[0:M, :], -1.0, kab[0:M, :], ALU.add, ALU.mult),
                   r=["T1", "rwkv_k_a_bc"], w=["T3"])
            self.V(lambda e: e.scalar_tensor_tensor(kf[0:M, :], t3[0:M, :], 1.0, zk[0:M, :], ALU.add, ALU.mult),
                   r=["T3", "zk"], w=["T4"])
            self.P(lambda e: e.tensor_tensor(be[0:M, :], kk[0:M, :], a_[0:M, :], ALU.mult), r=["T2", "T1"], w=["T5"])
            self.P(lambda e: e.tensor_tensor(t3[0:M, :], zr[0:M, :], kf[0:M, :], ALU.mult), r=["zr", "T4"], w=["T3"])
            self.P(lambda e: e.tensor_tensor(t3[0:M, :], t3[0:M, :], rkb[0:M, :], ALU.mult), r=["T3", "rwkv_r_k_bc"], w=["T3"])
            self.V(lambda e, R=R: e.tensor_reduce(R.bon[0:M, :], h3(t3[0:M, :]), AX.X, ALU.add), r=["T3"], w=[K("bon")])
            if sample:
                fw.act(T[6][0:M, :], sg[0:M, :], AF.Exp, r=["T0"], w=["T6"], scale=CDEC)
                for x, (tl, tk) in enumerate([(zr, "zr"), (T[6], "T6"), (kf, "T4"), (R.zv, K("zv")), (kk, "T2"), (be, "T5")]):
                    fw.dma(self.sq[x], tl[0:M, :], r=[tk], w=[("sq", x)], key="sqw%d" % x)
                return
            pli, plik = self.pf()
            fw.mm(pli[:, :], tri[:, 0:128], sg[:, :], True, True, r=["tri", "T0"], w=[plik])
            ple, plek = self.pf()
            fw.mm(ple[:, :], tri[:, 128:256], sg[:, :], True, True, r=["tri", "T0"], w=[plek])
            eL, eLm, enL = T[6], T[7], T[3]
            fw.act(eL[:, :], pli[:, :], AF.Exp, r=[plik], w=["T6"])
            fw.act(eLm[:, :], ple[:, :], AF.Exp, r=[plek], w=["T7"])
            fw.act(enL[:, :], pli[:, :], AF.Exp, r=[plik], w=["T3"], scale=-1.0)
            self.V(lambda e: e.tensor_tensor(rt[:, :], zr[:, :], eL[:, :], ALU.mult), r=["zr", "T6"], w=["rt"])
            self.V(lambda e: e.tensor_tensor(kat[:, :], kk[:, :], eLm[:, :], ALU.mult), r=["T2", "T7"], w=["kat"])
            self.P(lambda e, R=R: e.tensor_tensor(R.ktt[:, :], kf[:, :], enL[:, :], ALU.mult), r=["T4", "T3"], w=[K("ktt")])
            self.V(lambda e, R=R: e.scalar_tensor_tensor(R.bnt[:, :], be[:, :], -1.0, enL[:, :], ALU.mult, ALU.mult),
                   r=["T5", "T3"], w=[K("bnt")])
            fw.act(R.vb[:, :], R.zv[:, :], AF.Copy, r=[K("zv")], w=[K("vb")])
            pwc, pwck = self.pf()
            for j in range(4):
                fw.mm(pwc[:, j:j + 1], eL[:, j * 128:(j + 1) * 128], clast[:, :], True, True, r=["T6", "clast"], w=[pwck])
            fw.act(R.WC[:, :], pwc[:, 0:4], AF.Copy, r=[pwck], w=[K("WC")])
            for (src, skey, dstf, dk) in [(rt, "rt", None, "RKT"), (kat, "kat", None, "RKT"),
                                          (R.ktt, K("ktt"), None, "KT"), (R.bnt, K("bnt"), None, "BT")]:
                pbk, pk = self.pb()
                for j in range(4):
                    fw.tr(pbk[:, j * 128:(j + 1) * 128], src[:, j * 128:(j + 1) * 128], identb[:, :], r=[skey, "identb"], w=[pk])
                if dk == "RKT":
                    which = 0 if skey == "rt" else 1
                    fw.act(R.RKT[:, :, which, :], pbk[:, 0:512].rearrange("p (j t) -> p j t", j=4), AF.Copy, r=[pk], w=["RKT%d" % which])
                else:
                    dst = KT if dk == "KT" else BT
                    self.V(lambda e, dst=dst, pbk=pbk: e.tensor_copy(dst[:, :, :], pbk[:, 0:512].rearrange("p (j t) -> p j t", j=4)),
                           r=[pk], w=[dk])

        def stageAB(R):
            K = R.K
            RK = [K("RKT0"), K("RKT1")]
            RKT, G4, ZF = R.RKT, R.G4, R.ZF
            zb = [self.pf(), self.pf()]
            for j in range(4):
                for hh in range(2):
                    o = hh * 64
                    pZ, pzk = zb[hh]
                    fw.mm(pZ[:, j * 128:(j + 1) * 128], RKT[o:o + 64, j, 1, :], BT[o:o + 64, j, :], True, True, r=["BT", K("RKT1")], w=[pzk])
            mlb = maskL[:, :].unsqueeze(1).to_broadcast([128, 4, 128])
            for hh in range(2):
                pZ, pzk = zb[hh]
                self.V(lambda e, pZ=pZ, hh=hh: e.tensor_tensor(FFa[0][:, :, hh, :], pZ[:, :].rearrange("p (j c) -> p j c", j=4), mlb, ALU.mult),
                       r=[pzk, "maskL"], w=["FF%d_0" % j for j in range(4)])
            for j in range(4):
                bk = [self.pf(), self.pf()]
                for hh in range(2):
                    o = hh * 64
                    ps, pk = bk[hh]
                    rhs = RKT[o:o + 64, j, :, :].rearrange("p a t -> p (a t)")
                    fw.mm(ps[:, 0:256], KT[o:o + 64, j, :], rhs, True, True, r=["KT"] + RK, w=[pk])
                    fw.mm(ps[:, 256:512], BT[o:o + 64, j, :], rhs, True, True, r=["BT"] + RK, w=[pk])
                for hh in range(2):
                    ps, pk = bk[hh]
                    self.V(lambda e, j=j, hh=hh, ps=ps, G4=G4: e.tensor_tensor(
                        G4[j][:, hh, 0:512].rearrange("p (a c) -> p a c", a=2), ps[:, :].rearrange("p (a c) -> p a c", a=2),
                        mask2[:, :].unsqueeze(1).to_broadcast([128, 2, 256]), ALU.mult), r=[pk, "mask2"], w=[K("G4_%d" % j)])
            for lev in range(7):
                a, b = lev % 2, (lev + 1) % 2
                for j in range(4):
                    fk, fn_ = "FF%d_%d" % (j, a), "FF%d_%d" % (j, b)
                    ezn = "EZ%d_%d" % (j, b)
                    if lev == 0:
                        ezk = K("G4_%d" % j)
                        EZs = lambda hh, j=j, G4=G4: G4[j][:, hh, 384:640]
                        Es = lambda hh, j=j, G4=G4: G4[j][:, hh, 384:512]
                        Zs = lambda j=j, G4=G4: G4[j][:, :, 512:640]
                    else:
                        ezk = "EZ%d_%d" % (j, a)
                        EZs = lambda hh, j=j, a=a: EZ[j][a][:, hh, :, :].rearrange("p a t -> p (a t)")
                        Es = lambda hh, j=j, a=a: EZ[j][a][:, hh, 0, :]
                        Zs = lambda j=j, a=a: EZ[j][a][:, :, 1, :]
                    if lev < 6:
                        pL, plk = self.pf()
                        for hh in range(2):
                            fw.mm(pL[:, hh * 256:(hh + 1) * 256], FF[j][a][:, hh, :], EZs(hh), True, True, r=[ezk, fk], w=[plk])
                        pF, pfk = self.pf()
                        for hh in range(2):
                            fw.mm(pF[:, hh * 128:(hh + 1) * 128], Es(hh), FF[j][a][:, hh, :], True, True, r=[ezk, fk], w=[pfk])
                        l3 = pL[:, :].rearrange("p (h c) -> p h c", h=2)
                        fw.act(EZ[j][b][:, :, 0, :], l3[:, :, 0:128], AF.Copy, r=[plk], w=[ezn])
                        self.V(lambda e, j=j, b=b, l3=l3, Zs=Zs: e.tensor_tensor(EZ[j][b][:, :, 1, :], l3[:, :, 128:256], Zs(), ALU.add),
                               r=[plk, ezk], w=[ezn])
                        fw.act(FF[j][b][:, :, :], pF[:, 0:256].rearrange("p (h c) -> p h c", h=2), AF.Copy, r=[pfk], w=[fn_])
                    else:
                        pL, plk = self.pf()
                        for hh in range(2):
                            fw.mm(pL[:, hh * 128:(hh + 1) * 128], FF[j][a][:, hh, :], EZ[j][a][:, hh, 1, :], True, True, r=[ezk, fk], w=[plk])
                        self.V(lambda e, j=j, a=a, pL=pL, ZF=ZF: e.tensor_tensor(ZF[j][:, :, :], pL[:, 0:256].rearrange("p (h c) -> p h c", h=2),
                                                                      EZ[j][a][:, :, 1, :], ALU.add), r=[plk, ezk], w=[K("ZF%d" % j)])

        def stageC(R):
            K = R.K
            RKT, G4, ZF, vb = R.RKT, R.G4, R.ZF, R.vb
            for j in range(4):
                pU, puk = self.pf()
                for hh in range(2):
                    o, h = hh * 64, 2 * j + hh
                    fw.mm(pU[:, hh * 64:(hh + 1) * 64], RKT[o:o + 64, j, 1, :], Nb[o:o + 64, j, o:o + 64], True, False, r=[K("RKT1"), "Nb"], w=[puk])
                    fw.mm(pU[:, hh * 64:(hh + 1) * 64], G4[j][:, hh, 128:256], vb[:, h * 64:(h + 1) * 64], False, True, r=[K("G4_%d" % j), K("vb")], w=[puk])
                fw.act(U0b[j][:, :, :], pU[:, 0:128].rearrange("p (h c) -> p h c", h=2), AF.Copy, r=[puk], w=["U0b%d" % j])
            for j in range(4):
                pU, puk = self.pf()
                for hh in range(2):
                    fw.mm(pU[:, hh * 64:(hh + 1) * 64], ZF[j][:, hh, :], U0b[j][:, hh, :], True, True, r=[K("ZF%d" % j), "U0b%d" % j], w=[puk])
                fw.act(Ub[:, j * 128:(j + 1) * 128], pU[:, 0:128], AF.Copy, r=[puk], w=["Ub%d" % j])

        def stageD(R):
            K = R.K
            RKT, G4, vb = R.RKT, R.G4, R.vb
            psY, pyk = self.pf()
            for j in range(4):
                for hh in range(2):
                    o, h = hh * 64, 2 * j + hh
                    fw.mm(psY[:, h * 64:(h + 1) * 64], RKT[o:o + 64, j, 0, :], Nb[o:o + 64, j, o:o + 64], True, False, r=[K("RKT0"), "Nb"], w=[pyk])
                    fw.mm(psY[:, h * 64:(h + 1) * 64], G4[j][:, hh, 0:128], vb[:, h * 64:(h + 1) * 64], False, False, r=[K("G4_%d" % j), K("vb")], w=[pyk])
                    fw.mm(psY[:, h * 64:(h + 1) * 64], G4[j][:, hh, 256:384], Ub[:, h * 64:(h + 1) * 64], False, True, r=[K("G4_%d" % j), "Ub%d" % j], w=[pyk])
            return psY, pyk

        def n_update(R):
            K = R.K
            ktt, bnt, vb, WC = R.ktt, R.bnt, R.vb, R.WC
            pN, pnk = self.pf()
            for j in range(4):
                fw.mm(pN[:, j * 128:(j + 1) * 128], ktt[:, j * 128:(j + 1) * 128], vb[:, j * 128:(j + 1) * 128], True, False, r=[K("ktt"), K("vb")], w=[pnk])
                fw.mm(pN[:, j * 128:(j + 1) * 128], bnt[:, j * 128:(j + 1) * 128], Ub[:, j * 128:(j + 1) * 128], False, True, r=[K("bnt"), "Ub%d" % j], w=[pnk])
            n2 = Nst[:, :, :].rearrange("p j c -> p (j c)")
            self.V(lambda e: e.tensor_tensor(n2, pN[:, :], n2, ALU.add), r=[pnk, "Nst"], w=["Nst"])
            self.V(lambda e, WC=WC: e.tensor_tensor(Nst[:, :, :], Nst[:, :, :], bc3(WC[:, :], 128), ALU.mult), r=["Nst", K("WC")], w=["Nst"])
            fw.act(Nb[:, :, :], Nst[:, :, :], AF.Copy, r=["Nst"], w=["Nb"])


        def post(M, yap, ykeys, pg, pgk, R):
            K = R.K
            lng, lnb = bcs["rwkv_ln_g"], bcs["rwkv_ln_b"]
            y2, yc = TP_[0], TP_[1]
            ob = TP_[0].bitcast(BF)[:, 0:RD]
            self.V(lambda e: e.tensor_reduce(sm[0:M, 16:24], h3(yap), AX.X, ALU.add), r=ykeys, w=["sm2"])
            fw.act(y2[0:M, :], yap, AF.Square, r=ykeys, w=["TP0"])
            self.V(lambda e: e.tensor_reduce(sm[0:M, 24:32], h3(y2[0:M, :]), AX.X, ALU.add), r=["TP0"], w=["sm3"])
            mean, var = sm[0:M, 16:24], sm[0:M, 24:32]
            self.V(lambda e: e.tensor_scalar(mean, mean, 1.0 / 64, None, ALU.mult), r=["sm2"], w=["sm2"])
            self.V(lambda e: e.tensor_tensor(sm[0:M, 32:40], mean, mean, ALU.mult), r=["sm2"], w=["sm4"])
            self.V(lambda e: e.scalar_tensor_tensor(var, var, 1.0 / 64, sm[0:M, 32:40], ALU.mult, ALU.subtract), r=["sm3", "sm4"], w=["sm3"])
            self.V(lambda e: e.tensor_scalar(var, var, 64e-5, None, ALU.add), r=["sm3"], w=["sm3"])
            fw.act(var, var, AF.Sqrt, r=["sm3"], w=["sm3"])
            self.V(lambda e: e.reciprocal(var, var), r=["sm3"], w=["sm3"])
            self.V(lambda e: e.tensor_tensor(h3(yc[0:M, :]), h3(yap), bc3(mean, 64), ALU.subtract), r=list(ykeys) + ["sm2"], w=["TP1"])
            self.V(lambda e: e.tensor_tensor(h3(yc[0:M, :]), h3(yc[0:M, :]), bc3(var, 64), ALU.mult), r=["TP1", "sm3"], w=["TP1"])
            self.P(lambda e: e.tensor_tensor(yc[0:M, :], yc[0:M, :], lng[0:M, :], ALU.mult), r=["TP1", "rwkv_ln_g_bc"], w=["TP1"])
            self.P(lambda e: e.tensor_tensor(yc[0:M, :], yc[0:M, :], lnb[0:M, :], ALU.add), r=["TP1", "rwkv_ln_b_bc"], w=["TP1"])
            self.P(lambda e, R=R: e.tensor_tensor(h3(y2[0:M, :]), h3(R.zv[0:M, :]), bc3(R.bon[0:M, :], 64), ALU.mult), r=[K("zv"), K("bon")], w=["TP0"])
            self.V(lambda e: e.tensor_tensor(yc[0:M, :], yc[0:M, :], y2[0:M, :], ALU.add), r=["TP1", "TP0"], w=["TP1"])
            self.V(lambda e: e.tensor_tensor(ob[0:M, :], yc[0:M, :], pg[0:M, :], ALU.mult), r=["TP1", pgk], w=["TP0"])
            pbk, pk = self.pb()
            for j in range(4):
                fw.tr(pbk[:, j * M:(j + 1) * M], ob[0:M, j * 128:(j + 1) * 128], identb[0:M, 0:M], r=["TP0", "identb"], w=[pk])
            fw.act(orT[:, :, 0:M], pbk[:, 0:4 * M].rearrange("p (j t) -> p j t", j=4), AF.Copy, r=[pk], w=["orT"])

        def gate_branch(M, hcur, hk, mdst, mkey):
            for half in range(2):
                pg, pgk = self.pf()
                for q in range(4):
                    dc = half * 4 + q
                    for c in range(8):
                        fw.mm(pg[:, q * M:(q + 1) * M], Wg[:, c, dc * 128:(dc + 1) * 128], hcur(c), c == 0, c == 7, r=[hk, "Wg_%d" % c], w=[pgk])
                fw.act(sgr[:, half * 4:(half + 1) * 4, 0:M], pg[:, 0:4 * M].rearrange("p (q t) -> p q t", q=4), AF.Sigmoid, r=[pgk], w=["sgr%d" % half])
                pbr, pbk_ = self.pf()
                for q in range(4):
                    dc = half * 4 + q
                    for j in range(4):
                        fw.mm(pbr[:, q * M:(q + 1) * M], Wr[:, j, dc * 128:(dc + 1) * 128], orT[:, j, 0:M], j == 0, j == 3, r=["orT", "Wr_%d" % j], w=[pbk_])
                self.V(lambda e, half=half, pbr=pbr: e.tensor_tensor(mdst[:, half * 4:(half + 1) * 4, 0:M], sgr[:, half * 4:(half + 1) * 4, 0:M],
                                                                 pbr[:, 0:4 * M].rearrange("p (q t) -> p q t", q=4), ALU.mult),
                       r=["sgr%d" % half, pbk_], w=[mkey])

        R1 = mkrec(1)
        for j in range(4):
            self.P(lambda e, j=j: e.tensor_copy(R1.G4[j][:, :, 512:640], identb[:, :].unsqueeze(1).to_broadcast([128, 2, 128])),
                   r=["identb"], w=[R1.K("G4_%d" % j)])
        RR = [R0, R1]

        def H1(i):
            R, Rp = RR[i % 2], RR[(i + 1) % 2]
            K = R.K
            hT = R.hT
            xt, xk = self.xt[i % 2], "xt%d" % (i % 2)
            src, _ = self.xsrc(l, i)
            fw.dma(xt[:], src, r=[("xb", i)], w=[xk], key=xk)
            hk = K("hTr")
            if i == 0:
                self.V(lambda e, hT=hT: e.memset(hT[:, :, 0:1], 0.0), w=[hk])
            else:
                self.P(lambda e, hT=hT, hp=Rp.hT: e.tensor_copy(hT[:, :, 0:1], hp[:, :, 128:129]), r=[Rp.K("hTr")], w=[hk])
            self.norm_hT(xt, xk, 128, hT[:, :, 1:129], hk, identb)
            hcur = lambda c, hT=hT: hT[:, c, 1:129]
            hprev = lambda c, hT=hT: hT[:, c, 0:128]
            for g0, dst, dk in [(0, zr, "zr"), (512, zk, "zk"), (1024, R.zv, K("zv"))]:
                ps, pk = tok_proj(128, hcur, hprev, hk, g0, dk)
                fw.act(dst[:, :], ps[:, :], AF.Copy, r=[pk], w=[dk])
            ps, pk = feat_proj(128, hcur, hprev, hk, 1536)
            fw.act(lact[0:64, :], ps[0:64, 0:128], AF.Tanh, r=[pk], w=["lact"])
            fw.act(lact[64:128, :], ps[64:128, 0:128], AF.Copy, r=[pk], w=["lact"])
            ps, pk = feat_proj(128, hcur, hprev, hk, 1664)
            fw.act(R.sgT[:, :], ps[:, 0:128], AF.Sigmoid, r=[pk], w=[K("sgT")])
            if i == NT - 1:
                raw_last(lambda c, hT=hT: hT[:, c, 128:129], hk, 1, O["p_shift"][l:l + 1, :])
            prep(128, False, R)
            stageAB(R)

        def H2(i):
            R = RR[i % 2]
            K = R.K
            stageC(R)
            psY, pyk = stageD(R)
            n_update(R)
            pg, pgk = self.pf()
            fw.mm(pg[:, :], R.sgT[:, :], lg2[:, :], True, True, r=[K("sgT"), "lg2"], w=[pgk])
            post(128, psY[:, :], [pyk], pg, pgk, R)
            m, mk = mrT[0], "mrT0"
            gate_branch(128, lambda c, R=R: R.hT[:, c, 1:129], K("hTr"), m, mk)
            fw.dma(self.mrbuf[i].rearrange("p (c t) -> p c t", c=8), m[:, :, :], r=[mk], w=[("mr", i)], key=mk)

        self.pool = 0
        fw.replay([fw.capture(lambda: H1(0))])
        for i in range(NT):
            logs = []
            if i + 1 < NT:
                self.pool = 0
                logs.append(fw.capture(lambda: H1(i + 1)))
            self.pool = 1
            logs.append(fw.capture(lambda: H2(i)))
            fw.replay(logs, chunk=2)
        self.pool = None
        for j in range(4):
            ps, pk = self.pf()
            fw.tr(ps[:, 0:128], Nst[:, j, :], identf[:, :], r=["Nst", "identf"], w=[pk])
            fw.act(T[0][:, j * 128:(j + 1) * 128], ps[:, 0:128], AF.Copy, r=[pk], w=["T0"])
        for h_ in range(8):
            j, o = h_ // 2, (h_ % 2) * 64
            fw.dma(O["p_wkv"][l, h_], T[0][o:o + 64, j * 128 + o:j * 128 + o + 64], r=["T0"], key="T0")

        self.release(m1)
        RS = NSP()
        RS.zv = sbl("zv_s", [128, RD])
        RS.sgT = sbl("sgT_s", [128, 128], BF)
        RS.bon = sbl("bon_s", [128, 8])
        RS.K = lambda n: n + "#s"
        hTs = sbl("hTs", [128, 8, 80], BF)
        sadd = sbl("sadd", [16, RP])
        stT = sbl("stT", [128, 2, 16])
        zf = sbl("zf", [128, 2, 64])
        QH = sbl("QH", [128, 6, 4, 64])
        Sst = sbl("Sst", [128, 64, 64])
        Stmp = sbl("Stmp", [128, 64, 64])
        sk = sbl("sk", [128, 64])
        yh = sbl("yh", [128, 4, 64])
        ytm = T[7]
        self.V(lambda e: e.memset(hTs[:], 0.0), w=["hTs"])
        i = NT
        xt, xk = self.xt[i % 2], "xt%d" % (i % 2)
        src, _ = self.xsrc(l, i)
        fw.dma(xt[0:MS, :], src, r=[("xb", i)], w=[xk], key=xk)
        self.norm_hT(xt, xk, MS, hTs[:, :, 16:80], "hTs", identb)
        hcur = lambda c: hTs[:, c, 16:80]
        hprev = lambda c: hTs[:, c, 0:64]
        fw.dma(sadd[:, :], I["st_shift"][l], w=["sadd"], key="sadd")
        for q in range(2):
            ps, pk = self.pf()
            fw.tr(ps[:, 0:16], sadd[0:16, 1536 + q * 128:1536 + (q + 1) * 128], identf[0:16, 0:16], r=["sadd", "identf"], w=[pk])
            self.V(lambda e, q=q, ps=ps: e.tensor_scalar(stT[:, q, :], ps[:, 0:16], mucol[:, q:q + 1], None, ALU.mult), r=[pk, "mucol"], w=["stT"])
        for gi, g0 in enumerate(range(0, RP, 512)):
            n = min(512, RP - g0)
            self.bcast_load(T[4 + gi][0:16, 0:n], "T%d" % (4 + gi), I["rwkv_mu"][l, g0:g0 + n])
            self.V(lambda e, gi=gi, g0=g0, n=n: e.tensor_tensor(sadd[:, g0:g0 + n], sadd[:, g0:g0 + n], T[4 + gi][0:16, 0:n], ALU.mult),
                   r=["sadd", "T%d" % (4 + gi)], w=["sadd"])
        zv, sgT = RS.zv, RS.sgT
        for g0, dst, dk in [(0, zr, "zr"), (512, zk, "zk"), (1024, zv, RS.K("zv"))]:
            ps, pk = tok_proj(MS, hcur, hprev, "hTs", g0, dk)
            fw.act(dst[0:MS, :], ps[0:MS, :], AF.Copy, r=[pk], w=[dk])
            self.V(lambda e, dst=dst, g0=g0: e.tensor_tensor(dst[0:16, :], dst[0:16, :], sadd[0:16, g0:g0 + 512], ALU.add), r=[dk, "sadd"], w=[dk])
        for q, g0 in enumerate([1536, 1664]):
            ps, pk = feat_proj(MS, hcur, hprev, "hTs", g0)
            fw.act(zf[:, q, :], ps[:, 0:MS], AF.Copy, r=[pk], w=["zf"])
            self.V(lambda e, q=q: e.tensor_tensor(zf[:, q, 0:16], zf[:, q, 0:16], stT[:, q, :], ALU.add), r=["zf", "stT"], w=["zf"])
        fw.act(lact[0:64, 0:MS], zf[0:64, 0, :], AF.Tanh, r=["zf"], w=["lact"])
        fw.act(lact[64:128, 0:MS], zf[64:128, 0, :], AF.Copy, r=["zf"], w=["lact"])
        fw.act(sgT[:, 0:MS], zf[:, 1, :], AF.Sigmoid, r=["zf"], w=[RS.K("sgT")])
        prep(MS, True, RS)
        if l == 0:
            for nm, ap, k in [("s_zr", zr, "zr"), ("s_zk", zk, "zk"), ("s_zv", zv, "zv"), ("s_dec", T[6], "T6"), ("s_kk", T[2], "T2"),
                              ("s_kf", T[4], "T4"), ("s_be", T[5], "T5"), ("s_a", T[1], "T1")]:
                self.tap(nm, ap[0:MS, :], [k])
        sqv = self.sq.rearrange("x (t q) (h d) -> (q h) x t d", t=4, h=NH)
        for x in range(6):
            fw.dma(QH[:, x, :, :], sqv[:, x, :, :], r=[("sq", x)], w=["QH"], key="QH")
        fw.dma(Sst[:, :, :].rearrange("p v k -> p (v k)"), I["st_wkv"][l], w=["Sst"], key="Sst")
        for t in range(4):
            r_, w_, k_, v_, kk_, b_ = (QH[:, x, t, :] for x in range(6))
            rowb = lambda a: a.unsqueeze(1).to_broadcast([128, 64, 64])
            colb = lambda a: a.unsqueeze(2).to_broadcast([128, 64, 64])
            self.V(lambda e, kk_=kk_: e.tensor_tensor(Stmp[:, :, :], Sst[:, :, :], rowb(kk_), ALU.mult), r=["Sst", "QH"], w=["Stmp"])
            self.V(lambda e: e.tensor_reduce(sk[:, :], Stmp[:, :, :], AX.X, ALU.add), r=["Stmp"], w=["sk"])
            self.P(lambda e, w_=w_: e.tensor_tensor(Sst[:, :, :], Sst[:, :, :], rowb(w_), ALU.mult), r=["Sst", "QH", "Stmp"], w=["Sst"])
            self.V(lambda e, b_=b_: e.tensor_tensor(Stmp[:, :, :], colb(sk[:, :]), rowb(b_), ALU.mult), r=["sk", "QH"], w=["Stmp"])
            self.V(lambda e: e.tensor_tensor(Sst[:, :, :], Sst[:, :, :], Stmp[:, :, :], ALU.subtract), r=["Sst", "Stmp"], w=["Sst"])
            self.P(lambda e, v_=v_, k_=k_: e.tensor_tensor(Stmp[:, :, :], colb(v_), rowb(k_), ALU.mult), r=["QH", "Sst"], w=["Stmp"])
            self.V(lambda e: e.tensor_tensor(Sst[:, :, :], Sst[:, :, :], Stmp[:, :, :], ALU.add), r=["Sst", "Stmp"], w=["Sst"])
            self.P(lambda e, r_=r_: e.tensor_tensor(Stmp[:, :, :], Sst[:, :, :], rowb(r_), ALU.mult), r=["Sst", "QH"], w=["Stmp"])
            self.V(lambda e, t=t: e.tensor_reduce(yh[:, t, :], Stmp[:, :, :], AX.X, ALU.add), r=["Stmp"], w=["yh"])
        fw.dma(O["s_wkv"][l], Sst[:, :, :].rearrange("p v k -> p (v k)"), r=["Sst"], key="Sst")
        if l == 0:
            self.tap("s_QH", QH, ["QH"])
            self.tap("s_yh", yh, ["yh"])
        fw.dma(self.sy.rearrange("(t q) (h d) -> (q h) t d", t=4, h=NH), yh[:, :, :], r=["yh"], w=["sy"], key="yh")
        fw.dma(ytm[0:MS, :], self.sy, r=["sy"], w=["T7"], key="ytm")
        pg, pgk = self.pf()
        fw.mm(pg[0:MS, :], sgT[:, 0:MS], lg2[:, :], True, True, r=[RS.K("sgT"), "lg2"], w=[pgk])
        post(MS, ytm[0:MS, :], ["T7"], pg, pgk, RS)
        m, mk = mrT[0], "mrT0"
        gate_branch(MS, hcur, "hTs", m, mk)
        fw.dma(self.mrbuf[NT].rearrange("p (c t) -> p c t", c=8)[:, :, 0:MS], m[:, :, 0:MS], r=[mk], w=[("mr", NT)], key=mk)
        raw_last(lambda c: hTs[:, c, 64:80], "hTs", 16, O["s_shift"][l])

    def pass_attn(self, l, es2):
        fw, I, O, NT = self.fw, self.I, self.O, self.NT
        sbl = lambda n, s, dt=F32: self.sbl(es2, "a%d_" % l + n, s, dt)
        identb, identf = self.identb, self.identf
        Wq = sbl("Wq", [128, 8, 768], BF)
        Wg = sbl("Wg", [128, 8, D], BF)
        Wa = sbl("Wa", [128, 4, D], BF)
        Wo = sbl("Wo", [128, 8, D], BF)
        self.col_load(self.gcol[:], "gcol", I["norm_mix_g"][l], 8)
        m0 = self.aoff
        self.wstage = [sbl("wst%d" % i_, [128, 2048]) for i_ in range(2)]
        win = I["w_in"][l]
        gsc = lambda c: self.gcol[:, c:c + 1]
        self.prep_w(8, 512, lambda c, s0, n: win[c * 128:(c + 1) * 128, RP:RP + 512],
                    lambda c, s0, n: Wq[:, c, 0:512].rearrange("p (j g d) -> p g j d", j=4, g=2), lambda c: "Wq_%d" % c, "col", gsc,
                    sview=lambda a: a.rearrange("p (g j d) -> p g j d", g=2, j=4))
        self.prep_w(8, 256, lambda c, s0, n: win[c * 128:(c + 1) * 128, RP + 512:RP + 768],
                    lambda c, s0, n: Wq[:, c, 512:768], lambda c: "Wq_%d" % c, "col", gsc)
        self.prep_w(8, D, lambda c, s0, n: win[c * 128:(c + 1) * 128, 3584 + s0:3584 + s0 + n],
                    lambda c, s0, n: Wg[:, c, s0:s0 + n], lambda c: "Wga_%d" % c, "col", gsc)
        wbr = I["w_br_attn"][l]
        self.prep_w(4, D, lambda c, s0, n: wbr[c * 128:(c + 1) * 128, s0:s0 + n],
                    lambda c, s0, n: Wa[:, c, s0:s0 + n], lambda c: "Wa_%d" % c, "plain")
        wo = I["w_out"][l]
        self.prep_w(8, D, lambda c, s0, n: wo[c * 128:(c + 1) * 128, s0:s0 + n],
                    lambda c, s0, n: Wo[:, c, s0:s0 + n], lambda c: "Wo_%d" % c, "plain")
        self.release(m0)
        amask = sbl("amask", [128, 1024])
        fw.dma(amask[:, 0:768], I["c_amask"], w=["amask"], key="amask")
        fw.dma(amask[:, 768:1024], I["c_amask0"], w=["amask"], key="amask")
        smask = sbl("smask", [32, 132])
        fw.dma(smask[:], I["c_smask"], w=["smask"], key="smask")
        sinks = sbl("sinks", [128, NH])
        self.bcast_load(sinks[:], "sinks", I["attn_sinks"][l])
        hT = sbl("hT", [128, 8, 128], BF)
        qkv = sbl("qkv", [128, 768])
        rot = sbl("rot", [128, 640])
        rtmp = [sbl("rtmp%d" % i, [128, 320]) for i in range(2)]
        rotb = sbl("rotb", [128, 640], BF)
        cs = [sbl("cs%d" % i, [128, 64]) for i in range(2)]
        qT = sbl("qT", [128, 4, 128], BF)
        KTr = sbl("KTr", [128, 2, 128], BF)
        Vp = sbl("Vp", [128, 2, 2, 2, 128], BF)
        sc = sbl("sc", [128, 4, 256])
        st = sbl("st", [128, 16])
        pbf = sbl("pbf", [128, 4, 256], BF)
        pT = sbl("pT", [128, 4, 2, 128], BF)
        oT = sbl("oT", [128, 4, 128], BF)
        sga = sbl("sga", [128, 8, 128])
        mrl = [sbl("mrl%d" % i, [128, 8, 128], BF) for i in range(2)]
        mg = sbl("mg", [128, 8, 128], BF)
        xo = [sbl("xo%d" % i, [128, D]) for i in range(2)]
        KA = sbl("KA", [128, NS, 128])
        VA = sbl("VA", [128, NS, 128])
        VAb = sbl("VAb", [128, NS, 128], BF)
        KB = sbl("KB", [4, NS, 128])
        VBt = sbl("VB", [4, NS, 128])
        VBb = sbl("VBb", [4, NS, 128], BF)
        KAT = sbl("KAT", [128, NS, 128], BF)
        KBT = sbl("KBT", [128, NS, 4], BF)
        qbd = sbl("qbd", [128, NS, 32], BF)
        ssc = sbl("ssc", [32, NS, 132])
        sst = sbl("sst", [32, 4 * NS])
        spb = sbl("spb", [32, NS, 132], BF)
        spT = sbl("spT", [128, NS, 32], BF)
        spTB = sbl("spTB", [4, NS, 32], BF)
        oTs = sbl("oTs", [128, 4, MS], BF)

        self.V(lambda e: e.memset(Vp[:], 0.0), w=["Vp0", "Vp1"])
        self.V(lambda e: e.memset(KTr[:], 0.0), w=["KTr0", "KTr1"])
        self.V(lambda e: e.memset(qbd[:], 0.0), w=["qbd"])

        def proj_rope(M, hcur, hk, cosap, sinap, cskey):
            for g0, n in [(0, 512), (512, 256)]:
                ps, pk = self.pf()
                for c in range(8):
                    fw.mm(ps[0:M, 0:n], hcur(c), Wq[:, c, g0:g0 + n], c == 0, c == 7, r=[hk, "Wq_%d" % c], w=[pk])
                fw.act(qkv[0:M, g0:g0 + n], ps[0:M, 0:n], AF.Copy, r=[pk], w=["qkv%d" % (g0 // 512)])
            qk3 = qkv[0:M, 0:640].rearrange("p (h d) -> p h d", h=10)
            r3 = rot[0:M, :].rearrange("p (h d) -> p h d", h=10)
            x1, x2 = qk3[:, :, 0:32], qk3[:, :, 32:64]
            cb = cosap.unsqueeze(1).to_broadcast([M, 10, 32])
            sb_ = sinap.unsqueeze(1).to_broadcast([M, 10, 32])
            ta = rtmp[0][0:M, :].rearrange("p (h d) -> p h d", h=10)
            tb = rtmp[1][0:M, :].rearrange("p (h d) -> p h d", h=10)
            rk = ["qkv0", "qkv1", cskey]
            self.V(lambda e: e.tensor_tensor(ta, x1, cb, ALU.mult), r=rk, w=["rtmp0"])
            self.P(lambda e: e.tensor_tensor(tb, x2, sb_, ALU.mult), r=rk, w=["rtmp1"])
            self.V(lambda e: e.tensor_tensor(r3[:, :, 0:32], ta, tb, ALU.subtract), r=["rtmp0", "rtmp1"], w=["rot"])
            self.V(lambda e: e.tensor_tensor(ta, x2, cb, ALU.mult), r=rk + ["rot"], w=["rtmp0"])
            self.P(lambda e: e.tensor_tensor(tb, x1, sb_, ALU.mult), r=rk + ["rot"], w=["rtmp1"])
            self.V(lambda e: e.tensor_tensor(r3[:, :, 32:64], ta, tb, ALU.add), r=["rtmp0", "rtmp1"], w=["rot"])
            fw.act(rotb[0:M, :], rot[0:M, :], AF.Copy, r=["rot"], w=["rotb"])

        def q_transposes(M, dst, dkey):
            pbk, pk = self.pb()
            for jj in range(4):
                fw.tr(pbk[:, jj * M:(jj + 1) * M], rotb[0:M, jj * 128:(jj + 1) * 128], identb[0:M, 0:M], r=["rotb", "identb"], w=[pk])
            fw.act(dst, pbk[:, 0:4 * M].rearrange("p (j t) -> p j t", j=4), AF.Copy, r=[pk], w=[dkey])

        def gate_out(M, hcur, hk, oTt, okey, mr, mrk, xt, xk, xo_, xok):
            for half in range(2):
                pg, pgk = self.pf()
                for q in range(4):
                    dc = half * 4 + q
                    for c in range(8):
                        fw.mm(pg[:, q * M:(q + 1) * M], Wg[:, c, dc * 128:(dc + 1) * 128], hcur(c), c == 0, c == 7, r=[hk, "Wga_%d" % c], w=[pgk])
                fw.act(sga[:, half * 4:(half + 1) * 4, 0:M], pg[:, 0:4 * M].rearrange("p (q t) -> p q t", q=4), AF.Sigmoid, r=[pgk], w=["sga%d" % half])
                pbr, pbk_ = self.pf()
                for q in range(4):
                    dc = half * 4 + q
                    for cc in range(4):
                        fw.mm(pbr[:, q * M:(q + 1) * M], Wa[:, cc, dc * 128:(dc + 1) * 128], oTt[:, cc, 0:M], cc == 0, cc == 3, r=[okey, "Wa_%d" % cc], w=[pbk_])
                hs = slice(half * 4, (half + 1) * 4)
                self.V(lambda e, hs=hs, pbr=pbr: e.tensor_tensor(sga[:, hs, 0:M], sga[:, hs, 0:M], pbr[:, 0:4 * M].rearrange("p (q t) -> p q t", q=4), ALU.mult),
                       r=["sga%d" % half, pbk_], w=["sga%d" % half])
                self.V(lambda e, hs=hs: e.tensor_tensor(mg[:, hs, 0:M], sga[:, hs, 0:M], mr[:, hs, 0:M], ALU.add), r=["sga%d" % half, mrk], w=["mg%d" % half])
            for grp in range(2):
                px, pxk = self.pf()
                for dc in range(8):
                    fw.mm(px[0:M, :], mg[:, dc, 0:M], Wo[:, dc, grp * 512:(grp + 1) * 512], dc == 0, dc == 7, r=["mg%d" % (dc // 4), "Wo_%d" % dc], w=[pxk])
                self.V(lambda e, grp=grp, px=px: e.tensor_tensor(xo_[0:M, grp * 512:(grp + 1) * 512], xt[0:M, grp * 512:(grp + 1) * 512], px[0:M, :], ALU.add),
                       r=[xk, pxk], w=[xok])

        def put_kv(slot):
            pbk, pk = self.pb()
            fw.tr(pbk[:, 0:128], rotb[:, 512:640], identb[:, :], r=["rotb", "identb"], w=[pk])
            self.V(lambda e, pbk=pbk, slot=slot: e.tensor_copy(KTr[:, slot, :], pbk[:, 0:128]), r=[pk], w=["KTr%d" % slot])
            for g in range(2):
                vsrc = qkv[:, 640 + g * 64:640 + (g + 1) * 64]
                fw.act(Vp[:, slot, g, 0, 0:64], vsrc, AF.Copy, r=["qkv1"], w=["Vp%d" % slot])
                self.P(lambda e, g=g, vsrc=vsrc, slot=slot: e.tensor_copy(Vp[:, slot, g, 1, 64:128], vsrc), r=["qkv1"], w=["Vp%d" % slot])

        xt, xk = self.xt[1], "xt1"
        fw.dma(xt[:], (I["xh0"] if (l == 0 or NSEG == 1) else self.xh_dram), r=["xh_dram"], w=[xk], key=xk)
        fw.dma(cs[1][:, 0:32], I["c_cosh"], w=["cs1"], key="cs1")
        fw.dma(cs[1][:, 32:64], I["c_sinh"], w=["cs1"], key="cs1")
        self.norm_hT(xt, xk, 128, hT[:, :, :], "hT", identb)
        proj_rope(128, lambda c: hT[:, c, :], "hT", cs[1][:, 0:32], cs[1][:, 32:64], "cs1")
        put_kv(1)
        for i in range(NT):
            xt, xk = self.xt[i % 2], "xt%d" % (i % 2)
            src, _ = self.xsrc(l, i)
            fw.dma(xt[:], src, r=[("xb", i)], w=[xk], key=xk)
            mr, mrk = mrl[i % 2], "mrl%d" % (i % 2)
            fw.dma(mr[:, :, :], self.mrbuf[i].rearrange("p (c t) -> p c t", c=8), r=[("mr", i)], w=[mrk], key=mrk)
            ck_ = "cs%d" % (i % 2)
            fw.dma(cs[i % 2][:, 0:32], I["c_cosp"][i * 128:(i + 1) * 128, :], w=[ck_], key=ck_)
            fw.dma(cs[i % 2][:, 32:64], I["c_sinp"][i * 128:(i + 1) * 128, :], w=[ck_], key=ck_)
            self.norm_hT(xt, xk, 128, hT[:, :, :], "hT", identb)
            hcur = lambda c: hT[:, c, :]
            proj_rope(128, hcur, "hT", cs[i % 2][:, 0:32], cs[i % 2][:, 32:64], ck_)
            slot = i % 2
            if i == NT - 1:
                fw.dma(O["p_k"][l], rot[:, 512:640], r=["rot"], key="rot")
                fw.dma(O["p_v"][l], qkv[:, 640:768], r=["qkv1"], key="qkv1")
            q_transposes(128, qT[:, :, :], "qT")
            put_kv(slot)
            mvar = 3 if i == 0 else slot
            msk = amask[:, mvar * 256:(mvar + 1) * 256].unsqueeze(1).to_broadcast([128, 4, 256])
            for g in range(2):
                o = g * 64
                pS = []
                for jj in range(4):
                    if jj % 2 == 0:
                        ps, pk = self.pf()
                        pS.append((ps, pk))
                    fw.mm(ps[:, (jj % 2) * 256:(jj % 2 + 1) * 256], qT[o:o + 64, jj, :], KTr[o:o + 64, :, :].rearrange("p s t -> p (s t)"),
                          True, True, r=["qT", "KTr0", "KTr1"], w=[pk])
                for half, (ps, pk) in enumerate(pS):
                    self.V(lambda e, ps=ps, half=half, msk=msk: e.scalar_tensor_tensor(
                        sc[:, half * 2:(half + 1) * 2, :], ps[:, :].rearrange("p (j c) -> p j c", j=2), 0.125,
                        msk[:, 0:2, :], ALU.mult, ALU.add), r=[pk, "amask"], w=["sc%d" % half])
                sck = ["sc0", "sc1"]
                self.V(lambda e: e.tensor_reduce(st[:, 0:4], sc[:, :, :], AX.X, ALU.max), r=sck, w=["st"])
                self.V(lambda e, g=g: e.tensor_tensor(st[:, 0:4], st[:, 0:4], sinks[:, g * 4:(g + 1) * 4], ALU.max), r=["st", "sinks"], w=["st"])
                self.V(lambda e: e.tensor_tensor(sc[:, :, :], sc[:, :, :], bc3(st[:, 0:4], 256), ALU.subtract), r=sck + ["st"], w=sck)
                fw.act(sc[:, :, :], sc[:, :, :], AF.Exp, r=sck, w=sck)
                self.V(lambda e: e.tensor_reduce(st[:, 4:8], sc[:, :, :], AX.X, ALU.add), r=sck, w=["st2"])
                self.V(lambda e, g=g: e.tensor_tensor(st[:, 8:12], sinks[:, g * 4:(g + 1) * 4], st[:, 0:4], ALU.subtract), r=["st", "sinks"], w=["st3"])
                fw.act(st[:, 8:12], st[:, 8:12], AF.Exp, r=["st3"], w=["st3"])
                self.V(lambda e: e.tensor_tensor(st[:, 4:8], st[:, 4:8], st[:, 8:12], ALU.add), r=["st2", "st3"], w=["st2"])
                self.V(lambda e: e.reciprocal(st[:, 4:8], st[:, 4:8]), r=["st2"], w=["st2"])
                self.V(lambda e: e.tensor_tensor(pbf[:, :, :], sc[:, :, :], bc3(st[:, 4:8], 256), ALU.mult), r=sck + ["st2"], w=["pbf"])
                pbk, pk = self.pb()
                for jj in range(4):
                    for s_ in range(2):
                        fw.tr(pbk[:, (jj * 2 + s_) * 128:(jj * 2 + s_ + 1) * 128], pbf[:, jj, s_ * 128:(s_ + 1) * 128], identb[:, :], r=["pbf", "identb"], w=[pk])
                fw.act(pT[:, :, :, :], pbk[:, :].rearrange("p (j s t) -> p j s t", j=4, s=2), AF.Copy, r=[pk], w=["pT"])
                if g == 0:
                    pO, pok = self.pf()
                for c2 in range(2):
                    cc = g * 2 + c2
                    n = 0
                    for par in range(2):
                        jj = c2 * 2 + par
                        for s_ in range(2):
                            fw.mm(pO[:, cc * 128:(cc + 1) * 128], Vp[:, s_, g, par, :], pT[:, jj, s_, :], n == 0, n == 3,
                                  r=["Vp0", "Vp1", "pT"], w=[pok])
                            n += 1
            fw.act(oT[:, :, :], pO[:, :].rearrange("p (c t) -> p c t", c=4), AF.Copy, r=[pok], w=["oT"])
            xo_, xok = xo[i % 2], "xo%d" % (i % 2)
            gate_out(128, hcur, "hT", oT, "oT", mr, mrk, xt, xk, xo_, xok)
            fw.dma(self.xbuf[i * 128:(i + 1) * 128, :], xo_[:, :], r=[xok], w=[("xb", i)], key=xok)
        if NSEG > 1:
            self.gather_select(xo_[:, :], [xok], D, self.agX_in, self.agX_out, "agX")
            fw.dma(self.xh_dram, xo_[:, :], r=[xok], w=["xh_dram"], key="xhst")

        i = NT
        xt, xk = self.xt[i % 2], "xt%d" % (i % 2)
        src, _ = self.xsrc(l, i)
        fw.dma(xt[0:MS, :], src, r=[("xb", i)], w=[xk], key=xk)
        mr, mrk = mrl[i % 2], "mrl%d" % (i % 2)
        fw.dma(mr[:, :, 0:MS], self.mrbuf[NT].rearrange("p (c t) -> p c t", c=8)[:, :, 0:MS], r=[("mr", NT)], w=[mrk], key=mrk)
        ck_ = "cs%d" % (i % 2)
        fw.dma(cs[i % 2][0:MS, 0:32], I["c_coss"], w=[ck_], key=ck_)
        fw.dma(cs[i % 2][0:MS, 32:64], I["c_sins"], w=[ck_], key=ck_)
        self.norm_hT(xt, xk, MS, hT[:, :, 0:MS], "hT", identb)
        hcur = lambda c: hT[:, c, 0:MS]
        proj_rope(MS, hcur, "hT", cs[i % 2][0:MS, 0:32], cs[i % 2][0:MS, 32:64], ck_)
        for (cin, cout, srcap, srck, dkey) in [("ck", "s_k", rot[:, 512:640], "rot", "sk"), ("cv", "s_v", qkv[:, 640:768], "qkv1", "sv")]:
            fw.dma(O[cout][l, :, 0:124, :], I[cin][l, :, 4:128, :], w=[dkey], key=dkey + "c")
            for t in range(4):
                fw.dma(O[cout][l, :, 124 + t, :], srcap[t * 16:(t + 1) * 16, :], r=[srck], w=[dkey], key=dkey + "n")
        fw.dma(KA[:, :, :], O["s_k"][l].rearrange("q p c -> p q c"), r=["sk"], w=["KA"], key="KA")
        fw.dma(VA[:, :, :], O["s_v"][l].rearrange("q p c -> p q c"), r=["sv"], w=["VA"], key="VA")
        fw.dma(KB[:, :, :], I["ck"][l, :, 0:4, :].rearrange("q p c -> p q c"), w=["KB"], key="KB")
        fw.dma(VBt[:, :, :], I["cv"][l, :, 0:4, :].rearrange("q p c -> p q c"), w=["VB"], key="VB")
        self.P(lambda e: e.tensor_copy(VAb[:, :, :], VA[:, :, :]), r=["VA"], w=["VAb"])
        self.P(lambda e: e.tensor_copy(VBb[:, :, :], VBt[:, :, :]), r=["VB"], w=["VBb"])
        for q4 in range(4):
            ps, pk = self.pf()
            for qq in range(4):
                q = q4 * 4 + qq
                fw.tr(ps[:, qq * 128:(qq + 1) * 128], KA[:, q, :], identf[:, :], r=["KA", "identf"], w=[pk])
            fw.act(KAT[:, q4 * 4:(q4 + 1) * 4, :], ps[:, :].rearrange("p (q t) -> p q t", q=4), AF.Copy, r=[pk], w=["KAT"])
        ps, pk = self.pf()
        for q in range(NS):
            fw.tr(ps[:, q * 4:(q + 1) * 4], KB[0:4, q, :], identf[0:4, 0:4], r=["KB", "identf"], w=[pk])
        fw.act(KBT[:, :, :], ps[:, 0:64].rearrange("p (q t) -> p q t", q=NS), AF.Copy, r=[pk], w=["KBT"])
        q_transposes(MS, qT[:, :, 0:MS], "qT")
        for g in range(2):
            for jj in range(4):
                o = g * 64
                dst = qbd[o:o + 64, :, g * 16 + jj * 4:g * 16 + (jj + 1) * 4]
                srcq = qT[o:o + 64, jj, 0:MS].rearrange("p (t q) -> p q t", t=4)
                self.V(lambda e, dst=dst, srcq=srcq: e.tensor_copy(dst, srcq), r=["qT"], w=["qbd"])
        pSA = []
        for q4 in range(4):
            ps, pk = self.pf()
            pSA.append((ps, pk))
            for qq in range(4):
                q = q4 * 4 + qq
                fw.mm(ps[0:32, qq * 128:(qq + 1) * 128], qbd[:, q, :], KAT[:, q, :], True, True, r=["qbd", "KAT"], w=[pk])
        psB, pkB = self.pf()
        for q in range(NS):
            fw.mm(psB[0:32, q * 4:(q + 1) * 4], qbd[:, q, :], KBT[:, q, :], True, True, r=["qbd", "KBT"], w=[pkB])
        for q4, (ps, pk) in enumerate(pSA):
            self.V(lambda e, q4=q4, ps=ps: e.scalar_tensor_tensor(
                ssc[:, q4 * 4:(q4 + 1) * 4, 0:128], ps[0:32, :].rearrange("p (q c) -> p q c", q=4), 0.125,
                smask[:, 0:128].unsqueeze(1).to_broadcast([32, 4, 128]), ALU.mult, ALU.add), r=[pk, "smask"], w=["ssc"])
        self.V(lambda e: e.scalar_tensor_tensor(
            ssc[:, :, 128:132], psB[0:32, 0:64].rearrange("p (q c) -> p q c", q=NS), 0.125,
            smask[:, 128:132].unsqueeze(1).to_broadcast([32, NS, 4]), ALU.mult, ALU.add), r=[pkB, "smask"], w=["ssc"])
        sinkc = sbl("sinkc", [32, 1])
        for g in range(2):
            for jj in range(4):
                p0 = g * 16 + jj * 4
                fw.dma(sinkc[p0:p0 + 4, :], I["attn_sinks"][l, g * 4 + jj:g * 4 + jj + 1].partition_broadcast(4), w=["sinkc"], key="sinkc")
        self.V(lambda e: e.tensor_reduce(sst[:, 0:NS], ssc[:, :, :], AX.X, ALU.max), r=["ssc"], w=["sst"])
        self.V(lambda e: e.tensor_scalar(sst[:, 0:NS], sst[:, 0:NS], sinkc[:, 0:1], None, ALU.max), r=["sst", "sinkc"], w=["sst"])
        self.V(lambda e: e.tensor_tensor(ssc[:, :, :], ssc[:, :, :], bc3(sst[:, 0:NS], 132), ALU.subtract), r=["ssc", "sst"], w=["ssc"])
        fw.act(ssc[:, :, :], ssc[:, :, :], AF.Exp, r=["ssc"], w=["ssc"])
        self.V(lambda e: e.tensor_reduce(sst[:, NS:2 * NS], ssc[:, :, :], AX.X, ALU.add), r=["ssc"], w=["sst2"])
        self.V(lambda e: e.tensor_scalar(sst[:, 2 * NS:3 * NS], sst[:, 0:NS], sinkc[:, 0:1], None, ALU.subtract), r=["sst", "sinkc"], w=["sst3"])
        fw.act(sst[:, 2 * NS:3 * NS], sst[:, 2 * NS:3 * NS], AF.Exp, r=["sst3"], w=["sst3"], scale=-1.0)
        self.V(lambda e: e.tensor_tensor(sst[:, NS:2 * NS], sst[:, NS:2 * NS], sst[:, 2 * NS:3 * NS], ALU.add), r=["sst2", "sst3"], w=["sst2"])
        self.V(lambda e: e.reciprocal(sst[:, NS:2 * NS], sst[:, NS:2 * NS]), r=["sst2"], w=["sst2"])
        self.V(lambda e: e.tensor_tensor(spb[:, :, :], ssc[:, :, :], bc3(sst[:, NS:2 * NS], 132), ALU.mult), r=["ssc", "sst2"], w=["spb"])
        identb32 = identb[0:32, 0:32]
        for q8 in range(2):
            pbk, pk = self.pb()
            for qq in range(8):
                q = q8 * 8 + qq
                fw.tr(pbk[:, qq * 32:(qq + 1) * 32], spb[:, q, 0:128], identb32, r=["spb", "identb"], w=[pk])
            fw.act(spT[:, q8 * 8:(q8 + 1) * 8, :], pbk[:, 0:256].rearrange("p (q c) -> p q c", q=8), AF.Copy, r=[pk], w=["spT"])
        pbk, pk = self.pb()
        for q in range(NS):
            fw.tr(pbk[0:4, q * 32:(q + 1) * 32], spb[:, q, 128:132], identb32, r=["spb", "identb"], w=[pk])
        fw.act(spTB[:, :, :], pbk[0:4, 0:512].rearrange("p (q c) -> p q c", q=NS), AF.Copy, r=[pk], w=["spTB"])
        pO, pok = self.pf()
        for q in range(NS):
            fw.mm(pO[:, q * 32:(q + 1) * 32], VAb[:, q, :], spT[:, q, :], True, False, r=["VAb", "spT"], w=[pok])
            fw.mm(pO[:, q * 32:(q + 1) * 32], VBb[0:4, q, :], spTB[0:4, q, :], False, True, r=["VBb", "spTB"], w=[pok])
        oraw = sbl("oraw", [128, 32, NS], BF)
        fw.act(oraw.rearrange("p c q -> p q c"), pO[:, :].rearrange("p (q c) -> p q c", q=NS), AF.Copy, r=[pok], w=["oraw"])
        for g in range(2):
            for jj in range(4):
                cc, par = g * 2 + jj // 2, jj % 2
                c0 = g * 16 + jj * 4
                srco = oraw[g * 64:(g + 1) * 64, c0:c0 + 4, :].rearrange("p t q -> p (t q)")
                fw.dma(oTs[par * 64:(par + 1) * 64, cc, :], srco, r=["oraw"], w=["oTs"], key="oTs")
        xo_, xok = xo[i % 2], "xo%d" % (i % 2)
        gate_out(MS, hcur, "hT", oTs, "oTs", mr, mrk, xt, xk, xo_, xok)
        fw.dma(self.xsbuf, xo_[0:MS, :], r=[xok], w=[("xb", NT)], key=xok)

    def pass_ffn(self, l, es2):
        fw, I, O, NT = self.fw, self.I, self.O, self.NT
        sbl = lambda n, s, dt=F32: self.sbl(es2, "f%d_" % l + n, s, dt)
        identb, identf = self.identb, self.identf
        Wc = sbl("Wc", [128, 8, DFF], BF)
        Wu = sbl("Wu", [128, 8, DFF], BF)
        Wd = sbl("Wd", [128, NFC, D], BF)
        self.col_load(self.gcol[:], "gcol", I["norm_ffn_g"][l], 8)
        cw = sbl("cw", [128, 4, NFC])
        for j in range(3):
            self.col_load(cw[:, j, :], "cw", I["ffn_conv_w"][l, j], NFC)
        self.col_load(cw[:, 3, :], "cw", I["ffn_conv_b"][l], NFC)
        m0 = self.aoff
        self.wstage = [sbl("wst%d" % i_, [128, 2048]) for i_ in range(2)]
        wi = I["ffn_w_in"][l]
        gsc = lambda c: self.gcol[:, c:c + 1]
        self.prep_w(8, DFF, lambda c, s0, n: wi[c * 128:(c + 1) * 128, s0:s0 + n],
                    lambda c, s0, n: Wc[:, c, s0:s0 + n], lambda c: "Wc_%d" % c, "col", gsc)
        self.prep_w(8, DFF, lambda c, s0, n: wi[c * 128:(c + 1) * 128, DFF + s0:DFF + s0 + n],
                    lambda c, s0, n: Wu[:, c, s0:s0 + n], lambda c: "Wu_%d" % c, "col", gsc)
        wd = I["ffn_w_down"][l]
        self.prep_w(NFC, D, lambda c, s0, n: wd[c * 128:(c + 1) * 128, s0:s0 + n],
                    lambda c, s0, n: Wd[:, c, s0:s0 + n], lambda c: "Wd_%d" % c, "plain")
        self.release(m0)
        last = (l == 1)
        if last:
            gf = sbl("gf", [128, D])
            self.bcast_load(gf[:], "gf", I["norm_final_g"])
        hT = sbl("hT", [128, 8, 128], BF)
        cxf = sbl("cx", [128, NFC * 130])
        cx1 = cxf.rearrange("p (f t) -> p f t", f=NFC)
        cxs = cxf[:, 0:NFC * NS * 6].rearrange("p (f q j) -> p f q j", f=NFC, q=NS)
        acc = [sbl("acc%d" % i_, [128, 4, 128]) for i_ in range(2)]
        aT = sbl("aT", [128, NFC, 128], BF)
        xo = [sbl("xo%d" % i_, [128, D]) for i_ in range(2)]
        ctok = sbl("ctok", [128, DFF])
        cst = ctok
        jk = self.xn

        def finish(M, xt, xk, xo_, xok, dst_final, dst_x, dkey):
            for grp in range(2):
                px, pxk = self.pf()
                for fc in range(NFC):
                    fw.mm(px[0:M, :], aT[:, fc, 0:M], Wd[:, fc, grp * 512:(grp + 1) * 512], fc == 0, fc == NFC - 1, r=["aT", "Wd_%d" % fc], w=[pxk])
                self.V(lambda e, grp=grp, px=px: e.tensor_tensor(xo_[0:M, grp * 512:(grp + 1) * 512], xt[0:M, grp * 512:(grp + 1) * 512], px[0:M, :], ALU.add),
                       r=[xk, pxk], w=[xok])
            if not last:
                fw.dma(dst_x, xo_[0:M, :], r=[xok], w=[dkey], key=xok)
                return
            ss, t1 = self.ss, self.t1
            fw.act(jk[0:M, :], xo_[0:M, :], AF.Square, r=[xok], w=["xn", "ss"], accum_out=ss[0:M, :])
            self.V(lambda e: e.tensor_scalar(t1[0:M, :], ss[0:M, :], 1.0 / D, 1e-6, ALU.mult, ALU.add), r=["ss"], w=["t1"])
            fw.act(t1[0:M, :], t1[0:M, :], AF.Sqrt, r=["t1"], w=["t1"])
            self.V(lambda e: e.reciprocal(t1[0:M, :], t1[0:M, :]), r=["t1"], w=["t1"])
            self.V(lambda e: e.scalar_tensor_tensor(xo_[0:M, :], xo_[0:M, :], t1[0:M, 0:1], gf[0:M, :], ALU.mult, ALU.mult),
                   r=[xok, "t1", "gf"], w=[xok])
            fw.dma(dst_final, xo_[0:M, :], r=[xok], key=xok)

        def ffn_core(M, hcur, hk, cview, ckey, sample):
            for b0 in range(0, NFC, 4):
                nb = min(4, NFC - b0)
                pc, pck = self.pf()
                for q in range(nb):
                    fc = b0 + q
                    for c in range(8):
                        fw.mm(pc[:, q * M:(q + 1) * M], Wc[:, c, fc * 128:(fc + 1) * 128], hcur(c), c == 0, c == 7, r=[hk, "Wc_%d" % c], w=[pck])
                pu, puk = self.pf()
                for q in range(nb):
                    fc = b0 + q
                    for c in range(8):
                        fw.mm(pu[:, q * M:(q + 1) * M], Wu[:, c, fc * 128:(fc + 1) * 128], hcur(c), c == 0, c == 7, r=[hk, "Wu_%d" % c], w=[puk])
                if sample:
                    fw.act(cview[:, b0:b0 + nb, :, 2:6], pc[:, 0:nb * M].rearrange("p (f t q) -> p f q t", f=nb, t=4), AF.Copy, r=[pck], w=[ckey])
                else:
                    fw.act(cview[:, b0:b0 + nb, 2:130], pc[:, 0:nb * M].rearrange("p (f t) -> p f t", f=nb), AF.Copy, r=[pck], w=[ckey])
                a_ = acc[(b0 // 4) % 2]
                ak = "acc%d" % ((b0 // 4) % 2)
                for q in range(nb):
                    fc = b0 + q
                    if sample:
                        c0, c1, c2 = (cview[:, fc, :, s_:s_ + 4] for s_ in range(3))
                        av = a_[:, q, 0:M].rearrange("p (t q) -> p q t", t=4)
                    else:
                        c0, c1, c2 = (cview[:, fc, s_:s_ + 128] for s_ in range(3))
                        av = a_[:, q, :]
                    self.P(lambda e, av=av, c0=c0, fc=fc: e.tensor_scalar(av, c0, cw[:, 0, fc:fc + 1], cw[:, 3, fc:fc + 1], ALU.mult, ALU.add),
                           r=[ckey, "cw"], w=[ak])
                    self.V(lambda e, av=av, c1=c1, fc=fc: e.scalar_tensor_tensor(av, c1, cw[:, 1, fc:fc + 1], av, ALU.mult, ALU.add),
                           r=[ckey, "cw", ak], w=[ak])
                    self.V(lambda e, av=av, c2=c2, fc=fc: e.scalar_tensor_tensor(av, c2, cw[:, 2, fc:fc + 1], av, ALU.mult, ALU.add),
                           r=[ckey, "cw", ak], w=[ak])
                fw.act(a_[:, 0:nb, 0:M], a_[:, 0:nb, 0:M], AF.Gelu, r=[ak], w=[ak])
                self.V(lambda e, a_=a_, pu=pu, nb=nb, b0=b0: e.tensor_tensor(aT[:, b0:b0 + nb, 0:M], a_[:, 0:nb, 0:M],
                                                                       pu[:, 0:nb * M].rearrange("p (f t) -> p f t", f=nb), ALU.mult),
                       r=[ak, puk], w=["aT"])

        def c_token_major(M, hcur, hk, rows, dsts):
            for g0 in range(0, DFF, 512):
                n = min(512, DFF - g0)
                ps, pk = self.pf()
                for c in range(8):
                    fw.mm(ps[0:M, 0:n], hcur(c), Wc[:, c, g0:g0 + n], c == 0, c == 7, r=[hk, "Wc_%d" % c], w=[pk])
                fw.act(ctok[0:M, g0:g0 + n], ps[0:M, 0:n], AF.Copy, r=[pk], w=["ctok"])
            for (r0, r1), dst in zip(rows, dsts):
                fw.dma(dst, ctok[r0:r1, :], r=["ctok"], key="ctok")

        xt, xk = self.xt[1], "xt1"
        fw.dma(xt[:], (I["xh0"] if NSEG == 1 else self.xh_dram), r=["xh_dram"], w=[xk], key=xk)
        self.norm_hT(xt, xk, 128, hT[:, :, :], "hT", identb)
        pc, pck = self.pf()
        for fc in range(NFC):
            for c in range(8):
                fw.mm(pc[:, fc * 2:(fc + 1) * 2], Wc[:, c, fc * 128:(fc + 1) * 128], hT[:, c, 126:128], c == 0, c == 7, r=["hT", "Wc_%d" % c], w=[pck])
        fw.act(cx1[:, :, 0:2], pc[:, 0:2 * NFC].rearrange("p (f t) -> p f t", f=NFC), AF.Copy, r=[pck], w=["cx"])
        for i in range(NT):
            xt, xk = self.xt[i % 2], "xt%d" % (i % 2)
            fw.dma(xt[:], self.xbuf[i * 128:(i + 1) * 128, :], r=[("xb", i)], w=[xk], key=xk)
            self.norm_hT(xt, xk, 128, hT[:, :, :], "hT", identb)
            hcur = lambda c: hT[:, c, :]
            cv_, ckey = cx1, "cx"
            if i > 0:
                self.P(lambda e: e.tensor_copy(acc[0][:, 0, 0:2 * NFC].rearrange("p (f t) -> p f t", f=NFC), cx1[:, :, 128:130]), r=[ckey], w=["acc0"])
                self.P(lambda e: e.tensor_copy(cx1[:, :, 0:2], acc[0][:, 0, 0:2 * NFC].rearrange("p (f t) -> p f t", f=NFC)), r=["acc0"], w=[ckey])
            ffn_core(128, hcur, "hT", cv_, ckey, False)
            if i == NT - 1:
                c_token_major(128, hcur, "hT", [(126, 128)], [O["p_conv"][l]])
            xo_, xok = xo[i % 2], "xo%d" % (i % 2)
            finish(128, xt, xk, xo_, xok, O["yp"][i * 128:(i + 1) * 128, :], self.xbuf[i * 128:(i + 1) * 128, :], ("xb", i))
        if not last and NSEG > 1:
            self.gather_select(xo_[:, :], [xok], D, self.agX_in, self.agX_out, "agX")
            fw.dma(self.xh_dram, xo_[:, :], r=[xok], w=["xh_dram"], key="xhst")

        i = NT
        xt, xk = self.xt[i % 2], "xt%d" % (i % 2)
        fw.dma(xt[0:MS, :], self.xsbuf, r=[("xb", i)], w=[xk], key=xk)
        self.norm_hT(xt, xk, MS, hT[:, :, 0:MS], "hT", identb)
        hcur = lambda c: hT[:, c, 0:MS]
        fw.dma(cst[0:32, :], I["st_conv"][l], w=["ctok"], key="cst")
        for b0 in range(0, NFC, 4):
            nb = min(4, NFC - b0)
            ps, pk = self.pf()
            for q in range(nb):
                fc = b0 + q
                fw.tr(ps[:, q * 32:(q + 1) * 32], cst[0:32, fc * 128:(fc + 1) * 128], identf[0:32, 0:32], r=["ctok", "identf"], w=[pk])
            fw.act(cxs[:, b0:b0 + nb, :, 0:2], ps[:, 0:nb * 32].rearrange("p (f q j) -> p f q j", f=nb, j=2), AF.Copy, r=[pk], w=["cx"])
        ffn_core(MS, hcur, "hT", cxs, "cx", True)
        sc_ = O["s_conv"][l].rearrange("(q j) f -> j q f", j=2)
        c_token_major(MS, hcur, "hT", [(32, 48), (48, 64)], [sc_[0], sc_[1]])
        xo_, xok = xo[i % 2], "xo%d" % (i % 2)
        finish(MS, xt, xk, xo_, xok, O["ys"], self.xsbuf, ("xb", NT))


NSEG = 1


def _consts_shared():
    c = {}
    c["c_ident"] = np.eye(128, dtype=np.float32)
    inv = (10000.0 ** (-np.arange(0, HD, 2, dtype=np.float32) / HD)).astype(np.float32)
    pos_s = (PAST + np.repeat(np.arange(4), NS)).astype(np.float32)
    ang_s = pos_s[:, None] * inv[None, :]
    c["c_coss"] = np.cos(ang_s).astype(np.float32)
    c["c_sins"] = np.sin(ang_s).astype(np.float32)
    s = np.arange(128)[:, None]
    t = np.arange(128)[None, :]
    incl = (s <= t).astype(np.float32)
    strict = (s < t).astype(np.float32)
    c["c_tri"] = np.concatenate([incl * CDEC, strict * CDEC], 1).astype(np.float32)
    c["c_mask2"] = np.concatenate([incl, strict], 1).astype(np.float32)
    c["c_maskL"] = (s > t).astype(np.float32)
    i_ = np.arange(128)[:, None]
    j_ = np.arange(128)[None, :]
    cur = np.where(j_ <= i_, 0.0, NEG)
    prev = np.where(j_ > i_, 0.0, NEG)
    dead = np.full((128, 128), NEG)
    c["c_amask"] = np.concatenate([cur, prev, prev, cur, cur, dead], 1).astype(np.float32)
    c["_am_first"] = np.concatenate([cur, dead], 1).astype(np.float32)
    c["_am_mid"] = np.concatenate([cur, prev], 1).astype(np.float32)
    tt = (np.arange(32) % 4)[:, None]
    ia = np.arange(128)[None, :]
    ma = np.where(ia <= 124 + tt, 0.0, NEG)
    rb = np.arange(4)[None, :]
    mb = np.where(rb > tt, 0.0, NEG)
    c["c_smask"] = np.concatenate([ma, mb], 1).astype(np.float32)
    last = np.zeros((128, 1), np.float32)
    last[127, 0] = 1.0
    c["c_last"] = last
    c["_inv"] = inv
    return c


def _rope_tab(pos, inv):
    ang = pos.astype(np.float32)[:, None] * inv[None, :]
    return np.cos(ang).astype(np.float32), np.sin(ang).astype(np.float32)


_CACHE = {}
TAPS = False
TAP_OUT = {}


def kernel(**inp):
    inp = {k: np.asarray(v) for k, v in inp.items()}
    xp_all = inp["x_prompt"].astype(np.float32)
    B, SEQ_, _ = xp_all.shape
    TPC = SEQ_ // NSEG
    if TPC not in _CACHE:
        b_ = Builder(TPC, taps=TAPS)
        _CACHE[TPC] = (b_.build(), b_.tapnames)
    nc, tapnames = _CACHE[TPC]
    consts = _consts_shared()
    inv = consts.pop("_inv")
    am_first, am_mid = consts.pop("_am_first"), consts.pop("_am_mid")
    wnames = ["norm_mix_g", "w_in", "rwkv_mu", "rwkv_w0", "rwkv_w2", "rwkv_a0", "rwkv_a2", "rwkv_g2", "rwkv_k_k",
              "rwkv_k_a", "rwkv_ln_g", "rwkv_ln_b", "attn_sinks", "w_br_rwkv", "w_br_attn", "w_out", "norm_ffn_g",
              "ffn_w_in", "ffn_conv_w", "ffn_conv_b", "ffn_w_down", "norm_final_g"]
    shared = {n: np.ascontiguousarray(inp[n], dtype=np.float32) for n in wnames}
    shared["rwkv_r_k"] = np.ascontiguousarray(inp["rwkv_r_k"], dtype=np.float32).reshape(2, RD)
    shared.update(consts)
    in_maps = []
    ncores = 8
    for c in range(ncores):
        b, seg = (c // NSEG) % B, c % NSEG
        sl = slice(c * NS, (c + 1) * NS)
        m = dict(shared)
        t0 = seg * TPC
        m["xp"] = np.ascontiguousarray(xp_all[b, t0:t0 + TPC])
        m["xh0"] = np.ascontiguousarray(xp_all[b, t0 - 128:t0]) if seg > 0 else np.zeros((128, D), np.float32)
        m["c_cosp"], m["c_sinp"] = _rope_tab(t0 + np.arange(TPC), inv)
        m["c_cosh"], m["c_sinh"] = _rope_tab(np.maximum(t0 - 128 + np.arange(128), 0), inv)
        m["c_amask0"] = am_mid if seg > 0 else am_first
        sel = np.zeros((128, 8), np.float32)
        if seg > 0:
            sel[:, c - 1] = 1.0
        m["c_sel"] = sel
        m["xs"] = np.ascontiguousarray(inp["x_sample"][sl].transpose(1, 0, 2).reshape(MS, D))
        m["st_shift"] = np.ascontiguousarray(inp["state_rwkv_shift"][:, sl])
        m["st_wkv"] = np.ascontiguousarray(inp["state_rwkv_wkv"][:, sl]).reshape(2, 128, 4096)
        m["ck"] = np.ascontiguousarray(inp["cache_swa_k"][:, sl]).reshape(2, NS, 128, 128)
        m["cv"] = np.ascontiguousarray(inp["cache_swa_v"][:, sl]).reshape(2, NS, 128, 128)
        m["st_conv"] = np.ascontiguousarray(inp["state_ffn_conv"][:, sl]).reshape(2, 2 * NS, DFF)
        in_maps.append(m)
    res = run_bass_kernel_spmd(nc, in_maps, core_ids=list(range(ncores)))
    R = res.results
    for tn in tapnames:
        TAP_OUT[tn] = [np.asarray(R[c][tn]) for c in range(ncores)]
    f = np.float32
    lastc = [b * NSEG + NSEG - 1 for b in range(B)]
    y_prompt = np.stack([np.concatenate([R[b * NSEG + sg]["yp"] for sg in range(NSEG)], 0) for b in range(B)]).astype(f)
    y_sample = np.concatenate([R[c]["ys"].reshape(4, NS, D).transpose(1, 0, 2) for c in range(ncores)], 0).astype(f)
    p_shift = np.stack([R[c]["p_shift"] for c in lastc], 1).astype(f)
    p_wkv = np.stack([R[c]["p_wkv"] for c in lastc], 1).astype(f)
    p_k = np.stack([R[c]["p_k"] for c in lastc], 1).reshape(2, B, 128, 2, 64).astype(f)
    p_v = np.stack([R[c]["p_v"] for c in lastc], 1).reshape(2, B, 128, 2, 64).astype(f)
    p_conv = np.stack([R[c]["p_conv"] for c in lastc], 1).astype(f)
    s_shift = np.concatenate([R[c]["s_shift"] for c in range(ncores)], 1).astype(f)
    s_wkv = np.concatenate([R[c]["s_wkv"].reshape(2, NS, NH, 64, 64) for c in range(ncores)], 1).astype(f)
    s_k = np.concatenate([R[c]["s_k"].reshape(2, NS, 128, 2, 64) for c in range(ncores)], 1).astype(f)
    s_v = np.concatenate([R[c]["s_v"].reshape(2, NS, 128, 2, 64) for c in range(ncores)], 1).astype(f)
    s_conv = np.concatenate([R[c]["s_conv"].reshape(2, NS, 2, DFF) for c in range(ncores)], 1).astype(f)
    return (y_prompt, y_sample, p_shift, p_wkv, p_k, p_v, p_conv, s_shift, s_wkv, s_k, s_v, s_conv)
```

```python
import math
from contextlib import ExitStack

import numpy as np
import concourse.bass as bass
import concourse.mybir as mybir
from concourse.bass_utils import run_bass_kernel_spmd

F32 = mybir.dt.float32
BF = mybir.dt.bfloat16
AF = mybir.ActivationFunctionType
ALU = mybir.AluOpType
AX = mybir.AxisListType

ENGS = ["sp", "pe", "act", "dve", "pool"]
DEBUG_WHERE = True

D = 1024
HD = 64
NH = 8
RD = 512
RP = 1792
INP = 4608
DFF = 2816
NFC = 22
NS = 16
MS = 64
PAST = 16384
CDEC = -math.exp(-0.5)
NEG = -30000.0


class FW:
    def __init__(self, nc, es):
        self.nc = nc
        self.es = es
        self.ops = {e: [] for e in ENGS}
        self.lastw = {}
        self.readers = {}
        self.dma_count = {}
        self.inc = {}

    def sb(self, name, shape, dt=F32):
        return self.es.enter_context(self.nc.sbuf_tensor(name, list(shape), dt))

    def ps(self, name, shape, dt=F32):
        return self.es.enter_context(self.nc.psum_tensor(name, list(shape), dt))

    def capture(self, f):
        self.cap = []
        f()
        log, self.cap = self.cap, None
        return log

    def replay(self, logs, chunk=2):
        logs = [list(lg) for lg in logs if lg]
        if not logs:
            return
        mn = min(len(lg) for lg in logs)
        per = [max(1, int(round(chunk * len(lg) / mn))) for lg in logs]
        pos = [0] * len(logs)
        while any(p < len(lg) for p, lg in zip(pos, logs)):
            for k, lg in enumerate(logs):
                for _ in range(per[k]):
                    if pos[k] < len(lg):
                        self.op(*lg[pos[k]])
                        pos[k] += 1

    def op(self, eng, fn, r=(), w=(), dma=None):
        if getattr(self, "cap", None) is not None:
            self.cap.append((eng, fn, tuple(r), tuple(w), dma))
            return
        ops = self.ops[eng]
        idx = len(ops)
        deps = set()
        pr = [k for k in r if isinstance(k, str) and k[:2] in ("ps", "pb") and k[2:].isdigit()]
        if pr:
            r = [k for k in r if k not in pr]
            w = list(w) + pr
        for k in r:
            t = self.lastw.get(k)
            if t is not None:
                deps.add(t)
        for k in w:
            t = self.lastw.get(k)
            if t is not None:
                deps.add(t)
            for t2 in self.readers.get(k, {}).values():
                deps.add(t2)
        if dma is not None:
            c = self.dma_count.get(dma, 0) + 1
            self.dma_count[dma] = c
            tok = ("d", dma, c)
        else:
            tok = ("c", eng, idx)
        if eng == "pe":
            deps = {d for d in deps if not (d[0] == "c" and d[1] == "pe")}
        deps.discard(tok)
        rec = dict(fn=fn, deps=deps, tok=tok, signal=False)
        if DEBUG_WHERE:
            import sys as _s
            f_ = _s._getframe(1)
            wh = []
            while f_ is not None and len(wh) < 4:
                wh.append(f_.f_lineno)
                f_ = f_.f_back
            rec["where"] = wh
        ops.append(rec)
        for d in deps:
            if d[0] == "c":
                self.ops[d[1]][d[2]]["signal"] = True
        for k in w:
            self.lastw[k] = tok
            self.readers[k] = {}
        for k in r:
            rk = ("d", tok[1]) if tok[0] == "d" else tok[1]
            self.readers.setdefault(k, {})[rk] = tok
        return tok

    def fence(self):
        toks = set()
        for e in ENGS:
            for rec in reversed(self.ops[e]):
                if rec["tok"][0] == "c" and rec["fn"] is not None:
                    toks.add(rec["tok"])
                    rec["signal"] = True
                    break
        for k, c in self.dma_count.items():
            toks.add(("d", k, c))
        for e in ENGS:
            self.ops[e].append(dict(fn=None, deps=set(toks), tok=("c", e, len(self.ops[e])), signal=False))

    def dma(self, out, in_, r=(), w=(), key=None, eng="sp", **kw):
        self.op(eng, lambda e: e.dma_start(out=out, in_=in_, **kw), r=r, w=w, dma=key)

    def mm(self, out, lhsT, rhs, start, stop, r=(), w=()):
        self.op("pe", lambda e: e.matmul(out, lhsT, rhs, start=start, stop=stop), r=r, w=w)

    def tr(self, out, in_, ident, r=(), w=()):
        self.op("pe", lambda e: e.transpose(out, in_, ident), r=r, w=w)

    def act(self, out, in_, func, r=(), w=(), **kw):
        self.op("act", lambda e: e.activation(out, in_, func, **kw), r=r, w=w)

    def emit(self):
        nc = self.nc
        sems = {e: self.es.enter_context(nc.semaphore("s_" + e)) for e in ENGS}
        dsems = {}
        for i, k in enumerate(self.dma_count):
            dsems[k] = self.es.enter_context(nc.semaphore("d%d" % i))
        for e in ENGS:
            c = 0
            for rec in self.ops[e]:
                if rec["signal"] and rec["tok"][0] == "c":
                    c += 1
                rec["sigval"] = c
        final_counts = dict(self.dma_count)

        def run(engname, eng):
            waited = {}
            for rec in self.ops[engname]:
                need = {}
                for d in rec["deps"]:
                    if d[0] == "c":
                        s = ("c", d[1])
                        v = self.ops[d[1]][d[2]]["sigval"]
                    else:
                        s = ("d", d[1])
                        v = self.inc.get(d[1], 16) * d[2]
                    if need.get(s, 0) < v:
                        need[s] = v
                for s, v in need.items():
                    if waited.get(s, 0) >= v:
                        continue
                    waited[s] = v
                    eng.wait_ge(sems[s[1]] if s[0] == "c" else dsems[s[1]], v)
                if rec["fn"] is None:
                    continue
                try:
                    ins = rec["fn"](eng)
                except Exception:
                    print("EMIT FAILURE at lines", rec.get("where"), "engine", engname)
                    raise
                if rec["tok"][0] == "d":
                    ins.then_inc(dsems[rec["tok"][1]], self.inc.get(rec["tok"][1], 16))
                elif rec["signal"]:
                    ins.then_inc(sems[engname], 1)
            if engname == "sp":
                for k, c in final_counts.items():
                    v = self.inc.get(k, 16) * c
                    if waited.get(("d", k), 0) < v:
                        eng.wait_ge(dsems[k], v)

        with nc.Block() as block:
            @block.sync
            def _(e):
                run("sp", e)

            @block.tensor
            def _(e):
                run("pe", e)

            @block.scalar
            def _(e):
                run("act", e)

            @block.vector
            def _(e):
                run("dve", e)

            @block.gpsimd
            def _(e):
                run("pool", e)


def bc3(ap2, n):
    s = list(ap2.shape)
    return ap2.unsqueeze(2).to_broadcast([s[0], s[1], n])


def h3(ap2, h=NH):
    return ap2.rearrange("p (h d) -> p h d", h=h)


class Builder:
    def __init__(self, TP, taps=False):
        self.TP = TP
        self.NT = TP // 128
        self.taps = taps
        self.nc = bass.Bass("TRN2", target_bir_lowering=False)
        self.I = {}
        self.O = {}
        self.psi = 0
        self.pbi = 0
        self.tapnames = []
        self.pool = None
        self.pcnt = {}

    def din(self, n, s):
        self.I[n] = self.nc.dram_tensor(n, list(s), F32, kind="ExternalInput").ap()

    def dout(self, n, s):
        self.O[n] = self.nc.dram_tensor(n, list(s), F32, kind="ExternalOutput").ap()

    def declare(self):
        TP = self.TP
        for n, s in [("xp", (TP, D)), ("xs", (MS, D)), ("st_shift", (2, NS, RP)), ("st_wkv", (2, 128, 4096)),
                     ("ck", (2, NS, 128, 128)), ("cv", (2, NS, 128, 128)), ("st_conv", (2, 2 * NS, DFF)),
                     ("norm_mix_g", (2, D)), ("w_in", (2, D, INP)), ("rwkv_mu", (2, RP)), ("rwkv_w0", (2, RD)),
                     ("rwkv_w2", (2, 64, RD)), ("rwkv_a0", (2, RD)), ("rwkv_a2", (2, 64, RD)),
                     ("rwkv_g2", (2, 128, RD)), ("rwkv_k_k", (2, RD)), ("rwkv_k_a", (2, RD)),
                     ("rwkv_r_k", (2, RD)), ("rwkv_ln_g", (2, RD)), ("rwkv_ln_b", (2, RD)),
                     ("attn_sinks", (2, NH)), ("w_br_rwkv", (2, RD, D)), ("w_br_attn", (2, RD, D)),
                     ("w_out", (2, D, D)), ("norm_ffn_g", (2, D)), ("ffn_w_in", (2, D, 2 * DFF)),
                     ("ffn_conv_w", (2, 3, DFF)), ("ffn_conv_b", (2, DFF)), ("ffn_w_down", (2, DFF, D)),
                     ("norm_final_g", (D,)),
                     ("c_ident", (128, 128)), ("c_cosp", (TP, 32)), ("c_sinp", (TP, 32)),
                     ("c_coss", (MS, 32)), ("c_sins", (MS, 32)), ("c_tri", (128, 256)),
                     ("c_mask2", (128, 256)), ("c_maskL", (128, 128)), ("c_amask", (128, 768)),
                     ("c_smask", (32, 132)), ("c_last", (128, 1)),
                     ("xh0", (128, D)), ("c_cosh", (128, 32)), ("c_sinh", (128, 32)), ("c_amask0", (128, 256)), ("c_sel", (128, 8))]:
            self.din(n, s)
        for n, s in [("yp", (TP, D)), ("ys", (MS, D)), ("p_shift", (2, RP)), ("p_wkv", (2, NH, 64, 64)),
                     ("p_k", (2, 128, 128)), ("p_v", (2, 128, 128)), ("p_conv", (2, 2, DFF)),
                     ("s_shift", (2, NS, RP)), ("s_wkv", (2, 128, 4096)), ("s_k", (2, NS, 128, 128)),
                     ("s_v", (2, NS, 128, 128)), ("s_conv", (2, 2 * NS, DFF))]:
            self.dout(n, s)
        nc = self.nc
        self.xbuf = nc.dram_tensor("xbuf", [TP, D], F32).ap()
        self.xsbuf = nc.dram_tensor("xsbuf", [MS, D], F32).ap()
        self.mrbuf = nc.dram_tensor("mrbuf", [self.NT + 1, 128, 1024], BF).ap()
        self.xh_dram = nc.dram_tensor("xh_dram", [128, D], F32).ap()
        self.sq = nc.dram_tensor("sq", [6, MS, RD], F32).ap()
        self.sy = nc.dram_tensor("sy", [MS, RD], F32).ap()

    def alloc(self, name, shape, dt=F32):
        shape = list(shape)
        n = 1
        for d_ in shape[1:]:
            n *= d_
        nbytes = n * (4 if dt == F32 else 2)
        nw = (nbytes + 31) // 32 * 8
        off = self.aoff
        self.aoff += nw
        self.apeak = max(self.apeak, self.aoff)
        assert self.aoff <= self.ASZ, "SBUF arena overflow: %s needs %d words (limit %d)" % (name, self.aoff, self.ASZ)
        ap = self.arena[0:shape[0], off:off + nw]
        if dt != F32:
            ap = ap.bitcast(dt)
        ap = ap[:, 0:n]
        if len(shape) > 2:
            names = ["d%d" % i for i in range(len(shape) - 1)]
            pat = "p (%s) -> p %s" % (" ".join(names), " ".join(names))
            ap = ap.rearrange(pat, **{names[i]: shape[i + 1] for i in range(len(names))})
        return ap

    def release(self, mark):
        self.fw.fence()
        self.aoff = mark

    def pf(self):
        ids = {None: [0, 1, 2, 3, 4, 5], 0: [0, 1, 2], 1: [3, 4, 5]}[self.pool]
        c = self.pcnt.setdefault(("f", self.pool), 0)
        self.pcnt[("f", self.pool)] = c + 1
        k = ids[c % len(ids)]
        return self.PS[k], "ps%d" % k

    def pb(self):
        ids = {None: [0, 1], 0: [0], 1: [1]}[self.pool]
        c = self.pcnt.setdefault(("b", self.pool), 0)
        self.pcnt[("b", self.pool)] = c + 1
        k = ids[c % len(ids)]
        return self.PBK[k], "pb%d" % k

    def tap(self, name, ap, rkeys, dt=F32):
        if not self.taps:
            return
        shp = list(ap.shape)
        t = self.nc.dram_tensor("tap_" + name, shp, dt, kind="ExternalOutput").ap()
        self.tapnames.append("tap_" + name)
        self.fw.dma(t, ap, r=rkeys, key="tap_" + name)

    def V(self, fn, r=(), w=()):
        self.fw.op("dve", fn, r, w)

    def P(self, fn, r=(), w=()):
        self.fw.op("pool", fn, r, w)

    def col_load(self, dst, dkey, vec, n):
        fw = self.fw
        st = self.cstage
        fw.dma(st[0:n, :], vec.rearrange("(c p) -> c p", p=128), w=["cstage"], key="cstage")
        ps, pk = self.pf()
        fw.tr(ps[:, 0:n], st[0:n, :], self.identf[0:n, 0:n], r=["cstage", "identf"], w=[pk])
        fw.act(dst, ps[:, 0:n], AF.Copy, r=[pk], w=[dkey])

    def gather_select(self, src_ap, src_keys, n, ag_in, ag_out, name):
        fw = self.fw
        fw.dma(ag_in, src_ap, r=src_keys, w=[name + "_in"], key=name + "_st")
        self.gi = getattr(self, "gi", 0)
        ck = name + "_cc"
        fw.inc[ck] = 1
        fw.op("pool", lambda e: e.collective_compute("AllGather", ALU.bypass, replica_groups=[list(range(8))], ins=[ag_in], outs=[ag_out]),
              r=[name + "_in"], w=[name + "_out"], dma=ck)
        for r_ in range(8):
            st, sk = self.xt[r_ % 2], "xt%d" % (r_ % 2)
            fw.dma(st[:, 0:n], ag_out[r_ * 128:(r_ + 1) * 128, :], r=[name + "_out"], w=[sk], key=sk)
            if r_ == 0:
                self.V(lambda e, st=st: e.tensor_scalar(src_ap, st[:, 0:n], self.sel[:, 0:1], None, ALU.mult), r=[sk, "sel"], w=src_keys)
            else:
                self.V(lambda e, st=st, r_=r_: e.scalar_tensor_tensor(src_ap, st[:, 0:n], self.sel[:, r_:r_ + 1], src_ap, ALU.mult, ALU.add),
                       r=[sk, "sel"] + list(src_keys), w=src_keys)

    def bcast_load(self, dst, dkey, vec):
        self.fw.dma(dst, vec.partition_broadcast(dst.shape[0]), w=[dkey], key=dkey)

    def prep_w(self, nchunks, ncols, src, dst, dkey, mode, scale=None, mul=None, mulkey=None, sview=None):
        fw = self.fw
        for c in range(nchunks):
            for s0 in range(0, ncols, 2048):
                n = min(2048, ncols - s0)
                k = self.wst_i % 2
                self.wst_i += 1
                st = self.wstage[k]
                sk = "wst%d" % k
                fw.dma(st[:, 0:n], src(c, s0, n), w=[sk], key=sk)
                o = dst(c, s0, n)
                dk = dkey(c)
                if sview is not None:
                    sv_ = sview(st[:, 0:n])
                    sc = scale(c)
                    self.V(lambda eg, o=o, sv_=sv_, sc=sc: eg.tensor_scalar(o, sv_, sc, None, ALU.mult), r=[sk, "gcol"], w=[dk])
                    continue
                if mode == "plain":
                    e = ["dve", "pool", "act"][self.wst_i % 3]
                    if e == "act":
                        fw.act(o, st[:, 0:n], AF.Copy, r=[sk], w=[dk])
                    else:
                        fw.op(e, lambda eg, o=o, st=st, n=n: eg.tensor_copy(o, st[:, 0:n]), r=[sk], w=[dk])
                elif mode == "col":
                    sc = scale(c)
                    e = ["dve", "pool"][self.wst_i % 2]
                    fw.op(e, lambda eg, o=o, st=st, n=n, sc=sc: eg.tensor_scalar(o, st[:, 0:n], sc, None, ALU.mult),
                          r=[sk, "gcol"], w=[dk])
                else:
                    sc = scale(c)
                    m = mul(s0, n)
                    self.V(lambda eg, o=o, st=st, n=n, sc=sc, m=m: eg.scalar_tensor_tensor(
                        o, st[:, 0:n], sc, m, ALU.mult, ALU.mult), r=[sk, "gcol", mulkey], w=[dk])

    def norm_hT(self, xt, xk, M, hdst, hkey, identb):
        self.norm_a(xt, xk, M)
        self.norm_b(M, hdst, hkey, identb)

    def norm_a(self, xt, xk, M):
        fw = self.fw
        xn, ss, t1 = self.xn, self.ss, self.t1
        fw.act(xn[0:M, :], xt[0:M, :], AF.Square, r=[xk], w=["xn", "ss"], accum_out=ss[0:M, :])
        self.V(lambda e: e.tensor_scalar(t1[0:M, :], ss[0:M, :], 1.0 / D, 1e-6, ALU.mult, ALU.add), r=["ss"], w=["t1"])
        fw.act(t1[0:M, :], t1[0:M, :], AF.Sqrt, r=["t1"], w=["t1"])
        self.V(lambda e: e.reciprocal(t1[0:M, :], t1[0:M, :]), r=["t1"], w=["t1"])
        self.V(lambda e: e.tensor_scalar(xn[0:M, :], xt[0:M, :], t1[0:M, 0:1], None, ALU.mult), r=[xk, "t1"], w=["xn"])

    def norm_b(self, M, hdst, hkey, identb):
        fw = self.fw
        xn = self.xn
        pbk, pk = self.pb()
        for c in range(8):
            fw.tr(pbk[:, c * M:(c + 1) * M], xn[0:M, c * 128:(c + 1) * 128], identb[0:M, 0:M], r=["xn", "identb"], w=[pk])
        fw.act(hdst, pbk[:, 0:8 * M].rearrange("p (c t) -> p c t", c=8), AF.Copy, r=[pk], w=[hkey])

    def build(self):
        self.declare()
        nc = self.nc
        with ExitStack() as es:
            self.fw = fw = FW(nc, es)
            self.PS = [fw.ps("ps%d" % i, [128, 512], F32) for i in range(6)]
            self.PBK = [fw.ps("pb%d" % i, [128, 1024], BF) for i in range(2)]
            self.ASZ = 52224
            self.arena = fw.sb("arena", [128, self.ASZ])
            self.aoff = 0
            self.apeak = 0
            self.identf = self.alloc("identf", [128, 128])
            self.identb = self.alloc("identb", [128, 128], BF)
            self.cstage = self.alloc("cstage", [32, 128])
            self.wst_i = 0
            self.xn = self.alloc("xn", [128, D], BF)
            self.ss = self.alloc("ss", [128, 1])
            self.t1 = self.alloc("t1", [128, 1])
            self.gcol = self.alloc("gcol", [128, 8])
            self.xt = [self.alloc("xt%d" % i, [128, D]) for i in range(2)]
            self.sel = self.alloc("sel", [128, 8])
            fw.dma(self.sel[:], self.I["c_sel"], w=["sel"], key="sel")
            fw.dma(self.identf[:], self.I["c_ident"], w=["identf"], key="identf")
            self.V(lambda e: e.tensor_copy(self.identb[:], self.identf[:]), r=["identf"], w=["identb"])
            for l in range(2):
                for p_ in (self.pass_rwkv, self.pass_attn, self.pass_ffn):
                    mk_ = self.aoff
                    p_(l, None)
                    self.release(mk_)
            print("arena peak words", self.apeak, "of", self.ASZ)
            fw.emit()
        return nc

    def sbl(self, es2, name, shape, dt=F32):
        return self.alloc(name, shape, dt)

    def xsrc(self, l, i):
        if i < self.NT:
            src = self.I["xp"] if l == 0 else self.xbuf
            return src[i * 128:(i + 1) * 128, :], ("xb", i)
        src = self.I["xs"] if l == 0 else self.xsbuf
        return src, ("xb", i)

    def pass_rwkv(self, l, es2):
        fw, I, O, NT = self.fw, self.I, self.O, self.NT
        sbl = lambda n, s, dt=F32: self.sbl(es2, "r%d_" % l + n, s, dt)
        identb, identf = self.identb, self.identf
        W1 = sbl("W1", [128, 8, RP], BF)
        W2 = sbl("W2", [128, 8, RP], BF)
        Wg = sbl("Wg", [128, 8, D], BF)
        Wr = sbl("Wr", [128, 4, D], BF)
        lw2 = sbl("lw2", [128, RD], BF)
        lg2 = sbl("lg2", [128, RD], BF)
        bcs = {}
        for n in ["rwkv_w0", "rwkv_a0", "rwkv_k_k", "rwkv_k_a", "rwkv_r_k", "rwkv_ln_g", "rwkv_ln_b"]:
            bcs[n] = sbl(n, [128, RD])
            self.bcast_load(bcs[n][:], n + "_bc", I[n][l])
        mucol = sbl("mucol", [128, 2])
        tri = sbl("tri", [128, 256])
        mask2 = sbl("mask2", [128, 256])
        maskL = sbl("maskL", [128, 128])
        clast = sbl("clast", [128, 1])
        fw.dma(tri[:], I["c_tri"], w=["tri"], key="tri")
        fw.dma(mask2[:], I["c_mask2"], w=["mask2"], key="mask2")
        fw.dma(maskL[:], I["c_maskL"], w=["maskL"], key="maskL")
        fw.dma(clast[:], I["c_last"], w=["clast"], key="clast")
        self.col_load(self.gcol[:], "gcol", I["norm_mix_g"][l], 8)
        self.col_load(mucol[:], "mucol", I["rwkv_mu"][l, 1536:1792], 2)
        m0 = self.aoff
        self.wstage = [sbl("wst%d" % i_, [128, 2048]) for i_ in range(2)]
        mu_bc = sbl("mu_bc", [128, RP])
        omm_bc = sbl("omm_bc", [128, RP])
        self.bcast_load(mu_bc[:], "mu_bc", I["rwkv_mu"][l])
        self.V(lambda e: e.tensor_scalar(omm_bc[:], mu_bc[:], -1.0, 1.0, ALU.mult, ALU.add), r=["mu_bc"], w=["omm_bc"])
        win = I["w_in"][l]
        gsc = lambda c: self.gcol[:, c:c + 1]
        self.prep_w(8, RP, lambda c, s0, n: win[c * 128:(c + 1) * 128, s0:s0 + n],
                    lambda c, s0, n: W1[:, c, s0:s0 + n], lambda c: "W1_%d" % c, "colmul", gsc,
                    lambda s0, n: omm_bc[:, s0:s0 + n], "omm_bc")
        self.prep_w(8, RP, lambda c, s0, n: win[c * 128:(c + 1) * 128, s0:s0 + n],
                    lambda c, s0, n: W2[:, c, s0:s0 + n], lambda c: "W2_%d" % c, "colmul", gsc,
                    lambda s0, n: mu_bc[:, s0:s0 + n], "mu_bc")
        self.prep_w(8, D, lambda c, s0, n: win[c * 128:(c + 1) * 128, 2560 + s0:2560 + s0 + n],
                    lambda c, s0, n: Wg[:, c, s0:s0 + n], lambda c: "Wg_%d" % c, "col", gsc)
        wbr = I["w_br_rwkv"][l]
        self.prep_w(4, D, lambda c, s0, n: wbr[c * 128:(c + 1) * 128, s0:s0 + n],
                    lambda c, s0, n: Wr[:, c, s0:s0 + n], lambda c: "Wr_%d" % c, "plain")
        for (nm, p0, dk_) in [("rwkv_w2", 0, "lw2a"), ("rwkv_a2", 64, "lw2b")]:
            k = self.wst_i % 2
            self.wst_i += 1
            wsk = self.wstage[k]
            fw.dma(wsk[p0:p0 + 64, 0:RD], I[nm][l], w=["wst%d" % k], key="wst%d" % k)
            self.P(lambda e, wsk=wsk, p0=p0: e.tensor_copy(lw2[p0:p0 + 64, :], wsk[p0:p0 + 64, 0:RD]), r=["wst%d" % k], w=[dk_])
        self.prep_w(1, RD, lambda c, s0, n: I["rwkv_g2"][l], lambda c, s0, n: lg2[:, :], lambda c: "lg2", "plain")
        WK1 = ["W1_%d" % c for c in range(8)]
        WK2 = ["W2_%d" % c for c in range(8)]
        self.release(m0)
        class NSP:
            pass
        zr, zk = sbl("zr", [128, RD]), sbl("zk", [128, RD])
        lact = sbl("lact", [128, 128], BF)
        T = [sbl("tmp%d" % i_, [128, RD]) for i_ in range(8)]
        sm = sbl("sm", [128, 64])
        orT = sbl("orT", [128, 4, 128], BF)
        sgr = sbl("sgr", [128, 8, 128], BF)
        mrT0_ = sbl("mrT0", [128, 8, 128], BF)
        mrT = [mrT0_, mrT0_]
        TP_ = [sbl("tpost%d" % i_, [128, RD]) for i_ in range(2)]
        m1 = self.aoff
        NRB = 9864

        def mkrec(k):
            R = NSP()
            rb = sbl("RB%d" % k, [128, NRB], BF)
            rf = sbl("RF%d" % k, [128, 528])
            R.rb, R.rf, R.k = rb, rf, k
            R.RKT = rb[:, 0:1024].rearrange("p (j a t) -> p j a t", j=4, a=2)
            R.G4 = [rb[:, 1024 + j * 1280:1024 + (j + 1) * 1280].rearrange("p (h c) -> p h c", h=2) for j in range(4)]
            R.ZF = [rb[:, 6144 + j * 256:6144 + (j + 1) * 256].rearrange("p (h c) -> p h c", h=2) for j in range(4)]
            R.vb, R.ktt, R.bnt = rb[:, 7168:7680], rb[:, 7680:8192], rb[:, 8192:8704]
            R.sgT = rb[:, 8704:8832]
            R.hT = rb[:, 8832:9864].rearrange("p (c t) -> p c t", c=8)
            R.zv, R.WC, R.bon = rf[:, 0:512], rf[:, 512:516], rf[:, 516:524]
            R.K = (lambda k_: (lambda n: "%s#%d" % (n, k_)))(k)
            return R
        R0 = mkrec(0)
        U0b = [sbl("U0b%d" % j, [128, 2, 64], BF) for j in range(4)]
        Ub = sbl("Ub", [128, RD], BF)
        Nst = sbl("Nst", [128, 4, 128])
        Nb = sbl("Nb", [128, 4, 128], BF)
        self.V(lambda e: e.memset(Nst[:], 0.0), w=["Nst"])
        self.V(lambda e: e.memset(Nb[:], 0.0), w=["Nb"])
        m2 = self.aoff
        rt, kat = sbl("rt", [128, RD], BF), sbl("kat", [128, RD], BF)
        KT = sbl("KT", [128, 4, 128], BF)
        BT = sbl("BT", [128, 4, 128], BF)
        for j in range(4):
            self.P(lambda e, j=j: e.tensor_copy(R0.G4[j][:, :, 512:640], identb[:, :].unsqueeze(1).to_broadcast([128, 2, 128])),
                   r=["identb"], w=["G4_%d" % j])
        EZ = [[sbl("EZ%d_%d" % (j, a), [128, 2, 2, 128], BF) for a in range(2)] for j in range(4)]
        FFa = [sbl("FFa%d" % a, [128, 4, 2, 128], BF) for a in range(2)]
        FF = [[FFa[a][:, j] for a in range(2)] for j in range(4)]

        def tok_proj(M, hcur, hprev, hk, g0, dstkey):
            ps, pk = self.pf()
            n = 0
            for c in range(8):
                fw.mm(ps[0:M, :], hcur(c), W1[:, c, g0:g0 + 512], n == 0, False, r=[hk, WK1[c]], w=[pk])
                n += 1
            for c in range(8):
                fw.mm(ps[0:M, :], hprev(c), W2[:, c, g0:g0 + 512], False, c == 7, r=[hk, WK2[c]], w=[pk])
            return ps, pk

        def feat_proj(M, hcur, hprev, hk, g0):
            ps, pk = self.pf()
            for c in range(8):
                fw.mm(ps[:, 0:M], W1[:, c, g0:g0 + 128], hcur(c), c == 0, False, r=[hk, WK1[c]], w=[pk])
            for c in range(8):
                fw.mm(ps[:, 0:M], W2[:, c, g0:g0 + 128], hprev(c), False, c == 7, r=[hk, WK2[c]], w=[pk])
            return ps, pk

        def raw_last(hl, hk, M, dst):
            for gi, g0 in enumerate(range(0, RP, 512)):
                n = min(512, RP - g0)
                ps, pk = self.pf()
                for c in range(8):
                    fw.mm(ps[0:M, 0:n], hl(c), W1[:, c, g0:g0 + n], c == 0, False, r=[hk, WK1[c]], w=[pk])
                for c in range(8):
                    fw.mm(ps[0:M, 0:n], hl(c), W2[:, c, g0:g0 + n], False, c == 7, r=[hk, WK2[c]], w=[pk])
                fw.act(T[gi][0:M, 0:n], ps[0:M, 0:n], AF.Copy, r=[pk], w=["T%d" % gi])
                fw.dma(dst[:, g0:g0 + n], T[gi][0:M, 0:n], r=["T%d" % gi], key="zl%d" % gi)

        def prep(M, sample, R):
            K = R.K
            w0, a0 = bcs["rwkv_w0"], bcs["rwkv_a0"]
            kkb, kab, rkb = bcs["rwkv_k_k"], bcs["rwkv_k_a"], bcs["rwkv_r_k"]
            pw, pwk = self.pf()
            fw.mm(pw[0:M, :], lact[0:64, 0:M], lw2[0:64, :], True, True, r=["lact", "lw2a"], w=[pwk])
            pa, pak = self.pf()
            fw.mm(pa[0:M, :], lact[64:128, 0:M], lw2[64:128, :], True, True, r=["lact", "lw2b"], w=[pak])
            sg, a_, kk, t3, kf, be = T[0], T[1], T[2], T[3], T[4], T[5]
            self.V(lambda e: e.tensor_tensor(sg[0:M, :], pw[0:M, :], w0[0:M, :], ALU.add), r=[pwk, "rwkv_w0_bc"], w=["T0"])
            fw.act(sg[0:M, :], sg[0:M, :], AF.Sigmoid, r=["T0"], w=["T0"])
            self.V(lambda e: e.tensor_tensor(a_[0:M, :], pa[0:M, :], a0[0:M, :], ALU.add), r=[pak, "rwkv_a0_bc"], w=["T1"])
            fw.act(a_[0:M, :], a_[0:M, :], AF.Sigmoid, r=["T1"], w=["T1"])
            self.P(lambda e: e.tensor_tensor(kk[0:M, :], zk[0:M, :], kkb[0:M, :], ALU.mult), r=["zk", "rwkv_k_k_bc"], w=["T2"])
            self.P(lambda e: e.tensor_tensor(t3[0:M, :], kk[0:M, :], kk[0:M, :], ALU.mult), r=["T2"], w=["T3"])
            self.V(lambda e: e.tensor_reduce(sm[0:M, 0:8], h3(t3[0:M, :]), AX.X, ALU.add), r=["T3"], w=["sm0"])
            fw.act(sm[0:M, 0:8], sm[0:M, 0:8], AF.Sqrt, r=["sm0"], w=["sm0"])
            self.V(lambda e: e.tensor_scalar(sm[0:M, 0:8], sm[0:M, 0:8], 1e-12, None, ALU.max), r=["sm0"], w=["sm0"])
            self.V(lambda e: e.reciprocal(sm[0:M, 0:8], sm[0:M, 0:8]), r=["sm0"], w=["sm0"])
            self.V(lambda e: e.tensor_tensor(h3(kk[0:M, :]), h3(kk[0:M, :]), bc3(sm[0:M, 0:8], 64), ALU.mult),
                   r=["T2", "sm0"], w=["T2"])
            self.V(lambda e: e.scalar_tensor_tensor(t3[0:M, :], a_[0:M, :], -1.0, kab[0:M, :], ALU.add, ALU.mult),
                   r=["T1", "rwkv_k_a_bc"], w=["T3"])
            self.V(lambda e: e.scalar_tensor_tensor(kf[0:M, :], t3[0:M, :], 1.0, zk[0:M, :], ALU.add, ALU.mult),
                   r=["T3", "zk"], w=["T4"])
            self.P(lambda e: e.tensor_tensor(be[0:M, :], kk[0:M, :], a_[0:M, :], ALU.mult), r=["T2", "T1"], w=["T5"])
            self.P(lambda e: e.tensor_tensor(t3[0:M, :], zr[0:M, :], kf[0:M, :], ALU.mult), r=["zr", "T4"], w=["T3"])
            self.P(lambda e: e.tensor_tensor(t3[0:M, :], t3[0:M, :], rkb[0:M, :], ALU.mult), r=["T3", "rwkv_r_k_bc"], w=["T3"])
            self.V(lambda e, R=R: e.tensor_reduce(R.bon[0:M, :], h3(t3[0:M, :]), AX.X, ALU.add), r=["T3"], w=[K("bon")])
            if sample:
                fw.act(T[6][0:M, :], sg[0:M, :], AF.Exp, r=["T0"], w=["T6"], scale=CDEC)
                for x, (tl, tk) in enumerate([(zr, "zr"), (T[6], "T6"), (kf, "T4"), (R.zv, K("zv")), (kk, "T2"), (be, "T5")]):
                    fw.dma(self.sq[x], tl[0:M, :], r=[tk], w=[("sq", x)], key="sqw%d" % x)
                return
            pli, plik = self.pf()
            fw.mm(pli[:, :], tri[:, 0:128], sg[:, :], True, True, r=["tri", "T0"], w=[plik])
            ple, plek = self.pf()
            fw.mm(ple[:, :], tri[:, 128:256], sg[:, :], True, True, r=["tri", "T0"], w=[plek])
            eL, eLm, enL = T[6], T[7], T[3]
            fw.act(eL[:, :], pli[:, :], AF.Exp, r=[plik], w=["T6"])
            fw.act(eLm[:, :], ple[:, :], AF.Exp, r=[plek], w=["T7"])
            fw.act(enL[:, :], pli[:, :], AF.Exp, r=[plik], w=["T3"], scale=-1.0)
            self.V(lambda e: e.tensor_tensor(rt[:, :], zr[:, :], eL[:, :], ALU.mult), r=["zr", "T6"], w=["rt"])
            self.V(lambda e: e.tensor_tensor(kat[:, :], kk[:, :], eLm[:, :], ALU.mult), r=["T2", "T7"], w=["kat"])
            self.P(lambda e, R=R: e.tensor_tensor(R.ktt[:, :], kf[:, :], enL[:, :], ALU.mult), r=["T4", "T3"], w=[K("ktt")])
            self.V(lambda e, R=R: e.scalar_tensor_tensor(R.bnt[:, :], be[:, :], -1.0, enL[:, :], ALU.mult, ALU.mult),
                   r=["T5", "T3"], w=[K("bnt")])
            fw.act(R.vb[:, :], R.zv[:, :], AF.Copy, r=[K("zv")], w=[K("vb")])
            pwc, pwck = self.pf()
            for j in range(4):
                fw.mm(pwc[:, j:j + 1], eL[:, j * 128:(j + 1) * 128], clast[:, :], True, True, r=["T6", "clast"], w=[pwck])
            fw.act(R.WC[:, :], pwc[:, 0:4], AF.Copy, r=[pwck], w=[K("WC")])
            for (src, skey, dstf, dk) in [(rt, "rt", None, "RKT"), (kat, "kat", None, "RKT"),
                                          (R.ktt, K("ktt"), None, "KT"), (R.bnt, K("bnt"), None, "BT")]:
                pbk, pk = self.pb()
                for j in range(4):
                    fw.tr(pbk[:, j * 128:(j + 1) * 128], src[:, j * 128:(j + 1) * 128], identb[:, :], r=[skey, "identb"], w=[pk])
                if dk == "RKT":
                    which = 0 if skey == "rt" else 1
                    fw.act(R.RKT[:, :, which, :], pbk[:, 0:512].rearrange("p (j t) -> p j t", j=4), AF.Copy, r=[pk], w=["RKT%d" % which])
                else:
                    dst = KT if dk == "KT" else BT
                    self.V(lambda e, dst=dst, pbk=pbk: e.tensor_copy(dst[:, :, :], pbk[:, 0:512].rearrange("p (j t) -> p j t", j=4)),
                           r=[pk], w=[dk])

        def stageAB(R):
            K = R.K
            RK = [K("RKT0"), K("RKT1")]
            RKT, G4, ZF = R.RKT, R.G4, R.ZF
            zb = [self.pf(), self.pf()]
            for j in range(4):
                for hh in range(2):
                    o = hh * 64
                    pZ, pzk = zb[hh]
                    fw.mm(pZ[:, j * 128:(j + 1) * 128], RKT[o:o + 64, j, 1, :], BT[o:o + 64, j, :], True, True, r=["BT", K("RKT1")], w=[pzk])
            mlb = maskL[:, :].unsqueeze(1).to_broadcast([128, 4, 128])
            for hh in range(2):
                pZ, pzk = zb[hh]
                self.V(lambda e, pZ=pZ, hh=hh: e.tensor_tensor(FFa[0][:, :, hh, :], pZ[:, :].rearrange("p (j c) -> p j c", j=4), mlb, ALU.mult),
                       r=[pzk, "maskL"], w=["FF%d_0" % j for j in range(4)])
            for j in range(4):
                bk = [self.pf(), self.pf()]
                for hh in range(2):
                    o = hh * 64
                    ps, pk = bk[hh]
                    rhs = RKT[o:o + 64, j, :, :].rearrange("p a t -> p (a t)")
                    fw.mm(ps[:, 0:256], KT[o:o + 64, j, :], rhs, True, True, r=["KT"] + RK, w=[pk])
                    fw.mm(ps[:, 256:512], BT[o:o + 64, j, :], rhs, True, True, r=["BT"] + RK, w=[pk])
                for hh in range(2):
                    ps, pk = bk[hh]
                    self.V(lambda e, j=j, hh=hh, ps=ps, G4=G4: e.tensor_tensor(
                        G4[j][:, hh, 0:512].rearrange("p (a c) -> p a c", a=2), ps[:, :].rearrange("p (a c) -> p a c", a=2),
                        mask2[:, :].unsqueeze(1).to_broadcast([128, 2, 256]), ALU.mult), r=[pk, "mask2"], w=[K("G4_%d" % j)])
            for lev in range(7):
                a, b = lev % 2, (lev + 1) % 2
                for j in range(4):
                    fk, fn_ = "FF%d_%d" % (j, a), "FF%d_%d" % (j, b)
                    ezn = "EZ%d_%d" % (j, b)
                    if lev == 0:
                        ezk = K("G4_%d" % j)
                        EZs = lambda hh, j=j, G4=G4: G4[j][:, hh, 384:640]
                        Es = lambda hh, j=j, G4=G4: G4[j][:, hh, 384:512]
                        Zs = lambda j=j, G4=G4: G4[j][:, :, 512:640]
                    else:
                        ezk = "EZ%d_%d" % (j, a)
                        EZs = lambda hh, j=j, a=a: EZ[j][a][:, hh, :, :].rearrange("p a t -> p (a t)")
                        Es = lambda hh, j=j, a=a: EZ[j][a][:, hh, 0, :]
                        Zs = lambda j=j, a=a: EZ[j][a][:, :, 1, :]
                    if lev < 6:
                        pL, plk = self.pf()
                        for hh in range(2):
                            fw.mm(pL[:, hh * 256:(hh + 1) * 256], FF[j][a][:, hh, :], EZs(hh), True, True, r=[ezk, fk], w=[plk])
                        pF, pfk = self.pf()
                        for hh in range(2):
                            fw.mm(pF[:, hh * 128:(hh + 1) * 128], Es(hh), FF[j][a][:, hh, :], True, True, r=[ezk, fk], w=[pfk])
                        l3 = pL[:, :].rearrange("p (h c) -> p h c", h=2)
                        fw.act(EZ[j][b][:, :, 0, :], l3[:, :, 0:128], AF.Copy, r=[plk], w=[ezn])
                        self.V(lambda e, j=j, b=b, l3=l3, Zs=Zs: e.tensor_tensor(EZ[j][b][:, :, 1, :], l3[:, :, 128:256], Zs(), ALU.add),
                               r=[plk, ezk], w=[ezn])
                        fw.act(FF[j][b][:, :, :], pF[:, 0:256].rearrange("p (h c) -> p h c", h=2), AF.Copy, r=[pfk], w=[fn_])
                    else:
                        pL, plk = self.pf()
                        for hh in range(2):
                            fw.mm(pL[:, hh * 128:(hh + 1) * 128], FF[j][a][:, hh, :], EZ[j][a][:, hh, 1, :], True, True, r=[ezk, fk], w=[plk])
                        self.V(lambda e, j=j, a=a, pL=pL, ZF=ZF: e.tensor_tensor(ZF[j][:, :, :], pL[:, 0:256].rearrange("p (h c) -> p h c", h=2),
                                                                      EZ[j][a][:, :, 1, :], ALU.add), r=[plk, ezk], w=[K("ZF%d" % j)])

        def stageC(R):
            K = R.K
            RKT, G4, ZF, vb = R.RKT, R.G4, R.ZF, R.vb
            for j in range(4):
                pU, puk = self.pf()
                for hh in range(2):
                    o, h = hh * 64, 2 * j + hh
                    fw.mm(pU[:, hh * 64:(hh + 1) * 64], RKT[o:o + 64, j, 1, :], Nb[o:o + 64, j, o:o + 64], True, False, r=[K("RKT1"), "Nb"], w=[puk])
                    fw.mm(pU[:, hh * 64:(hh + 1) * 64], G4[j][:, hh, 128:256], vb[:, h * 64:(h + 1) * 64], False, True, r=[K("G4_%d" % j), K("vb")], w=[puk])
                fw.act(U0b[j][:, :, :], pU[:, 0:128].rearrange("p (h c) -> p h c", h=2), AF.Copy, r=[puk], w=["U0b%d" % j])
            for j in range(4):
                pU, puk = self.pf()
                for hh in range(2):
                    fw.mm(pU[:, hh * 64:(hh + 1) * 64], ZF[j][:, hh, :], U0b[j][:, hh, :], True, True, r=[K("ZF%d" % j), "U0b%d" % j], w=[puk])
                fw.act(Ub[:, j * 128:(j + 1) * 128], pU[:, 0:128], AF.Copy, r=[puk], w=["Ub%d" % j])

        def stageD(R):
            K = R.K
            RKT, G4, vb = R.RKT, R.G4, R.vb
            psY, pyk = self.pf()
            for j in range(4):
                for hh in range(2):
                    o, h = hh * 64, 2 * j + hh
                    fw.mm(psY[:, h * 64:(h + 1) * 64], RKT[o:o + 64, j, 0, :], Nb[o:o + 64, j, o:o + 64], True, False, r=[K("RKT0"), "Nb"], w=[pyk])
                    fw.mm(psY[:, h * 64:(h + 1) * 64], G4[j][:, hh, 0:128], vb[:, h * 64:(h + 1) * 64], False, False, r=[K("G4_%d" % j), K("vb")], w=[pyk])
                    fw.mm(psY[:, h * 64:(h + 1) * 64], G4[j][:, hh, 256:384], Ub[:, h * 64:(h + 1) * 64], False, True, r=[K("G4_%d" % j), "Ub%d" % j], w=[pyk])
            return psY, pyk

        def n_update(R):
            K = R.K
            ktt, bnt, vb, WC = R.ktt, R.bnt, R.vb, R.WC
            pN, pnk = self.pf()
            for j in range(4):
                fw.mm(pN[:, j * 128:(j + 1) * 128], ktt[:, j * 128:(j + 1) * 128], vb[:, j * 128:(j + 1) * 128], True, False, r=[K("ktt"), K("vb")], w=[pnk])
                fw.mm(pN[:, j * 128:(j + 1) * 128], bnt[:, j * 128:(j + 1) * 128], Ub[:, j * 128:(j + 1) * 128], False, True, r=[K("bnt"), "Ub%d" % j], w=[pnk])
            n2 = Nst[:, :, :].rearrange("p j c -> p (j c)")
            self.V(lambda e: e.tensor_tensor(n2, pN[:, :], n2, ALU.add), r=[pnk, "Nst"], w=["Nst"])
            self.V(lambda e, WC=WC: e.tensor_tensor(Nst[:, :, :], Nst[:, :, :], bc3(WC[:, :], 128), ALU.mult), r=["Nst", K("WC")], w=["Nst"])
            fw.act(Nb[:, :, :], Nst[:, :, :], AF.Copy, r=["Nst"], w=["Nb"])


        def post(M, yap, ykeys, pg, pgk, R):
            K = R.K
            lng, lnb = bcs["rwkv_ln_g"], bcs["rwkv_ln_b"]
            y2, yc = TP_[0], TP_[1]
            ob = TP_[0].bitcast(BF)[:, 0:RD]
            self.V(lambda e: e.tensor_reduce(sm[0:M, 16:24], h3(yap), AX.X, ALU.add), r=ykeys, w=["sm2"])
            fw.act(y2[0:M, :], yap, AF.Square, r=ykeys, w=["TP0"])
            self.V(lambda e: e.tensor_reduce(sm[0:M, 24:32], h3(y2[0:M, :]), AX.X, ALU.add), r=["TP0"], w=["sm3"])
            mean, var = sm[0:M, 16:24], sm[0:M, 24:32]
            self.V(lambda e: e.tensor_scalar(mean, mean, 1.0 / 64, None, ALU.mult), r=["sm2"], w=["sm2"])
            self.V(lambda e: e.tensor_tensor(sm[0:M, 32:40], mean, mean, ALU.mult), r=["sm2"], w=["sm4"])
            self.V(lambda e: e.scalar_tensor_tensor(var, var, 1.0 / 64, sm[0:M, 32:40], ALU.mult, ALU.subtract), r=["sm3", "sm4"], w=["sm3"])
            self.V(lambda e: e.tensor_scalar(var, var, 64e-5, None, ALU.add), r=["sm3"], w=["sm3"])
            fw.act(var, var, AF.Sqrt, r=["sm3"], w=["sm3"])
            self.V(lambda e: e.reciprocal(var, var), r=["sm3"], w=["sm3"])
            self.V(lambda e: e.tensor_tensor(h3(yc[0:M, :]), h3(yap), bc3(mean, 64), ALU.subtract), r=list(ykeys) + ["sm2"], w=["TP1"])
            self.V(lambda e: e.tensor_tensor(h3(yc[0:M, :]), h3(yc[0:M, :]), bc3(var, 64), ALU.mult), r=["TP1", "sm3"], w=["TP1"])
            self.P(lambda e: e.tensor_tensor(yc[0:M, :], yc[0:M, :], lng[0:M, :], ALU.mult), r=["TP1", "rwkv_ln_g_bc"], w=["TP1"])
            self.P(lambda e: e.tensor_tensor(yc[0:M, :], yc[0:M, :], lnb[0:M, :], ALU.add), r=["TP1", "rwkv_ln_b_bc"], w=["TP1"])
            self.P(lambda e, R=R: e.tensor_tensor(h3(y2[0:M, :]), h3(R.zv[0:M, :]), bc3(R.bon[0:M, :], 64), ALU.mult), r=[K("zv"), K("bon")], w=["TP0"])
            self.V(lambda e: e.tensor_tensor(yc[0:M, :], yc[0:M, :], y2[0:M, :], ALU.add), r=["TP1", "TP0"], w=["TP1"])
            self.V(lambda e: e.tensor_tensor(ob[0:M, :], yc[0:M, :], pg[0:M, :], ALU.mult), r=["TP1", pgk], w=["TP0"])
            pbk, pk = self.pb()
            for j in range(4):
                fw.tr(pbk[:, j * M:(j + 1) * M], ob[0:M, j * 128:(j + 1) * 128], identb[0:M, 0:M], r=["TP0", "identb"], w=[pk])
            fw.act(orT[:, :, 0:M], pbk[:, 0:4 * M].rearrange("p (j t) -> p j t", j=4), AF.Copy, r=[pk], w=["orT"])

        def gate_branch(M, hcur, hk, mdst, mkey):
            for half in range(2):
                pg, pgk = self.pf()
                for q in range(4):
                    dc = half * 4 + q
                    for c in range(8):
                        fw.mm(pg[:, q * M:(q + 1) * M], Wg[:, c, dc * 128:(dc + 1) * 128], hcur(c), c == 0, c == 7, r=[hk, "Wg_%d" % c], w=[pgk])
                fw.act(sgr[:, half * 4:(half + 1) * 4, 0:M], pg[:, 0:4 * M].rearrange("p (q t) -> p q t", q=4), AF.Sigmoid, r=[pgk], w=["sgr%d" % half])
                pbr, pbk_ = self.pf()
                for q in range(4):
                    dc = half * 4 + q
                    for j in range(4):
                        fw.mm(pbr[:, q * M:(q + 1) * M], Wr[:, j, dc * 128:(dc + 1) * 128], orT[:, j, 0:M], j == 0, j == 3, r=["orT", "Wr_%d" % j], w=[pbk_])
                self.V(lambda e, half=half, pbr=pbr: e.tensor_tensor(mdst[:, half * 4:(half + 1) * 4, 0:M], sgr[:, half * 4:(half + 1) * 4, 0:M],
                                                                 pbr[:, 0:4 * M].rearrange("p (q t) -> p q t", q=4), ALU.mult),
                       r=["sgr%d" % half, pbk_], w=[mkey])

        R1 = mkrec(1)
        for j in range(4):
            self.P(lambda e, j=j: e.tensor_copy(R1.G4[j][:, :, 512:640], identb[:, :].unsqueeze(1).to_broadcast([128, 2, 128])),
                   r=["identb"], w=[R1.K("G4_%d" % j)])
        RR = [R0, R1]

        def H1a(i):
            R, Rp = RR[i % 2], RR[(i + 1) % 2]
            hT = R.hT
            xt, xk = self.xt[i % 2], "xt%d" % (i % 2)
            src, _ = self.xsrc(l, i)
            fw.dma(xt[:], src, r=[("xb", i)], w=[xk], key=xk)
            hk = R.K("hTr")
            if i == 0:
                self.V(lambda e, hT=hT: e.memset(hT[:, :, 0:1], 0.0), w=[hk])
            else:
                self.P(lambda e, hT=hT, hp=Rp.hT: e.tensor_copy(hT[:, :, 0:1], hp[:, :, 128:129]), r=[Rp.K("hTr")], w=[hk])
            self.norm_a(xt, xk, 128)

        def H1b(i):
            R = RR[i % 2]
            K = R.K
            hT = R.hT
            hk = K("hTr")
            self.norm_b(128, hT[:, :, 1:129], hk, identb)
            hcur = lambda c, hT=hT: hT[:, c, 1:129]
            hprev = lambda c, hT=hT: hT[:, c, 0:128]
            for g0, dst, dk in [(0, zr, "zr"), (512, zk, "zk"), (1024, R.zv, K("zv"))]:
                ps, pk = tok_proj(128, hcur, hprev, hk, g0, dk)
                fw.act(dst[:, :], ps[:, :], AF.Copy, r=[pk], w=[dk])
            ps, pk = feat_proj(128, hcur, hprev, hk, 1536)
            fw.act(lact[0:64, :], ps[0:64, 0:128], AF.Tanh, r=[pk], w=["lact"])
            fw.act(lact[64:128, :], ps[64:128, 0:128], AF.Copy, r=[pk], w=["lact"])
            ps, pk = feat_proj(128, hcur, hprev, hk, 1664)
            fw.act(R.sgT[:, :], ps[:, 0:128], AF.Sigmoid, r=[pk], w=[K("sgT")])
            if i == NT - 1:
                raw_last(lambda c, hT=hT: hT[:, c, 128:129], hk, 1, O["p_shift"][l:l + 1, :])

        def H1c(i):
            prep(128, False, RR[i % 2])

        def H1d(i):
            stageAB(RR[i % 2])

        H2st = {}

        def H2a(i):
            R = RR[i % 2]
            stageC(R)
            psY, pyk = stageD(R)
            n_update(R)
            pg, pgk = self.pf()
            fw.mm(pg[:, :], R.sgT[:, :], lg2[:, :], True, True, r=[R.K("sgT"), "lg2"], w=[pgk])
            H2st[i] = (psY, pyk, pg, pgk)

        def H2b(i):
            psY, pyk, pg, pgk = H2st.pop(i)
            post(128, psY[:, :], [pyk], pg, pgk, RR[i % 2])

        def H2c(i):
            R = RR[i % 2]
            m, mk = mrT[0], "mrT0"
            gate_branch(128, lambda c, R=R: R.hT[:, c, 1:129], R.K("hTr"), m, mk)
            fw.dma(self.mrbuf[i].rearrange("p (c t) -> p c t", c=8), m[:, :, :], r=[mk], w=[("mr", i)], key=mk)

        def cap(pool, f, i):
            self.pool = pool
            return fw.capture(lambda: f(i))

        for f in (H1a, H1b, H1c, H1d):
            fw.replay([cap(0, f, 0)])
        for i in range(NT):
            nx = i + 1 < NT
            if nx:
                fw.replay([cap(0, H1a, i + 1)])
            fw.replay([cap(1, H2a, i)])
            fw.replay(([cap(0, H1b, i + 1)] if nx else []) + [cap(1, H2b, i)])
            fw.replay(([cap(0, H1c, i + 1)] if nx else []) + [cap(1, H2c, i)])
            if nx:
                fw.replay([cap(0, H1d, i + 1)])
        self.pool = None
        for j in range(4):
            ps, pk = self.pf()
            fw.tr(ps[:, 0:128], Nst[:, j, :], identf[:, :], r=["Nst", "identf"], w=[pk])
            fw.act(T[0][:, j * 128:(j + 1) * 128], ps[:, 0:128], AF.Copy, r=[pk], w=["T0"])
        for h_ in range(8):
            j, o = h_ // 2, (h_ % 2) * 64
            fw.dma(O["p_wkv"][l, h_], T[0][o:o + 64, j * 128 + o:j * 128 + o + 64], r=["T0"], key="T0")

        self.release(m1)
        RS = NSP()
        RS.zv = sbl("zv_s", [128, RD])
        RS.sgT = sbl("sgT_s", [128, 128], BF)
        RS.bon = sbl("bon_s", [128, 8])
        RS.K = lambda n: n + "#s"
        hTs = sbl("hTs", [128, 8, 80], BF)
        sadd = sbl("sadd", [16, RP])
        stT = sbl("stT", [128, 2, 16])
        zf = sbl("zf", [128, 2, 64])
        QH = sbl("QH", [128, 6, 4, 64])
        Sst = sbl("Sst", [128, 64, 64])
        Stmp = sbl("Stmp", [128, 64, 64])
        sk = sbl("sk", [128, 64])
        yh = sbl("yh", [128, 4, 64])
        ytm = T[7]
        self.V(lambda e: e.memset(hTs[:], 0.0), w=["hTs"])
        i = NT
        xt, xk = self.xt[i % 2], "xt%d" % (i % 2)
        src, _ = self.xsrc(l, i)
        fw.dma(xt[0:MS, :], src, r=[("xb", i)], w=[xk], key=xk)
        self.norm_hT(xt, xk, MS, hTs[:, :, 16:80], "hTs", identb)
        hcur = lambda c: hTs[:, c, 16:80]
        hprev = lambda c: hTs[:, c, 0:64]
        fw.dma(sadd[:, :], I["st_shift"][l], w=["sadd"], key="sadd")
        for q in range(2):
            ps, pk = self.pf()
            fw.tr(ps[:, 0:16], sadd[0:16, 1536 + q * 128:1536 + (q + 1) * 128], identf[0:16, 0:16], r=["sadd", "identf"], w=[pk])
            self.V(lambda e, q=q, ps=ps: e.tensor_scalar(stT[:, q, :], ps[:, 0:16], mucol[:, q:q + 1], None, ALU.mult), r=[pk, "mucol"], w=["stT"])
        for gi, g0 in enumerate(range(0, RP, 512)):
            n = min(512, RP - g0)
            self.bcast_load(T[4 + gi][0:16, 0:n], "T%d" % (4 + gi), I["rwkv_mu"][l, g0:g0 + n])
            self.V(lambda e, gi=gi, g0=g0, n=n: e.tensor_tensor(sadd[:, g0:g0 + n], sadd[:, g0:g0 + n], T[4 + gi][0:16, 0:n], ALU.mult),
                   r=["sadd", "T%d" % (4 + gi)], w=["sadd"])
        zv, sgT = RS.zv, RS.sgT
        for g0, dst, dk in [(0, zr, "zr"), (512, zk, "zk"), (1024, zv, RS.K("zv"))]:
            ps, pk = tok_proj(MS, hcur, hprev, "hTs", g0, dk)
            fw.act(dst[0:MS, :], ps[0:MS, :], AF.Copy, r=[pk], w=[dk])
            self.V(lambda e, dst=dst, g0=g0: e.tensor_tensor(dst[0:16, :], dst[0:16, :], sadd[0:16, g0:g0 + 512], ALU.add), r=[dk, "sadd"], w=[dk])
        for q, g0 in enumerate([1536, 1664]):
            ps, pk = feat_proj(MS, hcur, hprev, "hTs", g0)
            fw.act(zf[:, q, :], ps[:, 0:MS], AF.Copy, r=[pk], w=["zf"])
            self.V(lambda e, q=q: e.tensor_tensor(zf[:, q, 0:16], zf[:, q, 0:16], stT[:, q, :], ALU.add), r=["zf", "stT"], w=["zf"])
        fw.act(lact[0:64, 0:MS], zf[0:64, 0, :], AF.Tanh, r=["zf"], w=["lact"])
        fw.act(lact[64:128, 0:MS], zf[64:128, 0, :], AF.Copy, r=["zf"], w=["lact"])
        fw.act(sgT[:, 0:MS], zf[:, 1, :], AF.Sigmoid, r=["zf"], w=[RS.K("sgT")])
        prep(MS, True, RS)
        if l == 0:
            for nm, ap, k in [("s_zr", zr, "zr"), ("s_zk", zk, "zk"), ("s_zv", zv, "zv"), ("s_dec", T[6], "T6"), ("s_kk", T[2], "T2"),
                              ("s_kf", T[4], "T4"), ("s_be", T[5], "T5"), ("s_a", T[1], "T1")]:
                self.tap(nm, ap[0:MS, :], [k])
        sqv = self.sq.rearrange("x (t q) (h d) -> (q h) x t d", t=4, h=NH)
        for x in range(6):
            fw.dma(QH[:, x, :, :], sqv[:, x, :, :], r=[("sq", x)], w=["QH"], key="QH")
        fw.dma(Sst[:, :, :].rearrange("p v k -> p (v k)"), I["st_wkv"][l], w=["Sst"], key="Sst")
        for t in range(4):
            r_, w_, k_, v_, kk_, b_ = (QH[:, x, t, :] for x in range(6))
            rowb = lambda a: a.unsqueeze(1).to_broadcast([128, 64, 64])
            colb = lambda a: a.unsqueeze(2).to_broadcast([128, 64, 64])
            self.V(lambda e, kk_=kk_: e.tensor_tensor(Stmp[:, :, :], Sst[:, :, :], rowb(kk_), ALU.mult), r=["Sst", "QH"], w=["Stmp"])
            self.V(lambda e: e.tensor_reduce(sk[:, :], Stmp[:, :, :], AX.X, ALU.add), r=["Stmp"], w=["sk"])
            self.P(lambda e, w_=w_: e.tensor_tensor(Sst[:, :, :], Sst[:, :, :], rowb(w_), ALU.mult), r=["Sst", "QH", "Stmp"], w=["Sst"])
            self.V(lambda e, b_=b_: e.tensor_tensor(Stmp[:, :, :], colb(sk[:, :]), rowb(b_), ALU.mult), r=["sk", "QH"], w=["Stmp"])
            self.V(lambda e: e.tensor_tensor(Sst[:, :, :], Sst[:, :, :], Stmp[:, :, :], ALU.subtract), r=["Sst", "Stmp"], w=["Sst"])
            self.P(lambda e, v_=v_, k_=k_: e.tensor_tensor(Stmp[:, :, :], colb(v_), rowb(k_), ALU.mult), r=["QH", "Sst"], w=["Stmp"])
            self.V(lambda e: e.tensor_tensor(Sst[:, :, :], Sst[:, :, :], Stmp[:, :, :], ALU.add), r=["Sst", "Stmp"], w=["Sst"])
            self.P(lambda e, r_=r_: e.tensor_tensor(Stmp[:, :, :], Sst[:, :, :], rowb(r_), ALU.mult), r=["Sst", "QH"], w=["Stmp"])
            self.V(lambda e, t=t: e.tensor_reduce(yh[:, t, :], Stmp[:, :, :], AX.X, ALU.add), r=["Stmp"], w=["yh"])
        fw.dma(O["s_wkv"][l], Sst[:, :, :].rearrange("p v k -> p (v k)"), r=["Sst"], key="Sst")
        if l == 0:
            self.tap("s_QH", QH, ["QH"])
            self.tap("s_yh", yh, ["yh"])
        fw.dma(self.sy.rearrange("(t q) (h d) -> (q h) t d", t=4, h=NH), yh[:, :, :], r=["yh"], w=["sy"], key="yh")
        fw.dma(ytm[0:MS, :], self.sy, r=["sy"], w=["T7"], key="ytm")
        pg, pgk = self.pf()
        fw.mm(pg[0:MS, :], sgT[:, 0:MS], lg2[:, :], True, True, r=[RS.K("sgT"), "lg2"], w=[pgk])
        post(MS, ytm[0:MS, :], ["T7"], pg, pgk, RS)
        m, mk = mrT[0], "mrT0"
        gate_branch(MS, hcur, "hTs", m, mk)
        fw.dma(self.mrbuf[NT].rearrange("p (c t) -> p c t", c=8)[:, :, 0:MS], m[:, :, 0:MS], r=[mk], w=[("mr", NT)], key=mk)
        raw_last(lambda c: hTs[:, c, 64:80], "hTs", 16, O["s_shift"][l])

    def pass_attn(self, l, es2):
        fw, I, O, NT = self.fw, self.I, self.O, self.NT
        sbl = lambda n, s, dt=F32: self.sbl(es2, "a%d_" % l + n, s, dt)
        identb, identf = self.identb, self.identf
        Wq = sbl("Wq", [128, 8, 768], BF)
        Wg = sbl("Wg", [128, 8, D], BF)
        Wa = sbl("Wa", [128, 4, D], BF)
        Wo = sbl("Wo", [128, 8, D], BF)
        self.col_load(self.gcol[:], "gcol", I["norm_mix_g"][l], 8)
        m0 = self.aoff
        self.wstage = [sbl("wst%d" % i_, [128, 2048]) for i_ in range(2)]
        win = I["w_in"][l]
        gsc = lambda c: self.gcol[:, c:c + 1]
        self.prep_w(8, 512, lambda c, s0, n: win[c * 128:(c + 1) * 128, RP:RP + 512],
                    lambda c, s0, n: Wq[:, c, 0:512].rearrange("p (j g d) -> p g j d", j=4, g=2), lambda c: "Wq_%d" % c, "col", gsc,
                    sview=lambda a: a.rearrange("p (g j d) -> p g j d", g=2, j=4))
        self.prep_w(8, 256, lambda c, s0, n: win[c * 128:(c + 1) * 128, RP + 512:RP + 768],
                    lambda c, s0, n: Wq[:, c, 512:768], lambda c: "Wq_%d" % c, "col", gsc)
        self.prep_w(8, D, lambda c, s0, n: win[c * 128:(c + 1) * 128, 3584 + s0:3584 + s0 + n],
                    lambda c, s0, n: Wg[:, c, s0:s0 + n], lambda c: "Wga_%d" % c, "col", gsc)
        wbr = I["w_br_attn"][l]
        self.prep_w(4, D, lambda c, s0, n: wbr[c * 128:(c + 1) * 128, s0:s0 + n],
                    lambda c, s0, n: Wa[:, c, s0:s0 + n], lambda c: "Wa_%d" % c, "plain")
        wo = I["w_out"][l]
        self.prep_w(8, D, lambda c, s0, n: wo[c * 128:(c + 1) * 128, s0:s0 + n],
                    lambda c, s0, n: Wo[:, c, s0:s0 + n], lambda c: "Wo_%d" % c, "plain")
        self.release(m0)
        amask = sbl("amask", [128, 1024])
        fw.dma(amask[:, 0:768], I["c_amask"], w=["amask"], key="amask")
        fw.dma(amask[:, 768:1024], I["c_amask0"], w=["amask"], key="amask")
        smask = sbl("smask", [32, 132])
        fw.dma(smask[:], I["c_smask"], w=["smask"], key="smask")
        sinks = sbl("sinks", [128, NH])
        self.bcast_load(sinks[:], "sinks", I["attn_sinks"][l])
        hT = sbl("hT", [128, 8, 128], BF)
        qkv = sbl("qkv", [128, 768])
        rot = sbl("rot", [128, 640])
        rtmp = [sbl("rtmp%d" % i, [128, 320]) for i in range(2)]
        rotb = sbl("rotb", [128, 640], BF)
        cs = [sbl("cs%d" % i, [128, 64]) for i in range(2)]
        qT = sbl("qT", [128, 4, 128], BF)
        KTr = sbl("KTr", [128, 2, 128], BF)
        Vp = sbl("Vp", [128, 2, 2, 2, 128], BF)
        sc = sbl("sc", [128, 4, 256])
        st = sbl("st", [128, 16])
        pbf = sbl("pbf", [128, 4, 256], BF)
        pT = sbl("pT", [128, 4, 2, 128], BF)
        oT = sbl("oT", [128, 4, 128], BF)
        sga = sbl("sga", [128, 8, 128])
        mrl = [sbl("mrl%d" % i, [128, 8, 128], BF) for i in range(2)]
        mg = sbl("mg", [128, 8, 128], BF)
        xo = [sbl("xo%d" % i, [128, D]) for i in range(2)]
        KA = sbl("KA", [128, NS, 128])
        VA = sbl("VA", [128, NS, 128])
        VAb = sbl("VAb", [128, NS, 128], BF)
        KB = sbl("KB", [4, NS, 128])
        VBt = sbl("VB", [4, NS, 128])
        VBb = sbl("VBb", [4, NS, 128], BF)
        KAT = sbl("KAT", [128, NS, 128], BF)
        KBT = sbl("KBT", [128, NS, 4], BF)
        qbd = sbl("qbd", [128, NS, 32], BF)
        ssc = sbl("ssc", [32, NS, 132])
        sst = sbl("sst", [32, 4 * NS])
        spb = sbl("spb", [32, NS, 132], BF)
        spT = sbl("spT", [128, NS, 32], BF)
        spTB = sbl("spTB", [4, NS, 32], BF)
        oTs = sbl("oTs", [128, 4, MS], BF)

        self.V(lambda e: e.memset(Vp[:], 0.0), w=["Vp0", "Vp1"])
        self.V(lambda e: e.memset(KTr[:], 0.0), w=["KTr0", "KTr1"])
        self.V(lambda e: e.memset(qbd[:], 0.0), w=["qbd"])

        def proj_rope(M, hcur, hk, cosap, sinap, cskey):
            for g0, n in [(0, 512), (512, 256)]:
                ps, pk = self.pf()
                for c in range(8):
                    fw.mm(ps[0:M, 0:n], hcur(c), Wq[:, c, g0:g0 + n], c == 0, c == 7, r=[hk, "Wq_%d" % c], w=[pk])
                fw.act(qkv[0:M, g0:g0 + n], ps[0:M, 0:n], AF.Copy, r=[pk], w=["qkv%d" % (g0 // 512)])
            qk3 = qkv[0:M, 0:640].rearrange("p (h d) -> p h d", h=10)
            r3 = rot[0:M, :].rearrange("p (h d) -> p h d", h=10)
            x1, x2 = qk3[:, :, 0:32], qk3[:, :, 32:64]
            cb = cosap.unsqueeze(1).to_broadcast([M, 10, 32])
            sb_ = sinap.unsqueeze(1).to_broadcast([M, 10, 32])
            ta = rtmp[0][0:M, :].rearrange("p (h d) -> p h d", h=10)
            tb = rtmp[1][0:M, :].rearrange("p (h d) -> p h d", h=10)
            rk = ["qkv0", "qkv1", cskey]
            self.V(lambda e: e.tensor_tensor(ta, x1, cb, ALU.mult), r=rk, w=["rtmp0"])
            self.P(lambda e: e.tensor_tensor(tb, x2, sb_, ALU.mult), r=rk, w=["rtmp1"])
            self.V(lambda e: e.tensor_tensor(r3[:, :, 0:32], ta, tb, ALU.subtract), r=["rtmp0", "rtmp1"], w=["rot"])
            self.V(lambda e: e.tensor_tensor(ta, x2, cb, ALU.mult), r=rk + ["rot"], w=["rtmp0"])
            self.P(lambda e: e.tensor_tensor(tb, x1, sb_, ALU.mult), r=rk + ["rot"], w=["rtmp1"])
            self.V(lambda e: e.tensor_tensor(r3[:, :, 32:64], ta, tb, ALU.add), r=["rtmp0", "rtmp1"], w=["rot"])
            fw.act(rotb[0:M, :], rot[0:M, :], AF.Copy, r=["rot"], w=["rotb"])

        def q_transposes(M, dst, dkey):
            pbk, pk = self.pb()
            for jj in range(4):
                fw.tr(pbk[:, jj * M:(jj + 1) * M], rotb[0:M, jj * 128:(jj + 1) * 128], identb[0:M, 0:M], r=["rotb", "identb"], w=[pk])
            fw.act(dst, pbk[:, 0:4 * M].rearrange("p (j t) -> p j t", j=4), AF.Copy, r=[pk], w=[dkey])

        def gate_out(M, hcur, hk, oTt, okey, mr, mrk, xt, xk, xo_, xok):
            for half in range(2):
                pg, pgk = self.pf()
                for q in range(4):
                    dc = half * 4 + q
                    for c in range(8):
                        fw.mm(pg[:, q * M:(q + 1) * M], Wg[:, c, dc * 128:(dc + 1) * 128], hcur(c), c == 0, c == 7, r=[hk, "Wga_%d" % c], w=[pgk])
                fw.act(sga[:, half * 4:(half + 1) * 4, 0:M], pg[:, 0:4 * M].rearrange("p (q t) -> p q t", q=4), AF.Sigmoid, r=[pgk], w=["sga%d" % half])
                pbr, pbk_ = self.pf()
                for q in range(4):
                    dc = half * 4 + q
                    for cc in range(4):
                        fw.mm(pbr[:, q * M:(q + 1) * M], Wa[:, cc, dc * 128:(dc + 1) * 128], oTt[:, cc, 0:M], cc == 0, cc == 3, r=[okey, "Wa_%d" % cc], w=[pbk_])
                hs = slice(half * 4, (half + 1) * 4)
                self.V(lambda e, hs=hs, pbr=pbr: e.tensor_tensor(sga[:, hs, 0:M], sga[:, hs, 0:M], pbr[:, 0:4 * M].rearrange("p (q t) -> p q t", q=4), ALU.mult),
                       r=["sga%d" % half, pbk_], w=["sga%d" % half])
                self.V(lambda e, hs=hs: e.tensor_tensor(mg[:, hs, 0:M], sga[:, hs, 0:M], mr[:, hs, 0:M], ALU.add), r=["sga%d" % half, mrk], w=["mg%d" % half])
            for grp in range(2):
                px, pxk = self.pf()
                for dc in range(8):
                    fw.mm(px[0:M, :], mg[:, dc, 0:M], Wo[:, dc, grp * 512:(grp + 1) * 512], dc == 0, dc == 7, r=["mg%d" % (dc // 4), "Wo_%d" % dc], w=[pxk])
                self.V(lambda e, grp=grp, px=px: e.tensor_tensor(xo_[0:M, grp * 512:(grp + 1) * 512], xt[0:M, grp * 512:(grp + 1) * 512], px[0:M, :], ALU.add),
                       r=[xk, pxk], w=[xok])

        def put_kv(slot):
            pbk, pk = self.pb()
            fw.tr(pbk[:, 0:128], rotb[:, 512:640], identb[:, :], r=["rotb", "identb"], w=[pk])
            self.V(lambda e, pbk=pbk, slot=slot: e.tensor_copy(KTr[:, slot, :], pbk[:, 0:128]), r=[pk], w=["KTr%d" % slot])
            for g in range(2):
                vsrc = qkv[:, 640 + g * 64:640 + (g + 1) * 64]
                fw.act(Vp[:, slot, g, 0, 0:64], vsrc, AF.Copy, r=["qkv1"], w=["Vp%d" % slot])
                self.P(lambda e, g=g, vsrc=vsrc, slot=slot: e.tensor_copy(Vp[:, slot, g, 1, 64:128], vsrc), r=["qkv1"], w=["Vp%d" % slot])

        xt, xk = self.xt[1], "xt1"
        fw.dma(xt[:], (I["xh0"] if (l == 0 or NSEG == 1) else self.xh_dram), r=["xh_dram"], w=[xk], key=xk)
        fw.dma(cs[1][:, 0:32], I["c_cosh"], w=["cs1"], key="cs1")
        fw.dma(cs[1][:, 32:64], I["c_sinh"], w=["cs1"], key="cs1")
        self.norm_hT(xt, xk, 128, hT[:, :, :], "hT", identb)
        proj_rope(128, lambda c: hT[:, c, :], "hT", cs[1][:, 0:32], cs[1][:, 32:64], "cs1")
        put_kv(1)
        for i in range(NT):
            xt, xk = self.xt[i % 2], "xt%d" % (i % 2)
            src, _ = self.xsrc(l, i)
            fw.dma(xt[:], src, r=[("xb", i)], w=[xk], key=xk)
            mr, mrk = mrl[i % 2], "mrl%d" % (i % 2)
            fw.dma(mr[:, :, :], self.mrbuf[i].rearrange("p (c t) -> p c t", c=8), r=[("mr", i)], w=[mrk], key=mrk)
            ck_ = "cs%d" % (i % 2)
            fw.dma(cs[i % 2][:, 0:32], I["c_cosp"][i * 128:(i + 1) * 128, :], w=[ck_], key=ck_)
            fw.dma(cs[i % 2][:, 32:64], I["c_sinp"][i * 128:(i + 1) * 128, :], w=[ck_], key=ck_)
            self.norm_hT(xt, xk, 128, hT[:, :, :], "hT", identb)
            hcur = lambda c: hT[:, c, :]
            proj_rope(128, hcur, "hT", cs[i % 2][:, 0:32], cs[i % 2][:, 32:64], ck_)
            slot = i % 2
            if i == NT - 1:
                fw.dma(O["p_k"][l], rot[:, 512:640], r=["rot"], key="rot")
                fw.dma(O["p_v"][l], qkv[:, 640:768], r=["qkv1"], key="qkv1")
            q_transposes(128, qT[:, :, :], "qT")
            put_kv(slot)
            mvar = 3 if i == 0 else slot
            msk = amask[:, mvar * 256:(mvar + 1) * 256].unsqueeze(1).to_broadcast([128, 4, 256])
            for g in range(2):
                o = g * 64
                pS = []
                for jj in range(4):
                    if jj % 2 == 0:
                        ps, pk = self.pf()
                        pS.append((ps, pk))
                    fw.mm(ps[:, (jj % 2) * 256:(jj % 2 + 1) * 256], qT[o:o + 64, jj, :], KTr[o:o + 64, :, :].rearrange("p s t -> p (s t)"),
                          True, True, r=["qT", "KTr0", "KTr1"], w=[pk])
                for half, (ps, pk) in enumerate(pS):
                    self.V(lambda e, ps=ps, half=half, msk=msk: e.scalar_tensor_tensor(
                        sc[:, half * 2:(half + 1) * 2, :], ps[:, :].rearrange("p (j c) -> p j c", j=2), 0.125,
                        msk[:, 0:2, :], ALU.mult, ALU.add), r=[pk, "amask"], w=["sc%d" % half])
                sck = ["sc0", "sc1"]
                self.V(lambda e: e.tensor_reduce(st[:, 0:4], sc[:, :, :], AX.X, ALU.max), r=sck, w=["st"])
                self.V(lambda e, g=g: e.tensor_tensor(st[:, 0:4], st[:, 0:4], sinks[:, g * 4:(g + 1) * 4], ALU.max), r=["st", "sinks"], w=["st"])
                self.V(lambda e: e.tensor_tensor(sc[:, :, :], sc[:, :, :], bc3(st[:, 0:4], 256), ALU.subtract), r=sck + ["st"], w=sck)
                fw.act(sc[:, :, :], sc[:, :, :], AF.Exp, r=sck, w=sck)
                self.V(lambda e: e.tensor_reduce(st[:, 4:8], sc[:, :, :], AX.X, ALU.add), r=sck, w=["st2"])
                self.V(lambda e, g=g: e.tensor_tensor(st[:, 8:12], sinks[:, g * 4:(g + 1) * 4], st[:, 0:4], ALU.subtract), r=["st", "sinks"], w=["st3"])
                fw.act(st[:, 8:12], st[:, 8:12], AF.Exp, r=["st3"], w=["st3"])
                self.V(lambda e: e.tensor_tensor(st[:, 4:8], st[:, 4:8], st[:, 8:12], ALU.add), r=["st2", "st3"], w=["st2"])
                self.V(lambda e: e.reciprocal(st[:, 4:8], st[:, 4:8]), r=["st2"], w=["st2"])
                self.V(lambda e: e.tensor_tensor(pbf[:, :, :], sc[:, :, :], bc3(st[:, 4:8], 256), ALU.mult), r=sck + ["st2"], w=["pbf"])
                pbk, pk = self.pb()
                for jj in range(4):
                    for s_ in range(2):
                        fw.tr(pbk[:, (jj * 2 + s_) * 128:(jj * 2 + s_ + 1) * 128], pbf[:, jj, s_ * 128:(s_ + 1) * 128], identb[:, :], r=["pbf", "identb"], w=[pk])
                fw.act(pT[:, :, :, :], pbk[:, :].rearrange("p (j s t) -> p j s t", j=4, s=2), AF.Copy, r=[pk], w=["pT"])
                if g == 0:
                    pO, pok = self.pf()
                for c2 in range(2):
                    cc = g * 2 + c2
                    n = 0
                    for par in range(2):
                        jj = c2 * 2 + par
                        for s_ in range(2):
                            fw.mm(pO[:, cc * 128:(cc + 1) * 128], Vp[:, s_, g, par, :], pT[:, jj, s_, :], n == 0, n == 3,
                                  r=["Vp0", "Vp1", "pT"], w=[pok])
                            n += 1
            fw.act(oT[:, :, :], pO[:, :].rearrange("p (c t) -> p c t", c=4), AF.Copy, r=[pok], w=["oT"])
            xo_, xok = xo[i % 2], "xo%d" % (i % 2)
            gate_out(128, hcur, "hT", oT, "oT", mr, mrk, xt, xk, xo_, xok)
            fw.dma(self.xbuf[i * 128:(i + 1) * 128, :], xo_[:, :], r=[xok], w=[("xb", i)], key=xok)
        if NSEG > 1:
            self.gather_select(xo_[:, :], [xok], D, self.agX_in, self.agX_out, "agX")
            fw.dma(self.xh_dram, xo_[:, :], r=[xok], w=["xh_dram"], key="xhst")

        i = NT
        xt, xk = self.xt[i % 2], "xt%d" % (i % 2)
        src, _ = self.xsrc(l, i)
        fw.dma(xt[0:MS, :], src, r=[("xb", i)], w=[xk], key=xk)
        mr, mrk = mrl[i % 2], "mrl%d" % (i % 2)
        fw.dma(mr[:, :, 0:MS], self.mrbuf[NT].rearrange("p (c t) -> p c t", c=8)[:, :, 0:MS], r=[("mr", NT)], w=[mrk], key=mrk)
        ck_ = "cs%d" % (i % 2)
        fw.dma(cs[i % 2][0:MS, 0:32], I["c_coss"], w=[ck_], key=ck_)
        fw.dma(cs[i % 2][0:MS, 32:64], I["c_sins"], w=[ck_], key=ck_)
        self.norm_hT(xt, xk, MS, hT[:, :, 0:MS], "hT", identb)
        hcur = lambda c: hT[:, c, 0:MS]
        proj_rope(MS, hcur, "hT", cs[i % 2][0:MS, 0:32], cs[i % 2][0:MS, 32:64], ck_)
        for (cin, cout, srcap, srck, dkey) in [("ck", "s_k", rot[:, 512:640], "rot", "sk"), ("cv", "s_v", qkv[:, 640:768], "qkv1", "sv")]:
            fw.dma(O[cout][l, :, 0:124, :], I[cin][l, :, 4:128, :], w=[dkey], key=dkey + "c")
            for t in range(4):
                fw.dma(O[cout][l, :, 124 + t, :], srcap[t * 16:(t + 1) * 16, :], r=[srck], w=[dkey], key=dkey + "n")
        fw.dma(KA[:, :, :], O["s_k"][l].rearrange("q p c -> p q c"), r=["sk"], w=["KA"], key="KA")
        fw.dma(VA[:, :, :], O["s_v"][l].rearrange("q p c -> p q c"), r=["sv"], w=["VA"], key="VA")
        fw.dma(KB[:, :, :], I["ck"][l, :, 0:4, :].rearrange("q p c -> p q c"), w=["KB"], key="KB")
        fw.dma(VBt[:, :, :], I["cv"][l, :, 0:4, :].rearrange("q p c -> p q c"), w=["VB"], key="VB")
        self.P(lambda e: e.tensor_copy(VAb[:, :, :], VA[:, :, :]), r=["VA"], w=["VAb"])
        self.P(lambda e: e.tensor_copy(VBb[:, :, :], VBt[:, :, :]), r=["VB"], w=["VBb"])
        for q4 in range(4):
            ps, pk = self.pf()
            for qq in range(4):
                q = q4 * 4 + qq
                fw.tr(ps[:, qq * 128:(qq + 1) * 128], KA[:, q, :], identf[:, :], r=["KA", "identf"], w=[pk])
            fw.act(KAT[:, q4 * 4:(q4 + 1) * 4, :], ps[:, :].rearrange("p (q t) -> p q t", q=4), AF.Copy, r=[pk], w=["KAT"])
        ps, pk = self.pf()
        for q in range(NS):
            fw.tr(ps[:, q * 4:(q + 1) * 4], KB[0:4, q, :], identf[0:4, 0:4], r=["KB", "identf"], w=[pk])
        fw.act(KBT[:, :, :], ps[:, 0:64].rearrange("p (q t) -> p q t", q=NS), AF.Copy, r=[pk], w=["KBT"])
        q_transposes(MS, qT[:, :, 0:MS], "qT")
        for g in range(2):
            for jj in range(4):
                o = g * 64
                dst = qbd[o:o + 64, :, g * 16 + jj * 4:g * 16 + (jj + 1) * 4]
                srcq = qT[o:o + 64, jj, 0:MS].rearrange("p (t q) -> p q t", t=4)
                self.V(lambda e, dst=dst, srcq=srcq: e.tensor_copy(dst, srcq), r=["qT"], w=["qbd"])
        pSA = []
        for q4 in range(4):
            ps, pk = self.pf()
            pSA.append((ps, pk))
            for qq in range(4):
                q = q4 * 4 + qq
                fw.mm(ps[0:32, qq * 128:(qq + 1) * 128], qbd[:, q, :], KAT[:, q, :], True, True, r=["qbd", "KAT"], w=[pk])
        psB, pkB = self.pf()
        for q in range(NS):
            fw.mm(psB[0:32, q * 4:(q + 1) * 4], qbd[:, q, :], KBT[:, q, :], True, True, r=["qbd", "KBT"], w=[pkB])
        for q4, (ps, pk) in enumerate(pSA):
            self.V(lambda e, q4=q4, ps=ps: e.scalar_tensor_tensor(
                ssc[:, q4 * 4:(q4 + 1) * 4, 0:128], ps[0:32, :].rearrange("p (q c) -> p q c", q=4), 0.125,
                smask[:, 0:128].unsqueeze(1).to_broadcast([32, 4, 128]), ALU.mult, ALU.add), r=[pk, "smask"], w=["ssc"])
        self.V(lambda e: e.scalar_tensor_tensor(
            ssc[:, :, 128:132], psB[0:32, 0:64].rearrange("p (q c) -> p q c", q=NS), 0.125,
            smask[:, 128:132].unsqueeze(1).to_broadcast([32, NS, 4]), ALU.mult, ALU.add), r=[pkB, "smask"], w=["ssc"])
        sinkc = sbl("sinkc", [32, 1])
        for g in range(2):
            for jj in range(4):
                p0 = g * 16 + jj * 4
                fw.dma(sinkc[p0:p0 + 4, :], I["attn_sinks"][l, g * 4 + jj:g * 4 + jj + 1].partition_broadcast(4), w=["sinkc"], key="sinkc")
        self.V(lambda e: e.tensor_reduce(sst[:, 0:NS], ssc[:, :, :], AX.X, ALU.max), r=["ssc"], w=["sst"])
        self.V(lambda e: e.tensor_scalar(sst[:, 0:NS], sst[:, 0:NS], sinkc[:, 0:1], None, ALU.max), r=["sst", "sinkc"], w=["sst"])
        self.V(lambda e: e.tensor_tensor(ssc[:, :, :], ssc[:, :, :], bc3(sst[:, 0:NS], 132), ALU.subtract), r=["ssc", "sst"], w=["ssc"])
        fw.act(ssc[:, :, :], ssc[:, :, :], AF.Exp, r=["ssc"], w=["ssc"])
        self.V(lambda e: e.tensor_reduce(sst[:, NS:2 * NS], ssc[:, :, :], AX.X, ALU.add), r=["ssc"], w=["sst2"])
        self.V(lambda e: e.tensor_scalar(sst[:, 2 * NS:3 * NS], sst[:, 0:NS], sinkc[:, 0:1], None, ALU.subtract), r=["sst", "sinkc"], w=["sst3"])
        fw.act(sst[:, 2 * NS:3 * NS], sst[:, 2 * NS:3 * NS], AF.Exp, r=["sst3"], w=["sst3"], scale=-1.0)
        self.V(lambda e: e.tensor_tensor(sst[:, NS:2 * NS], sst[:, NS:2 * NS], sst[:, 2 * NS:3 * NS], ALU.add), r=["sst2", "sst3"], w=["sst2"])
        self.V(lambda e: e.reciprocal(sst[:, NS:2 * NS], sst[:, NS:2 * NS]), r=["sst2"], w=["sst2"])
        self.V(lambda e: e.tensor_tensor(spb[:, :, :], ssc[:, :, :], bc3(sst[:, NS:2 * NS], 132), ALU.mult), r=["ssc", "sst2"], w=["spb"])
        identb32 = identb[0:32, 0:32]
        for q8 in range(2):
            pbk, pk = self.pb()
            for qq in range(8):
                q = q8 * 8 + qq
                fw.tr(pbk[:, qq * 32:(qq + 1) * 32], spb[:, q, 0:128], identb32, r=["spb", "identb"], w=[pk])
            fw.act(spT[:, q8 * 8:(q8 + 1) * 8, :], pbk[:, 0:256].rearrange("p (q c) -> p q c", q=8), AF.Copy, r=[pk], w=["spT"])
        pbk, pk = self.pb()
        for q in range(NS):
            fw.tr(pbk[0:4, q * 32:(q + 1) * 32], spb[:, q, 128:132], identb32, r=["spb", "identb"], w=[pk])
        fw.act(spTB[:, :, :], pbk[0:4, 0:512].rearrange("p (q c) -> p q c", q=NS), AF.Copy, r=[pk], w=["spTB"])
        pO, pok = self.pf()
        for q in range(NS):
            fw.mm(pO[:, q * 32:(q + 1) * 32], VAb[:, q, :], spT[:, q, :], True, False, r=["VAb", "spT"], w=[pok])
            fw.mm(pO[:, q * 32:(q + 1) * 32], VBb[0:4, q, :], spTB[0:4, q, :], False, True, r=["VBb", "spTB"], w=[pok])
        oraw = sbl("oraw", [128, 32, NS], BF)
        fw.act(oraw.rearrange("p c q -> p q c"), pO[:, :].rearrange("p (q c) -> p q c", q=NS), AF.Copy, r=[pok], w=["oraw"])
        for g in range(2):
            for jj in range(4):
                cc, par = g * 2 + jj // 2, jj % 2
                c0 = g * 16 + jj * 4
                srco = oraw[g * 64:(g + 1) * 64, c0:c0 + 4, :].rearrange("p t q -> p (t q)")
                fw.dma(oTs[par * 64:(par + 1) * 64, cc, :], srco, r=["oraw"], w=["oTs"], key="oTs")
        xo_, xok = xo[i % 2], "xo%d" % (i % 2)
        gate_out(MS, hcur, "hT", oTs, "oTs", mr, mrk, xt, xk, xo_, xok)
        fw.dma(self.xsbuf, xo_[0:MS, :], r=[xok], w=[("xb", NT)], key=xok)

    def pass_ffn(self, l, es2):
        fw, I, O, NT = self.fw, self.I, self.O, self.NT
        sbl = lambda n, s, dt=F32: self.sbl(es2, "f%d_" % l + n, s, dt)
        identb, identf = self.identb, self.identf
        Wc = sbl("Wc", [128, 8, DFF], BF)
        Wu = sbl("Wu", [128, 8, DFF], BF)
        Wd = sbl("Wd", [128, NFC, D], BF)
        self.col_load(self.gcol[:], "gcol", I["norm_ffn_g"][l], 8)
        cw = sbl("cw", [128, 4, NFC])
        for j in range(3):
            self.col_load(cw[:, j, :], "cw", I["ffn_conv_w"][l, j], NFC)
        self.col_load(cw[:, 3, :], "cw", I["ffn_conv_b"][l], NFC)
        m0 = self.aoff
        self.wstage = [sbl("wst%d" % i_, [128, 2048]) for i_ in range(2)]
        wi = I["ffn_w_in"][l]
        gsc = lambda c: self.gcol[:, c:c + 1]
        self.prep_w(8, DFF, lambda c, s0, n: wi[c * 128:(c + 1) * 128, s0:s0 + n],
                    lambda c, s0, n: Wc[:, c, s0:s0 + n], lambda c: "Wc_%d" % c, "col", gsc)
        self.prep_w(8, DFF, lambda c, s0, n: wi[c * 128:(c + 1) * 128, DFF + s0:DFF + s0 + n],
                    lambda c, s0, n: Wu[:, c, s0:s0 + n], lambda c: "Wu_%d" % c, "col", gsc)
        wd = I["ffn_w_down"][l]
        self.prep_w(NFC, D, lambda c, s0, n: wd[c * 128:(c + 1) * 128, s0:s0 + n],
                    lambda c, s0, n: Wd[:, c, s0:s0 + n], lambda c: "Wd_%d" % c, "plain")
        self.release(m0)
        last = (l == 1)
        if last:
            gf = sbl("gf", [128, D])
            self.bcast_load(gf[:], "gf", I["norm_final_g"])
        hT = sbl("hT", [128, 8, 128], BF)
        cxf = sbl("cx", [128, NFC * 130])
        cx1 = cxf.rearrange("p (f t) -> p f t", f=NFC)
        cxs = cxf[:, 0:NFC * NS * 6].rearrange("p (f q j) -> p f q j", f=NFC, q=NS)
        acc = [sbl("acc%d" % i_, [128, 4, 128]) for i_ in range(2)]
        aT = sbl("aT", [128, NFC, 128], BF)
        xo = [sbl("xo%d" % i_, [128, D]) for i_ in range(2)]
        ctok = sbl("ctok", [128, DFF])
        cst = ctok
        jk = self.xn

        def finish(M, xt, xk, xo_, xok, dst_final, dst_x, dkey):
            for grp in range(2):
                px, pxk = self.pf()
                for fc in range(NFC):
                    fw.mm(px[0:M, :], aT[:, fc, 0:M], Wd[:, fc, grp * 512:(grp + 1) * 512], fc == 0, fc == NFC - 1, r=["aT", "Wd_%d" % fc], w=[pxk])
                self.V(lambda e, grp=grp, px=px: e.tensor_tensor(xo_[0:M, grp * 512:(grp + 1) * 512], xt[0:M, grp * 512:(grp + 1) * 512], px[0:M, :], ALU.add),
                       r=[xk, pxk], w=[xok])
            if not last:
                fw.dma(dst_x, xo_[0:M, :], r=[xok], w=[dkey], key=xok)
                return
            ss, t1 = self.ss, self.t1
            fw.act(jk[0:M, :], xo_[0:M, :], AF.Square, r=[xok], w=["xn", "ss"], accum_out=ss[0:M, :])
            self.V(lambda e: e.tensor_scalar(t1[0:M, :], ss[0:M, :], 1.0 / D, 1e-6, ALU.mult, ALU.add), r=["ss"], w=["t1"])
            fw.act(t1[0:M, :], t1[0:M, :], AF.Sqrt, r=["t1"], w=["t1"])
            self.V(lambda e: e.reciprocal(t1[0:M, :], t1[0:M, :]), r=["t1"], w=["t1"])
            self.V(lambda e: e.scalar_tensor_tensor(xo_[0:M, :], xo_[0:M, :], t1[0:M, 0:1], gf[0:M, :], ALU.mult, ALU.mult),
                   r=[xok, "t1", "gf"], w=[xok])
            fw.dma(dst_final, xo_[0:M, :], r=[xok], key=xok)

        def ffn_core(M, hcur, hk, cview, ckey, sample):
            for b0 in range(0, NFC, 4):
                nb = min(4, NFC - b0)
                pc, pck = self.pf()
                for q in range(nb):
                    fc = b0 + q
                    for c in range(8):
                        fw.mm(pc[:, q * M:(q + 1) * M], Wc[:, c, fc * 128:(fc + 1) * 128], hcur(c), c == 0, c == 7, r=[hk, "Wc_%d" % c], w=[pck])
                pu, puk = self.pf()
                for q in range(nb):
                    fc = b0 + q
                    for c in range(8):
                        fw.mm(pu[:, q * M:(q + 1) * M], Wu[:, c, fc * 128:(fc + 1) * 128], hcur(c), c == 0, c == 7, r=[hk, "Wu_%d" % c], w=[puk])
                if sample:
                    fw.act(cview[:, b0:b0 + nb, :, 2:6], pc[:, 0:nb * M].rearrange("p (f t q) -> p f q t", f=nb, t=4), AF.Copy, r=[pck], w=[ckey])
                else:
                    fw.act(cview[:, b0:b0 + nb, 2:130], pc[:, 0:nb * M].rearrange("p (f t) -> p f t", f=nb), AF.Copy, r=[pck], w=[ckey])
                a_ = acc[(b0 // 4) % 2]
                ak = "acc%d" % ((b0 // 4) % 2)
                for q in range(nb):
                    fc = b0 + q
                    if sample:
                        c0, c1, c2 = (cview[:, fc, :, s_:s_ + 4] for s_ in range(3))
                        av = a_[:, q, 0:M].rearrange("p (t q) -> p q t", t=4)
                    else:
                        c0, c1, c2 = (cview[:, fc, s_:s_ + 128] for s_ in range(3))
                        av = a_[:, q, :]
                    self.P(lambda e, av=av, c0=c0, fc=fc: e.tensor_scalar(av, c0, cw[:, 0, fc:fc + 1], cw[:, 3, fc:fc + 1], ALU.mult, ALU.add),
                           r=[ckey, "cw"], w=[ak])
                    self.V(lambda e, av=av, c1=c1, fc=fc: e.scalar_tensor_tensor(av, c1, cw[:, 1, fc:fc + 1], av, ALU.mult, ALU.add),
                           r=[ckey, "cw", ak], w=[ak])
                    self.V(lambda e, av=av, c2=c2, fc=fc: e.scalar_tensor_tensor(av, c2, cw[:, 2, fc:fc + 1], av, ALU.mult, ALU.add),
                           r=[ckey, "cw", ak], w=[ak])
                fw.act(a_[:, 0:nb, 0:M], a_[:, 0:nb, 0:M], AF.Gelu, r=[ak], w=[ak])
                self.V(lambda e, a_=a_, pu=pu, nb=nb, b0=b0: e.tensor_tensor(aT[:, b0:b0 + nb, 0:M], a_[:, 0:nb, 0:M],
                                                                       pu[:, 0:nb * M].rearrange("p (f t) -> p f t", f=nb), ALU.mult),
                       r=[ak, puk], w=["aT"])

        def c_token_major(M, hcur, hk, rows, dsts):
            for g0 in range(0, DFF, 512):
                n = min(512, DFF - g0)
                ps, pk = self.pf()
                for c in range(8):
                    fw.mm(ps[0:M, 0:n], hcur(c), Wc[:, c, g0:g0 + n], c == 0, c == 7, r=[hk, "Wc_%d" % c], w=[pk])
                fw.act(ctok[0:M, g0:g0 + n], ps[0:M, 0:n], AF.Copy, r=[pk], w=["ctok"])
            for (r0, r1), dst in zip(rows, dsts):
                fw.dma(dst, ctok[r0:r1, :], r=["ctok"], key="ctok")

        xt, xk = self.xt[1], "xt1"
        fw.dma(xt[:], (I["xh0"] if NSEG == 1 else self.xh_dram), r=["xh_dram"], w=[xk], key=xk)
        self.norm_hT(xt, xk, 128, hT[:, :, :], "hT", identb)
        pc, pck = self.pf()
        for fc in range(NFC):
            for c in range(8):
                fw.mm(pc[:, fc * 2:(fc + 1) * 2], Wc[:, c, fc * 128:(fc + 1) * 128], hT[:, c, 126:128], c == 0, c == 7, r=["hT", "Wc_%d" % c], w=[pck])
        fw.act(cx1[:, :, 0:2], pc[:, 0:2 * NFC].rearrange("p (f t) -> p f t", f=NFC), AF.Copy, r=[pck], w=["cx"])
        for i in range(NT):
            xt, xk = self.xt[i % 2], "xt%d" % (i % 2)
            fw.dma(xt[:], self.xbuf[i * 128:(i + 1) * 128, :], r=[("xb", i)], w=[xk], key=xk)
            self.norm_hT(xt, xk, 128, hT[:, :, :], "hT", identb)
            hcur = lambda c: hT[:, c, :]
            cv_, ckey = cx1, "cx"
            if i > 0:
                self.P(lambda e: e.tensor_copy(acc[0][:, 0, 0:2 * NFC].rearrange("p (f t) -> p f t", f=NFC), cx1[:, :, 128:130]), r=[ckey], w=["acc0"])
                self.P(lambda e: e.tensor_copy(cx1[:, :, 0:2], acc[0][:, 0, 0:2 * NFC].rearrange("p (f t) -> p f t", f=NFC)), r=["acc0"], w=[ckey])
            ffn_core(128, hcur, "hT", cv_, ckey, False)
            if i == NT - 1:
                c_token_major(128, hcur, "hT", [(126, 128)], [O["p_conv"][l]])
            xo_, xok = xo[i % 2], "xo%d" % (i % 2)
            finish(128, xt, xk, xo_, xok, O["yp"][i * 128:(i + 1) * 128, :], self.xbuf[i * 128:(i + 1) * 128, :], ("xb", i))
        if not last and NSEG > 1:
            self.gather_select(xo_[:, :], [xok], D, self.agX_in, self.agX_out, "agX")
            fw.dma(self.xh_dram, xo_[:, :], r=[xok], w=["xh_dram"], key="xhst")

        i = NT
        xt, xk = self.xt[i % 2], "xt%d" % (i % 2)
        fw.dma(xt[0:MS, :], self.xsbuf, r=[("xb", i)], w=[xk], key=xk)
        self.norm_hT(xt, xk, MS, hT[:, :, 0:MS], "hT", identb)
        hcur = lambda c: hT[:, c, 0:MS]
        fw.dma(cst[0:32, :], I["st_conv"][l], w=["ctok"], key="cst")
        for b0 in range(0, NFC, 4):
            nb = min(4, NFC - b0)
            ps, pk = self.pf()
            for q in range(nb):
                fc = b0 + q
                fw.tr(ps[:, q * 32:(q + 1) * 32], cst[0:32, fc * 128:(fc + 1) * 128], identf[0:32, 0:32], r=["ctok", "identf"], w=[pk])
            fw.act(cxs[:, b0:b0 + nb, :, 0:2], ps[:, 0:nb * 32].rearrange("p (f q j) -> p f q j", f=nb, j=2), AF.Copy, r=[pk], w=["cx"])
        ffn_core(MS, hcur, "hT", cxs, "cx", True)
        sc_ = O["s_conv"][l].rearrange("(q j) f -> j q f", j=2)
        c_token_major(MS, hcur, "hT", [(32, 48), (48, 64)], [sc_[0], sc_[1]])
        xo_, xok = xo[i % 2], "xo%d" % (i % 2)
        finish(MS, xt, xk, xo_, xok, O["ys"], self.xsbuf, ("xb", NT))


NSEG = 1


def _consts_shared():
    c = {}
    c["c_ident"] = np.eye(128, dtype=np.float32)
    inv = (10000.0 ** (-np.arange(0, HD, 2, dtype=np.float32) / HD)).astype(np.float32)
    pos_s = (PAST + np.repeat(np.arange(4), NS)).astype(np.float32)
    ang_s = pos_s[:, None] * inv[None, :]
    c["c_coss"] = np.cos(ang_s).astype(np.float32)
    c["c_sins"] = np.sin(ang_s).astype(np.float32)
    s = np.arange(128)[:, None]
    t = np.arange(128)[None, :]
    incl = (s <= t).astype(np.float32)
    strict = (s < t).astype(np.float32)
    c["c_tri"] = np.concatenate([incl * CDEC, strict * CDEC], 1).astype(np.float32)
    c["c_mask2"] = np.concatenate([incl, strict], 1).astype(np.float32)
    c["c_maskL"] = (s > t).astype(np.float32)
    i_ = np.arange(128)[:, None]
    j_ = np.arange(128)[None, :]
    cur = np.where(j_ <= i_, 0.0, NEG)
    prev = np.where(j_ > i_, 0.0, NEG)
    dead = np.full((128, 128), NEG)
    c["c_amask"] = np.concatenate([cur, prev, prev, cur, cur, dead], 1).astype(np.float32)
    c["_am_first"] = np.concatenate([cur, dead], 1).astype(np.float32)
    c["_am_mid"] = np.concatenate([cur, prev], 1).astype(np.float32)
    tt = (np.arange(32) % 4)[:, None]
    ia = np.arange(128)[None, :]
    ma = np.where(ia <= 124 + tt, 0.0, NEG)
    rb = np.arange(4)[None, :]
    mb = np.where(rb > tt, 0.0, NEG)
    c["c_smask"] = np.concatenate([ma, mb], 1).astype(np.float32)
    last = np.zeros((128, 1), np.float32)
    last[127, 0] = 1.0
    c["c_last"] = last
    c["_inv"] = inv
    return c


def _rope_tab(pos, inv):
    ang = pos.astype(np.float32)[:, None] * inv[None, :]
    return np.cos(ang).astype(np.float32), np.sin(ang).astype(np.float32)


_CACHE = {}
TAPS = False
TAP_OUT = {}


def kernel(**inp):
    inp = {k: np.asarray(v) for k, v in inp.items()}
    xp_all = inp["x_prompt"].astype(np.float32)
    B, SEQ_, _ = xp_all.shape
    TPC = SEQ_ // NSEG
    if TPC not in _CACHE:
        b_ = Builder(TPC, taps=TAPS)
        _CACHE[TPC] = (b_.build(), b_.tapnames)
    nc, tapnames = _CACHE[TPC]
    consts = _consts_shared()
    inv = consts.pop("_inv")
    am_first, am_mid = consts.pop("_am_first"), consts.pop("_am_mid")
    wnames = ["norm_mix_g", "w_in", "rwkv_mu", "rwkv_w0", "rwkv_w2", "rwkv_a0", "rwkv_a2", "rwkv_g2", "rwkv_k_k",
              "rwkv_k_a", "rwkv_ln_g", "rwkv_ln_b", "attn_sinks", "w_br_rwkv", "w_br_attn", "w_out", "norm_ffn_g",
              "ffn_w_in", "ffn_conv_w", "ffn_conv_b", "ffn_w_down", "norm_final_g"]
    shared = {n: np.ascontiguousarray(inp[n], dtype=np.float32) for n in wnames}
    shared["rwkv_r_k"] = np.ascontiguousarray(inp["rwkv_r_k"], dtype=np.float32).reshape(2, RD)
    shared.update(consts)
    in_maps = []
    ncores = 8
    for c in range(ncores):
        b, seg = (c // NSEG) % B, c % NSEG
        sl = slice(c * NS, (c + 1) * NS)
        m = dict(shared)
        t0 = seg * TPC
        m["xp"] = np.ascontiguousarray(xp_all[b, t0:t0 + TPC])
        m["xh0"] = np.ascontiguousarray(xp_all[b, t0 - 128:t0]) if seg > 0 else np.zeros((128, D), np.float32)
        m["c_cosp"], m["c_sinp"] = _rope_tab(t0 + np.arange(TPC), inv)
        m["c_cosh"], m["c_sinh"] = _rope_tab(np.maximum(t0 - 128 + np.arange(128), 0), inv)
        m["c_amask0"] = am_mid if seg > 0 else am_first
        sel = np.zeros((128, 8), np.float32)
        if seg > 0:
            sel[:, c - 1] = 1.0
        m["c_sel"] = sel
        m["xs"] = np.ascontiguousarray(inp["x_sample"][sl].transpose(1, 0, 2).reshape(MS, D))
        m["st_shift"] = np.ascontiguousarray(inp["state_rwkv_shift"][:, sl])
        m["st_wkv"] = np.ascontiguousarray(inp["state_rwkv_wkv"][:, sl]).reshape(2, 128, 4096)
        m["ck"] = np.ascontiguousarray(inp["cache_swa_k"][:, sl]).reshape(2, NS, 128, 128)
        m["cv"] = np.ascontiguousarray(inp["cache_swa_v"][:, sl]).reshape(2, NS, 128, 128)
        m["st_conv"] = np.ascontiguousarray(inp["state_ffn_conv"][:, sl]).reshape(2, 2 * NS, DFF)
        in_maps.append(m)
    res = run_bass_kernel_spmd(nc, in_maps, core_ids=list(range(ncores)))
    R = res.results
    for tn in tapnames:
        TAP_OUT[tn] = [np.asarray(R[c][tn]) for c in range(ncores)]
    f = np.float32
    lastc = [b * NSEG + NSEG - 1 for b in range(B)]
    y_prompt = np.stack([np.concatenate([R[b * NSEG + sg]["yp"] for sg in range(NSEG)], 0) for b in range(B)]).astype(f)
    y_sample = np.concatenate([R[c]["ys"].reshape(4, NS, D).transpose(1, 0, 2) for c in range(ncores)], 0).astype(f)
    p_shift = np.stack([R[c]["p_shift"] for c in lastc], 1).astype(f)
    p_wkv = np.stack([R[c]["p_wkv"] for c in lastc], 1).astype(f)
    p_k = np.stack([R[c]["p_k"] for c in lastc], 1).reshape(2, B, 128, 2, 64).astype(f)
    p_v = np.stack([R[c]["p_v"] for c in lastc], 1).reshape(2, B, 128, 2, 64).astype(f)
    p_conv = np.stack([R[c]["p_conv"] for c in lastc], 1).astype(f)
    s_shift = np.concatenate([R[c]["s_shift"] for c in range(ncores)], 1).astype(f)
    s_wkv = np.concatenate([R[c]["s_wkv"].reshape(2, NS, NH, 64, 64) for c in range(ncores)], 1).astype(f)
    s_k = np.concatenate([R[c]["s_k"].reshape(2, NS, 128, 2, 64) for c in range(ncores)], 1).astype(f)
    s_v = np.concatenate([R[c]["s_v"].reshape(2, NS, 128, 2, 64) for c in range(ncores)], 1).astype(f)
    s_conv = np.concatenate([R[c]["s_conv"].reshape(2, NS, 2, DFF) for c in range(ncores)], 1).astype(f)
    return (y_prompt, y_sample, p_shift, p_wkv, p_k, p_v, p_conv, s_shift, s_wkv, s_k, s_v, s_conv)
```

```python
import math
from contextlib import ExitStack

import numpy as np
import concourse.bass as bass
import concourse.mybir as mybir
from concourse.bass_utils import run_bass_kernel_spmd

F32 = mybir.dt.float32
BF = mybir.dt.bfloat16
AF = mybir.ActivationFunctionType
ALU = mybir.AluOpType
AX = mybir.AxisListType

ENGS = ["sp", "pe", "act", "dve", "pool"]
DEBUG_WHERE = True

D = 1024
HD = 64
NH = 8
RD = 512
RP = 1792
INP = 4608
DFF = 2816
NFC = 22
NS = 16
MS = 64
PAST = 16384
CDEC = -math.exp(-0.5)
NEG = -30000.0


class FW:
    def __init__(self, nc, es):
        self.nc = nc
        self.es = es
        self.ops = {e: [] for e in ENGS}
        self.lastw = {}
        self.readers = {}
        self.dma_count = {}
        self.inc = {}

    def sb(self, name, shape, dt=F32):
        return self.es.enter_context(self.nc.sbuf_tensor(name, list(shape), dt))

    def ps(self, name, shape, dt=F32):
        return self.es.enter_context(self.nc.psum_tensor(name, list(shape), dt))

    def capture(self, f):
        self.cap = []
        f()
        log, self.cap = self.cap, None
        return log

    def replay(self, logs, chunk=2):
        logs = [list(lg) for lg in logs if lg]
        if not logs:
            return
        mn = min(len(lg) for lg in logs)
        per = [max(1, int(round(chunk * len(lg) / mn))) for lg in logs]
        pos = [0] * len(logs)
        while any(p < len(lg) for p, lg in zip(pos, logs)):
            for k, lg in enumerate(logs):
                for _ in range(per[k]):
                    if pos[k] < len(lg):
                        self.op(*lg[pos[k]])
                        pos[k] += 1

    def op(self, eng, fn, r=(), w=(), dma=None):
        if getattr(self, "cap", None) is not None:
            self.cap.append((eng, fn, tuple(r), tuple(w), dma))
            return
        ops = self.ops[eng]
        idx = len(ops)
        deps = set()
        pr = [k for k in r if isinstance(k, str) and k[:2] in ("ps", "pb") and k[2:].isdigit()]
        if pr:
            r = [k for k in r if k not in pr]
            w = list(w) + pr
        for k in r:
            t = self.lastw.get(k)
            if t is not None:
                deps.add(t)
        for k in w:
            t = self.lastw.get(k)
            if t is not None:
                deps.add(t)
            for t2 in self.readers.get(k, {}).values():
                deps.add(t2)
        if dma is not None:
            c = self.dma_count.get(dma, 0) + 1
            self.dma_count[dma] = c
            tok = ("d", dma, c)
        else:
            tok = ("c", eng, idx)
        if eng == "pe":
            deps = {d for d in deps if not (d[0] == "c" and d[1] == "pe")}
        deps.discard(tok)
        rec = dict(fn=fn, deps=deps, tok=tok, signal=False)
        if DEBUG_WHERE:
            import sys as _s
            f_ = _s._getframe(1)
            wh = []
            while f_ is not None and len(wh) < 4:
                wh.append(f_.f_lineno)
                f_ = f_.f_back
            rec["where"] = wh
        ops.append(rec)
        for d in deps:
            if d[0] == "c":
                self.ops[d[1]][d[2]]["signal"] = True
        for k in w:
            self.lastw[k] = tok
            self.readers[k] = {}
        for k in r:
            rk = ("d", tok[1]) if tok[0] == "d" else tok[1]
            self.readers.setdefault(k, {})[rk] = tok
        return tok

    def fence(self):
        toks = set()
        for e in ENGS:
            for rec in reversed(self.ops[e]):
                if rec["tok"][0] == "c" and rec["fn"] is not None:
                    toks.add(rec["tok"])
                    rec["signal"] = True
                    break
        for k, c in self.dma_count.items():
            toks.add(("d", k, c))
        for e in ENGS:
            self.ops[e].append(dict(fn=None, deps=set(toks), tok=("c", e, len(self.ops[e])), signal=False))

    def dma(self, out, in_, r=(), w=(), key=None, eng="sp", **kw):
        self.op(eng, lambda e: e.dma_start(out=out, in_=in_, **kw), r=r, w=w, dma=key)

    def mm(self, out, lhsT, rhs, start, stop, r=(), w=()):
        self.op("pe", lambda e: e.matmul(out, lhsT, rhs, start=start, stop=stop), r=r, w=w)

    def tr(self, out, in_, ident, r=(), w=()):
        self.op("pe", lambda e: e.transpose(out, in_, ident), r=r, w=w)

    def act(self, out, in_, func, r=(), w=(), **kw):
        self.op("act", lambda e: e.activation(out, in_, func, **kw), r=r, w=w)

    def emit(self):
        nc = self.nc
        sems = {e: self.es.enter_context(nc.semaphore("s_" + e)) for e in ENGS}
        dsems = {}
        for i, k in enumerate(self.dma_count):
            dsems[k] = self.es.enter_context(nc.semaphore("d%d" % i))
        for e in ENGS:
            c = 0
            for rec in self.ops[e]:
                if rec["signal"] and rec["tok"][0] == "c":
                    c += 1
                rec["sigval"] = c
        final_counts = dict(self.dma_count)

        def run(engname, eng):
            waited = {}
            for rec in self.ops[engname]:
                need = {}
                for d in rec["deps"]:
                    if d[0] == "c":
                        s = ("c", d[1])
                        v = self.ops[d[1]][d[2]]["sigval"]
                    else:
                        s = ("d", d[1])
                        v = self.inc.get(d[1], 16) * d[2]
                    if need.get(s, 0) < v:
                        need[s] = v
                for s, v in need.items():
                    if waited.get(s, 0) >= v:
                        continue
                    waited[s] = v
                    eng.wait_ge(sems[s[1]] if s[0] == "c" else dsems[s[1]], v)
                if rec["fn"] is None:
                    continue
                try:
                    ins = rec["fn"](eng)
                except Exception:
                    print("EMIT FAILURE at lines", rec.get("where"), "engine", engname)
                    raise
                if rec["tok"][0] == "d":
                    ins.then_inc(dsems[rec["tok"][1]], self.inc.get(rec["tok"][1], 16))
                elif rec["signal"]:
                    ins.then_inc(sems[engname], 1)
            if engname == "sp":
                for k, c in final_counts.items():
                    v = self.inc.get(k, 16) * c
                    if waited.get(("d", k), 0) < v:
                        eng.wait_ge(dsems[k], v)

        with nc.Block() as block:
            @block.sync
            def _(e):
                run("sp", e)

            @block.tensor
            def _(e):
                run("pe", e)

            @block.scalar
            def _(e):
                run("act", e)

            @block.vector
            def _(e):
                run("dve", e)

            @block.gpsimd
            def _(e):
                run("pool", e)


def bc3(ap2, n):
    s = list(ap2.shape)
    return ap2.unsqueeze(2).to_broadcast([s[0], s[1], n])


def h3(ap2, h=NH):
    return ap2.rearrange("p (h d) -> p h d", h=h)


class Builder:
    def __init__(self, TP, taps=False):
        self.TP = TP
        self.NT = TP // 128
        self.taps = taps
        self.nc = bass.Bass("TRN2", target_bir_lowering=False)
        self.I = {}
        self.O = {}
        self.psi = 0
        self.pbi = 0
        self.tapnames = []
        self.pool = None
        self.pcnt = {}

    def din(self, n, s):
        self.I[n] = self.nc.dram_tensor(n, list(s), F32, kind="ExternalInput").ap()

    def dout(self, n, s):
        self.O[n] = self.nc.dram_tensor(n, list(s), F32, kind="ExternalOutput").ap()

    def declare(self):
        TP = self.TP
        for n, s in [("xp", (TP, D)), ("xs", (MS, D)), ("st_shift", (2, NS, RP)), ("st_wkv", (2, 128, 4096)),
                     ("ck", (2, NS, 128, 128)), ("cv", (2, NS, 128, 128)), ("st_conv", (2, 2 * NS, DFF)),
                     ("norm_mix_g", (2, D)), ("w_in", (2, D, INP)), ("rwkv_mu", (2, RP)), ("rwkv_w0", (2, RD)),
                     ("rwkv_w2", (2, 64, RD)), ("rwkv_a0", (2, RD)), ("rwkv_a2", (2, 64, RD)),
                     ("rwkv_g2", (2, 128, RD)), ("rwkv_k_k", (2, RD)), ("rwkv_k_a", (2, RD)),
                     ("rwkv_r_k", (2, RD)), ("rwkv_ln_g", (2, RD)), ("rwkv_ln_b", (2, RD)),
                     ("attn_sinks", (2, NH)), ("w_br_rwkv", (2, RD, D)), ("w_br_attn", (2, RD, D)),
                     ("w_out", (2, D, D)), ("norm_ffn_g", (2, D)), ("ffn_w_in", (2, D, 2 * DFF)),
                     ("ffn_conv_w", (2, 3, DFF)), ("ffn_conv_b", (2, DFF)), ("ffn_w_down", (2, DFF, D)),
                     ("norm_final_g", (D,)),
                     ("c_ident", (128, 128)), ("c_cosp", (TP, 32)), ("c_sinp", (TP, 32)),
                     ("c_coss", (MS, 32)), ("c_sins", (MS, 32)), ("c_tri", (128, 256)),
                     ("c_mask2", (128, 256)), ("c_maskL", (128, 128)), ("c_amask", (128, 768)),
                     ("c_smask", (32, 132)), ("c_last", (128, 1)),
                     ("xh0", (128, D)), ("c_cosh", (128, 32)), ("c_sinh", (128, 32)), ("c_amask0", (128, 256)), ("c_sel", (128, 8))]:
            self.din(n, s)
        for n, s in [("yp", (TP, D)), ("ys", (MS, D)), ("p_shift", (2, RP)), ("p_wkv", (2, NH, 64, 64)),
                     ("p_k", (2, 128, 128)), ("p_v", (2, 128, 128)), ("p_conv", (2, 2, DFF)),
                     ("s_shift", (2, NS, RP)), ("s_wkv", (2, 128, 4096)), ("s_k", (2, NS, 128, 128)),
                     ("s_v", (2, NS, 128, 128)), ("s_conv", (2, 2 * NS, DFF))]:
            self.dout(n, s)
        nc = self.nc
        self.xbuf = nc.dram_tensor("xbuf", [TP, D], F32).ap()
        self.xsbuf = nc.dram_tensor("xsbuf", [MS, D], F32).ap()
        self.mrbuf = nc.dram_tensor("mrbuf", [self.NT + 1, 128, 1024], BF).ap()
        self.xh_dram = nc.dram_tensor("xh_dram", [128, D], F32).ap()
        self.sq = nc.dram_tensor("sq", [6, MS, RD], F32).ap()
        self.sy = nc.dram_tensor("sy", [MS, RD], F32).ap()

    def alloc(self, name, shape, dt=F32):
        shape = list(shape)
        n = 1
        for d_ in shape[1:]:
            n *= d_
        nbytes = n * (4 if dt == F32 else 2)
        nw = (nbytes + 31) // 32 * 8
        off = self.aoff
        self.aoff += nw
        self.apeak = max(self.apeak, self.aoff)
        assert self.aoff <= self.ASZ, "SBUF arena overflow: %s needs %d words (limit %d)" % (name, self.aoff, self.ASZ)
        ap = self.arena[0:shape[0], off:off + nw]
        if dt != F32:
            ap = ap.bitcast(dt)
        ap = ap[:, 0:n]
        if len(shape) > 2:
            names = ["d%d" % i for i in range(len(shape) - 1)]
            pat = "p (%s) -> p %s" % (" ".join(names), " ".join(names))
            ap = ap.rearrange(pat, **{names[i]: shape[i + 1] for i in range(len(names))})
        return ap

    def release(self, mark):
        self.fw.fence()
        self.aoff = mark

    def pf(self):
        ids = {None: [0, 1, 2, 3, 4, 5], 0: [0, 1, 2], 1: [3, 4, 5]}[self.pool]
        c = self.pcnt.setdefault(("f", self.pool), 0)
        self.pcnt[("f", self.pool)] = c + 1
        k = ids[c % len(ids)]
        return self.PS[k], "ps%d" % k

    def pb(self):
        ids = {None: [0, 1], 0: [0], 1: [1]}[self.pool]
        c = self.pcnt.setdefault(("b", self.pool), 0)
        self.pcnt[("b", self.pool)] = c + 1
        k = ids[c % len(ids)]
        return self.PBK[k], "pb%d" % k

    def tap(self, name, ap, rkeys, dt=F32):
        if not self.taps:
            return
        shp = list(ap.shape)
        t = self.nc.dram_tensor("tap_" + name, shp, dt, kind="ExternalOutput").ap()
        self.tapnames.append("tap_" + name)
        self.fw.dma(t, ap, r=rkeys, key="tap_" + name)

    def V(self, fn, r=(), w=()):
        self.fw.op("dve", fn, r, w)

    def P(self, fn, r=(), w=()):
        self.fw.op("pool", fn, r, w)

    def col_load(self, dst, dkey, vec, n):
        fw = self.fw
        st = self.cstage
        fw.dma(st[0:n, :], vec.rearrange("(c p) -> c p", p=128), w=["cstage"], key="cstage")
        ps, pk = self.pf()
        fw.tr(ps[:, 0:n], st[0:n, :], self.identf[0:n, 0:n], r=["cstage", "identf"], w=[pk])
        fw.act(dst, ps[:, 0:n], AF.Copy, r=[pk], w=[dkey])

    def gather_select(self, src_ap, src_keys, n, ag_in, ag_out, name):
        fw = self.fw
        fw.dma(ag_in, src_ap, r=src_keys, w=[name + "_in"], key=name + "_st")
        self.gi = getattr(self, "gi", 0)
        ck = name + "_cc"
        fw.inc[ck] = 1
        fw.op("pool", lambda e: e.collective_compute("AllGather", ALU.bypass, replica_groups=[list(range(8))], ins=[ag_in], outs=[ag_out]),
              r=[name + "_in"], w=[name + "_out"], dma=ck)
        for r_ in range(8):
            st, sk = self.xt[r_ % 2], "xt%d" % (r_ % 2)
            fw.dma(st[:, 0:n], ag_out[r_ * 128:(r_ + 1) * 128, :], r=[name + "_out"], w=[sk], key=sk)
            if r_ == 0:
                self.V(lambda e, st=st: e.tensor_scalar(src_ap, st[:, 0:n], self.sel[:, 0:1], None, ALU.mult), r=[sk, "sel"], w=src_keys)
            else:
                self.V(lambda e, st=st, r_=r_: e.scalar_tensor_tensor(src_ap, st[:, 0:n], self.sel[:, r_:r_ + 1], src_ap, ALU.mult, ALU.add),
                       r=[sk, "sel"] + list(src_keys), w=src_keys)

    def bcast_load(self, dst, dkey, vec):
        self.fw.dma(dst, vec.partition_broadcast(dst.shape[0]), w=[dkey], key=dkey)

    def prep_w(self, nchunks, ncols, src, dst, dkey, mode, scale=None, mul=None, mulkey=None, sview=None):
        fw = self.fw
        for c in range(nchunks):
            for s0 in range(0, ncols, 2048):
                n = min(2048, ncols - s0)
                k = self.wst_i % 2
                self.wst_i += 1
                st = self.wstage[k]
                sk = "wst%d" % k
                fw.dma(st[:, 0:n], src(c, s0, n), w=[sk], key=sk)
                o = dst(c, s0, n)
                dk = dkey(c)
                if sview is not None:
                    sv_ = sview(st[:, 0:n])
                    sc = scale(c)
                    self.V(lambda eg, o=o, sv_=sv_, sc=sc: eg.tensor_scalar(o, sv_, sc, None, ALU.mult), r=[sk, "gcol"], w=[dk])
                    continue
                if mode == "plain":
                    e = ["dve", "pool", "act"][self.wst_i % 3]
                    if e == "act":
                        fw.act(o, st[:, 0:n], AF.Copy, r=[sk], w=[dk])
                    else:
                        fw.op(e, lambda eg, o=o, st=st, n=n: eg.tensor_copy(o, st[:, 0:n]), r=[sk], w=[dk])
                elif mode == "col":
                    sc = scale(c)
                    e = ["dve", "pool"][self.wst_i % 2]
                    fw.op(e, lambda eg, o=o, st=st, n=n, sc=sc: eg.tensor_scalar(o, st[:, 0:n], sc, None, ALU.mult),
                          r=[sk, "gcol"], w=[dk])
                else:
                    sc = scale(c)
                    m = mul(s0, n)
                    self.V(lambda eg, o=o, st=st, n=n, sc=sc, m=m: eg.scalar_tensor_tensor(
                        o, st[:, 0:n], sc, m, ALU.mult, ALU.mult), r=[sk, "gcol", mulkey], w=[dk])

    def norm_hT(self, xt, xk, M, hdst, hkey, identb):
        self.norm_a(xt, xk, M)
        self.norm_b(M, hdst, hkey, identb)

    def norm_a(self, xt, xk, M):
        fw = self.fw
        xn, ss, t1 = self.xn, self.ss, self.t1
        fw.act(xn[0:M, :], xt[0:M, :], AF.Square, r=[xk], w=["xn", "ss"], accum_out=ss[0:M, :])
        self.V(lambda e: e.tensor_scalar(t1[0:M, :], ss[0:M, :], 1.0 / D, 1e-6, ALU.mult, ALU.add), r=["ss"], w=["t1"])
        fw.act(t1[0:M, :], t1[0:M, :], AF.Sqrt, r=["t1"], w=["t1"])
        self.V(lambda e: e.reciprocal(t1[0:M, :], t1[0:M, :]), r=["t1"], w=["t1"])
        self.V(lambda e: e.tensor_scalar(xn[0:M, :], xt[0:M, :], t1[0:M, 0:1], None, ALU.mult), r=[xk, "t1"], w=["xn"])

    def norm_b(self, M, hdst, hkey, identb):
        fw = self.fw
        xn = self.xn
        pbk, pk = self.pb()
        for c in range(8):
            fw.tr(pbk[:, c * M:(c + 1) * M], xn[0:M, c * 128:(c + 1) * 128], identb[0:M, 0:M], r=["xn", "identb"], w=[pk])
        fw.act(hdst, pbk[:, 0:8 * M].rearrange("p (c t) -> p c t", c=8), AF.Copy, r=[pk], w=[hkey])

    def build(self):
        self.declare()
        nc = self.nc
        with ExitStack() as es:
            self.fw = fw = FW(nc, es)
            self.PS = [fw.ps("ps%d" % i, [128, 512], F32) for i in range(6)]
            self.PBK = [fw.ps("pb%d" % i, [128, 1024], BF) for i in range(2)]
            self.ASZ = 52224
            self.arena = fw.sb("arena", [128, self.ASZ])
            self.aoff = 0
            self.apeak = 0
            self.identf = self.alloc("identf", [128, 128])
            self.identb = self.alloc("identb", [128, 128], BF)
            self.cstage = self.alloc("cstage", [32, 128])
            self.wst_i = 0
            self.xn = self.alloc("xn", [128, D], BF)
            self.ss = self.alloc("ss", [128, 1])
            self.t1 = self.alloc("t1", [128, 1])
            self.gcol = self.alloc("gcol", [128, 8])
            self.xt = [self.alloc("xt%d" % i, [128, D]) for i in range(2)]
            self.sel = self.alloc("sel", [128, 8])
            fw.dma(self.sel[:], self.I["c_sel"], w=["sel"], key="sel")
            fw.dma(self.identf[:], self.I["c_ident"], w=["identf"], key="identf")
            self.V(lambda e: e.tensor_copy(self.identb[:], self.identf[:]), r=["identf"], w=["identb"])
            for l in range(2):
                for p_ in (self.pass_rwkv, self.pass_attn, self.pass_ffn):
                    mk_ = self.aoff
                    p_(l, None)
                    self.release(mk_)
            print("arena peak words", self.apeak, "of", self.ASZ)
            fw.emit()
        return nc

    def sbl(self, es2, name, shape, dt=F32):
        return self.alloc(name, shape, dt)

    def xsrc(self, l, i):
        if i < self.NT:
            src = self.I["xp"] if l == 0 else self.xbuf
            return src[i * 128:(i + 1) * 128, :], ("xb", i)
        src = self.I["xs"] if l == 0 else self.xsbuf
        return src, ("xb", i)

    def pass_rwkv(self, l, es2):
        fw, I, O, NT = self.fw, self.I, self.O, self.NT
        sbl = lambda n, s, dt=F32: self.sbl(es2, "r%d_" % l + n, s, dt)
        identb, identf = self.identb, self.identf
        W1 = sbl("W1", [128, 8, RP], BF)
        W2 = sbl("W2", [128, 8, RP], BF)
        Wg = sbl("Wg", [128, 8, D], BF)
        Wr = sbl("Wr", [128, 4, D], BF)
        lw2 = sbl("lw2", [128, RD], BF)
        lg2 = sbl("lg2", [128, RD], BF)
        bcs = {}
        for n in ["rwkv_w0", "rwkv_a0", "rwkv_k_k", "rwkv_k_a", "rwkv_r_k", "rwkv_ln_g", "rwkv_ln_b"]:
            bcs[n] = sbl(n, [128, RD])
            self.bcast_load(bcs[n][:], n + "_bc", I[n][l])
        mucol = sbl("mucol", [128, 2])
        tri = sbl("tri", [128, 256])
        mask2 = sbl("mask2", [128, 256])
        maskL = sbl("maskL", [128, 128])
        clast = sbl("clast", [128, 1])
        fw.dma(tri[:], I["c_tri"], w=["tri"], key="tri")
        fw.dma(mask2[:], I["c_mask2"], w=["mask2"], key="mask2")
        fw.dma(maskL[:], I["c_maskL"], w=["maskL"], key="maskL")
        fw.dma(clast[:], I["c_last"], w=["clast"], key="clast")
        self.col_load(self.gcol[:], "gcol", I["norm_mix_g"][l], 8)
        self.col_load(mucol[:], "mucol", I["rwkv_mu"][l, 1536:1792], 2)
        m0 = self.aoff
        self.wstage = [sbl("wst%d" % i_, [128, 2048]) for i_ in range(2)]
        mu_bc = sbl("mu_bc", [128, RP])
        omm_bc = sbl("omm_bc", [128, RP])
        self.bcast_load(mu_bc[:], "mu_bc", I["rwkv_mu"][l])
        self.V(lambda e: e.tensor_scalar(omm_bc[:], mu_bc[:], -1.0, 1.0, ALU.mult, ALU.add), r=["mu_bc"], w=["omm_bc"])
        win = I["w_in"][l]
        gsc = lambda c: self.gcol[:, c:c + 1]
        self.prep_w(8, RP, lambda c, s0, n: win[c * 128:(c + 1) * 128, s0:s0 + n],
                    lambda c, s0, n: W1[:, c, s0:s0 + n], lambda c: "W1_%d" % c, "colmul", gsc,
                    lambda s0, n: omm_bc[:, s0:s0 + n], "omm_bc")
        self.prep_w(8, RP, lambda c, s0, n: win[c * 128:(c + 1) * 128, s0:s0 + n],
                    lambda c, s0, n: W2[:, c, s0:s0 + n], lambda c: "W2_%d" % c, "colmul", gsc,
                    lambda s0, n: mu_bc[:, s0:s0 + n], "mu_bc")
        self.prep_w(8, D, lambda c, s0, n: win[c * 128:(c + 1) * 128, 2560 + s0:2560 + s0 + n],
                    lambda c, s0, n: Wg[:, c, s0:s0 + n], lambda c: "Wg_%d" % c, "col", gsc)
        wbr = I["w_br_rwkv"][l]
        self.prep_w(4, D, lambda c, s0, n: wbr[c * 128:(c + 1) * 128, s0:s0 + n],
                    lambda c, s0, n: Wr[:, c, s0:s0 + n], lambda c: "Wr_%d" % c, "plain")
        for (nm, p0, dk_) in [("rwkv_w2", 0, "lw2a"), ("rwkv_a2", 64, "lw2b")]:
            k = self.wst_i % 2
            self.wst_i += 1
            wsk = self.wstage[k]
            fw.dma(wsk[p0:p0 + 64, 0:RD], I[nm][l], w=["wst%d" % k], key="wst%d" % k)
            self.P(lambda e, wsk=wsk, p0=p0: e.tensor_copy(lw2[p0:p0 + 64, :], wsk[p0:p0 + 64, 0:RD]), r=["wst%d" % k], w=[dk_])
        self.prep_w(1, RD, lambda c, s0, n: I["rwkv_g2"][l], lambda c, s0, n: lg2[:, :], lambda c: "lg2", "plain")
        WK1 = ["W1_%d" % c for c in range(8)]
        WK2 = ["W2_%d" % c for c in range(8)]
        self.release(m0)
        class NSP:
            pass
        zr, zk = sbl("zr", [128, RD]), sbl("zk", [128, RD])
        lact = sbl("lact", [128, 128], BF)
        T = [sbl("tmp%d" % i_, [128, RD]) for i_ in range(8)]
        sm = sbl("sm", [128, 64])
        orT = sbl("orT", [128, 4, 128], BF)
        sgr = sbl("sgr", [128, 8, 128], BF)
        mrT0_ = sbl("mrT0", [128, 8, 128], BF)
        mrT = [mrT0_, mrT0_]
        TP_ = [sbl("tpost%d" % i_, [128, RD]) for i_ in range(2)]
        m1 = self.aoff
        NRB = 9864

        def mkrec(k):
            R = NSP()
            rb = sbl("RB%d" % k, [128, NRB], BF)
            rf = sbl("RF%d" % k, [128, 528])
            R.rb, R.rf, R.k = rb, rf, k
            R.RKT = rb[:, 0:1024].rearrange("p (j a t) -> p j a t", j=4, a=2)
            R.G4 = [rb[:, 1024 + j * 1280:1024 + (j + 1) * 1280].rearrange("p (h c) -> p h c", h=2) for j in range(4)]
            R.ZF = [rb[:, 6144 + j * 256:6144 + (j + 1) * 256].rearrange("p (h c) -> p h c", h=2) for j in range(4)]
            R.vb, R.ktt, R.bnt = rb[:, 7168:7680], rb[:, 7680:8192], rb[:, 8192:8704]
            R.sgT = rb[:, 8704:8832]
            R.hT = rb[:, 8832:9864].rearrange("p (c t) -> p c t", c=8)
            R.zv, R.WC, R.bon = rf[:, 0:512], rf[:, 512:516], rf[:, 516:524]
            R.K = (lambda k_: (lambda n: "%s#%d" % (n, k_)))(k)
            return R
        R0 = mkrec(0)
        U0b = [sbl("U0b%d" % j, [128, 2, 64], BF) for j in range(4)]
        Ub = sbl("Ub", [128, RD], BF)
        Nst = sbl("Nst", [128, 4, 128])
        Nb = sbl("Nb", [128, 4, 128], BF)
        self.V(lambda e: e.memset(Nst[:], 0.0), w=["Nst"])
        self.V(lambda e: e.memset(Nb[:], 0.0), w=["Nb"])
        m2 = self.aoff
        rt, kat = sbl("rt", [128, RD], BF), sbl("kat", [128, RD], BF)
        KT = sbl("KT", [128, 4, 128], BF)
        BT = sbl("BT", [128, 4, 128], BF)
        for j in range(4):
            self.P(lambda e, j=j: e.tensor_copy(R0.G4[j][:, :, 512:640], identb[:, :].unsqueeze(1).to_broadcast([128, 2, 128])),
                   r=["identb"], w=["G4_%d" % j])
        EZ = [[sbl("EZ%d_%d" % (j, a), [128, 2, 2, 128], BF) for a in range(2)] for j in range(4)]
        FFa = [sbl("FFa%d" % a, [128, 4, 2, 128], BF) for a in range(2)]
        FF = [[FFa[a][:, j] for a in range(2)] for j in range(4)]

        def tok_proj(M, hcur, hprev, hk, g0, dstkey):
            ps, pk = self.pf()
            n = 0
            for c in range(8):
                fw.mm(ps[0:M, :], hcur(c), W1[:, c, g0:g0 + 512], n == 0, False, r=[hk, WK1[c]], w=[pk])
                n += 1
            for c in range(8):
                fw.mm(ps[0:M, :], hprev(c), W2[:, c, g0:g0 + 512], False, c == 7, r=[hk, WK2[c]], w=[pk])
            return ps, pk

        def feat_proj(M, hcur, hprev, hk, g0):
            ps, pk = self.pf()
            for c in range(8):
                fw.mm(ps[:, 0:M], W1[:, c, g0:g0 + 128], hcur(c), c == 0, False, r=[hk, WK1[c]], w=[pk])
            for c in range(8):
                fw.mm(ps[:, 0:M], W2[:, c, g0:g0 + 128], hprev(c), False, c == 7, r=[hk, WK2[c]], w=[pk])
            return ps, pk

        def raw_last(hl, hk, M, dst):
            for gi, g0 in enumerate(range(0, RP, 512)):
                n = min(512, RP - g0)
                ps, pk = self.pf()
                for c in range(8):
                    fw.mm(ps[0:M, 0:n], hl(c), W1[:, c, g0:g0 + n], c == 0, False, r=[hk, WK1[c]], w=[pk])
                for c in range(8):
                    fw.mm(ps[0:M, 0:n], hl(c), W2[:, c, g0:g0 + n], False, c == 7, r=[hk, WK2[c]], w=[pk])
                fw.act(T[gi][0:M, 0:n], ps[0:M, 0:n], AF.Copy, r=[pk], w=["T%d" % gi])
                fw.dma(dst[:, g0:g0 + n], T[gi][0:M, 0:n], r=["T%d" % gi], key="zl%d" % gi)

        def prep(M, sample, R):
            K = R.K
            w0, a0 = bcs["rwkv_w0"], bcs["rwkv_a0"]
            kkb, kab, rkb = bcs["rwkv_k_k"], bcs["rwkv_k_a"], bcs["rwkv_r_k"]
            pw, pwk = self.pf()
            fw.mm(pw[0:M, :], lact[0:64, 0:M], lw2[0:64, :], True, True, r=["lact", "lw2a"], w=[pwk])
            pa, pak = self.pf()
            fw.mm(pa[0:M, :], lact[64:128, 0:M], lw2[64:128, :], True, True, r=["lact", "lw2b"], w=[pak])
            sg, a_, kk, t3, kf, be = T[0], T[1], T[2], T[3], T[4], T[5]
            self.V(lambda e: e.tensor_tensor(sg[0:M, :], pw[0:M, :], w0[0:M, :], ALU.add), r=[pwk, "rwkv_w0_bc"], w=["T0"])
            fw.act(sg[0:M, :], sg[0:M, :], AF.Sigmoid, r=["T0"], w=["T0"])
            self.V(lambda e: e.tensor_tensor(a_[0:M, :], pa[0:M, :], a0[0:M, :], ALU.add), r=[pak, "rwkv_a0_bc"], w=["T1"])
            fw.act(a_[0:M, :], a_[0:M, :], AF.Sigmoid, r=["T1"], w=["T1"])
            self.P(lambda e: e.tensor_tensor(kk[0:M, :], zk[0:M, :], kkb[0:M, :], ALU.mult), r=["zk", "rwkv_k_k_bc"], w=["T2"])
            self.P(lambda e: e.tensor_tensor(t3[0:M, :], kk[0:M, :], kk[0:M, :], ALU.mult), r=["T2"], w=["T3"])
            self.V(lambda e: e.tensor_reduce(sm[0:M, 0:8], h3(t3[0:M, :]), AX.X, ALU.add), r=["T3"], w=["sm0"])
            fw.act(sm[0:M, 0:8], sm[0:M, 0:8], AF.Sqrt, r=["sm0"], w=["sm0"])
            self.V(lambda e: e.tensor_scalar(sm[0:M, 0:8], sm[0:M, 0:8], 1e-12, None, ALU.max), r=["sm0"], w=["sm0"])
            self.V(lambda e: e.reciprocal(sm[0:M, 0:8], sm[0:M, 0:8]), r=["sm0"], w=["sm0"])
            self.V(lambda e: e.tensor_tensor(h3(kk[0:M, :]), h3(kk[0:M, :]), bc3(sm[0:M, 0:8], 64), ALU.mult),
                   r=["T2", "sm0"], w=["T2"])
            self.V(lambda e: e.scalar_tensor_tensor(t3[0:M, :], a_[0:M, :], -1.0, kab[0:M, :], ALU.add, ALU.mult),
                   r=["T1", "rwkv_k_a_bc"], w=["T3"])
            self.V(lambda e: e.scalar_tensor_tensor(kf[0:M, :], t3[0:M, :], 1.0, zk[0:M, :], ALU.add, ALU.mult),
                   r=["T3", "zk"], w=["T4"])
            self.P(lambda e: e.tensor_tensor(be[0:M, :], kk[0:M, :], a_[0:M, :], ALU.mult), r=["T2", "T1"], w=["T5"])
            self.P(lambda e: e.tensor_tensor(t3[0:M, :], zr[0:M, :], kf[0:M, :], ALU.mult), r=["zr", "T4"], w=["T3"])
            self.P(lambda e: e.tensor_tensor(t3[0:M, :], t3[0:M, :], rkb[0:M, :], ALU.mult), r=["T3", "rwkv_r_k_bc"], w=["T3"])
            self.V(lambda e, R=R: e.tensor_reduce(R.bon[0:M, :], h3(t3[0:M, :]), AX.X, ALU.add), r=["T3"], w=[K("bon")])
            if sample:
                fw.act(T[6][0:M, :], sg[0:M, :], AF.Exp, r=["T0"], w=["T6"], scale=CDEC)
                for x, (tl, tk) in enumerate([(zr, "zr"), (T[6], "T6"), (kf, "T4"), (R.zv, K("zv")), (kk, "T2"), (be, "T5")]):
                    fw.dma(self.sq[x], tl[0:M, :], r=[tk], w=[("sq", x)], key="sqw%d" % x)
                return
            pli, plik = self.pf()
            fw.mm(pli[:, :], tri[:, 0:128], sg[:, :], True, True, r=["tri", "T0"], w=[plik])
            ple, plek = self.pf()
            fw.mm(ple[:, :], tri[:, 128:256], sg[:, :], True, True, r=["tri", "T0"], w=[plek])
            eL, eLm, enL = T[6], T[7], T[3]
            fw.act(eL[:, :], pli[:, :], AF.Exp, r=[plik], w=["T6"])
            fw.act(eLm[:, :], ple[:, :], AF.Exp, r=[plek], w=["T7"])
            fw.act(enL[:, :], pli[:, :], AF.Exp, r=[plik], w=["T3"], scale=-1.0)
            self.V(lambda e: e.tensor_tensor(rt[:, :], zr[:, :], eL[:, :], ALU.mult), r=["zr", "T6"], w=["rt"])
            self.V(lambda e: e.tensor_tensor(kat[:, :], kk[:, :], eLm[:, :], ALU.mult), r=["T2", "T7"], w=["kat"])
            self.P(lambda e, R=R: e.tensor_tensor(R.ktt[:, :], kf[:, :], enL[:, :], ALU.mult), r=["T4", "T3"], w=[K("ktt")])
            self.V(lambda e, R=R: e.scalar_tensor_tensor(R.bnt[:, :], be[:, :], -1.0, enL[:, :], ALU.mult, ALU.mult),
                   r=["T5", "T3"], w=[K("bnt")])
            fw.act(R.vb[:, :], R.zv[:, :], AF.Copy, r=[K("zv")], w=[K("vb")])
            pwc, pwck = self.pf()
            for j in range(4):
                fw.mm(pwc[:, j:j + 1], eL[:, j * 128:(j + 1) * 128], clast[:, :], True, True, r=["T6", "clast"], w=[pwck])
            fw.act(R.WC[:, :], pwc[:, 0:4], AF.Copy, r=[pwck], w=[K("WC")])
            for (src, skey, dstf, dk) in [(rt, "rt", None, "RKT"), (kat, "kat", None, "RKT"),
                                          (R.ktt, K("ktt"), None, "KT"), (R.bnt, K("bnt"), None, "BT")]:
                pbk, pk = self.pb()
                for j in range(4):
                    fw.tr(pbk[:, j * 128:(j + 1) * 128], src[:, j * 128:(j + 1) * 128], identb[:, :], r=[skey, "identb"], w=[pk])
                if dk == "RKT":
                    which = 0 if skey == "rt" else 1
                    fw.act(R.RKT[:, :, which, :], pbk[:, 0:512].rearrange("p (j t) -> p j t", j=4), AF.Copy, r=[pk], w=["RKT%d" % which])
                else:
                    dst = KT if dk == "KT" else BT
                    self.V(lambda e, dst=dst, pbk=pbk: e.tensor_copy(dst[:, :, :], pbk[:, 0:512].rearrange("p (j t) -> p j t", j=4)),
                           r=[pk], w=[dk])

        def stageAB(R):
            K = R.K
            RK = [K("RKT0"), K("RKT1")]
            RKT, G4, ZF = R.RKT, R.G4, R.ZF
            zb = [self.pf(), self.pf()]
            for j in range(4):
                for hh in range(2):
                    o = hh * 64
                    pZ, pzk = zb[hh]
                    fw.mm(pZ[:, j * 128:(j + 1) * 128], RKT[o:o + 64, j, 1, :], BT[o:o + 64, j, :], True, True, r=["BT", K("RKT1")], w=[pzk])
            mlb = maskL[:, :].unsqueeze(1).to_broadcast([128, 4, 128])
            for hh in range(2):
                pZ, pzk = zb[hh]
                self.V(lambda e, pZ=pZ, hh=hh: e.tensor_tensor(FFa[0][:, :, hh, :], pZ[:, :].rearrange("p (j c) -> p j c", j=4), mlb, ALU.mult),
                       r=[pzk, "maskL"], w=["FF%d_0" % j for j in range(4)])
            for j in range(4):
                bk = [self.pf(), self.pf()]
                for hh in range(2):
                    o = hh * 64
                    ps, pk = bk[hh]
                    rhs = RKT[o:o + 64, j, :, :].rearrange("p a t -> p (a t)")
                    fw.mm(ps[:, 0:256], KT[o:o + 64, j, :], rhs, True, True, r=["KT"] + RK, w=[pk])
                    fw.mm(ps[:, 256:512], BT[o:o + 64, j, :], rhs, True, True, r=["BT"] + RK, w=[pk])
                for hh in range(2):
                    ps, pk = bk[hh]
                    self.V(lambda e, j=j, hh=hh, ps=ps, G4=G4: e.tensor_tensor(
                        G4[j][:, hh, 0:512].rearrange("p (a c) -> p a c", a=2), ps[:, :].rearrange("p (a c) -> p a c", a=2),
                        mask2[:, :].unsqueeze(1).to_broadcast([128, 2, 256]), ALU.mult), r=[pk, "mask2"], w=[K("G4_%d" % j)])
            for lev in range(7):
                a, b = lev % 2, (lev + 1) % 2
                for j in range(4):
                    fk, fn_ = "FF%d_%d" % (j, a), "FF%d_%d" % (j, b)
                    ezn = "EZ%d_%d" % (j, b)
                    if lev == 0:
                        ezk = K("G4_%d" % j)
                        EZs = lambda hh, j=j, G4=G4: G4[j][:, hh, 384:640]
                        Es = lambda hh, j=j, G4=G4: G4[j][:, hh, 384:512]
                        Zs = lambda j=j, G4=G4: G4[j][:, :, 512:640]
                    else:
                        ezk = "EZ%d_%d" % (j, a)
                        EZs = lambda hh, j=j, a=a: EZ[j][a][:, hh, :, :].rearrange("p a t -> p (a t)")
                        Es = lambda hh, j=j, a=a: EZ[j][a][:, hh, 0, :]
                        Zs = lambda j=j, a=a: EZ[j][a][:, :, 1, :]
                    if lev < 6:
                        pL, plk = self.pf()
                        for hh in range(2):
                            fw.mm(pL[:, hh * 256:(hh + 1) * 256], FF[j][a][:, hh, :], EZs(hh), True, True, r=[ezk, fk], w=[plk])
                        pF, pfk = self.pf()
                        for hh in range(2):
                            fw.mm(pF[:, hh * 128:(hh + 1) * 128], Es(hh), FF[j][a][:, hh, :], True, True, r=[ezk, fk], w=[pfk])
                        l3 = pL[:, :].rearrange("p (h c) -> p h c", h=2)
                        fw.act(EZ[j][b][:, :, 0, :], l3[:, :, 0:128], AF.Copy, r=[plk], w=[ezn])
                        self.V(lambda e, j=j, b=b, l3=l3, Zs=Zs: e.tensor_tensor(EZ[j][b][:, :, 1, :], l3[:, :, 128:256], Zs(), ALU.add),
                               r=[plk, ezk], w=[ezn])
                        fw.act(FF[j][b][:, :, :], pF[:, 0:256].rearrange("p (h c) -> p h c", h=2), AF.Copy, r=[pfk], w=[fn_])
                    else:
                        pL, plk = self.pf()
                        for hh in range(2):
                            fw.mm(pL[:, hh * 128:(hh + 1) * 128], FF[j][a][:, hh, :], EZ[j][a][:, hh, 1, :], True, True, r=[ezk, fk], w=[plk])
                        self.V(lambda e, j=j, a=a, pL=pL, ZF=ZF: e.tensor_tensor(ZF[j][:, :, :], pL[:, 0:256].rearrange("p (h c) -> p h c", h=2),
                                                                      EZ[j][a][:, :, 1, :], ALU.add), r=[plk, ezk], w=[K("ZF%d" % j)])

        def stageC(R):
            K = R.K
            RKT, G4, ZF, vb = R.RKT, R.G4, R.ZF, R.vb
            for j in range(4):
                pU, puk = self.pf()
                for hh in range(2):
                    o, h = hh * 64, 2 * j + hh
                    fw.mm(pU[:, hh * 64:(hh + 1) * 64], RKT[o:o + 64, j, 1, :], Nb[o:o + 64, j, o:o + 64], True, False, r=[K("RKT1"), "Nb"], w=[puk])
                    fw.mm(pU[:, hh * 64:(hh + 1) * 64], G4[j][:, hh, 128:256], vb[:, h * 64:(h + 1) * 64], False, True, r=[K("G4_%d" % j), K("vb")], w=[puk])
                fw.act(U0b[j][:, :, :], pU[:, 0:128].rearrange("p (h c) -> p h c", h=2), AF.Copy, r=[puk], w=["U0b%d" % j])
            for j in range(4):
                pU, puk = self.pf()
                for hh in range(2):
                    fw.mm(pU[:, hh * 64:(hh + 1) * 64], ZF[j][:, hh, :], U0b[j][:, hh, :], True, True, r=[K("ZF%d" % j), "U0b%d" % j], w=[puk])
                fw.act(Ub[:, j * 128:(j + 1) * 128], pU[:, 0:128], AF.Copy, r=[puk], w=["Ub%d" % j])

        def stageD(R):
            K = R.K
            RKT, G4, vb = R.RKT, R.G4, R.vb
            psY, pyk = self.pf()
            for j in range(4):
                for hh in range(2):
                    o, h = hh * 64, 2 * j + hh
                    fw.mm(psY[:, h * 64:(h + 1) * 64], RKT[o:o + 64, j, 0, :], Nb[o:o + 64, j, o:o + 64], True, False, r=[K("RKT0"), "Nb"], w=[pyk])
                    fw.mm(psY[:, h * 64:(h + 1) * 64], G4[j][:, hh, 0:128], vb[:, h * 64:(h + 1) * 64], False, False, r=[K("G4_%d" % j), K("vb")], w=[pyk])
                    fw.mm(psY[:, h * 64:(h + 1) * 64], G4[j][:, hh, 256:384], Ub[:, h * 64:(h + 1) * 64], False, True, r=[K("G4_%d" % j), "Ub%d" % j], w=[pyk])
            return psY, pyk

        def n_update(R):
            K = R.K
            ktt, bnt, vb, WC = R.ktt, R.bnt, R.vb, R.WC
            pN, pnk = self.pf()
            for j in range(4):
                fw.mm(pN[:, j * 128:(j + 1) * 128], ktt[:, j * 128:(j + 1) * 128], vb[:, j * 128:(j + 1) * 128], True, False, r=[K("ktt"), K("vb")], w=[pnk])
                fw.mm(pN[:, j * 128:(j + 1) * 128], bnt[:, j * 128:(j + 1) * 128], Ub[:, j * 128:(j + 1) * 128], False, True, r=[K("bnt"), "Ub%d" % j], w=[pnk])
            n2 = Nst[:, :, :].rearrange("p j c -> p (j c)")
            self.V(lambda e: e.tensor_tensor(n2, pN[:, :], n2, ALU.add), r=[pnk, "Nst"], w=["Nst"])
            self.V(lambda e, WC=WC: e.tensor_tensor(Nst[:, :, :], Nst[:, :, :], bc3(WC[:, :], 128), ALU.mult), r=["Nst", K("WC")], w=["Nst"])
            fw.act(Nb[:, :, :], Nst[:, :, :], AF.Copy, r=["Nst"], w=["Nb"])


        def post(M, yap, ykeys, pg, pgk, R):
            K = R.K
            lng, lnb = bcs["rwkv_ln_g"], bcs["rwkv_ln_b"]
            y2, yc = TP_[0], TP_[1]
            ob = TP_[0].bitcast(BF)[:, 0:RD]
            self.V(lambda e: e.tensor_reduce(sm[0:M, 16:24], h3(yap), AX.X, ALU.add), r=ykeys, w=["sm2"])
            fw.act(y2[0:M, :], yap, AF.Square, r=ykeys, w=["TP0"])
            self.V(lambda e: e.tensor_reduce(sm[0:M, 24:32], h3(y2[0:M, :]), AX.X, ALU.add), r=["TP0"], w=["sm3"])
            mean, var = sm[0:M, 16:24], sm[0:M, 24:32]
            self.V(lambda e: e.tensor_scalar(mean, mean, 1.0 / 64, None, ALU.mult), r=["sm2"], w=["sm2"])
            self.V(lambda e: e.tensor_tensor(sm[0:M, 32:40], mean, mean, ALU.mult), r=["sm2"], w=["sm4"])
            self.V(lambda e: e.scalar_tensor_tensor(var, var, 1.0 / 64, sm[0:M, 32:40], ALU.mult, ALU.subtract), r=["sm3", "sm4"], w=["sm3"])
            self.V(lambda e: e.tensor_scalar(var, var, 64e-5, None, ALU.add), r=["sm3"], w=["sm3"])
            fw.act(var, var, AF.Sqrt, r=["sm3"], w=["sm3"])
            self.V(lambda e: e.reciprocal(var, var), r=["sm3"], w=["sm3"])
            self.V(lambda e: e.tensor_tensor(h3(yc[0:M, :]), h3(yap), bc3(mean, 64), ALU.subtract), r=list(ykeys) + ["sm2"], w=["TP1"])
            self.V(lambda e: e.tensor_tensor(h3(yc[0:M, :]), h3(yc[0:M, :]), bc3(var, 64), ALU.mult), r=["TP1", "sm3"], w=["TP1"])
            self.P(lambda e: e.tensor_tensor(yc[0:M, :], yc[0:M, :], lng[0:M, :], ALU.mult), r=["TP1", "rwkv_ln_g_bc"], w=["TP1"])
            self.P(lambda e: e.tensor_tensor(yc[0:M, :], yc[0:M, :], lnb[0:M, :], ALU.add), r=["TP1", "rwkv_ln_b_bc"], w=["TP1"])
            self.P(lambda e, R=R: e.tensor_tensor(h3(y2[0:M, :]), h3(R.zv[0:M, :]), bc3(R.bon[0:M, :], 64), ALU.mult), r=[K("zv"), K("bon")], w=["TP0"])
            self.V(lambda e: e.tensor_tensor(yc[0:M, :], yc[0:M, :], y2[0:M, :], ALU.add), r=["TP1", "TP0"], w=["TP1"])
            self.V(lambda e: e.tensor_tensor(ob[0:M, :], yc[0:M, :], pg[0:M, :], ALU.mult), r=["TP1", pgk], w=["TP0"])
            pbk, pk = self.pb()
            for j in range(4):
                fw.tr(pbk[:, j * M:(j + 1) * M], ob[0:M, j * 128:(j + 1) * 128], identb[0:M, 0:M], r=["TP0", "identb"], w=[pk])
            fw.act(orT[:, :, 0:M], pbk[:, 0:4 * M].rearrange("p (j t) -> p j t", j=4), AF.Copy, r=[pk], w=["orT"])

        def gate_branch(M, hcur, hk, mdst, mkey):
            for half in range(2):
                pg, pgk = self.pf()
                for q in range(4):
                    dc = half * 4 + q
                    for c in range(8):
                        fw.mm(pg[:, q * M:(q + 1) * M], Wg[:, c, dc * 128:(dc + 1) * 128], hcur(c), c == 0, c == 7, r=[hk, "Wg_%d" % c], w=[pgk])
                fw.act(sgr[:, half * 4:(half + 1) * 4, 0:M], pg[:, 0:4 * M].rearrange("p (q t) -> p q t", q=4), AF.Sigmoid, r=[pgk], w=["sgr%d" % half])
                pbr, pbk_ = self.pf()
                for q in range(4):
                    dc = half * 4 + q
                    for j in range(4):
                        fw.mm(pbr[:, q * M:(q + 1) * M], Wr[:, j, dc * 128:(dc + 1) * 128], orT[:, j, 0:M], j == 0, j == 3, r=["orT", "Wr_%d" % j], w=[pbk_])
                self.V(lambda e, half=half, pbr=pbr: e.tensor_tensor(mdst[:, half * 4:(half + 1) * 4, 0:M], sgr[:, half * 4:(half + 1) * 4, 0:M],
                                                                 pbr[:, 0:4 * M].rearrange("p (q t) -> p q t", q=4), ALU.mult),
                       r=["sgr%d" % half, pbk_], w=[mkey])

        R1 = mkrec(1)
        for j in range(4):
            self.P(lambda e, j=j: e.tensor_copy(R1.G4[j][:, :, 512:640], identb[:, :].unsqueeze(1).to_broadcast([128, 2, 128])),
                   r=["identb"], w=[R1.K("G4_%d" % j)])
        RR = [R0, R1]

        def H1a(i):
            R, Rp = RR[i % 2], RR[(i + 1) % 2]
            hT = R.hT
            xt, xk = self.xt[i % 2], "xt%d" % (i % 2)
            src, _ = self.xsrc(l, i)
            fw.dma(xt[:], src, r=[("xb", i)], w=[xk], key=xk)
            hk = R.K("hTr")
            if i == 0:
                self.V(lambda e, hT=hT: e.memset(hT[:, :, 0:1], 0.0), w=[hk])
            else:
                self.P(lambda e, hT=hT, hp=Rp.hT: e.tensor_copy(hT[:, :, 0:1], hp[:, :, 128:129]), r=[Rp.K("hTr")], w=[hk])
            self.norm_a(xt, xk, 128)

        def H1b(i):
            R = RR[i % 2]
            K = R.K
            hT = R.hT
            hk = K("hTr")
            self.norm_b(128, hT[:, :, 1:129], hk, identb)
            hcur = lambda c, hT=hT: hT[:, c, 1:129]
            hprev = lambda c, hT=hT: hT[:, c, 0:128]
            for g0, dst, dk in [(0, zr, "zr"), (512, zk, "zk"), (1024, R.zv, K("zv"))]:
                ps, pk = tok_proj(128, hcur, hprev, hk, g0, dk)
                fw.act(dst[:, :], ps[:, :], AF.Copy, r=[pk], w=[dk])
            ps, pk = feat_proj(128, hcur, hprev, hk, 1536)
            fw.act(lact[0:64, :], ps[0:64, 0:128], AF.Tanh, r=[pk], w=["lact"])
            fw.act(lact[64:128, :], ps[64:128, 0:128], AF.Copy, r=[pk], w=["lact"])
            ps, pk = feat_proj(128, hcur, hprev, hk, 1664)
            fw.act(R.sgT[:, :], ps[:, 0:128], AF.Sigmoid, r=[pk], w=[K("sgT")])
            if i == NT - 1:
                raw_last(lambda c, hT=hT: hT[:, c, 128:129], hk, 1, O["p_shift"][l:l + 1, :])

        def H1c(i):
            prep(128, False, RR[i % 2])

        def H1d(i):
            stageAB(RR[i % 2])

        H2st = {}

        def H2a(i):
            R = RR[i % 2]
            stageC(R)
            psY, pyk = stageD(R)
            n_update(R)
            pg, pgk = self.pf()
            fw.mm(pg[:, :], R.sgT[:, :], lg2[:, :], True, True, r=[R.K("sgT"), "lg2"], w=[pgk])
            H2st[i] = (psY, pyk, pg, pgk)

        def H2b(i):
            psY, pyk, pg, pgk = H2st.pop(i)
            post(128, psY[:, :], [pyk], pg, pgk, RR[i % 2])

        def H2c(i):
            R = RR[i % 2]
            m, mk = mrT[0], "mrT0"
            gate_branch(128, lambda c, R=R: R.hT[:, c, 1:129], R.K("hTr"), m, mk)
            fw.dma(self.mrbuf[i].rearrange("p (c t) -> p c t", c=8), m[:, :, :], r=[mk], w=[("mr", i)], key=mk)

        def cap(pool, f, i):
            self.pool = pool
            return fw.capture(lambda: f(i))

        for f in (H1a, H1b, H1c, H1d):
            fw.replay([cap(0, f, 0)])
        for i in range(NT):
            nx = i + 1 < NT
            if nx:
                fw.replay([cap(0, H1a, i + 1)])
            fw.replay([cap(1, H2a, i)])
            fw.replay(([cap(0, H1b, i + 1)] if nx else []) + [cap(1, H2b, i)])
            fw.replay(([cap(0, H1c, i + 1)] if nx else []) + [cap(1, H2c, i)])
            if nx:
                fw.replay([cap(0, H1d, i + 1)])
        self.pool = None
        for j in range(4):
            ps, pk = self.pf()
            fw.tr(ps[:, 0:128], Nst[:, j, :], identf[:, :], r=["Nst", "identf"], w=[pk])
            fw.act(T[0][:, j * 128:(j + 1) * 128], ps[:, 0:128], AF.Copy, r=[pk], w=["T0"])
        for h_ in range(8):
            j, o = h_ // 2, (h_ % 2) * 64
            fw.dma(O["p_wkv"][l, h_], T[0][o:o + 64, j * 128 + o:j * 128 + o + 64], r=["T0"], key="T0")

        self.release(m1)
        RS = NSP()
        RS.zv = sbl("zv_s", [128, RD])
        RS.sgT = sbl("sgT_s", [128, 128], BF)
        RS.bon = sbl("bon_s", [128, 8])
        RS.K = lambda n: n + "#s"
        hTs = sbl("hTs", [128, 8, 80], BF)
        sadd = sbl("sadd", [16, RP])
        stT = sbl("stT", [128, 2, 16])
        zf = sbl("zf", [128, 2, 64])
        QH = sbl("QH", [128, 6, 4, 64])
        Sst = sbl("Sst", [128, 64, 64])
        Stmp = sbl("Stmp", [128, 64, 64])
        sk = sbl("sk", [128, 64])
        yh = sbl("yh", [128, 4, 64])
        ytm = T[7]
        self.V(lambda e: e.memset(hTs[:], 0.0), w=["hTs"])
        i = NT
        xt, xk = self.xt[i % 2], "xt%d" % (i % 2)
        src, _ = self.xsrc(l, i)
        fw.dma(xt[0:MS, :], src, r=[("xb", i)], w=[xk], key=xk)
        self.norm_hT(xt, xk, MS, hTs[:, :, 16:80], "hTs", identb)
        hcur = lambda c: hTs[:, c, 16:80]
        hprev = lambda c: hTs[:, c, 0:64]
        fw.dma(sadd[:, :], I["st_shift"][l], w=["sadd"], key="sadd")
        for q in range(2):
            ps, pk = self.pf()
            fw.tr(ps[:, 0:16], sadd[0:16, 1536 + q * 128:1536 + (q + 1) * 128], identf[0:16, 0:16], r=["sadd", "identf"], w=[pk])
            self.V(lambda e, q=q, ps=ps: e.tensor_scalar(stT[:, q, :], ps[:, 0:16], mucol[:, q:q + 1], None, ALU.mult), r=[pk, "mucol"], w=["stT"])
        for gi, g0 in enumerate(range(0, RP, 512)):
            n = min(512, RP - g0)
            self.bcast_load(T[4 + gi][0:16, 0:n], "T%d" % (4 + gi), I["rwkv_mu"][l, g0:g0 + n])
            self.V(lambda e, gi=gi, g0=g0, n=n: e.tensor_tensor(sadd[:, g0:g0 + n], sadd[:, g0:g0 + n], T[4 + gi][0:16, 0:n], ALU.mult),
                   r=["sadd", "T%d" % (4 + gi)], w=["sadd"])
        zv, sgT = RS.zv, RS.sgT
        for g0, dst, dk in [(0, zr, "zr"), (512, zk, "zk"), (1024, zv, RS.K("zv"))]:
            ps, pk = tok_proj(MS, hcur, hprev, "hTs", g0, dk)
            fw.act(dst[0:MS, :], ps[0:MS, :], AF.Copy, r=[pk], w=[dk])
            self.V(lambda e, dst=dst, g0=g0: e.tensor_tensor(dst[0:16, :], dst[0:16, :], sadd[0:16, g0:g0 + 512], ALU.add), r=[dk, "sadd"], w=[dk])
        for q, g0 in enumerate([1536, 1664]):
            ps, pk = feat_proj(MS, hcur, hprev, "hTs", g0)
            fw.act(zf[:, q, :], ps[:, 0:MS], AF.Copy, r=[pk], w=["zf"])
            self.V(lambda e, q=q: e.tensor_tensor(zf[:, q, 0:16], zf[:, q, 0:16], stT[:, q, :], ALU.add), r=["zf", "stT"], w=["zf"])
        fw.act(lact[0:64, 0:MS], zf[0:64, 0, :], AF.Tanh, r=["zf"], w=["lact"])
        fw.act(lact[64:128, 0:MS], zf[64:128, 0, :], AF.Copy, r=["zf"], w=["lact"])
        fw.act(sgT[:, 0:MS], zf[:, 1, :], AF.Sigmoid, r=["zf"], w=[RS.K("sgT")])
        prep(MS, True, RS)
        if l == 0:
            for nm, ap, k in [("s_zr", zr, "zr"), ("s_zk", zk, "zk"), ("s_zv", zv, "zv"), ("s_dec", T[6], "T6"), ("s_kk", T[2], "T2"),
                              ("s_kf", T[4], "T4"), ("s_be", T[5], "T5"), ("s_a", T[1], "T1")]:
                self.tap(nm, ap[0:MS, :], [k])
        sqv = self.sq.rearrange("x (t q) (h d) -> (q h) x t d", t=4, h=NH)
        for x in range(6):
            fw.dma(QH[:, x, :, :], sqv[:, x, :, :], r=[("sq", x)], w=["QH"], key="QH")
        fw.dma(Sst[:, :, :].rearrange("p v k -> p (v k)"), I["st_wkv"][l], w=["Sst"], key="Sst")
        for t in range(4):
            r_, w_, k_, v_, kk_, b_ = (QH[:, x, t, :] for x in range(6))
            rowb = lambda a: a.unsqueeze(1).to_broadcast([128, 64, 64])
            colb = lambda a: a.unsqueeze(2).to_broadcast([128, 64, 64])
            self.V(lambda e, kk_=kk_: e.tensor_tensor(Stmp[:, :, :], Sst[:, :, :], rowb(kk_), ALU.mult), r=["Sst", "QH"], w=["Stmp"])
            self.V(lambda e: e.tensor_reduce(sk[:, :], Stmp[:, :, :], AX.X, ALU.add), r=["Stmp"], w=["sk"])
            self.P(lambda e, w_=w_: e.tensor_tensor(Sst[:, :, :], Sst[:, :, :], rowb(w_), ALU.mult), r=["Sst", "QH", "Stmp"], w=["Sst"])
            self.V(lambda e, b_=b_: e.tensor_tensor(Stmp[:, :, :], colb(sk[:, :]), rowb(b_), ALU.mult), r=["sk", "QH"], w=["Stmp"])
            self.V(lambda e: e.tensor_tensor(Sst[:, :, :], Sst[:, :, :], Stmp[:, :, :], ALU.subtract), r=["Sst", "Stmp"], w=["Sst"])
            self.P(lambda e, v_=v_, k_=k_: e.tensor_tensor(Stmp[:, :, :], colb(v_), rowb(k_), ALU.mult), r=["QH", "Sst"], w=["Stmp"])
            self.V(lambda e: e.tensor_tensor(Sst[:, :, :], Sst[:, :, :], Stmp[:, :, :], ALU.add), r=["Sst", "Stmp"], w=["Sst"])
            self.P(lambda e, r_=r_: e.tensor_tensor(Stmp[:, :, :], Sst[:, :, :], rowb(r_), ALU.mult), r=["Sst", "QH"], w=["Stmp"])
            self.V(lambda e, t=t: e.tensor_reduce(yh[:, t, :], Stmp[:, :, :], AX.X, ALU.add), r=["Stmp"], w=["yh"])
        fw.dma(O["s_wkv"][l], Sst[:, :, :].rearrange("p v k -> p (v k)"), r=["Sst"], key="Sst")
        if l == 0:
            self.tap("s_QH", QH, ["QH"])
            self.tap("s_yh", yh, ["yh"])
        fw.dma(self.sy.rearrange("(t q) (h d) -> (q h) t d", t=4, h=NH), yh[:, :, :], r=["yh"], w=["sy"], key="yh")
        fw.dma(ytm[0:MS, :], self.sy, r=["sy"], w=["T7"], key="ytm")
        pg, pgk = self.pf()
        fw.mm(pg[0:MS, :], sgT[:, 0:MS], lg2[:, :], True, True, r=[RS.K("sgT"), "lg2"], w=[pgk])
        post(MS, ytm[0:MS, :], ["T7"], pg, pgk, RS)
        m, mk = mrT[0], "mrT0"
        gate_branch(MS, hcur, "hTs", m, mk)
        fw.dma(self.mrbuf[NT].rearrange("p (c t) -> p c t", c=8)[:, :, 0:MS], m[:, :, 0:MS], r=[mk], w=[("mr", NT)], key=mk)
        raw_last(lambda c: hTs[:, c, 64:80], "hTs", 16, O["s_shift"][l])

    def pass_attn(self, l, es2):
        fw, I, O, NT = self.fw, self.I, self.O, self.NT
        sbl = lambda n, s, dt=F32: self.sbl(es2, "a%d_" % l + n, s, dt)
        identb, identf = self.identb, self.identf
        Wq = sbl("Wq", [128, 8, 768], BF)
        Wg = sbl("Wg", [128, 8, D], BF)
        Wa = sbl("Wa", [128, 4, D], BF)
        Wo = sbl("Wo", [128, 8, D], BF)
        self.col_load(self.gcol[:], "gcol", I["norm_mix_g"][l], 8)
        m0 = self.aoff
        self.wstage = [sbl("wst%d" % i_, [128, 2048]) for i_ in range(2)]
        win = I["w_in"][l]
        gsc = lambda c: self.gcol[:, c:c + 1]
        self.prep_w(8, 512, lambda c, s0, n: win[c * 128:(c + 1) * 128, RP:RP + 512],
                    lambda c, s0, n: Wq[:, c, 0:512].rearrange("p (j g d) -> p g j d", j=4, g=2), lambda c: "Wq_%d" % c, "col", gsc,
                    sview=lambda a: a.rearrange("p (g j d) -> p g j d", g=2, j=4))
        self.prep_w(8, 256, lambda c, s0, n: win[c * 128:(c + 1) * 128, RP + 512:RP + 768],
                    lambda c, s0, n: Wq[:, c, 512:768], lambda c: "Wq_%d" % c, "col", gsc)
        self.prep_w(8, D, lambda c, s0, n: win[c * 128:(c + 1) * 128, 3584 + s0:3584 + s0 + n],
                    lambda c, s0, n: Wg[:, c, s0:s0 + n], lambda c: "Wga_%d" % c, "col", gsc)
        wbr = I["w_br_attn"][l]
        self.prep_w(4, D, lambda c, s0, n: wbr[c * 128:(c + 1) * 128, s0:s0 + n],
                    lambda c, s0, n: Wa[:, c, s0:s0 + n], lambda c: "Wa_%d" % c, "plain")
        wo = I["w_out"][l]
        self.prep_w(8, D, lambda c, s0, n: wo[c * 128:(c + 1) * 128, s0:s0 + n],
                    lambda c, s0, n: Wo[:, c, s0:s0 + n], lambda c: "Wo_%d" % c, "plain")
        self.release(m0)
        amask = sbl("amask", [128, 1024])
        fw.dma(amask[:, 0:768], I["c_amask"], w=["amask"], key="amask")
        fw.dma(amask[:, 768:1024], I["c_amask0"], w=["amask"], key="amask")
        smask = sbl("smask", [32, 132])
        fw.dma(smask[:], I["c_smask"], w=["smask"], key="smask")
        sinks = sbl("sinks", [128, NH])
        self.bcast_load(sinks[:], "sinks", I["attn_sinks"][l])
        hTd = [sbl("hT%d" % i_, [128, 8, 128], BF) for i_ in range(2)]
        hT = hTd[1]
        qkv = sbl("qkv", [128, 768])
        rot = sbl("rot", [128, 640])
        rtmp = [sbl("rtmp%d" % i, [128, 320]) for i in range(2)]
        rotb = sbl("rotb", [128, 640], BF)
        cs = [sbl("cs%d" % i, [128, 64]) for i in range(2)]
        qT = sbl("qT", [128, 4, 128], BF)
        KTr = sbl("KTr", [128, 2, 128], BF)
        Vp = sbl("Vp", [128, 2, 2, 2, 128], BF)
        sc = sbl("sc", [128, 4, 256])
        st = sbl("st", [128, 16])
        pbf = sbl("pbf", [128, 4, 256], BF)
        pT = sbl("pT", [128, 4, 2, 128], BF)
        oT = sbl("oT", [128, 4, 128], BF)
        sga = sbl("sga", [128, 8, 128])
        mrl = [sbl("mrl%d" % i, [128, 8, 128], BF) for i in range(2)]
        mg = sbl("mg", [128, 8, 128], BF)
        xo = [sbl("xo%d" % i, [128, D]) for i in range(2)]
        KA = sbl("KA", [128, NS, 128])
        VA = sbl("VA", [128, NS, 128])
        VAb = sbl("VAb", [128, NS, 128], BF)
        KB = sbl("KB", [4, NS, 128])
        VBt = sbl("VB", [4, NS, 128])
        VBb = sbl("VBb", [4, NS, 128], BF)
        KAT = sbl("KAT", [128, NS, 128], BF)
        KBT = sbl("KBT", [128, NS, 4], BF)
        qbd = sbl("qbd", [128, NS, 32], BF)
        ssc = sbl("ssc", [32, NS, 132])
        sst = sbl("sst", [32, 4 * NS])
        spb = sbl("spb", [32, NS, 132], BF)
        spT = sbl("spT", [128, NS, 32], BF)
        spTB = sbl("spTB", [4, NS, 32], BF)
        oTs = sbl("oTs", [128, 4, MS], BF)

        self.V(lambda e: e.memset(Vp[:], 0.0), w=["Vp0", "Vp1"])
        self.V(lambda e: e.memset(KTr[:], 0.0), w=["KTr0", "KTr1"])
        self.V(lambda e: e.memset(qbd[:], 0.0), w=["qbd"])

        def proj_rope(M, hcur, hk, cosap, sinap, cskey):
            for g0, n in [(0, 512), (512, 256)]:
                ps, pk = self.pf()
                for c in range(8):
                    fw.mm(ps[0:M, 0:n], hcur(c), Wq[:, c, g0:g0 + n], c == 0, c == 7, r=[hk, "Wq_%d" % c], w=[pk])
                fw.act(qkv[0:M, g0:g0 + n], ps[0:M, 0:n], AF.Copy, r=[pk], w=["qkv%d" % (g0 // 512)])
            qk3 = qkv[0:M, 0:640].rearrange("p (h d) -> p h d", h=10)
            r3 = rot[0:M, :].rearrange("p (h d) -> p h d", h=10)
            x1, x2 = qk3[:, :, 0:32], qk3[:, :, 32:64]
            cb = cosap.unsqueeze(1).to_broadcast([M, 10, 32])
            sb_ = sinap.unsqueeze(1).to_broadcast([M, 10, 32])
            ta = rtmp[0][0:M, :].rearrange("p (h d) -> p h d", h=10)
            tb = rtmp[1][0:M, :].rearrange("p (h d) -> p h d", h=10)
            rk = ["qkv0", "qkv1", cskey]
            self.V(lambda e: e.tensor_tensor(ta, x1, cb, ALU.mult), r=rk, w=["rtmp0"])
            self.P(lambda e: e.tensor_tensor(tb, x2, sb_, ALU.mult), r=rk, w=["rtmp1"])
            self.V(lambda e: e.tensor_tensor(r3[:, :, 0:32], ta, tb, ALU.subtract), r=["rtmp0", "rtmp1"], w=["rot"])
            self.V(lambda e: e.tensor_tensor(ta, x2, cb, ALU.mult), r=rk + ["rot"], w=["rtmp0"])
            self.P(lambda e: e.tensor_tensor(tb, x1, sb_, ALU.mult), r=rk + ["rot"], w=["rtmp1"])
            self.V(lambda e: e.tensor_tensor(r3[:, :, 32:64], ta, tb, ALU.add), r=["rtmp0", "rtmp1"], w=["rot"])
            fw.act(rotb[0:M, :], rot[0:M, :], AF.Copy, r=["rot"], w=["rotb"])

        def q_transposes(M, dst, dkey):
            pbk, pk = self.pb()
            for jj in range(4):
                fw.tr(pbk[:, jj * M:(jj + 1) * M], rotb[0:M, jj * 128:(jj + 1) * 128], identb[0:M, 0:M], r=["rotb", "identb"], w=[pk])
            fw.act(dst, pbk[:, 0:4 * M].rearrange("p (j t) -> p j t", j=4), AF.Copy, r=[pk], w=[dkey])

        def gate_out(M, hcur, hk, oTt, okey, mr, mrk, xt, xk, xo_, xok):
            for half in range(2):
                pg, pgk = self.pf()
                for q in range(4):
                    dc = half * 4 + q
                    for c in range(8):
                        fw.mm(pg[:, q * M:(q + 1) * M], Wg[:, c, dc * 128:(dc + 1) * 128], hcur(c), c == 0, c == 7, r=[hk, "Wga_%d" % c], w=[pgk])
                fw.act(sga[:, half * 4:(half + 1) * 4, 0:M], pg[:, 0:4 * M].rearrange("p (q t) -> p q t", q=4), AF.Sigmoid, r=[pgk], w=["sga%d" % half])
                pbr, pbk_ = self.pf()
                for q in range(4):
                    dc = half * 4 + q
                    for cc in range(4):
                        fw.mm(pbr[:, q * M:(q + 1) * M], Wa[:, cc, dc * 128:(dc + 1) * 128], oTt[:, cc, 0:M], cc == 0, cc == 3, r=[okey, "Wa_%d" % cc], w=[pbk_])
                hs = slice(half * 4, (half + 1) * 4)
                self.V(lambda e, hs=hs, pbr=pbr: e.tensor_tensor(sga[:, hs, 0:M], sga[:, hs, 0:M], pbr[:, 0:4 * M].rearrange("p (q t) -> p q t", q=4), ALU.mult),
                       r=["sga%d" % half, pbk_], w=["sga%d" % half])
                self.V(lambda e, hs=hs: e.tensor_tensor(mg[:, hs, 0:M], sga[:, hs, 0:M], mr[:, hs, 0:M], ALU.add), r=["sga%d" % half, mrk], w=["mg%d" % half])
            for grp in range(2):
                px, pxk = self.pf()
                for dc in range(8):
                    fw.mm(px[0:M, :], mg[:, dc, 0:M], Wo[:, dc, grp * 512:(grp + 1) * 512], dc == 0, dc == 7, r=["mg%d" % (dc // 4), "Wo_%d" % dc], w=[pxk])
                self.V(lambda e, grp=grp, px=px: e.tensor_tensor(xo_[0:M, grp * 512:(grp + 1) * 512], xt[0:M, grp * 512:(grp + 1) * 512], px[0:M, :], ALU.add),
                       r=[xk, pxk], w=[xok])

        def put_kv(slot):
            pbk, pk = self.pb()
            fw.tr(pbk[:, 0:128], rotb[:, 512:640], identb[:, :], r=["rotb", "identb"], w=[pk])
            self.V(lambda e, pbk=pbk, slot=slot: e.tensor_copy(KTr[:, slot, :], pbk[:, 0:128]), r=[pk], w=["KTr%d" % slot])
            for g in range(2):
                vsrc = qkv[:, 640 + g * 64:640 + (g + 1) * 64]
                fw.act(Vp[:, slot, g, 0, 0:64], vsrc, AF.Copy, r=["qkv1"], w=["Vp%d" % slot])
                self.P(lambda e, g=g, vsrc=vsrc, slot=slot: e.tensor_copy(Vp[:, slot, g, 1, 64:128], vsrc), r=["qkv1"], w=["Vp%d" % slot])

        xt, xk = self.xt[1], "xt1"
        fw.dma(xt[:], (I["xh0"] if (l == 0 or NSEG == 1) else self.xh_dram), r=["xh_dram"], w=[xk], key=xk)
        fw.dma(cs[1][:, 0:32], I["c_cosh"], w=["cs1"], key="cs1")
        fw.dma(cs[1][:, 32:64], I["c_sinh"], w=["cs1"], key="cs1")
        self.norm_hT(xt, xk, 128, hT[:, :, :], "hT1", identb)
        proj_rope(128, lambda c: hT[:, c, :], "hT1", cs[1][:, 0:32], cs[1][:, 32:64], "cs1")
        put_kv(1)
        def pre(i):
            xt, xk = self.xt[i % 2], "xt%d" % (i % 2)
            src, _ = self.xsrc(l, i)
            fw.dma(xt[:], src, r=[("xb", i)], w=[xk], key=xk)
            mr, mrk = mrl[i % 2], "mrl%d" % (i % 2)
            fw.dma(mr[:, :, :], self.mrbuf[i].rearrange("p (c t) -> p c t", c=8), r=[("mr", i)], w=[mrk], key=mrk)
            ck_ = "cs%d" % (i % 2)
            fw.dma(cs[i % 2][:, 0:32], I["c_cosp"][i * 128:(i + 1) * 128, :], w=[ck_], key=ck_)
            fw.dma(cs[i % 2][:, 32:64], I["c_sinp"][i * 128:(i + 1) * 128, :], w=[ck_], key=ck_)
            self.norm_hT(xt, xk, 128, hTd[i % 2][:, :, :], "hT%d" % (i % 2), identb)

        pre(0)
        for i in range(NT):
            xt, xk = self.xt[i % 2], "xt%d" % (i % 2)
            mr, mrk = mrl[i % 2], "mrl%d" % (i % 2)
            ck_ = "cs%d" % (i % 2)
            hkk = "hT%d" % (i % 2)
            hcur = lambda c, i=i: hTd[i % 2][:, c, :]
            proj_rope(128, hcur, hkk, cs[i % 2][:, 0:32], cs[i % 2][:, 32:64], ck_)
            slot = i % 2
            if i == NT - 1:
                fw.dma(O["p_k"][l], rot[:, 512:640], r=["rot"], key="rot")
                fw.dma(O["p_v"][l], qkv[:, 640:768], r=["qkv1"], key="qkv1")
            q_transposes(128, qT[:, :, :], "qT")
            put_kv(slot)
            mvar = 3 if i == 0 else slot
            msk = amask[:, mvar * 256:(mvar + 1) * 256].unsqueeze(1).to_broadcast([128, 4, 256])
            for g in range(2):
                o = g * 64
                if g == 1 and i + 1 < NT:
                    pre(i + 1)
                pS = []
                for jj in range(4):
                    if jj % 2 == 0:
                        ps, pk = self.pf()
                        pS.append((ps, pk))
                    fw.mm(ps[:, (jj % 2) * 256:(jj % 2 + 1) * 256], qT[o:o + 64, jj, :], KTr[o:o + 64, :, :].rearrange("p s t -> p (s t)"),
                          True, True, r=["qT", "KTr0", "KTr1"], w=[pk])
                for half, (ps, pk) in enumerate(pS):
                    self.V(lambda e, ps=ps, half=half, msk=msk: e.scalar_tensor_tensor(
                        sc[:, half * 2:(half + 1) * 2, :], ps[:, :].rearrange("p (j c) -> p j c", j=2), 0.125,
                        msk[:, 0:2, :], ALU.mult, ALU.add), r=[pk, "amask"], w=["sc%d" % half])
                sck = ["sc0", "sc1"]
                self.V(lambda e: e.tensor_reduce(st[:, 0:4], sc[:, :, :], AX.X, ALU.max), r=sck, w=["st"])
                self.V(lambda e, g=g: e.tensor_tensor(st[:, 0:4], st[:, 0:4], sinks[:, g * 4:(g + 1) * 4], ALU.max), r=["st", "sinks"], w=["st"])
                self.V(lambda e: e.tensor_tensor(sc[:, :, :], sc[:, :, :], bc3(st[:, 0:4], 256), ALU.subtract), r=sck + ["st"], w=sck)
                fw.act(sc[:, :, :], sc[:, :, :], AF.Exp, r=sck, w=sck)
                self.V(lambda e: e.tensor_reduce(st[:, 4:8], sc[:, :, :], AX.X, ALU.add), r=sck, w=["st2"])
                self.V(lambda e, g=g: e.tensor_tensor(st[:, 8:12], sinks[:, g * 4:(g + 1) * 4], st[:, 0:4], ALU.subtract), r=["st", "sinks"], w=["st3"])
                fw.act(st[:, 8:12], st[:, 8:12], AF.Exp, r=["st3"], w=["st3"])
                self.V(lambda e: e.tensor_tensor(st[:, 4:8], st[:, 4:8], st[:, 8:12], ALU.add), r=["st2", "st3"], w=["st2"])
                self.V(lambda e: e.reciprocal(st[:, 4:8], st[:, 4:8]), r=["st2"], w=["st2"])
                self.V(lambda e: e.tensor_tensor(pbf[:, :, :], sc[:, :, :], bc3(st[:, 4:8], 256), ALU.mult), r=sck + ["st2"], w=["pbf"])
                pbk, pk = self.pb()
                for jj in range(4):
                    for s_ in range(2):
                        fw.tr(pbk[:, (jj * 2 + s_) * 128:(jj * 2 + s_ + 1) * 128], pbf[:, jj, s_ * 128:(s_ + 1) * 128], identb[:, :], r=["pbf", "identb"], w=[pk])
                fw.act(pT[:, :, :, :], pbk[:, :].rearrange("p (j s t) -> p j s t", j=4, s=2), AF.Copy, r=[pk], w=["pT"])
                if g == 0:
                    pO, pok = self.pf()
                for c2 in range(2):
                    cc = g * 2 + c2
                    n = 0
                    for par in range(2):
                        jj = c2 * 2 + par
                        for s_ in range(2):
                            fw.mm(pO[:, cc * 128:(cc + 1) * 128], Vp[:, s_, g, par, :], pT[:, jj, s_, :], n == 0, n == 3,
                                  r=["Vp0", "Vp1", "pT"], w=[pok])
                            n += 1
            fw.act(oT[:, :, :], pO[:, :].rearrange("p (c t) -> p c t", c=4), AF.Copy, r=[pok], w=["oT"])
            xo_, xok = xo[i % 2], "xo%d" % (i % 2)
            gate_out(128, hcur, hkk, oT, "oT", mr, mrk, xt, xk, xo_, xok)
            fw.dma(self.xbuf[i * 128:(i + 1) * 128, :], xo_[:, :], r=[xok], w=[("xb", i)], key=xok)
        if NSEG > 1:
            self.gather_select(xo_[:, :], [xok], D, self.agX_in, self.agX_out, "agX")
            fw.dma(self.xh_dram, xo_[:, :], r=[xok], w=["xh_dram"], key="xhst")

        i = NT
        xt, xk = self.xt[i % 2], "xt%d" % (i % 2)
        src, _ = self.xsrc(l, i)
        fw.dma(xt[0:MS, :], src, r=[("xb", i)], w=[xk], key=xk)
        mr, mrk = mrl[i % 2], "mrl%d" % (i % 2)
        fw.dma(mr[:, :, 0:MS], self.mrbuf[NT].rearrange("p (c t) -> p c t", c=8)[:, :, 0:MS], r=[("mr", NT)], w=[mrk], key=mrk)
        ck_ = "cs%d" % (i % 2)
        fw.dma(cs[i % 2][0:MS, 0:32], I["c_coss"], w=[ck_], key=ck_)
        fw.dma(cs[i % 2][0:MS, 32:64], I["c_sins"], w=[ck_], key=ck_)
        self.norm_hT(xt, xk, MS, hT[:, :, 0:MS], "hT1", identb)
        hcur = lambda c: hT[:, c, 0:MS]
        proj_rope(MS, hcur, "hT1", cs[i % 2][0:MS, 0:32], cs[i % 2][0:MS, 32:64], ck_)
        for (cin, cout, srcap, srck, dkey) in [("ck", "s_k", rot[:, 512:640], "rot", "sk"), ("cv", "s_v", qkv[:, 640:768], "qkv1", "sv")]:
            fw.dma(O[cout][l, :, 0:124, :], I[cin][l, :, 4:128, :], w=[dkey], key=dkey + "c")
            for t in range(4):
                fw.dma(O[cout][l, :, 124 + t, :], srcap[t * 16:(t + 1) * 16, :], r=[srck], w=[dkey], key=dkey + "n")
        fw.dma(KA[:, :, :], O["s_k"][l].rearrange("q p c -> p q c"), r=["sk"], w=["KA"], key="KA")
        fw.dma(VA[:, :, :], O["s_v"][l].rearrange("q p c -> p q c"), r=["sv"], w=["VA"], key="VA")
        fw.dma(KB[:, :, :], I["ck"][l, :, 0:4, :].rearrange("q p c -> p q c"), w=["KB"], key="KB")
        fw.dma(VBt[:, :, :], I["cv"][l, :, 0:4, :].rearrange("q p c -> p q c"), w=["VB"], key="VB")
        self.P(lambda e: e.tensor_copy(VAb[:, :, :], VA[:, :, :]), r=["VA"], w=["VAb"])
        self.P(lambda e: e.tensor_copy(VBb[:, :, :], VBt[:, :, :]), r=["VB"], w=["VBb"])
        for q4 in range(4):
            ps, pk = self.pf()
            for qq in range(4):
                q = q4 * 4 + qq
                fw.tr(ps[:, qq * 128:(qq + 1) * 128], KA[:, q, :], identf[:, :], r=["KA", "identf"], w=[pk])
            fw.act(KAT[:, q4 * 4:(q4 + 1) * 4, :], ps[:, :].rearrange("p (q t) -> p q t", q=4), AF.Copy, r=[pk], w=["KAT"])
        ps, pk = self.pf()
        for q in range(NS):
            fw.tr(ps[:, q * 4:(q + 1) * 4], KB[0:4, q, :], identf[0:4, 0:4], r=["KB", "identf"], w=[pk])
        fw.act(KBT[:, :, :], ps[:, 0:64].rearrange("p (q t) -> p q t", q=NS), AF.Copy, r=[pk], w=["KBT"])
        q_transposes(MS, qT[:, :, 0:MS], "qT")
        for g in range(2):
            for jj in range(4):
                o = g * 64
                dst = qbd[o:o + 64, :, g * 16 + jj * 4:g * 16 + (jj + 1) * 4]
                srcq = qT[o:o + 64, jj, 0:MS].rearrange("p (t q) -> p q t", t=4)
                self.V(lambda e, dst=dst, srcq=srcq: e.tensor_copy(dst, srcq), r=["qT"], w=["qbd"])
        pSA = []
        for q4 in range(4):
            ps, pk = self.pf()
            pSA.append((ps, pk))
            for qq in range(4):
                q = q4 * 4 + qq
                fw.mm(ps[0:32, qq * 128:(qq + 1) * 128], qbd[:, q, :], KAT[:, q, :], True, True, r=["qbd", "KAT"], w=[pk])
        psB, pkB = self.pf()
        for q in range(NS):
            fw.mm(psB[0:32, q * 4:(q + 1) * 4], qbd[:, q, :], KBT[:, q, :], True, True, r=["qbd", "KBT"], w=[pkB])
        for q4, (ps, pk) in enumerate(pSA):
            self.V(lambda e, q4=q4, ps=ps: e.scalar_tensor_tensor(
                ssc[:, q4 * 4:(q4 + 1) * 4, 0:128], ps[0:32, :].rearrange("p (q c) -> p q c", q=4), 0.125,
                smask[:, 0:128].unsqueeze(1).to_broadcast([32, 4, 128]), ALU.mult, ALU.add), r=[pk, "smask"], w=["ssc"])
        self.V(lambda e: e.scalar_tensor_tensor(
            ssc[:, :, 128:132], psB[0:32, 0:64].rearrange("p (q c) -> p q c", q=NS), 0.125,
            smask[:, 128:132].unsqueeze(1).to_broadcast([32, NS, 4]), ALU.mult, ALU.add), r=[pkB, "smask"], w=["ssc"])
        sinkc = sbl("sinkc", [32, 1])
        for g in range(2):
            for jj in range(4):
                p0 = g * 16 + jj * 4
                fw.dma(sinkc[p0:p0 + 4, :], I["attn_sinks"][l, g * 4 + jj:g * 4 + jj + 1].partition_broadcast(4), w=["sinkc"], key="sinkc")
        self.V(lambda e: e.tensor_reduce(sst[:, 0:NS], ssc[:, :, :], AX.X, ALU.max), r=["ssc"], w=["sst"])
        self.V(lambda e: e.tensor_scalar(sst[:, 0:NS], sst[:, 0:NS], sinkc[:, 0:1], None, ALU.max), r=["sst", "sinkc"], w=["sst"])
        self.V(lambda e: e.tensor_tensor(ssc[:, :, :], ssc[:, :, :], bc3(sst[:, 0:NS], 132), ALU.subtract), r=["ssc", "sst"], w=["ssc"])
        fw.act(ssc[:, :, :], ssc[:, :, :], AF.Exp, r=["ssc"], w=["ssc"])
        self.V(lambda e: e.tensor_reduce(sst[:, NS:2 * NS], ssc[:, :, :], AX.X, ALU.add), r=["ssc"], w=["sst2"])
        self.V(lambda e: e.tensor_scalar(sst[:, 2 * NS:3 * NS], sst[:, 0:NS], sinkc[:, 0:1], None, ALU.subtract), r=["sst", "sinkc"], w=["sst3"])
        fw.act(sst[:, 2 * NS:3 * NS], sst[:, 2 * NS:3 * NS], AF.Exp, r=["sst3"], w=["sst3"], scale=-1.0)
        self.V(lambda e: e.tensor_tensor(sst[:, NS:2 * NS], sst[:, NS:2 * NS], sst[:, 2 * NS:3 * NS], ALU.add), r=["sst2", "sst3"], w=["sst2"])
        self.V(lambda e: e.reciprocal(sst[:, NS:2 * NS], sst[:, NS:2 * NS]), r=["sst2"], w=["sst2"])
        self.V(lambda e: e.tensor_tensor(spb[:, :, :], ssc[:, :, :], bc3(sst[:, NS:2 * NS], 132), ALU.mult), r=["ssc", "sst2"], w=["spb"])
        identb32 = identb[0:32, 0:32]
        for q8 in range(2):
            pbk, pk = self.pb()
            for qq in range(8):
                q = q8 * 8 + qq
                fw.tr(pbk[:, qq * 32:(qq + 1) * 32], spb[:, q, 0:128], identb32, r=["spb", "identb"], w=[pk])
            fw.act(spT[:, q8 * 8:(q8 + 1) * 8, :], pbk[:, 0:256].rearrange("p (q c) -> p q c", q=8), AF.Copy, r=[pk], w=["spT"])
        pbk, pk = self.pb()
        for q in range(NS):
            fw.tr(pbk[0:4, q * 32:(q + 1) * 32], spb[:, q, 128:132], identb32, r=["spb", "identb"], w=[pk])
        fw.act(spTB[:, :, :], pbk[0:4, 0:512].rearrange("p (q c) -> p q c", q=NS), AF.Copy, r=[pk], w=["spTB"])
        pO, pok = self.pf()
        for q in range(NS):
            fw.mm(pO[:, q * 32:(q + 1) * 32], VAb[:, q, :], spT[:, q, :], True, False, r=["VAb", "spT"], w=[pok])
            fw.mm(pO[:, q * 32:(q + 1) * 32], VBb[0:4, q, :], spTB[0:4, q, :], False, True, r=["VBb", "spTB"], w=[pok])
        oraw = sbl("oraw", [128, 32, NS], BF)
        fw.act(oraw.rearrange("p c q -> p q c"), pO[:, :].rearrange("p (q c) -> p q c", q=NS), AF.Copy, r=[pok], w=["oraw"])
        for g in range(2):
            for jj in range(4):
                cc, par = g * 2 + jj // 2, jj % 2
                c0 = g * 16 + jj * 4
                srco = oraw[g * 64:(g + 1) * 64, c0:c0 + 4, :].rearrange("p t q -> p (t q)")
                fw.dma(oTs[par * 64:(par + 1) * 64, cc, :], srco, r=["oraw"], w=["oTs"], key="oTs")
        xo_, xok = xo[i % 2], "xo%d" % (i % 2)
        gate_out(MS, hcur, "hT1", oTs, "oTs", mr, mrk, xt, xk, xo_, xok)
        fw.dma(self.xsbuf, xo_[0:MS, :], r=[xok], w=[("xb", NT)], key=xok)

    def pass_ffn(self, l, es2):
        fw, I, O, NT = self.fw, self.I, self.O, self.NT
        sbl = lambda n, s, dt=F32: self.sbl(es2, "f%d_" % l + n, s, dt)
        identb, identf = self.identb, self.identf
        Wc = sbl("Wc", [128, 8, DFF], BF)
        Wu = sbl("Wu", [128, 8, DFF], BF)
        Wd = sbl("Wd", [128, NFC, D], BF)
        self.col_load(self.gcol[:], "gcol", I["norm_ffn_g"][l], 8)
        cw = sbl("cw", [128, 4, NFC])
        for j in range(3):
            self.col_load(cw[:, j, :], "cw", I["ffn_conv_w"][l, j], NFC)
        self.col_load(cw[:, 3, :], "cw", I["ffn_conv_b"][l], NFC)
        m0 = self.aoff
        self.wstage = [sbl("wst%d" % i_, [128, 2048]) for i_ in range(2)]
        wi = I["ffn_w_in"][l]
        gsc = lambda c: self.gcol[:, c:c + 1]
        self.prep_w(8, DFF, lambda c, s0, n: wi[c * 128:(c + 1) * 128, s0:s0 + n],
                    lambda c, s0, n: Wc[:, c, s0:s0 + n], lambda c: "Wc_%d" % c, "col", gsc)
        self.prep_w(8, DFF, lambda c, s0, n: wi[c * 128:(c + 1) * 128, DFF + s0:DFF + s0 + n],
                    lambda c, s0, n: Wu[:, c, s0:s0 + n], lambda c: "Wu_%d" % c, "col", gsc)
        wd = I["ffn_w_down"][l]
        self.prep_w(NFC, D, lambda c, s0, n: wd[c * 128:(c + 1) * 128, s0:s0 + n],
                    lambda c, s0, n: Wd[:, c, s0:s0 + n], lambda c: "Wd_%d" % c, "plain")
        self.release(m0)
        last = (l == 1)
        if last:
            gf = sbl("gf", [128, D])
            self.bcast_load(gf[:], "gf", I["norm_final_g"])
        hTd = [sbl("hT%d" % i_, [128, 8, 128], BF) for i_ in range(2)]
        hT = hTd[0]
        cxf = sbl("cx", [128, NFC * 130])
        cx1 = cxf.rearrange("p (f t) -> p f t", f=NFC)
        cxs = cxf[:, 0:NFC * NS * 6].rearrange("p (f q j) -> p f q j", f=NFC, q=NS)
        acc = [sbl("acc%d" % i_, [128, 4, 128]) for i_ in range(2)]
        aT = sbl("aT", [128, NFC, 128], BF)
        xo = [sbl("xo%d" % i_, [128, D]) for i_ in range(2)]
        ctok = sbl("ctok", [128, DFF])
        cst = ctok
        jk = self.xn

        def finish(M, xt, xk, xo_, xok, dst_final, dst_x, dkey):
            for grp in range(2):
                px, pxk = self.pf()
                for fc in range(NFC):
                    fw.mm(px[0:M, :], aT[:, fc, 0:M], Wd[:, fc, grp * 512:(grp + 1) * 512], fc == 0, fc == NFC - 1, r=["aT", "Wd_%d" % fc], w=[pxk])
                self.V(lambda e, grp=grp, px=px: e.tensor_tensor(xo_[0:M, grp * 512:(grp + 1) * 512], xt[0:M, grp * 512:(grp + 1) * 512], px[0:M, :], ALU.add),
                       r=[xk, pxk], w=[xok])
            if not last:
                fw.dma(dst_x, xo_[0:M, :], r=[xok], w=[dkey], key=xok)
                return
            ss, t1 = self.ss, self.t1
            fw.act(jk[0:M, :], xo_[0:M, :], AF.Square, r=[xok], w=["xn", "ss"], accum_out=ss[0:M, :])
            self.V(lambda e: e.tensor_scalar(t1[0:M, :], ss[0:M, :], 1.0 / D, 1e-6, ALU.mult, ALU.add), r=["ss"], w=["t1"])
            fw.act(t1[0:M, :], t1[0:M, :], AF.Sqrt, r=["t1"], w=["t1"])
            self.V(lambda e: e.reciprocal(t1[0:M, :], t1[0:M, :]), r=["t1"], w=["t1"])
            self.V(lambda e: e.scalar_tensor_tensor(xo_[0:M, :], xo_[0:M, :], t1[0:M, 0:1], gf[0:M, :], ALU.mult, ALU.mult),
                   r=[xok, "t1", "gf"], w=[xok])
            fw.dma(dst_final, xo_[0:M, :], r=[xok], key=xok)

        def ffn_core(M, hcur, hk, cview, ckey, sample, mid=None):
            for b0 in range(0, NFC, 4):
                nb = min(4, NFC - b0)
                pc, pck = self.pf()
                for q in range(nb):
                    fc = b0 + q
                    for c in range(8):
                        fw.mm(pc[:, q * M:(q + 1) * M], Wc[:, c, fc * 128:(fc + 1) * 128], hcur(c), c == 0, c == 7, r=[hk, "Wc_%d" % c], w=[pck])
                pu, puk = self.pf()
                for q in range(nb):
                    fc = b0 + q
                    for c in range(8):
                        fw.mm(pu[:, q * M:(q + 1) * M], Wu[:, c, fc * 128:(fc + 1) * 128], hcur(c), c == 0, c == 7, r=[hk, "Wu_%d" % c], w=[puk])
                if sample:
                    fw.act(cview[:, b0:b0 + nb, :, 2:6], pc[:, 0:nb * M].rearrange("p (f t q) -> p f q t", f=nb, t=4), AF.Copy, r=[pck], w=[ckey])
                else:
                    fw.act(cview[:, b0:b0 + nb, 2:130], pc[:, 0:nb * M].rearrange("p (f t) -> p f t", f=nb), AF.Copy, r=[pck], w=[ckey])
                a_ = acc[(b0 // 4) % 2]
                ak = "acc%d" % ((b0 // 4) % 2)
                for q in range(nb):
                    fc = b0 + q
                    if sample:
                        c0, c1, c2 = (cview[:, fc, :, s_:s_ + 4] for s_ in range(3))
                        av = a_[:, q, 0:M].rearrange("p (t q) -> p q t", t=4)
                    else:
                        c0, c1, c2 = (cview[:, fc, s_:s_ + 128] for s_ in range(3))
                        av = a_[:, q, :]
                    self.P(lambda e, av=av, c0=c0, fc=fc: e.tensor_scalar(av, c0, cw[:, 0, fc:fc + 1], cw[:, 3, fc:fc + 1], ALU.mult, ALU.add),
                           r=[ckey, "cw"], w=[ak])
                    self.V(lambda e, av=av, c1=c1, fc=fc: e.scalar_tensor_tensor(av, c1, cw[:, 1, fc:fc + 1], av, ALU.mult, ALU.add),
                           r=[ckey, "cw", ak], w=[ak])
                    self.V(lambda e, av=av, c2=c2, fc=fc: e.scalar_tensor_tensor(av, c2, cw[:, 2, fc:fc + 1], av, ALU.mult, ALU.add),
                           r=[ckey, "cw", ak], w=[ak])
                fw.act(a_[:, 0:nb, 0:M], a_[:, 0:nb, 0:M], AF.Gelu, r=[ak], w=[ak])
                self.V(lambda e, a_=a_, pu=pu, nb=nb, b0=b0: e.tensor_tensor(aT[:, b0:b0 + nb, 0:M], a_[:, 0:nb, 0:M],
                                                                       pu[:, 0:nb * M].rearrange("p (f t) -> p f t", f=nb), ALU.mult),
                       r=[ak, puk], w=["aT"])
                if mid is not None and b0 == 4:
                    mid()

        def c_token_major(M, hcur, hk, rows, dsts):
            for g0 in range(0, DFF, 512):
                n = min(512, DFF - g0)
                ps, pk = self.pf()
                for c in range(8):
                    fw.mm(ps[0:M, 0:n], hcur(c), Wc[:, c, g0:g0 + n], c == 0, c == 7, r=[hk, "Wc_%d" % c], w=[pk])
                fw.act(ctok[0:M, g0:g0 + n], ps[0:M, 0:n], AF.Copy, r=[pk], w=["ctok"])
            for (r0, r1), dst in zip(rows, dsts):
                fw.dma(dst, ctok[r0:r1, :], r=["ctok"], key="ctok")

        xt, xk = self.xt[1], "xt1"
        fw.dma(xt[:], (I["xh0"] if NSEG == 1 else self.xh_dram), r=["xh_dram"], w=[xk], key=xk)
        self.norm_hT(xt, xk, 128, hT[:, :, :], "hT0", identb)
        pc, pck = self.pf()
        for fc in range(NFC):
            for c in range(8):
                fw.mm(pc[:, fc * 2:(fc + 1) * 2], Wc[:, c, fc * 128:(fc + 1) * 128], hT[:, c, 126:128], c == 0, c == 7, r=["hT0", "Wc_%d" % c], w=[pck])
        fw.act(cx1[:, :, 0:2], pc[:, 0:2 * NFC].rearrange("p (f t) -> p f t", f=NFC), AF.Copy, r=[pck], w=["cx"])
        def pre(i):
            xt, xk = self.xt[i % 2], "xt%d" % (i % 2)
            fw.dma(xt[:], self.xbuf[i * 128:(i + 1) * 128, :], r=[("xb", i)], w=[xk], key=xk)
            self.norm_hT(xt, xk, 128, hTd[i % 2][:, :, :], "hT%d" % (i % 2), identb)

        pre(0)
        for i in range(NT):
            xt, xk = self.xt[i % 2], "xt%d" % (i % 2)
            hcur = lambda c, i=i: hTd[i % 2][:, c, :]
            hkk = "hT%d" % (i % 2)
            mid = (lambda i=i: pre(i + 1)) if i + 1 < NT else None
            cv_, ckey = cx1, "cx"
            if i > 0:
                self.P(lambda e: e.tensor_copy(acc[0][:, 0, 0:2 * NFC].rearrange("p (f t) -> p f t", f=NFC), cx1[:, :, 128:130]), r=[ckey], w=["acc0"])
                self.P(lambda e: e.tensor_copy(cx1[:, :, 0:2], acc[0][:, 0, 0:2 * NFC].rearrange("p (f t) -> p f t", f=NFC)), r=["acc0"], w=[ckey])
            ffn_core(128, hcur, hkk, cv_, ckey, False, mid)
            if i == NT - 1:
                c_token_major(128, hcur, hkk, [(126, 128)], [O["p_conv"][l]])
            xo_, xok = xo[i % 2], "xo%d" % (i % 2)
            finish(128, xt, xk, xo_, xok, O["yp"][i * 128:(i + 1) * 128, :], self.xbuf[i * 128:(i + 1) * 128, :], ("xb", i))
        if not last and NSEG > 1:
            self.gather_select(xo_[:, :], [xok], D, self.agX_in, self.agX_out, "agX")
            fw.dma(self.xh_dram, xo_[:, :], r=[xok], w=["xh_dram"], key="xhst")

        i = NT
        xt, xk = self.xt[i % 2], "xt%d" % (i % 2)
        fw.dma(xt[0:MS, :], self.xsbuf, r=[("xb", i)], w=[xk], key=xk)
        self.norm_hT(xt, xk, MS, hT[:, :, 0:MS], "hT0", identb)
        hcur = lambda c: hT[:, c, 0:MS]
        fw.dma(cst[0:32, :], I["st_conv"][l], w=["ctok"], key="cst")
        for b0 in range(0, NFC, 4):
            nb = min(4, NFC - b0)
            ps, pk = self.pf()
            for q in range(nb):
                fc = b0 + q
                fw.tr(ps[:, q * 32:(q + 1) * 32], cst[0:32, fc * 128:(fc + 1) * 128], identf[0:32, 0:32], r=["ctok", "identf"], w=[pk])
            fw.act(cxs[:, b0:b0 + nb, :, 0:2], ps[:, 0:nb * 32].rearrange("p (f q j) -> p f q j", f=nb, j=2), AF.Copy, r=[pk], w=["cx"])
        ffn_core(MS, hcur, "hT0", cxs, "cx", True)
        sc_ = O["s_conv"][l].rearrange("(q j) f -> j q f", j=2)
        c_token_major(MS, hcur, "hT0", [(32, 48), (48, 64)], [sc_[0], sc_[1]])
        xo_, xok = xo[i % 2], "xo%d" % (i % 2)
        finish(MS, xt, xk, xo_, xok, O["ys"], self.xsbuf, ("xb", NT))


NSEG = 1


def _consts_shared():
    c = {}
    c["c_ident"] = np.eye(128, dtype=np.float32)
    inv = (10000.0 ** (-np.arange(0, HD, 2, dtype=np.float32) / HD)).astype(np.float32)
    pos_s = (PAST + np.repeat(np.arange(4), NS)).astype(np.float32)
    ang_s = pos_s[:, None] * inv[None, :]
    c["c_coss"] = np.cos(ang_s).astype(np.float32)
    c["c_sins"] = np.sin(ang_s).astype(np.float32)
    s = np.arange(128)[:, None]
    t = np.arange(128)[None, :]
    incl = (s <= t).astype(np.float32)
    strict = (s < t).astype(np.float32)
    c["c_tri"] = np.concatenate([incl * CDEC, strict * CDEC], 1).astype(np.float32)
    c["c_mask2"] = np.concatenate([incl, strict], 1).astype(np.float32)
    c["c_maskL"] = (s > t).astype(np.float32)
    i_ = np.arange(128)[:, None]
    j_ = np.arange(128)[None, :]
    cur = np.where(j_ <= i_, 0.0, NEG)
    prev = np.where(j_ > i_, 0.0, NEG)
    dead = np.full((128, 128), NEG)
    c["c_amask"] = np.concatenate([cur, prev, prev, cur, cur, dead], 1).astype(np.float32)
    c["_am_first"] = np.concatenate([cur, dead], 1).astype(np.float32)
    c["_am_mid"] = np.concatenate([cur, prev], 1).astype(np.float32)
    tt = (np.arange(32) % 4)[:, None]
    ia = np.arange(128)[None, :]
    ma = np.where(ia <= 124 + tt, 0.0, NEG)
    rb = np.arange(4)[None, :]
    mb = np.where(rb > tt, 0.0, NEG)
    c["c_smask"] = np.concatenate([ma, mb], 1).astype(np.float32)
    last = np.zeros((128, 1), np.float32)
    last[127, 0] = 1.0
    c["c_last"] = last
    c["_inv"] = inv
    return c


def _rope_tab(pos, inv):
    ang = pos.astype(np.float32)[:, None] * inv[None, :]
    return np.cos(ang).astype(np.float32), np.sin(ang).astype(np.float32)


_CACHE = {}
TAPS = False
TAP_OUT = {}


def kernel(**inp):
    inp = {k: np.asarray(v) for k, v in inp.items()}
    xp_all = inp["x_prompt"].astype(np.float32)
    B, SEQ_, _ = xp_all.shape
    TPC = SEQ_ // NSEG
    if TPC not in _CACHE:
        b_ = Builder(TPC, taps=TAPS)
        _CACHE[TPC] = (b_.build(), b_.tapnames)
    nc, tapnames = _CACHE[TPC]
    consts = _consts_shared()
    inv = consts.pop("_inv")
    am_first, am_mid = consts.pop("_am_first"), consts.pop("_am_mid")
    wnames = ["norm_mix_g", "w_in", "rwkv_mu", "rwkv_w0", "rwkv_w2", "rwkv_a0", "rwkv_a2", "rwkv_g2", "rwkv_k_k",
              "rwkv_k_a", "rwkv_ln_g", "rwkv_ln_b", "attn_sinks", "w_br_rwkv", "w_br_attn", "w_out", "norm_ffn_g",
              "ffn_w_in", "ffn_conv_w", "ffn_conv_b", "ffn_w_down", "norm_final_g"]
    shared = {n: np.ascontiguousarray(inp[n], dtype=np.float32) for n in wnames}
    shared["rwkv_r_k"] = np.ascontiguousarray(inp["rwkv_r_k"], dtype=np.float32).reshape(2, RD)
    shared.update(consts)
    in_maps = []
    ncores = 8
    for c in range(ncores):
        b, seg = (c // NSEG) % B, c % NSEG
        sl = slice(c * NS, (c + 1) * NS)
        m = dict(shared)
        t0 = seg * TPC
        m["xp"] = np.ascontiguousarray(xp_all[b, t0:t0 + TPC])
        m["xh0"] = np.ascontiguousarray(xp_all[b, t0 - 128:t0]) if seg > 0 else np.zeros((128, D), np.float32)
        m["c_cosp"], m["c_sinp"] = _rope_tab(t0 + np.arange(TPC), inv)
        m["c_cosh"], m["c_sinh"] = _rope_tab(np.maximum(t0 - 128 + np.arange(128), 0), inv)
        m["c_amask0"] = am_mid if seg > 0 else am_first
        sel = np.zeros((128, 8), np.float32)
        if seg > 0:
            sel[:, c - 1] = 1.0
        m["c_sel"] = sel
        m["xs"] = np.ascontiguousarray(inp["x_sample"][sl].transpose(1, 0, 2).reshape(MS, D))
        m["st_shift"] = np.ascontiguousarray(inp["state_rwkv_shift"][:, sl])
        m["st_wkv"] = np.ascontiguousarray(inp["state_rwkv_wkv"][:, sl]).reshape(2, 128, 4096)
        m["ck"] = np.ascontiguousarray(inp["cache_swa_k"][:, sl]).reshape(2, NS, 128, 128)
        m["cv"] = np.ascontiguousarray(inp["cache_swa_v"][:, sl]).reshape(2, NS, 128, 128)
        m["st_conv"] = np.ascontiguousarray(inp["state_ffn_conv"][:, sl]).reshape(2, 2 * NS, DFF)
        in_maps.append(m)
    res = run_bass_kernel_spmd(nc, in_maps, core_ids=list(range(ncores)))
    R = res.results
    for tn in tapnames:
        TAP_OUT[tn] = [np.asarray(R[c][tn]) for c in range(ncores)]
    f = np.float32
    lastc = [b * NSEG + NSEG - 1 for b in range(B)]
    y_prompt = np.stack([np.concatenate([R[b * NSEG + sg]["yp"] for sg in range(NSEG)], 0) for b in range(B)]).astype(f)
    y_sample = np.concatenate([R[c]["ys"].reshape(4, NS, D).transpose(1, 0, 2) for c in range(ncores)], 0).astype(f)
    p_shift = np.stack([R[c]["p_shift"] for c in lastc], 1).astype(f)
    p_wkv = np.stack([R[c]["p_wkv"] for c in lastc], 1).astype(f)
    p_k = np.stack([R[c]["p_k"] for c in lastc], 1).reshape(2, B, 128, 2, 64).astype(f)
    p_v = np.stack([R[c]["p_v"] for c in lastc], 1).reshape(2, B, 128, 2, 64).astype(f)
    p_conv = np.stack([R[c]["p_conv"] for c in lastc], 1).astype(f)
    s_shift = np.concatenate([R[c]["s_shift"] for c in range(ncores)], 1).astype(f)
    s_wkv = np.concatenate([R[c]["s_wkv"].reshape(2, NS, NH, 64, 64) for c in range(ncores)], 1).astype(f)
    s_k = np.concatenate([R[c]["s_k"].reshape(2, NS, 128, 2, 64) for c in range(ncores)], 1).astype(f)
    s_v = np.concatenate([R[c]["s_v"].reshape(2, NS, 128, 2, 64) for c in range(ncores)], 1).astype(f)
    s_conv = np.concatenate([R[c]["s_conv"].reshape(2, NS, 2, DFF) for c in range(ncores)], 1).astype(f)
    return (y_prompt, y_sample, p_shift, p_wkv, p_k, p_v, p_conv, s_shift, s_wkv, s_k, s_v, s_conv)
```

```python
import math
from contextlib import ExitStack

import numpy as np
import concourse.bass as bass
import concourse.mybir as mybir
from concourse.bass_utils import run_bass_kernel_spmd

F32 = mybir.dt.float32
BF = mybir.dt.bfloat16
AF = mybir.ActivationFunctionType
ALU = mybir.AluOpType
AX = mybir.AxisListType

ENGS = ["sp", "pe", "act", "dve", "pool"]
DEBUG_WHERE = True

D = 1024
HD = 64
NH = 8
RD = 512
RP = 1792
INP = 4608
DFF = 2816
NFC = 22
NS = 16
MS = 64
PAST = 16384
CDEC = -math.exp(-0.5)
NEG = -30000.0


class FW:
    def __init__(self, nc, es):
        self.nc = nc
        self.es = es
        self.ops = {e: [] for e in ENGS}
        self.lastw = {}
        self.readers = {}
        self.dma_count = {}
        self.inc = {}

    def sb(self, name, shape, dt=F32):
        return self.es.enter_context(self.nc.sbuf_tensor(name, list(shape), dt))

    def ps(self, name, shape, dt=F32):
        return self.es.enter_context(self.nc.psum_tensor(name, list(shape), dt))

    def capture(self, f):
        self.cap = []
        f()
        log, self.cap = self.cap, None
        return log

    def replay(self, logs, chunk=2):
        logs = [list(lg) for lg in logs if lg]
        if not logs:
            return
        mn = min(len(lg) for lg in logs)
        per = [max(1, int(round(chunk * len(lg) / mn))) for lg in logs]
        pos = [0] * len(logs)
        while any(p < len(lg) for p, lg in zip(pos, logs)):
            for k, lg in enumerate(logs):
                for _ in range(per[k]):
                    if pos[k] < len(lg):
                        self.op(*lg[pos[k]])
                        pos[k] += 1

    def op(self, eng, fn, r=(), w=(), dma=None):
        if getattr(self, "cap", None) is not None:
            self.cap.append((eng, fn, tuple(r), tuple(w), dma))
            return
        ops = self.ops[eng]
        idx = len(ops)
        deps = set()
        pr = [k for k in r if isinstance(k, str) and k[:2] in ("ps", "pb") and k[2:].isdigit()]
        if pr:
            r = [k for k in r if k not in pr]
            w = list(w) + pr
        for k in r:
            t = self.lastw.get(k)
            if t is not None:
                deps.add(t)
        for k in w:
            t = self.lastw.get(k)
            if t is not None:
                deps.add(t)
            for t2 in self.readers.get(k, {}).values():
                deps.add(t2)
        if dma is not None:
            c = self.dma_count.get(dma, 0) + 1
            self.dma_count[dma] = c
            tok = ("d", dma, c)
        else:
            tok = ("c", eng, idx)
        if eng == "pe":
            deps = {d for d in deps if not (d[0] == "c" and d[1] == "pe")}
        deps.discard(tok)
        rec = dict(fn=fn, deps=deps, tok=tok, signal=False)
        if DEBUG_WHERE:
            import sys as _s
            f_ = _s._getframe(1)
            wh = []
            while f_ is not None and len(wh) < 4:
                wh.append(f_.f_lineno)
                f_ = f_.f_back
            rec["where"] = wh
        ops.append(rec)
        for d in deps:
            if d[0] == "c":
                self.ops[d[1]][d[2]]["signal"] = True
        for k in w:
            self.lastw[k] = tok
            self.readers[k] = {}
        for k in r:
            rk = ("d", tok[1]) if tok[0] == "d" else tok[1]
            self.readers.setdefault(k, {})[rk] = tok
        return tok

    def fence(self):
        toks = set()
        for e in ENGS:
            for rec in reversed(self.ops[e]):
                if rec["tok"][0] == "c" and rec["fn"] is not None:
                    toks.add(rec["tok"])
                    rec["signal"] = True
                    break
        for k, c in self.dma_count.items():
            toks.add(("d", k, c))
        for e in ENGS:
            self.ops[e].append(dict(fn=None, deps=set(toks), tok=("c", e, len(self.ops[e])), signal=False))

    def dma(self, out, in_, r=(), w=(), key=None, eng="sp", **kw):
        self.op(eng, lambda e: e.dma_start(out=out, in_=in_, **kw), r=r, w=w, dma=key)

    def mm(self, out, lhsT, rhs, start, stop, r=(), w=()):
        self.op("pe", lambda e: e.matmul(out, lhsT, rhs, start=start, stop=stop), r=r, w=w)

    def tr(self, out, in_, ident, r=(), w=()):
        self.op("pe", lambda e: e.transpose(out, in_, ident), r=r, w=w)

    def act(self, out, in_, func, r=(), w=(), **kw):
        self.op("act", lambda e: e.activation(out, in_, func, **kw), r=r, w=w)

    def emit(self):
        nc = self.nc
        sems = {e: self.es.enter_context(nc.semaphore("s_" + e)) for e in ENGS}
        dsems = {}
        for i, k in enumerate(self.dma_count):
            dsems[k] = self.es.enter_context(nc.semaphore("d%d" % i))
        for e in ENGS:
            c = 0
            for rec in self.ops[e]:
                if rec["signal"] and rec["tok"][0] == "c":
                    c += 1
                rec["sigval"] = c
        final_counts = dict(self.dma_count)

        def run(engname, eng):
            waited = {}
            for rec in self.ops[engname]:
                need = {}
                for d in rec["deps"]:
                    if d[0] == "c":
                        s = ("c", d[1])
                        v = self.ops[d[1]][d[2]]["sigval"]
                    else:
                        s = ("d", d[1])
                        v = self.inc.get(d[1], 16) * d[2]
                    if need.get(s, 0) < v:
                        need[s] = v
                for s, v in need.items():
                    if waited.get(s, 0) >= v:
                        continue
                    waited[s] = v
                    eng.wait_ge(sems[s[1]] if s[0] == "c" else dsems[s[1]], v)
                if rec["fn"] is None:
                    continue
                try:
                    ins = rec["fn"](eng)
                except Exception:
                    print("EMIT FAILURE at lines", rec.get("where"), "engine", engname)
                    raise
                if rec["tok"][0] == "d":
                    ins.then_inc(dsems[rec["tok"][1]], self.inc.get(rec["tok"][1], 16))
                elif rec["signal"]:
                    ins.then_inc(sems[engname], 1)
            if engname == "sp":
                for k, c in final_counts.items():
                    v = self.inc.get(k, 16) * c
                    if waited.get(("d", k), 0) < v:
                        eng.wait_ge(dsems[k], v)

        with nc.Block() as block:
            @block.sync
            def _(e):
                run("sp", e)

            @block.tensor
            def _(e):
                run("pe", e)

            @block.scalar
            def _(e):
                run("act", e)

            @block.vector
            def _(e):
                run("dve", e)

            @block.gpsimd
            def _(e):
                run("pool", e)


def bc3(ap2, n):
    s = list(ap2.shape)
    return ap2.unsqueeze(2).to_broadcast([s[0], s[1], n])


def h3(ap2, h=NH):
    return ap2.rearrange("p (h d) -> p h d", h=h)


class Builder:
    def __init__(self, TP, taps=False):
        self.TP = TP
        self.NT = TP // 128
        self.taps = taps
        self.nc = bass.Bass("TRN2", target_bir_lowering=False)
        self.I = {}
        self.O = {}
        self.psi = 0
        self.pbi = 0
        self.tapnames = []
        self.pool = None
        self.pcnt = {}

    def din(self, n, s):
        self.I[n] = self.nc.dram_tensor(n, list(s), F32, kind="ExternalInput").ap()

    def dout(self, n, s):
        self.O[n] = self.nc.dram_tensor(n, list(s), F32, kind="ExternalOutput").ap()

    def declare(self):
        TP = self.TP
        for n, s in [("xp", (TP, D)), ("xs", (MS, D)), ("st_shift", (2, NS, RP)), ("st_wkv", (2, 128, 4096)),
                     ("ck", (2, NS, 128, 128)), ("cv", (2, NS, 128, 128)), ("st_conv", (2, 2 * NS, DFF)),
                     ("norm_mix_g", (2, D)), ("w_in", (2, D, INP)), ("rwkv_mu", (2, RP)), ("rwkv_w0", (2, RD)),
                     ("rwkv_w2", (2, 64, RD)), ("rwkv_a0", (2, RD)), ("rwkv_a2", (2, 64, RD)),
                     ("rwkv_g2", (2, 128, RD)), ("rwkv_k_k", (2, RD)), ("rwkv_k_a", (2, RD)),
                     ("rwkv_r_k", (2, RD)), ("rwkv_ln_g", (2, RD)), ("rwkv_ln_b", (2, RD)),
                     ("attn_sinks", (2, NH)), ("w_br_rwkv", (2, RD, D)), ("w_br_attn", (2, RD, D)),
                     ("w_out", (2, D, D)), ("norm_ffn_g", (2, D)), ("ffn_w_in", (2, D, 2 * DFF)),
                     ("ffn_conv_w", (2, 3, DFF)), ("ffn_conv_b", (2, DFF)), ("ffn_w_down", (2, DFF, D)),
                     ("norm_final_g", (D,)),
                     ("c_ident", (128, 128)), ("c_cosp", (TP, 32)), ("c_sinp", (TP, 32)),
                     ("c_coss", (MS, 32)), ("c_sins", (MS, 32)), ("c_tri", (128, 256)),
                     ("c_mask2", (128, 256)), ("c_maskL", (128, 128)), ("c_amask", (128, 768)),
                     ("c_smask", (32, 132)), ("c_last", (128, 1)),
                     ("xh0", (128, D)), ("c_cosh", (128, 32)), ("c_sinh", (128, 32)), ("c_amask0", (128, 256)), ("c_sel", (128, 8))]:
            self.din(n, s)
        for n, s in [("yp", (TP, D)), ("ys", (MS, D)), ("p_shift", (2, RP)), ("p_wkv", (2, NH, 64, 64)),
                     ("p_k", (2, 128, 128)), ("p_v", (2, 128, 128)), ("p_conv", (2, 2, DFF)),
                     ("s_shift", (2, NS, RP)), ("s_wkv", (2, 128, 4096)), ("s_k", (2, NS, 128, 128)),
                     ("s_v", (2, NS, 128, 128)), ("s_conv", (2, 2 * NS, DFF))]:
            self.dout(n, s)
        nc = self.nc
        self.xbuf = nc.dram_tensor("xbuf", [TP, D], F32).ap()
        self.xsbuf = nc.dram_tensor("xsbuf", [MS, D], F32).ap()
        self.mrbuf = nc.dram_tensor("mrbuf", [self.NT + 1, 128, 1024], BF).ap()
        self.xh_dram = nc.dram_tensor("xh_dram", [128, D], F32).ap()
        self.sq = nc.dram_tensor("sq", [6, MS, RD], F32).ap()
        self.sy = nc.dram_tensor("sy", [MS, RD], F32).ap()

    def alloc(self, name, shape, dt=F32):
        shape = list(shape)
        n = 1
        for d_ in shape[1:]:
            n *= d_
        nbytes = n * (4 if dt == F32 else 2)
        nw = (nbytes + 31) // 32 * 8
        off = self.aoff
        self.aoff += nw
        self.apeak = max(self.apeak, self.aoff)
        assert self.aoff <= self.ASZ, "SBUF arena overflow: %s needs %d words (limit %d)" % (name, self.aoff, self.ASZ)
        ap = self.arena[0:shape[0], off:off + nw]
        if dt != F32:
            ap = ap.bitcast(dt)
        ap = ap[:, 0:n]
        if len(shape) > 2:
            names = ["d%d" % i for i in range(len(shape) - 1)]
            pat = "p (%s) -> p %s" % (" ".join(names), " ".join(names))
            ap = ap.rearrange(pat, **{names[i]: shape[i + 1] for i in range(len(names))})
        return ap

    def release(self, mark):
        self.fw.fence()
        self.aoff = mark

    def pf(self):
        ids = {None: [0, 1, 2, 3, 4, 5], 0: [0, 1, 2], 1: [3, 4, 5]}[self.pool]
        c = self.pcnt.setdefault(("f", self.pool), 0)
        self.pcnt[("f", self.pool)] = c + 1
        k = ids[c % len(ids)]
        return self.PS[k], "ps%d" % k

    def pb(self):
        ids = {None: [0, 1], 0: [0], 1: [1]}[self.pool]
        c = self.pcnt.setdefault(("b", self.pool), 0)
        self.pcnt[("b", self.pool)] = c + 1
        k = ids[c % len(ids)]
        return self.PBK[k], "pb%d" % k

    def tap(self, name, ap, rkeys, dt=F32):
        if not self.taps:
            return
        shp = list(ap.shape)
        t = self.nc.dram_tensor("tap_" + name, shp, dt, kind="ExternalOutput").ap()
        self.tapnames.append("tap_" + name)
        self.fw.dma(t, ap, r=rkeys, key="tap_" + name)

    def V(self, fn, r=(), w=()):
        self.fw.op("dve", fn, r, w)

    def P(self, fn, r=(), w=()):
        self.fw.op("pool", fn, r, w)

    def col_load(self, dst, dkey, vec, n):
        fw = self.fw
        st = self.cstage
        fw.dma(st[0:n, :], vec.rearrange("(c p) -> c p", p=128), w=["cstage"], key="cstage")
        ps, pk = self.pf()
        fw.tr(ps[:, 0:n], st[0:n, :], self.identf[0:n, 0:n], r=["cstage", "identf"], w=[pk])
        fw.act(dst, ps[:, 0:n], AF.Copy, r=[pk], w=[dkey])

    def gather_select(self, src_ap, src_keys, n, ag_in, ag_out, name):
        fw = self.fw
        fw.dma(ag_in, src_ap, r=src_keys, w=[name + "_in"], key=name + "_st")
        self.gi = getattr(self, "gi", 0)
        ck = name + "_cc"
        fw.inc[ck] = 1
        fw.op("pool", lambda e: e.collective_compute("AllGather", ALU.bypass, replica_groups=[list(range(8))], ins=[ag_in], outs=[ag_out]),
              r=[name + "_in"], w=[name + "_out"], dma=ck)
        for r_ in range(8):
            st, sk = self.xt[r_ % 2], "xt%d" % (r_ % 2)
            fw.dma(st[:, 0:n], ag_out[r_ * 128:(r_ + 1) * 128, :], r=[name + "_out"], w=[sk], key=sk)
            if r_ == 0:
                self.V(lambda e, st=st: e.tensor_scalar(src_ap, st[:, 0:n], self.sel[:, 0:1], None, ALU.mult), r=[sk, "sel"], w=src_keys)
            else:
                self.V(lambda e, st=st, r_=r_: e.scalar_tensor_tensor(src_ap, st[:, 0:n], self.sel[:, r_:r_ + 1], src_ap, ALU.mult, ALU.add),
                       r=[sk, "sel"] + list(src_keys), w=src_keys)

    def bcast_load(self, dst, dkey, vec):
        self.fw.dma(dst, vec.partition_broadcast(dst.shape[0]), w=[dkey], key=dkey)

    def prep_w(self, nchunks, ncols, src, dst, dkey, mode, scale=None, mul=None, mulkey=None, sview=None):
        fw = self.fw
        for c in range(nchunks):
            for s0 in range(0, ncols, 2048):
                n = min(2048, ncols - s0)
                k = self.wst_i % 2
                self.wst_i += 1
                st = self.wstage[k]
                sk = "wst%d" % k
                fw.dma(st[:, 0:n], src(c, s0, n), w=[sk], key=sk)
                o = dst(c, s0, n)
                dk = dkey(c)
                if sview is not None:
                    sv_ = sview(st[:, 0:n])
                    sc = scale(c)
                    self.V(lambda eg, o=o, sv_=sv_, sc=sc: eg.tensor_scalar(o, sv_, sc, None, ALU.mult), r=[sk, "gcol"], w=[dk])
                    continue
                if mode == "plain":
                    e = ["dve", "pool", "act"][self.wst_i % 3]
                    if e == "act":
                        fw.act(o, st[:, 0:n], AF.Copy, r=[sk], w=[dk])
                    else:
                        fw.op(e, lambda eg, o=o, st=st, n=n: eg.tensor_copy(o, st[:, 0:n]), r=[sk], w=[dk])
                elif mode == "col":
                    sc = scale(c)
                    e = ["dve", "pool"][self.wst_i % 2]
                    fw.op(e, lambda eg, o=o, st=st, n=n, sc=sc: eg.tensor_scalar(o, st[:, 0:n], sc, None, ALU.mult),
                          r=[sk, "gcol"], w=[dk])
                else:
                    sc = scale(c)
                    m = mul(s0, n)
                    self.V(lambda eg, o=o, st=st, n=n, sc=sc, m=m: eg.scalar_tensor_tensor(
                        o, st[:, 0:n], sc, m, ALU.mult, ALU.mult), r=[sk, "gcol", mulkey], w=[dk])

    def norm_hT(self, xt, xk, M, hdst, hkey, identb):
        self.norm_a(xt, xk, M)
        self.norm_b(M, hdst, hkey, identb)

    def norm_a(self, xt, xk, M):
        fw = self.fw
        xn, ss, t1 = self.xn, self.ss, self.t1
        fw.act(xn[0:M, :], xt[0:M, :], AF.Square, r=[xk], w=["xn", "ss"], accum_out=ss[0:M, :])
        self.V(lambda e: e.tensor_scalar(t1[0:M, :], ss[0:M, :], 1.0 / D, 1e-6, ALU.mult, ALU.add), r=["ss"], w=["t1"])
        fw.act(t1[0:M, :], t1[0:M, :], AF.Sqrt, r=["t1"], w=["t1"])
        self.V(lambda e: e.reciprocal(t1[0:M, :], t1[0:M, :]), r=["t1"], w=["t1"])
        self.V(lambda e: e.tensor_scalar(xn[0:M, :], xt[0:M, :], t1[0:M, 0:1], None, ALU.mult), r=[xk, "t1"], w=["xn"])

    def norm_b(self, M, hdst, hkey, identb):
        fw = self.fw
        xn = self.xn
        pbk, pk = self.pb()
        for c in range(8):
            fw.tr(pbk[:, c * M:(c + 1) * M], xn[0:M, c * 128:(c + 1) * 128], identb[0:M, 0:M], r=["xn", "identb"], w=[pk])
        fw.act(hdst, pbk[:, 0:8 * M].rearrange("p (c t) -> p c t", c=8), AF.Copy, r=[pk], w=[hkey])

    def build(self):
        self.declare()
        nc = self.nc
        with ExitStack() as es:
            self.fw = fw = FW(nc, es)
            self.PS = [fw.ps("ps%d" % i, [128, 512], F32) for i in range(6)]
            self.PBK = [fw.ps("pb%d" % i, [128, 1024], BF) for i in range(2)]
            self.ASZ = 52224
            self.arena = fw.sb("arena", [128, self.ASZ])
            self.aoff = 0
            self.apeak = 0
            self.identf = self.alloc("identf", [128, 128])
            self.identb = self.alloc("identb", [128, 128], BF)
            self.cstage = self.alloc("cstage", [32, 128])
            self.wst_i = 0
            self.xn = self.alloc("xn", [128, D], BF)
            self.ss = self.alloc("ss", [128, 1])
            self.t1 = self.alloc("t1", [128, 1])
            self.gcol = self.alloc("gcol", [128, 8])
            self.xt = [self.alloc("xt%d" % i, [128, D]) for i in range(2)]
            self.sel = self.alloc("sel", [128, 8])
            fw.dma(self.sel[:], self.I["c_sel"], w=["sel"], key="sel")
            fw.dma(self.identf[:], self.I["c_ident"], w=["identf"], key="identf")
            self.V(lambda e: e.tensor_copy(self.identb[:], self.identf[:]), r=["identf"], w=["identb"])
            for l in range(2):
                for p_ in (self.pass_rwkv, self.pass_attn, self.pass_ffn):
                    mk_ = self.aoff
                    p_(l, None)
                    self.release(mk_)
            print("arena peak words", self.apeak, "of", self.ASZ)
            fw.emit()
        return nc

    def sbl(self, es2, name, shape, dt=F32):
        return self.alloc(name, shape, dt)

    def xsrc(self, l, i):
        if i < self.NT:
            src = self.I["xp"] if l == 0 else self.xbuf
            return src[i * 128:(i + 1) * 128, :], ("xb", i)
        src = self.I["xs"] if l == 0 else self.xsbuf
        return src, ("xb", i)

    def pass_rwkv(self, l, es2):
        fw, I, O, NT = self.fw, self.I, self.O, self.NT
        sbl = lambda n, s, dt=F32: self.sbl(es2, "r%d_" % l + n, s, dt)
        identb, identf = self.identb, self.identf
        W1 = sbl("W1", [128, 8, RP], BF)
        W2 = sbl("W2", [128, 8, RP], BF)
        Wg = sbl("Wg", [128, 8, D], BF)
        Wr = sbl("Wr", [128, 4, D], BF)
        lw2 = sbl("lw2", [128, RD], BF)
        lg2 = sbl("lg2", [128, RD], BF)
        bcs = {}
        for n in ["rwkv_w0", "rwkv_a0", "rwkv_k_k", "rwkv_k_a", "rwkv_r_k", "rwkv_ln_g", "rwkv_ln_b"]:
            bcs[n] = sbl(n, [128, RD])
            self.bcast_load(bcs[n][:], n + "_bc", I[n][l])
        mucol = sbl("mucol", [128, 2])
        tri = sbl("tri", [128, 256])
        mask2 = sbl("mask2", [128, 256])
        maskL = sbl("maskL", [128, 128])
        clast = sbl("clast", [128, 1])
        fw.dma(tri[:], I["c_tri"], w=["tri"], key="tri")
        fw.dma(mask2[:], I["c_mask2"], w=["mask2"], key="mask2")
        fw.dma(maskL[:], I["c_maskL"], w=["maskL"], key="maskL")
        fw.dma(clast[:], I["c_last"], w=["clast"], key="clast")
        self.col_load(self.gcol[:], "gcol", I["norm_mix_g"][l], 8)
        self.col_load(mucol[:], "mucol", I["rwkv_mu"][l, 1536:1792], 2)
        m0 = self.aoff
        self.wstage = [sbl("wst%d" % i_, [128, 2048]) for i_ in range(2)]
        mu_bc = sbl("mu_bc", [128, RP])
        omm_bc = sbl("omm_bc", [128, RP])
        self.bcast_load(mu_bc[:], "mu_bc", I["rwkv_mu"][l])
        self.V(lambda e: e.tensor_scalar(omm_bc[:], mu_bc[:], -1.0, 1.0, ALU.mult, ALU.add), r=["mu_bc"], w=["omm_bc"])
        win = I["w_in"][l]
        gsc = lambda c: self.gcol[:, c:c + 1]
        self.prep_w(8, RP, lambda c, s0, n: win[c * 128:(c + 1) * 128, s0:s0 + n],
                    lambda c, s0, n: W1[:, c, s0:s0 + n], lambda c: "W1_%d" % c, "colmul", gsc,
                    lambda s0, n: omm_bc[:, s0:s0 + n], "omm_bc")
        self.prep_w(8, RP, lambda c, s0, n: win[c * 128:(c + 1) * 128, s0:s0 + n],
                    lambda c, s0, n: W2[:, c, s0:s0 + n], lambda c: "W2_%d" % c, "colmul", gsc,
                    lambda s0, n: mu_bc[:, s0:s0 + n], "mu_bc")
        self.prep_w(8, D, lambda c, s0, n: win[c * 128:(c + 1) * 128, 2560 + s0:2560 + s0 + n],
                    lambda c, s0, n: Wg[:, c, s0:s0 + n], lambda c: "Wg_%d" % c, "col", gsc)
        wbr = I["w_br_rwkv"][l]
        self.prep_w(4, D, lambda c, s0, n: wbr[c * 128:(c + 1) * 128, s0:s0 + n],
                    lambda c, s0, n: Wr[:, c, s0:s0 + n], lambda c: "Wr_%d" % c, "plain")
        for (nm, p0, dk_) in [("rwkv_w2", 0, "lw2a"), ("rwkv_a2", 64, "lw2b")]:
            k = self.wst_i % 2
            self.wst_i += 1
            wsk = self.wstage[k]
            fw.dma(wsk[p0:p0 + 64, 0:RD], I[nm][l], w=["wst%d" % k], key="wst%d" % k)
            self.P(lambda e, wsk=wsk, p0=p0: e.tensor_copy(lw2[p0:p0 + 64, :], wsk[p0:p0 + 64, 0:RD]), r=["wst%d" % k], w=[dk_])
        self.prep_w(1, RD, lambda c, s0, n: I["rwkv_g2"][l], lambda c, s0, n: lg2[:, :], lambda c: "lg2", "plain")
        WK1 = ["W1_%d" % c for c in range(8)]
        WK2 = ["W2_%d" % c for c in range(8)]
        self.release(m0)
        class NSP:
            pass
        zr, zk = sbl("zr", [128, RD]), sbl("zk", [128, RD])
        lact = sbl("lact", [128, 128], BF)
        T = [sbl("tmp%d" % i_, [128, RD]) for i_ in range(8)]
        sm = sbl("sm", [128, 64])
        orT = sbl("orT", [128, 4, 128], BF)
        sgr = sbl("sgr", [128, 8, 128], BF)
        mrT0_ = sbl("mrT0", [128, 8, 128], BF)
        mrT = [mrT0_, mrT0_]
        TP_ = [sbl("tpost%d" % i_, [128, RD]) for i_ in range(2)]
        m1 = self.aoff
        NRB = 9864

        def mkrec(k):
            R = NSP()
            rb = sbl("RB%d" % k, [128, NRB], BF)
            rf = sbl("RF%d" % k, [128, 528])
            R.rb, R.rf, R.k = rb, rf, k
            R.RKT = rb[:, 0:1024].rearrange("p (j a t) -> p j a t", j=4, a=2)
            R.G4 = [rb[:, 1024 + j * 1280:1024 + (j + 1) * 1280].rearrange("p (h c) -> p h c", h=2) for j in range(4)]
            R.ZF = [rb[:, 6144 + j * 256:6144 + (j + 1) * 256].rearrange("p (h c) -> p h c", h=2) for j in range(4)]
            R.vb, R.ktt, R.bnt = rb[:, 7168:7680], rb[:, 7680:8192], rb[:, 8192:8704]
            R.sgT = rb[:, 8704:8832]
            R.hT = rb[:, 8832:9864].rearrange("p (c t) -> p c t", c=8)
            R.zv, R.WC, R.bon = rf[:, 0:512], rf[:, 512:516], rf[:, 516:524]
            R.K = (lambda k_: (lambda n: "%s#%d" % (n, k_)))(k)
            return R
        R0 = mkrec(0)
        U0b = [sbl("U0b%d" % j, [128, 2, 64], BF) for j in range(4)]
        Ub = sbl("Ub", [128, RD], BF)
        Nst = sbl("Nst", [128, 4, 128])
        Nb = sbl("Nb", [128, 4, 128], BF)
        self.V(lambda e: e.memset(Nst[:], 0.0), w=["Nst"])
        self.V(lambda e: e.memset(Nb[:], 0.0), w=["Nb"])
        m2 = self.aoff
        rt, kat = sbl("rt", [128, RD], BF), sbl("kat", [128, RD], BF)
        KT = sbl("KT", [128, 4, 128], BF)
        BT = sbl("BT", [128, 4, 128], BF)
        for j in range(4):
            self.P(lambda e, j=j: e.tensor_copy(R0.G4[j][:, :, 512:640], identb[:, :].unsqueeze(1).to_broadcast([128, 2, 128])),
                   r=["identb"], w=["G4_%d" % j])
        EZ = [[sbl("EZ%d_%d" % (j, a), [128, 2, 2, 128], BF) for a in range(2)] for j in range(4)]
        FFa = [sbl("FFa%d" % a, [128, 4, 2, 128], BF) for a in range(2)]
        FF = [[FFa[a][:, j] for a in range(2)] for j in range(4)]

        def tok_proj(M, hcur, hprev, hk, g0, dstkey):
            ps, pk = self.pf()
            n = 0
            for c in range(8):
                fw.mm(ps[0:M, :], hcur(c), W1[:, c, g0:g0 + 512], n == 0, False, r=[hk, WK1[c]], w=[pk])
                n += 1
            for c in range(8):
                fw.mm(ps[0:M, :], hprev(c), W2[:, c, g0:g0 + 512], False, c == 7, r=[hk, WK2[c]], w=[pk])
            return ps, pk

        def feat_proj(M, hcur, hprev, hk, g0):
            ps, pk = self.pf()
            for c in range(8):
                fw.mm(ps[:, 0:M], W1[:, c, g0:g0 + 128], hcur(c), c == 0, False, r=[hk, WK1[c]], w=[pk])
            for c in range(8):
                fw.mm(ps[:, 0:M], W2[:, c, g0:g0 + 128], hprev(c), False, c == 7, r=[hk, WK2[c]], w=[pk])
            return ps, pk

        def raw_last(hl, hk, M, dst):
            for gi, g0 in enumerate(range(0, RP, 512)):
                n = min(512, RP - g0)
                ps, pk = self.pf()
                for c in range(8):
                    fw.mm(ps[0:M, 0:n], hl(c), W1[:, c, g0:g0 + n], c == 0, False, r=[hk, WK1[c]], w=[pk])
                for c in range(8):
                    fw.mm(ps[0:M, 0:n], hl(c), W2[:, c, g0:g0 + n], False, c == 7, r=[hk, WK2[c]], w=[pk])
                fw.act(T[gi][0:M, 0:n], ps[0:M, 0:n], AF.Copy, r=[pk], w=["T%d" % gi])
                fw.dma(dst[:, g0:g0 + n], T[gi][0:M, 0:n], r=["T%d" % gi], key="zl%d" % gi)

        def prep(M, sample, R):
            K = R.K
            w0, a0 = bcs["rwkv_w0"], bcs["rwkv_a0"]
            kkb, kab, rkb = bcs["rwkv_k_k"], bcs["rwkv_k_a"], bcs["rwkv_r_k"]
            pw, pwk = self.pf()
            fw.mm(pw[0:M, :], lact[0:64, 0:M], lw2[0:64, :], True, True, r=["lact", "lw2a"], w=[pwk])
            pa, pak = self.pf()
            fw.mm(pa[0:M, :], lact[64:128, 0:M], lw2[64:128, :], True, True, r=["lact", "lw2b"], w=[pak])
            sg, a_, kk, t3, kf, be = T[0], T[1], T[2], T[3], T[4], T[5]
            self.V(lambda e: e.tensor_tensor(sg[0:M, :], pw[0:M, :], w0[0:M, :], ALU.add), r=[pwk, "rwkv_w0_bc"], w=["T0"])
            fw.act(sg[0:M, :], sg[0:M, :], AF.Sigmoid, r=["T0"], w=["T0"])
            self.V(lambda e: e.tensor_tensor(a_[0:M, :], pa[0:M, :], a0[0:M, :], ALU.add), r=[pak, "rwkv_a0_bc"], w=["T1"])
            fw.act(a_[0:M, :], a_[0:M, :], AF.Sigmoid, r=["T1"], w=["T1"])
            self.P(lambda e: e.tensor_tensor(kk[0:M, :], zk[0:M, :], kkb[0:M, :], ALU.mult), r=["zk", "rwkv_k_k_bc"], w=["T2"])
            self.P(lambda e: e.tensor_tensor(t3[0:M, :], kk[0:M, :], kk[0:M, :], ALU.mult), r=["T2"], w=["T3"])
            self.V(lambda e: e.tensor_reduce(sm[0:M, 0:8], h3(t3[0:M, :]), AX.X, ALU.add), r=["T3"], w=["sm0"])
            fw.act(sm[0:M, 0:8], sm[0:M, 0:8], AF.Sqrt, r=["sm0"], w=["sm0"])
            self.V(lambda e: e.tensor_scalar(sm[0:M, 0:8], sm[0:M, 0:8], 1e-12, None, ALU.max), r=["sm0"], w=["sm0"])
            self.V(lambda e: e.reciprocal(sm[0:M, 0:8], sm[0:M, 0:8]), r=["sm0"], w=["sm0"])
            self.V(lambda e: e.tensor_tensor(h3(kk[0:M, :]), h3(kk[0:M, :]), bc3(sm[0:M, 0:8], 64), ALU.mult),
                   r=["T2", "sm0"], w=["T2"])
            self.V(lambda e: e.scalar_tensor_tensor(t3[0:M, :], a_[0:M, :], -1.0, kab[0:M, :], ALU.add, ALU.mult),
                   r=["T1", "rwkv_k_a_bc"], w=["T3"])
            self.V(lambda e: e.scalar_tensor_tensor(kf[0:M, :], t3[0:M, :], 1.0, zk[0:M, :], ALU.add, ALU.mult),
                   r=["T3", "zk"], w=["T4"])
            self.P(lambda e: e.tensor_tensor(be[0:M, :], kk[0:M, :], a_[0:M, :], ALU.mult), r=["T2", "T1"], w=["T5"])
            self.P(lambda e: e.tensor_tensor(t3[0:M, :], zr[0:M, :], kf[0:M, :], ALU.mult), r=["zr", "T4"], w=["T3"])
            self.P(lambda e: e.tensor_tensor(t3[0:M, :], t3[0:M, :], rkb[0:M, :], ALU.mult), r=["T3", "rwkv_r_k_bc"], w=["T3"])
            self.V(lambda e, R=R: e.tensor_reduce(R.bon[0:M, :], h3(t3[0:M, :]), AX.X, ALU.add), r=["T3"], w=[K("bon")])
            if sample:
                fw.act(T[6][0:M, :], sg[0:M, :], AF.Exp, r=["T0"], w=["T6"], scale=CDEC)
                for x, (tl, tk) in enumerate([(zr, "zr"), (T[6], "T6"), (kf, "T4"), (R.zv, K("zv")), (kk, "T2"), (be, "T5")]):
                    fw.dma(self.sq[x], tl[0:M, :], r=[tk], w=[("sq", x)], key="sqw%d" % x)
                return
            pli, plik = self.pf()
            fw.mm(pli[:, :], tri[:, 0:128], sg[:, :], True, True, r=["tri", "T0"], w=[plik])
            ple, plek = self.pf()
            fw.mm(ple[:, :], tri[:, 128:256], sg[:, :], True, True, r=["tri", "T0"], w=[plek])
            eL, eLm, enL = T[6], T[7], T[3]
            fw.act(eL[:, :], pli[:, :], AF.Exp, r=[plik], w=["T6"])
            fw.act(eLm[:, :], ple[:, :], AF.Exp, r=[plek], w=["T7"])
            fw.act(enL[:, :], pli[:, :], AF.Exp, r=[plik], w=["T3"], scale=-1.0)
            self.V(lambda e: e.tensor_tensor(rt[:, :], zr[:, :], eL[:, :], ALU.mult), r=["zr", "T6"], w=["rt"])
            self.V(lambda e: e.tensor_tensor(kat[:, :], kk[:, :], eLm[:, :], ALU.mult), r=["T2", "T7"], w=["kat"])
            self.P(lambda e, R=R: e.tensor_tensor(R.ktt[:, :], kf[:, :], enL[:, :], ALU.mult), r=["T4", "T3"], w=[K("ktt")])
            self.V(lambda e, R=R: e.scalar_tensor_tensor(R.bnt[:, :], be[:, :], -1.0, enL[:, :], ALU.mult, ALU.mult),
                   r=["T5", "T3"], w=[K("bnt")])
            fw.act(R.vb[:, :], R.zv[:, :], AF.Copy, r=[K("zv")], w=[K("vb")])
            pwc, pwck = self.pf()
            for j in range(4):
                fw.mm(pwc[:, j:j + 1], eL[:, j * 128:(j + 1) * 128], clast[:, :], True, True, r=["T6", "clast"], w=[pwck])
            fw.act(R.WC[:, :], pwc[:, 0:4], AF.Copy, r=[pwck], w=[K("WC")])
            for (src, skey, dstf, dk) in [(rt, "rt", None, "RKT"), (kat, "kat", None, "RKT"),
                                          (R.ktt, K("ktt"), None, "KT"), (R.bnt, K("bnt"), None, "BT")]:
                pbk, pk = self.pb()
                for j in range(4):
                    fw.tr(pbk[:, j * 128:(j + 1) * 128], src[:, j * 128:(j + 1) * 128], identb[:, :], r=[skey, "identb"], w=[pk])
                if dk == "RKT":
                    which = 0 if skey == "rt" else 1
                    fw.act(R.RKT[:, :, which, :], pbk[:, 0:512].rearrange("p (j t) -> p j t", j=4), AF.Copy, r=[pk], w=["RKT%d" % which])
                else:
                    dst = KT if dk == "KT" else BT
                    self.V(lambda e, dst=dst, pbk=pbk: e.tensor_copy(dst[:, :, :], pbk[:, 0:512].rearrange("p (j t) -> p j t", j=4)),
                           r=[pk], w=[dk])

        def stageAB(R):
            K = R.K
            RK = [K("RKT0"), K("RKT1")]
            RKT, G4, ZF = R.RKT, R.G4, R.ZF
            zb = [self.pf(), self.pf()]
            for j in range(4):
                for hh in range(2):
                    o = hh * 64
                    pZ, pzk = zb[hh]
                    fw.mm(pZ[:, j * 128:(j + 1) * 128], RKT[o:o + 64, j, 1, :], BT[o:o + 64, j, :], True, True, r=["BT", K("RKT1")], w=[pzk])
            mlb = maskL[:, :].unsqueeze(1).to_broadcast([128, 4, 128])
            for hh in range(2):
                pZ, pzk = zb[hh]
                self.V(lambda e, pZ=pZ, hh=hh: e.tensor_tensor(FFa[0][:, :, hh, :], pZ[:, :].rearrange("p (j c) -> p j c", j=4), mlb, ALU.mult),
                       r=[pzk, "maskL"], w=["FF%d_0" % j for j in range(4)])
            for j in range(4):
                bk = [self.pf(), self.pf()]
                for hh in range(2):
                    o = hh * 64
                    ps, pk = bk[hh]
                    rhs = RKT[o:o + 64, j, :, :].rearrange("p a t -> p (a t)")
                    fw.mm(ps[:, 0:256], KT[o:o + 64, j, :], rhs, True, True, r=["KT"] + RK, w=[pk])
                    fw.mm(ps[:, 256:512], BT[o:o + 64, j, :], rhs, True, True, r=["BT"] + RK, w=[pk])
                for hh in range(2):
                    ps, pk = bk[hh]
                    self.V(lambda e, j=j, hh=hh, ps=ps, G4=G4: e.tensor_tensor(
                        G4[j][:, hh, 0:512].rearrange("p (a c) -> p a c", a=2), ps[:, :].rearrange("p (a c) -> p a c", a=2),
                        mask2[:, :].unsqueeze(1).to_broadcast([128, 2, 256]), ALU.mult), r=[pk, "mask2"], w=[K("G4_%d" % j)])
            for lev in range(7):
                a, b = lev % 2, (lev + 1) % 2
                for j in range(4):
                    fk, fn_ = "FF%d_%d" % (j, a), "FF%d_%d" % (j, b)
                    ezn = "EZ%d_%d" % (j, b)
                    if lev == 0:
                        ezk = K("G4_%d" % j)
                        EZs = lambda hh, j=j, G4=G4: G4[j][:, hh, 384:640]
                        Es = lambda hh, j=j, G4=G4: G4[j][:, hh, 384:512]
                        Zs = lambda j=j, G4=G4: G4[j][:, :, 512:640]
                    else:
                        ezk = "EZ%d_%d" % (j, a)
                        EZs = lambda hh, j=j, a=a: EZ[j][a][:, hh, :, :].rearrange("p a t -> p (a t)")
                        Es = lambda hh, j=j, a=a: EZ[j][a][:, hh, 0, :]
                        Zs = lambda j=j, a=a: EZ[j][a][:, :, 1, :]
                    if lev < 6:
                        pL, plk = self.pf()
                        for hh in range(2):
                            fw.mm(pL[:, hh * 256:(hh + 1) * 256], FF[j][a][:, hh, :], EZs(hh), True, True, r=[ezk, fk], w=[plk])
                        pF, pfk = self.pf()
                        for hh in range(2):
                            fw.mm(pF[:, hh * 128:(hh + 1) * 128], Es(hh), FF[j][a][:, hh, :], True, True, r=[ezk, fk], w=[pfk])
                        l3 = pL[:, :].rearrange("p (h c) -> p h c", h=2)
                        fw.act(EZ[j][b][:, :, 0, :], l3[:, :, 0:128], AF.Copy, r=[plk], w=[ezn])
                        self.V(lambda e, j=j, b=b, l3=l3, Zs=Zs: e.tensor_tensor(EZ[j][b][:, :, 1, :], l3[:, :, 128:256], Zs(), ALU.add),
                               r=[plk, ezk], w=[ezn])
                        fw.act(FF[j][b][:, :, :], pF[:, 0:256].rearrange("p (h c) -> p h c", h=2), AF.Copy, r=[pfk], w=[fn_])
                    else:
                        pL, plk = self.pf()
                        for hh in range(2):
                            fw.mm(pL[:, hh * 128:(hh + 1) * 128], FF[j][a][:, hh, :], EZ[j][a][:, hh, 1, :], True, True, r=[ezk, fk], w=[plk])
                        self.V(lambda e, j=j, a=a, pL=pL, ZF=ZF: e.tensor_tensor(ZF[j][:, :, :], pL[:, 0:256].rearrange("p (h c) -> p h c", h=2),
                                                                      EZ[j][a][:, :, 1, :], ALU.add), r=[plk, ezk], w=[K("ZF%d" % j)])

        def stageC(R):
            K = R.K
            RKT, G4, ZF, vb = R.RKT, R.G4, R.ZF, R.vb
            for j in range(4):
                pU, puk = self.pf()
                for hh in range(2):
                    o, h = hh * 64, 2 * j + hh
                    fw.mm(pU[:, hh * 64:(hh + 1) * 64], RKT[o:o + 64, j, 1, :], Nb[o:o + 64, j, o:o + 64], True, False, r=[K("RKT1"), "Nb"], w=[puk])
                    fw.mm(pU[:, hh * 64:(hh + 1) * 64], G4[j][:, hh, 128:256], vb[:, h * 64:(h + 1) * 64], False, True, r=[K("G4_%d" % j), K("vb")], w=[puk])
                fw.act(U0b[j][:, :, :], pU[:, 0:128].rearrange("p (h c) -> p h c", h=2), AF.Copy, r=[puk], w=["U0b%d" % j])
            for j in range(4):
                pU, puk = self.pf()
                for hh in range(2):
                    fw.mm(pU[:, hh * 64:(hh + 1) * 64], ZF[j][:, hh, :], U0b[j][:, hh, :], True, True, r=[K("ZF%d" % j), "U0b%d" % j], w=[puk])
                fw.act(Ub[:, j * 128:(j + 1) * 128], pU[:, 0:128], AF.Copy, r=[puk], w=["Ub%d" % j])

        def stageD(R):
            K = R.K
            RKT, G4, vb = R.RKT, R.G4, R.vb
            psY, pyk = self.pf()
            for j in range(4):
                for hh in range(2):
                    o, h = hh * 64, 2 * j + hh
                    fw.mm(psY[:, h * 64:(h + 1) * 64], RKT[o:o + 64, j, 0, :], Nb[o:o + 64, j, o:o + 64], True, False, r=[K("RKT0"), "Nb"], w=[pyk])
                    fw.mm(psY[:, h * 64:(h + 1) * 64], G4[j][:, hh, 0:128], vb[:, h * 64:(h + 1) * 64], False, False, r=[K("G4_%d" % j), K("vb")], w=[pyk])
                    fw.mm(psY[:, h * 64:(h + 1) * 64], G4[j][:, hh, 256:384], Ub[:, h * 64:(h + 1) * 64], False, True, r=[K("G4_%d" % j), "Ub%d" % j], w=[pyk])
            return psY, pyk

        def n_update(R):
            K = R.K
            ktt, bnt, vb, WC = R.ktt, R.bnt, R.vb, R.WC
            pN, pnk = self.pf()
            for j in range(4):
                fw.mm(pN[:, j * 128:(j + 1) * 128], ktt[:, j * 128:(j + 1) * 128], vb[:, j * 128:(j + 1) * 128], True, False, r=[K("ktt"), K("vb")], w=[pnk])
                fw.mm(pN[:, j * 128:(j + 1) * 128], bnt[:, j * 128:(j + 1) * 128], Ub[:, j * 128:(j + 1) * 128], False, True, r=[K("bnt"), "Ub%d" % j], w=[pnk])
            n2 = Nst[:, :, :].rearrange("p j c -> p (j c)")
            self.V(lambda e: e.tensor_tensor(n2, pN[:, :], n2, ALU.add), r=[pnk, "Nst"], w=["Nst"])
            self.V(lambda e, WC=WC: e.tensor_tensor(Nst[:, :, :], Nst[:, :, :], bc3(WC[:, :], 128), ALU.mult), r=["Nst", K("WC")], w=["Nst"])
            fw.act(Nb[:, :, :], Nst[:, :, :], AF.Copy, r=["Nst"], w=["Nb"])


        def post(M, yap, ykeys, pg, pgk, R):
            K = R.K
            lng, lnb = bcs["rwkv_ln_g"], bcs["rwkv_ln_b"]
            y2, yc = TP_[0], TP_[1]
            ob = TP_[0].bitcast(BF)[:, 0:RD]
            self.V(lambda e: e.tensor_reduce(sm[0:M, 16:24], h3(yap), AX.X, ALU.add), r=ykeys, w=["sm2"])
            fw.act(y2[0:M, :], yap, AF.Square, r=ykeys, w=["TP0"])
            self.V(lambda e: e.tensor_reduce(sm[0:M, 24:32], h3(y2[0:M, :]), AX.X, ALU.add), r=["TP0"], w=["sm3"])
            mean, var = sm[0:M, 16:24], sm[0:M, 24:32]
            self.V(lambda e: e.tensor_scalar(mean, mean, 1.0 / 64, None, ALU.mult), r=["sm2"], w=["sm2"])
            self.V(lambda e: e.tensor_tensor(sm[0:M, 32:40], mean, mean, ALU.mult), r=["sm2"], w=["sm4"])
            self.V(lambda e: e.scalar_tensor_tensor(var, var, 1.0 / 64, sm[0:M, 32:40], ALU.mult, ALU.subtract), r=["sm3", "sm4"], w=["sm3"])
            self.V(lambda e: e.tensor_scalar(var, var, 64e-5, None, ALU.add), r=["sm3"], w=["sm3"])
            fw.act(var, var, AF.Sqrt, r=["sm3"], w=["sm3"])
            self.V(lambda e: e.reciprocal(var, var), r=["sm3"], w=["sm3"])
            self.V(lambda e: e.tensor_tensor(h3(yc[0:M, :]), h3(yap), bc3(mean, 64), ALU.subtract), r=list(ykeys) + ["sm2"], w=["TP1"])
            self.V(lambda e: e.tensor_tensor(h3(yc[0:M, :]), h3(yc[0:M, :]), bc3(var, 64), ALU.mult), r=["TP1", "sm3"], w=["TP1"])
            self.P(lambda e: e.tensor_tensor(yc[0:M, :], yc[0:M, :], lng[0:M, :], ALU.mult), r=["TP1", "rwkv_ln_g_bc"], w=["TP1"])
            self.P(lambda e: e.tensor_tensor(yc[0:M, :], yc[0:M, :], lnb[0:M, :], ALU.add), r=["TP1", "rwkv_ln_b_bc"], w=["TP1"])
            self.P(lambda e, R=R: e.tensor_tensor(h3(y2[0:M, :]), h3(R.zv[0:M, :]), bc3(R.bon[0:M, :], 64), ALU.mult), r=[K("zv"), K("bon")], w=["TP0"])
            self.V(lambda e: e.tensor_tensor(yc[0:M, :], yc[0:M, :], y2[0:M, :], ALU.add), r=["TP1", "TP0"], w=["TP1"])
            self.V(lambda e: e.tensor_tensor(ob[0:M, :], yc[0:M, :], pg[0:M, :], ALU.mult), r=["TP1", pgk], w=["TP0"])
            pbk, pk = self.pb()
            for j in range(4):
                fw.tr(pbk[:, j * M:(j + 1) * M], ob[0:M, j * 128:(j + 1) * 128], identb[0:M, 0:M], r=["TP0", "identb"], w=[pk])
            fw.act(orT[:, :, 0:M], pbk[:, 0:4 * M].rearrange("p (j t) -> p j t", j=4), AF.Copy, r=[pk], w=["orT"])

        def gate_branch(M, hcur, hk, mdst, mkey):
            for half in range(2):
                pg, pgk = self.pf()
                for q in range(4):
                    dc = half * 4 + q
                    for c in range(8):
                        fw.mm(pg[:, q * M:(q + 1) * M], Wg[:, c, dc * 128:(dc + 1) * 128], hcur(c), c == 0, c == 7, r=[hk, "Wg_%d" % c], w=[pgk])
                fw.act(sgr[:, half * 4:(half + 1) * 4, 0:M], pg[:, 0:4 * M].rearrange("p (q t) -> p q t", q=4), AF.Sigmoid, r=[pgk], w=["sgr%d" % half])
                pbr, pbk_ = self.pf()
                for q in range(4):
                    dc = half * 4 + q
                    for j in range(4):
                        fw.mm(pbr[:, q * M:(q + 1) * M], Wr[:, j, dc * 128:(dc + 1) * 128], orT[:, j, 0:M], j == 0, j == 3, r=["orT", "Wr_%d" % j], w=[pbk_])
                self.V(lambda e, half=half, pbr=pbr: e.tensor_tensor(mdst[:, half * 4:(half + 1) * 4, 0:M], sgr[:, half * 4:(half + 1) * 4, 0:M],
                                                                 pbr[:, 0:4 * M].rearrange("p (q t) -> p q t", q=4), ALU.mult),
                       r=["sgr%d" % half, pbk_], w=[mkey])

        R1 = mkrec(1)
        for j in range(4):
            self.P(lambda e, j=j: e.tensor_copy(R1.G4[j][:, :, 512:640], identb[:, :].unsqueeze(1).to_broadcast([128, 2, 128])),
                   r=["identb"], w=[R1.K("G4_%d" % j)])
        RR = [R0, R1]

        def H1a(i):
            R, Rp = RR[i % 2], RR[(i + 1) % 2]
            hT = R.hT
            xt, xk = self.xt[i % 2], "xt%d" % (i % 2)
            src, _ = self.xsrc(l, i)
            fw.dma(xt[:], src, r=[("xb", i)], w=[xk], key=xk)
            hk = R.K("hTr")
            if i == 0:
                self.V(lambda e, hT=hT: e.memset(hT[:, :, 0:1], 0.0), w=[hk])
            else:
                self.P(lambda e, hT=hT, hp=Rp.hT: e.tensor_copy(hT[:, :, 0:1], hp[:, :, 128:129]), r=[Rp.K("hTr")], w=[hk])
            self.norm_a(xt, xk, 128)

        def H1b(i):
            R = RR[i % 2]
            K = R.K
            hT = R.hT
            hk = K("hTr")
            self.norm_b(128, hT[:, :, 1:129], hk, identb)
            hcur = lambda c, hT=hT: hT[:, c, 1:129]
            hprev = lambda c, hT=hT: hT[:, c, 0:128]
            for g0, dst, dk in [(0, zr, "zr"), (512, zk, "zk"), (1024, R.zv, K("zv"))]:
                ps, pk = tok_proj(128, hcur, hprev, hk, g0, dk)
                fw.act(dst[:, :], ps[:, :], AF.Copy, r=[pk], w=[dk])
            ps, pk = feat_proj(128, hcur, hprev, hk, 1536)
            fw.act(lact[0:64, :], ps[0:64, 0:128], AF.Tanh, r=[pk], w=["lact"])
            fw.act(lact[64:128, :], ps[64:128, 0:128], AF.Copy, r=[pk], w=["lact"])
            ps, pk = feat_proj(128, hcur, hprev, hk, 1664)
            fw.act(R.sgT[:, :], ps[:, 0:128], AF.Sigmoid, r=[pk], w=[K("sgT")])
            if i == NT - 1:
                raw_last(lambda c, hT=hT: hT[:, c, 128:129], hk, 1, O["p_shift"][l:l + 1, :])

        def H1c(i):
            prep(128, False, RR[i % 2])

        def H1d(i):
            stageAB(RR[i % 2])

        H2st = {}

        def H2a(i):
            R = RR[i % 2]
            stageC(R)
            psY, pyk = stageD(R)
            n_update(R)
            pg, pgk = self.pf()
            fw.mm(pg[:, :], R.sgT[:, :], lg2[:, :], True, True, r=[R.K("sgT"), "lg2"], w=[pgk])
            H2st[i] = (psY, pyk, pg, pgk)

        def H2b(i):
            psY, pyk, pg, pgk = H2st.pop(i)
            post(128, psY[:, :], [pyk], pg, pgk, RR[i % 2])

        def H2c(i):
            R = RR[i % 2]
            m, mk = mrT[0], "mrT0"
            gate_branch(128, lambda c, R=R: R.hT[:, c, 1:129], R.K("hTr"), m, mk)
            fw.dma(self.mrbuf[i].rearrange("p (c t) -> p c t", c=8), m[:, :, :], r=[mk], w=[("mr", i)], key=mk)

        def cap(pool, f, i):
            self.pool = pool
            return fw.capture(lambda: f(i))

        for f in (H1a, H1b, H1c, H1d):
            fw.replay([cap(0, f, 0)])
        for i in range(NT):
            nx = i + 1 < NT
            if nx:
                fw.replay([cap(0, H1a, i + 1)])
            fw.replay([cap(1, H2a, i)])
            fw.replay(([cap(0, H1b, i + 1)] if nx else []) + [cap(1, H2b, i)])
            fw.replay(([cap(0, H1c, i + 1)] if nx else []) + [cap(1, H2c, i)])
            if nx:
                fw.replay([cap(0, H1d, i + 1)])
        self.pool = None
        for j in range(4):
            ps, pk = self.pf()
            fw.tr(ps[:, 0:128], Nst[:, j, :], identf[:, :], r=["Nst", "identf"], w=[pk])
            fw.act(T[0][:, j * 128:(j + 1) * 128], ps[:, 0:128], AF.Copy, r=[pk], w=["T0"])
        for h_ in range(8):
            j, o = h_ // 2, (h_ % 2) * 64
            fw.dma(O["p_wkv"][l, h_], T[0][o:o + 64, j * 128 + o:j * 128 + o + 64], r=["T0"], key="T0")

        self.release(m1)
        RS = NSP()
        RS.zv = sbl("zv_s", [128, RD])
        RS.sgT = sbl("sgT_s", [128, 128], BF)
        RS.bon = sbl("bon_s", [128, 8])
        RS.K = lambda n: n + "#s"
        hTs = sbl("hTs", [128, 8, 80], BF)
        sadd = sbl("sadd", [16, RP])
        stT = sbl("stT", [128, 2, 16])
        zf = sbl("zf", [128, 2, 64])
        QH = sbl("QH", [128, 6, 4, 64])
        Sst = sbl("Sst", [128, 64, 64])
        Stmp = sbl("Stmp", [128, 64, 64])
        sk = sbl("sk", [128, 64])
        yh = sbl("yh", [128, 4, 64])
        ytm = T[7]
        self.V(lambda e: e.memset(hTs[:], 0.0), w=["hTs"])
        i = NT
        xt, xk = self.xt[i % 2], "xt%d" % (i % 2)
        src, _ = self.xsrc(l, i)
        fw.dma(xt[0:MS, :], src, r=[("xb", i)], w=[xk], key=xk)
        self.norm_hT(xt, xk, MS, hTs[:, :, 16:80], "hTs", identb)
        hcur = lambda c: hTs[:, c, 16:80]
        hprev = lambda c: hTs[:, c, 0:64]
        fw.dma(sadd[:, :], I["st_shift"][l], w=["sadd"], key="sadd")
        for q in range(2):
            ps, pk = self.pf()
            fw.tr(ps[:, 0:16], sadd[0:16, 1536 + q * 128:1536 + (q + 1) * 128], identf[0:16, 0:16], r=["sadd", "identf"], w=[pk])
            self.V(lambda e, q=q, ps=ps: e.tensor_scalar(stT[:, q, :], ps[:, 0:16], mucol[:, q:q + 1], None, ALU.mult), r=[pk, "mucol"], w=["stT"])
        for gi, g0 in enumerate(range(0, RP, 512)):
            n = min(512, RP - g0)
            self.bcast_load(T[4 + gi][0:16, 0:n], "T%d" % (4 + gi), I["rwkv_mu"][l, g0:g0 + n])
            self.V(lambda e, gi=gi, g0=g0, n=n: e.tensor_tensor(sadd[:, g0:g0 + n], sadd[:, g0:g0 + n], T[4 + gi][0:16, 0:n], ALU.mult),
                   r=["sadd", "T%d" % (4 + gi)], w=["sadd"])
        zv, sgT = RS.zv, RS.sgT
        for g0, dst, dk in [(0, zr, "zr"), (512, zk, "zk"), (1024, zv, RS.K("zv"))]:
            ps, pk = tok_proj(MS, hcur, hprev, "hTs", g0, dk)
            fw.act(dst[0:MS, :], ps[0:MS, :], AF.Copy, r=[pk], w=[dk])
            self.V(lambda e, dst=dst, g0=g0: e.tensor_tensor(dst[0:16, :], dst[0:16, :], sadd[0:16, g0:g0 + 512], ALU.add), r=[dk, "sadd"], w=[dk])
        for q, g0 in enumerate([1536, 1664]):
            ps, pk = feat_proj(MS, hcur, hprev, "hTs", g0)
            fw.act(zf[:, q, :], ps[:, 0:MS], AF.Copy, r=[pk], w=["zf"])
            self.V(lambda e, q=q: e.tensor_tensor(zf[:, q, 0:16], zf[:, q, 0:16], stT[:, q, :], ALU.add), r=["zf", "stT"], w=["zf"])
        fw.act(lact[0:64, 0:MS], zf[0:64, 0, :], AF.Tanh, r=["zf"], w=["lact"])
        fw.act(lact[64:128, 0:MS], zf[64:128, 0, :], AF.Copy, r=["zf"], w=["lact"])
        fw.act(sgT[:, 0:MS], zf[:, 1, :], AF.Sigmoid, r=["zf"], w=[RS.K("sgT")])
        prep(MS, True, RS)
        if l == 0:
            for nm, ap, k in [("s_zr", zr, "zr"), ("s_zk", zk, "zk"), ("s_zv", zv, "zv"), ("s_dec", T[6], "T6"), ("s_kk", T[2], "T2"),
                              ("s_kf", T[4], "T4"), ("s_be", T[5], "T5"), ("s_a", T[1], "T1")]:
                self.tap(nm, ap[0:MS, :], [k])
        sqv = self.sq.rearrange("x (t q) (h d) -> (q h) x t d", t=4, h=NH)
        for x in range(6):
            fw.dma(QH[:, x, :, :], sqv[:, x, :, :], r=[("sq", x)], w=["QH"], key="QH")
        fw.dma(Sst[:, :, :].rearrange("p v k -> p (v k)"), I["st_wkv"][l], w=["Sst"], key="Sst")
        for t in range(4):
            r_, w_, k_, v_, kk_, b_ = (QH[:, x, t, :] for x in range(6))
            rowb = lambda a: a.unsqueeze(1).to_broadcast([128, 64, 64])
            colb = lambda a: a.unsqueeze(2).to_broadcast([128, 64, 64])
            self.V(lambda e, kk_=kk_: e.tensor_tensor(Stmp[:, :, :], Sst[:, :, :], rowb(kk_), ALU.mult), r=["Sst", "QH"], w=["Stmp"])
            self.V(lambda e: e.tensor_reduce(sk[:, :], Stmp[:, :, :], AX.X, ALU.add), r=["Stmp"], w=["sk"])
            self.P(lambda e, w_=w_: e.tensor_tensor(Sst[:, :, :], Sst[:, :, :], rowb(w_), ALU.mult), r=["Sst", "QH", "Stmp"], w=["Sst"])
            self.V(lambda e, b_=b_: e.tensor_tensor(Stmp[:, :, :], colb(sk[:, :]), rowb(b_), ALU.mult), r=["sk", "QH"], w=["Stmp"])
            self.V(lambda e: e.tensor_tensor(Sst[:, :, :], Sst[:, :, :], Stmp[:, :, :], ALU.subtract), r=["Sst", "Stmp"], w=["Sst"])
            self.P(lambda e, v_=v_, k_=k_: e.tensor_tensor(Stmp[:, :, :], colb(v_), rowb(k_), ALU.mult), r=["QH", "Sst"], w=["Stmp"])
            self.V(lambda e: e.tensor_tensor(Sst[:, :, :], Sst[:, :, :], Stmp[:, :, :], ALU.add), r=["Sst", "Stmp"], w=["Sst"])
            self.P(lambda e, r_=r_: e.tensor_tensor(Stmp[:, :, :], Sst[:, :, :], rowb(r_), ALU.mult), r=["Sst", "QH"], w=["Stmp"])
            self.V(lambda e, t=t: e.tensor_reduce(yh[:, t, :], Stmp[:, :, :], AX.X, ALU.add), r=["Stmp"], w=["yh"])
        fw.dma(O["s_wkv"][l], Sst[:, :, :].rearrange("p v k -> p (v k)"), r=["Sst"], key="Sst")
        if l == 0:
            self.tap("s_QH", QH, ["QH"])
            self.tap("s_yh", yh, ["yh"])
        fw.dma(self.sy.rearrange("(t q) (h d) -> (q h) t d", t=4, h=NH), yh[:, :, :], r=["yh"], w=["sy"], key="yh")
        fw.dma(ytm[0:MS, :], self.sy, r=["sy"], w=["T7"], key="ytm")
        pg, pgk = self.pf()
        fw.mm(pg[0:MS, :], sgT[:, 0:MS], lg2[:, :], True, True, r=[RS.K("sgT"), "lg2"], w=[pgk])
        post(MS, ytm[0:MS, :], ["T7"], pg, pgk, RS)
        m, mk = mrT[0], "mrT0"
        gate_branch(MS, hcur, "hTs", m, mk)
        fw.dma(self.mrbuf[NT].rearrange("p (c t) -> p c t", c=8)[:, :, 0:MS], m[:, :, 0:MS], r=[mk], w=[("mr", NT)], key=mk)
        raw_last(lambda c: hTs[:, c, 64:80], "hTs", 16, O["s_shift"][l])

    def pass_attn(self, l, es2):
        fw, I, O, NT = self.fw, self.I, self.O, self.NT
        sbl = lambda n, s, dt=F32: self.sbl(es2, "a%d_" % l + n, s, dt)
        identb, identf = self.identb, self.identf
        Wq = sbl("Wq", [128, 8, 768], BF)
        Wg = sbl("Wg", [128, 8, D], BF)
        Wa = sbl("Wa", [128, 4, D], BF)
        Wo = sbl("Wo", [128, 8, D], BF)
        self.col_load(self.gcol[:], "gcol", I["norm_mix_g"][l], 8)
        m0 = self.aoff
        self.wstage = [sbl("wst%d" % i_, [128, 2048]) for i_ in range(2)]
        win = I["w_in"][l]
        gsc = lambda c: self.gcol[:, c:c + 1]
        self.prep_w(8, 512, lambda c, s0, n: win[c * 128:(c + 1) * 128, RP:RP + 512],
                    lambda c, s0, n: Wq[:, c, 0:512].rearrange("p (j g d) -> p g j d", j=4, g=2), lambda c: "Wq_%d" % c, "col", gsc,
                    sview=lambda a: a.rearrange("p (g j d) -> p g j d", g=2, j=4))
        self.prep_w(8, 256, lambda c, s0, n: win[c * 128:(c + 1) * 128, RP + 512:RP + 768],
                    lambda c, s0, n: Wq[:, c, 512:768], lambda c: "Wq_%d" % c, "col", gsc)
        self.prep_w(8, D, lambda c, s0, n: win[c * 128:(c + 1) * 128, 3584 + s0:3584 + s0 + n],
                    lambda c, s0, n: Wg[:, c, s0:s0 + n], lambda c: "Wga_%d" % c, "col", gsc)
        wbr = I["w_br_attn"][l]
        self.prep_w(4, D, lambda c, s0, n: wbr[c * 128:(c + 1) * 128, s0:s0 + n],
                    lambda c, s0, n: Wa[:, c, s0:s0 + n], lambda c: "Wa_%d" % c, "plain")
        wo = I["w_out"][l]
        self.prep_w(8, D, lambda c, s0, n: wo[c * 128:(c + 1) * 128, s0:s0 + n],
                    lambda c, s0, n: Wo[:, c, s0:s0 + n], lambda c: "Wo_%d" % c, "plain")
        self.release(m0)
        amask = sbl("amask", [128, 1024])
        fw.dma(amask[:, 0:768], I["c_amask"], w=["amask"], key="amask")
        fw.dma(amask[:, 768:1024], I["c_amask0"], w=["amask"], key="amask")
        smask = sbl("smask", [32, 132])
        fw.dma(smask[:], I["c_smask"], w=["smask"], key="smask")
        sinks = sbl("sinks", [128, NH])
        self.bcast_load(sinks[:], "sinks", I["attn_sinks"][l])
        hTd = [sbl("hT%d" % i_, [128, 8, 128], BF) for i_ in range(2)]
        hT = hTd[1]
        qkv = sbl("qkv", [128, 768])
        rot = sbl("rot", [128, 640])
        rtmp = [sbl("rtmp%d" % i, [128, 320]) for i in range(2)]
        rotb = sbl("rotb", [128, 640], BF)
        cs = [sbl("cs%d" % i, [128, 64]) for i in range(2)]
        qT = sbl("qT", [128, 4, 128], BF)
        KTr = sbl("KTr", [128, 2, 128], BF)
        Vp = sbl("Vp", [128, 2, 2, 2, 128], BF)
        scg = [sbl("sc%d" % g_, [128, 4, 256]) for g_ in range(2)]
        stg = [sbl("st%d" % g_, [128, 16]) for g_ in range(2)]
        pbfg = [sbl("pbf%d" % g_, [128, 4, 256], BF) for g_ in range(2)]
        pTg = [sbl("pT%d" % g_, [128, 4, 2, 128], BF) for g_ in range(2)]
        oT = sbl("oT", [128, 4, 128], BF)
        sga = sbl("sga", [128, 8, 128])
        mrl = [sbl("mrl%d" % i, [128, 8, 128], BF) for i in range(2)]
        mg = sbl("mg", [128, 8, 128], BF)
        xo = [sbl("xo%d" % i, [128, D]) for i in range(2)]
        KA = sbl("KA", [128, NS, 128])
        VA = sbl("VA", [128, NS, 128])
        VAb = sbl("VAb", [128, NS, 128], BF)
        KB = sbl("KB", [4, NS, 128])
        VBt = sbl("VB", [4, NS, 128])
        VBb = sbl("VBb", [4, NS, 128], BF)
        KAT = sbl("KAT", [128, NS, 128], BF)
        KBT = sbl("KBT", [128, NS, 4], BF)
        qbd = sbl("qbd", [128, NS, 32], BF)
        ssc = sbl("ssc", [32, NS, 132])
        sst = sbl("sst", [32, 4 * NS])
        spb = sbl("spb", [32, NS, 132], BF)
        spT = sbl("spT", [128, NS, 32], BF)
        spTB = sbl("spTB", [4, NS, 32], BF)
        oTs = sbl("oTs", [128, 4, MS], BF)

        self.V(lambda e: e.memset(Vp[:], 0.0), w=["Vp0", "Vp1"])
        self.V(lambda e: e.memset(KTr[:], 0.0), w=["KTr0", "KTr1"])
        self.V(lambda e: e.memset(qbd[:], 0.0), w=["qbd"])

        def proj_rope(M, hcur, hk, cosap, sinap, cskey):
            for g0, n in [(0, 512), (512, 256)]:
                ps, pk = self.pf()
                for c in range(8):
                    fw.mm(ps[0:M, 0:n], hcur(c), Wq[:, c, g0:g0 + n], c == 0, c == 7, r=[hk, "Wq_%d" % c], w=[pk])
                fw.act(qkv[0:M, g0:g0 + n], ps[0:M, 0:n], AF.Copy, r=[pk], w=["qkv%d" % (g0 // 512)])
            qk3 = qkv[0:M, 0:640].rearrange("p (h d) -> p h d", h=10)
            r3 = rot[0:M, :].rearrange("p (h d) -> p h d", h=10)
            x1, x2 = qk3[:, :, 0:32], qk3[:, :, 32:64]
            cb = cosap.unsqueeze(1).to_broadcast([M, 10, 32])
            sb_ = sinap.unsqueeze(1).to_broadcast([M, 10, 32])
            ta = rtmp[0][0:M, :].rearrange("p (h d) -> p h d", h=10)
            tb = rtmp[1][0:M, :].rearrange("p (h d) -> p h d", h=10)
            rk = ["qkv0", "qkv1", cskey]
            self.V(lambda e: e.tensor_tensor(ta, x1, cb, ALU.mult), r=rk, w=["rtmp0"])
            self.P(lambda e: e.tensor_tensor(tb, x2, sb_, ALU.mult), r=rk, w=["rtmp1"])
            self.V(lambda e: e.tensor_tensor(r3[:, :, 0:32], ta, tb, ALU.subtract), r=["rtmp0", "rtmp1"], w=["rot"])
            self.V(lambda e: e.tensor_tensor(ta, x2, cb, ALU.mult), r=rk + ["rot"], w=["rtmp0"])
            self.P(lambda e: e.tensor_tensor(tb, x1, sb_, ALU.mult), r=rk + ["rot"], w=["rtmp1"])
            self.V(lambda e: e.tensor_tensor(r3[:, :, 32:64], ta, tb, ALU.add), r=["rtmp0", "rtmp1"], w=["rot"])
            fw.act(rotb[0:M, :], rot[0:M, :], AF.Copy, r=["rot"], w=["rotb"])

        def q_transposes(M, dst, dkey):
            pbk, pk = self.pb()
            for jj in range(4):
                fw.tr(pbk[:, jj * M:(jj + 1) * M], rotb[0:M, jj * 128:(jj + 1) * 128], identb[0:M, 0:M], r=["rotb", "identb"], w=[pk])
            fw.act(dst, pbk[:, 0:4 * M].rearrange("p (j t) -> p j t", j=4), AF.Copy, r=[pk], w=[dkey])

        def gates_part(M, hcur, hk):
            for half in range(2):
                pg, pgk = self.pf()
                for q in range(4):
                    dc = half * 4 + q
                    for c in range(8):
                        fw.mm(pg[:, q * M:(q + 1) * M], Wg[:, c, dc * 128:(dc + 1) * 128], hcur(c), c == 0, c == 7, r=[hk, "Wga_%d" % c], w=[pgk])
                fw.act(sga[:, half * 4:(half + 1) * 4, 0:M], pg[:, 0:4 * M].rearrange("p (q t) -> p q t", q=4), AF.Sigmoid, r=[pgk], w=["sga%d" % half])

        def gate_out(M, hcur, hk, oTt, okey, mr, mrk, xt, xk, xo_, xok, do_gates=True):
            if do_gates:
                gates_part(M, hcur, hk)
            for half in range(2):
                pbr, pbk_ = self.pf()
                for q in range(4):
                    dc = half * 4 + q
                    for cc in range(4):
                        fw.mm(pbr[:, q * M:(q + 1) * M], Wa[:, cc, dc * 128:(dc + 1) * 128], oTt[:, cc, 0:M], cc == 0, cc == 3, r=[okey, "Wa_%d" % cc], w=[pbk_])
                hs = slice(half * 4, (half + 1) * 4)
                self.V(lambda e, hs=hs, pbr=pbr: e.tensor_tensor(sga[:, hs, 0:M], sga[:, hs, 0:M], pbr[:, 0:4 * M].rearrange("p (q t) -> p q t", q=4), ALU.mult),
                       r=["sga%d" % half, pbk_], w=["sga%d" % half])
                self.V(lambda e, hs=hs: e.tensor_tensor(mg[:, hs, 0:M], sga[:, hs, 0:M], mr[:, hs, 0:M], ALU.add), r=["sga%d" % half, mrk], w=["mg%d" % half])
            for grp in range(2):
                px, pxk = self.pf()
                for dc in range(8):
                    fw.mm(px[0:M, :], mg[:, dc, 0:M], Wo[:, dc, grp * 512:(grp + 1) * 512], dc == 0, dc == 7, r=["mg%d" % (dc // 4), "Wo_%d" % dc], w=[pxk])
                self.V(lambda e, grp=grp, px=px: e.tensor_tensor(xo_[0:M, grp * 512:(grp + 1) * 512], xt[0:M, grp * 512:(grp + 1) * 512], px[0:M, :], ALU.add),
                       r=[xk, pxk], w=[xok])

        def put_kv(slot):
            pbk, pk = self.pb()
            fw.tr(pbk[:, 0:128], rotb[:, 512:640], identb[:, :], r=["rotb", "identb"], w=[pk])
            self.V(lambda e, pbk=pbk, slot=slot: e.tensor_copy(KTr[:, slot, :], pbk[:, 0:128]), r=[pk], w=["KTr%d" % slot])
            for g in range(2):
                vsrc = qkv[:, 640 + g * 64:640 + (g + 1) * 64]
                fw.act(Vp[:, slot, g, 0, 0:64], vsrc, AF.Copy, r=["qkv1"], w=["Vp%d" % slot])
                self.P(lambda e, g=g, vsrc=vsrc, slot=slot: e.tensor_copy(Vp[:, slot, g, 1, 64:128], vsrc), r=["qkv1"], w=["Vp%d" % slot])

        xt, xk = self.xt[1], "xt1"
        fw.dma(xt[:], (I["xh0"] if (l == 0 or NSEG == 1) else self.xh_dram), r=["xh_dram"], w=[xk], key=xk)
        fw.dma(cs[1][:, 0:32], I["c_cosh"], w=["cs1"], key="cs1")
        fw.dma(cs[1][:, 32:64], I["c_sinh"], w=["cs1"], key="cs1")
        self.norm_hT(xt, xk, 128, hT[:, :, :], "hT1", identb)
        proj_rope(128, lambda c: hT[:, c, :], "hT1", cs[1][:, 0:32], cs[1][:, 32:64], "cs1")
        put_kv(1)
        def pre(i):
            xt, xk = self.xt[i % 2], "xt%d" % (i % 2)
            src, _ = self.xsrc(l, i)
            fw.dma(xt[:], src, r=[("xb", i)], w=[xk], key=xk)
            mr, mrk = mrl[i % 2], "mrl%d" % (i % 2)
            fw.dma(mr[:, :, :], self.mrbuf[i].rearrange("p (c t) -> p c t", c=8), r=[("mr", i)], w=[mrk], key=mrk)
            ck_ = "cs%d" % (i % 2)
            fw.dma(cs[i % 2][:, 0:32], I["c_cosp"][i * 128:(i + 1) * 128, :], w=[ck_], key=ck_)
            fw.dma(cs[i % 2][:, 32:64], I["c_sinp"][i * 128:(i + 1) * 128, :], w=[ck_], key=ck_)
            self.norm_hT(xt, xk, 128, hTd[i % 2][:, :, :], "hT%d" % (i % 2), identb)

        pre(0)
        for i in range(NT):
            xt, xk = self.xt[i % 2], "xt%d" % (i % 2)
            mr, mrk = mrl[i % 2], "mrl%d" % (i % 2)
            ck_ = "cs%d" % (i % 2)
            hkk = "hT%d" % (i % 2)
            hcur = lambda c, i=i: hTd[i % 2][:, c, :]
            proj_rope(128, hcur, hkk, cs[i % 2][:, 0:32], cs[i % 2][:, 32:64], ck_)
            slot = i % 2
            if i == NT - 1:
                fw.dma(O["p_k"][l], rot[:, 512:640], r=["rot"], key="rot")
                fw.dma(O["p_v"][l], qkv[:, 640:768], r=["qkv1"], key="qkv1")
            q_transposes(128, qT[:, :, :], "qT")
            put_kv(slot)
            mvar = 3 if i == 0 else slot
            msk = amask[:, mvar * 256:(mvar + 1) * 256].unsqueeze(1).to_broadcast([128, 4, 256])
            pSg = []
            for g in range(2):
                o = g * 64
                pS = []
                for jj in range(4):
                    if jj % 2 == 0:
                        ps, pk = self.pf()
                        pS.append((ps, pk))
                    fw.mm(ps[:, (jj % 2) * 256:(jj % 2 + 1) * 256], qT[o:o + 64, jj, :], KTr[o:o + 64, :, :].rearrange("p s t -> p (s t)"),
                          True, True, r=["qT", "KTr0", "KTr1"], w=[pk])
                pSg.append(pS)
            gates_part(128, hcur, hkk)

            def softmax(g):
                sc, st, pbf = scg[g], stg[g], pbfg[g]
                sck = ["sc%d_0" % g, "sc%d_1" % g]
                for half, (ps, pk) in enumerate(pSg[g]):
                    self.V(lambda e, ps=ps, half=half, msk=msk, sc=sc: e.scalar_tensor_tensor(
                        sc[:, half * 2:(half + 1) * 2, :], ps[:, :].rearrange("p (j c) -> p j c", j=2), 0.125,
                        msk[:, 0:2, :], ALU.mult, ALU.add), r=[pk, "amask"], w=[sck[half]])
                k0, k2, k3 = "st%d" % g, "st%d_2" % g, "st%d_3" % g
                self.V(lambda e: e.tensor_reduce(st[:, 0:4], sc[:, :, :], AX.X, ALU.max), r=sck, w=[k0])
                self.V(lambda e: e.tensor_tensor(st[:, 0:4], st[:, 0:4], sinks[:, g * 4:(g + 1) * 4], ALU.max), r=[k0, "sinks"], w=[k0])
                self.V(lambda e: e.tensor_tensor(sc[:, :, :], sc[:, :, :], bc3(st[:, 0:4], 256), ALU.subtract), r=sck + [k0], w=sck)
                fw.act(sc[:, :, :], sc[:, :, :], AF.Exp, r=sck, w=sck)
                self.V(lambda e: e.tensor_reduce(st[:, 4:8], sc[:, :, :], AX.X, ALU.add), r=sck, w=[k2])
                self.V(lambda e: e.tensor_tensor(st[:, 8:12], sinks[:, g * 4:(g + 1) * 4], st[:, 0:4], ALU.subtract), r=[k0, "sinks"], w=[k3])
                fw.act(st[:, 8:12], st[:, 8:12], AF.Exp, r=[k3], w=[k3])
                self.V(lambda e: e.tensor_tensor(st[:, 4:8], st[:, 4:8], st[:, 8:12], ALU.add), r=[k2, k3], w=[k2])
                self.V(lambda e: e.reciprocal(st[:, 4:8], st[:, 4:8]), r=[k2], w=[k2])
                self.V(lambda e: e.tensor_tensor(pbf[:, :, :], sc[:, :, :], bc3(st[:, 4:8], 256), ALU.mult), r=sck + [k2], w=["pbf%d" % g])

            def p_transposes(g):
                pbf, pT = pbfg[g], pTg[g]
                pbk, pk = self.pb()
                for jj in range(4):
                    for s_ in range(2):
                        fw.tr(pbk[:, (jj * 2 + s_) * 128:(jj * 2 + s_ + 1) * 128], pbf[:, jj, s_ * 128:(s_ + 1) * 128], identb[:, :], r=["pbf%d" % g, "identb"], w=[pk])
                fw.act(pT[:, :, :, :], pbk[:, :].rearrange("p (j s t) -> p j s t", j=4, s=2), AF.Copy, r=[pk], w=["pT%d" % g])

            def pv(g, pO, pok):
                pT = pTg[g]
                for c2 in range(2):
                    cc = g * 2 + c2
                    n = 0
                    for par in range(2):
                        jj = c2 * 2 + par
                        for s_ in range(2):
                            fw.mm(pO[:, cc * 128:(cc + 1) * 128], Vp[:, s_, g, par, :], pT[:, jj, s_, :], n == 0, n == 3,
                                  r=["Vp0", "Vp1", "pT%d" % g], w=[pok])
                            n += 1

            softmax(0)
            if i + 1 < NT:
                pre(i + 1)
            p_transposes(0)
            pO, pok = self.pf()
            pv(0, pO, pok)
            softmax(1)
            p_transposes(1)
            pv(1, pO, pok)
            fw.act(oT[:, :, :], pO[:, :].rearrange("p (c t) -> p c t", c=4), AF.Copy, r=[pok], w=["oT"])
            xo_, xok = xo[i % 2], "xo%d" % (i % 2)
            gate_out(128, hcur, hkk, oT, "oT", mr, mrk, xt, xk, xo_, xok, do_gates=False)
            fw.dma(self.xbuf[i * 128:(i + 1) * 128, :], xo_[:, :], r=[xok], w=[("xb", i)], key=xok)
        if NSEG > 1:
            self.gather_select(xo_[:, :], [xok], D, self.agX_in, self.agX_out, "agX")
            fw.dma(self.xh_dram, xo_[:, :], r=[xok], w=["xh_dram"], key="xhst")

        i = NT
        xt, xk = self.xt[i % 2], "xt%d" % (i % 2)
        src, _ = self.xsrc(l, i)
        fw.dma(xt[0:MS, :], src, r=[("xb", i)], w=[xk], key=xk)
        mr, mrk = mrl[i % 2], "mrl%d" % (i % 2)
        fw.dma(mr[:, :, 0:MS], self.mrbuf[NT].rearrange("p (c t) -> p c t", c=8)[:, :, 0:MS], r=[("mr", NT)], w=[mrk], key=mrk)
        ck_ = "cs%d" % (i % 2)
        fw.dma(cs[i % 2][0:MS, 0:32], I["c_coss"], w=[ck_], key=ck_)
        fw.dma(cs[i % 2][0:MS, 32:64], I["c_sins"], w=[ck_], key=ck_)
        self.norm_hT(xt, xk, MS, hT[:, :, 0:MS], "hT1", identb)
        hcur = lambda c: hT[:, c, 0:MS]
        proj_rope(MS, hcur, "hT1", cs[i % 2][0:MS, 0:32], cs[i % 2][0:MS, 32:64], ck_)
        for (cin, cout, srcap, srck, dkey) in [("ck", "s_k", rot[:, 512:640], "rot", "sk"), ("cv", "s_v", qkv[:, 640:768], "qkv1", "sv")]:
            fw.dma(O[cout][l, :, 0:124, :], I[cin][l, :, 4:128, :], w=[dkey], key=dkey + "c")
            for t in range(4):
                fw.dma(O[cout][l, :, 124 + t, :], srcap[t * 16:(t + 1) * 16, :], r=[srck], w=[dkey], key=dkey + "n")
        fw.dma(KA[:, :, :], O["s_k"][l].rearrange("q p c -> p q c"), r=["sk"], w=["KA"], key="KA")
        fw.dma(VA[:, :, :], O["s_v"][l].rearrange("q p c -> p q c"), r=["sv"], w=["VA"], key="VA")
        fw.dma(KB[:, :, :], I["ck"][l, :, 0:4, :].rearrange("q p c -> p q c"), w=["KB"], key="KB")
        fw.dma(VBt[:, :, :], I["cv"][l, :, 0:4, :].rearrange("q p c -> p q c"), w=["VB"], key="VB")
        self.P(lambda e: e.tensor_copy(VAb[:, :, :], VA[:, :, :]), r=["VA"], w=["VAb"])
        self.P(lambda e: e.tensor_copy(VBb[:, :, :], VBt[:, :, :]), r=["VB"], w=["VBb"])
        for q4 in range(4):
            ps, pk = self.pf()
            for qq in range(4):
                q = q4 * 4 + qq
                fw.tr(ps[:, qq * 128:(qq + 1) * 128], KA[:, q, :], identf[:, :], r=["KA", "identf"], w=[pk])
            fw.act(KAT[:, q4 * 4:(q4 + 1) * 4, :], ps[:, :].rearrange("p (q t) -> p q t", q=4), AF.Copy, r=[pk], w=["KAT"])
        ps, pk = self.pf()
        for q in range(NS):
            fw.tr(ps[:, q * 4:(q + 1) * 4], KB[0:4, q, :], identf[0:4, 0:4], r=["KB", "identf"], w=[pk])
        fw.act(KBT[:, :, :], ps[:, 0:64].rearrange("p (q t) -> p q t", q=NS), AF.Copy, r=[pk], w=["KBT"])
        q_transposes(MS, qT[:, :, 0:MS], "qT")
        for g in range(2):
            for jj in range(4):
                o = g * 64
                dst = qbd[o:o + 64, :, g * 16 + jj * 4:g * 16 + (jj + 1) * 4]
                srcq = qT[o:o + 64, jj, 0:MS].rearrange("p (t q) -> p q t", t=4)
                self.V(lambda e, dst=dst, srcq=srcq: e.tensor_copy(dst, srcq), r=["qT"], w=["qbd"])
        pSA = []
        for q4 in range(4):
            ps, pk = self.pf()
            pSA.append((ps, pk))
            for qq in range(4):
                q = q4 * 4 + qq
                fw.mm(ps[0:32, qq * 128:(qq + 1) * 128], qbd[:, q, :], KAT[:, q, :], True, True, r=["qbd", "KAT"], w=[pk])
        psB, pkB = self.pf()
        for q in range(NS):
            fw.mm(psB[0:32, q * 4:(q + 1) * 4], qbd[:, q, :], KBT[:, q, :], True, True, r=["qbd", "KBT"], w=[pkB])
        for q4, (ps, pk) in enumerate(pSA):
            self.V(lambda e, q4=q4, ps=ps: e.scalar_tensor_tensor(
                ssc[:, q4 * 4:(q4 + 1) * 4, 0:128], ps[0:32, :].rearrange("p (q c) -> p q c", q=4), 0.125,
                smask[:, 0:128].unsqueeze(1).to_broadcast([32, 4, 128]), ALU.mult, ALU.add), r=[pk, "smask"], w=["ssc"])
        self.V(lambda e: e.scalar_tensor_tensor(
            ssc[:, :, 128:132], psB[0:32, 0:64].rearrange("p (q c) -> p q c", q=NS), 0.125,
            smask[:, 128:132].unsqueeze(1).to_broadcast([32, NS, 4]), ALU.mult, ALU.add), r=[pkB, "smask"], w=["ssc"])
        sinkc = sbl("sinkc", [32, 1])
        for g in range(2):
            for jj in range(4):
                p0 = g * 16 + jj * 4
                fw.dma(sinkc[p0:p0 + 4, :], I["attn_sinks"][l, g * 4 + jj:g * 4 + jj + 1].partition_broadcast(4), w=["sinkc"], key="sinkc")
        self.V(lambda e: e.tensor_reduce(sst[:, 0:NS], ssc[:, :, :], AX.X, ALU.max), r=["ssc"], w=["sst"])
        self.V(lambda e: e.tensor_scalar(sst[:, 0:NS], sst[:, 0:NS], sinkc[:, 0:1], None, ALU.max), r=["sst", "sinkc"], w=["sst"])
        self.V(lambda e: e.tensor_tensor(ssc[:, :, :], ssc[:, :, :], bc3(sst[:, 0:NS], 132), ALU.subtract), r=["ssc", "sst"], w=["ssc"])
        fw.act(ssc[:, :, :], ssc[:, :, :], AF.Exp, r=["ssc"], w=["ssc"])
        self.V(lambda e: e.tensor_reduce(sst[:, NS:2 * NS], ssc[:, :, :], AX.X, ALU.add), r=["ssc"], w=["sst2"])
        self.V(lambda e: e.tensor_scalar(sst[:, 2 * NS:3 * NS], sst[:, 0:NS], sinkc[:, 0:1], None, ALU.subtract), r=["sst", "sinkc"], w=["sst3"])
        fw.act(sst[:, 2 * NS:3 * NS], sst[:, 2 * NS:3 * NS], AF.Exp, r=["sst3"], w=["sst3"], scale=-1.0)
        self.V(lambda e: e.tensor_tensor(sst[:, NS:2 * NS], sst[:, NS:2 * NS], sst[:, 2 * NS:3 * NS], ALU.add), r=["sst2", "sst3"], w=["sst2"])
        self.V(lambda e: e.reciprocal(sst[:, NS:2 * NS], sst[:, NS:2 * NS]), r=["sst2"], w=["sst2"])
        self.V(lambda e: e.tensor_tensor(spb[:, :, :], ssc[:, :, :], bc3(sst[:, NS:2 * NS], 132), ALU.mult), r=["ssc", "sst2"], w=["spb"])
        identb32 = identb[0:32, 0:32]
        for q8 in range(2):
            pbk, pk = self.pb()
            for qq in range(8):
                q = q8 * 8 + qq
                fw.tr(pbk[:, qq * 32:(qq + 1) * 32], spb[:, q, 0:128], identb32, r=["spb", "identb"], w=[pk])
            fw.act(spT[:, q8 * 8:(q8 + 1) * 8, :], pbk[:, 0:256].rearrange("p (q c) -> p q c", q=8), AF.Copy, r=[pk], w=["spT"])
        pbk, pk = self.pb()
        for q in range(NS):
            fw.tr(pbk[0:4, q * 32:(q + 1) * 32], spb[:, q, 128:132], identb32, r=["spb", "identb"], w=[pk])
        fw.act(spTB[:, :, :], pbk[0:4, 0:512].rearrange("p (q c) -> p q c", q=NS), AF.Copy, r=[pk], w=["spTB"])
        pO, pok = self.pf()
        for q in range(NS):
            fw.mm(pO[:, q * 32:(q + 1) * 32], VAb[:, q, :], spT[:, q, :], True, False, r=["VAb", "spT"], w=[pok])
            fw.mm(pO[:, q * 32:(q + 1) * 32], VBb[0:4, q, :], spTB[0:4, q, :], False, True, r=["VBb", "spTB"], w=[pok])
        oraw = sbl("oraw", [128, 32, NS], BF)
        fw.act(oraw.rearrange("p c q -> p q c"), pO[:, :].rearrange("p (q c) -> p q c", q=NS), AF.Copy, r=[pok], w=["oraw"])
        for g in range(2):
            for jj in range(4):
                cc, par = g * 2 + jj // 2, jj % 2
                c0 = g * 16 + jj * 4
                srco = oraw[g * 64:(g + 1) * 64, c0:c0 + 4, :].rearrange("p t q -> p (t q)")
                fw.dma(oTs[par * 64:(par + 1) * 64, cc, :], srco, r=["oraw"], w=["oTs"], key="oTs")
        xo_, xok = xo[i % 2], "xo%d" % (i % 2)
        gate_out(MS, hcur, "hT1", oTs, "oTs", mr, mrk, xt, xk, xo_, xok)
        fw.dma(self.xsbuf, xo_[0:MS, :], r=[xok], w=[("xb", NT)], key=xok)

    def pass_ffn(self, l, es2):
        fw, I, O, NT = self.fw, self.I, self.O, self.NT
        sbl = lambda n, s, dt=F32: self.sbl(es2, "f%d_" % l + n, s, dt)
        identb, identf = self.identb, self.identf
        Wc = sbl("Wc", [128, 8, DFF], BF)
        Wu = sbl("Wu", [128, 8, DFF], BF)
        Wd = sbl("Wd", [128, NFC, D], BF)
        self.col_load(self.gcol[:], "gcol", I["norm_ffn_g"][l], 8)
        cw = sbl("cw", [128, 4, NFC])
        for j in range(3):
            self.col_load(cw[:, j, :], "cw", I["ffn_conv_w"][l, j], NFC)
        self.col_load(cw[:, 3, :], "cw", I["ffn_conv_b"][l], NFC)
        m0 = self.aoff
        self.wstage = [sbl("wst%d" % i_, [128, 2048]) for i_ in range(2)]
        wi = I["ffn_w_in"][l]
        gsc = lambda c: self.gcol[:, c:c + 1]
        self.prep_w(8, DFF, lambda c, s0, n: wi[c * 128:(c + 1) * 128, s0:s0 + n],
                    lambda c, s0, n: Wc[:, c, s0:s0 + n], lambda c: "Wc_%d" % c, "col", gsc)
        self.prep_w(8, DFF, lambda c, s0, n: wi[c * 128:(c + 1) * 128, DFF + s0:DFF + s0 + n],
                    lambda c, s0, n: Wu[:, c, s0:s0 + n], lambda c: "Wu_%d" % c, "col", gsc)
        wd = I["ffn_w_down"][l]
        self.prep_w(NFC, D, lambda c, s0, n: wd[c * 128:(c + 1) * 128, s0:s0 + n],
                    lambda c, s0, n: Wd[:, c, s0:s0 + n], lambda c: "Wd_%d" % c, "plain")
        self.release(m0)
        last = (l == 1)
        if last:
            gf = sbl("gf", [128, D])
            self.bcast_load(gf[:], "gf", I["norm_final_g"])
        hTd = [sbl("hT%d" % i_, [128, 8, 128], BF) for i_ in range(2)]
        hT = hTd[0]
        cxf = sbl("cx", [128, NFC * 130])
        cx1 = cxf.rearrange("p (f t) -> p f t", f=NFC)
        cxs = cxf[:, 0:NFC * NS * 6].rearrange("p (f q j) -> p f q j", f=NFC, q=NS)
        acc = [sbl("acc%d" % i_, [128, 4, 128]) for i_ in range(2)]
        aT = sbl("aT", [128, NFC, 128], BF)
        xo = [sbl("xo%d" % i_, [128, D]) for i_ in range(2)]
        ctok = sbl("ctok", [128, DFF])
        cst = ctok
        jk = self.xn

        def finish(M, xt, xk, xo_, xok, dst_final, dst_x, dkey):
            for grp in range(2):
                px, pxk = self.pf()
                for fc in range(NFC):
                    fw.mm(px[0:M, :], aT[:, fc, 0:M], Wd[:, fc, grp * 512:(grp + 1) * 512], fc == 0, fc == NFC - 1, r=["aT", "Wd_%d" % fc], w=[pxk])
                self.V(lambda e, grp=grp, px=px: e.tensor_tensor(xo_[0:M, grp * 512:(grp + 1) * 512], xt[0:M, grp * 512:(grp + 1) * 512], px[0:M, :], ALU.add),
                       r=[xk, pxk], w=[xok])
            if not last:
                fw.dma(dst_x, xo_[0:M, :], r=[xok], w=[dkey], key=xok)
                return
            ss, t1 = self.ss, self.t1
            fw.act(jk[0:M, :], xo_[0:M, :], AF.Square, r=[xok], w=["xn", "ss"], accum_out=ss[0:M, :])
            self.V(lambda e: e.tensor_scalar(t1[0:M, :], ss[0:M, :], 1.0 / D, 1e-6, ALU.mult, ALU.add), r=["ss"], w=["t1"])
            fw.act(t1[0:M, :], t1[0:M, :], AF.Sqrt, r=["t1"], w=["t1"])
            self.V(lambda e: e.reciprocal(t1[0:M, :], t1[0:M, :]), r=["t1"], w=["t1"])
            self.V(lambda e: e.scalar_tensor_tensor(xo_[0:M, :], xo_[0:M, :], t1[0:M, 0:1], gf[0:M, :], ALU.mult, ALU.mult),
                   r=[xok, "t1", "gf"], w=[xok])
            fw.dma(dst_final, xo_[0:M, :], r=[xok], key=xok)

        def ffn_core(M, hcur, hk, cview, ckey, sample, mid=None):
            for b0 in range(0, NFC, 4):
                nb = min(4, NFC - b0)
                pc, pck = self.pf()
                for q in range(nb):
                    fc = b0 + q
                    for c in range(8):
                        fw.mm(pc[:, q * M:(q + 1) * M], Wc[:, c, fc * 128:(fc + 1) * 128], hcur(c), c == 0, c == 7, r=[hk, "Wc_%d" % c], w=[pck])
                pu, puk = self.pf()
                for q in range(nb):
                    fc = b0 + q
                    for c in range(8):
                        fw.mm(pu[:, q * M:(q + 1) * M], Wu[:, c, fc * 128:(fc + 1) * 128], hcur(c), c == 0, c == 7, r=[hk, "Wu_%d" % c], w=[puk])
                if sample:
                    fw.act(cview[:, b0:b0 + nb, :, 2:6], pc[:, 0:nb * M].rearrange("p (f t q) -> p f q t", f=nb, t=4), AF.Copy, r=[pck], w=[ckey])
                else:
                    fw.act(cview[:, b0:b0 + nb, 2:130], pc[:, 0:nb * M].rearrange("p (f t) -> p f t", f=nb), AF.Copy, r=[pck], w=[ckey])
                a_ = acc[(b0 // 4) % 2]
                ak = "acc%d" % ((b0 // 4) % 2)
                for q in range(nb):
                    fc = b0 + q
                    if sample:
                        c0, c1, c2 = (cview[:, fc, :, s_:s_ + 4] for s_ in range(3))
                        av = a_[:, q, 0:M].rearrange("p (t q) -> p q t", t=4)
                    else:
                        c0, c1, c2 = (cview[:, fc, s_:s_ + 128] for s_ in range(3))
                        av = a_[:, q, :]
                    self.P(lambda e, av=av, c0=c0, fc=fc: e.tensor_scalar(av, c0, cw[:, 0, fc:fc + 1], cw[:, 3, fc:fc + 1], ALU.mult, ALU.add),
                           r=[ckey, "cw"], w=[ak])
                    self.V(lambda e, av=av, c1=c1, fc=fc: e.scalar_tensor_tensor(av, c1, cw[:, 1, fc:fc + 1], av, ALU.mult, ALU.add),
                           r=[ckey, "cw", ak], w=[ak])
                    self.V(lambda e, av=av, c2=c2, fc=fc: e.scalar_tensor_tensor(av, c2, cw[:, 2, fc:fc + 1], av, ALU.mult, ALU.add),
                           r=[ckey, "cw", ak], w=[ak])
                fw.act(a_[:, 0:nb, 0:M], a_[:, 0:nb, 0:M], AF.Gelu, r=[ak], w=[ak])
                self.V(lambda e, a_=a_, pu=pu, nb=nb, b0=b0: e.tensor_tensor(aT[:, b0:b0 + nb, 0:M], a_[:, 0:nb, 0:M],
                                                                       pu[:, 0:nb * M].rearrange("p (f t) -> p f t", f=nb), ALU.mult),
                       r=[ak, puk], w=["aT"])
                if mid is not None and b0 == 4:
                    mid()

        def c_token_major(M, hcur, hk, rows, dsts):
            for g0 in range(0, DFF, 512):
                n = min(512, DFF - g0)
                ps, pk = self.pf()
                for c in range(8):
                    fw.mm(ps[0:M, 0:n], hcur(c), Wc[:, c, g0:g0 + n], c == 0, c == 7, r=[hk, "Wc_%d" % c], w=[pk])
                fw.act(ctok[0:M, g0:g0 + n], ps[0:M, 0:n], AF.Copy, r=[pk], w=["ctok"])
            for (r0, r1), dst in zip(rows, dsts):
                fw.dma(dst, ctok[r0:r1, :], r=["ctok"], key="ctok")

        xt, xk = self.xt[1], "xt1"
        fw.dma(xt[:], (I["xh0"] if NSEG == 1 else self.xh_dram), r=["xh_dram"], w=[xk], key=xk)
        self.norm_hT(xt, xk, 128, hT[:, :, :], "hT0", identb)
        pc, pck = self.pf()
        for fc in range(NFC):
            for c in range(8):
                fw.mm(pc[:, fc * 2:(fc + 1) * 2], Wc[:, c, fc * 128:(fc + 1) * 128], hT[:, c, 126:128], c == 0, c == 7, r=["hT0", "Wc_%d" % c], w=[pck])
        fw.act(cx1[:, :, 0:2], pc[:, 0:2 * NFC].rearrange("p (f t) -> p f t", f=NFC), AF.Copy, r=[pck], w=["cx"])
        def pre(i):
            xt, xk = self.xt[i % 2], "xt%d" % (i % 2)
            fw.dma(xt[:], self.xbuf[i * 128:(i + 1) * 128, :], r=[("xb", i)], w=[xk], key=xk)
            self.norm_hT(xt, xk, 128, hTd[i % 2][:, :, :], "hT%d" % (i % 2), identb)

        pre(0)
        for i in range(NT):
            xt, xk = self.xt[i % 2], "xt%d" % (i % 2)
            hcur = lambda c, i=i: hTd[i % 2][:, c, :]
            hkk = "hT%d" % (i % 2)
            mid = (lambda i=i: pre(i + 1)) if i + 1 < NT else None
            cv_, ckey = cx1, "cx"
            if i > 0:
                self.P(lambda e: e.tensor_copy(acc[0][:, 0, 0:2 * NFC].rearrange("p (f t) -> p f t", f=NFC), cx1[:, :, 128:130]), r=[ckey], w=["acc0"])
                self.P(lambda e: e.tensor_copy(cx1[:, :, 0:2], acc[0][:, 0, 0:2 * NFC].rearrange("p (f t) -> p f t", f=NFC)), r=["acc0"], w=[ckey])
            ffn_core(128, hcur, hkk, cv_, ckey, False, mid)
            if i == NT - 1:
                c_token_major(128, hcur, hkk, [(126, 128)], [O["p_conv"][l]])
            xo_, xok = xo[i % 2], "xo%d" % (i % 2)
            finish(128, xt, xk, xo_, xok, O["yp"][i * 128:(i + 1) * 128, :], self.xbuf[i * 128:(i + 1) * 128, :], ("xb", i))
        if not last and NSEG > 1:
            self.gather_select(xo_[:, :], [xok], D, self.agX_in, self.agX_out, "agX")
            fw.dma(self.xh_dram, xo_[:, :], r=[xok], w=["xh_dram"], key="xhst")

        i = NT
        xt, xk = self.xt[i % 2], "xt%d" % (i % 2)
        fw.dma(xt[0:MS, :], self.xsbuf, r=[("xb", i)], w=[xk], key=xk)
        self.norm_hT(xt, xk, MS, hT[:, :, 0:MS], "hT0", identb)
        hcur = lambda c: hT[:, c, 0:MS]
        fw.dma(cst[0:32, :], I["st_conv"][l], w=["ctok"], key="cst")
        for b0 in range(0, NFC, 4):
            nb = min(4, NFC - b0)
            ps, pk = self.pf()
            for q in range(nb):
                fc = b0 + q
                fw.tr(ps[:, q * 32:(q + 1) * 32], cst[0:32, fc * 128:(fc + 1) * 128], identf[0:32, 0:32], r=["ctok", "identf"], w=[pk])
            fw.act(cxs[:, b0:b0 + nb, :, 0:2], ps[:, 0:nb * 32].rearrange("p (f q j) -> p f q j", f=nb, j=2), AF.Copy, r=[pk], w=["cx"])
        ffn_core(MS, hcur, "hT0", cxs, "cx", True)
        sc_ = O["s_conv"][l].rearrange("(q j) f -> j q f", j=2)
        c_token_major(MS, hcur, "hT0", [(32, 48), (48, 64)], [sc_[0], sc_[1]])
        xo_, xok = xo[i % 2], "xo%d" % (i % 2)
        finish(MS, xt, xk, xo_, xok, O["ys"], self.xsbuf, ("xb", NT))


NSEG = 1


def _consts_shared():
    c = {}
    c["c_ident"] = np.eye(128, dtype=np.float32)
    inv = (10000.0 ** (-np.arange(0, HD, 2, dtype=np.float32) / HD)).astype(np.float32)
    pos_s = (PAST + np.repeat(np.arange(4), NS)).astype(np.float32)
    ang_s = pos_s[:, None] * inv[None, :]
    c["c_coss"] = np.cos(ang_s).astype(np.float32)
    c["c_sins"] = np.sin(ang_s).astype(np.float32)
    s = np.arange(128)[:, None]
    t = np.arange(128)[None, :]
    incl = (s <= t).astype(np.float32)
    strict = (s < t).astype(np.float32)
    c["c_tri"] = np.concatenate([incl * CDEC, strict * CDEC], 1).astype(np.float32)
    c["c_mask2"] = np.concatenate([incl, strict], 1).astype(np.float32)
    c["c_maskL"] = (s > t).astype(np.float32)
    i_ = np.arange(128)[:, None]
    j_ = np.arange(128)[None, :]
    cur = np.where(j_ <= i_, 0.0, NEG)
    prev = np.where(j_ > i_, 0.0, NEG)
    dead = np.full((128, 128), NEG)
    c["c_amask"] = np.concatenate([cur, prev, prev, cur, cur, dead], 1).astype(np.float32)
    c["_am_first"] = np.concatenate([cur, dead], 1).astype(np.float32)
    c["_am_mid"] = np.concatenate([cur, prev], 1).astype(np.float32)
    tt = (np.arange(32) % 4)[:, None]
    ia = np.arange(128)[None, :]
    ma = np.where(ia <= 124 + tt, 0.0, NEG)
    rb = np.arange(4)[None, :]
    mb = np.where(rb > tt, 0.0, NEG)
    c["c_smask"] = np.concatenate([ma, mb], 1).astype(np.float32)
    last = np.zeros((128, 1), np.float32)
    last[127, 0] = 1.0
    c["c_last"] = last
    c["_inv"] = inv
    return c


def _rope_tab(pos, inv):
    ang = pos.astype(np.float32)[:, None] * inv[None, :]
    return np.cos(ang).astype(np.float32), np.sin(ang).astype(np.float32)


_CACHE = {}
TAPS = False
TAP_OUT = {}


def kernel(**inp):
    inp = {k: np.asarray(v) for k, v in inp.items()}
    xp_all = inp["x_prompt"].astype(np.float32)
    B, SEQ_, _ = xp_all.shape
    TPC = SEQ_ // NSEG
    if TPC not in _CACHE:
        b_ = Builder(TPC, taps=TAPS)
        _CACHE[TPC] = (b_.build(), b_.tapnames)
    nc, tapnames = _CACHE[TPC]
    consts = _consts_shared()
    inv = consts.pop("_inv")
    am_first, am_mid = consts.pop("_am_first"), consts.pop("_am_mid")
    wnames = ["norm_mix_g", "w_in", "rwkv_mu", "rwkv_w0", "rwkv_w2", "rwkv_a0", "rwkv_a2", "rwkv_g2", "rwkv_k_k",
              "rwkv_k_a", "rwkv_ln_g", "rwkv_ln_b", "attn_sinks", "w_br_rwkv", "w_br_attn", "w_out", "norm_ffn_g",
              "ffn_w_in", "ffn_conv_w", "ffn_conv_b", "ffn_w_down", "norm_final_g"]
    shared = {n: np.ascontiguousarray(inp[n], dtype=np.float32) for n in wnames}
    shared["rwkv_r_k"] = np.ascontiguousarray(inp["rwkv_r_k"], dtype=np.float32).reshape(2, RD)
    shared.update(consts)
    in_maps = []
    ncores = 8
    for c in range(ncores):
        b, seg = (c // NSEG) % B, c % NSEG
        sl = slice(c * NS, (c + 1) * NS)
        m = dict(shared)
        t0 = seg * TPC
        m["xp"] = np.ascontiguousarray(xp_all[b, t0:t0 + TPC])
        m["xh0"] = np.ascontiguousarray(xp_all[b, t0 - 128:t0]) if seg > 0 else np.zeros((128, D), np.float32)
        m["c_cosp"], m["c_sinp"] = _rope_tab(t0 + np.arange(TPC), inv)
        m["c_cosh"], m["c_sinh"] = _rope_tab(np.maximum(t0 - 128 + np.arange(128), 0), inv)
        m["c_amask0"] = am_mid if seg > 0 else am_first
        sel = np.zeros((128, 8), np.float32)
        if seg > 0:
            sel[:, c - 1] = 1.0
        m["c_sel"] = sel
        m["xs"] = np.ascontiguousarray(inp["x_sample"][sl].transpose(1, 0, 2).reshape(MS, D))
        m["st_shift"] = np.ascontiguousarray(inp["state_rwkv_shift"][:, sl])
        m["st_wkv"] = np.ascontiguousarray(inp["state_rwkv_wkv"][:, sl]).reshape(2, 128, 4096)
        m["ck"] = np.ascontiguousarray(inp["cache_swa_k"][:, sl]).reshape(2, NS, 128, 128)
        m["cv"] = np.ascontiguousarray(inp["cache_swa_v"][:, sl]).reshape(2, NS, 128, 128)
        m["st_conv"] = np.ascontiguousarray(inp["state_ffn_conv"][:, sl]).reshape(2, 2 * NS, DFF)
        in_maps.append(m)
    res = run_bass_kernel_spmd(nc, in_maps, core_ids=list(range(ncores)))
    R = res.results
    for tn in tapnames:
        TAP_OUT[tn] = [np.asarray(R[c][tn]) for c in range(ncores)]
    f = np.float32
    lastc = [b * NSEG + NSEG - 1 for b in range(B)]
    y_prompt = np.stack([np.concatenate([R[b * NSEG + sg]["yp"] for sg in range(NSEG)], 0) for b in range(B)]).astype(f)
    y_sample = np.concatenate([R[c]["ys"].reshape(4, NS, D).transpose(1, 0, 2) for c in range(ncores)], 0).astype(f)
    p_shift = np.stack([R[c]["p_shift"] for c in lastc], 1).astype(f)
    p_wkv = np.stack([R[c]["p_wkv"] for c in lastc], 1).astype(f)
    p_k = np.stack([R[c]["p_k"] for c in lastc], 1).reshape(2, B, 128, 2, 64).astype(f)
    p_v = np.stack([R[c]["p_v"] for c in lastc], 1).reshape(2, B, 128, 2, 64).astype(f)
    p_conv = np.stack([R[c]["p_conv"] for c in lastc], 1).astype(f)
    s_shift = np.concatenate([R[c]["s_shift"] for c in range(ncores)], 1).astype(f)
    s_wkv = np.concatenate([R[c]["s_wkv"].reshape(2, NS, NH, 64, 64) for c in range(ncores)], 1).astype(f)
    s_k = np.concatenate([R[c]["s_k"].reshape(2, NS, 128, 2, 64) for c in range(ncores)], 1).astype(f)
    s_v = np.concatenate([R[c]["s_v"].reshape(2, NS, 128, 2, 64) for c in range(ncores)], 1).astype(f)
    s_conv = np.concatenate([R[c]["s_conv"].reshape(2, NS, 2, DFF) for c in range(ncores)], 1).astype(f)
    return (y_prompt, y_sample, p_shift, p_wkv, p_k, p_v, p_conv, s_shift, s_wkv, s_k, s_v, s_conv)
```

```python
import math
from contextlib import ExitStack

import numpy as np
import concourse.bass as bass
import concourse.mybir as mybir
from concourse.bass_utils import run_bass_kernel_spmd

F32 = mybir.dt.float32
BF = mybir.dt.bfloat16
AF = mybir.ActivationFunctionType
ALU = mybir.AluOpType
AX = mybir.AxisListType

ENGS = ["sp", "pe", "act", "dve", "pool"]
DEBUG_WHERE = True

D = 1024
HD = 64
NH = 8
RD = 512
RP = 1792
INP = 4608
DFF = 2816
NFC = 22
NS = 16
MS = 64
PAST = 16384
CDEC = -math.exp(-0.5)
NEG = -30000.0


class FW:
    def __init__(self, nc, es):
        self.nc = nc
        self.es = es
        self.ops = {e: [] for e in ENGS}
        self.lastw = {}
        self.readers = {}
        self.dma_count = {}
        self.inc = {}

    def sb(self, name, shape, dt=F32):
        return self.es.enter_context(self.nc.sbuf_tensor(name, list(shape), dt))

    def ps(self, name, shape, dt=F32):
        return self.es.enter_context(self.nc.psum_tensor(name, list(shape), dt))

    def capture(self, f):
        self.cap = []
        f()
        log, self.cap = self.cap, None
        return log

    def replay(self, logs, chunk=2):
        logs = [list(lg) for lg in logs if lg]
        if not logs:
            return
        mn = min(len(lg) for lg in logs)
        per = [max(1, int(round(chunk * len(lg) / mn))) for lg in logs]
        pos = [0] * len(logs)
        while any(p < len(lg) for p, lg in zip(pos, logs)):
            for k, lg in enumerate(logs):
                for _ in range(per[k]):
                    if pos[k] < len(lg):
                        self.op(*lg[pos[k]])
                        pos[k] += 1

    def op(self, eng, fn, r=(), w=(), dma=None):
        if getattr(self, "cap", None) is not None:
            self.cap.append((eng, fn, tuple(r), tuple(w), dma))
            return
        ops = self.ops[eng]
        idx = len(ops)
        deps = set()
        pr = [k for k in r if isinstance(k, str) and k[:2] in ("ps", "pb") and k[2:].isdigit()]
        if pr:
            r = [k for k in r if k not in pr]
            w = list(w) + pr
        for k in r:
            t = self.lastw.get(k)
            if t is not None:
                deps.add(t)
        for k in w:
            t = self.lastw.get(k)
            if t is not None:
                deps.add(t)
            for t2 in self.readers.get(k, {}).values():
                deps.add(t2)
        if dma is not None:
            c = self.dma_count.get(dma, 0) + 1
            self.dma_count[dma] = c
            tok = ("d", dma, c)
        else:
            tok = ("c", eng, idx)
        if eng == "pe":
            deps = {d for d in deps if not (d[0] == "c" and d[1] == "pe")}
        deps.discard(tok)
        rec = dict(fn=fn, deps=deps, tok=tok, signal=False)
        if DEBUG_WHERE:
            import sys as _s
            f_ = _s._getframe(1)
            wh = []
            while f_ is not None and len(wh) < 4:
                wh.append(f_.f_lineno)
                f_ = f_.f_back
            rec["where"] = wh
        ops.append(rec)
        for d in deps:
            if d[0] == "c":
                self.ops[d[1]][d[2]]["signal"] = True
        for k in w:
            self.lastw[k] = tok
            self.readers[k] = {}
        for k in r:
            rk = ("d", tok[1]) if tok[0] == "d" else tok[1]
            self.readers.setdefault(k, {})[rk] = tok
        return tok

    def fence(self):
        toks = set()
        for e in ENGS:
            for rec in reversed(self.ops[e]):
                if rec["tok"][0] == "c" and rec["fn"] is not None:
                    toks.add(rec["tok"])
                    rec["signal"] = True
                    break
        for k, c in self.dma_count.items():
            toks.add(("d", k, c))
        for e in ENGS:
            self.ops[e].append(dict(fn=None, deps=set(toks), tok=("c", e, len(self.ops[e])), signal=False))

    def dma(self, out, in_, r=(), w=(), key=None, eng="sp", **kw):
        self.op(eng, lambda e: e.dma_start(out=out, in_=in_, **kw), r=r, w=w, dma=key)

    def mm(self, out, lhsT, rhs, start, stop, r=(), w=()):
        self.op("pe", lambda e: e.matmul(out, lhsT, rhs, start=start, stop=stop), r=r, w=w)

    def tr(self, out, in_, ident, r=(), w=()):
        self.op("pe", lambda e: e.transpose(out, in_, ident), r=r, w=w)

    def act(self, out, in_, func, r=(), w=(), **kw):
        self.op("act", lambda e: e.activation(out, in_, func, **kw), r=r, w=w)

    def emit(self):
        nc = self.nc
        sems = {e: self.es.enter_context(nc.semaphore("s_" + e)) for e in ENGS}
        dsems = {}
        for i, k in enumerate(self.dma_count):
            dsems[k] = self.es.enter_context(nc.semaphore("d%d" % i))
        for e in ENGS:
            c = 0
            for rec in self.ops[e]:
                if rec["signal"] and rec["tok"][0] == "c":
                    c += 1
                rec["sigval"] = c
        final_counts = dict(self.dma_count)

        def run(engname, eng):
            waited = {}
            for rec in self.ops[engname]:
                need = {}
                for d in rec["deps"]:
                    if d[0] == "c":
                        s = ("c", d[1])
                        v = self.ops[d[1]][d[2]]["sigval"]
                    else:
                        s = ("d", d[1])
                        v = self.inc.get(d[1], 16) * d[2]
                    if need.get(s, 0) < v:
                        need[s] = v
                for s, v in need.items():
                    if waited.get(s, 0) >= v:
                        continue
                    waited[s] = v
                    eng.wait_ge(sems[s[1]] if s[0] == "c" else dsems[s[1]], v)
                if rec["fn"] is None:
                    continue
                try:
                    ins = rec["fn"](eng)
                except Exception:
                    print("EMIT FAILURE at lines", rec.get("where"), "engine", engname)
                    raise
                if rec["tok"][0] == "d":
                    ins.then_inc(dsems[rec["tok"][1]], self.inc.get(rec["tok"][1], 16))
                elif rec["signal"]:
                    ins.then_inc(sems[engname], 1)
            if engname == "sp":
                for k, c in final_counts.items():
                    v = self.inc.get(k, 16) * c
                    if waited.get(("d", k), 0) < v:
                        eng.wait_ge(dsems[k], v)

        with nc.Block() as block:
            @block.sync
            def _(e):
                run("sp", e)

            @block.tensor
            def _(e):
                run("pe", e)

            @block.scalar
            def _(e):
                run("act", e)

            @block.vector
            def _(e):
                run("dve", e)

            @block.gpsimd
            def _(e):
                run("pool", e)


def bc3(ap2, n):
    s = list(ap2.shape)
    return ap2.unsqueeze(2).to_broadcast([s[0], s[1], n])


def h3(ap2, h=NH):
    return ap2.rearrange("p (h d) -> p h d", h=h)


class Builder:
    def __init__(self, TP, taps=False):
        self.TP = TP
        self.NT = TP // 128
        self.taps = taps
        self.nc = bass.Bass("TRN2", target_bir_lowering=False)
        self.I = {}
        self.O = {}
        self.psi = 0
        self.pbi = 0
        self.tapnames = []
        self.pool = None
        self.pcnt = {}

    def din(self, n, s):
        self.I[n] = self.nc.dram_tensor(n, list(s), F32, kind="ExternalInput").ap()

    def dout(self, n, s):
        self.O[n] = self.nc.dram_tensor(n, list(s), F32, kind="ExternalOutput").ap()

    def declare(self):
        TP = self.TP
        for n, s in [("xp", (TP, D)), ("xs", (MS, D)), ("st_shift", (2, NS, RP)), ("st_wkv", (2, 128, 4096)),
                     ("ck", (2, NS, 128, 128)), ("cv", (2, NS, 128, 128)), ("st_conv", (2, 2 * NS, DFF)),
                     ("norm_mix_g", (2, D)), ("w_in", (2, D, INP)), ("rwkv_mu", (2, RP)), ("rwkv_w0", (2, RD)),
                     ("rwkv_w2", (2, 64, RD)), ("rwkv_a0", (2, RD)), ("rwkv_a2", (2, 64, RD)),
                     ("rwkv_g2", (2, 128, RD)), ("rwkv_k_k", (2, RD)), ("rwkv_k_a", (2, RD)),
                     ("rwkv_r_k", (2, RD)), ("rwkv_ln_g", (2, RD)), ("rwkv_ln_b", (2, RD)),
                     ("attn_sinks", (2, NH)), ("w_br_rwkv", (2, RD, D)), ("w_br_attn", (2, RD, D)),
                     ("w_out", (2, D, D)), ("norm_ffn_g", (2, D)), ("ffn_w_in", (2, D, 2 * DFF)),
                     ("ffn_conv_w", (2, 3, DFF)), ("ffn_conv_b", (2, DFF)), ("ffn_w_down", (2, DFF, D)),
                     ("norm_final_g", (D,)),
                     ("c_ident", (128, 128)), ("c_cosp", (TP, 32)), ("c_sinp", (TP, 32)),
                     ("c_coss", (MS, 32)), ("c_sins", (MS, 32)), ("c_tri", (128, 256)),
                     ("c_mask2", (128, 256)), ("c_maskL", (128, 128)), ("c_amask", (128, 768)),
                     ("c_smask", (32, 132)), ("c_last", (128, 1)),
                     ("xh0", (128, D)), ("c_cosh", (128, 32)), ("c_sinh", (128, 32)), ("c_amask0", (128, 256)), ("c_sel", (128, 8))]:
            self.din(n, s)
        for n, s in [("yp", (TP, D)), ("ys", (MS, D)), ("p_shift", (2, RP)), ("p_wkv", (2, NH, 64, 64)),
                     ("p_k", (2, 128, 128)), ("p_v", (2, 128, 128)), ("p_conv", (2, 2, DFF)),
                     ("s_shift", (2, NS, RP)), ("s_wkv", (2, 128, 4096)), ("s_k", (2, NS, 128, 128)),
                     ("s_v", (2, NS, 128, 128)), ("s_conv", (2, 2 * NS, DFF))]:
            self.dout(n, s)
        nc = self.nc
        self.xbuf = nc.dram_tensor("xbuf", [TP, D], F32).ap()
        self.xsbuf = nc.dram_tensor("xsbuf", [MS, D], F32).ap()
        self.mrbuf = nc.dram_tensor("mrbuf", [self.NT + 1, 128, 1024], BF).ap()
        self.xh_dram = nc.dram_tensor("xh_dram", [128, D], F32).ap()
        self.sq = nc.dram_tensor("sq", [6, MS, RD], F32).ap()
        self.sy = nc.dram_tensor("sy", [MS, RD], F32).ap()

    def alloc(self, name, shape, dt=F32):
        shape = list(shape)
        n = 1
        for d_ in shape[1:]:
            n *= d_
        nbytes = n * (4 if dt == F32 else 2)
        nw = (nbytes + 31) // 32 * 8
        off = self.aoff
        self.aoff += nw
        self.apeak = max(self.apeak, self.aoff)
        assert self.aoff <= self.ASZ, "SBUF arena overflow: %s needs %d words (limit %d)" % (name, self.aoff, self.ASZ)
        ap = self.arena[0:shape[0], off:off + nw]
        if dt != F32:
            ap = ap.bitcast(dt)
        ap = ap[:, 0:n]
        if len(shape) > 2:
            names = ["d%d" % i for i in range(len(shape) - 1)]
            pat = "p (%s) -> p %s" % (" ".join(names), " ".join(names))
            ap = ap.rearrange(pat, **{names[i]: shape[i + 1] for i in range(len(names))})
        return ap

    def release(self, mark):
        self.fw.fence()
        self.aoff = mark

    def pf(self):
        ids = {None: [0, 1, 2, 3, 4, 5], 0: [0, 1, 2], 1: [3, 4, 5]}[self.pool]
        c = self.pcnt.setdefault(("f", self.pool), 0)
        self.pcnt[("f", self.pool)] = c + 1
        k = ids[c % len(ids)]
        return self.PS[k], "ps%d" % k

    def pb(self):
        ids = {None: [0, 1], 0: [0], 1: [1]}[self.pool]
        c = self.pcnt.setdefault(("b", self.pool), 0)
        self.pcnt[("b", self.pool)] = c + 1
        k = ids[c % len(ids)]
        return self.PBK[k], "pb%d" % k

    def tap(self, name, ap, rkeys, dt=F32):
        if not self.taps:
            return
        shp = list(ap.shape)
        t = self.nc.dram_tensor("tap_" + name, shp, dt, kind="ExternalOutput").ap()
        self.tapnames.append("tap_" + name)
        self.fw.dma(t, ap, r=rkeys, key="tap_" + name)

    def V(self, fn, r=(), w=()):
        self.fw.op("dve", fn, r, w)

    def P(self, fn, r=(), w=()):
        self.fw.op("pool", fn, r, w)

    def col_load(self, dst, dkey, vec, n):
        fw = self.fw
        st = self.cstage
        fw.dma(st[0:n, :], vec.rearrange("(c p) -> c p", p=128), w=["cstage"], key="cstage")
        ps, pk = self.pf()
        fw.tr(ps[:, 0:n], st[0:n, :], self.identf[0:n, 0:n], r=["cstage", "identf"], w=[pk])
        fw.act(dst, ps[:, 0:n], AF.Copy, r=[pk], w=[dkey])

    def gather_select(self, src_ap, src_keys, n, ag_in, ag_out, name):
        fw = self.fw
        fw.dma(ag_in, src_ap, r=src_keys, w=[name + "_in"], key=name + "_st")
        self.gi = getattr(self, "gi", 0)
        ck = name + "_cc"
        fw.inc[ck] = 1
        fw.op("pool", lambda e: e.collective_compute("AllGather", ALU.bypass, replica_groups=[list(range(8))], ins=[ag_in], outs=[ag_out]),
              r=[name + "_in"], w=[name + "_out"], dma=ck)
        for r_ in range(8):
            st, sk = self.xt[r_ % 2], "xt%d" % (r_ % 2)
            fw.dma(st[:, 0:n], ag_out[r_ * 128:(r_ + 1) * 128, :], r=[name + "_out"], w=[sk], key=sk)
            if r_ == 0:
                self.V(lambda e, st=st: e.tensor_scalar(src_ap, st[:, 0:n], self.sel[:, 0:1], None, ALU.mult), r=[sk, "sel"], w=src_keys)
            else:
                self.V(lambda e, st=st, r_=r_: e.scalar_tensor_tensor(src_ap, st[:, 0:n], self.sel[:, r_:r_ + 1], src_ap, ALU.mult, ALU.add),
                       r=[sk, "sel"] + list(src_keys), w=src_keys)

    def bcast_load(self, dst, dkey, vec):
        self.fw.dma(dst, vec.partition_broadcast(dst.shape[0]), w=[dkey], key=dkey)

    def prep_w(self, nchunks, ncols, src, dst, dkey, mode, scale=None, mul=None, mulkey=None, sview=None):
        fw = self.fw
        for c in range(nchunks):
            for s0 in range(0, ncols, 2048):
                n = min(2048, ncols - s0)
                k = self.wst_i % 2
                self.wst_i += 1
                st = self.wstage[k]
                sk = "wst%d" % k
                fw.dma(st[:, 0:n], src(c, s0, n), w=[sk], key=sk)
                o = dst(c, s0, n)
                dk = dkey(c)
                if sview is not None:
                    sv_ = sview(st[:, 0:n])
                    sc = scale(c)
                    self.V(lambda eg, o=o, sv_=sv_, sc=sc: eg.tensor_scalar(o, sv_, sc, None, ALU.mult), r=[sk, "gcol"], w=[dk])
                    continue
                if mode == "plain":
                    e = ["dve", "pool", "act"][self.wst_i % 3]
                    if e == "act":
                        fw.act(o, st[:, 0:n], AF.Copy, r=[sk], w=[dk])
                    else:
                        fw.op(e, lambda eg, o=o, st=st, n=n: eg.tensor_copy(o, st[:, 0:n]), r=[sk], w=[dk])
                elif mode == "col":
                    sc = scale(c)
                    e = ["dve", "pool"][self.wst_i % 2]
                    fw.op(e, lambda eg, o=o, st=st, n=n, sc=sc: eg.tensor_scalar(o, st[:, 0:n], sc, None, ALU.mult),
                          r=[sk, "gcol"], w=[dk])
                else:
                    sc = scale(c)
                    m = mul(s0, n)
                    self.V(lambda eg, o=o, st=st, n=n, sc=sc, m=m: eg.scalar_tensor_tensor(
                        o, st[:, 0:n], sc, m, ALU.mult, ALU.mult), r=[sk, "gcol", mulkey], w=[dk])

    def norm_hT(self, xt, xk, M, hdst, hkey, identb):
        self.norm_a(xt, xk, M)
        self.norm_b(M, hdst, hkey, identb)

    def norm_a(self, xt, xk, M):
        fw = self.fw
        xn, ss, t1 = self.xn, self.ss, self.t1
        fw.act(xn[0:M, :], xt[0:M, :], AF.Square, r=[xk], w=["xn", "ss"], accum_out=ss[0:M, :])
        self.V(lambda e: e.tensor_scalar(t1[0:M, :], ss[0:M, :], 1.0 / D, 1e-6, ALU.mult, ALU.add), r=["ss"], w=["t1"])
        fw.act(t1[0:M, :], t1[0:M, :], AF.Sqrt, r=["t1"], w=["t1"])
        self.V(lambda e: e.reciprocal(t1[0:M, :], t1[0:M, :]), r=["t1"], w=["t1"])
        self.V(lambda e: e.tensor_scalar(xn[0:M, :], xt[0:M, :], t1[0:M, 0:1], None, ALU.mult), r=[xk, "t1"], w=["xn"])

    def norm_b(self, M, hdst, hkey, identb):
        fw = self.fw
        xn = self.xn
        pbk, pk = self.pb()
        for c in range(8):
            fw.tr(pbk[:, c * M:(c + 1) * M], xn[0:M, c * 128:(c + 1) * 128], identb[0:M, 0:M], r=["xn", "identb"], w=[pk])
        fw.act(hdst, pbk[:, 0:8 * M].rearrange("p (c t) -> p c t", c=8), AF.Copy, r=[pk], w=[hkey])

    def build(self):
        self.declare()
        nc = self.nc
        with ExitStack() as es:
            self.fw = fw = FW(nc, es)
            self.PS = [fw.ps("ps%d" % i, [128, 512], F32) for i in range(6)]
            self.PBK = [fw.ps("pb%d" % i, [128, 1024], BF) for i in range(2)]
            self.ASZ = 52224
            self.arena = fw.sb("arena", [128, self.ASZ])
            self.aoff = 0
            self.apeak = 0
            self.identf = self.alloc("identf", [128, 128])
            self.identb = self.alloc("identb", [128, 128], BF)
            self.cstage = self.alloc("cstage", [32, 128])
            self.wst_i = 0
            self.xn = self.alloc("xn", [128, D], BF)
            self.ss = self.alloc("ss", [128, 1])
            self.t1 = self.alloc("t1", [128, 1])
            self.gcol = self.alloc("gcol", [128, 8])
            self.xt = [self.alloc("xt%d" % i, [128, D]) for i in range(2)]
            self.sel = self.alloc("sel", [128, 8])
            fw.dma(self.sel[:], self.I["c_sel"], w=["sel"], key="sel")
            fw.dma(self.identf[:], self.I["c_ident"], w=["identf"], key="identf")
            self.V(lambda e: e.tensor_copy(self.identb[:], self.identf[:]), r=["identf"], w=["identb"])
            for l in range(2):
                for p_ in (self.pass_rwkv, self.pass_attn, self.pass_ffn):
                    mk_ = self.aoff
                    p_(l, None)
                    self.release(mk_)
            print("arena peak words", self.apeak, "of", self.ASZ)
            fw.emit()
        return nc

    def sbl(self, es2, name, shape, dt=F32):
        return self.alloc(name, shape, dt)

    def xsrc(self, l, i):
        if i < self.NT:
            src = self.I["xp"] if l == 0 else self.xbuf
            return src[i * 128:(i + 1) * 128, :], ("xb", i)
        src = self.I["xs"] if l == 0 else self.xsbuf
        return src, ("xb", i)

    def pass_rwkv(self, l, es2):
        fw, I, O, NT = self.fw, self.I, self.O, self.NT
        sbl = lambda n, s, dt=F32: self.sbl(es2, "r%d_" % l + n, s, dt)
        identb, identf = self.identb, self.identf
        W1 = sbl("W1", [128, 8, RP], BF)
        W2 = sbl("W2", [128, 8, RP], BF)
        Wg = sbl("Wg", [128, 8, D], BF)
        Wr = sbl("Wr", [128, 4, D], BF)
        lw2 = sbl("lw2", [128, RD], BF)
        lg2 = sbl("lg2", [128, RD], BF)
        bcs = {}
        for n in ["rwkv_w0", "rwkv_a0", "rwkv_k_k", "rwkv_k_a", "rwkv_r_k", "rwkv_ln_g", "rwkv_ln_b"]:
            bcs[n] = sbl(n, [128, RD])
            self.bcast_load(bcs[n][:], n + "_bc", I[n][l])
        mucol = sbl("mucol", [128, 2])
        tri = sbl("tri", [128, 256])
        mask2 = sbl("mask2", [128, 256])
        maskL = sbl("maskL", [128, 128])
        clast = sbl("clast", [128, 1])
        fw.dma(tri[:], I["c_tri"], w=["tri"], key="tri")
        fw.dma(mask2[:], I["c_mask2"], w=["mask2"], key="mask2")
        fw.dma(maskL[:], I["c_maskL"], w=["maskL"], key="maskL")
        fw.dma(clast[:], I["c_last"], w=["clast"], key="clast")
        self.col_load(self.gcol[:], "gcol", I["norm_mix_g"][l], 8)
        self.col_load(mucol[:], "mucol", I["rwkv_mu"][l, 1536:1792], 2)
        m0 = self.aoff
        self.wstage = [sbl("wst%d" % i_, [128, 2048]) for i_ in range(2)]
        mu_bc = sbl("mu_bc", [128, RP])
        omm_bc = sbl("omm_bc", [128, RP])
        self.bcast_load(mu_bc[:], "mu_bc", I["rwkv_mu"][l])
        self.V(lambda e: e.tensor_scalar(omm_bc[:], mu_bc[:], -1.0, 1.0, ALU.mult, ALU.add), r=["mu_bc"], w=["omm_bc"])
        win = I["w_in"][l]
        gsc = lambda c: self.gcol[:, c:c + 1]
        self.prep_w(8, RP, lambda c, s0, n: win[c * 128:(c + 1) * 128, s0:s0 + n],
                    lambda c, s0, n: W1[:, c, s0:s0 + n], lambda c: "W1_%d" % c, "colmul", gsc,
                    lambda s0, n: omm_bc[:, s0:s0 + n], "omm_bc")
        self.prep_w(8, RP, lambda c, s0, n: win[c * 128:(c + 1) * 128, s0:s0 + n],
                    lambda c, s0, n: W2[:, c, s0:s0 + n], lambda c: "W2_%d" % c, "colmul", gsc,
                    lambda s0, n: mu_bc[:, s0:s0 + n], "mu_bc")
        self.prep_w(8, D, lambda c, s0, n: win[c * 128:(c + 1) * 128, 2560 + s0:2560 + s0 + n],
                    lambda c, s0, n: Wg[:, c, s0:s0 + n], lambda c: "Wg_%d" % c, "col", gsc)
        wbr = I["w_br_rwkv"][l]
        self.prep_w(4, D, lambda c, s0, n: wbr[c * 128:(c + 1) * 128, s0:s0 + n],
                    lambda c, s0, n: Wr[:, c, s0:s0 + n], lambda c: "Wr_%d" % c, "plain")
        for (nm, p0, dk_) in [("rwkv_w2", 0, "lw2a"), ("rwkv_a2", 64, "lw2b")]:
            k = self.wst_i % 2
            self.wst_i += 1
            wsk = self.wstage[k]
            fw.dma(wsk[p0:p0 + 64, 0:RD], I[nm][l], w=["wst%d" % k], key="wst%d" % k)
            self.P(lambda e, wsk=wsk, p0=p0: e.tensor_copy(lw2[p0:p0 + 64, :], wsk[p0:p0 + 64, 0:RD]), r=["wst%d" % k], w=[dk_])
        self.prep_w(1, RD, lambda c, s0, n: I["rwkv_g2"][l], lambda c, s0, n: lg2[:, :], lambda c: "lg2", "plain")
        WK1 = ["W1_%d" % c for c in range(8)]
        WK2 = ["W2_%d" % c for c in range(8)]
        self.release(m0)
        class NSP:
            pass
        zr, zk = sbl("zr", [128, RD]), sbl("zk", [128, RD])
        lact = sbl("lact", [128, 128], BF)
        T = [sbl("tmp%d" % i_, [128, RD]) for i_ in range(8)]
        sm = sbl("sm", [128, 64])
        orT = sbl("orT", [128, 4, 128], BF)
        sgr = sbl("sgr", [128, 8, 128], BF)
        mrT0_ = sbl("mrT0", [128, 8, 128], BF)
        mrT = [mrT0_, mrT0_]
        TP_ = [sbl("tpost%d" % i_, [128, RD]) for i_ in range(2)]
        m1 = self.aoff
        NRB = 9864

        def mkrec(k):
            R = NSP()
            rb = sbl("RB%d" % k, [128, NRB], BF)
            rf = sbl("RF%d" % k, [128, 528])
            R.rb, R.rf, R.k = rb, rf, k
            R.RKT = rb[:, 0:1024].rearrange("p (j a t) -> p j a t", j=4, a=2)
            R.G4 = [rb[:, 1024 + j * 1280:1024 + (j + 1) * 1280].rearrange("p (h c) -> p h c", h=2) for j in range(4)]
            R.ZF = [rb[:, 6144 + j * 256:6144 + (j + 1) * 256].rearrange("p (h c) -> p h c", h=2) for j in range(4)]
            R.vb, R.ktt, R.bnt = rb[:, 7168:7680], rb[:, 7680:8192], rb[:, 8192:8704]
            R.sgT = rb[:, 8704:8832]
            R.hT = rb[:, 8832:9864].rearrange("p (c t) -> p c t", c=8)
            R.zv, R.WC, R.bon = rf[:, 0:512], rf[:, 512:516], rf[:, 516:524]
            R.K = (lambda k_: (lambda n: "%s#%d" % (n, k_)))(k)
            return R
        R0 = mkrec(0)
        U0b = [sbl("U0b%d" % j, [128, 2, 64], BF) for j in range(4)]
        Ub = sbl("Ub", [128, RD], BF)
        Nst = sbl("Nst", [128, 4, 128])
        Nb = sbl("Nb", [128, 4, 128], BF)
        self.V(lambda e: e.memset(Nst[:], 0.0), w=["Nst"])
        self.V(lambda e: e.memset(Nb[:], 0.0), w=["Nb"])
        m2 = self.aoff
        rt, kat = sbl("rt", [128, RD], BF), sbl("kat", [128, RD], BF)
        KT = sbl("KT", [128, 4, 128], BF)
        BT = sbl("BT", [128, 4, 128], BF)
        for j in range(4):
            self.P(lambda e, j=j: e.tensor_copy(R0.G4[j][:, :, 512:640], identb[:, :].unsqueeze(1).to_broadcast([128, 2, 128])),
                   r=["identb"], w=["G4_%d" % j])
        EZ = [[sbl("EZ%d_%d" % (j, a), [128, 2, 2, 128], BF) for a in range(2)] for j in range(4)]
        FFa = [sbl("FFa%d" % a, [128, 4, 2, 128], BF) for a in range(2)]
        FF = [[FFa[a][:, j] for a in range(2)] for j in range(4)]

        def tok_proj(M, hcur, hprev, hk, g0, dstkey):
            ps, pk = self.pf()
            n = 0
            for c in range(8):
                fw.mm(ps[0:M, :], hcur(c), W1[:, c, g0:g0 + 512], n == 0, False, r=[hk, WK1[c]], w=[pk])
                n += 1
            for c in range(8):
                fw.mm(ps[0:M, :], hprev(c), W2[:, c, g0:g0 + 512], False, c == 7, r=[hk, WK2[c]], w=[pk])
            return ps, pk

        def feat_proj(M, hcur, hprev, hk, g0):
            ps, pk = self.pf()
            for c in range(8):
                fw.mm(ps[:, 0:M], W1[:, c, g0:g0 + 128], hcur(c), c == 0, False, r=[hk, WK1[c]], w=[pk])
            for c in range(8):
                fw.mm(ps[:, 0:M], W2[:, c, g0:g0 + 128], hprev(c), False, c == 7, r=[hk, WK2[c]], w=[pk])
            return ps, pk

        def raw_last(hl, hk, M, dst):
            for gi, g0 in enumerate(range(0, RP, 512)):
                n = min(512, RP - g0)
                ps, pk = self.pf()
                for c in range(8):
                    fw.mm(ps[0:M, 0:n], hl(c), W1[:, c, g0:g0 + n], c == 0, False, r=[hk, WK1[c]], w=[pk])
                for c in range(8):
                    fw.mm(ps[0:M, 0:n], hl(c), W2[:, c, g0:g0 + n], False, c == 7, r=[hk, WK2[c]], w=[pk])
                fw.act(T[gi][0:M, 0:n], ps[0:M, 0:n], AF.Copy, r=[pk], w=["T%d" % gi])
                fw.dma(dst[:, g0:g0 + n], T[gi][0:M, 0:n], r=["T%d" % gi], key="zl%d" % gi)

        def prep(M, sample, R):
            K = R.K
            w0, a0 = bcs["rwkv_w0"], bcs["rwkv_a0"]
            kkb, kab, rkb = bcs["rwkv_k_k"], bcs["rwkv_k_a"], bcs["rwkv_r_k"]
            pw, pwk = self.pf()
            fw.mm(pw[0:M, :], lact[0:64, 0:M], lw2[0:64, :], True, True, r=["lact", "lw2a"], w=[pwk])
            pa, pak = self.pf()
            fw.mm(pa[0:M, :], lact[64:128, 0:M], lw2[64:128, :], True, True, r=["lact", "lw2b"], w=[pak])
            sg, a_, kk, t3, kf, be = T[0], T[1], T[2], T[3], T[4], T[5]
            self.V(lambda e: e.tensor_tensor(sg[0:M, :], pw[0:M, :], w0[0:M, :], ALU.add), r=[pwk, "rwkv_w0_bc"], w=["T0"])
            fw.act(sg[0:M, :], sg[0:M, :], AF.Sigmoid, r=["T0"], w=["T0"])
            self.V(lambda e: e.tensor_tensor(a_[0:M, :], pa[0:M, :], a0[0:M, :], ALU.add), r=[pak, "rwkv_a0_bc"], w=["T1"])
            fw.act(a_[0:M, :], a_[0:M, :], AF.Sigmoid, r=["T1"], w=["T1"])
            self.P(lambda e: e.tensor_tensor(kk[0:M, :], zk[0:M, :], kkb[0:M, :], ALU.mult), r=["zk", "rwkv_k_k_bc"], w=["T2"])
            self.P(lambda e: e.tensor_tensor(t3[0:M, :], kk[0:M, :], kk[0:M, :], ALU.mult), r=["T2"], w=["T3"])
            self.V(lambda e: e.tensor_reduce(sm[0:M, 0:8], h3(t3[0:M, :]), AX.X, ALU.add), r=["T3"], w=["sm0"])
            fw.act(sm[0:M, 0:8], sm[0:M, 0:8], AF.Sqrt, r=["sm0"], w=["sm0"])
            self.V(lambda e: e.tensor_scalar(sm[0:M, 0:8], sm[0:M, 0:8], 1e-12, None, ALU.max), r=["sm0"], w=["sm0"])
            self.V(lambda e: e.reciprocal(sm[0:M, 0:8], sm[0:M, 0:8]), r=["sm0"], w=["sm0"])
            self.V(lambda e: e.tensor_tensor(h3(kk[0:M, :]), h3(kk[0:M, :]), bc3(sm[0:M, 0:8], 64), ALU.mult),
                   r=["T2", "sm0"], w=["T2"])
            self.V(lambda e: e.scalar_tensor_tensor(t3[0:M, :], a_[0:M, :], -1.0, kab[0:M, :], ALU.add, ALU.mult),
                   r=["T1", "rwkv_k_a_bc"], w=["T3"])
            self.V(lambda e: e.scalar_tensor_tensor(kf[0:M, :], t3[0:M, :], 1.0, zk[0:M, :], ALU.add, ALU.mult),
                   r=["T3", "zk"], w=["T4"])
            self.P(lambda e: e.tensor_tensor(be[0:M, :], kk[0:M, :], a_[0:M, :], ALU.mult), r=["T2", "T1"], w=["T5"])
            self.P(lambda e: e.tensor_tensor(t3[0:M, :], zr[0:M, :], kf[0:M, :], ALU.mult), r=["zr", "T4"], w=["T3"])
            self.P(lambda e: e.tensor_tensor(t3[0:M, :], t3[0:M, :], rkb[0:M, :], ALU.mult), r=["T3", "rwkv_r_k_bc"], w=["T3"])
            self.V(lambda e, R=R: e.tensor_reduce(R.bon[0:M, :], h3(t3[0:M, :]), AX.X, ALU.add), r=["T3"], w=[K("bon")])
            if sample:
                fw.act(T[6][0:M, :], sg[0:M, :], AF.Exp, r=["T0"], w=["T6"], scale=CDEC)
                for x, (tl, tk) in enumerate([(zr, "zr"), (T[6], "T6"), (kf, "T4"), (R.zv, K("zv")), (kk, "T2"), (be, "T5")]):
                    fw.dma(self.sq[x], tl[0:M, :], r=[tk], w=[("sq", x)], key="sqw%d" % x)
                return
            pli, plik = self.pf()
            fw.mm(pli[:, :], tri[:, 0:128], sg[:, :], True, True, r=["tri", "T0"], w=[plik])
            ple, plek = self.pf()
            fw.mm(ple[:, :], tri[:, 128:256], sg[:, :], True, True, r=["tri", "T0"], w=[plek])
            eL, eLm, enL = T[6], T[7], T[3]
            fw.act(eL[:, :], pli[:, :], AF.Exp, r=[plik], w=["T6"])
            fw.act(eLm[:, :], ple[:, :], AF.Exp, r=[plek], w=["T7"])
            fw.act(enL[:, :], pli[:, :], AF.Exp, r=[plik], w=["T3"], scale=-1.0)
            self.V(lambda e: e.tensor_tensor(rt[:, :], zr[:, :], eL[:, :], ALU.mult), r=["zr", "T6"], w=["rt"])
            self.V(lambda e: e.tensor_tensor(kat[:, :], kk[:, :], eLm[:, :], ALU.mult), r=["T2", "T7"], w=["kat"])
            self.P(lambda e, R=R: e.tensor_tensor(R.ktt[:, :], kf[:, :], enL[:, :], ALU.mult), r=["T4", "T3"], w=[K("ktt")])
            self.V(lambda e, R=R: e.scalar_tensor_tensor(R.bnt[:, :], be[:, :], -1.0, enL[:, :], ALU.mult, ALU.mult),
                   r=["T5", "T3"], w=[K("bnt")])
            fw.act(R.vb[:, :], R.zv[:, :], AF.Copy, r=[K("zv")], w=[K("vb")])
            pwc, pwck = self.pf()
            for j in range(4):
                fw.mm(pwc[:, j:j + 1], eL[:, j * 128:(j + 1) * 128], clast[:, :], True, True, r=["T6", "clast"], w=[pwck])
            fw.act(R.WC[:, :], pwc[:, 0:4], AF.Copy, r=[pwck], w=[K("WC")])
            for (src, skey, dstf, dk) in [(rt, "rt", None, "RKT"), (kat, "kat", None, "RKT"),
                                          (R.ktt, K("ktt"), None, "KT"), (R.bnt, K("bnt"), None, "BT")]:
                pbk, pk = self.pb()
                for j in range(4):
                    fw.tr(pbk[:, j * 128:(j + 1) * 128], src[:, j * 128:(j + 1) * 128], identb[:, :], r=[skey, "identb"], w=[pk])
                if dk == "RKT":
                    which = 0 if skey == "rt" else 1
                    fw.act(R.RKT[:, :, which, :], pbk[:, 0:512].rearrange("p (j t) -> p j t", j=4), AF.Copy, r=[pk], w=["RKT%d" % which])
                else:
                    dst = KT if dk == "KT" else BT
                    self.V(lambda e, dst=dst, pbk=pbk: e.tensor_copy(dst[:, :, :], pbk[:, 0:512].rearrange("p (j t) -> p j t", j=4)),
                           r=[pk], w=[dk])

        def stageAB(R):
            K = R.K
            RK = [K("RKT0"), K("RKT1")]
            RKT, G4, ZF = R.RKT, R.G4, R.ZF
            zb = [self.pf(), self.pf()]
            for j in range(4):
                for hh in range(2):
                    o = hh * 64
                    pZ, pzk = zb[hh]
                    fw.mm(pZ[:, j * 128:(j + 1) * 128], RKT[o:o + 64, j, 1, :], BT[o:o + 64, j, :], True, True, r=["BT", K("RKT1")], w=[pzk])
            mlb = maskL[:, :].unsqueeze(1).to_broadcast([128, 4, 128])
            for hh in range(2):
                pZ, pzk = zb[hh]
                self.V(lambda e, pZ=pZ, hh=hh: e.tensor_tensor(FFa[0][:, :, hh, :], pZ[:, :].rearrange("p (j c) -> p j c", j=4), mlb, ALU.mult),
                       r=[pzk, "maskL"], w=["FF%d_0" % j for j in range(4)])
            for j in range(4):
                bk = [self.pf(), self.pf()]
                for hh in range(2):
                    o = hh * 64
                    ps, pk = bk[hh]
                    rhs = RKT[o:o + 64, j, :, :].rearrange("p a t -> p (a t)")
                    fw.mm(ps[:, 0:256], KT[o:o + 64, j, :], rhs, True, True, r=["KT"] + RK, w=[pk])
                    fw.mm(ps[:, 256:512], BT[o:o + 64, j, :], rhs, True, True, r=["BT"] + RK, w=[pk])
                for hh in range(2):
                    ps, pk = bk[hh]
                    self.V(lambda e, j=j, hh=hh, ps=ps, G4=G4: e.tensor_tensor(
                        G4[j][:, hh, 0:512].rearrange("p (a c) -> p a c", a=2), ps[:, :].rearrange("p (a c) -> p a c", a=2),
                        mask2[:, :].unsqueeze(1).to_broadcast([128, 2, 256]), ALU.mult), r=[pk, "mask2"], w=[K("G4_%d" % j)])
            for lev in range(7):
                a, b = lev % 2, (lev + 1) % 2
                for j in range(4):
                    fk, fn_ = "FF%d_%d" % (j, a), "FF%d_%d" % (j, b)
                    ezn = "EZ%d_%d" % (j, b)
                    if lev == 0:
                        ezk = K("G4_%d" % j)
                        EZs = lambda hh, j=j, G4=G4: G4[j][:, hh, 384:640]
                        Es = lambda hh, j=j, G4=G4: G4[j][:, hh, 384:512]
                        Zs = lambda j=j, G4=G4: G4[j][:, :, 512:640]
                    else:
                        ezk = "EZ%d_%d" % (j, a)
                        EZs = lambda hh, j=j, a=a: EZ[j][a][:, hh, :, :].rearrange("p a t -> p (a t)")
                        Es = lambda hh, j=j, a=a: EZ[j][a][:, hh, 0, :]
                        Zs = lambda j=j, a=a: EZ[j][a][:, :, 1, :]
                    if lev < 6:
                        pL, plk = self.pf()
                        for hh in range(2):
                            fw.mm(pL[:, hh * 256:(hh + 1) * 256], FF[j][a][:, hh, :], EZs(hh), True, True, r=[ezk, fk], w=[plk])
                        pF, pfk = self.pf()
                        for hh in range(2):
                            fw.mm(pF[:, hh * 128:(hh + 1) * 128], Es(hh), FF[j][a][:, hh, :], True, True, r=[ezk, fk], w=[pfk])
                        l3 = pL[:, :].rearrange("p (h c) -> p h c", h=2)
                        fw.act(EZ[j][b][:, :, 0, :], l3[:, :, 0:128], AF.Copy, r=[plk], w=[ezn])
                        self.V(lambda e, j=j, b=b, l3=l3, Zs=Zs: e.tensor_tensor(EZ[j][b][:, :, 1, :], l3[:, :, 128:256], Zs(), ALU.add),
                               r=[plk, ezk], w=[ezn])
                        fw.act(FF[j][b][:, :, :], pF[:, 0:256].rearrange("p (h c) -> p h c", h=2), AF.Copy, r=[pfk], w=[fn_])
                    else:
                        pL, plk = self.pf()
                        for hh in range(2):
                            fw.mm(pL[:, hh * 128:(hh + 1) * 128], FF[j][a][:, hh, :], EZ[j][a][:, hh, 1, :], True, True, r=[ezk, fk], w=[plk])
                        self.V(lambda e, j=j, a=a, pL=pL, ZF=ZF: e.tensor_tensor(ZF[j][:, :, :], pL[:, 0:256].rearrange("p (h c) -> p h c", h=2),
                                                                      EZ[j][a][:, :, 1, :], ALU.add), r=[plk, ezk], w=[K("ZF%d" % j)])

        def stageC(R):
            K = R.K
            RKT, G4, ZF, vb = R.RKT, R.G4, R.ZF, R.vb
            for j in range(4):
                pU, puk = self.pf()
                for hh in range(2):
                    o, h = hh * 64, 2 * j + hh
                    fw.mm(pU[:, hh * 64:(hh + 1) * 64], RKT[o:o + 64, j, 1, :], Nb[o:o + 64, j, o:o + 64], True, False, r=[K("RKT1"), "Nb"], w=[puk])
                    fw.mm(pU[:, hh * 64:(hh + 1) * 64], G4[j][:, hh, 128:256], vb[:, h * 64:(h + 1) * 64], False, True, r=[K("G4_%d" % j), K("vb")], w=[puk])
                fw.act(U0b[j][:, :, :], pU[:, 0:128].rearrange("p (h c) -> p h c", h=2), AF.Copy, r=[puk], w=["U0b%d" % j])
            for j in range(4):
                pU, puk = self.pf()
                for hh in range(2):
                    fw.mm(pU[:, hh * 64:(hh + 1) * 64], ZF[j][:, hh, :], U0b[j][:, hh, :], True, True, r=[K("ZF%d" % j), "U0b%d" % j], w=[puk])
                fw.act(Ub[:, j * 128:(j + 1) * 128], pU[:, 0:128], AF.Copy, r=[puk], w=["Ub%d" % j])

        def stageD(R):
            K = R.K
            RKT, G4, vb = R.RKT, R.G4, R.vb
            psY, pyk = self.pf()
            for j in range(4):
                for hh in range(2):
                    o, h = hh * 64, 2 * j + hh
                    fw.mm(psY[:, h * 64:(h + 1) * 64], RKT[o:o + 64, j, 0, :], Nb[o:o + 64, j, o:o + 64], True, False, r=[K("RKT0"), "Nb"], w=[pyk])
                    fw.mm(psY[:, h * 64:(h + 1) * 64], G4[j][:, hh, 0:128], vb[:, h * 64:(h + 1) * 64], False, False, r=[K("G4_%d" % j), K("vb")], w=[pyk])
                    fw.mm(psY[:, h * 64:(h + 1) * 64], G4[j][:, hh, 256:384], Ub[:, h * 64:(h + 1) * 64], False, True, r=[K("G4_%d" % j), "Ub%d" % j], w=[pyk])
            return psY, pyk

        def n_update(R):
            K = R.K
            ktt, bnt, vb, WC = R.ktt, R.bnt, R.vb, R.WC
            pN, pnk = self.pf()
            for j in range(4):
                fw.mm(pN[:, j * 128:(j + 1) * 128], ktt[:, j * 128:(j + 1) * 128], vb[:, j * 128:(j + 1) * 128], True, False, r=[K("ktt"), K("vb")], w=[pnk])
                fw.mm(pN[:, j * 128:(j + 1) * 128], bnt[:, j * 128:(j + 1) * 128], Ub[:, j * 128:(j + 1) * 128], False, True, r=[K("bnt"), "Ub%d" % j], w=[pnk])
            n2 = Nst[:, :, :].rearrange("p j c -> p (j c)")
            self.V(lambda e: e.tensor_tensor(n2, pN[:, :], n2, ALU.add), r=[pnk, "Nst"], w=["Nst"])
            self.V(lambda e, WC=WC: e.tensor_tensor(Nst[:, :, :], Nst[:, :, :], bc3(WC[:, :], 128), ALU.mult), r=["Nst", K("WC")], w=["Nst"])
            fw.act(Nb[:, :, :], Nst[:, :, :], AF.Copy, r=["Nst"], w=["Nb"])


        def post(M, yap, ykeys, pg, pgk, R):
            K = R.K
            lng, lnb = bcs["rwkv_ln_g"], bcs["rwkv_ln_b"]
            y2, yc = TP_[0], TP_[1]
            ob = TP_[0].bitcast(BF)[:, 0:RD]
            self.V(lambda e: e.tensor_reduce(sm[0:M, 16:24], h3(yap), AX.X, ALU.add), r=ykeys, w=["sm2"])
            fw.act(y2[0:M, :], yap, AF.Square, r=ykeys, w=["TP0"])
            self.V(lambda e: e.tensor_reduce(sm[0:M, 24:32], h3(y2[0:M, :]), AX.X, ALU.add), r=["TP0"], w=["sm3"])
            mean, var = sm[0:M, 16:24], sm[0:M, 24:32]
            self.V(lambda e: e.tensor_scalar(mean, mean, 1.0 / 64, None, ALU.mult), r=["sm2"], w=["sm2"])
            self.V(lambda e: e.tensor_tensor(sm[0:M, 32:40], mean, mean, ALU.mult), r=["sm2"], w=["sm4"])
            self.V(lambda e: e.scalar_tensor_tensor(var, var, 1.0 / 64, sm[0:M, 32:40], ALU.mult, ALU.subtract), r=["sm3", "sm4"], w=["sm3"])
            self.V(lambda e: e.tensor_scalar(var, var, 64e-5, None, ALU.add), r=["sm3"], w=["sm3"])
            fw.act(var, var, AF.Sqrt, r=["sm3"], w=["sm3"])
            self.V(lambda e: e.reciprocal(var, var), r=["sm3"], w=["sm3"])
            self.V(lambda e: e.tensor_tensor(h3(yc[0:M, :]), h3(yap), bc3(mean, 64), ALU.subtract), r=list(ykeys) + ["sm2"], w=["TP1"])
            self.V(lambda e: e.tensor_tensor(h3(yc[0:M, :]), h3(yc[0:M, :]), bc3(var, 64), ALU.mult), r=["TP1", "sm3"], w=["TP1"])
            self.P(lambda e: e.tensor_tensor(yc[0:M, :], yc[0:M, :], lng[0:M, :], ALU.mult), r=["TP1", "rwkv_ln_g_bc"], w=["TP1"])
            self.P(lambda e: e.tensor_tensor(yc[0:M, :], yc[0:M, :], lnb[0:M, :], ALU.add), r=["TP1", "rwkv_ln_b_bc"], w=["TP1"])
            self.P(lambda e, R=R: e.tensor_tensor(h3(y2[0:M, :]), h3(R.zv[0:M, :]), bc3(R.bon[0:M, :], 64), ALU.mult), r=[K("zv"), K("bon")], w=["TP0"])
            self.V(lambda e: e.tensor_tensor(yc[0:M, :], yc[0:M, :], y2[0:M, :], ALU.add), r=["TP1", "TP0"], w=["TP1"])
            self.V(lambda e: e.tensor_tensor(ob[0:M, :], yc[0:M, :], pg[0:M, :], ALU.mult), r=["TP1", pgk], w=["TP0"])
            pbk, pk = self.pb()
            for j in range(4):
                fw.tr(pbk[:, j * M:(j + 1) * M], ob[0:M, j * 128:(j + 1) * 128], identb[0:M, 0:M], r=["TP0", "identb"], w=[pk])
            fw.act(orT[:, :, 0:M], pbk[:, 0:4 * M].rearrange("p (j t) -> p j t", j=4), AF.Copy, r=[pk], w=["orT"])

        def gate_branch(M, hcur, hk, mdst, mkey):
            for half in range(2):
                pg, pgk = self.pf()
                for q in range(4):
                    dc = half * 4 + q
                    for c in range(8):
                        fw.mm(pg[:, q * M:(q + 1) * M], Wg[:, c, dc * 128:(dc + 1) * 128], hcur(c), c == 0, c == 7, r=[hk, "Wg_%d" % c], w=[pgk])
                fw.act(sgr[:, half * 4:(half + 1) * 4, 0:M], pg[:, 0:4 * M].rearrange("p (q t) -> p q t", q=4), AF.Sigmoid, r=[pgk], w=["sgr%d" % half])
                pbr, pbk_ = self.pf()
                for q in range(4):
                    dc = half * 4 + q
                    for j in range(4):
                        fw.mm(pbr[:, q * M:(q + 1) * M], Wr[:, j, dc * 128:(dc + 1) * 128], orT[:, j, 0:M], j == 0, j == 3, r=["orT", "Wr_%d" % j], w=[pbk_])
                self.V(lambda e, half=half, pbr=pbr: e.tensor_tensor(mdst[:, half * 4:(half + 1) * 4, 0:M], sgr[:, half * 4:(half + 1) * 4, 0:M],
                                                                 pbr[:, 0:4 * M].rearrange("p (q t) -> p q t", q=4), ALU.mult),
                       r=["sgr%d" % half, pbk_], w=[mkey])

        R1 = mkrec(1)
        for j in range(4):
            self.P(lambda e, j=j: e.tensor_copy(R1.G4[j][:, :, 512:640], identb[:, :].unsqueeze(1).to_broadcast([128, 2, 128])),
                   r=["identb"], w=[R1.K("G4_%d" % j)])
        RR = [R0, R1]

        def H1a(i):
            R, Rp = RR[i % 2], RR[(i + 1) % 2]
            hT = R.hT
            xt, xk = self.xt[i % 2], "xt%d" % (i % 2)
            src, _ = self.xsrc(l, i)
            fw.dma(xt[:], src, r=[("xb", i)], w=[xk], key=xk)
            hk = R.K("hTr")
            if i == 0:
                self.V(lambda e, hT=hT: e.memset(hT[:, :, 0:1], 0.0), w=[hk])
            else:
                self.P(lambda e, hT=hT, hp=Rp.hT: e.tensor_copy(hT[:, :, 0:1], hp[:, :, 128:129]), r=[Rp.K("hTr")], w=[hk])
            self.norm_a(xt, xk, 128)

        def H1b(i):
            R = RR[i % 2]
            K = R.K
            hT = R.hT
            hk = K("hTr")
            self.norm_b(128, hT[:, :, 1:129], hk, identb)
            hcur = lambda c, hT=hT: hT[:, c, 1:129]
            hprev = lambda c, hT=hT: hT[:, c, 0:128]
            for g0, dst, dk in [(0, zr, "zr"), (512, zk, "zk"), (1024, R.zv, K("zv"))]:
                ps, pk = tok_proj(128, hcur, hprev, hk, g0, dk)
                fw.act(dst[:, :], ps[:, :], AF.Copy, r=[pk], w=[dk])
            ps, pk = feat_proj(128, hcur, hprev, hk, 1536)
            fw.act(lact[0:64, :], ps[0:64, 0:128], AF.Tanh, r=[pk], w=["lact"])
            fw.act(lact[64:128, :], ps[64:128, 0:128], AF.Copy, r=[pk], w=["lact"])
            ps, pk = feat_proj(128, hcur, hprev, hk, 1664)
            fw.act(R.sgT[:, :], ps[:, 0:128], AF.Sigmoid, r=[pk], w=[K("sgT")])
            if i == NT - 1:
                raw_last(lambda c, hT=hT: hT[:, c, 128:129], hk, 1, O["p_shift"][l:l + 1, :])

        def H1c(i):
            prep(128, False, RR[i % 2])

        def H1d(i):
            stageAB(RR[i % 2])

        H2st = {}

        def H2a(i):
            R = RR[i % 2]
            stageC(R)
            psY, pyk = stageD(R)
            n_update(R)
            pg, pgk = self.pf()
            fw.mm(pg[:, :], R.sgT[:, :], lg2[:, :], True, True, r=[R.K("sgT"), "lg2"], w=[pgk])
            H2st[i] = (psY, pyk, pg, pgk)

        def H2b(i):
            psY, pyk, pg, pgk = H2st.pop(i)
            post(128, psY[:, :], [pyk], pg, pgk, RR[i % 2])

        def H2c(i):
            R = RR[i % 2]
            m, mk = mrT[0], "mrT0"
            gate_branch(128, lambda c, R=R: R.hT[:, c, 1:129], R.K("hTr"), m, mk)
            fw.dma(self.mrbuf[i].rearrange("p (c t) -> p c t", c=8), m[:, :, :], r=[mk], w=[("mr", i)], key=mk)

        def cap(pool, f, i):
            self.pool = pool
            return fw.capture(lambda: f(i))

        for f in (H1a, H1b, H1c):
            fw.replay([cap(0, f, 0)])
        fw.replay([cap(None, H1d, 0)])
        for i in range(NT):
            nx = i + 1 < NT
            if nx:
                fw.replay([cap(0, H1a, i + 1)])
            fw.replay([cap(1, H2a, i)])
            fw.replay(([cap(0, H1b, i + 1)] if nx else []) + [cap(1, H2b, i)])
            fw.replay(([cap(0, H1c, i + 1)] if nx else []) + [cap(1, H2c, i)])
            if nx:
                fw.replay([cap(None, H1d, i + 1)])
        self.pool = None
        for j in range(4):
            ps, pk = self.pf()
            fw.tr(ps[:, 0:128], Nst[:, j, :], identf[:, :], r=["Nst", "identf"], w=[pk])
            fw.act(T[0][:, j * 128:(j + 1) * 128], ps[:, 0:128], AF.Copy, r=[pk], w=["T0"])
        for h_ in range(8):
            j, o = h_ // 2, (h_ % 2) * 64
            fw.dma(O["p_wkv"][l, h_], T[0][o:o + 64, j * 128 + o:j * 128 + o + 64], r=["T0"], key="T0")

        self.release(m1)
        RS = NSP()
        RS.zv = sbl("zv_s", [128, RD])
        RS.sgT = sbl("sgT_s", [128, 128], BF)
        RS.bon = sbl("bon_s", [128, 8])
        RS.K = lambda n: n + "#s"
        hTs = sbl("hTs", [128, 8, 80], BF)
        sadd = sbl("sadd", [16, RP])
        stT = sbl("stT", [128, 2, 16])
        zf = sbl("zf", [128, 2, 64])
        QH = sbl("QH", [128, 6, 4, 64])
        Sst = sbl("Sst", [128, 64, 64])
        Stmp = sbl("Stmp", [128, 64, 64])
        sk = sbl("sk", [128, 64])
        yh = sbl("yh", [128, 4, 64])
        ytm = T[7]
        self.V(lambda e: e.memset(hTs[:], 0.0), w=["hTs"])
        i = NT
        xt, xk = self.xt[i % 2], "xt%d" % (i % 2)
        src, _ = self.xsrc(l, i)
        fw.dma(xt[0:MS, :], src, r=[("xb", i)], w=[xk], key=xk)
        self.norm_hT(xt, xk, MS, hTs[:, :, 16:80], "hTs", identb)
        hcur = lambda c: hTs[:, c, 16:80]
        hprev = lambda c: hTs[:, c, 0:64]
        fw.dma(sadd[:, :], I["st_shift"][l], w=["sadd"], key="sadd")
        for q in range(2):
            ps, pk = self.pf()
            fw.tr(ps[:, 0:16], sadd[0:16, 1536 + q * 128:1536 + (q + 1) * 128], identf[0:16, 0:16], r=["sadd", "identf"], w=[pk])
            self.V(lambda e, q=q, ps=ps: e.tensor_scalar(stT[:, q, :], ps[:, 0:16], mucol[:, q:q + 1], None, ALU.mult), r=[pk, "mucol"], w=["stT"])
        for gi, g0 in enumerate(range(0, RP, 512)):
            n = min(512, RP - g0)
            self.bcast_load(T[4 + gi][0:16, 0:n], "T%d" % (4 + gi), I["rwkv_mu"][l, g0:g0 + n])
            self.V(lambda e, gi=gi, g0=g0, n=n: e.tensor_tensor(sadd[:, g0:g0 + n], sadd[:, g0:g0 + n], T[4 + gi][0:16, 0:n], ALU.mult),
                   r=["sadd", "T%d" % (4 + gi)], w=["sadd"])
        zv, sgT = RS.zv, RS.sgT
        for g0, dst, dk in [(0, zr, "zr"), (512, zk, "zk"), (1024, zv, RS.K("zv"))]:
            ps, pk = tok_proj(MS, hcur, hprev, "hTs", g0, dk)
            fw.act(dst[0:MS, :], ps[0:MS, :], AF.Copy, r=[pk], w=[dk])
            self.V(lambda e, dst=dst, g0=g0: e.tensor_tensor(dst[0:16, :], dst[0:16, :], sadd[0:16, g0:g0 + 512], ALU.add), r=[dk, "sadd"], w=[dk])
        for q, g0 in enumerate([1536, 1664]):
            ps, pk = feat_proj(MS, hcur, hprev, "hTs", g0)
            fw.act(zf[:, q, :], ps[:, 0:MS], AF.Copy, r=[pk], w=["zf"])
            self.V(lambda e, q=q: e.tensor_tensor(zf[:, q, 0:16], zf[:, q, 0:16], stT[:, q, :], ALU.add), r=["zf", "stT"], w=["zf"])
        fw.act(lact[0:64, 0:MS], zf[0:64, 0, :], AF.Tanh, r=["zf"], w=["lact"])
        fw.act(lact[64:128, 0:MS], zf[64:128, 0, :], AF.Copy, r=["zf"], w=["lact"])
        fw.act(sgT[:, 0:MS], zf[:, 1, :], AF.Sigmoid, r=["zf"], w=[RS.K("sgT")])
        prep(MS, True, RS)
        if l == 0:
            for nm, ap, k in [("s_zr", zr, "zr"), ("s_zk", zk, "zk"), ("s_zv", zv, "zv"), ("s_dec", T[6], "T6"), ("s_kk", T[2], "T2"),
                              ("s_kf", T[4], "T4"), ("s_be", T[5], "T5"), ("s_a", T[1], "T1")]:
                self.tap(nm, ap[0:MS, :], [k])
        sqv = self.sq.rearrange("x (t q) (h d) -> (q h) x t d", t=4, h=NH)
        for x in range(6):
            fw.dma(QH[:, x, :, :], sqv[:, x, :, :], r=[("sq", x)], w=["QH"], key="QH")
        fw.dma(Sst[:, :, :].rearrange("p v k -> p (v k)"), I["st_wkv"][l], w=["Sst"], key="Sst")
        for t in range(4):
            r_, w_, k_, v_, kk_, b_ = (QH[:, x, t, :] for x in range(6))
            rowb = lambda a: a.unsqueeze(1).to_broadcast([128, 64, 64])
            colb = lambda a: a.unsqueeze(2).to_broadcast([128, 64, 64])
            self.V(lambda e, kk_=kk_: e.tensor_tensor(Stmp[:, :, :], Sst[:, :, :], rowb(kk_), ALU.mult), r=["Sst", "QH"], w=["Stmp"])
            self.V(lambda e: e.tensor_reduce(sk[:, :], Stmp[:, :, :], AX.X, ALU.add), r=["Stmp"], w=["sk"])
            self.P(lambda e, w_=w_: e.tensor_tensor(Sst[:, :, :], Sst[:, :, :], rowb(w_), ALU.mult), r=["Sst", "QH", "Stmp"], w=["Sst"])
            self.V(lambda e, b_=b_: e.tensor_tensor(Stmp[:, :, :], colb(sk[:, :]), rowb(b_), ALU.mult), r=["sk", "QH"], w=["Stmp"])
            self.V(lambda e: e.tensor_tensor(Sst[:, :, :], Sst[:, :, :], Stmp[:, :, :], ALU.subtract), r=["Sst", "Stmp"], w=["Sst"])
            self.P(lambda e, v_=v_, k_=k_: e.tensor_tensor(Stmp[:, :, :], colb(v_), rowb(k_), ALU.mult), r=["QH", "Sst"], w=["Stmp"])
            self.V(lambda e: e.tensor_tensor(Sst[:, :, :], Sst[:, :, :], Stmp[:, :, :], ALU.add), r=["Sst", "Stmp"], w=["Sst"])
            self.P(lambda e, r_=r_: e.tensor_tensor(Stmp[:, :, :], Sst[:, :, :], rowb(r_), ALU.mult), r=["Sst", "QH"], w=["Stmp"])
            self.V(lambda e, t=t: e.tensor_reduce(yh[:, t, :], Stmp[:, :, :], AX.X, ALU.add), r=["Stmp"], w=["yh"])
        fw.dma(O["s_wkv"][l], Sst[:, :, :].rearrange("p v k -> p (v k)"), r=["Sst"], key="Sst")
        if l == 0:
            self.tap("s_QH", QH, ["QH"])
            self.tap("s_yh", yh, ["yh"])
        fw.dma(self.sy.rearrange("(t q) (h d) -> (q h) t d", t=4, h=NH), yh[:, :, :], r=["yh"], w=["sy"], key="yh")
        fw.dma(ytm[0:MS, :], self.sy, r=["sy"], w=["T7"], key="ytm")
        pg, pgk = self.pf()
        fw.mm(pg[0:MS, :], sgT[:, 0:MS], lg2[:, :], True, True, r=[RS.K("sgT"), "lg2"], w=[pgk])
        post(MS, ytm[0:MS, :], ["T7"], pg, pgk, RS)
        m, mk = mrT[0], "mrT0"
        gate_branch(MS, hcur, "hTs", m, mk)
        fw.dma(self.mrbuf[NT].rearrange("p (c t) -> p c t", c=8)[:, :, 0:MS], m[:, :, 0:MS], r=[mk], w=[("mr", NT)], key=mk)
        raw_last(lambda c: hTs[:, c, 64:80], "hTs", 16, O["s_shift"][l])

    def pass_attn(self, l, es2):
        fw, I, O, NT = self.fw, self.I, self.O, self.NT
        sbl = lambda n, s, dt=F32: self.sbl(es2, "a%d_" % l + n, s, dt)
        identb, identf = self.identb, self.identf
        Wq = sbl("Wq", [128, 8, 768], BF)
        Wg = sbl("Wg", [128, 8, D], BF)
        Wa = sbl("Wa", [128, 4, D], BF)
        Wo = sbl("Wo", [128, 8, D], BF)
        self.col_load(self.gcol[:], "gcol", I["norm_mix_g"][l], 8)
        m0 = self.aoff
        self.wstage = [sbl("wst%d" % i_, [128, 2048]) for i_ in range(2)]
        win = I["w_in"][l]
        gsc = lambda c: self.gcol[:, c:c + 1]
        self.prep_w(8, 512, lambda c, s0, n: win[c * 128:(c + 1) * 128, RP:RP + 512],
                    lambda c, s0, n: Wq[:, c, 0:512].rearrange("p (j g d) -> p g j d", j=4, g=2), lambda c: "Wq_%d" % c, "col", gsc,
                    sview=lambda a: a.rearrange("p (g j d) -> p g j d", g=2, j=4))
        self.prep_w(8, 256, lambda c, s0, n: win[c * 128:(c + 1) * 128, RP + 512:RP + 768],
                    lambda c, s0, n: Wq[:, c, 512:768], lambda c: "Wq_%d" % c, "col", gsc)
        self.prep_w(8, D, lambda c, s0, n: win[c * 128:(c + 1) * 128, 3584 + s0:3584 + s0 + n],
                    lambda c, s0, n: Wg[:, c, s0:s0 + n], lambda c: "Wga_%d" % c, "col", gsc)
        wbr = I["w_br_attn"][l]
        self.prep_w(4, D, lambda c, s0, n: wbr[c * 128:(c + 1) * 128, s0:s0 + n],
                    lambda c, s0, n: Wa[:, c, s0:s0 + n], lambda c: "Wa_%d" % c, "plain")
        wo = I["w_out"][l]
        self.prep_w(8, D, lambda c, s0, n: wo[c * 128:(c + 1) * 128, s0:s0 + n],
                    lambda c, s0, n: Wo[:, c, s0:s0 + n], lambda c: "Wo_%d" % c, "plain")
        self.release(m0)
        amask = sbl("amask", [128, 1024])
        fw.dma(amask[:, 0:768], I["c_amask"], w=["amask"], key="amask")
        fw.dma(amask[:, 768:1024], I["c_amask0"], w=["amask"], key="amask")
        smask = sbl("smask", [32, 132])
        fw.dma(smask[:], I["c_smask"], w=["smask"], key="smask")
        sinks = sbl("sinks", [128, NH])
        self.bcast_load(sinks[:], "sinks", I["attn_sinks"][l])
        hTd = [sbl("hT%d" % i_, [128, 8, 128], BF) for i_ in range(2)]
        hT = hTd[1]
        qkv = sbl("qkv", [128, 768])
        rot = sbl("rot", [128, 640])
        rtmp = [sbl("rtmp%d" % i, [128, 320]) for i in range(2)]
        rotb = sbl("rotb", [128, 640], BF)
        cs = [sbl("cs%d" % i, [128, 64]) for i in range(2)]
        qT = sbl("qT", [128, 4, 128], BF)
        KTr = sbl("KTr", [128, 2, 128], BF)
        Vp = sbl("Vp", [128, 2, 2, 2, 128], BF)
        scg = [sbl("sc%d" % g_, [128, 4, 256]) for g_ in range(2)]
        stg = [sbl("st%d" % g_, [128, 16]) for g_ in range(2)]
        pbfg = [sbl("pbf%d" % g_, [128, 4, 256], BF) for g_ in range(2)]
        pTg = [sbl("pT%d" % g_, [128, 4, 2, 128], BF) for g_ in range(2)]
        oT = sbl("oT", [128, 4, 128], BF)
        sga = sbl("sga", [128, 8, 128])
        mrl = [sbl("mrl%d" % i, [128, 8, 128], BF) for i in range(2)]
        mg = sbl("mg", [128, 8, 128], BF)
        xo = [sbl("xo%d" % i, [128, D]) for i in range(2)]
        KA = sbl("KA", [128, NS, 128])
        VA = sbl("VA", [128, NS, 128])
        VAb = sbl("VAb", [128, NS, 128], BF)
        KB = sbl("KB", [4, NS, 128])
        VBt = sbl("VB", [4, NS, 128])
        VBb = sbl("VBb", [4, NS, 128], BF)
        KAT = sbl("KAT", [128, NS, 128], BF)
        KBT = sbl("KBT", [128, NS, 4], BF)
        qbd = sbl("qbd", [128, NS, 32], BF)
        ssc = sbl("ssc", [32, NS, 132])
        sst = sbl("sst", [32, 4 * NS])
        spb = sbl("spb", [32, NS, 132], BF)
        spT = sbl("spT", [128, NS, 32], BF)
        spTB = sbl("spTB", [4, NS, 32], BF)
        oTs = sbl("oTs", [128, 4, MS], BF)

        self.V(lambda e: e.memset(Vp[:], 0.0), w=["Vp0", "Vp1"])
        self.V(lambda e: e.memset(KTr[:], 0.0), w=["KTr0", "KTr1"])
        self.V(lambda e: e.memset(qbd[:], 0.0), w=["qbd"])

        def proj_rope(M, hcur, hk, cosap, sinap, cskey):
            for g0, n in [(0, 512), (512, 256)]:
                ps, pk = self.pf()
                for c in range(8):
                    fw.mm(ps[0:M, 0:n], hcur(c), Wq[:, c, g0:g0 + n], c == 0, c == 7, r=[hk, "Wq_%d" % c], w=[pk])
                fw.act(qkv[0:M, g0:g0 + n], ps[0:M, 0:n], AF.Copy, r=[pk], w=["qkv%d" % (g0 // 512)])
            qk3 = qkv[0:M, 0:640].rearrange("p (h d) -> p h d", h=10)
            r3 = rot[0:M, :].rearrange("p (h d) -> p h d", h=10)
            x1, x2 = qk3[:, :, 0:32], qk3[:, :, 32:64]
            cb = cosap.unsqueeze(1).to_broadcast([M, 10, 32])
            sb_ = sinap.unsqueeze(1).to_broadcast([M, 10, 32])
            ta = rtmp[0][0:M, :].rearrange("p (h d) -> p h d", h=10)
            tb = rtmp[1][0:M, :].rearrange("p (h d) -> p h d", h=10)
            rk = ["qkv0", "qkv1", cskey]
            self.V(lambda e: e.tensor_tensor(ta, x1, cb, ALU.mult), r=rk, w=["rtmp0"])
            self.P(lambda e: e.tensor_tensor(tb, x2, sb_, ALU.mult), r=rk, w=["rtmp1"])
            self.V(lambda e: e.tensor_tensor(r3[:, :, 0:32], ta, tb, ALU.subtract), r=["rtmp0", "rtmp1"], w=["rot"])
            self.V(lambda e: e.tensor_tensor(ta, x2, cb, ALU.mult), r=rk + ["rot"], w=["rtmp0"])
            self.P(lambda e: e.tensor_tensor(tb, x1, sb_, ALU.mult), r=rk + ["rot"], w=["rtmp1"])
            self.V(lambda e: e.tensor_tensor(r3[:, :, 32:64], ta, tb, ALU.add), r=["rtmp0", "rtmp1"], w=["rot"])
            fw.act(rotb[0:M, :], rot[0:M, :], AF.Copy, r=["rot"], w=["rotb"])

        def q_transposes(M, dst, dkey):
            pbk, pk = self.pb()
            for jj in range(4):
                fw.tr(pbk[:, jj * M:(jj + 1) * M], rotb[0:M, jj * 128:(jj + 1) * 128], identb[0:M, 0:M], r=["rotb", "identb"], w=[pk])
            fw.act(dst, pbk[:, 0:4 * M].rearrange("p (j t) -> p j t", j=4), AF.Copy, r=[pk], w=[dkey])

        def gates_part(M, hcur, hk):
            for half in range(2):
                pg, pgk = self.pf()
                for q in range(4):
                    dc = half * 4 + q
                    for c in range(8):
                        fw.mm(pg[:, q * M:(q + 1) * M], Wg[:, c, dc * 128:(dc + 1) * 128], hcur(c), c == 0, c == 7, r=[hk, "Wga_%d" % c], w=[pgk])
                fw.act(sga[:, half * 4:(half + 1) * 4, 0:M], pg[:, 0:4 * M].rearrange("p (q t) -> p q t", q=4), AF.Sigmoid, r=[pgk], w=["sga%d" % half])

        def gate_out(M, hcur, hk, oTt, okey, mr, mrk, xt, xk, xo_, xok, do_gates=True):
            if do_gates:
                gates_part(M, hcur, hk)
            for half in range(2):
                pbr, pbk_ = self.pf()
                for q in range(4):
                    dc = half * 4 + q
                    for cc in range(4):
                        fw.mm(pbr[:, q * M:(q + 1) * M], Wa[:, cc, dc * 128:(dc + 1) * 128], oTt[:, cc, 0:M], cc == 0, cc == 3, r=[okey, "Wa_%d" % cc], w=[pbk_])
                hs = slice(half * 4, (half + 1) * 4)
                self.V(lambda e, hs=hs, pbr=pbr: e.tensor_tensor(sga[:, hs, 0:M], sga[:, hs, 0:M], pbr[:, 0:4 * M].rearrange("p (q t) -> p q t", q=4), ALU.mult),
                       r=["sga%d" % half, pbk_], w=["sga%d" % half])
                self.V(lambda e, hs=hs: e.tensor_tensor(mg[:, hs, 0:M], sga[:, hs, 0:M], mr[:, hs, 0:M], ALU.add), r=["sga%d" % half, mrk], w=["mg%d" % half])
            for grp in range(2):
                px, pxk = self.pf()
                for dc in range(8):
                    fw.mm(px[0:M, :], mg[:, dc, 0:M], Wo[:, dc, grp * 512:(grp + 1) * 512], dc == 0, dc == 7, r=["mg%d" % (dc // 4), "Wo_%d" % dc], w=[pxk])
                self.V(lambda e, grp=grp, px=px: e.tensor_tensor(xo_[0:M, grp * 512:(grp + 1) * 512], xt[0:M, grp * 512:(grp + 1) * 512], px[0:M, :], ALU.add),
                       r=[xk, pxk], w=[xok])

        def put_kv(slot):
            pbk, pk = self.pb()
            fw.tr(pbk[:, 0:128], rotb[:, 512:640], identb[:, :], r=["rotb", "identb"], w=[pk])
            self.V(lambda e, pbk=pbk, slot=slot: e.tensor_copy(KTr[:, slot, :], pbk[:, 0:128]), r=[pk], w=["KTr%d" % slot])
            for g in range(2):
                vsrc = qkv[:, 640 + g * 64:640 + (g + 1) * 64]
                fw.act(Vp[:, slot, g, 0, 0:64], vsrc, AF.Copy, r=["qkv1"], w=["Vp%d" % slot])
                self.P(lambda e, g=g, vsrc=vsrc, slot=slot: e.tensor_copy(Vp[:, slot, g, 1, 64:128], vsrc), r=["qkv1"], w=["Vp%d" % slot])

        xt, xk = self.xt[1], "xt1"
        fw.dma(xt[:], (I["xh0"] if (l == 0 or NSEG == 1) else self.xh_dram), r=["xh_dram"], w=[xk], key=xk)
        fw.dma(cs[1][:, 0:32], I["c_cosh"], w=["cs1"], key="cs1")
        fw.dma(cs[1][:, 32:64], I["c_sinh"], w=["cs1"], key="cs1")
        self.norm_hT(xt, xk, 128, hT[:, :, :], "hT1", identb)
        proj_rope(128, lambda c: hT[:, c, :], "hT1", cs[1][:, 0:32], cs[1][:, 32:64], "cs1")
        put_kv(1)
        def pre(i):
            xt, xk = self.xt[i % 2], "xt%d" % (i % 2)
            src, _ = self.xsrc(l, i)
            fw.dma(xt[:], src, r=[("xb", i)], w=[xk], key=xk)
            mr, mrk = mrl[i % 2], "mrl%d" % (i % 2)
            fw.dma(mr[:, :, :], self.mrbuf[i].rearrange("p (c t) -> p c t", c=8), r=[("mr", i)], w=[mrk], key=mrk)
            ck_ = "cs%d" % (i % 2)
            fw.dma(cs[i % 2][:, 0:32], I["c_cosp"][i * 128:(i + 1) * 128, :], w=[ck_], key=ck_)
            fw.dma(cs[i % 2][:, 32:64], I["c_sinp"][i * 128:(i + 1) * 128, :], w=[ck_], key=ck_)
            self.norm_hT(xt, xk, 128, hTd[i % 2][:, :, :], "hT%d" % (i % 2), identb)

        pre(0)
        for i in range(NT):
            xt, xk = self.xt[i % 2], "xt%d" % (i % 2)
            mr, mrk = mrl[i % 2], "mrl%d" % (i % 2)
            ck_ = "cs%d" % (i % 2)
            hkk = "hT%d" % (i % 2)
            hcur = lambda c, i=i: hTd[i % 2][:, c, :]
            proj_rope(128, hcur, hkk, cs[i % 2][:, 0:32], cs[i % 2][:, 32:64], ck_)
            slot = i % 2
            if i == NT - 1:
                fw.dma(O["p_k"][l], rot[:, 512:640], r=["rot"], key="rot")
                fw.dma(O["p_v"][l], qkv[:, 640:768], r=["qkv1"], key="qkv1")
            q_transposes(128, qT[:, :, :], "qT")
            put_kv(slot)
            mvar = 3 if i == 0 else slot
            msk = amask[:, mvar * 256:(mvar + 1) * 256].unsqueeze(1).to_broadcast([128, 4, 256])
            pSg = []
            for g in range(2):
                o = g * 64
                pS = []
                for jj in range(4):
                    if jj % 2 == 0:
                        ps, pk = self.pf()
                        pS.append((ps, pk))
                    fw.mm(ps[:, (jj % 2) * 256:(jj % 2 + 1) * 256], qT[o:o + 64, jj, :], KTr[o:o + 64, :, :].rearrange("p s t -> p (s t)"),
                          True, True, r=["qT", "KTr0", "KTr1"], w=[pk])
                pSg.append(pS)
            gates_part(128, hcur, hkk)

            def softmax(g):
                sc, st, pbf = scg[g], stg[g], pbfg[g]
                sck = ["sc%d_0" % g, "sc%d_1" % g]
                for half, (ps, pk) in enumerate(pSg[g]):
                    self.V(lambda e, ps=ps, half=half, msk=msk, sc=sc: e.scalar_tensor_tensor(
                        sc[:, half * 2:(half + 1) * 2, :], ps[:, :].rearrange("p (j c) -> p j c", j=2), 0.125,
                        msk[:, 0:2, :], ALU.mult, ALU.add), r=[pk, "amask"], w=[sck[half]])
                k0, k2, k3 = "st%d" % g, "st%d_2" % g, "st%d_3" % g
                self.V(lambda e: e.tensor_reduce(st[:, 0:4], sc[:, :, :], AX.X, ALU.max), r=sck, w=[k0])
                self.V(lambda e: e.tensor_tensor(st[:, 0:4], st[:, 0:4], sinks[:, g * 4:(g + 1) * 4], ALU.max), r=[k0, "sinks"], w=[k0])
                self.V(lambda e: e.tensor_tensor(sc[:, :, :], sc[:, :, :], bc3(st[:, 0:4], 256), ALU.subtract), r=sck + [k0], w=sck)
                fw.act(sc[:, :, :], sc[:, :, :], AF.Exp, r=sck, w=sck)
                self.V(lambda e: e.tensor_reduce(st[:, 4:8], sc[:, :, :], AX.X, ALU.add), r=sck, w=[k2])
                self.V(lambda e: e.tensor_tensor(st[:, 8:12], sinks[:, g * 4:(g + 1) * 4], st[:, 0:4], ALU.subtract), r=[k0, "sinks"], w=[k3])
                fw.act(st[:, 8:12], st[:, 8:12], AF.Exp, r=[k3], w=[k3])
                self.V(lambda e: e.tensor_tensor(st[:, 4:8], st[:, 4:8], st[:, 8:12], ALU.add), r=[k2, k3], w=[k2])
                self.V(lambda e: e.reciprocal(st[:, 4:8], st[:, 4:8]), r=[k2], w=[k2])
                self.V(lambda e: e.tensor_tensor(pbf[:, :, :], sc[:, :, :], bc3(st[:, 4:8], 256), ALU.mult), r=sck + [k2], w=["pbf%d" % g])

            def p_transposes(g):
                pbf, pT = pbfg[g], pTg[g]
                pbk, pk = self.pb()
                for jj in range(4):
                    for s_ in range(2):
                        fw.tr(pbk[:, (jj * 2 + s_) * 128:(jj * 2 + s_ + 1) * 128], pbf[:, jj, s_ * 128:(s_ + 1) * 128], identb[:, :], r=["pbf%d" % g, "identb"], w=[pk])
                fw.act(pT[:, :, :, :], pbk[:, :].rearrange("p (j s t) -> p j s t", j=4, s=2), AF.Copy, r=[pk], w=["pT%d" % g])

            def pv(g, pO, pok):
                pT = pTg[g]
                for c2 in range(2):
                    cc = g * 2 + c2
                    n = 0
                    for par in range(2):
                        jj = c2 * 2 + par
                        for s_ in range(2):
                            fw.mm(pO[:, cc * 128:(cc + 1) * 128], Vp[:, s_, g, par, :], pT[:, jj, s_, :], n == 0, n == 3,
                                  r=["Vp0", "Vp1", "pT%d" % g], w=[pok])
                            n += 1

            softmax(0)
            if i + 1 < NT:
                pre(i + 1)
            p_transposes(0)
            pO, pok = self.pf()
            pv(0, pO, pok)
            softmax(1)
            p_transposes(1)
            pv(1, pO, pok)
            fw.act(oT[:, :, :], pO[:, :].rearrange("p (c t) -> p c t", c=4), AF.Copy, r=[pok], w=["oT"])
            xo_, xok = xo[i % 2], "xo%d" % (i % 2)
            gate_out(128, hcur, hkk, oT, "oT", mr, mrk, xt, xk, xo_, xok, do_gates=False)
            fw.dma(self.xbuf[i * 128:(i + 1) * 128, :], xo_[:, :], r=[xok], w=[("xb", i)], key=xok)
        if NSEG > 1:
            self.gather_select(xo_[:, :], [xok], D, self.agX_in, self.agX_out, "agX")
            fw.dma(self.xh_dram, xo_[:, :], r=[xok], w=["xh_dram"], key="xhst")

        i = NT
        xt, xk = self.xt[i % 2], "xt%d" % (i % 2)
        src, _ = self.xsrc(l, i)
        fw.dma(xt[0:MS, :], src, r=[("xb", i)], w=[xk], key=xk)
        mr, mrk = mrl[i % 2], "mrl%d" % (i % 2)
        fw.dma(mr[:, :, 0:MS], self.mrbuf[NT].rearrange("p (c t) -> p c t", c=8)[:, :, 0:MS], r=[("mr", NT)], w=[mrk], key=mrk)
        ck_ = "cs%d" % (i % 2)
        fw.dma(cs[i % 2][0:MS, 0:32], I["c_coss"], w=[ck_], key=ck_)
        fw.dma(cs[i % 2][0:MS, 32:64], I["c_sins"], w=[ck_], key=ck_)
        self.norm_hT(xt, xk, MS, hT[:, :, 0:MS], "hT1", identb)
        hcur = lambda c: hT[:, c, 0:MS]
        proj_rope(MS, hcur, "hT1", cs[i % 2][0:MS, 0:32], cs[i % 2][0:MS, 32:64], ck_)
        for (cin, cout, srcap, srck, dkey) in [("ck", "s_k", rot[:, 512:640], "rot", "sk"), ("cv", "s_v", qkv[:, 640:768], "qkv1", "sv")]:
            fw.dma(O[cout][l, :, 0:124, :], I[cin][l, :, 4:128, :], w=[dkey], key=dkey + "c")
            for t in range(4):
                fw.dma(O[cout][l, :, 124 + t, :], srcap[t * 16:(t + 1) * 16, :], r=[srck], w=[dkey], key=dkey + "n")
        fw.dma(KA[:, :, :], O["s_k"][l].rearrange("q p c -> p q c"), r=["sk"], w=["KA"], key="KA")
        fw.dma(VA[:, :, :], O["s_v"][l].rearrange("q p c -> p q c"), r=["sv"], w=["VA"], key="VA")
        fw.dma(KB[:, :, :], I["ck"][l, :, 0:4, :].rearrange("q p c -> p q c"), w=["KB"], key="KB")
        fw.dma(VBt[:, :, :], I["cv"][l, :, 0:4, :].rearrange("q p c -> p q c"), w=["VB"], key="VB")
        self.P(lambda e: e.tensor_copy(VAb[:, :, :], VA[:, :, :]), r=["VA"], w=["VAb"])
        self.P(lambda e: e.tensor_copy(VBb[:, :, :], VBt[:, :, :]), r=["VB"], w=["VBb"])
        for q4 in range(4):
            ps, pk = self.pf()
            for qq in range(4):
                q = q4 * 4 + qq
                fw.tr(ps[:, qq * 128:(qq + 1) * 128], KA[:, q, :], identf[:, :], r=["KA", "identf"], w=[pk])
            fw.act(KAT[:, q4 * 4:(q4 + 1) * 4, :], ps[:, :].rearrange("p (q t) -> p q t", q=4), AF.Copy, r=[pk], w=["KAT"])
        ps, pk = self.pf()
        for q in range(NS):
            fw.tr(ps[:, q * 4:(q + 1) * 4], KB[0:4, q, :], identf[0:4, 0:4], r=["KB", "identf"], w=[pk])
        fw.act(KBT[:, :, :], ps[:, 0:64].rearrange("p (q t) -> p q t", q=NS), AF.Copy, r=[pk], w=["KBT"])
        q_transposes(MS, qT[:, :, 0:MS], "qT")
        for g in range(2):
            for jj in range(4):
                o = g * 64
                dst = qbd[o:o + 64, :, g * 16 + jj * 4:g * 16 + (jj + 1) * 4]
                srcq = qT[o:o + 64, jj, 0:MS].rearrange("p (t q) -> p q t", t=4)
                self.V(lambda e, dst=dst, srcq=srcq: e.tensor_copy(dst, srcq), r=["qT"], w=["qbd"])
        pSA = []
        for q4 in range(4):
            ps, pk = self.pf()
            pSA.append((ps, pk))
            for qq in range(4):
                q = q4 * 4 + qq
                fw.mm(ps[0:32, qq * 128:(qq + 1) * 128], qbd[:, q, :], KAT[:, q, :], True, True, r=["qbd", "KAT"], w=[pk])
        psB, pkB = self.pf()
        for q in range(NS):
            fw.mm(psB[0:32, q * 4:(q + 1) * 4], qbd[:, q, :], KBT[:, q, :], True, True, r=["qbd", "KBT"], w=[pkB])
        for q4, (ps, pk) in enumerate(pSA):
            self.V(lambda e, q4=q4, ps=ps: e.scalar_tensor_tensor(
                ssc[:, q4 * 4:(q4 + 1) * 4, 0:128], ps[0:32, :].rearrange("p (q c) -> p q c", q=4), 0.125,
                smask[:, 0:128].unsqueeze(1).to_broadcast([32, 4, 128]), ALU.mult, ALU.add), r=[pk, "smask"], w=["ssc"])
        self.V(lambda e: e.scalar_tensor_tensor(
            ssc[:, :, 128:132], psB[0:32, 0:64].rearrange("p (q c) -> p q c", q=NS), 0.125,
            smask[:, 128:132].unsqueeze(1).to_broadcast([32, NS, 4]), ALU.mult, ALU.add), r=[pkB, "smask"], w=["ssc"])
        sinkc = sbl("sinkc", [32, 1])
        for g in range(2):
            for jj in range(4):
                p0 = g * 16 + jj * 4
                fw.dma(sinkc[p0:p0 + 4, :], I["attn_sinks"][l, g * 4 + jj:g * 4 + jj + 1].partition_broadcast(4), w=["sinkc"], key="sinkc")
        self.V(lambda e: e.tensor_reduce(sst[:, 0:NS], ssc[:, :, :], AX.X, ALU.max), r=["ssc"], w=["sst"])
        self.V(lambda e: e.tensor_scalar(sst[:, 0:NS], sst[:, 0:NS], sinkc[:, 0:1], None, ALU.max), r=["sst", "sinkc"], w=["sst"])
        self.V(lambda e: e.tensor_tensor(ssc[:, :, :], ssc[:, :, :], bc3(sst[:, 0:NS], 132), ALU.subtract), r=["ssc", "sst"], w=["ssc"])
        fw.act(ssc[:, :, :], ssc[:, :, :], AF.Exp, r=["ssc"], w=["ssc"])
        self.V(lambda e: e.tensor_reduce(sst[:, NS:2 * NS], ssc[:, :, :], AX.X, ALU.add), r=["ssc"], w=["sst2"])
        self.V(lambda e: e.tensor_scalar(sst[:, 2 * NS:3 * NS], sst[:, 0:NS], sinkc[:, 0:1], None, ALU.subtract), r=["sst", "sinkc"], w=["sst3"])
        fw.act(sst[:, 2 * NS:3 * NS], sst[:, 2 * NS:3 * NS], AF.Exp, r=["sst3"], w=["sst3"], scale=-1.0)
        self.V(lambda e: e.tensor_tensor(sst[:, NS:2 * NS], sst[:, NS:2 * NS], sst[:, 2 * NS:3 * NS], ALU.add), r=["sst2", "sst3"], w=["sst2"])
        self.V(lambda e: e.reciprocal(sst[:, NS:2 * NS], sst[:, NS:2 * NS]), r=["sst2"], w=["sst2"])
        self.V(lambda e: e.tensor_tensor(spb[:, :, :], ssc[:, :, :], bc3(sst[:, NS:2 * NS], 132), ALU.mult), r=["ssc", "sst2"], w=["spb"])
        identb32 = identb[0:32, 0:32]
        for q8 in range(2):
            pbk, pk = self.pb()
            for qq in range(8):
                q = q8 * 8 + qq
                fw.tr(pbk[:, qq * 32:(qq + 1) * 32], spb[:, q, 0:128], identb32, r=["spb", "identb"], w=[pk])
            fw.act(spT[:, q8 * 8:(q8 + 1) * 8, :], pbk[:, 0:256].rearrange("p (q c) -> p q c", q=8), AF.Copy, r=[pk], w=["spT"])
        pbk, pk = self.pb()
        for q in range(NS):
            fw.tr(pbk[0:4, q * 32:(q + 1) * 32], spb[:, q, 128:132], identb32, r=["spb", "identb"], w=[pk])
        fw.act(spTB[:, :, :], pbk[0:4, 0:512].rearrange("p (q c) -> p q c", q=NS), AF.Copy, r=[pk], w=["spTB"])
        pO, pok = self.pf()
        for q in range(NS):
            fw.mm(pO[:, q * 32:(q + 1) * 32], VAb[:, q, :], spT[:, q, :], True, False, r=["VAb", "spT"], w=[pok])
            fw.mm(pO[:, q * 32:(q + 1) * 32], VBb[0:4, q, :], spTB[0:4, q, :], False, True, r=["VBb", "spTB"], w=[pok])
        oraw = sbl("oraw", [128, 32, NS], BF)
        fw.act(oraw.rearrange("p c q -> p q c"), pO[:, :].rearrange("p (q c) -> p q c", q=NS), AF.Copy, r=[pok], w=["oraw"])
        for g in range(2):
            for jj in range(4):
                cc, par = g * 2 + jj // 2, jj % 2
                c0 = g * 16 + jj * 4
                srco = oraw[g * 64:(g + 1) * 64, c0:c0 + 4, :].rearrange("p t q -> p (t q)")
                fw.dma(oTs[par * 64:(par + 1) * 64, cc, :], srco, r=["oraw"], w=["oTs"], key="oTs")
        xo_, xok = xo[i % 2], "xo%d" % (i % 2)
        gate_out(MS, hcur, "hT1", oTs, "oTs", mr, mrk, xt, xk, xo_, xok)
        fw.dma(self.xsbuf, xo_[0:MS, :], r=[xok], w=[("xb", NT)], key=xok)

    def pass_ffn(self, l, es2):
        fw, I, O, NT = self.fw, self.I, self.O, self.NT
        sbl = lambda n, s, dt=F32: self.sbl(es2, "f%d_" % l + n, s, dt)
        identb, identf = self.identb, self.identf
        Wc = sbl("Wc", [128, 8, DFF], BF)
        Wu = sbl("Wu", [128, 8, DFF], BF)
        Wd = sbl("Wd", [128, NFC, D], BF)
        self.col_load(self.gcol[:], "gcol", I["norm_ffn_g"][l], 8)
        cw = sbl("cw", [128, 4, NFC])
        for j in range(3):
            self.col_load(cw[:, j, :], "cw", I["ffn_conv_w"][l, j], NFC)
        self.col_load(cw[:, 3, :], "cw", I["ffn_conv_b"][l], NFC)
        m0 = self.aoff
        self.wstage = [sbl("wst%d" % i_, [128, 2048]) for i_ in range(2)]
        wi = I["ffn_w_in"][l]
        gsc = lambda c: self.gcol[:, c:c + 1]
        self.prep_w(8, DFF, lambda c, s0, n: wi[c * 128:(c + 1) * 128, s0:s0 + n],
                    lambda c, s0, n: Wc[:, c, s0:s0 + n], lambda c: "Wc_%d" % c, "col", gsc)
        self.prep_w(8, DFF, lambda c, s0, n: wi[c * 128:(c + 1) * 128, DFF + s0:DFF + s0 + n],
                    lambda c, s0, n: Wu[:, c, s0:s0 + n], lambda c: "Wu_%d" % c, "col", gsc)
        wd = I["ffn_w_down"][l]
        self.prep_w(NFC, D, lambda c, s0, n: wd[c * 128:(c + 1) * 128, s0:s0 + n],
                    lambda c, s0, n: Wd[:, c, s0:s0 + n], lambda c: "Wd_%d" % c, "plain")
        self.release(m0)
        last = (l == 1)
        if last:
            gf = sbl("gf", [128, D])
            self.bcast_load(gf[:], "gf", I["norm_final_g"])
        hTd = [sbl("hT%d" % i_, [128, 8, 128], BF) for i_ in range(2)]
        hT = hTd[0]
        cxf = sbl("cx", [128, NFC * 130])
        cx1 = cxf.rearrange("p (f t) -> p f t", f=NFC)
        cxs = cxf[:, 0:NFC * NS * 6].rearrange("p (f q j) -> p f q j", f=NFC, q=NS)
        acc = [sbl("acc%d" % i_, [128, 4, 128]) for i_ in range(2)]
        aTd = [sbl("aT%d" % i_, [128, NFC, 128], BF) for i_ in range(2)]
        xo = [sbl("xo%d" % i_, [128, D]) for i_ in range(2)]
        ctok = sbl("ctok", [128, DFF])
        cst = ctok
        jk = self.xn

        def finish(M, xt, xk, xo_, xok, dst_final, dst_x, dkey, aT, aTk):
            for grp in range(2):
                px, pxk = self.pf()
                for fc in range(NFC):
                    fw.mm(px[0:M, :], aT[:, fc, 0:M], Wd[:, fc, grp * 512:(grp + 1) * 512], fc == 0, fc == NFC - 1, r=[aTk, "Wd_%d" % fc], w=[pxk])
                self.V(lambda e, grp=grp, px=px: e.tensor_tensor(xo_[0:M, grp * 512:(grp + 1) * 512], xt[0:M, grp * 512:(grp + 1) * 512], px[0:M, :], ALU.add),
                       r=[xk, pxk], w=[xok])
            if not last:
                fw.dma(dst_x, xo_[0:M, :], r=[xok], w=[dkey], key=xok)
                return
            ss, t1 = self.ss, self.t1
            fw.act(jk[0:M, :], xo_[0:M, :], AF.Square, r=[xok], w=["xn", "ss"], accum_out=ss[0:M, :])
            self.V(lambda e: e.tensor_scalar(t1[0:M, :], ss[0:M, :], 1.0 / D, 1e-6, ALU.mult, ALU.add), r=["ss"], w=["t1"])
            fw.act(t1[0:M, :], t1[0:M, :], AF.Sqrt, r=["t1"], w=["t1"])
            self.V(lambda e: e.reciprocal(t1[0:M, :], t1[0:M, :]), r=["t1"], w=["t1"])
            self.V(lambda e: e.scalar_tensor_tensor(xo_[0:M, :], xo_[0:M, :], t1[0:M, 0:1], gf[0:M, :], ALU.mult, ALU.mult),
                   r=[xok, "t1", "gf"], w=[xok])
            fw.dma(dst_final, xo_[0:M, :], r=[xok], key=xok)

        def ffn_core(M, hcur, hk, cview, ckey, sample, aT, aTk, mid=None, groups=None):
            for b0 in (groups if groups is not None else range(0, NFC, 4)):
                nb = min(4, NFC - b0)
                pc, pck = self.pf()
                for q in range(nb):
                    fc = b0 + q
                    for c in range(8):
                        fw.mm(pc[:, q * M:(q + 1) * M], Wc[:, c, fc * 128:(fc + 1) * 128], hcur(c), c == 0, c == 7, r=[hk, "Wc_%d" % c], w=[pck])
                pu, puk = self.pf()
                for q in range(nb):
                    fc = b0 + q
                    for c in range(8):
                        fw.mm(pu[:, q * M:(q + 1) * M], Wu[:, c, fc * 128:(fc + 1) * 128], hcur(c), c == 0, c == 7, r=[hk, "Wu_%d" % c], w=[puk])
                if sample:
                    fw.act(cview[:, b0:b0 + nb, :, 2:6], pc[:, 0:nb * M].rearrange("p (f t q) -> p f q t", f=nb, t=4), AF.Copy, r=[pck], w=[ckey])
                else:
                    fw.act(cview[:, b0:b0 + nb, 2:130], pc[:, 0:nb * M].rearrange("p (f t) -> p f t", f=nb), AF.Copy, r=[pck], w=[ckey])
                a_ = acc[(b0 // 4) % 2]
                ak = "acc%d" % ((b0 // 4) % 2)
                for q in range(nb):
                    fc = b0 + q
                    if sample:
                        c0, c1, c2 = (cview[:, fc, :, s_:s_ + 4] for s_ in range(3))
                        av = a_[:, q, 0:M].rearrange("p (t q) -> p q t", t=4)
                    else:
                        c0, c1, c2 = (cview[:, fc, s_:s_ + 128] for s_ in range(3))
                        av = a_[:, q, :]
                    self.P(lambda e, av=av, c0=c0, fc=fc: e.tensor_scalar(av, c0, cw[:, 0, fc:fc + 1], cw[:, 3, fc:fc + 1], ALU.mult, ALU.add),
                           r=[ckey, "cw"], w=[ak])
                    self.V(lambda e, av=av, c1=c1, fc=fc: e.scalar_tensor_tensor(av, c1, cw[:, 1, fc:fc + 1], av, ALU.mult, ALU.add),
                           r=[ckey, "cw", ak], w=[ak])
                    self.V(lambda e, av=av, c2=c2, fc=fc: e.scalar_tensor_tensor(av, c2, cw[:, 2, fc:fc + 1], av, ALU.mult, ALU.add),
                           r=[ckey, "cw", ak], w=[ak])
                fw.act(a_[:, 0:nb, 0:M], a_[:, 0:nb, 0:M], AF.Gelu, r=[ak], w=[ak])
                self.V(lambda e, a_=a_, pu=pu, nb=nb, b0=b0, aT=aT: e.tensor_tensor(aT[:, b0:b0 + nb, 0:M], a_[:, 0:nb, 0:M],
                                                                              pu[:, 0:nb * M].rearrange("p (f t) -> p f t", f=nb), ALU.mult),
                       r=[ak, puk], w=[aTk])
                if mid is not None and b0 == 8:
                    mid()

        def c_token_major(M, hcur, hk, rows, dsts):
            for g0 in range(0, DFF, 512):
                n = min(512, DFF - g0)
                ps, pk = self.pf()
                for c in range(8):
                    fw.mm(ps[0:M, 0:n], hcur(c), Wc[:, c, g0:g0 + n], c == 0, c == 7, r=[hk, "Wc_%d" % c], w=[pk])
                fw.act(ctok[0:M, g0:g0 + n], ps[0:M, 0:n], AF.Copy, r=[pk], w=["ctok"])
            for (r0, r1), dst in zip(rows, dsts):
                fw.dma(dst, ctok[r0:r1, :], r=["ctok"], key="ctok")

        xt, xk = self.xt[1], "xt1"
        fw.dma(xt[:], (I["xh0"] if NSEG == 1 else self.xh_dram), r=["xh_dram"], w=[xk], key=xk)
        self.norm_hT(xt, xk, 128, hT[:, :, :], "hT0", identb)
        pc, pck = self.pf()
        for fc in range(NFC):
            for c in range(8):
                fw.mm(pc[:, fc * 2:(fc + 1) * 2], Wc[:, c, fc * 128:(fc + 1) * 128], hT[:, c, 126:128], c == 0, c == 7, r=["hT0", "Wc_%d" % c], w=[pck])
        fw.act(cx1[:, :, 0:2], pc[:, 0:2 * NFC].rearrange("p (f t) -> p f t", f=NFC), AF.Copy, r=[pck], w=["cx"])
        def pre(i):
            xt, xk = self.xt[i % 2], "xt%d" % (i % 2)
            fw.dma(xt[:], self.xbuf[i * 128:(i + 1) * 128, :], r=[("xb", i)], w=[xk], key=xk)
            self.norm_hT(xt, xk, 128, hTd[i % 2][:, :, :], "hT%d" % (i % 2), identb)

        def head(i):
            if i > 0:
                self.P(lambda e: e.tensor_copy(acc[0][:, 0, 0:2 * NFC].rearrange("p (f t) -> p f t", f=NFC), cx1[:, :, 128:130]), r=["cx"], w=["acc0"])
                self.P(lambda e: e.tensor_copy(cx1[:, :, 0:2], acc[0][:, 0, 0:2 * NFC].rearrange("p (f t) -> p f t", f=NFC)), r=["acc0"], w=["cx"])
            ffn_core(128, lambda c, i=i: hTd[i % 2][:, c, :], "hT%d" % (i % 2), cx1, "cx", False, aTd[i % 2], "aT%d" % (i % 2), groups=[0])

        pre(0)
        head(0)
        for i in range(NT):
            xt, xk = self.xt[i % 2], "xt%d" % (i % 2)
            hcur = lambda c, i=i: hTd[i % 2][:, c, :]
            hkk = "hT%d" % (i % 2)
            mid = (lambda i=i: pre(i + 1)) if i + 1 < NT else None
            ffn_core(128, hcur, hkk, cx1, "cx", False, aTd[i % 2], "aT%d" % (i % 2), mid, groups=list(range(4, NFC, 4)))
            if i == NT - 1:
                c_token_major(128, hcur, hkk, [(126, 128)], [O["p_conv"][l]])
            if i + 1 < NT:
                head(i + 1)
            xo_, xok = xo[i % 2], "xo%d" % (i % 2)
            finish(128, xt, xk, xo_, xok, O["yp"][i * 128:(i + 1) * 128, :], self.xbuf[i * 128:(i + 1) * 128, :], ("xb", i),
                   aTd[i % 2], "aT%d" % (i % 2))
        if not last and NSEG > 1:
            self.gather_select(xo_[:, :], [xok], D, self.agX_in, self.agX_out, "agX")
            fw.dma(self.xh_dram, xo_[:, :], r=[xok], w=["xh_dram"], key="xhst")

        i = NT
        xt, xk = self.xt[i % 2], "xt%d" % (i % 2)
        fw.dma(xt[0:MS, :], self.xsbuf, r=[("xb", i)], w=[xk], key=xk)
        self.norm_hT(xt, xk, MS, hT[:, :, 0:MS], "hT0", identb)
        hcur = lambda c: hT[:, c, 0:MS]
        fw.dma(cst[0:32, :], I["st_conv"][l], w=["ctok"], key="cst")
        for b0 in range(0, NFC, 4):
            nb = min(4, NFC - b0)
            ps, pk = self.pf()
            for q in range(nb):
                fc = b0 + q
                fw.tr(ps[:, q * 32:(q + 1) * 32], cst[0:32, fc * 128:(fc + 1) * 128], identf[0:32, 0:32], r=["ctok", "identf"], w=[pk])
            fw.act(cxs[:, b0:b0 + nb, :, 0:2], ps[:, 0:nb * 32].rearrange("p (f q j) -> p f q j", f=nb, j=2), AF.Copy, r=[pk], w=["cx"])
        ffn_core(MS, hcur, "hT0", cxs, "cx", True, aTd[0], "aT0")
        sc_ = O["s_conv"][l].rearrange("(q j) f -> j q f", j=2)
        c_token_major(MS, hcur, "hT0", [(32, 48), (48, 64)], [sc_[0], sc_[1]])
        xo_, xok = xo[i % 2], "xo%d" % (i % 2)
        finish(MS, xt, xk, xo_, xok, O["ys"], self.xsbuf, ("xb", NT), aTd[0], "aT0")


NSEG = 1


def _consts_shared():
    c = {}
    c["c_ident"] = np.eye(128, dtype=np.float32)
    inv = (10000.0 ** (-np.arange(0, HD, 2, dtype=np.float32) / HD)).astype(np.float32)
    pos_s = (PAST + np.repeat(np.arange(4), NS)).astype(np.float32)
    ang_s = pos_s[:, None] * inv[None, :]
    c["c_coss"] = np.cos(ang_s).astype(np.float32)
    c["c_sins"] = np.sin(ang_s).astype(np.float32)
    s = np.arange(128)[:, None]
    t = np.arange(128)[None, :]
    incl = (s <= t).astype(np.float32)
    strict = (s < t).astype(np.float32)
    c["c_tri"] = np.concatenate([incl * CDEC, strict * CDEC], 1).astype(np.float32)
    c["c_mask2"] = np.concatenate([incl, strict], 1).astype(np.float32)
    c["c_maskL"] = (s > t).astype(np.float32)
    i_ = np.arange(128)[:, None]
    j_ = np.arange(128)[None, :]
    cur = np.where(j_ <= i_, 0.0, NEG)
    prev = np.where(j_ > i_, 0.0, NEG)
    dead = np.full((128, 128), NEG)
    c["c_amask"] = np.concatenate([cur, prev, prev, cur, cur, dead], 1).astype(np.float32)
    c["_am_first"] = np.concatenate([cur, dead], 1).astype(np.float32)
    c["_am_mid"] = np.concatenate([cur, prev], 1).astype(np.float32)
    tt = (np.arange(32) % 4)[:, None]
    ia = np.arange(128)[None, :]
    ma = np.where(ia <= 124 + tt, 0.0, NEG)
    rb = np.arange(4)[None, :]
    mb = np.where(rb > tt, 0.0, NEG)
    c["c_smask"] = np.concatenate([ma, mb], 1).astype(np.float32)
    last = np.zeros((128, 1), np.float32)
    last[127, 0] = 1.0
    c["c_last"] = last
    c["_inv"] = inv
    return c


def _rope_tab(pos, inv):
    ang = pos.astype(np.float32)[:, None] * inv[None, :]
    return np.cos(ang).astype(np.float32), np.sin(ang).astype(np.float32)


_CACHE = {}
TAPS = False
TAP_OUT = {}


def kernel(**inp):
    inp = {k: np.asarray(v) for k, v in inp.items()}
    xp_all = inp["x_prompt"].astype(np.float32)
    B, SEQ_, _ = xp_all.shape
    TPC = SEQ_ // NSEG
    if TPC not in _CACHE:
        b_ = Builder(TPC, taps=TAPS)
        _CACHE[TPC] = (b_.build(), b_.tapnames)
    nc, tapnames = _CACHE[TPC]
    consts = _consts_shared()
    inv = consts.pop("_inv")
    am_first, am_mid = consts.pop("_am_first"), consts.pop("_am_mid")
    wnames = ["norm_mix_g", "w_in", "rwkv_mu", "rwkv_w0", "rwkv_w2", "rwkv_a0", "rwkv_a2", "rwkv_g2", "rwkv_k_k",
              "rwkv_k_a", "rwkv_ln_g", "rwkv_ln_b", "attn_sinks", "w_br_rwkv", "w_br_attn", "w_out", "norm_ffn_g",
              "ffn_w_in", "ffn_conv_w", "ffn_conv_b", "ffn_w_down", "norm_final_g"]
    shared = {n: np.ascontiguousarray(inp[n], dtype=np.float32) for n in wnames}
    shared["rwkv_r_k"] = np.ascontiguousarray(inp["rwkv_r_k"], dtype=np.float32).reshape(2, RD)
    shared.update(consts)
    in_maps = []
    ncores = 8
    for c in range(ncores):
        b, seg = (c // NSEG) % B, c % NSEG
        sl = slice(c * NS, (c + 1) * NS)
        m = dict(shared)
        t0 = seg * TPC
        m["xp"] = np.ascontiguousarray(xp_all[b, t0:t0 + TPC])
        m["xh0"] = np.ascontiguousarray(xp_all[b, t0 - 128:t0]) if seg > 0 else np.zeros((128, D), np.float32)
        m["c_cosp"], m["c_sinp"] = _rope_tab(t0 + np.arange(TPC), inv)
        m["c_cosh"], m["c_sinh"] = _rope_tab(np.maximum(t0 - 128 + np.arange(128), 0), inv)
        m["c_amask0"] = am_mid if seg > 0 else am_first
        sel = np.zeros((128, 8), np.float32)
        if seg > 0:
            sel[:, c - 1] = 1.0
        m["c_sel"] = sel
        m["xs"] = np.ascontiguousarray(inp["x_sample"][sl].transpose(1, 0, 2).reshape(MS, D))
        m["st_shift"] = np.ascontiguousarray(inp["state_rwkv_shift"][:, sl])
        m["st_wkv"] = np.ascontiguousarray(inp["state_rwkv_wkv"][:, sl]).reshape(2, 128, 4096)
        m["ck"] = np.ascontiguousarray(inp["cache_swa_k"][:, sl]).reshape(2, NS, 128, 128)
        m["cv"] = np.ascontiguousarray(inp["cache_swa_v"][:, sl]).reshape(2, NS, 128, 128)
        m["st_conv"] = np.ascontiguousarray(inp["state_ffn_conv"][:, sl]).reshape(2, 2 * NS, DFF)
        in_maps.append(m)
    res = run_bass_kernel_spmd(nc, in_maps, core_ids=list(range(ncores)))
    R = res.results
    for tn in tapnames:
        TAP_OUT[tn] = [np.asarray(R[c][tn]) for c in range(ncores)]
    f = np.float32
    lastc = [b * NSEG + NSEG - 1 for b in range(B)]
    y_prompt = np.stack([np.concatenate([R[b * NSEG + sg]["yp"] for sg in range(NSEG)], 0) for b in range(B)]).astype(f)
    y_sample = np.concatenate([R[c]["ys"].reshape(4, NS, D).transpose(1, 0, 2) for c in range(ncores)], 0).astype(f)
    p_shift = np.stack([R[c]["p_shift"] for c in lastc], 1).astype(f)
    p_wkv = np.stack([R[c]["p_wkv"] for c in lastc], 1).astype(f)
    p_k = np.stack([R[c]["p_k"] for c in lastc], 1).reshape(2, B, 128, 2, 64).astype(f)
    p_v = np.stack([R[c]["p_v"] for c in lastc], 1).reshape(2, B, 128, 2, 64).astype(f)
    p_conv = np.stack([R[c]["p_conv"] for c in lastc], 1).astype(f)
    s_shift = np.concatenate([R[c]["s_shift"] for c in range(ncores)], 1).astype(f)
    s_wkv = np.concatenate([R[c]["s_wkv"].reshape(2, NS, NH, 64, 64) for c in range(ncores)], 1).astype(f)
    s_k = np.concatenate([R[c]["s_k"].reshape(2, NS, 128, 2, 64) for c in range(ncores)], 1).astype(f)
    s_v = np.concatenate([R[c]["s_v"].reshape(2, NS, 128, 2, 64) for c in range(ncores)], 1).astype(f)
    s_conv = np.concatenate([R[c]["s_conv"].reshape(2, NS, 2, DFF) for c in range(ncores)], 1).astype(f)
    return (y_prompt, y_sample, p_shift, p_wkv, p_k, p_v, p_conv, s_shift, s_wkv, s_k, s_v, s_conv)
```

```python
import math
from contextlib import ExitStack

import numpy as np
import concourse.bass as bass
import concourse.mybir as mybir
from concourse.bass_utils import run_bass_kernel_spmd

F32 = mybir.dt.float32
BF = mybir.dt.bfloat16
AF = mybir.ActivationFunctionType
ALU = mybir.AluOpType
AX = mybir.AxisListType

ENGS = ["sp", "pe", "act", "dve", "pool"]
DEBUG_WHERE = True

D = 1024
HD = 64
NH = 8
RD = 512
RP = 1792
INP = 4608
DFF = 2816
NFC = 22
NS = 16
MS = 64
PAST = 16384
CDEC = -math.exp(-0.5)
NEG = -30000.0


class FW:
    def __init__(self, nc, es):
        self.nc = nc
        self.es = es
        self.ops = {e: [] for e in ENGS}
        self.lastw = {}
        self.readers = {}
        self.dma_count = {}
        self.inc = {}

    def sb(self, name, shape, dt=F32):
        return self.es.enter_context(self.nc.sbuf_tensor(name, list(shape), dt))

    def ps(self, name, shape, dt=F32):
        return self.es.enter_context(self.nc.psum_tensor(name, list(shape), dt))

    def capture(self, f):
        self.cap = []
        f()
        log, self.cap = self.cap, None
        return log

    def replay(self, logs, chunk=2):
        logs = [list(lg) for lg in logs if lg]
        if not logs:
            return
        mn = min(len(lg) for lg in logs)
        per = [max(1, int(round(chunk * len(lg) / mn))) for lg in logs]
        pos = [0] * len(logs)
        while any(p < len(lg) for p, lg in zip(pos, logs)):
            for k, lg in enumerate(logs):
                for _ in range(per[k]):
                    if pos[k] < len(lg):
                        self.op(*lg[pos[k]])
                        pos[k] += 1

    def op(self, eng, fn, r=(), w=(), dma=None):
        if getattr(self, "cap", None) is not None:
            self.cap.append((eng, fn, tuple(r), tuple(w), dma))
            return
        ops = self.ops[eng]
        idx = len(ops)
        deps = set()
        pr = [k for k in r if isinstance(k, str) and k[:2] in ("ps", "pb") and k[2:].isdigit()]
        if pr:
            r = [k for k in r if k not in pr]
            w = list(w) + pr
        for k in r:
            t = self.lastw.get(k)
            if t is not None:
                deps.add(t)
        for k in w:
            t = self.lastw.get(k)
            if t is not None:
                deps.add(t)
            for t2 in self.readers.get(k, {}).values():
                deps.add(t2)
        if dma is not None:
            c = self.dma_count.get(dma, 0) + 1
            self.dma_count[dma] = c
            tok = ("d", dma, c)
        else:
            tok = ("c", eng, idx)
        if eng == "pe":
            deps = {d for d in deps if not (d[0] == "c" and d[1] == "pe")}
        deps.discard(tok)
        rec = dict(fn=fn, deps=deps, tok=tok, signal=False)
        if DEBUG_WHERE:
            import sys as _s
            f_ = _s._getframe(1)
            wh = []
            while f_ is not None and len(wh) < 4:
                wh.append(f_.f_lineno)
                f_ = f_.f_back
            rec["where"] = wh
        ops.append(rec)
        for d in deps:
            if d[0] == "c":
                self.ops[d[1]][d[2]]["signal"] = True
        for k in w:
            self.lastw[k] = tok
            self.readers[k] = {}
        for k in r:
            rk = ("d", tok[1]) if tok[0] == "d" else tok[1]
            self.readers.setdefault(k, {})[rk] = tok
        return tok

    def fence(self):
        toks = set()
        for e in ENGS:
            for rec in reversed(self.ops[e]):
                if rec["tok"][0] == "c" and rec["fn"] is not None:
                    toks.add(rec["tok"])
                    rec["signal"] = True
                    break
        for k, c in self.dma_count.items():
            toks.add(("d", k, c))
        for e in ENGS:
            self.ops[e].append(dict(fn=None, deps=set(toks), tok=("c", e, len(self.ops[e])), signal=False))

    def dma(self, out, in_, r=(), w=(), key=None, eng="sp", **kw):
        self.op(eng, lambda e: e.dma_start(out=out, in_=in_, **kw), r=r, w=w, dma=key)

    def mm(self, out, lhsT, rhs, start, stop, r=(), w=()):
        self.op("pe", lambda e: e.matmul(out, lhsT, rhs, start=start, stop=stop), r=r, w=w)

    def tr(self, out, in_, ident, r=(), w=()):
        self.op("pe", lambda e: e.transpose(out, in_, ident), r=r, w=w)

    def act(self, out, in_, func, r=(), w=(), **kw):
        self.op("act", lambda e: e.activation(out, in_, func, **kw), r=r, w=w)

    def emit(self):
        nc = self.nc
        sems = {e: self.es.enter_context(nc.semaphore("s_" + e)) for e in ENGS}
        dsems = {}
        for i, k in enumerate(self.dma_count):
            dsems[k] = self.es.enter_context(nc.semaphore("d%d" % i))
        for e in ENGS:
            c = 0
            for rec in self.ops[e]:
                if rec["signal"] and rec["tok"][0] == "c":
                    c += 1
                rec["sigval"] = c
        final_counts = dict(self.dma_count)

        def run(engname, eng):
            waited = {}
            for rec in self.ops[engname]:
                need = {}
                for d in rec["deps"]:
                    if d[0] == "c":
                        s = ("c", d[1])
                        v = self.ops[d[1]][d[2]]["sigval"]
                    else:
                        s = ("d", d[1])
                        v = self.inc.get(d[1], 16) * d[2]
                    if need.get(s, 0) < v:
                        need[s] = v
                for s, v in need.items():
                    if waited.get(s, 0) >= v:
                        continue
                    waited[s] = v
                    eng.wait_ge(sems[s[1]] if s[0] == "c" else dsems[s[1]], v)
                if rec["fn"] is None:
                    continue
                try:
                    ins = rec["fn"](eng)
                except Exception:
                    print("EMIT FAILURE at lines", rec.get("where"), "engine", engname)
                    raise
                if rec["tok"][0] == "d":
                    ins.then_inc(dsems[rec["tok"][1]], self.inc.get(rec["tok"][1], 16))
                elif rec["signal"]:
                    ins.then_inc(sems[engname], 1)
            if engname == "sp":
                for k, c in final_counts.items():
                    v = self.inc.get(k, 16) * c
                    if waited.get(("d", k), 0) < v:
                        eng.wait_ge(dsems[k], v)

        with nc.Block() as block:
            @block.sync
            def _(e):
                run("sp", e)

            @block.tensor
            def _(e):
                run("pe", e)

            @block.scalar
            def _(e):
                run("act", e)

            @block.vector
            def _(e):
                run("dve", e)

            @block.gpsimd
            def _(e):
                run("pool", e)


def bc3(ap2, n):
    s = list(ap2.shape)
    return ap2.unsqueeze(2).to_broadcast([s[0], s[1], n])


def h3(ap2, h=NH):
    return ap2.rearrange("p (h d) -> p h d", h=h)


class Builder:
    def __init__(self, TP, taps=False):
        self.TP = TP
        self.NT = TP // 128
        self.taps = taps
        self.nc = bass.Bass("TRN2", target_bir_lowering=False)
        self.I = {}
        self.O = {}
        self.psi = 0
        self.pbi = 0
        self.tapnames = []
        self.pool = None
        self.pcnt = {}

    def din(self, n, s):
        self.I[n] = self.nc.dram_tensor(n, list(s), F32, kind="ExternalInput").ap()

    def dout(self, n, s):
        self.O[n] = self.nc.dram_tensor(n, list(s), F32, kind="ExternalOutput").ap()

    def declare(self):
        TP = self.TP
        for n, s in [("xp", (TP, D)), ("xs", (MS, D)), ("st_shift", (2, NS, RP)), ("st_wkv", (2, 128, 4096)),
                     ("ck", (2, NS, 128, 128)), ("cv", (2, NS, 128, 128)), ("st_conv", (2, 2 * NS, DFF)),
                     ("norm_mix_g", (2, D)), ("w_in", (2, D, INP)), ("rwkv_mu", (2, RP)), ("rwkv_w0", (2, RD)),
                     ("rwkv_w2", (2, 64, RD)), ("rwkv_a0", (2, RD)), ("rwkv_a2", (2, 64, RD)),
                     ("rwkv_g2", (2, 128, RD)), ("rwkv_k_k", (2, RD)), ("rwkv_k_a", (2, RD)),
                     ("rwkv_r_k", (2, RD)), ("rwkv_ln_g", (2, RD)), ("rwkv_ln_b", (2, RD)),
                     ("attn_sinks", (2, NH)), ("w_br_rwkv", (2, RD, D)), ("w_br_attn", (2, RD, D)),
                     ("w_out", (2, D, D)), ("norm_ffn_g", (2, D)), ("ffn_w_in", (2, D, 2 * DFF)),
                     ("ffn_conv_w", (2, 3, DFF)), ("ffn_conv_b", (2, DFF)), ("ffn_w_down", (2, DFF, D)),
                     ("norm_final_g", (D,)),
                     ("c_ident", (128, 128)), ("c_cosp", (TP, 32)), ("c_sinp", (TP, 32)),
                     ("c_coss", (MS, 32)), ("c_sins", (MS, 32)), ("c_tri", (128, 256)),
                     ("c_mask2", (128, 256)), ("c_maskL", (128, 128)), ("c_amask", (128, 768)),
                     ("c_smask", (32, 132)), ("c_last", (128, 1)),
                     ("xh0", (128, D)), ("c_cosh", (128, 32)), ("c_sinh", (128, 32)), ("c_amask0", (128, 256)), ("c_sel", (128, 8))]:
            self.din(n, s)
        for n, s in [("yp", (TP, D)), ("ys", (MS, D)), ("p_shift", (2, RP)), ("p_wkv", (2, NH, 64, 64)),
                     ("p_k", (2, 128, 128)), ("p_v", (2, 128, 128)), ("p_conv", (2, 2, DFF)),
                     ("s_shift", (2, NS, RP)), ("s_wkv", (2, 128, 4096)), ("s_k", (2, NS, 128, 128)),
                     ("s_v", (2, NS, 128, 128)), ("s_conv", (2, 2 * NS, DFF))]:
            self.dout(n, s)
        nc = self.nc
        self.xbuf = nc.dram_tensor("xbuf", [TP, D], F32).ap()
        self.xsbuf = nc.dram_tensor("xsbuf", [MS, D], F32).ap()
        self.mrbuf = nc.dram_tensor("mrbuf", [self.NT + 1, 128, 1024], BF).ap()
        self.xh_dram = nc.dram_tensor("xh_dram", [128, D], F32).ap()
        self.sq = nc.dram_tensor("sq", [6, MS, RD], F32).ap()
        self.sy = nc.dram_tensor("sy", [MS, RD], F32).ap()

    def alloc(self, name, shape, dt=F32):
        shape = list(shape)
        n = 1
        for d_ in shape[1:]:
            n *= d_
        nbytes = n * (4 if dt == F32 else 2)
        nw = (nbytes + 31) // 32 * 8
        off = self.aoff
        self.aoff += nw
        self.apeak = max(self.apeak, self.aoff)
        assert self.aoff <= self.ASZ, "SBUF arena overflow: %s needs %d words (limit %d)" % (name, self.aoff, self.ASZ)
        ap = self.arena[0:shape[0], off:off + nw]
        if dt != F32:
            ap = ap.bitcast(dt)
        ap = ap[:, 0:n]
        if len(shape) > 2:
            names = ["d%d" % i for i in range(len(shape) - 1)]
            pat = "p (%s) -> p %s" % (" ".join(names), " ".join(names))
            ap = ap.rearrange(pat, **{names[i]: shape[i + 1] for i in range(len(names))})
        return ap

    def release(self, mark):
        self.fw.fence()
        self.aoff = mark

    def pf(self):
        ids = {None: [0, 1, 2, 3, 4, 5], 0: [0, 1, 2], 1: [3, 4, 5]}[self.pool]
        c = self.pcnt.setdefault(("f", self.pool), 0)
        self.pcnt[("f", self.pool)] = c + 1
        k = ids[c % len(ids)]
        return self.PS[k], "ps%d" % k

    def pb(self):
        ids = {None: [0, 1], 0: [0], 1: [1]}[self.pool]
        c = self.pcnt.setdefault(("b", self.pool), 0)
        self.pcnt[("b", self.pool)] = c + 1
        k = ids[c % len(ids)]
        return self.PBK[k], "pb%d" % k

    def tap(self, name, ap, rkeys, dt=F32):
        if not self.taps:
            return
        shp = list(ap.shape)
        t = self.nc.dram_tensor("tap_" + name, shp, dt, kind="ExternalOutput").ap()
        self.tapnames.append("tap_" + name)
        self.fw.dma(t, ap, r=rkeys, key="tap_" + name)

    def V(self, fn, r=(), w=()):
        self.fw.op("dve", fn, r, w)

    def P(self, fn, r=(), w=()):
        self.fw.op("pool", fn, r, w)

    def col_load(self, dst, dkey, vec, n):
        fw = self.fw
        st = self.cstage
        fw.dma(st[0:n, :], vec.rearrange("(c p) -> c p", p=128), w=["cstage"], key="cstage")
        ps, pk = self.pf()
        fw.tr(ps[:, 0:n], st[0:n, :], self.identf[0:n, 0:n], r=["cstage", "identf"], w=[pk])
        fw.act(dst, ps[:, 0:n], AF.Copy, r=[pk], w=[dkey])

    def gather_select(self, src_ap, src_keys, n, ag_in, ag_out, name):
        fw = self.fw
        fw.dma(ag_in, src_ap, r=src_keys, w=[name + "_in"], key=name + "_st")
        self.gi = getattr(self, "gi", 0)
        ck = name + "_cc"
        fw.inc[ck] = 1
        fw.op("pool", lambda e: e.collective_compute("AllGather", ALU.bypass, replica_groups=[list(range(8))], ins=[ag_in], outs=[ag_out]),
              r=[name + "_in"], w=[name + "_out"], dma=ck)
        for r_ in range(8):
            st, sk = self.xt[r_ % 2], "xt%d" % (r_ % 2)
            fw.dma(st[:, 0:n], ag_out[r_ * 128:(r_ + 1) * 128, :], r=[name + "_out"], w=[sk], key=sk)
            if r_ == 0:
                self.V(lambda e, st=st: e.tensor_scalar(src_ap, st[:, 0:n], self.sel[:, 0:1], None, ALU.mult), r=[sk, "sel"], w=src_keys)
            else:
                self.V(lambda e, st=st, r_=r_: e.scalar_tensor_tensor(src_ap, st[:, 0:n], self.sel[:, r_:r_ + 1], src_ap, ALU.mult, ALU.add),
                       r=[sk, "sel"] + list(src_keys), w=src_keys)

    def bcast_load(self, dst, dkey, vec):
        self.fw.dma(dst, vec.partition_broadcast(dst.shape[0]), w=[dkey], key=dkey)

    def prep_w(self, nchunks, ncols, src, dst, dkey, mode, scale=None, mul=None, mulkey=None, sview=None):
        fw = self.fw
        for c in range(nchunks):
            for s0 in range(0, ncols, 2048):
                n = min(2048, ncols - s0)
                k = self.wst_i % 4
                self.wst_i += 1
                st = self.wstage[k]
                sk = "wst%d" % k
                fw.dma(st[:, 0:n], src(c, s0, n), w=[sk], key=sk)
                o = dst(c, s0, n)
                dk = dkey(c)
                if sview is not None:
                    sv_ = sview(st[:, 0:n])
                    sc = scale(c)
                    self.V(lambda eg, o=o, sv_=sv_, sc=sc: eg.tensor_scalar(o, sv_, sc, None, ALU.mult), r=[sk, "gcol"], w=[dk])
                    continue
                if mode == "plain":
                    e = ["dve", "pool", "act"][self.wst_i % 3]
                    if e == "act":
                        fw.act(o, st[:, 0:n], AF.Copy, r=[sk], w=[dk])
                    else:
                        fw.op(e, lambda eg, o=o, st=st, n=n: eg.tensor_copy(o, st[:, 0:n]), r=[sk], w=[dk])
                elif mode == "col":
                    sc = scale(c)
                    e = ["dve", "pool"][self.wst_i % 2]
                    fw.op(e, lambda eg, o=o, st=st, n=n, sc=sc: eg.tensor_scalar(o, st[:, 0:n], sc, None, ALU.mult),
                          r=[sk, "gcol"], w=[dk])
                else:
                    sc = scale(c)
                    m = mul(s0, n)
                    self.V(lambda eg, o=o, st=st, n=n, sc=sc, m=m: eg.scalar_tensor_tensor(
                        o, st[:, 0:n], sc, m, ALU.mult, ALU.mult), r=[sk, "gcol", mulkey], w=[dk])

    def norm_hT(self, xt, xk, M, hdst, hkey, identb):
        self.norm_a(xt, xk, M)
        self.norm_b(M, hdst, hkey, identb)

    def norm_a(self, xt, xk, M):
        fw = self.fw
        xn, ss, t1 = self.xn, self.ss, self.t1
        fw.act(xn[0:M, :], xt[0:M, :], AF.Square, r=[xk], w=["xn", "ss"], accum_out=ss[0:M, :])
        self.V(lambda e: e.tensor_scalar(t1[0:M, :], ss[0:M, :], 1.0 / D, 1e-6, ALU.mult, ALU.add), r=["ss"], w=["t1"])
        fw.act(t1[0:M, :], t1[0:M, :], AF.Sqrt, r=["t1"], w=["t1"])
        self.V(lambda e: e.reciprocal(t1[0:M, :], t1[0:M, :]), r=["t1"], w=["t1"])
        self.V(lambda e: e.tensor_scalar(xn[0:M, :], xt[0:M, :], t1[0:M, 0:1], None, ALU.mult), r=[xk, "t1"], w=["xn"])

    def norm_b(self, M, hdst, hkey, identb):
        fw = self.fw
        xn = self.xn
        pbk, pk = self.pb()
        for c in range(8):
            fw.tr(pbk[:, c * M:(c + 1) * M], xn[0:M, c * 128:(c + 1) * 128], identb[0:M, 0:M], r=["xn", "identb"], w=[pk])
        fw.act(hdst, pbk[:, 0:8 * M].rearrange("p (c t) -> p c t", c=8), AF.Copy, r=[pk], w=[hkey])

    def build(self):
        self.declare()
        nc = self.nc
        with ExitStack() as es:
            self.fw = fw = FW(nc, es)
            self.PS = [fw.ps("ps%d" % i, [128, 512], F32) for i in range(6)]
            self.PBK = [fw.ps("pb%d" % i, [128, 1024], BF) for i in range(2)]
            self.ASZ = 52224
            self.arena = fw.sb("arena", [128, self.ASZ])
            self.aoff = 0
            self.apeak = 0
            self.identf = self.alloc("identf", [128, 128])
            self.identb = self.alloc("identb", [128, 128], BF)
            self.cstage = self.alloc("cstage", [32, 128])
            self.wst_i = 0
            self.xn = self.alloc("xn", [128, D], BF)
            self.ss = self.alloc("ss", [128, 1])
            self.t1 = self.alloc("t1", [128, 1])
            self.gcol = self.alloc("gcol", [128, 8])
            self.xt = [self.alloc("xt%d" % i, [128, D]) for i in range(2)]
            self.sel = self.alloc("sel", [128, 8])
            fw.dma(self.sel[:], self.I["c_sel"], w=["sel"], key="sel")
            fw.dma(self.identf[:], self.I["c_ident"], w=["identf"], key="identf")
            self.V(lambda e: e.tensor_copy(self.identb[:], self.identf[:]), r=["identf"], w=["identb"])
            for l in range(2):
                for p_ in (self.pass_rwkv, self.pass_attn, self.pass_ffn):
                    mk_ = self.aoff
                    p_(l, None)
                    self.release(mk_)
            print("arena peak words", self.apeak, "of", self.ASZ)
            fw.emit()
        return nc

    def sbl(self, es2, name, shape, dt=F32):
        return self.alloc(name, shape, dt)

    def xsrc(self, l, i):
        if i < self.NT:
            src = self.I["xp"] if l == 0 else self.xbuf
            return src[i * 128:(i + 1) * 128, :], ("xb", i)
        src = self.I["xs"] if l == 0 else self.xsbuf
        return src, ("xb", i)

    def pass_rwkv(self, l, es2):
        fw, I, O, NT = self.fw, self.I, self.O, self.NT
        sbl = lambda n, s, dt=F32: self.sbl(es2, "r%d_" % l + n, s, dt)
        identb, identf = self.identb, self.identf
        W1 = sbl("W1", [128, 8, RP], BF)
        W2 = sbl("W2", [128, 8, RP], BF)
        Wg = sbl("Wg", [128, 8, D], BF)
        Wr = sbl("Wr", [128, 4, D], BF)
        lw2 = sbl("lw2", [128, RD], BF)
        lg2 = sbl("lg2", [128, RD], BF)
        bcs = {}
        for n in ["rwkv_w0", "rwkv_a0", "rwkv_k_k", "rwkv_k_a", "rwkv_r_k", "rwkv_ln_g", "rwkv_ln_b"]:
            bcs[n] = sbl(n, [128, RD])
            self.bcast_load(bcs[n][:], n + "_bc", I[n][l])
        mucol = sbl("mucol", [128, 2])
        tri = sbl("tri", [128, 256])
        mask2 = sbl("mask2", [128, 256])
        maskL = sbl("maskL", [128, 128])
        clast = sbl("clast", [128, 1])
        fw.dma(tri[:], I["c_tri"], w=["tri"], key="tri")
        fw.dma(mask2[:], I["c_mask2"], w=["mask2"], key="mask2")
        fw.dma(maskL[:], I["c_maskL"], w=["maskL"], key="maskL")
        fw.dma(clast[:], I["c_last"], w=["clast"], key="clast")
        self.col_load(self.gcol[:], "gcol", I["norm_mix_g"][l], 8)
        self.col_load(mucol[:], "mucol", I["rwkv_mu"][l, 1536:1792], 2)
        m0 = self.aoff
        self.wstage = [sbl("wst%d" % i_, [128, 2048]) for i_ in range(4)]
        mu_bc = sbl("mu_bc", [128, RP])
        omm_bc = sbl("omm_bc", [128, RP])
        self.bcast_load(mu_bc[:], "mu_bc", I["rwkv_mu"][l])
        self.V(lambda e: e.tensor_scalar(omm_bc[:], mu_bc[:], -1.0, 1.0, ALU.mult, ALU.add), r=["mu_bc"], w=["omm_bc"])
        win = I["w_in"][l]
        gsc = lambda c: self.gcol[:, c:c + 1]
        self.prep_w(8, RP, lambda c, s0, n: win[c * 128:(c + 1) * 128, s0:s0 + n],
                    lambda c, s0, n: W1[:, c, s0:s0 + n], lambda c: "W1_%d" % c, "colmul", gsc,
                    lambda s0, n: omm_bc[:, s0:s0 + n], "omm_bc")
        self.prep_w(8, RP, lambda c, s0, n: win[c * 128:(c + 1) * 128, s0:s0 + n],
                    lambda c, s0, n: W2[:, c, s0:s0 + n], lambda c: "W2_%d" % c, "colmul", gsc,
                    lambda s0, n: mu_bc[:, s0:s0 + n], "mu_bc")
        self.prep_w(8, D, lambda c, s0, n: win[c * 128:(c + 1) * 128, 2560 + s0:2560 + s0 + n],
                    lambda c, s0, n: Wg[:, c, s0:s0 + n], lambda c: "Wg_%d" % c, "col", gsc)
        wbr = I["w_br_rwkv"][l]
        self.prep_w(4, D, lambda c, s0, n: wbr[c * 128:(c + 1) * 128, s0:s0 + n],
                    lambda c, s0, n: Wr[:, c, s0:s0 + n], lambda c: "Wr_%d" % c, "plain")
        for (nm, p0, dk_) in [("rwkv_w2", 0, "lw2a"), ("rwkv_a2", 64, "lw2b")]:
            k = self.wst_i % 4
            self.wst_i += 1
            wsk = self.wstage[k]
            fw.dma(wsk[p0:p0 + 64, 0:RD], I[nm][l], w=["wst%d" % k], key="wst%d" % k)
            self.P(lambda e, wsk=wsk, p0=p0: e.tensor_copy(lw2[p0:p0 + 64, :], wsk[p0:p0 + 64, 0:RD]), r=["wst%d" % k], w=[dk_])
        self.prep_w(1, RD, lambda c, s0, n: I["rwkv_g2"][l], lambda c, s0, n: lg2[:, :], lambda c: "lg2", "plain")
        WK1 = ["W1_%d" % c for c in range(8)]
        WK2 = ["W2_%d" % c for c in range(8)]
        self.release(m0)
        class NSP:
            pass
        zr, zk = sbl("zr", [128, RD]), sbl("zk", [128, RD])
        lact = sbl("lact", [128, 128], BF)
        T = [sbl("tmp%d" % i_, [128, RD]) for i_ in range(8)]
        sm = sbl("sm", [128, 64])
        orT = sbl("orT", [128, 4, 128], BF)
        sgr = sbl("sgr", [128, 8, 128], BF)
        mrT0_ = sbl("mrT0", [128, 8, 128], BF)
        mrT = [mrT0_, mrT0_]
        TP_ = [sbl("tpost%d" % i_, [128, RD]) for i_ in range(2)]
        m1 = self.aoff
        NRB = 9864

        def mkrec(k):
            R = NSP()
            rb = sbl("RB%d" % k, [128, NRB], BF)
            rf = sbl("RF%d" % k, [128, 528])
            R.rb, R.rf, R.k = rb, rf, k
            R.RKT = rb[:, 0:1024].rearrange("p (j a t) -> p j a t", j=4, a=2)
            R.G4 = [rb[:, 1024 + j * 1280:1024 + (j + 1) * 1280].rearrange("p (h c) -> p h c", h=2) for j in range(4)]
            R.ZF = [rb[:, 6144 + j * 256:6144 + (j + 1) * 256].rearrange("p (h c) -> p h c", h=2) for j in range(4)]
            R.vb, R.ktt, R.bnt = rb[:, 7168:7680], rb[:, 7680:8192], rb[:, 8192:8704]
            R.sgT = rb[:, 8704:8832]
            R.hT = rb[:, 8832:9864].rearrange("p (c t) -> p c t", c=8)
            R.zv, R.WC, R.bon = rf[:, 0:512], rf[:, 512:516], rf[:, 516:524]
            R.K = (lambda k_: (lambda n: "%s#%d" % (n, k_)))(k)
            return R
        R0 = mkrec(0)
        U0b = [sbl("U0b%d" % j, [128, 2, 64], BF) for j in range(4)]
        Ub = sbl("Ub", [128, RD], BF)
        Nst = sbl("Nst", [128, 4, 128])
        Nb = sbl("Nb", [128, 4, 128], BF)
        self.V(lambda e: e.memset(Nst[:], 0.0), w=["Nst"])
        self.V(lambda e: e.memset(Nb[:], 0.0), w=["Nb"])
        m2 = self.aoff
        rt, kat = sbl("rt", [128, RD], BF), sbl("kat", [128, RD], BF)
        KT = sbl("KT", [128, 4, 128], BF)
        BT = sbl("BT", [128, 4, 128], BF)
        for j in range(4):
            self.P(lambda e, j=j: e.tensor_copy(R0.G4[j][:, :, 512:640], identb[:, :].unsqueeze(1).to_broadcast([128, 2, 128])),
                   r=["identb"], w=["G4_%d" % j])
        EZ = [[sbl("EZ%d_%d" % (j, a), [128, 2, 2, 128], BF) for a in range(2)] for j in range(4)]
        FFa = [sbl("FFa%d" % a, [128, 4, 2, 128], BF) for a in range(2)]
        FF = [[FFa[a][:, j] for a in range(2)] for j in range(4)]

        def tok_proj(M, hcur, hprev, hk, g0, dstkey):
            ps, pk = self.pf()
            n = 0
            for c in range(8):
                fw.mm(ps[0:M, :], hcur(c), W1[:, c, g0:g0 + 512], n == 0, False, r=[hk, WK1[c]], w=[pk])
                n += 1
            for c in range(8):
                fw.mm(ps[0:M, :], hprev(c), W2[:, c, g0:g0 + 512], False, c == 7, r=[hk, WK2[c]], w=[pk])
            return ps, pk

        def feat_proj(M, hcur, hprev, hk, g0):
            ps, pk = self.pf()
            for c in range(8):
                fw.mm(ps[:, 0:M], W1[:, c, g0:g0 + 128], hcur(c), c == 0, False, r=[hk, WK1[c]], w=[pk])
            for c in range(8):
                fw.mm(ps[:, 0:M], W2[:, c, g0:g0 + 128], hprev(c), False, c == 7, r=[hk, WK2[c]], w=[pk])
            return ps, pk

        def raw_last(hl, hk, M, dst):
            for gi, g0 in enumerate(range(0, RP, 512)):
                n = min(512, RP - g0)
                ps, pk = self.pf()
                for c in range(8):
                    fw.mm(ps[0:M, 0:n], hl(c), W1[:, c, g0:g0 + n], c == 0, False, r=[hk, WK1[c]], w=[pk])
                for c in range(8):
                    fw.mm(ps[0:M, 0:n], hl(c), W2[:, c, g0:g0 + n], False, c == 7, r=[hk, WK2[c]], w=[pk])
                fw.act(T[gi][0:M, 0:n], ps[0:M, 0:n], AF.Copy, r=[pk], w=["T%d" % gi])
                fw.dma(dst[:, g0:g0 + n], T[gi][0:M, 0:n], r=["T%d" % gi], key="zl%d" % gi)

        def prep(M, sample, R):
            K = R.K
            w0, a0 = bcs["rwkv_w0"], bcs["rwkv_a0"]
            kkb, kab, rkb = bcs["rwkv_k_k"], bcs["rwkv_k_a"], bcs["rwkv_r_k"]
            pw, pwk = self.pf()
            fw.mm(pw[0:M, :], lact[0:64, 0:M], lw2[0:64, :], True, True, r=["lact", "lw2a"], w=[pwk])
            pa, pak = self.pf()
            fw.mm(pa[0:M, :], lact[64:128, 0:M], lw2[64:128, :], True, True, r=["lact", "lw2b"], w=[pak])
            sg, a_, kk, t3, kf, be = T[0], T[1], T[2], T[3], T[4], T[5]
            self.V(lambda e: e.tensor_tensor(sg[0:M, :], pw[0:M, :], w0[0:M, :], ALU.add), r=[pwk, "rwkv_w0_bc"], w=["T0"])
            fw.act(sg[0:M, :], sg[0:M, :], AF.Sigmoid, r=["T0"], w=["T0"])
            self.V(lambda e: e.tensor_tensor(a_[0:M, :], pa[0:M, :], a0[0:M, :], ALU.add), r=[pak, "rwkv_a0_bc"], w=["T1"])
            fw.act(a_[0:M, :], a_[0:M, :], AF.Sigmoid, r=["T1"], w=["T1"])
            self.P(lambda e: e.tensor_tensor(kk[0:M, :], zk[0:M, :], kkb[0:M, :], ALU.mult), r=["zk", "rwkv_k_k_bc"], w=["T2"])
            self.P(lambda e: e.tensor_tensor(t3[0:M, :], kk[0:M, :], kk[0:M, :], ALU.mult), r=["T2"], w=["T3"])
            self.V(lambda e: e.tensor_reduce(sm[0:M, 0:8], h3(t3[0:M, :]), AX.X, ALU.add), r=["T3"], w=["sm0"])
            fw.act(sm[0:M, 0:8], sm[0:M, 0:8], AF.Sqrt, r=["sm0"], w=["sm0"])
            self.V(lambda e: e.tensor_scalar(sm[0:M, 0:8], sm[0:M, 0:8], 1e-12, None, ALU.max), r=["sm0"], w=["sm0"])
            self.V(lambda e: e.reciprocal(sm[0:M, 0:8], sm[0:M, 0:8]), r=["sm0"], w=["sm0"])
            self.V(lambda e: e.tensor_tensor(h3(kk[0:M, :]), h3(kk[0:M, :]), bc3(sm[0:M, 0:8], 64), ALU.mult),
                   r=["T2", "sm0"], w=["T2"])
            self.V(lambda e: e.scalar_tensor_tensor(t3[0:M, :], a_[0:M, :], -1.0, kab[0:M, :], ALU.add, ALU.mult),
                   r=["T1", "rwkv_k_a_bc"], w=["T3"])
            self.V(lambda e: e.scalar_tensor_tensor(kf[0:M, :], t3[0:M, :], 1.0, zk[0:M, :], ALU.add, ALU.mult),
                   r=["T3", "zk"], w=["T4"])
            self.P(lambda e: e.tensor_tensor(be[0:M, :], kk[0:M, :], a_[0:M, :], ALU.mult), r=["T2", "T1"], w=["T5"])
            self.P(lambda e: e.tensor_tensor(t3[0:M, :], zr[0:M, :], kf[0:M, :], ALU.mult), r=["zr", "T4"], w=["T3"])
            self.P(lambda e: e.tensor_tensor(t3[0:M, :], t3[0:M, :], rkb[0:M, :], ALU.mult), r=["T3", "rwkv_r_k_bc"], w=["T3"])
            self.V(lambda e, R=R: e.tensor_reduce(R.bon[0:M, :], h3(t3[0:M, :]), AX.X, ALU.add), r=["T3"], w=[K("bon")])
            if sample:
                fw.act(T[6][0:M, :], sg[0:M, :], AF.Exp, r=["T0"], w=["T6"], scale=CDEC)
                for x, (tl, tk) in enumerate([(zr, "zr"), (T[6], "T6"), (kf, "T4"), (R.zv, K("zv")), (kk, "T2"), (be, "T5")]):
                    fw.dma(self.sq[x], tl[0:M, :], r=[tk], w=[("sq", x)], key="sqw%d" % x)
                return
            pli, plik = self.pf()
            fw.mm(pli[:, :], tri[:, 0:128], sg[:, :], True, True, r=["tri", "T0"], w=[plik])
            ple, plek = self.pf()
            fw.mm(ple[:, :], tri[:, 128:256], sg[:, :], True, True, r=["tri", "T0"], w=[plek])
            eL, eLm, enL = T[6], T[7], T[3]
            fw.act(eL[:, :], pli[:, :], AF.Exp, r=[plik], w=["T6"])
            fw.act(eLm[:, :], ple[:, :], AF.Exp, r=[plek], w=["T7"])
            fw.act(enL[:, :], pli[:, :], AF.Exp, r=[plik], w=["T3"], scale=-1.0)
            self.V(lambda e: e.tensor_tensor(rt[:, :], zr[:, :], eL[:, :], ALU.mult), r=["zr", "T6"], w=["rt"])
            self.V(lambda e: e.tensor_tensor(kat[:, :], kk[:, :], eLm[:, :], ALU.mult), r=["T2", "T7"], w=["kat"])
            self.P(lambda e, R=R: e.tensor_tensor(R.ktt[:, :], kf[:, :], enL[:, :], ALU.mult), r=["T4", "T3"], w=[K("ktt")])
            self.V(lambda e, R=R: e.scalar_tensor_tensor(R.bnt[:, :], be[:, :], -1.0, enL[:, :], ALU.mult, ALU.mult),
                   r=["T5", "T3"], w=[K("bnt")])
            fw.act(R.vb[:, :], R.zv[:, :], AF.Copy, r=[K("zv")], w=[K("vb")])
            pwc, pwck = self.pf()
            for j in range(4):
                fw.mm(pwc[:, j:j + 1], eL[:, j * 128:(j + 1) * 128], clast[:, :], True, True, r=["T6", "clast"], w=[pwck])
            fw.act(R.WC[:, :], pwc[:, 0:4], AF.Copy, r=[pwck], w=[K("WC")])
            for (src, skey, dstf, dk) in [(rt, "rt", None, "RKT"), (kat, "kat", None, "RKT"),
                                          (R.ktt, K("ktt"), None, "KT"), (R.bnt, K("bnt"), None, "BT")]:
                pbk, pk = self.pb()
                for j in range(4):
                    fw.tr(pbk[:, j * 128:(j + 1) * 128], src[:, j * 128:(j + 1) * 128], identb[:, :], r=[skey, "identb"], w=[pk])
                if dk == "RKT":
                    which = 0 if skey == "rt" else 1
                    fw.act(R.RKT[:, :, which, :], pbk[:, 0:512].rearrange("p (j t) -> p j t", j=4), AF.Copy, r=[pk], w=["RKT%d" % which])
                else:
                    dst = KT if dk == "KT" else BT
                    self.V(lambda e, dst=dst, pbk=pbk: e.tensor_copy(dst[:, :, :], pbk[:, 0:512].rearrange("p (j t) -> p j t", j=4)),
                           r=[pk], w=[dk])

        def stageAB(R):
            K = R.K
            RK = [K("RKT0"), K("RKT1")]
            RKT, G4, ZF = R.RKT, R.G4, R.ZF
            zb = [self.pf(), self.pf()]
            for j in range(4):
                for hh in range(2):
                    o = hh * 64
                    pZ, pzk = zb[hh]
                    fw.mm(pZ[:, j * 128:(j + 1) * 128], RKT[o:o + 64, j, 1, :], BT[o:o + 64, j, :], True, True, r=["BT", K("RKT1")], w=[pzk])
            mlb = maskL[:, :].unsqueeze(1).to_broadcast([128, 4, 128])
            for hh in range(2):
                pZ, pzk = zb[hh]
                self.V(lambda e, pZ=pZ, hh=hh: e.tensor_tensor(FFa[0][:, :, hh, :], pZ[:, :].rearrange("p (j c) -> p j c", j=4), mlb, ALU.mult),
                       r=[pzk, "maskL"], w=["FF%d_0" % j for j in range(4)])
            for j in range(4):
                bk = [self.pf(), self.pf()]
                for hh in range(2):
                    o = hh * 64
                    ps, pk = bk[hh]
                    rhs = RKT[o:o + 64, j, :, :].rearrange("p a t -> p (a t)")
                    fw.mm(ps[:, 0:256], KT[o:o + 64, j, :], rhs, True, True, r=["KT"] + RK, w=[pk])
                    fw.mm(ps[:, 256:512], BT[o:o + 64, j, :], rhs, True, True, r=["BT"] + RK, w=[pk])
                for hh in range(2):
                    ps, pk = bk[hh]
                    self.V(lambda e, j=j, hh=hh, ps=ps, G4=G4: e.tensor_tensor(
                        G4[j][:, hh, 0:512].rearrange("p (a c) -> p a c", a=2), ps[:, :].rearrange("p (a c) -> p a c", a=2),
                        mask2[:, :].unsqueeze(1).to_broadcast([128, 2, 256]), ALU.mult), r=[pk, "mask2"], w=[K("G4_%d" % j)])
            for lev in range(7):
                a, b = lev % 2, (lev + 1) % 2
                for j in range(4):
                    fk, fn_ = "FF%d_%d" % (j, a), "FF%d_%d" % (j, b)
                    ezn = "EZ%d_%d" % (j, b)
                    if lev == 0:
                        ezk = K("G4_%d" % j)
                        EZs = lambda hh, j=j, G4=G4: G4[j][:, hh, 384:640]
                        Es = lambda hh, j=j, G4=G4: G4[j][:, hh, 384:512]
                        Zs = lambda j=j, G4=G4: G4[j][:, :, 512:640]
                    else:
                        ezk = "EZ%d_%d" % (j, a)
                        EZs = lambda hh, j=j, a=a: EZ[j][a][:, hh, :, :].rearrange("p a t -> p (a t)")
                        Es = lambda hh, j=j, a=a: EZ[j][a][:, hh, 0, :]
                        Zs = lambda j=j, a=a: EZ[j][a][:, :, 1, :]
                    if lev < 6:
                        pL, plk = self.pf()
                        for hh in range(2):
                            fw.mm(pL[:, hh * 256:(hh + 1) * 256], FF[j][a][:, hh, :], EZs(hh), True, True, r=[ezk, fk], w=[plk])
                        pF, pfk = self.pf()
                        for hh in range(2):
                            fw.mm(pF[:, hh * 128:(hh + 1) * 128], Es(hh), FF[j][a][:, hh, :], True, True, r=[ezk, fk], w=[pfk])
                        l3 = pL[:, :].rearrange("p (h c) -> p h c", h=2)
                        fw.act(EZ[j][b][:, :, 0, :], l3[:, :, 0:128], AF.Copy, r=[plk], w=[ezn])
                        self.V(lambda e, j=j, b=b, l3=l3, Zs=Zs: e.tensor_tensor(EZ[j][b][:, :, 1, :], l3[:, :, 128:256], Zs(), ALU.add),
                               r=[plk, ezk], w=[ezn])
                        fw.act(FF[j][b][:, :, :], pF[:, 0:256].rearrange("p (h c) -> p h c", h=2), AF.Copy, r=[pfk], w=[fn_])
                    else:
                        pL, plk = self.pf()
                        for hh in range(2):
                            fw.mm(pL[:, hh * 128:(hh + 1) * 128], FF[j][a][:, hh, :], EZ[j][a][:, hh, 1, :], True, True, r=[ezk, fk], w=[plk])
                        self.V(lambda e, j=j, a=a, pL=pL, ZF=ZF: e.tensor_tensor(ZF[j][:, :, :], pL[:, 0:256].rearrange("p (h c) -> p h c", h=2),
                                                                      EZ[j][a][:, :, 1, :], ALU.add), r=[plk, ezk], w=[K("ZF%d" % j)])

        def stageC(R):
            K = R.K
            RKT, G4, ZF, vb = R.RKT, R.G4, R.ZF, R.vb
            for j in range(4):
                pU, puk = self.pf()
                for hh in range(2):
                    o, h = hh * 64, 2 * j + hh
                    fw.mm(pU[:, hh * 64:(hh + 1) * 64], RKT[o:o + 64, j, 1, :], Nb[o:o + 64, j, o:o + 64], True, False, r=[K("RKT1"), "Nb"], w=[puk])
                    fw.mm(pU[:, hh * 64:(hh + 1) * 64], G4[j][:, hh, 128:256], vb[:, h * 64:(h + 1) * 64], False, True, r=[K("G4_%d" % j), K("vb")], w=[puk])
                fw.act(U0b[j][:, :, :], pU[:, 0:128].rearrange("p (h c) -> p h c", h=2), AF.Copy, r=[puk], w=["U0b%d" % j])
            for j in range(4):
                pU, puk = self.pf()
                for hh in range(2):
                    fw.mm(pU[:, hh * 64:(hh + 1) * 64], ZF[j][:, hh, :], U0b[j][:, hh, :], True, True, r=[K("ZF%d" % j), "U0b%d" % j], w=[puk])
                fw.act(Ub[:, j * 128:(j + 1) * 128], pU[:, 0:128], AF.Copy, r=[puk], w=["Ub%d" % j])

        def stageD(R):
            K = R.K
            RKT, G4, vb = R.RKT, R.G4, R.vb
            psY, pyk = self.pf()
            for j in range(4):
                for hh in range(2):
                    o, h = hh * 64, 2 * j + hh
                    fw.mm(psY[:, h * 64:(h + 1) * 64], RKT[o:o + 64, j, 0, :], Nb[o:o + 64, j, o:o + 64], True, False, r=[K("RKT0"), "Nb"], w=[pyk])
                    fw.mm(psY[:, h * 64:(h + 1) * 64], G4[j][:, hh, 0:128], vb[:, h * 64:(h + 1) * 64], False, False, r=[K("G4_%d" % j), K("vb")], w=[pyk])
                    fw.mm(psY[:, h * 64:(h + 1) * 64], G4[j][:, hh, 256:384], Ub[:, h * 64:(h + 1) * 64], False, True, r=[K("G4_%d" % j), "Ub%d" % j], w=[pyk])
            return psY, pyk

        def n_update(R):
            K = R.K
            ktt, bnt, vb, WC = R.ktt, R.bnt, R.vb, R.WC
            pN, pnk = self.pf()
            for j in range(4):
                fw.mm(pN[:, j * 128:(j + 1) * 128], ktt[:, j * 128:(j + 1) * 128], vb[:, j * 128:(j + 1) * 128], True, False, r=[K("ktt"), K("vb")], w=[pnk])
                fw.mm(pN[:, j * 128:(j + 1) * 128], bnt[:, j * 128:(j + 1) * 128], Ub[:, j * 128:(j + 1) * 128], False, True, r=[K("bnt"), "Ub%d" % j], w=[pnk])
            n2 = Nst[:, :, :].rearrange("p j c -> p (j c)")
            self.V(lambda e: e.tensor_tensor(n2, pN[:, :], n2, ALU.add), r=[pnk, "Nst"], w=["Nst"])
            self.V(lambda e, WC=WC: e.tensor_tensor(Nst[:, :, :], Nst[:, :, :], bc3(WC[:, :], 128), ALU.mult), r=["Nst", K("WC")], w=["Nst"])
            fw.act(Nb[:, :, :], Nst[:, :, :], AF.Copy, r=["Nst"], w=["Nb"])


        def post(M, yap, ykeys, pg, pgk, R):
            K = R.K
            lng, lnb = bcs["rwkv_ln_g"], bcs["rwkv_ln_b"]
            y2, yc = TP_[0], TP_[1]
            ob = TP_[0].bitcast(BF)[:, 0:RD]
            self.V(lambda e: e.tensor_reduce(sm[0:M, 16:24], h3(yap), AX.X, ALU.add), r=ykeys, w=["sm2"])
            fw.act(y2[0:M, :], yap, AF.Square, r=ykeys, w=["TP0"])
            self.V(lambda e: e.tensor_reduce(sm[0:M, 24:32], h3(y2[0:M, :]), AX.X, ALU.add), r=["TP0"], w=["sm3"])
            mean, var = sm[0:M, 16:24], sm[0:M, 24:32]
            self.V(lambda e: e.tensor_scalar(mean, mean, 1.0 / 64, None, ALU.mult), r=["sm2"], w=["sm2"])
            self.V(lambda e: e.tensor_tensor(sm[0:M, 32:40], mean, mean, ALU.mult), r=["sm2"], w=["sm4"])
            self.V(lambda e: e.scalar_tensor_tensor(var, var, 1.0 / 64, sm[0:M, 32:40], ALU.mult, ALU.subtract), r=["sm3", "sm4"], w=["sm3"])
            self.V(lambda e: e.tensor_scalar(var, var, 64e-5, None, ALU.add), r=["sm3"], w=["sm3"])
            fw.act(var, var, AF.Sqrt, r=["sm3"], w=["sm3"])
            self.V(lambda e: e.reciprocal(var, var), r=["sm3"], w=["sm3"])
            self.V(lambda e: e.tensor_tensor(h3(yc[0:M, :]), h3(yap), bc3(mean, 64), ALU.subtract), r=list(ykeys) + ["sm2"], w=["TP1"])
            self.V(lambda e: e.tensor_tensor(h3(yc[0:M, :]), h3(yc[0:M, :]), bc3(var, 64), ALU.mult), r=["TP1", "sm3"], w=["TP1"])
            self.P(lambda e: e.tensor_tensor(yc[0:M, :], yc[0:M, :], lng[0:M, :], ALU.mult), r=["TP1", "rwkv_ln_g_bc"], w=["TP1"])
            self.P(lambda e: e.tensor_tensor(yc[0:M, :], yc[0:M, :], lnb[0:M, :], ALU.add), r=["TP1", "rwkv_ln_b_bc"], w=["TP1"])
            self.P(lambda e, R=R: e.tensor_tensor(h3(y2[0:M, :]), h3(R.zv[0:M, :]), bc3(R.bon[0:M, :], 64), ALU.mult), r=[K("zv"), K("bon")], w=["TP0"])
            self.V(lambda e: e.tensor_tensor(yc[0:M, :], yc[0:M, :], y2[0:M, :], ALU.add), r=["TP1", "TP0"], w=["TP1"])
            self.V(lambda e: e.tensor_tensor(ob[0:M, :], yc[0:M, :], pg[0:M, :], ALU.mult), r=["TP1", pgk], w=["TP0"])
            pbk, pk = self.pb()
            for j in range(4):
                fw.tr(pbk[:, j * M:(j + 1) * M], ob[0:M, j * 128:(j + 1) * 128], identb[0:M, 0:M], r=["TP0", "identb"], w=[pk])
            fw.act(orT[:, :, 0:M], pbk[:, 0:4 * M].rearrange("p (j t) -> p j t", j=4), AF.Copy, r=[pk], w=["orT"])

        def gate_branch(M, hcur, hk, mdst, mkey):
            for half in range(2):
                pg, pgk = self.pf()
                for q in range(4):
                    dc = half * 4 + q
                    for c in range(8):
                        fw.mm(pg[:, q * M:(q + 1) * M], Wg[:, c, dc * 128:(dc + 1) * 128], hcur(c), c == 0, c == 7, r=[hk, "Wg_%d" % c], w=[pgk])
                fw.act(sgr[:, half * 4:(half + 1) * 4, 0:M], pg[:, 0:4 * M].rearrange("p (q t) -> p q t", q=4), AF.Sigmoid, r=[pgk], w=["sgr%d" % half])
                pbr, pbk_ = self.pf()
                for q in range(4):
                    dc = half * 4 + q
                    for j in range(4):
                        fw.mm(pbr[:, q * M:(q + 1) * M], Wr[:, j, dc * 128:(dc + 1) * 128], orT[:, j, 0:M], j == 0, j == 3, r=["orT", "Wr_%d" % j], w=[pbk_])
                self.V(lambda e, half=half, pbr=pbr: e.tensor_tensor(mdst[:, half * 4:(half + 1) * 4, 0:M], sgr[:, half * 4:(half + 1) * 4, 0:M],
                                                                 pbr[:, 0:4 * M].rearrange("p (q t) -> p q t", q=4), ALU.mult),
                       r=["sgr%d" % half, pbk_], w=[mkey])

        R1 = mkrec(1)
        for j in range(4):
            self.P(lambda e, j=j: e.tensor_copy(R1.G4[j][:, :, 512:640], identb[:, :].unsqueeze(1).to_broadcast([128, 2, 128])),
                   r=["identb"], w=[R1.K("G4_%d" % j)])
        RR = [R0, R1]

        def H1a(i):
            R, Rp = RR[i % 2], RR[(i + 1) % 2]
            hT = R.hT
            xt, xk = self.xt[i % 2], "xt%d" % (i % 2)
            src, _ = self.xsrc(l, i)
            fw.dma(xt[:], src, r=[("xb", i)], w=[xk], key=xk)
            hk = R.K("hTr")
            if i == 0:
                self.V(lambda e, hT=hT: e.memset(hT[:, :, 0:1], 0.0), w=[hk])
            else:
                self.P(lambda e, hT=hT, hp=Rp.hT: e.tensor_copy(hT[:, :, 0:1], hp[:, :, 128:129]), r=[Rp.K("hTr")], w=[hk])
            self.norm_a(xt, xk, 128)

        def H1b(i):
            R = RR[i % 2]
            K = R.K
            hT = R.hT
            hk = K("hTr")
            self.norm_b(128, hT[:, :, 1:129], hk, identb)
            hcur = lambda c, hT=hT: hT[:, c, 1:129]
            hprev = lambda c, hT=hT: hT[:, c, 0:128]
            for g0, dst, dk in [(0, zr, "zr"), (512, zk, "zk"), (1024, R.zv, K("zv"))]:
                ps, pk = tok_proj(128, hcur, hprev, hk, g0, dk)
                fw.act(dst[:, :], ps[:, :], AF.Copy, r=[pk], w=[dk])
            ps, pk = feat_proj(128, hcur, hprev, hk, 1536)
            fw.act(lact[0:64, :], ps[0:64, 0:128], AF.Tanh, r=[pk], w=["lact"])
            fw.act(lact[64:128, :], ps[64:128, 0:128], AF.Copy, r=[pk], w=["lact"])
            ps, pk = feat_proj(128, hcur, hprev, hk, 1664)
            fw.act(R.sgT[:, :], ps[:, 0:128], AF.Sigmoid, r=[pk], w=[K("sgT")])
            if i == NT - 1:
                raw_last(lambda c, hT=hT: hT[:, c, 128:129], hk, 1, O["p_shift"][l:l + 1, :])

        def H1c(i):
            prep(128, False, RR[i % 2])

        def H1d(i):
            stageAB(RR[i % 2])

        H2st = {}

        def H2a(i):
            R = RR[i % 2]
            stageC(R)
            psY, pyk = stageD(R)
            n_update(R)
            pg, pgk = self.pf()
            fw.mm(pg[:, :], R.sgT[:, :], lg2[:, :], True, True, r=[R.K("sgT"), "lg2"], w=[pgk])
            H2st[i] = (psY, pyk, pg, pgk)

        def H2b(i):
            psY, pyk, pg, pgk = H2st.pop(i)
            post(128, psY[:, :], [pyk], pg, pgk, RR[i % 2])

        def H2c(i):
            R = RR[i % 2]
            m, mk = mrT[0], "mrT0"
            gate_branch(128, lambda c, R=R: R.hT[:, c, 1:129], R.K("hTr"), m, mk)
            fw.dma(self.mrbuf[i].rearrange("p (c t) -> p c t", c=8), m[:, :, :], r=[mk], w=[("mr", i)], key=mk)

        def cap(pool, f, i):
            self.pool = pool
            return fw.capture(lambda: f(i))

        for f in (H1a, H1b, H1c):
            fw.replay([cap(0, f, 0)])
        fw.replay([cap(None, H1d, 0)])
        for i in range(NT):
            nx = i + 1 < NT
            if nx:
                fw.replay([cap(0, H1a, i + 1)])
            fw.replay([cap(1, H2a, i)])
            fw.replay(([cap(0, H1b, i + 1)] if nx else []) + [cap(1, H2b, i)])
            fw.replay(([cap(0, H1c, i + 1)] if nx else []) + [cap(1, H2c, i)])
            if nx:
                fw.replay([cap(None, H1d, i + 1)])
        self.pool = None
        for j in range(4):
            ps, pk = self.pf()
            fw.tr(ps[:, 0:128], Nst[:, j, :], identf[:, :], r=["Nst", "identf"], w=[pk])
            fw.act(T[0][:, j * 128:(j + 1) * 128], ps[:, 0:128], AF.Copy, r=[pk], w=["T0"])
        for h_ in range(8):
            j, o = h_ // 2, (h_ % 2) * 64
            fw.dma(O["p_wkv"][l, h_], T[0][o:o + 64, j * 128 + o:j * 128 + o + 64], r=["T0"], key="T0")

        self.release(m1)
        RS = NSP()
        RS.zv = sbl("zv_s", [128, RD])
        RS.sgT = sbl("sgT_s", [128, 128], BF)
        RS.bon = sbl("bon_s", [128, 8])
        RS.K = lambda n: n + "#s"
        hTs = sbl("hTs", [128, 8, 80], BF)
        sadd = sbl("sadd", [16, RP])
        stT = sbl("stT", [128, 2, 16])
        zf = sbl("zf", [128, 2, 64])
        QH = sbl("QH", [128, 6, 4, 64])
        Sst = sbl("Sst", [128, 64, 64])
        Stmp = sbl("Stmp", [128, 64, 64])
        sk = sbl("sk", [128, 64])
        yh = sbl("yh", [128, 4, 64])
        ytm = T[7]
        self.V(lambda e: e.memset(hTs[:], 0.0), w=["hTs"])
        i = NT
        xt, xk = self.xt[i % 2], "xt%d" % (i % 2)
        src, _ = self.xsrc(l, i)
        fw.dma(xt[0:MS, :], src, r=[("xb", i)], w=[xk], key=xk)
        self.norm_hT(xt, xk, MS, hTs[:, :, 16:80], "hTs", identb)
        hcur = lambda c: hTs[:, c, 16:80]
        hprev = lambda c: hTs[:, c, 0:64]
        fw.dma(sadd[:, :], I["st_shift"][l], w=["sadd"], key="sadd")
        for q in range(2):
            ps, pk = self.pf()
            fw.tr(ps[:, 0:16], sadd[0:16, 1536 + q * 128:1536 + (q + 1) * 128], identf[0:16, 0:16], r=["sadd", "identf"], w=[pk])
            self.V(lambda e, q=q, ps=ps: e.tensor_scalar(stT[:, q, :], ps[:, 0:16], mucol[:, q:q + 1], None, ALU.mult), r=[pk, "mucol"], w=["stT"])
        for gi, g0 in enumerate(range(0, RP, 512)):
            n = min(512, RP - g0)
            self.bcast_load(T[4 + gi][0:16, 0:n], "T%d" % (4 + gi), I["rwkv_mu"][l, g0:g0 + n])
            self.V(lambda e, gi=gi, g0=g0, n=n: e.tensor_tensor(sadd[:, g0:g0 + n], sadd[:, g0:g0 + n], T[4 + gi][0:16, 0:n], ALU.mult),
                   r=["sadd", "T%d" % (4 + gi)], w=["sadd"])
        zv, sgT = RS.zv, RS.sgT
        for g0, dst, dk in [(0, zr, "zr"), (512, zk, "zk"), (1024, zv, RS.K("zv"))]:
            ps, pk = tok_proj(MS, hcur, hprev, "hTs", g0, dk)
            fw.act(dst[0:MS, :], ps[0:MS, :], AF.Copy, r=[pk], w=[dk])
            self.V(lambda e, dst=dst, g0=g0: e.tensor_tensor(dst[0:16, :], dst[0:16, :], sadd[0:16, g0:g0 + 512], ALU.add), r=[dk, "sadd"], w=[dk])
        for q, g0 in enumerate([1536, 1664]):
            ps, pk = feat_proj(MS, hcur, hprev, "hTs", g0)
            fw.act(zf[:, q, :], ps[:, 0:MS], AF.Copy, r=[pk], w=["zf"])
            self.V(lambda e, q=q: e.tensor_tensor(zf[:, q, 0:16], zf[:, q, 0:16], stT[:, q, :], ALU.add), r=["zf", "stT"], w=["zf"])
        fw.act(lact[0:64, 0:MS], zf[0:64, 0, :], AF.Tanh, r=["zf"], w=["lact"])
        fw.act(lact[64:128, 0:MS], zf[64:128, 0, :], AF.Copy, r=["zf"], w=["lact"])
        fw.act(sgT[:, 0:MS], zf[:, 1, :], AF.Sigmoid, r=["zf"], w=[RS.K("sgT")])
        prep(MS, True, RS)
        if l == 0:
            for nm, ap, k in [("s_zr", zr, "zr"), ("s_zk", zk, "zk"), ("s_zv", zv, "zv"), ("s_dec", T[6], "T6"), ("s_kk", T[2], "T2"),
                              ("s_kf", T[4], "T4"), ("s_be", T[5], "T5"), ("s_a", T[1], "T1")]:
                self.tap(nm, ap[0:MS, :], [k])
        sqv = self.sq.rearrange("x (t q) (h d) -> (q h) x t d", t=4, h=NH)
        for x in range(6):
            fw.dma(QH[:, x, :, :], sqv[:, x, :, :], r=[("sq", x)], w=["QH"], key="QH")
        fw.dma(Sst[:, :, :].rearrange("p v k -> p (v k)"), I["st_wkv"][l], w=["Sst"], key="Sst")
        for t in range(4):
            r_, w_, k_, v_, kk_, b_ = (QH[:, x, t, :] for x in range(6))
            rowb = lambda a: a.unsqueeze(1).to_broadcast([128, 64, 64])
            colb = lambda a: a.unsqueeze(2).to_broadcast([128, 64, 64])
            self.V(lambda e, kk_=kk_: e.tensor_tensor(Stmp[:, :, :], Sst[:, :, :], rowb(kk_), ALU.mult), r=["Sst", "QH"], w=["Stmp"])
            self.V(lambda e: e.tensor_reduce(sk[:, :], Stmp[:, :, :], AX.X, ALU.add), r=["Stmp"], w=["sk"])
            self.P(lambda e, w_=w_: e.tensor_tensor(Sst[:, :, :], Sst[:, :, :], rowb(w_), ALU.mult), r=["Sst", "QH", "Stmp"], w=["Sst"])
            self.V(lambda e, b_=b_: e.tensor_tensor(Stmp[:, :, :], colb(sk[:, :]), rowb(b_), ALU.mult), r=["sk", "QH"], w=["Stmp"])
            self.V(lambda e: e.tensor_tensor(Sst[:, :, :], Sst[:, :, :], Stmp[:, :, :], ALU.subtract), r=["Sst", "Stmp"], w=["Sst"])
            self.P(lambda e, v_=v_, k_=k_: e.tensor_tensor(Stmp[:, :, :], colb(v_), rowb(k_), ALU.mult), r=["QH", "Sst"], w=["Stmp"])
            self.V(lambda e: e.tensor_tensor(Sst[:, :, :], Sst[:, :, :], Stmp[:, :, :], ALU.add), r=["Sst", "Stmp"], w=["Sst"])
            self.P(lambda e, r_=r_: e.tensor_tensor(Stmp[:, :, :], Sst[:, :, :], rowb(r_), ALU.mult), r=["Sst", "QH"], w=["Stmp"])
            self.V(lambda e, t=t: e.tensor_reduce(yh[:, t, :], Stmp[:, :, :], AX.X, ALU.add), r=["Stmp"], w=["yh"])
        fw.dma(O["s_wkv"][l], Sst[:, :, :].rearrange("p v k -> p (v k)"), r=["Sst"], key="Sst")
        if l == 0:
            self.tap("s_QH", QH, ["QH"])
            self.tap("s_yh", yh, ["yh"])
        fw.dma(self.sy.rearrange("(t q) (h d) -> (q h) t d", t=4, h=NH), yh[:, :, :], r=["yh"], w=["sy"], key="yh")
        fw.dma(ytm[0:MS, :], self.sy, r=["sy"], w=["T7"], key="ytm")
        pg, pgk = self.pf()
        fw.mm(pg[0:MS, :], sgT[:, 0:MS], lg2[:, :], True, True, r=[RS.K("sgT"), "lg2"], w=[pgk])
        post(MS, ytm[0:MS, :], ["T7"], pg, pgk, RS)
        m, mk = mrT[0], "mrT0"
        gate_branch(MS, hcur, "hTs", m, mk)
        fw.dma(self.mrbuf[NT].rearrange("p (c t) -> p c t", c=8)[:, :, 0:MS], m[:, :, 0:MS], r=[mk], w=[("mr", NT)], key=mk)
        raw_last(lambda c: hTs[:, c, 64:80], "hTs", 16, O["s_shift"][l])

    def pass_attn(self, l, es2):
        fw, I, O, NT = self.fw, self.I, self.O, self.NT
        sbl = lambda n, s, dt=F32: self.sbl(es2, "a%d_" % l + n, s, dt)
        identb, identf = self.identb, self.identf
        Wq = sbl("Wq", [128, 8, 768], BF)
        Wg = sbl("Wg", [128, 8, D], BF)
        Wa = sbl("Wa", [128, 4, D], BF)
        Wo = sbl("Wo", [128, 8, D], BF)
        self.col_load(self.gcol[:], "gcol", I["norm_mix_g"][l], 8)
        m0 = self.aoff
        self.wstage = [sbl("wst%d" % i_, [128, 2048]) for i_ in range(4)]
        win = I["w_in"][l]
        gsc = lambda c: self.gcol[:, c:c + 1]
        self.prep_w(8, 512, lambda c, s0, n: win[c * 128:(c + 1) * 128, RP:RP + 512],
                    lambda c, s0, n: Wq[:, c, 0:512].rearrange("p (j g d) -> p g j d", j=4, g=2), lambda c: "Wq_%d" % c, "col", gsc,
                    sview=lambda a: a.rearrange("p (g j d) -> p g j d", g=2, j=4))
        self.prep_w(8, 256, lambda c, s0, n: win[c * 128:(c + 1) * 128, RP + 512:RP + 768],
                    lambda c, s0, n: Wq[:, c, 512:768], lambda c: "Wq_%d" % c, "col", gsc)
        self.prep_w(8, D, lambda c, s0, n: win[c * 128:(c + 1) * 128, 3584 + s0:3584 + s0 + n],
                    lambda c, s0, n: Wg[:, c, s0:s0 + n], lambda c: "Wga_%d" % c, "col", gsc)
        wbr = I["w_br_attn"][l]
        self.prep_w(4, D, lambda c, s0, n: wbr[c * 128:(c + 1) * 128, s0:s0 + n],
                    lambda c, s0, n: Wa[:, c, s0:s0 + n], lambda c: "Wa_%d" % c, "plain")
        wo = I["w_out"][l]
        self.prep_w(8, D, lambda c, s0, n: wo[c * 128:(c + 1) * 128, s0:s0 + n],
                    lambda c, s0, n: Wo[:, c, s0:s0 + n], lambda c: "Wo_%d" % c, "plain")
        self.release(m0)
        amask = sbl("amask", [128, 1024])
        fw.dma(amask[:, 0:768], I["c_amask"], w=["amask"], key="amask")
        fw.dma(amask[:, 768:1024], I["c_amask0"], w=["amask"], key="amask")
        smask = sbl("smask", [32, 132])
        fw.dma(smask[:], I["c_smask"], w=["smask"], key="smask")
        sinks = sbl("sinks", [128, NH])
        self.bcast_load(sinks[:], "sinks", I["attn_sinks"][l])
        hTd = [sbl("hT%d" % i_, [128, 8, 128], BF) for i_ in range(2)]
        hT = hTd[1]
        qkv = sbl("qkv", [128, 768])
        rot = sbl("rot", [128, 640])
        rtmp = [sbl("rtmp%d" % i, [128, 320]) for i in range(2)]
        rotb = sbl("rotb", [128, 640], BF)
        cs = [sbl("cs%d" % i, [128, 64]) for i in range(2)]
        qT = sbl("qT", [128, 4, 128], BF)

        class NSB:
            pass
        B0, B1 = NSB(), NSB()
        B0.qkv, B0.rot, B0.rotb, B0.qT, B0.s = qkv, rot, rotb, qT, ""
        B1.qkv, B1.rot, B1.rotb, B1.qT, B1.s = (sbl("qkvb", [128, 768]), sbl("rotbb", [128, 640]), sbl("rotbbb", [128, 640], BF),
                                                sbl("qTb", [128, 4, 128], BF), "b")
        Bs = [B0, B1]
        KTr = sbl("KTr", [128, 2, 128], BF)
        Vp = sbl("Vp", [128, 2, 2, 2, 128], BF)
        scg = [sbl("sc%d" % g_, [128, 4, 256]) for g_ in range(2)]
        stg = [sbl("st%d" % g_, [128, 16]) for g_ in range(2)]
        pbfg = [sbl("pbf%d" % g_, [128, 4, 256], BF) for g_ in range(2)]
        pTg = [sbl("pT%d" % g_, [128, 4, 2, 128], BF) for g_ in range(2)]
        oT = sbl("oT", [128, 4, 128], BF)
        sga = sbl("sga", [128, 8, 128])
        mrl = [sbl("mrl%d" % i, [128, 8, 128], BF) for i in range(2)]
        mg = sbl("mg", [128, 8, 128], BF)
        xo = [sbl("xo%d" % i, [128, D]) for i in range(2)]
        KA = sbl("KA", [128, NS, 128])
        VA = sbl("VA", [128, NS, 128])
        VAb = sbl("VAb", [128, NS, 128], BF)
        KB = sbl("KB", [4, NS, 128])
        VBt = sbl("VB", [4, NS, 128])
        VBb = sbl("VBb", [4, NS, 128], BF)
        KAT = sbl("KAT", [128, NS, 128], BF)
        KBT = sbl("KBT", [128, NS, 4], BF)
        qbd = sbl("qbd", [128, NS, 32], BF)
        ssc = sbl("ssc", [32, NS, 132])
        sst = sbl("sst", [32, 4 * NS])
        spb = sbl("spb", [32, NS, 132], BF)
        spT = sbl("spT", [128, NS, 32], BF)
        spTB = sbl("spTB", [4, NS, 32], BF)
        oTs = sbl("oTs", [128, 4, MS], BF)

        self.V(lambda e: e.memset(Vp[:], 0.0), w=["Vp0", "Vp1"])
        self.V(lambda e: e.memset(KTr[:], 0.0), w=["KTr0", "KTr1"])
        self.V(lambda e: e.memset(qbd[:], 0.0), w=["qbd"])

        def proj_rope(B, M, hcur, hk, cosap, sinap, cskey):
            for g0, n in [(0, 512), (512, 256)]:
                ps, pk = self.pf()
                for c in range(8):
                    fw.mm(ps[0:M, 0:n], hcur(c), Wq[:, c, g0:g0 + n], c == 0, c == 7, r=[hk, "Wq_%d" % c], w=[pk])
                fw.act(B.qkv[0:M, g0:g0 + n], ps[0:M, 0:n], AF.Copy, r=[pk], w=["qkv%d" % (g0 // 512) + B.s])
            qk3 = B.qkv[0:M, 0:640].rearrange("p (h d) -> p h d", h=10)
            r3 = B.rot[0:M, :].rearrange("p (h d) -> p h d", h=10)
            x1, x2 = qk3[:, :, 0:32], qk3[:, :, 32:64]
            cb = cosap.unsqueeze(1).to_broadcast([M, 10, 32])
            sb_ = sinap.unsqueeze(1).to_broadcast([M, 10, 32])
            ta = rtmp[0][0:M, :].rearrange("p (h d) -> p h d", h=10)
            tb = rtmp[1][0:M, :].rearrange("p (h d) -> p h d", h=10)
            rk = ["qkv0" + B.s, "qkv1" + B.s, cskey]
            rotk = "rot" + B.s
            self.V(lambda e: e.tensor_tensor(ta, x1, cb, ALU.mult), r=rk, w=["rtmp0"])
            self.P(lambda e: e.tensor_tensor(tb, x2, sb_, ALU.mult), r=rk, w=["rtmp1"])
            self.V(lambda e: e.tensor_tensor(r3[:, :, 0:32], ta, tb, ALU.subtract), r=["rtmp0", "rtmp1"], w=[rotk])
            self.V(lambda e: e.tensor_tensor(ta, x2, cb, ALU.mult), r=rk + [rotk], w=["rtmp0"])
            self.P(lambda e: e.tensor_tensor(tb, x1, sb_, ALU.mult), r=rk + [rotk], w=["rtmp1"])
            self.V(lambda e: e.tensor_tensor(r3[:, :, 32:64], ta, tb, ALU.add), r=["rtmp0", "rtmp1"], w=[rotk])
            fw.act(B.rotb[0:M, :], B.rot[0:M, :], AF.Copy, r=[rotk], w=["rotb" + B.s])

        def q_transposes(B, M, dst, dkey):
            pbk, pk = self.pb()
            for jj in range(4):
                fw.tr(pbk[:, jj * M:(jj + 1) * M], B.rotb[0:M, jj * 128:(jj + 1) * 128], identb[0:M, 0:M], r=["rotb" + B.s, "identb"], w=[pk])
            fw.act(dst, pbk[:, 0:4 * M].rearrange("p (j t) -> p j t", j=4), AF.Copy, r=[pk], w=[dkey])

        def gates_part(M, hcur, hk):
            for half in range(2):
                pg, pgk = self.pf()
                for q in range(4):
                    dc = half * 4 + q
                    for c in range(8):
                        fw.mm(pg[:, q * M:(q + 1) * M], Wg[:, c, dc * 128:(dc + 1) * 128], hcur(c), c == 0, c == 7, r=[hk, "Wga_%d" % c], w=[pgk])
                fw.act(sga[:, half * 4:(half + 1) * 4, 0:M], pg[:, 0:4 * M].rearrange("p (q t) -> p q t", q=4), AF.Sigmoid, r=[pgk], w=["sga%d" % half])

        def gate_out(M, hcur, hk, oTt, okey, mr, mrk, xt, xk, xo_, xok, do_gates=True):
            if do_gates:
                gates_part(M, hcur, hk)
            for half in range(2):
                pbr, pbk_ = self.pf()
                for q in range(4):
                    dc = half * 4 + q
                    for cc in range(4):
                        fw.mm(pbr[:, q * M:(q + 1) * M], Wa[:, cc, dc * 128:(dc + 1) * 128], oTt[:, cc, 0:M], cc == 0, cc == 3, r=[okey, "Wa_%d" % cc], w=[pbk_])
                hs = slice(half * 4, (half + 1) * 4)
                self.V(lambda e, hs=hs, pbr=pbr: e.tensor_tensor(sga[:, hs, 0:M], sga[:, hs, 0:M], pbr[:, 0:4 * M].rearrange("p (q t) -> p q t", q=4), ALU.mult),
                       r=["sga%d" % half, pbk_], w=["sga%d" % half])
                self.V(lambda e, hs=hs: e.tensor_tensor(mg[:, hs, 0:M], sga[:, hs, 0:M], mr[:, hs, 0:M], ALU.add), r=["sga%d" % half, mrk], w=["mg%d" % half])
            for grp in range(2):
                px, pxk = self.pf()
                for dc in range(8):
                    fw.mm(px[0:M, :], mg[:, dc, 0:M], Wo[:, dc, grp * 512:(grp + 1) * 512], dc == 0, dc == 7, r=["mg%d" % (dc // 4), "Wo_%d" % dc], w=[pxk])
                self.V(lambda e, grp=grp, px=px: e.tensor_tensor(xo_[0:M, grp * 512:(grp + 1) * 512], xt[0:M, grp * 512:(grp + 1) * 512], px[0:M, :], ALU.add),
                       r=[xk, pxk], w=[xok])

        def put_kv(B, slot):
            pbk, pk = self.pb()
            fw.tr(pbk[:, 0:128], B.rotb[:, 512:640], identb[:, :], r=["rotb" + B.s, "identb"], w=[pk])
            self.V(lambda e, pbk=pbk, slot=slot: e.tensor_copy(KTr[:, slot, :], pbk[:, 0:128]), r=[pk], w=["KTr%d" % slot])
            for g in range(2):
                vsrc = B.qkv[:, 640 + g * 64:640 + (g + 1) * 64]
                fw.act(Vp[:, slot, g, 0, 0:64], vsrc, AF.Copy, r=["qkv1" + B.s], w=["Vp%d" % slot])
                self.P(lambda e, g=g, vsrc=vsrc, slot=slot: e.tensor_copy(Vp[:, slot, g, 1, 64:128], vsrc), r=["qkv1" + B.s], w=["Vp%d" % slot])

        xt, xk = self.xt[1], "xt1"
        fw.dma(xt[:], (I["xh0"] if (l == 0 or NSEG == 1) else self.xh_dram), r=["xh_dram"], w=[xk], key=xk)
        fw.dma(cs[1][:, 0:32], I["c_cosh"], w=["cs1"], key="cs1")
        fw.dma(cs[1][:, 32:64], I["c_sinh"], w=["cs1"], key="cs1")
        self.norm_hT(xt, xk, 128, hT[:, :, :], "hT1", identb)
        proj_rope(B1, 128, lambda c: hT[:, c, :], "hT1", cs[1][:, 0:32], cs[1][:, 32:64], "cs1")
        put_kv(B1, 1)
        def pre(i):
            xt, xk = self.xt[i % 2], "xt%d" % (i % 2)
            src, _ = self.xsrc(l, i)
            fw.dma(xt[:], src, r=[("xb", i)], w=[xk], key=xk)
            mr, mrk = mrl[i % 2], "mrl%d" % (i % 2)
            fw.dma(mr[:, :, :], self.mrbuf[i].rearrange("p (c t) -> p c t", c=8), r=[("mr", i)], w=[mrk], key=mrk)
            ck_ = "cs%d" % (i % 2)
            fw.dma(cs[i % 2][:, 0:32], I["c_cosp"][i * 128:(i + 1) * 128, :], w=[ck_], key=ck_)
            fw.dma(cs[i % 2][:, 32:64], I["c_sinp"][i * 128:(i + 1) * 128, :], w=[ck_], key=ck_)
            self.norm_hT(xt, xk, 128, hTd[i % 2][:, :, :], "hT%d" % (i % 2), identb)
            B = Bs[i % 2]
            proj_rope(B, 128, lambda c, i=i: hTd[i % 2][:, c, :], "hT%d" % (i % 2), cs[i % 2][:, 0:32], cs[i % 2][:, 32:64], ck_)
            q_transposes(B, 128, B.qT[:, :, :], "qT" + B.s)

        pre(0)
        for i in range(NT):
            xt, xk = self.xt[i % 2], "xt%d" % (i % 2)
            mr, mrk = mrl[i % 2], "mrl%d" % (i % 2)
            ck_ = "cs%d" % (i % 2)
            hkk = "hT%d" % (i % 2)
            hcur = lambda c, i=i: hTd[i % 2][:, c, :]
            B = Bs[i % 2]
            slot = i % 2
            if i == NT - 1:
                fw.dma(O["p_k"][l], B.rot[:, 512:640], r=["rot" + B.s], key="rot")
                fw.dma(O["p_v"][l], B.qkv[:, 640:768], r=["qkv1" + B.s], key="qkv1")
            put_kv(B, slot)
            mvar = 3 if i == 0 else slot
            msk = amask[:, mvar * 256:(mvar + 1) * 256].unsqueeze(1).to_broadcast([128, 4, 256])
            pSg = []
            for g in range(2):
                o = g * 64
                pS = []
                for jj in range(4):
                    if jj % 2 == 0:
                        ps, pk = self.pf()
                        pS.append((ps, pk))
                    fw.mm(ps[:, (jj % 2) * 256:(jj % 2 + 1) * 256], B.qT[o:o + 64, jj, :], KTr[o:o + 64, :, :].rearrange("p s t -> p (s t)"),
                          True, True, r=["qT" + B.s, "KTr0", "KTr1"], w=[pk])
                pSg.append(pS)
            gates_part(128, hcur, hkk)

            def softmax(g):
                sc, st, pbf = scg[g], stg[g], pbfg[g]
                sck = ["sc%d_0" % g, "sc%d_1" % g]
                for half, (ps, pk) in enumerate(pSg[g]):
                    self.V(lambda e, ps=ps, half=half, msk=msk, sc=sc: e.scalar_tensor_tensor(
                        sc[:, half * 2:(half + 1) * 2, :], ps[:, :].rearrange("p (j c) -> p j c", j=2), 0.125,
                        msk[:, 0:2, :], ALU.mult, ALU.add), r=[pk, "amask"], w=[sck[half]])
                k0, k2, k3 = "st%d" % g, "st%d_2" % g, "st%d_3" % g
                self.V(lambda e: e.tensor_reduce(st[:, 0:4], sc[:, :, :], AX.X, ALU.max), r=sck, w=[k0])
                self.V(lambda e: e.tensor_tensor(st[:, 0:4], st[:, 0:4], sinks[:, g * 4:(g + 1) * 4], ALU.max), r=[k0, "sinks"], w=[k0])
                self.V(lambda e: e.tensor_tensor(sc[:, :, :], sc[:, :, :], bc3(st[:, 0:4], 256), ALU.subtract), r=sck + [k0], w=sck)
                fw.act(sc[:, :, :], sc[:, :, :], AF.Exp, r=sck, w=sck)
                self.V(lambda e: e.tensor_reduce(st[:, 4:8], sc[:, :, :], AX.X, ALU.add), r=sck, w=[k2])
                self.V(lambda e: e.tensor_tensor(st[:, 8:12], sinks[:, g * 4:(g + 1) * 4], st[:, 0:4], ALU.subtract), r=[k0, "sinks"], w=[k3])
                fw.act(st[:, 8:12], st[:, 8:12], AF.Exp, r=[k3], w=[k3])
                self.V(lambda e: e.tensor_tensor(st[:, 4:8], st[:, 4:8], st[:, 8:12], ALU.add), r=[k2, k3], w=[k2])
                self.V(lambda e: e.reciprocal(st[:, 4:8], st[:, 4:8]), r=[k2], w=[k2])
                self.V(lambda e: e.tensor_tensor(pbf[:, :, :], sc[:, :, :], bc3(st[:, 4:8], 256), ALU.mult), r=sck + [k2], w=["pbf%d" % g])

            def p_transposes(g):
                pbf, pT = pbfg[g], pTg[g]
                pbk, pk = self.pb()
                for jj in range(4):
                    for s_ in range(2):
                        fw.tr(pbk[:, (jj * 2 + s_) * 128:(jj * 2 + s_ + 1) * 128], pbf[:, jj, s_ * 128:(s_ + 1) * 128], identb[:, :], r=["pbf%d" % g, "identb"], w=[pk])
                fw.act(pT[:, :, :, :], pbk[:, :].rearrange("p (j s t) -> p j s t", j=4, s=2), AF.Copy, r=[pk], w=["pT%d" % g])

            def pv(g, pO, pok):
                pT = pTg[g]
                for c2 in range(2):
                    cc = g * 2 + c2
                    n = 0
                    for par in range(2):
                        jj = c2 * 2 + par
                        for s_ in range(2):
                            fw.mm(pO[:, cc * 128:(cc + 1) * 128], Vp[:, s_, g, par, :], pT[:, jj, s_, :], n == 0, n == 3,
                                  r=["Vp0", "Vp1", "pT%d" % g], w=[pok])
                            n += 1

            softmax(0)
            p_transposes(0)
            pO, pok = self.pf()
            pv(0, pO, pok)
            softmax(1)
            if i + 1 < NT:
                pre(i + 1)
            p_transposes(1)
            pv(1, pO, pok)
            fw.act(oT[:, :, :], pO[:, :].rearrange("p (c t) -> p c t", c=4), AF.Copy, r=[pok], w=["oT"])
            xo_, xok = xo[i % 2], "xo%d" % (i % 2)
            gate_out(128, hcur, hkk, oT, "oT", mr, mrk, xt, xk, xo_, xok, do_gates=False)
            fw.dma(self.xbuf[i * 128:(i + 1) * 128, :], xo_[:, :], r=[xok], w=[("xb", i)], key=xok)
        if NSEG > 1:
            self.gather_select(xo_[:, :], [xok], D, self.agX_in, self.agX_out, "agX")
            fw.dma(self.xh_dram, xo_[:, :], r=[xok], w=["xh_dram"], key="xhst")

        i = NT
        xt, xk = self.xt[i % 2], "xt%d" % (i % 2)
        src, _ = self.xsrc(l, i)
        fw.dma(xt[0:MS, :], src, r=[("xb", i)], w=[xk], key=xk)
        mr, mrk = mrl[i % 2], "mrl%d" % (i % 2)
        fw.dma(mr[:, :, 0:MS], self.mrbuf[NT].rearrange("p (c t) -> p c t", c=8)[:, :, 0:MS], r=[("mr", NT)], w=[mrk], key=mrk)
        ck_ = "cs%d" % (i % 2)
        fw.dma(cs[i % 2][0:MS, 0:32], I["c_coss"], w=[ck_], key=ck_)
        fw.dma(cs[i % 2][0:MS, 32:64], I["c_sins"], w=[ck_], key=ck_)
        self.norm_hT(xt, xk, MS, hT[:, :, 0:MS], "hT1", identb)
        hcur = lambda c: hT[:, c, 0:MS]
        proj_rope(B0, MS, hcur, "hT1", cs[i % 2][0:MS, 0:32], cs[i % 2][0:MS, 32:64], ck_)
        for (cin, cout, srcap, srck, dkey) in [("ck", "s_k", rot[:, 512:640], "rot", "sk"), ("cv", "s_v", qkv[:, 640:768], "qkv1", "sv")]:
            fw.dma(O[cout][l, :, 0:124, :], I[cin][l, :, 4:128, :], w=[dkey], key=dkey + "c")
            for t in range(4):
                fw.dma(O[cout][l, :, 124 + t, :], srcap[t * 16:(t + 1) * 16, :], r=[srck], w=[dkey], key=dkey + "n")
        fw.dma(KA[:, :, :], O["s_k"][l].rearrange("q p c -> p q c"), r=["sk"], w=["KA"], key="KA")
        fw.dma(VA[:, :, :], O["s_v"][l].rearrange("q p c -> p q c"), r=["sv"], w=["VA"], key="VA")
        fw.dma(KB[:, :, :], I["ck"][l, :, 0:4, :].rearrange("q p c -> p q c"), w=["KB"], key="KB")
        fw.dma(VBt[:, :, :], I["cv"][l, :, 0:4, :].rearrange("q p c -> p q c"), w=["VB"], key="VB")
        self.P(lambda e: e.tensor_copy(VAb[:, :, :], VA[:, :, :]), r=["VA"], w=["VAb"])
        self.P(lambda e: e.tensor_copy(VBb[:, :, :], VBt[:, :, :]), r=["VB"], w=["VBb"])
        for q4 in range(4):
            ps, pk = self.pf()
            for qq in range(4):
                q = q4 * 4 + qq
                fw.tr(ps[:, qq * 128:(qq + 1) * 128], KA[:, q, :], identf[:, :], r=["KA", "identf"], w=[pk])
            fw.act(KAT[:, q4 * 4:(q4 + 1) * 4, :], ps[:, :].rearrange("p (q t) -> p q t", q=4), AF.Copy, r=[pk], w=["KAT"])
        ps, pk = self.pf()
        for q in range(NS):
            fw.tr(ps[:, q * 4:(q + 1) * 4], KB[0:4, q, :], identf[0:4, 0:4], r=["KB", "identf"], w=[pk])
        fw.act(KBT[:, :, :], ps[:, 0:64].rearrange("p (q t) -> p q t", q=NS), AF.Copy, r=[pk], w=["KBT"])
        q_transposes(B0, MS, qT[:, :, 0:MS], "qT")
        for g in range(2):
            for jj in range(4):
                o = g * 64
                dst = qbd[o:o + 64, :, g * 16 + jj * 4:g * 16 + (jj + 1) * 4]
                srcq = qT[o:o + 64, jj, 0:MS].rearrange("p (t q) -> p q t", t=4)
                self.V(lambda e, dst=dst, srcq=srcq: e.tensor_copy(dst, srcq), r=["qT"], w=["qbd"])
        pSA = []
        for q4 in range(4):
            ps, pk = self.pf()
            pSA.append((ps, pk))
            for qq in range(4):
                q = q4 * 4 + qq
                fw.mm(ps[0:32, qq * 128:(qq + 1) * 128], qbd[:, q, :], KAT[:, q, :], True, True, r=["qbd", "KAT"], w=[pk])
        psB, pkB = self.pf()
        for q in range(NS):
            fw.mm(psB[0:32, q * 4:(q + 1) * 4], qbd[:, q, :], KBT[:, q, :], True, True, r=["qbd", "KBT"], w=[pkB])
        for q4, (ps, pk) in enumerate(pSA):
            self.V(lambda e, q4=q4, ps=ps: e.scalar_tensor_tensor(
                ssc[:, q4 * 4:(q4 + 1) * 4, 0:128], ps[0:32, :].rearrange("p (q c) -> p q c", q=4), 0.125,
                smask[:, 0:128].unsqueeze(1).to_broadcast([32, 4, 128]), ALU.mult, ALU.add), r=[pk, "smask"], w=["ssc"])
        self.V(lambda e: e.scalar_tensor_tensor(
            ssc[:, :, 128:132], psB[0:32, 0:64].rearrange("p (q c) -> p q c", q=NS), 0.125,
            smask[:, 128:132].unsqueeze(1).to_broadcast([32, NS, 4]), ALU.mult, ALU.add), r=[pkB, "smask"], w=["ssc"])
        sinkc = sbl("sinkc", [32, 1])
        for g in range(2):
            for jj in range(4):
                p0 = g * 16 + jj * 4
                fw.dma(sinkc[p0:p0 + 4, :], I["attn_sinks"][l, g * 4 + jj:g * 4 + jj + 1].partition_broadcast(4), w=["sinkc"], key="sinkc")
        self.V(lambda e: e.tensor_reduce(sst[:, 0:NS], ssc[:, :, :], AX.X, ALU.max), r=["ssc"], w=["sst"])
        self.V(lambda e: e.tensor_scalar(sst[:, 0:NS], sst[:, 0:NS], sinkc[:, 0:1], None, ALU.max), r=["sst", "sinkc"], w=["sst"])
        self.V(lambda e: e.tensor_tensor(ssc[:, :, :], ssc[:, :, :], bc3(sst[:, 0:NS], 132), ALU.subtract), r=["ssc", "sst"], w=["ssc"])
        fw.act(ssc[:, :, :], ssc[:, :, :], AF.Exp, r=["ssc"], w=["ssc"])
        self.V(lambda e: e.tensor_reduce(sst[:, NS:2 * NS], ssc[:, :, :], AX.X, ALU.add), r=["ssc"], w=["sst2"])
        self.V(lambda e: e.tensor_scalar(sst[:, 2 * NS:3 * NS], sst[:, 0:NS], sinkc[:, 0:1], None, ALU.subtract), r=["sst", "sinkc"], w=["sst3"])
        fw.act(sst[:, 2 * NS:3 * NS], sst[:, 2 * NS:3 * NS], AF.Exp, r=["sst3"], w=["sst3"], scale=-1.0)
        self.V(lambda e: e.tensor_tensor(sst[:, NS:2 * NS], sst[:, NS:2 * NS], sst[:, 2 * NS:3 * NS], ALU.add), r=["sst2", "sst3"], w=["sst2"])
        self.V(lambda e: e.reciprocal(sst[:, NS:2 * NS], sst[:, NS:2 * NS]), r=["sst2"], w=["sst2"])
        self.V(lambda e: e.tensor_tensor(spb[:, :, :], ssc[:, :, :], bc3(sst[:, NS:2 * NS], 132), ALU.mult), r=["ssc", "sst2"], w=["spb"])
        identb32 = identb[0:32, 0:32]
        for q8 in range(2):
            pbk, pk = self.pb()
            for qq in range(8):
                q = q8 * 8 + qq
                fw.tr(pbk[:, qq * 32:(qq + 1) * 32], spb[:, q, 0:128], identb32, r=["spb", "identb"], w=[pk])
            fw.act(spT[:, q8 * 8:(q8 + 1) * 8, :], pbk[:, 0:256].rearrange("p (q c) -> p q c", q=8), AF.Copy, r=[pk], w=["spT"])
        pbk, pk = self.pb()
        for q in range(NS):
            fw.tr(pbk[0:4, q * 32:(q + 1) * 32], spb[:, q, 128:132], identb32, r=["spb", "identb"], w=[pk])
        fw.act(spTB[:, :, :], pbk[0:4, 0:512].rearrange("p (q c) -> p q c", q=NS), AF.Copy, r=[pk], w=["spTB"])
        pO, pok = self.pf()
        for q in range(NS):
            fw.mm(pO[:, q * 32:(q + 1) * 32], VAb[:, q, :], spT[:, q, :], True, False, r=["VAb", "spT"], w=[pok])
            fw.mm(pO[:, q * 32:(q + 1) * 32], VBb[0:4, q, :], spTB[0:4, q, :], False, True, r=["VBb", "spTB"], w=[pok])
        oraw = sbl("oraw", [128, 32, NS], BF)
        fw.act(oraw.rearrange("p c q -> p q c"), pO[:, :].rearrange("p (q c) -> p q c", q=NS), AF.Copy, r=[pok], w=["oraw"])
        for g in range(2):
            for jj in range(4):
                cc, par = g * 2 + jj // 2, jj % 2
                c0 = g * 16 + jj * 4
                srco = oraw[g * 64:(g + 1) * 64, c0:c0 + 4, :].rearrange("p t q -> p (t q)")
                fw.dma(oTs[par * 64:(par + 1) * 64, cc, :], srco, r=["oraw"], w=["oTs"], key="oTs")
        xo_, xok = xo[i % 2], "xo%d" % (i % 2)
        gate_out(MS, hcur, "hT1", oTs, "oTs", mr, mrk, xt, xk, xo_, xok)
        fw.dma(self.xsbuf, xo_[0:MS, :], r=[xok], w=[("xb", NT)], key=xok)

    def pass_ffn(self, l, es2):
        fw, I, O, NT = self.fw, self.I, self.O, self.NT
        sbl = lambda n, s, dt=F32: self.sbl(es2, "f%d_" % l + n, s, dt)
        identb, identf = self.identb, self.identf
        Wc = sbl("Wc", [128, 8, DFF], BF)
        Wu = sbl("Wu", [128, 8, DFF], BF)
        Wd = sbl("Wd", [128, NFC, D], BF)
        self.col_load(self.gcol[:], "gcol", I["norm_ffn_g"][l], 8)
        cw = sbl("cw", [128, 4, NFC])
        for j in range(3):
            self.col_load(cw[:, j, :], "cw", I["ffn_conv_w"][l, j], NFC)
        self.col_load(cw[:, 3, :], "cw", I["ffn_conv_b"][l], NFC)
        m0 = self.aoff
        self.wstage = [sbl("wst%d" % i_, [128, 2048]) for i_ in range(4)]
        wi = I["ffn_w_in"][l]
        gsc = lambda c: self.gcol[:, c:c + 1]
        self.prep_w(8, DFF, lambda c, s0, n: wi[c * 128:(c + 1) * 128, s0:s0 + n],
                    lambda c, s0, n: Wc[:, c, s0:s0 + n], lambda c: "Wc_%d" % c, "col", gsc)
        self.prep_w(8, DFF, lambda c, s0, n: wi[c * 128:(c + 1) * 128, DFF + s0:DFF + s0 + n],
                    lambda c, s0, n: Wu[:, c, s0:s0 + n], lambda c: "Wu_%d" % c, "col", gsc)
        wd = I["ffn_w_down"][l]
        self.prep_w(NFC, D, lambda c, s0, n: wd[c * 128:(c + 1) * 128, s0:s0 + n],
                    lambda c, s0, n: Wd[:, c, s0:s0 + n], lambda c: "Wd_%d" % c, "plain")
        self.release(m0)
        last = (l == 1)
        if last:
            gf = sbl("gf", [128, D])
            self.bcast_load(gf[:], "gf", I["norm_final_g"])
        hTd = [sbl("hT%d" % i_, [128, 8, 128], BF) for i_ in range(2)]
        hT = hTd[0]
        cxf = sbl("cx", [128, NFC * 130])
        cx1 = cxf.rearrange("p (f t) -> p f t", f=NFC)
        cxs = cxf[:, 0:NFC * NS * 6].rearrange("p (f q j) -> p f q j", f=NFC, q=NS)
        acc = [sbl("acc%d" % i_, [128, 4, 128]) for i_ in range(2)]
        aTd = [sbl("aT%d" % i_, [128, NFC, 128], BF) for i_ in range(2)]
        xo = [sbl("xo%d" % i_, [128, D]) for i_ in range(2)]
        ctok = sbl("ctok", [128, DFF])
        cst = ctok
        jk = self.xn

        def finish(M, xt, xk, xo_, xok, dst_final, dst_x, dkey, aT, aTk):
            for grp in range(2):
                px, pxk = self.pf()
                for fc in range(NFC):
                    fw.mm(px[0:M, :], aT[:, fc, 0:M], Wd[:, fc, grp * 512:(grp + 1) * 512], fc == 0, fc == NFC - 1, r=[aTk, "Wd_%d" % fc], w=[pxk])
                self.V(lambda e, grp=grp, px=px: e.tensor_tensor(xo_[0:M, grp * 512:(grp + 1) * 512], xt[0:M, grp * 512:(grp + 1) * 512], px[0:M, :], ALU.add),
                       r=[xk, pxk], w=[xok])
            if not last:
                fw.dma(dst_x, xo_[0:M, :], r=[xok], w=[dkey], key=xok)
                return
            ss, t1 = self.ss, self.t1
            fw.act(jk[0:M, :], xo_[0:M, :], AF.Square, r=[xok], w=["xn", "ss"], accum_out=ss[0:M, :])
            self.V(lambda e: e.tensor_scalar(t1[0:M, :], ss[0:M, :], 1.0 / D, 1e-6, ALU.mult, ALU.add), r=["ss"], w=["t1"])
            fw.act(t1[0:M, :], t1[0:M, :], AF.Sqrt, r=["t1"], w=["t1"])
            self.V(lambda e: e.reciprocal(t1[0:M, :], t1[0:M, :]), r=["t1"], w=["t1"])
            self.V(lambda e: e.scalar_tensor_tensor(xo_[0:M, :], xo_[0:M, :], t1[0:M, 0:1], gf[0:M, :], ALU.mult, ALU.mult),
                   r=[xok, "t1", "gf"], w=[xok])
            fw.dma(dst_final, xo_[0:M, :], r=[xok], key=xok)

        def ffn_core(M, hcur, hk, cview, ckey, sample, aT, aTk, mid=None, groups=None):
            for b0 in (groups if groups is not None else range(0, NFC, 4)):
                nb = min(4, NFC - b0)
                pc, pck = self.pf()
                for q in range(nb):
                    fc = b0 + q
                    for c in range(8):
                        fw.mm(pc[:, q * M:(q + 1) * M], Wc[:, c, fc * 128:(fc + 1) * 128], hcur(c), c == 0, c == 7, r=[hk, "Wc_%d" % c], w=[pck])
                pu, puk = self.pf()
                for q in range(nb):
                    fc = b0 + q
                    for c in range(8):
                        fw.mm(pu[:, q * M:(q + 1) * M], Wu[:, c, fc * 128:(fc + 1) * 128], hcur(c), c == 0, c == 7, r=[hk, "Wu_%d" % c], w=[puk])
                if sample:
                    fw.act(cview[:, b0:b0 + nb, :, 2:6], pc[:, 0:nb * M].rearrange("p (f t q) -> p f q t", f=nb, t=4), AF.Copy, r=[pck], w=[ckey])
                else:
                    fw.act(cview[:, b0:b0 + nb, 2:130], pc[:, 0:nb * M].rearrange("p (f t) -> p f t", f=nb), AF.Copy, r=[pck], w=[ckey])
                a_ = acc[(b0 // 4) % 2]
                ak = "acc%d" % ((b0 // 4) % 2)
                for q in range(nb):
                    fc = b0 + q
                    if sample:
                        c0, c1, c2 = (cview[:, fc, :, s_:s_ + 4] for s_ in range(3))
                        av = a_[:, q, 0:M].rearrange("p (t q) -> p q t", t=4)
                    else:
                        c0, c1, c2 = (cview[:, fc, s_:s_ + 128] for s_ in range(3))
                        av = a_[:, q, :]
                    self.P(lambda e, av=av, c0=c0, fc=fc: e.tensor_scalar(av, c0, cw[:, 0, fc:fc + 1], cw[:, 3, fc:fc + 1], ALU.mult, ALU.add),
                           r=[ckey, "cw"], w=[ak])
                    self.V(lambda e, av=av, c1=c1, fc=fc: e.scalar_tensor_tensor(av, c1, cw[:, 1, fc:fc + 1], av, ALU.mult, ALU.add),
                           r=[ckey, "cw", ak], w=[ak])
                    self.V(lambda e, av=av, c2=c2, fc=fc: e.scalar_tensor_tensor(av, c2, cw[:, 2, fc:fc + 1], av, ALU.mult, ALU.add),
                           r=[ckey, "cw", ak], w=[ak])
                fw.act(a_[:, 0:nb, 0:M], a_[:, 0:nb, 0:M], AF.Gelu, r=[ak], w=[ak])
                self.V(lambda e, a_=a_, pu=pu, nb=nb, b0=b0, aT=aT: e.tensor_tensor(aT[:, b0:b0 + nb, 0:M], a_[:, 0:nb, 0:M],
                                                                              pu[:, 0:nb * M].rearrange("p (f t) -> p f t", f=nb), ALU.mult),
                       r=[ak, puk], w=[aTk])
                if mid is not None and b0 == 8:
                    mid()

        def c_token_major(M, hcur, hk, rows, dsts):
            for g0 in range(0, DFF, 512):
                n = min(512, DFF - g0)
                ps, pk = self.pf()
                for c in range(8):
                    fw.mm(ps[0:M, 0:n], hcur(c), Wc[:, c, g0:g0 + n], c == 0, c == 7, r=[hk, "Wc_%d" % c], w=[pk])
                fw.act(ctok[0:M, g0:g0 + n], ps[0:M, 0:n], AF.Copy, r=[pk], w=["ctok"])
            for (r0, r1), dst in zip(rows, dsts):
                fw.dma(dst, ctok[r0:r1, :], r=["ctok"], key="ctok")

        xt, xk = self.xt[1], "xt1"
        fw.dma(xt[:], (I["xh0"] if NSEG == 1 else self.xh_dram), r=["xh_dram"], w=[xk], key=xk)
        self.norm_hT(xt, xk, 128, hT[:, :, :], "hT0", identb)
        pc, pck = self.pf()
        for fc in range(NFC):
            for c in range(8):
                fw.mm(pc[:, fc * 2:(fc + 1) * 2], Wc[:, c, fc * 128:(fc + 1) * 128], hT[:, c, 126:128], c == 0, c == 7, r=["hT0", "Wc_%d" % c], w=[pck])
        fw.act(cx1[:, :, 0:2], pc[:, 0:2 * NFC].rearrange("p (f t) -> p f t", f=NFC), AF.Copy, r=[pck], w=["cx"])
        def pre(i):
            xt, xk = self.xt[i % 2], "xt%d" % (i % 2)
            fw.dma(xt[:], self.xbuf[i * 128:(i + 1) * 128, :], r=[("xb", i)], w=[xk], key=xk)
            self.norm_hT(xt, xk, 128, hTd[i % 2][:, :, :], "hT%d" % (i % 2), identb)

        def head(i):
            if i > 0:
                self.P(lambda e: e.tensor_copy(acc[0][:, 0, 0:2 * NFC].rearrange("p (f t) -> p f t", f=NFC), cx1[:, :, 128:130]), r=["cx"], w=["acc0"])
                self.P(lambda e: e.tensor_copy(cx1[:, :, 0:2], acc[0][:, 0, 0:2 * NFC].rearrange("p (f t) -> p f t", f=NFC)), r=["acc0"], w=["cx"])
            ffn_core(128, lambda c, i=i: hTd[i % 2][:, c, :], "hT%d" % (i % 2), cx1, "cx", False, aTd[i % 2], "aT%d" % (i % 2), groups=[0])

        pre(0)
        head(0)
        for i in range(NT):
            xt, xk = self.xt[i % 2], "xt%d" % (i % 2)
            hcur = lambda c, i=i: hTd[i % 2][:, c, :]
            hkk = "hT%d" % (i % 2)
            mid = (lambda i=i: pre(i + 1)) if i + 1 < NT else None
            ffn_core(128, hcur, hkk, cx1, "cx", False, aTd[i % 2], "aT%d" % (i % 2), mid, groups=list(range(4, NFC, 4)))
            if i == NT - 1:
                c_token_major(128, hcur, hkk, [(126, 128)], [O["p_conv"][l]])
            if i + 1 < NT:
                head(i + 1)
            xo_, xok = xo[i % 2], "xo%d" % (i % 2)
            finish(128, xt, xk, xo_, xok, O["yp"][i * 128:(i + 1) * 128, :], self.xbuf[i * 128:(i + 1) * 128, :], ("xb", i),
                   aTd[i % 2], "aT%d" % (i % 2))
        if not last and NSEG > 1:
            self.gather_select(xo_[:, :], [xok], D, self.agX_in, self.agX_out, "agX")
            fw.dma(self.xh_dram, xo_[:, :], r=[xok], w=["xh_dram"], key="xhst")

        i = NT
        xt, xk = self.xt[i % 2], "xt%d" % (i % 2)
        fw.dma(xt[0:MS, :], self.xsbuf, r=[("xb", i)], w=[xk], key=xk)
        self.norm_hT(xt, xk, MS, hT[:, :, 0:MS], "hT0", identb)
        hcur = lambda c: hT[:, c, 0:MS]
        fw.dma(cst[0:32, :], I["st_conv"][l], w=["ctok"], key="cst")
        for b0 in range(0, NFC, 4):
            nb = min(4, NFC - b0)
            ps, pk = self.pf()
            for q in range(nb):
                fc = b0 + q
                fw.tr(ps[:, q * 32:(q + 1) * 32], cst[0:32, fc * 128:(fc + 1) * 128], identf[0:32, 0:32], r=["ctok", "identf"], w=[pk])
            fw.act(cxs[:, b0:b0 + nb, :, 0:2], ps[:, 0:nb * 32].rearrange("p (f q j) -> p f q j", f=nb, j=2), AF.Copy, r=[pk], w=["cx"])
        ffn_core(MS, hcur, "hT0", cxs, "cx", True, aTd[0], "aT0")
        sc_ = O["s_conv"][l].rearrange("(q j) f -> j q f", j=2)
        c_token_major(MS, hcur, "hT0", [(32, 48), (48, 64)], [sc_[0], sc_[1]])
        xo_, xok = xo[i % 2], "xo%d" % (i % 2)
        finish(MS, xt, xk, xo_, xok, O["ys"], self.xsbuf, ("xb", NT), aTd[0], "aT0")


NSEG = 1


def _consts_shared():
    c = {}
    c["c_ident"] = np.eye(128, dtype=np.float32)
    inv = (10000.0 ** (-np.arange(0, HD, 2, dtype=np.float32) / HD)).astype(np.float32)
    pos_s = (PAST + np.repeat(np.arange(4), NS)).astype(np.float32)
    ang_s = pos_s[:, None] * inv[None, :]
    c["c_coss"] = np.cos(ang_s).astype(np.float32)
    c["c_sins"] = np.sin(ang_s).astype(np.float32)
    s = np.arange(128)[:, None]
    t = np.arange(128)[None, :]
    incl = (s <= t).astype(np.float32)
    strict = (s < t).astype(np.float32)
    c["c_tri"] = np.concatenate([incl * CDEC, strict * CDEC], 1).astype(np.float32)
    c["c_mask2"] = np.concatenate([incl, strict], 1).astype(np.float32)
    c["c_maskL"] = (s > t).astype(np.float32)
    i_ = np.arange(128)[:, None]
    j_ = np.arange(128)[None, :]
    cur = np.where(j_ <= i_, 0.0, NEG)
    prev = np.where(j_ > i_, 0.0, NEG)
    dead = np.full((128, 128), NEG)
    c["c_amask"] = np.concatenate([cur, prev, prev, cur, cur, dead], 1).astype(np.float32)
    c["_am_first"] = np.concatenate([cur, dead], 1).astype(np.float32)
    c["_am_mid"] = np.concatenate([cur, prev], 1).astype(np.float32)
    tt = (np.arange(32) % 4)[:, None]
    ia = np.arange(128)[None, :]
    ma = np.where(ia <= 124 + tt, 0.0, NEG)
    rb = np.arange(4)[None, :]
    mb = np.where(rb > tt, 0.0, NEG)
    c["c_smask"] = np.concatenate([ma, mb], 1).astype(np.float32)
    last = np.zeros((128, 1), np.float32)
    last[127, 0] = 1.0
    c["c_last"] = last
    c["_inv"] = inv
    return c


def _rope_tab(pos, inv):
    ang = pos.astype(np.float32)[:, None] * inv[None, :]
    return np.cos(ang).astype(np.float32), np.sin(ang).astype(np.float32)


_CACHE = {}
TAPS = False
TAP_OUT = {}


def kernel(**inp):
    inp = {k: np.asarray(v) for k, v in inp.items()}
    xp_all = inp["x_prompt"].astype(np.float32)
    B, SEQ_, _ = xp_all.shape
    TPC = SEQ_ // NSEG
    if TPC not in _CACHE:
        b_ = Builder(TPC, taps=TAPS)
        _CACHE[TPC] = (b_.build(), b_.tapnames)
    nc, tapnames = _CACHE[TPC]
    consts = _consts_shared()
    inv = consts.pop("_inv")
    am_first, am_mid = consts.pop("_am_first"), consts.pop("_am_mid")
    wnames = ["norm_mix_g", "w_in", "rwkv_mu", "rwkv_w0", "rwkv_w2", "rwkv_a0", "rwkv_a2", "rwkv_g2", "rwkv_k_k",
              "rwkv_k_a", "rwkv_ln_g", "rwkv_ln_b", "attn_sinks", "w_br_rwkv", "w_br_attn", "w_out", "norm_ffn_g",
              "ffn_w_in", "ffn_conv_w", "ffn_conv_b", "ffn_w_down", "norm_final_g"]
    shared = {n: np.ascontiguousarray(inp[n], dtype=np.float32) for n in wnames}
    shared["rwkv_r_k"] = np.ascontiguousarray(inp["rwkv_r_k"], dtype=np.float32).reshape(2, RD)
    shared.update(consts)
    in_maps = []
    ncores = 8
    for c in range(ncores):
        b, seg = (c // NSEG) % B, c % NSEG
        sl = slice(c * NS, (c + 1) * NS)
        m = dict(shared)
        t0 = seg * TPC
        m["xp"] = np.ascontiguousarray(xp_all[b, t0:t0 + TPC])
        m["xh0"] = np.ascontiguousarray(xp_all[b, t0 - 128:t0]) if seg > 0 else np.zeros((128, D), np.float32)
        m["c_cosp"], m["c_sinp"] = _rope_tab(t0 + np.arange(TPC), inv)
        m["c_cosh"], m["c_sinh"] = _rope_tab(np.maximum(t0 - 128 + np.arange(128), 0), inv)
        m["c_amask0"] = am_mid if seg > 0 else am_first
        sel = np.zeros((128, 8), np.float32)
        if seg > 0:
            sel[:, c - 1] = 1.0
        m["c_sel"] = sel
        m["xs"] = np.ascontiguousarray(inp["x_sample"][sl].transpose(1, 0, 2).reshape(MS, D))
        m["st_shift"] = np.ascontiguousarray(inp["state_rwkv_shift"][:, sl])
        m["st_wkv"] = np.ascontiguousarray(inp["state_rwkv_wkv"][:, sl]).reshape(2, 128, 4096)
        m["ck"] = np.ascontiguousarray(inp["cache_swa_k"][:, sl]).reshape(2, NS, 128, 128)
        m["cv"] = np.ascontiguousarray(inp["cache_swa_v"][:, sl]).reshape(2, NS, 128, 128)
        m["st_conv"] = np.ascontiguousarray(inp["state_ffn_conv"][:, sl]).reshape(2, 2 * NS, DFF)
        in_maps.append(m)
    res = run_bass_kernel_spmd(nc, in_maps, core_ids=list(range(ncores)))
    R = res.results
    for tn in tapnames:
        TAP_OUT[tn] = [np.asarray(R[c][tn]) for c in range(ncores)]
    f = np.float32
    lastc = [b * NSEG + NSEG - 1 for b in range(B)]
    y_prompt = np.stack([np.concatenate([R[b * NSEG + sg]["yp"] for sg in range(NSEG)], 0) for b in range(B)]).astype(f)
    y_sample = np.concatenate([R[c]["ys"].reshape(4, NS, D).transpose(1, 0, 2) for c in range(ncores)], 0).astype(f)
    p_shift = np.stack([R[c]["p_shift"] for c in lastc], 1).astype(f)
    p_wkv = np.stack([R[c]["p_wkv"] for c in lastc], 1).astype(f)
    p_k = np.stack([R[c]["p_k"] for c in lastc], 1).reshape(2, B, 128, 2, 64).astype(f)
    p_v = np.stack([R[c]["p_v"] for c in lastc], 1).reshape(2, B, 128, 2, 64).astype(f)
    p_conv = np.stack([R[c]["p_conv"] for c in lastc], 1).astype(f)
    s_shift = np.concatenate([R[c]["s_shift"] for c in range(ncores)], 1).astype(f)
    s_wkv = np.concatenate([R[c]["s_wkv"].reshape(2, NS, NH, 64, 64) for c in range(ncores)], 1).astype(f)
    s_k = np.concatenate([R[c]["s_k"].reshape(2, NS, 128, 2, 64) for c in range(ncores)], 1).astype(f)
    s_v = np.concatenate([R[c]["s_v"].reshape(2, NS, 128, 2, 64) for c in range(ncores)], 1).astype(f)
    s_conv = np.concatenate([R[c]["s_conv"].reshape(2, NS, 2, DFF) for c in range(ncores)], 1).astype(f)
    return (y_prompt, y_sample, p_shift, p_wkv, p_k, p_v, p_conv, s_shift, s_wkv, s_k, s_v, s_conv)
```

```python
import math
from contextlib import ExitStack

import numpy as np
import concourse.bass as bass
import concourse.mybir as mybir
from concourse.bass_utils import run_bass_kernel_spmd

F32 = mybir.dt.float32
BF = mybir.dt.bfloat16
AF = mybir.ActivationFunctionType
ALU = mybir.AluOpType
AX = mybir.AxisListType

ENGS = ["sp", "pe", "act", "dve", "pool"]
DEBUG_WHERE = True

D = 1024
HD = 64
NH = 8
RD = 512
RP = 1792
INP = 4608
DFF = 2816
NFC = 22
NS = 16
MS = 64
PAST = 16384
CDEC = -math.exp(-0.5)
NEG = -30000.0


class FW:
    def __init__(self, nc, es):
        self.nc = nc
        self.es = es
        self.ops = {e: [] for e in ENGS}
        self.lastw = {}
        self.readers = {}
        self.dma_count = {}
        self.inc = {}

    def sb(self, name, shape, dt=F32):
        return self.es.enter_context(self.nc.sbuf_tensor(name, list(shape), dt))

    def ps(self, name, shape, dt=F32):
        return self.es.enter_context(self.nc.psum_tensor(name, list(shape), dt))

    def capture(self, f):
        self.cap = []
        f()
        log, self.cap = self.cap, None
        return log

    def replay(self, logs, chunk=2):
        logs = [list(lg) for lg in logs if lg]
        if not logs:
            return
        mn = min(len(lg) for lg in logs)
        per = [max(1, int(round(chunk * len(lg) / mn))) for lg in logs]
        pos = [0] * len(logs)
        while any(p < len(lg) for p, lg in zip(pos, logs)):
            for k, lg in enumerate(logs):
                for _ in range(per[k]):
                    if pos[k] < len(lg):
                        self.op(*lg[pos[k]])
                        pos[k] += 1

    def op(self, eng, fn, r=(), w=(), dma=None):
        if getattr(self, "cap", None) is not None:
            self.cap.append((eng, fn, tuple(r), tuple(w), dma))
            return
        ops = self.ops[eng]
        idx = len(ops)
        deps = set()
        pr = [k for k in r if isinstance(k, str) and k[:2] in ("ps", "pb") and k[2:].isdigit()]
        if pr:
            r = [k for k in r if k not in pr]
            w = list(w) + pr
        for k in r:
            t = self.lastw.get(k)
            if t is not None:
                deps.add(t)
        for k in w:
            t = self.lastw.get(k)
            if t is not None:
                deps.add(t)
            for t2 in self.readers.get(k, {}).values():
                deps.add(t2)
        if dma is not None:
            c = self.dma_count.get(dma, 0) + 1
            self.dma_count[dma] = c
            tok = ("d", dma, c)
        else:
            tok = ("c", eng, idx)
        if eng == "pe":
            deps = {d for d in deps if not (d[0] == "c" and d[1] == "pe")}
        deps.discard(tok)
        rec = dict(fn=fn, deps=deps, tok=tok, signal=False)
        if DEBUG_WHERE:
            import sys as _s
            f_ = _s._getframe(1)
            wh = []
            while f_ is not None and len(wh) < 4:
                wh.append(f_.f_lineno)
                f_ = f_.f_back
            rec["where"] = wh
        ops.append(rec)
        for d in deps:
            if d[0] == "c":
                self.ops[d[1]][d[2]]["signal"] = True
        for k in w:
            self.lastw[k] = tok
            self.readers[k] = {}
        for k in r:
            rk = ("d", tok[1]) if tok[0] == "d" else tok[1]
            self.readers.setdefault(k, {})[rk] = tok
        return tok

    def fence(self):
        toks = set()
        for e in ENGS:
            for rec in reversed(self.ops[e]):
                if rec["tok"][0] == "c" and rec["fn"] is not None:
                    toks.add(rec["tok"])
                    rec["signal"] = True
                    break
        for k, c in self.dma_count.items():
            toks.add(("d", k, c))
        for e in ENGS:
            self.ops[e].append(dict(fn=None, deps=set(toks), tok=("c", e, len(self.ops[e])), signal=False))

    def dma(self, out, in_, r=(), w=(), key=None, eng="sp", **kw):
        self.op(eng, lambda e: e.dma_start(out=out, in_=in_, **kw), r=r, w=w, dma=key)

    def mm(self, out, lhsT, rhs, start, stop, r=(), w=()):
        self.op("pe", lambda e: e.matmul(out, lhsT, rhs, start=start, stop=stop), r=r, w=w)

    def tr(self, out, in_, ident, r=(), w=()):
        self.op("pe", lambda e: e.transpose(out, in_, ident), r=r, w=w)

    def act(self, out, in_, func, r=(), w=(), **kw):
        self.op("act", lambda e: e.activation(out, in_, func, **kw), r=r, w=w)

    def emit(self):
        nc = self.nc
        sems = {e: self.es.enter_context(nc.semaphore("s_" + e)) for e in ENGS}
        dsems = {}
        for i, k in enumerate(self.dma_count):
            dsems[k] = self.es.enter_context(nc.semaphore("d%d" % i))
        for e in ENGS:
            c = 0
            for rec in self.ops[e]:
                if rec["signal"] and rec["tok"][0] == "c":
                    c += 1
                rec["sigval"] = c
        final_counts = dict(self.dma_count)

        def run(engname, eng):
            waited = {}
            for rec in self.ops[engname]:
                need = {}
                for d in rec["deps"]:
                    if d[0] == "c":
                        s = ("c", d[1])
                        v = self.ops[d[1]][d[2]]["sigval"]
                    else:
                        s = ("d", d[1])
                        v = self.inc.get(d[1], 16) * d[2]
                    if need.get(s, 0) < v:
                        need[s] = v
                for s, v in need.items():
                    if waited.get(s, 0) >= v:
                        continue
                    waited[s] = v
                    eng.wait_ge(sems[s[1]] if s[0] == "c" else dsems[s[1]], v)
                if rec["fn"] is None:
                    continue
                try:
                    ins = rec["fn"](eng)
                except Exception:
                    print("EMIT FAILURE at lines", rec.get("where"), "engine", engname)
                    raise
                if rec["tok"][0] == "d":
                    ins.then_inc(dsems[rec["tok"][1]], self.inc.get(rec["tok"][1], 16))
                elif rec["signal"]:
                    ins.then_inc(sems[engname], 1)
            if engname == "sp":
                for k, c in final_counts.items():
                    v = self.inc.get(k, 16) * c
                    if waited.get(("d", k), 0) < v:
                        eng.wait_ge(dsems[k], v)

        with nc.Block() as block:
            @block.sync
            def _(e):
                run("sp", e)

            @block.tensor
            def _(e):
                run("pe", e)

            @block.scalar
            def _(e):
                run("act", e)

            @block.vector
            def _(e):
                run("dve", e)

            @block.gpsimd
            def _(e):
                run("pool", e)


def bc3(ap2, n):
    s = list(ap2.shape)
    return ap2.unsqueeze(2).to_broadcast([s[0], s[1], n])


def h3(ap2, h=NH):
    return ap2.rearrange("p (h d) -> p h d", h=h)


class Builder:
    def __init__(self, TP, taps=False):
        self.TP = TP
        self.NT = TP // 128
        self.taps = taps
        self.nc = bass.Bass("TRN2", target_bir_lowering=False)
        self.I = {}
        self.O = {}
        self.psi = 0
        self.pbi = 0
        self.tapnames = []
        self.pool = None
        self.pcnt = {}

    def din(self, n, s):
        self.I[n] = self.nc.dram_tensor(n, list(s), F32, kind="ExternalInput").ap()

    def dout(self, n, s):
        self.O[n] = self.nc.dram_tensor(n, list(s), F32, kind="ExternalOutput").ap()

    def declare(self):
        TP = self.TP
        for n, s in [("xp", (TP, D)), ("xs", (MS, D)), ("st_shift", (2, NS, RP)), ("st_wkv", (2, 128, 4096)),
                     ("ck", (2, NS, 128, 128)), ("cv", (2, NS, 128, 128)), ("st_conv", (2, 2 * NS, DFF)),
                     ("norm_mix_g", (2, D)), ("w_in", (2, D, INP)), ("rwkv_mu", (2, RP)), ("rwkv_w0", (2, RD)),
                     ("rwkv_w2", (2, 64, RD)), ("rwkv_a0", (2, RD)), ("rwkv_a2", (2, 64, RD)),
                     ("rwkv_g2", (2, 128, RD)), ("rwkv_k_k", (2, RD)), ("rwkv_k_a", (2, RD)),
                     ("rwkv_r_k", (2, RD)), ("rwkv_ln_g", (2, RD)), ("rwkv_ln_b", (2, RD)),
                     ("attn_sinks", (2, NH)), ("w_br_rwkv", (2, RD, D)), ("w_br_attn", (2, RD, D)),
                     ("w_out", (2, D, D)), ("norm_ffn_g", (2, D)), ("ffn_w_in", (2, D, 2 * DFF)),
                     ("ffn_conv_w", (2, 3, DFF)), ("ffn_conv_b", (2, DFF)), ("ffn_w_down", (2, DFF, D)),
                     ("norm_final_g", (D,)),
                     ("c_ident", (128, 128)), ("c_cosp", (TP, 32)), ("c_sinp", (TP, 32)),
                     ("c_coss", (MS, 32)), ("c_sins", (MS, 32)), ("c_tri", (128, 256)),
                     ("c_mask2", (128, 256)), ("c_maskL", (128, 128)), ("c_amask", (128, 768)),
                     ("c_smask", (32, 132)), ("c_last", (128, 1)),
                     ("xh0", (128, D)), ("c_cosh", (128, 32)), ("c_sinh", (128, 32)), ("c_amask0", (128, 256)), ("c_sel", (128, 8))]:
            self.din(n, s)
        for n, s in [("yp", (TP, D)), ("ys", (MS, D)), ("p_shift", (2, RP)), ("p_wkv", (2, NH, 64, 64)),
                     ("p_k", (2, 128, 128)), ("p_v", (2, 128, 128)), ("p_conv", (2, 2, DFF)),
                     ("s_shift", (2, NS, RP)), ("s_wkv", (2, 128, 4096)), ("s_k", (2, NS, 128, 128)),
                     ("s_v", (2, NS, 128, 128)), ("s_conv", (2, 2 * NS, DFF))]:
            self.dout(n, s)
        nc = self.nc
        self.xbuf = nc.dram_tensor("xbuf", [TP, D], F32).ap()
        self.xsbuf = nc.dram_tensor("xsbuf", [MS, D], F32).ap()
        self.mrbuf = nc.dram_tensor("mrbuf", [self.NT + 1, 128, 1024], BF).ap()
        self.xh_dram = nc.dram_tensor("xh_dram", [128, D], F32).ap()
        self.sq = nc.dram_tensor("sq", [6, MS, RD], F32).ap()
        self.sy = nc.dram_tensor("sy", [MS, RD], F32).ap()

    def alloc(self, name, shape, dt=F32):
        shape = list(shape)
        n = 1
        for d_ in shape[1:]:
            n *= d_
        nbytes = n * (4 if dt == F32 else 2)
        nw = (nbytes + 31) // 32 * 8
        off = self.aoff
        self.aoff += nw
        self.apeak = max(self.apeak, self.aoff)
        assert self.aoff <= self.ASZ, "SBUF arena overflow: %s needs %d words (limit %d)" % (name, self.aoff, self.ASZ)
        ap = self.arena[0:shape[0], off:off + nw]
        if dt != F32:
            ap = ap.bitcast(dt)
        ap = ap[:, 0:n]
        if len(shape) > 2:
            names = ["d%d" % i for i in range(len(shape) - 1)]
            pat = "p (%s) -> p %s" % (" ".join(names), " ".join(names))
            ap = ap.rearrange(pat, **{names[i]: shape[i + 1] for i in range(len(names))})
        return ap

    def release(self, mark):
        self.fw.fence()
        self.aoff = mark

    def pf(self):
        ids = {None: [0, 1, 2, 3, 4, 5], 0: [0, 1, 2], 1: [3, 4, 5]}[self.pool]
        c = self.pcnt.setdefault(("f", self.pool), 0)
        self.pcnt[("f", self.pool)] = c + 1
        k = ids[c % len(ids)]
        return self.PS[k], "ps%d" % k

    def pb(self):
        ids = {None: [0, 1], 0: [0], 1: [1]}[self.pool]
        c = self.pcnt.setdefault(("b", self.pool), 0)
        self.pcnt[("b", self.pool)] = c + 1
        k = ids[c % len(ids)]
        return self.PBK[k], "pb%d" % k

    def tap(self, name, ap, rkeys, dt=F32):
        if not self.taps:
            return
        shp = list(ap.shape)
        t = self.nc.dram_tensor("tap_" + name, shp, dt, kind="ExternalOutput").ap()
        self.tapnames.append("tap_" + name)
        self.fw.dma(t, ap, r=rkeys, key="tap_" + name)

    def V(self, fn, r=(), w=()):
        self.fw.op("dve", fn, r, w)

    def P(self, fn, r=(), w=()):
        self.fw.op("pool", fn, r, w)

    def col_load(self, dst, dkey, vec, n):
        fw = self.fw
        st = self.cstage
        fw.dma(st[0:n, :], vec.rearrange("(c p) -> c p", p=128), w=["cstage"], key="cstage")
        ps, pk = self.pf()
        fw.tr(ps[:, 0:n], st[0:n, :], self.identf[0:n, 0:n], r=["cstage", "identf"], w=[pk])
        fw.act(dst, ps[:, 0:n], AF.Copy, r=[pk], w=[dkey])

    def gather_select(self, src_ap, src_keys, n, ag_in, ag_out, name):
        fw = self.fw
        fw.dma(ag_in, src_ap, r=src_keys, w=[name + "_in"], key=name + "_st")
        self.gi = getattr(self, "gi", 0)
        ck = name + "_cc"
        fw.inc[ck] = 1
        fw.op("pool", lambda e: e.collective_compute("AllGather", ALU.bypass, replica_groups=[list(range(8))], ins=[ag_in], outs=[ag_out]),
              r=[name + "_in"], w=[name + "_out"], dma=ck)
        for r_ in range(8):
            st, sk = self.xt[r_ % 2], "xt%d" % (r_ % 2)
            fw.dma(st[:, 0:n], ag_out[r_ * 128:(r_ + 1) * 128, :], r=[name + "_out"], w=[sk], key=sk)
            if r_ == 0:
                self.V(lambda e, st=st: e.tensor_scalar(src_ap, st[:, 0:n], self.sel[:, 0:1], None, ALU.mult), r=[sk, "sel"], w=src_keys)
            else:
                self.V(lambda e, st=st, r_=r_: e.scalar_tensor_tensor(src_ap, st[:, 0:n], self.sel[:, r_:r_ + 1], src_ap, ALU.mult, ALU.add),
                       r=[sk, "sel"] + list(src_keys), w=src_keys)

    def bcast_load(self, dst, dkey, vec):
        self.fw.dma(dst, vec.partition_broadcast(dst.shape[0]), w=[dkey], key=dkey)

    def prep_w(self, nchunks, ncols, src, dst, dkey, mode, scale=None, mul=None, mulkey=None, sview=None):
        fw = self.fw
        for c in range(nchunks):
            for s0 in range(0, ncols, 2048):
                n = min(2048, ncols - s0)
                k = self.wst_i % 4
                self.wst_i += 1
                st = self.wstage[k]
                sk = "wst%d" % k
                fw.dma(st[:, 0:n], src(c, s0, n), w=[sk], key=sk)
                o = dst(c, s0, n)
                dk = dkey(c)
                if sview is not None:
                    sv_ = sview(st[:, 0:n])
                    sc = scale(c)
                    self.V(lambda eg, o=o, sv_=sv_, sc=sc: eg.tensor_scalar(o, sv_, sc, None, ALU.mult), r=[sk, "gcol"], w=[dk])
                    continue
                if mode == "plain":
                    e = ["dve", "pool", "act"][self.wst_i % 3]
                    if e == "act":
                        fw.act(o, st[:, 0:n], AF.Copy, r=[sk], w=[dk])
                    else:
                        fw.op(e, lambda eg, o=o, st=st, n=n: eg.tensor_copy(o, st[:, 0:n]), r=[sk], w=[dk])
                elif mode == "col":
                    sc = scale(c)
                    e = ["dve", "pool"][self.wst_i % 2]
                    fw.op(e, lambda eg, o=o, st=st, n=n, sc=sc: eg.tensor_scalar(o, st[:, 0:n], sc, None, ALU.mult),
                          r=[sk, "gcol"], w=[dk])
                else:
                    sc = scale(c)
                    m = mul(s0, n)
                    self.V(lambda eg, o=o, st=st, n=n, sc=sc, m=m: eg.scalar_tensor_tensor(
                        o, st[:, 0:n], sc, m, ALU.mult, ALU.mult), r=[sk, "gcol", mulkey], w=[dk])

    def norm_hT(self, xt, xk, M, hdst, hkey, identb):
        self.norm_a(xt, xk, M)
        self.norm_b(M, hdst, hkey, identb)

    def norm_a(self, xt, xk, M):
        fw = self.fw
        xn, ss, t1 = self.xn, self.ss, self.t1
        fw.act(xn[0:M, :], xt[0:M, :], AF.Square, r=[xk], w=["xn", "ss"], accum_out=ss[0:M, :])
        self.V(lambda e: e.tensor_scalar(t1[0:M, :], ss[0:M, :], 1.0 / D, 1e-6, ALU.mult, ALU.add), r=["ss"], w=["t1"])
        fw.act(t1[0:M, :], t1[0:M, :], AF.Sqrt, r=["t1"], w=["t1"])
        self.V(lambda e: e.reciprocal(t1[0:M, :], t1[0:M, :]), r=["t1"], w=["t1"])
        self.V(lambda e: e.tensor_scalar(xn[0:M, :], xt[0:M, :], t1[0:M, 0:1], None, ALU.mult), r=[xk, "t1"], w=["xn"])

    def norm_b(self, M, hdst, hkey, identb):
        fw = self.fw
        xn = self.xn
        pbk, pk = self.pb()
        for c in range(8):
            fw.tr(pbk[:, c * M:(c + 1) * M], xn[0:M, c * 128:(c + 1) * 128], identb[0:M, 0:M], r=["xn", "identb"], w=[pk])
        fw.act(hdst, pbk[:, 0:8 * M].rearrange("p (c t) -> p c t", c=8), AF.Copy, r=[pk], w=[hkey])

    def build(self):
        self.declare()
        nc = self.nc
        with ExitStack() as es:
            self.fw = fw = FW(nc, es)
            self.PS = [fw.ps("ps%d" % i, [128, 512], F32) for i in range(6)]
            self.PBK = [fw.ps("pb%d" % i, [128, 1024], BF) for i in range(2)]
            self.ASZ = 52224
            self.arena = fw.sb("arena", [128, self.ASZ])
            self.aoff = 0
            self.apeak = 0
            self.identf = self.alloc("identf", [128, 128])
            self.identb = self.alloc("identb", [128, 128], BF)
            self.cstage = self.alloc("cstage", [32, 128])
            self.wst_i = 0
            self.xn = self.alloc("xn", [128, D], BF)
            self.ss = self.alloc("ss", [128, 1])
            self.t1 = self.alloc("t1", [128, 1])
            self.gcol = self.alloc("gcol", [128, 8])
            self.xt = [self.alloc("xt%d" % i, [128, D]) for i in range(2)]
            self.sel = self.alloc("sel", [128, 8])
            fw.dma(self.sel[:], self.I["c_sel"], w=["sel"], key="sel")
            fw.dma(self.identf[:], self.I["c_ident"], w=["identf"], key="identf")
            self.V(lambda e: e.tensor_copy(self.identb[:], self.identf[:]), r=["identf"], w=["identb"])
            for l in range(2):
                for p_ in (self.pass_rwkv, self.pass_attn, self.pass_ffn):
                    mk_ = self.aoff
                    p_(l, None)
                    self.release(mk_)
            print("arena peak words", self.apeak, "of", self.ASZ)
            fw.emit()
        return nc

    def sbl(self, es2, name, shape, dt=F32):
        return self.alloc(name, shape, dt)

    def xsrc(self, l, i):
        if i < self.NT:
            src = self.I["xp"] if l == 0 else self.xbuf
            return src[i * 128:(i + 1) * 128, :], ("xb", i)
        src = self.I["xs"] if l == 0 else self.xsbuf
        return src, ("xb", i)

    def pass_rwkv(self, l, es2):
        fw, I, O, NT = self.fw, self.I, self.O, self.NT
        sbl = lambda n, s, dt=F32: self.sbl(es2, "r%d_" % l + n, s, dt)
        identb, identf = self.identb, self.identf
        W1 = sbl("W1", [128, 8, RP], BF)
        W2 = sbl("W2", [128, 8, RP], BF)
        Wg = sbl("Wg", [128, 8, D], BF)
        Wr = sbl("Wr", [128, 4, D], BF)
        lw2 = sbl("lw2", [128, RD], BF)
        lg2 = sbl("lg2", [128, RD], BF)
        bcs = {}
        for n in ["rwkv_w0", "rwkv_a0", "rwkv_k_k", "rwkv_k_a", "rwkv_r_k", "rwkv_ln_g", "rwkv_ln_b"]:
            bcs[n] = sbl(n, [128, RD])
            self.bcast_load(bcs[n][:], n + "_bc", I[n][l])
        mucol = sbl("mucol", [128, 2])
        tri = sbl("tri", [128, 256])
        mask2 = sbl("mask2", [128, 256])
        maskL = sbl("maskL", [128, 128])
        clast = sbl("clast", [128, 1])
        fw.dma(tri[:], I["c_tri"], w=["tri"], key="tri")
        fw.dma(mask2[:], I["c_mask2"], w=["mask2"], key="mask2")
        fw.dma(maskL[:], I["c_maskL"], w=["maskL"], key="maskL")
        fw.dma(clast[:], I["c_last"], w=["clast"], key="clast")
        self.col_load(self.gcol[:], "gcol", I["norm_mix_g"][l], 8)
        self.col_load(mucol[:], "mucol", I["rwkv_mu"][l, 1536:1792], 2)
        m0 = self.aoff
        self.wstage = [sbl("wst%d" % i_, [128, 2048]) for i_ in range(4)]
        mu_bc = sbl("mu_bc", [128, RP])
        omm_bc = sbl("omm_bc", [128, RP])
        self.bcast_load(mu_bc[:], "mu_bc", I["rwkv_mu"][l])
        self.V(lambda e: e.tensor_scalar(omm_bc[:], mu_bc[:], -1.0, 1.0, ALU.mult, ALU.add), r=["mu_bc"], w=["omm_bc"])
        win = I["w_in"][l]
        gsc = lambda c: self.gcol[:, c:c + 1]
        self.prep_w(8, RP, lambda c, s0, n: win[c * 128:(c + 1) * 128, s0:s0 + n],
                    lambda c, s0, n: W1[:, c, s0:s0 + n], lambda c: "W1_%d" % c, "colmul", gsc,
                    lambda s0, n: omm_bc[:, s0:s0 + n], "omm_bc")
        self.prep_w(8, RP, lambda c, s0, n: win[c * 128:(c + 1) * 128, s0:s0 + n],
                    lambda c, s0, n: W2[:, c, s0:s0 + n], lambda c: "W2_%d" % c, "colmul", gsc,
                    lambda s0, n: mu_bc[:, s0:s0 + n], "mu_bc")
        self.prep_w(8, D, lambda c, s0, n: win[c * 128:(c + 1) * 128, 2560 + s0:2560 + s0 + n],
                    lambda c, s0, n: Wg[:, c, s0:s0 + n], lambda c: "Wg_%d" % c, "col", gsc)
        wbr = I["w_br_rwkv"][l]
        self.prep_w(4, D, lambda c, s0, n: wbr[c * 128:(c + 1) * 128, s0:s0 + n],
                    lambda c, s0, n: Wr[:, c, s0:s0 + n], lambda c: "Wr_%d" % c, "plain")
        for (nm, p0, dk_) in [("rwkv_w2", 0, "lw2a"), ("rwkv_a2", 64, "lw2b")]:
            k = self.wst_i % 4
            self.wst_i += 1
            wsk = self.wstage[k]
            fw.dma(wsk[p0:p0 + 64, 0:RD], I[nm][l], w=["wst%d" % k], key="wst%d" % k)
            self.P(lambda e, wsk=wsk, p0=p0: e.tensor_copy(lw2[p0:p0 + 64, :], wsk[p0:p0 + 64, 0:RD]), r=["wst%d" % k], w=[dk_])
        self.prep_w(1, RD, lambda c, s0, n: I["rwkv_g2"][l], lambda c, s0, n: lg2[:, :], lambda c: "lg2", "plain")
        WK1 = ["W1_%d" % c for c in range(8)]
        WK2 = ["W2_%d" % c for c in range(8)]
        self.release(m0)
        class NSP:
            pass
        zr, zk = sbl("zr", [128, RD]), sbl("zk", [128, RD])
        lact = sbl("lact", [128, 128], BF)
        T = [sbl("tmp%d" % i_, [128, RD]) for i_ in range(8)]
        sm = sbl("sm", [128, 64])
        orT = sbl("orT", [128, 4, 128], BF)
        sgr = sbl("sgr", [128, 8, 128], BF)
        mrT0_ = sbl("mrT0", [128, 8, 128], BF)
        mrT = [mrT0_, mrT0_]
        TP_ = [sbl("tpost%d" % i_, [128, RD]) for i_ in range(2)]
        m1 = self.aoff
        NRB = 9864

        def mkrec(k):
            R = NSP()
            rb = sbl("RB%d" % k, [128, NRB], BF)
            rf = sbl("RF%d" % k, [128, 528])
            R.rb, R.rf, R.k = rb, rf, k
            R.RKT = rb[:, 0:1024].rearrange("p (j a t) -> p j a t", j=4, a=2)
            R.G4 = [rb[:, 1024 + j * 1280:1024 + (j + 1) * 1280].rearrange("p (h c) -> p h c", h=2) for j in range(4)]
            R.ZF = [rb[:, 6144 + j * 256:6144 + (j + 1) * 256].rearrange("p (h c) -> p h c", h=2) for j in range(4)]
            R.vb, R.ktt, R.bnt = rb[:, 7168:7680], rb[:, 7680:8192], rb[:, 8192:8704]
            R.sgT = rb[:, 8704:8832]
            R.hT = rb[:, 8832:9864].rearrange("p (c t) -> p c t", c=8)
            R.zv, R.WC, R.bon = rf[:, 0:512], rf[:, 512:516], rf[:, 516:524]
            R.K = (lambda k_: (lambda n: "%s#%d" % (n, k_)))(k)
            return R
        R0 = mkrec(0)
        U0b = [sbl("U0b%d" % j, [128, 2, 64], BF) for j in range(4)]
        Ub = sbl("Ub", [128, RD], BF)
        Nst = sbl("Nst", [128, 4, 128])
        Nb = sbl("Nb", [128, 4, 128], BF)
        self.V(lambda e: e.memset(Nst[:], 0.0), w=["Nst"])
        self.V(lambda e: e.memset(Nb[:], 0.0), w=["Nb"])
        m2 = self.aoff
        rt, kat = sbl("rt", [128, RD], BF), sbl("kat", [128, RD], BF)
        KT = sbl("KT", [128, 4, 128], BF)
        BT = sbl("BT", [128, 4, 128], BF)
        for j in range(4):
            self.P(lambda e, j=j: e.tensor_copy(R0.G4[j][:, :, 512:640], identb[:, :].unsqueeze(1).to_broadcast([128, 2, 128])),
                   r=["identb"], w=["G4_%d" % j])
        EZ = [[sbl("EZ%d_%d" % (j, a), [128, 2, 2, 128], BF) for a in range(2)] for j in range(4)]
        FFa = [sbl("FFa%d" % a, [128, 4, 2, 128], BF) for a in range(2)]
        FF = [[FFa[a][:, j] for a in range(2)] for j in range(4)]

        def tok_proj(M, hcur, hprev, hk, g0, dstkey):
            ps, pk = self.pf()
            n = 0
            for c in range(8):
                fw.mm(ps[0:M, :], hcur(c), W1[:, c, g0:g0 + 512], n == 0, False, r=[hk, WK1[c]], w=[pk])
                n += 1
            for c in range(8):
                fw.mm(ps[0:M, :], hprev(c), W2[:, c, g0:g0 + 512], False, c == 7, r=[hk, WK2[c]], w=[pk])
            return ps, pk

        def feat_proj(M, hcur, hprev, hk, g0):
            ps, pk = self.pf()
            for c in range(8):
                fw.mm(ps[:, 0:M], W1[:, c, g0:g0 + 128], hcur(c), c == 0, False, r=[hk, WK1[c]], w=[pk])
            for c in range(8):
                fw.mm(ps[:, 0:M], W2[:, c, g0:g0 + 128], hprev(c), False, c == 7, r=[hk, WK2[c]], w=[pk])
            return ps, pk

        def raw_last(hl, hk, M, dst):
            for gi, g0 in enumerate(range(0, RP, 512)):
                n = min(512, RP - g0)
                ps, pk = self.pf()
                for c in range(8):
                    fw.mm(ps[0:M, 0:n], hl(c), W1[:, c, g0:g0 + n], c == 0, False, r=[hk, WK1[c]], w=[pk])
                for c in range(8):
                    fw.mm(ps[0:M, 0:n], hl(c), W2[:, c, g0:g0 + n], False, c == 7, r=[hk, WK2[c]], w=[pk])
                fw.act(T[gi][0:M, 0:n], ps[0:M, 0:n], AF.Copy, r=[pk], w=["T%d" % gi])
                fw.dma(dst[:, g0:g0 + n], T[gi][0:M, 0:n], r=["T%d" % gi], key="zl%d" % gi)

        def prep(M, sample, R):
            K = R.K
            w0, a0 = bcs["rwkv_w0"], bcs["rwkv_a0"]
            kkb, kab, rkb = bcs["rwkv_k_k"], bcs["rwkv_k_a"], bcs["rwkv_r_k"]
            pw, pwk = self.pf()
            fw.mm(pw[0:M, :], lact[0:64, 0:M], lw2[0:64, :], True, True, r=["lact", "lw2a"], w=[pwk])
            pa, pak = self.pf()
            fw.mm(pa[0:M, :], lact[64:128, 0:M], lw2[64:128, :], True, True, r=["lact", "lw2b"], w=[pak])
            sg, a_, kk, t3, kf, be = T[0], T[1], T[2], T[3], T[4], T[5]
            self.V(lambda e: e.tensor_tensor(sg[0:M, :], pw[0:M, :], w0[0:M, :], ALU.add), r=[pwk, "rwkv_w0_bc"], w=["T0"])
            fw.act(sg[0:M, :], sg[0:M, :], AF.Sigmoid, r=["T0"], w=["T0"])
            self.V(lambda e: e.tensor_tensor(a_[0:M, :], pa[0:M, :], a0[0:M, :], ALU.add), r=[pak, "rwkv_a0_bc"], w=["T1"])
            fw.act(a_[0:M, :], a_[0:M, :], AF.Sigmoid, r=["T1"], w=["T1"])
            self.P(lambda e: e.tensor_tensor(kk[0:M, :], zk[0:M, :], kkb[0:M, :], ALU.mult), r=["zk", "rwkv_k_k_bc"], w=["T2"])
            self.P(lambda e: e.tensor_tensor(t3[0:M, :], kk[0:M, :], kk[0:M, :], ALU.mult), r=["T2"], w=["T3"])
            self.V(lambda e: e.tensor_reduce(sm[0:M, 0:8], h3(t3[0:M, :]), AX.X, ALU.add), r=["T3"], w=["sm0"])
            fw.act(sm[0:M, 0:8], sm[0:M, 0:8], AF.Sqrt, r=["sm0"], w=["sm0"])
            self.V(lambda e: e.tensor_scalar(sm[0:M, 0:8], sm[0:M, 0:8], 1e-12, None, ALU.max), r=["sm0"], w=["sm0"])
            self.V(lambda e: e.reciprocal(sm[0:M, 0:8], sm[0:M, 0:8]), r=["sm0"], w=["sm0"])
            self.V(lambda e: e.tensor_tensor(h3(kk[0:M, :]), h3(kk[0:M, :]), bc3(sm[0:M, 0:8], 64), ALU.mult),
                   r=["T2", "sm0"], w=["T2"])
            self.V(lambda e: e.scalar_tensor_tensor(t3[0:M, :], a_[0:M, :], -1.0, kab[0:M, :], ALU.add, ALU.mult),
                   r=["T1", "rwkv_k_a_bc"], w=["T3"])
            self.V(lambda e: e.scalar_tensor_tensor(kf[0:M, :], t3[0:M, :], 1.0, zk[0:M, :], ALU.add, ALU.mult),
                   r=["T3", "zk"], w=["T4"])
            self.P(lambda e: e.tensor_tensor(be[0:M, :], kk[0:M, :], a_[0:M, :], ALU.mult), r=["T2", "T1"], w=["T5"])
            self.P(lambda e: e.tensor_tensor(t3[0:M, :], zr[0:M, :], kf[0:M, :], ALU.mult), r=["zr", "T4"], w=["T3"])
            self.P(lambda e: e.tensor_tensor(t3[0:M, :], t3[0:M, :], rkb[0:M, :], ALU.mult), r=["T3", "rwkv_r_k_bc"], w=["T3"])
            self.V(lambda e, R=R: e.tensor_reduce(R.bon[0:M, :], h3(t3[0:M, :]), AX.X, ALU.add), r=["T3"], w=[K("bon")])
            if sample:
                fw.act(T[6][0:M, :], sg[0:M, :], AF.Exp, r=["T0"], w=["T6"], scale=CDEC)
                for x, (tl, tk) in enumerate([(zr, "zr"), (T[6], "T6"), (kf, "T4"), (R.zv, K("zv")), (kk, "T2"), (be, "T5")]):
                    fw.dma(self.sq[x], tl[0:M, :], r=[tk], w=[("sq", x)], key="sqw%d" % x)
                return
            pli, plik = self.pf()
            fw.mm(pli[:, :], tri[:, 0:128], sg[:, :], True, True, r=["tri", "T0"], w=[plik])
            ple, plek = self.pf()
            fw.mm(ple[:, :], tri[:, 128:256], sg[:, :], True, True, r=["tri", "T0"], w=[plek])
            eL, eLm, enL = T[6], T[7], T[3]
            fw.act(eL[:, :], pli[:, :], AF.Exp, r=[plik], w=["T6"])
            fw.act(eLm[:, :], ple[:, :], AF.Exp, r=[plek], w=["T7"])
            fw.act(enL[:, :], pli[:, :], AF.Exp, r=[plik], w=["T3"], scale=-1.0)
            self.V(lambda e: e.tensor_tensor(rt[:, :], zr[:, :], eL[:, :], ALU.mult), r=["zr", "T6"], w=["rt"])
            self.V(lambda e: e.tensor_tensor(kat[:, :], kk[:, :], eLm[:, :], ALU.mult), r=["T2", "T7"], w=["kat"])
            self.P(lambda e, R=R: e.tensor_tensor(R.ktt[:, :], kf[:, :], enL[:, :], ALU.mult), r=["T4", "T3"], w=[K("ktt")])
            self.V(lambda e, R=R: e.scalar_tensor_tensor(R.bnt[:, :], be[:, :], -1.0, enL[:, :], ALU.mult, ALU.mult),
                   r=["T5", "T3"], w=[K("bnt")])
            fw.act(R.vb[:, :], R.zv[:, :], AF.Copy, r=[K("zv")], w=[K("vb")])
            pwc, pwck = self.pf()
            for j in range(4):
                fw.mm(pwc[:, j:j + 1], eL[:, j * 128:(j + 1) * 128], clast[:, :], True, True, r=["T6", "clast"], w=[pwck])
            fw.act(R.WC[:, :], pwc[:, 0:4], AF.Copy, r=[pwck], w=[K("WC")])
            for (src, skey, dstf, dk) in [(rt, "rt", None, "RKT"), (kat, "kat", None, "RKT"),
                                          (R.ktt, K("ktt"), None, "KT"), (R.bnt, K("bnt"), None, "BT")]:
                pbk, pk = self.pb()
                for j in range(4):
                    fw.tr(pbk[:, j * 128:(j + 1) * 128], src[:, j * 128:(j + 1) * 128], identb[:, :], r=[skey, "identb"], w=[pk])
                if dk == "RKT":
                    which = 0 if skey == "rt" else 1
                    fw.act(R.RKT[:, :, which, :], pbk[:, 0:512].rearrange("p (j t) -> p j t", j=4), AF.Copy, r=[pk], w=["RKT%d" % which])
                else:
                    dst = KT if dk == "KT" else BT
                    self.V(lambda e, dst=dst, pbk=pbk: e.tensor_copy(dst[:, :, :], pbk[:, 0:512].rearrange("p (j t) -> p j t", j=4)),
                           r=[pk], w=[dk])

        def stageAB(R):
            K = R.K
            RK = [K("RKT0"), K("RKT1")]
            RKT, G4, ZF = R.RKT, R.G4, R.ZF
            zb = [self.pf(), self.pf()]
            for j in range(4):
                for hh in range(2):
                    o = hh * 64
                    pZ, pzk = zb[hh]
                    fw.mm(pZ[:, j * 128:(j + 1) * 128], RKT[o:o + 64, j, 1, :], BT[o:o + 64, j, :], True, True, r=["BT", K("RKT1")], w=[pzk])
            mlb = maskL[:, :].unsqueeze(1).to_broadcast([128, 4, 128])
            for hh in range(2):
                pZ, pzk = zb[hh]
                self.V(lambda e, pZ=pZ, hh=hh: e.tensor_tensor(FFa[0][:, :, hh, :], pZ[:, :].rearrange("p (j c) -> p j c", j=4), mlb, ALU.mult),
                       r=[pzk, "maskL"], w=["FF%d_0" % j for j in range(4)])
            for j in range(4):
                bk = [self.pf(), self.pf()]
                for hh in range(2):
                    o = hh * 64
                    ps, pk = bk[hh]
                    rhs = RKT[o:o + 64, j, :, :].rearrange("p a t -> p (a t)")
                    fw.mm(ps[:, 0:256], KT[o:o + 64, j, :], rhs, True, True, r=["KT"] + RK, w=[pk])
                    fw.mm(ps[:, 256:512], BT[o:o + 64, j, :], rhs, True, True, r=["BT"] + RK, w=[pk])
                for hh in range(2):
                    ps, pk = bk[hh]
                    self.V(lambda e, j=j, hh=hh, ps=ps, G4=G4: e.tensor_tensor(
                        G4[j][:, hh, 0:512].rearrange("p (a c) -> p a c", a=2), ps[:, :].rearrange("p (a c) -> p a c", a=2),
                        mask2[:, :].unsqueeze(1).to_broadcast([128, 2, 256]), ALU.mult), r=[pk, "mask2"], w=[K("G4_%d" % j)])
            for lev in range(7):
                a, b = lev % 2, (lev + 1) % 2
                for j in range(4):
                    fk, fn_ = "FF%d_%d" % (j, a), "FF%d_%d" % (j, b)
                    ezn = "EZ%d_%d" % (j, b)
                    if lev == 0:
                        ezk = K("G4_%d" % j)
                        EZs = lambda hh, j=j, G4=G4: G4[j][:, hh, 384:640]
                        Es = lambda hh, j=j, G4=G4: G4[j][:, hh, 384:512]
                        Zs = lambda j=j, G4=G4: G4[j][:, :, 512:640]
                    else:
                        ezk = "EZ%d_%d" % (j, a)
                        EZs = lambda hh, j=j, a=a: EZ[j][a][:, hh, :, :].rearrange("p a t -> p (a t)")
                        Es = lambda hh, j=j, a=a: EZ[j][a][:, hh, 0, :]
                        Zs = lambda j=j, a=a: EZ[j][a][:, :, 1, :]
                    if lev < 6:
                        pL, plk = self.pf()
                        for hh in range(2):
                            fw.mm(pL[:, hh * 256:(hh + 1) * 256], FF[j][a][:, hh, :], EZs(hh), True, True, r=[ezk, fk], w=[plk])
                        pF, pfk = self.pf()
                        for hh in range(2):
                            fw.mm(pF[:, hh * 128:(hh + 1) * 128], Es(hh), FF[j][a][:, hh, :], True, True, r=[ezk, fk], w=[pfk])
                        l3 = pL[:, :].rearrange("p (h c) -> p h c", h=2)
                        fw.act(EZ[j][b][:, :, 0, :], l3[:, :, 0:128], AF.Copy, r=[plk], w=[ezn])
                        self.V(lambda e, j=j, b=b, l3=l3, Zs=Zs: e.tensor_tensor(EZ[j][b][:, :, 1, :], l3[:, :, 128:256], Zs(), ALU.add),
                               r=[plk, ezk], w=[ezn])
                        fw.act(FF[j][b][:, :, :], pF[:, 0:256].rearrange("p (h c) -> p h c", h=2), AF.Copy, r=[pfk], w=[fn_])
                    else:
                        pL, plk = self.pf()
                        for hh in range(2):
                            fw.mm(pL[:, hh * 128:(hh + 1) * 128], FF[j][a][:, hh, :], EZ[j][a][:, hh, 1, :], True, True, r=[ezk, fk], w=[plk])
                        self.V(lambda e, j=j, a=a, pL=pL, ZF=ZF: e.tensor_tensor(ZF[j][:, :, :], pL[:, 0:256].rearrange("p (h c) -> p h c", h=2),
                                                                      EZ[j][a][:, :, 1, :], ALU.add), r=[plk, ezk], w=[K("ZF%d" % j)])

        def stageC(R):
            K = R.K
            RKT, G4, ZF, vb = R.RKT, R.G4, R.ZF, R.vb
            for j in range(4):
                pU, puk = self.pf()
                for hh in range(2):
                    o, h = hh * 64, 2 * j + hh
                    fw.mm(pU[:, hh * 64:(hh + 1) * 64], RKT[o:o + 64, j, 1, :], Nb[o:o + 64, j, o:o + 64], True, False, r=[K("RKT1"), "Nb"], w=[puk])
                    fw.mm(pU[:, hh * 64:(hh + 1) * 64], G4[j][:, hh, 128:256], vb[:, h * 64:(h + 1) * 64], False, True, r=[K("G4_%d" % j), K("vb")], w=[puk])
                fw.act(U0b[j][:, :, :], pU[:, 0:128].rearrange("p (h c) -> p h c", h=2), AF.Copy, r=[puk], w=["U0b%d" % j])
            for j in range(4):
                pU, puk = self.pf()
                for hh in range(2):
                    fw.mm(pU[:, hh * 64:(hh + 1) * 64], ZF[j][:, hh, :], U0b[j][:, hh, :], True, True, r=[K("ZF%d" % j), "U0b%d" % j], w=[puk])
                fw.act(Ub[:, j * 128:(j + 1) * 128], pU[:, 0:128], AF.Copy, r=[puk], w=["Ub%d" % j])

        def stageD(R):
            K = R.K
            RKT, G4, vb = R.RKT, R.G4, R.vb
            psY, pyk = self.pf()
            for j in range(4):
                for hh in range(2):
                    o, h = hh * 64, 2 * j + hh
                    fw.mm(psY[:, h * 64:(h + 1) * 64], RKT[o:o + 64, j, 0, :], Nb[o:o + 64, j, o:o + 64], True, False, r=[K("RKT0"), "Nb"], w=[pyk])
                    fw.mm(psY[:, h * 64:(h + 1) * 64], G4[j][:, hh, 0:128], vb[:, h * 64:(h + 1) * 64], False, False, r=[K("G4_%d" % j), K("vb")], w=[pyk])
                    fw.mm(psY[:, h * 64:(h + 1) * 64], G4[j][:, hh, 256:384], Ub[:, h * 64:(h + 1) * 64], False, True, r=[K("G4_%d" % j), "Ub%d" % j], w=[pyk])
            return psY, pyk

        def n_update(R):
            K = R.K
            ktt, bnt, vb, WC = R.ktt, R.bnt, R.vb, R.WC
            pN, pnk = self.pf()
            for j in range(4):
                fw.mm(pN[:, j * 128:(j + 1) * 128], ktt[:, j * 128:(j + 1) * 128], vb[:, j * 128:(j + 1) * 128], True, False, r=[K("ktt"), K("vb")], w=[pnk])
                fw.mm(pN[:, j * 128:(j + 1) * 128], bnt[:, j * 128:(j + 1) * 128], Ub[:, j * 128:(j + 1) * 128], False, True, r=[K("bnt"), "Ub%d" % j], w=[pnk])
            n2 = Nst[:, :, :].rearrange("p j c -> p (j c)")
            self.V(lambda e: e.tensor_tensor(n2, pN[:, :], n2, ALU.add), r=[pnk, "Nst"], w=["Nst"])
            self.V(lambda e, WC=WC: e.tensor_tensor(Nst[:, :, :], Nst[:, :, :], bc3(WC[:, :], 128), ALU.mult), r=["Nst", K("WC")], w=["Nst"])
            fw.act(Nb[:, :, :], Nst[:, :, :], AF.Copy, r=["Nst"], w=["Nb"])


        def post(M, yap, ykeys, pg, pgk, R):
            K = R.K
            lng, lnb = bcs["rwkv_ln_g"], bcs["rwkv_ln_b"]
            y2, yc = TP_[0], TP_[1]
            ob = TP_[0].bitcast(BF)[:, 0:RD]
            self.V(lambda e: e.tensor_reduce(sm[0:M, 16:24], h3(yap), AX.X, ALU.add), r=ykeys, w=["sm2"])
            fw.act(y2[0:M, :], yap, AF.Square, r=ykeys, w=["TP0"])
            self.V(lambda e: e.tensor_reduce(sm[0:M, 24:32], h3(y2[0:M, :]), AX.X, ALU.add), r=["TP0"], w=["sm3"])
            mean, var = sm[0:M, 16:24], sm[0:M, 24:32]
            self.V(lambda e: e.tensor_scalar(mean, mean, 1.0 / 64, None, ALU.mult), r=["sm2"], w=["sm2"])
            self.V(lambda e: e.tensor_tensor(sm[0:M, 32:40], mean, mean, ALU.mult), r=["sm2"], w=["sm4"])
            self.V(lambda e: e.scalar_tensor_tensor(var, var, 1.0 / 64, sm[0:M, 32:40], ALU.mult, ALU.subtract), r=["sm3", "sm4"], w=["sm3"])
            self.V(lambda e: e.tensor_scalar(var, var, 64e-5, None, ALU.add), r=["sm3"], w=["sm3"])
            fw.act(var, var, AF.Sqrt, r=["sm3"], w=["sm3"])
            self.V(lambda e: e.reciprocal(var, var), r=["sm3"], w=["sm3"])
            self.V(lambda e: e.tensor_tensor(h3(yc[0:M, :]), h3(yap), bc3(mean, 64), ALU.subtract), r=list(ykeys) + ["sm2"], w=["TP1"])
            self.V(lambda e: e.tensor_tensor(h3(yc[0:M, :]), h3(yc[0:M, :]), bc3(var, 64), ALU.mult), r=["TP1", "sm3"], w=["TP1"])
            self.P(lambda e: e.tensor_tensor(yc[0:M, :], yc[0:M, :], lng[0:M, :], ALU.mult), r=["TP1", "rwkv_ln_g_bc"], w=["TP1"])
            self.P(lambda e: e.tensor_tensor(yc[0:M, :], yc[0:M, :], lnb[0:M, :], ALU.add), r=["TP1", "rwkv_ln_b_bc"], w=["TP1"])
            self.P(lambda e, R=R: e.tensor_tensor(h3(y2[0:M, :]), h3(R.zv[0:M, :]), bc3(R.bon[0:M, :], 64), ALU.mult), r=[K("zv"), K("bon")], w=["TP0"])
            self.V(lambda e: e.tensor_tensor(yc[0:M, :], yc[0:M, :], y2[0:M, :], ALU.add), r=["TP1", "TP0"], w=["TP1"])
            self.V(lambda e: e.tensor_tensor(ob[0:M, :], yc[0:M, :], pg[0:M, :], ALU.mult), r=["TP1", pgk], w=["TP0"])
            pbk, pk = self.pb()
            for j in range(4):
                fw.tr(pbk[:, j * M:(j + 1) * M], ob[0:M, j * 128:(j + 1) * 128], identb[0:M, 0:M], r=["TP0", "identb"], w=[pk])
            fw.act(orT[:, :, 0:M], pbk[:, 0:4 * M].rearrange("p (j t) -> p j t", j=4), AF.Copy, r=[pk], w=["orT"])

        def gate_branch(M, hcur, hk, mdst, mkey):
            for half in range(2):
                pg, pgk = self.pf()
                for q in range(4):
                    dc = half * 4 + q
                    for c in range(8):
                        fw.mm(pg[:, q * M:(q + 1) * M], Wg[:, c, dc * 128:(dc + 1) * 128], hcur(c), c == 0, c == 7, r=[hk, "Wg_%d" % c], w=[pgk])
                fw.act(sgr[:, half * 4:(half + 1) * 4, 0:M], pg[:, 0:4 * M].rearrange("p (q t) -> p q t", q=4), AF.Sigmoid, r=[pgk], w=["sgr%d" % half])
                pbr, pbk_ = self.pf()
                for q in range(4):
                    dc = half * 4 + q
                    for j in range(4):
                        fw.mm(pbr[:, q * M:(q + 1) * M], Wr[:, j, dc * 128:(dc + 1) * 128], orT[:, j, 0:M], j == 0, j == 3, r=["orT", "Wr_%d" % j], w=[pbk_])
                self.V(lambda e, half=half, pbr=pbr: e.tensor_tensor(mdst[:, half * 4:(half + 1) * 4, 0:M], sgr[:, half * 4:(half + 1) * 4, 0:M],
                                                                 pbr[:, 0:4 * M].rearrange("p (q t) -> p q t", q=4), ALU.mult),
                       r=["sgr%d" % half, pbk_], w=[mkey])

        R1 = mkrec(1)
        for j in range(4):
            self.P(lambda e, j=j: e.tensor_copy(R1.G4[j][:, :, 512:640], identb[:, :].unsqueeze(1).to_broadcast([128, 2, 128])),
                   r=["identb"], w=[R1.K("G4_%d" % j)])
        RR = [R0, R1]

        def H1a(i):
            R, Rp = RR[i % 2], RR[(i + 1) % 2]
            hT = R.hT
            xt, xk = self.xt[i % 2], "xt%d" % (i % 2)
            src, _ = self.xsrc(l, i)
            fw.dma(xt[:], src, r=[("xb", i)], w=[xk], key=xk)
            hk = R.K("hTr")
            if i == 0:
                self.V(lambda e, hT=hT: e.memset(hT[:, :, 0:1], 0.0), w=[hk])
            else:
                self.P(lambda e, hT=hT, hp=Rp.hT: e.tensor_copy(hT[:, :, 0:1], hp[:, :, 128:129]), r=[Rp.K("hTr")], w=[hk])
            self.norm_a(xt, xk, 128)

        def H1b(i):
            R = RR[i % 2]
            K = R.K
            hT = R.hT
            hk = K("hTr")
            self.norm_b(128, hT[:, :, 1:129], hk, identb)
            hcur = lambda c, hT=hT: hT[:, c, 1:129]
            hprev = lambda c, hT=hT: hT[:, c, 0:128]
            for g0, dst, dk in [(0, zr, "zr"), (512, zk, "zk"), (1024, R.zv, K("zv"))]:
                ps, pk = tok_proj(128, hcur, hprev, hk, g0, dk)
                fw.act(dst[:, :], ps[:, :], AF.Copy, r=[pk], w=[dk])
            ps, pk = feat_proj(128, hcur, hprev, hk, 1536)
            fw.act(lact[0:64, :], ps[0:64, 0:128], AF.Tanh, r=[pk], w=["lact"])
            fw.act(lact[64:128, :], ps[64:128, 0:128], AF.Copy, r=[pk], w=["lact"])
            ps, pk = feat_proj(128, hcur, hprev, hk, 1664)
            fw.act(R.sgT[:, :], ps[:, 0:128], AF.Sigmoid, r=[pk], w=[K("sgT")])
            if i == NT - 1:
                raw_last(lambda c, hT=hT: hT[:, c, 128:129], hk, 1, O["p_shift"][l:l + 1, :])

        def H1c(i):
            prep(128, False, RR[i % 2])

        def H1d(i):
            stageAB(RR[i % 2])

        H2st = {}

        def H2a(i):
            R = RR[i % 2]
            stageC(R)
            psY, pyk = stageD(R)
            n_update(R)
            pg, pgk = self.pf()
            fw.mm(pg[:, :], R.sgT[:, :], lg2[:, :], True, True, r=[R.K("sgT"), "lg2"], w=[pgk])
            H2st[i] = (psY, pyk, pg, pgk)

        def H2b(i):
            psY, pyk, pg, pgk = H2st.pop(i)
            post(128, psY[:, :], [pyk], pg, pgk, RR[i % 2])

        def H2c(i):
            R = RR[i % 2]
            m, mk = mrT[0], "mrT0"
            gate_branch(128, lambda c, R=R: R.hT[:, c, 1:129], R.K("hTr"), m, mk)
            fw.dma(self.mrbuf[i].rearrange("p (c t) -> p c t", c=8), m[:, :, :], r=[mk], w=[("mr", i)], key=mk)

        def cap(pool, f, i):
            self.pool = pool
            return fw.capture(lambda: f(i))

        for f in (H1a, H1b, H1c):
            fw.replay([cap(0, f, 0)])
        fw.replay([cap(None, H1d, 0)])
        for i in range(NT):
            nx = i + 1 < NT
            if nx:
                fw.replay([cap(0, H1a, i + 1)])
            fw.replay([cap(1, H2a, i)])
            fw.replay(([cap(0, H1b, i + 1)] if nx else []) + [cap(1, H2b, i)])
            fw.replay(([cap(0, H1c, i + 1)] if nx else []) + [cap(1, H2c, i)])
            if nx:
                fw.replay([cap(None, H1d, i + 1)])
        self.pool = None
        for j in range(4):
            ps, pk = self.pf()
            fw.tr(ps[:, 0:128], Nst[:, j, :], identf[:, :], r=["Nst", "identf"], w=[pk])
            fw.act(T[0][:, j * 128:(j + 1) * 128], ps[:, 0:128], AF.Copy, r=[pk], w=["T0"])
        for h_ in range(8):
            j, o = h_ // 2, (h_ % 2) * 64
            fw.dma(O["p_wkv"][l, h_], T[0][o:o + 64, j * 128 + o:j * 128 + o + 64], r=["T0"], key="T0")

        self.release(m1)
        RS = NSP()
        RS.zv = sbl("zv_s", [128, RD])
        RS.sgT = sbl("sgT_s", [128, 128], BF)
        RS.bon = sbl("bon_s", [128, 8])
        RS.K = lambda n: n + "#s"
        hTs = sbl("hTs", [128, 8, 80], BF)
        sadd = sbl("sadd", [16, RP])
        stT = sbl("stT", [128, 2, 16])
        zf = sbl("zf", [128, 2, 64])
        QH = sbl("QH", [128, 6, 4, 64])
        Sst = sbl("Sst", [128, 64, 64])
        Stmp = sbl("Stmp", [128, 64, 64])
        sk = sbl("sk", [128, 64])
        yh = sbl("yh", [128, 4, 64])
        ytm = T[7]
        self.V(lambda e: e.memset(hTs[:], 0.0), w=["hTs"])
        i = NT
        xt, xk = self.xt[i % 2], "xt%d" % (i % 2)
        src, _ = self.xsrc(l, i)
        fw.dma(xt[0:MS, :], src, r=[("xb", i)], w=[xk], key=xk)
        self.norm_hT(xt, xk, MS, hTs[:, :, 16:80], "hTs", identb)
        hcur = lambda c: hTs[:, c, 16:80]
        hprev = lambda c: hTs[:, c, 0:64]
        fw.dma(sadd[:, :], I["st_shift"][l], w=["sadd"], key="sadd")
        for q in range(2):
            ps, pk = self.pf()
            fw.tr(ps[:, 0:16], sadd[0:16, 1536 + q * 128:1536 + (q + 1) * 128], identf[0:16, 0:16], r=["sadd", "identf"], w=[pk])
            self.V(lambda e, q=q, ps=ps: e.tensor_scalar(stT[:, q, :], ps[:, 0:16], mucol[:, q:q + 1], None, ALU.mult), r=[pk, "mucol"], w=["stT"])
        for gi, g0 in enumerate(range(0, RP, 512)):
            n = min(512, RP - g0)
            self.bcast_load(T[4 + gi][0:16, 0:n], "T%d" % (4 + gi), I["rwkv_mu"][l, g0:g0 + n])
            self.V(lambda e, gi=gi, g0=g0, n=n: e.tensor_tensor(sadd[:, g0:g0 + n], sadd[:, g0:g0 + n], T[4 + gi][0:16, 0:n], ALU.mult),
                   r=["sadd", "T%d" % (4 + gi)], w=["sadd"])
        zv, sgT = RS.zv, RS.sgT
        for g0, dst, dk in [(0, zr, "zr"), (512, zk, "zk"), (1024, zv, RS.K("zv"))]:
            ps, pk = tok_proj(MS, hcur, hprev, "hTs", g0, dk)
            fw.act(dst[0:MS, :], ps[0:MS, :], AF.Copy, r=[pk], w=[dk])
            self.V(lambda e, dst=dst, g0=g0: e.tensor_tensor(dst[0:16, :], dst[0:16, :], sadd[0:16, g0:g0 + 512], ALU.add), r=[dk, "sadd"], w=[dk])
        for q, g0 in enumerate([1536, 1664]):
            ps, pk = feat_proj(MS, hcur, hprev, "hTs", g0)
            fw.act(zf[:, q, :], ps[:, 0:MS], AF.Copy, r=[pk], w=["zf"])
            self.V(lambda e, q=q: e.tensor_tensor(zf[:, q, 0:16], zf[:, q, 0:16], stT[:, q, :], ALU.add), r=["zf", "stT"], w=["zf"])
        fw.act(lact[0:64, 0:MS], zf[0:64, 0, :], AF.Tanh, r=["zf"], w=["lact"])
        fw.act(lact[64:128, 0:MS], zf[64:128, 0, :], AF.Copy, r=["zf"], w=["lact"])
        fw.act(sgT[:, 0:MS], zf[:, 1, :], AF.Sigmoid, r=["zf"], w=[RS.K("sgT")])
        prep(MS, True, RS)
        if l == 0:
            for nm, ap, k in [("s_zr", zr, "zr"), ("s_zk", zk, "zk"), ("s_zv", zv, "zv"), ("s_dec", T[6], "T6"), ("s_kk", T[2], "T2"),
                              ("s_kf", T[4], "T4"), ("s_be", T[5], "T5"), ("s_a", T[1], "T1")]:
                self.tap(nm, ap[0:MS, :], [k])
        sqv = self.sq.rearrange("x (t q) (h d) -> (q h) x t d", t=4, h=NH)
        for x in range(6):
            fw.dma(QH[:, x, :, :], sqv[:, x, :, :], r=[("sq", x)], w=["QH"], key="QH")
        fw.dma(Sst[:, :, :].rearrange("p v k -> p (v k)"), I["st_wkv"][l], w=["Sst"], key="Sst")
        for t in range(4):
            r_, w_, k_, v_, kk_, b_ = (QH[:, x, t, :] for x in range(6))
            rowb = lambda a: a.unsqueeze(1).to_broadcast([128, 64, 64])
            colb = lambda a: a.unsqueeze(2).to_broadcast([128, 64, 64])
            self.V(lambda e, kk_=kk_: e.tensor_tensor(Stmp[:, :, :], Sst[:, :, :], rowb(kk_), ALU.mult), r=["Sst", "QH"], w=["Stmp"])
            self.V(lambda e: e.tensor_reduce(sk[:, :], Stmp[:, :, :], AX.X, ALU.add), r=["Stmp"], w=["sk"])
            self.P(lambda e, w_=w_: e.tensor_tensor(Sst[:, :, :], Sst[:, :, :], rowb(w_), ALU.mult), r=["Sst", "QH", "Stmp"], w=["Sst"])
            self.V(lambda e, b_=b_: e.tensor_tensor(Stmp[:, :, :], colb(sk[:, :]), rowb(b_), ALU.mult), r=["sk", "QH"], w=["Stmp"])
            self.V(lambda e: e.tensor_tensor(Sst[:, :, :], Sst[:, :, :], Stmp[:, :, :], ALU.subtract), r=["Sst", "Stmp"], w=["Sst"])
            self.P(lambda e, v_=v_, k_=k_: e.tensor_tensor(Stmp[:, :, :], colb(v_), rowb(k_), ALU.mult), r=["QH", "Sst"], w=["Stmp"])
            self.V(lambda e: e.tensor_tensor(Sst[:, :, :], Sst[:, :, :], Stmp[:, :, :], ALU.add), r=["Sst", "Stmp"], w=["Sst"])
            self.P(lambda e, r_=r_: e.tensor_tensor(Stmp[:, :, :], Sst[:, :, :], rowb(r_), ALU.mult), r=["Sst", "QH"], w=["Stmp"])
            self.V(lambda e, t=t: e.tensor_reduce(yh[:, t, :], Stmp[:, :, :], AX.X, ALU.add), r=["Stmp"], w=["yh"])
        fw.dma(O["s_wkv"][l], Sst[:, :, :].rearrange("p v k -> p (v k)"), r=["Sst"], key="Sst")
        if l == 0:
            self.tap("s_QH", QH, ["QH"])
            self.tap("s_yh", yh, ["yh"])
        fw.dma(self.sy.rearrange("(t q) (h d) -> (q h) t d", t=4, h=NH), yh[:, :, :], r=["yh"], w=["sy"], key="yh")
        fw.dma(ytm[0:MS, :], self.sy, r=["sy"], w=["T7"], key="ytm")
        pg, pgk = self.pf()
        fw.mm(pg[0:MS, :], sgT[:, 0:MS], lg2[:, :], True, True, r=[RS.K("sgT"), "lg2"], w=[pgk])
        post(MS, ytm[0:MS, :], ["T7"], pg, pgk, RS)
        m, mk = mrT[0], "mrT0"
        gate_branch(MS, hcur, "hTs", m, mk)
        fw.dma(self.mrbuf[NT].rearrange("p (c t) -> p c t", c=8)[:, :, 0:MS], m[:, :, 0:MS], r=[mk], w=[("mr", NT)], key=mk)
        raw_last(lambda c: hTs[:, c, 64:80], "hTs", 16, O["s_shift"][l])

    def pass_attn(self, l, es2):
        fw, I, O, NT = self.fw, self.I, self.O, self.NT
        sbl = lambda n, s, dt=F32: self.sbl(es2, "a%d_" % l + n, s, dt)
        identb, identf = self.identb, self.identf
        Wq = sbl("Wq", [128, 8, 768], BF)
        Wg = sbl("Wg", [128, 8, D], BF)
        Wa = sbl("Wa", [128, 4, D], BF)
        Wo = sbl("Wo", [128, 8, D], BF)
        self.col_load(self.gcol[:], "gcol", I["norm_mix_g"][l], 8)
        m0 = self.aoff
        self.wstage = [sbl("wst%d" % i_, [128, 2048]) for i_ in range(4)]
        win = I["w_in"][l]
        gsc = lambda c: self.gcol[:, c:c + 1]
        self.prep_w(8, 512, lambda c, s0, n: win[c * 128:(c + 1) * 128, RP:RP + 512],
                    lambda c, s0, n: Wq[:, c, 0:512].rearrange("p (j g d) -> p g j d", j=4, g=2), lambda c: "Wq_%d" % c, "col", gsc,
                    sview=lambda a: a.rearrange("p (g j d) -> p g j d", g=2, j=4))
        self.prep_w(8, 256, lambda c, s0, n: win[c * 128:(c + 1) * 128, RP + 512:RP + 768],
                    lambda c, s0, n: Wq[:, c, 512:768], lambda c: "Wq_%d" % c, "col", gsc)
        self.prep_w(8, D, lambda c, s0, n: win[c * 128:(c + 1) * 128, 3584 + s0:3584 + s0 + n],
                    lambda c, s0, n: Wg[:, c, s0:s0 + n], lambda c: "Wga_%d" % c, "col", gsc)
        wbr = I["w_br_attn"][l]
        self.prep_w(4, D, lambda c, s0, n: wbr[c * 128:(c + 1) * 128, s0:s0 + n],
                    lambda c, s0, n: Wa[:, c, s0:s0 + n], lambda c: "Wa_%d" % c, "plain")
        wo = I["w_out"][l]
        self.prep_w(8, D, lambda c, s0, n: wo[c * 128:(c + 1) * 128, s0:s0 + n],
                    lambda c, s0, n: Wo[:, c, s0:s0 + n], lambda c: "Wo_%d" % c, "plain")
        self.release(m0)
        amask = sbl("amask", [128, 1024])
        fw.dma(amask[:, 0:768], I["c_amask"], w=["amask"], key="amask")
        fw.dma(amask[:, 768:1024], I["c_amask0"], w=["amask"], key="amask")
        smask = sbl("smask", [32, 132])
        fw.dma(smask[:], I["c_smask"], w=["smask"], key="smask")
        sinks = sbl("sinks", [128, NH])
        self.bcast_load(sinks[:], "sinks", I["attn_sinks"][l])
        hTd = [sbl("hT%d" % i_, [128, 8, 128], BF) for i_ in range(2)]
        hT = hTd[1]
        qkv = sbl("qkv", [128, 768])
        rot = sbl("rot", [128, 640])
        rtmp = [sbl("rtmp%d" % i, [128, 320]) for i in range(2)]
        rotb = sbl("rotb", [128, 640], BF)
        cs = [sbl("cs%d" % i, [128, 64]) for i in range(2)]
        qT = sbl("qT", [128, 4, 128], BF)

        class NSB:
            pass
        B0, B1 = NSB(), NSB()
        B0.qkv, B0.rot, B0.rotb, B0.qT, B0.s = qkv, rot, rotb, qT, ""
        B1.qkv, B1.rot, B1.rotb, B1.qT, B1.s = (sbl("qkvb", [128, 768]), sbl("rotbb", [128, 640]), sbl("rotbbb", [128, 640], BF),
                                                sbl("qTb", [128, 4, 128], BF), "b")
        Bs = [B0, B1]
        KTr = sbl("KTr", [128, 2, 128], BF)
        Vp = sbl("Vp", [128, 2, 2, 2, 128], BF)
        scg = [sbl("sc%d" % g_, [128, 4, 256]) for g_ in range(2)]
        stg = [sbl("st%d" % g_, [128, 16]) for g_ in range(2)]
        pbfg = [sbl("pbf%d" % g_, [128, 4, 256], BF) for g_ in range(2)]
        pTg = [sbl("pT%d" % g_, [128, 4, 2, 128], BF) for g_ in range(2)]
        oT = sbl("oT", [128, 4, 128], BF)
        sga = sbl("sga", [128, 8, 128])
        mrl = [sbl("mrl%d" % i, [128, 8, 128], BF) for i in range(2)]
        mg = sbl("mg", [128, 8, 128], BF)
        xo = [sbl("xo%d" % i, [128, D]) for i in range(2)]
        KA = sbl("KA", [128, NS, 128])
        VA = sbl("VA", [128, NS, 128])
        VAb = sbl("VAb", [128, NS, 128], BF)
        KB = sbl("KB", [4, NS, 128])
        VBt = sbl("VB", [4, NS, 128])
        VBb = sbl("VBb", [4, NS, 128], BF)
        KAT = sbl("KAT", [128, NS, 128], BF)
        KBT = sbl("KBT", [128, NS, 4], BF)
        qbd = sbl("qbd", [128, NS, 32], BF)
        ssc = sbl("ssc", [32, NS, 132])
        sst = sbl("sst", [32, 4 * NS])
        spb = sbl("spb", [32, NS, 132], BF)
        spT = sbl("spT", [128, NS, 32], BF)
        spTB = sbl("spTB", [4, NS, 32], BF)
        oTs = sbl("oTs", [128, 4, MS], BF)

        self.V(lambda e: e.memset(Vp[:], 0.0), w=["Vp0", "Vp1"])
        self.V(lambda e: e.memset(KTr[:], 0.0), w=["KTr0", "KTr1"])
        self.V(lambda e: e.memset(qbd[:], 0.0), w=["qbd"])

        def proj_rope(B, M, hcur, hk, cosap, sinap, cskey):
            for g0, n in [(0, 512), (512, 256)]:
                ps, pk = self.pf()
                for c in range(8):
                    fw.mm(ps[0:M, 0:n], hcur(c), Wq[:, c, g0:g0 + n], c == 0, c == 7, r=[hk, "Wq_%d" % c], w=[pk])
                fw.act(B.qkv[0:M, g0:g0 + n], ps[0:M, 0:n], AF.Copy, r=[pk], w=["qkv%d" % (g0 // 512) + B.s])
            qk3 = B.qkv[0:M, 0:640].rearrange("p (h d) -> p h d", h=10)
            r3 = B.rot[0:M, :].rearrange("p (h d) -> p h d", h=10)
            x1, x2 = qk3[:, :, 0:32], qk3[:, :, 32:64]
            cb = cosap.unsqueeze(1).to_broadcast([M, 10, 32])
            sb_ = sinap.unsqueeze(1).to_broadcast([M, 10, 32])
            ta = rtmp[0][0:M, :].rearrange("p (h d) -> p h d", h=10)
            tb = rtmp[1][0:M, :].rearrange("p (h d) -> p h d", h=10)
            rk = ["qkv0" + B.s, "qkv1" + B.s, cskey]
            rotk = "rot" + B.s
            self.V(lambda e: e.tensor_tensor(ta, x1, cb, ALU.mult), r=rk, w=["rtmp0"])
            self.P(lambda e: e.tensor_tensor(tb, x2, sb_, ALU.mult), r=rk, w=["rtmp1"])
            self.V(lambda e: e.tensor_tensor(r3[:, :, 0:32], ta, tb, ALU.subtract), r=["rtmp0", "rtmp1"], w=[rotk])
            self.V(lambda e: e.tensor_tensor(ta, x2, cb, ALU.mult), r=rk + [rotk], w=["rtmp0"])
            self.P(lambda e: e.tensor_tensor(tb, x1, sb_, ALU.mult), r=rk + [rotk], w=["rtmp1"])
            self.V(lambda e: e.tensor_tensor(r3[:, :, 32:64], ta, tb, ALU.add), r=["rtmp0", "rtmp1"], w=[rotk])
            fw.act(B.rotb[0:M, :], B.rot[0:M, :], AF.Copy, r=[rotk], w=["rotb" + B.s])

        def q_transposes(B, M, dst, dkey):
            pbk, pk = self.pb()
            for jj in range(4):
                fw.tr(pbk[:, jj * M:(jj + 1) * M], B.rotb[0:M, jj * 128:(jj + 1) * 128], identb[0:M, 0:M], r=["rotb" + B.s, "identb"], w=[pk])
            fw.act(dst, pbk[:, 0:4 * M].rearrange("p (j t) -> p j t", j=4), AF.Copy, r=[pk], w=[dkey])

        def gates_part(M, hcur, hk):
            for half in range(2):
                pg, pgk = self.pf()
                for q in range(4):
                    dc = half * 4 + q
                    for c in range(8):
                        fw.mm(pg[:, q * M:(q + 1) * M], Wg[:, c, dc * 128:(dc + 1) * 128], hcur(c), c == 0, c == 7, r=[hk, "Wga_%d" % c], w=[pgk])
                fw.act(sga[:, half * 4:(half + 1) * 4, 0:M], pg[:, 0:4 * M].rearrange("p (q t) -> p q t", q=4), AF.Sigmoid, r=[pgk], w=["sga%d" % half])

        def gate_out(M, hcur, hk, oTt, okey, mr, mrk, xt, xk, xo_, xok, do_gates=True):
            if do_gates:
                gates_part(M, hcur, hk)
            for half in range(2):
                pbr, pbk_ = self.pf()
                for q in range(4):
                    dc = half * 4 + q
                    for cc in range(4):
                        fw.mm(pbr[:, q * M:(q + 1) * M], Wa[:, cc, dc * 128:(dc + 1) * 128], oTt[:, cc, 0:M], cc == 0, cc == 3, r=[okey, "Wa_%d" % cc], w=[pbk_])
                hs = slice(half * 4, (half + 1) * 4)
                self.V(lambda e, hs=hs, pbr=pbr: e.tensor_tensor(sga[:, hs, 0:M], sga[:, hs, 0:M], pbr[:, 0:4 * M].rearrange("p (q t) -> p q t", q=4), ALU.mult),
                       r=["sga%d" % half, pbk_], w=["sga%d" % half])
                self.V(lambda e, hs=hs: e.tensor_tensor(mg[:, hs, 0:M], sga[:, hs, 0:M], mr[:, hs, 0:M], ALU.add), r=["sga%d" % half, mrk], w=["mg%d" % half])
            for grp in range(2):
                px, pxk = self.pf()
                for dc in range(8):
                    fw.mm(px[0:M, :], mg[:, dc, 0:M], Wo[:, dc, grp * 512:(grp + 1) * 512], dc == 0, dc == 7, r=["mg%d" % (dc // 4), "Wo_%d" % dc], w=[pxk])
                self.V(lambda e, grp=grp, px=px: e.tensor_tensor(xo_[0:M, grp * 512:(grp + 1) * 512], xt[0:M, grp * 512:(grp + 1) * 512], px[0:M, :], ALU.add),
                       r=[xk, pxk], w=[xok])

        def put_kv(B, slot):
            pbk, pk = self.pb()
            fw.tr(pbk[:, 0:128], B.rotb[:, 512:640], identb[:, :], r=["rotb" + B.s, "identb"], w=[pk])
            self.V(lambda e, pbk=pbk, slot=slot: e.tensor_copy(KTr[:, slot, :], pbk[:, 0:128]), r=[pk], w=["KTr%d" % slot])
            for g in range(2):
                vsrc = B.qkv[:, 640 + g * 64:640 + (g + 1) * 64]
                fw.act(Vp[:, slot, g, 0, 0:64], vsrc, AF.Copy, r=["qkv1" + B.s], w=["Vp%d" % slot])
                self.P(lambda e, g=g, vsrc=vsrc, slot=slot: e.tensor_copy(Vp[:, slot, g, 1, 64:128], vsrc), r=["qkv1" + B.s], w=["Vp%d" % slot])

        xt, xk = self.xt[1], "xt1"
        fw.dma(xt[:], (I["xh0"] if (l == 0 or NSEG == 1) else self.xh_dram), r=["xh_dram"], w=[xk], key=xk)
        fw.dma(cs[1][:, 0:32], I["c_cosh"], w=["cs1"], key="cs1")
        fw.dma(cs[1][:, 32:64], I["c_sinh"], w=["cs1"], key="cs1")
        self.norm_hT(xt, xk, 128, hT[:, :, :], "hT1", identb)
        proj_rope(B1, 128, lambda c: hT[:, c, :], "hT1", cs[1][:, 0:32], cs[1][:, 32:64], "cs1")
        put_kv(B1, 1)
        def pre(i):
            xt, xk = self.xt[i % 2], "xt%d" % (i % 2)
            src, _ = self.xsrc(l, i)
            fw.dma(xt[:], src, r=[("xb", i)], w=[xk], key=xk)
            mr, mrk = mrl[i % 2], "mrl%d" % (i % 2)
            fw.dma(mr[:, :, :], self.mrbuf[i].rearrange("p (c t) -> p c t", c=8), r=[("mr", i)], w=[mrk], key=mrk)
            ck_ = "cs%d" % (i % 2)
            fw.dma(cs[i % 2][:, 0:32], I["c_cosp"][i * 128:(i + 1) * 128, :], w=[ck_], key=ck_)
            fw.dma(cs[i % 2][:, 32:64], I["c_sinp"][i * 128:(i + 1) * 128, :], w=[ck_], key=ck_)
            self.norm_hT(xt, xk, 128, hTd[i % 2][:, :, :], "hT%d" % (i % 2), identb)
            B = Bs[i % 2]
            proj_rope(B, 128, lambda c, i=i: hTd[i % 2][:, c, :], "hT%d" % (i % 2), cs[i % 2][:, 0:32], cs[i % 2][:, 32:64], ck_)
            q_transposes(B, 128, B.qT[:, :, :], "qT" + B.s)

        pre(0)
        for i in range(NT):
            xt, xk = self.xt[i % 2], "xt%d" % (i % 2)
            mr, mrk = mrl[i % 2], "mrl%d" % (i % 2)
            ck_ = "cs%d" % (i % 2)
            hkk = "hT%d" % (i % 2)
            hcur = lambda c, i=i: hTd[i % 2][:, c, :]
            B = Bs[i % 2]
            slot = i % 2
            if i == NT - 1:
                fw.dma(O["p_k"][l], B.rot[:, 512:640], r=["rot" + B.s], key="rot")
                fw.dma(O["p_v"][l], B.qkv[:, 640:768], r=["qkv1" + B.s], key="qkv1")
            put_kv(B, slot)
            mvar = 3 if i == 0 else slot
            msk = amask[:, mvar * 256:(mvar + 1) * 256].unsqueeze(1).to_broadcast([128, 4, 256])
            pSg = []
            for g in range(2):
                o = g * 64
                pS = []
                for jj in range(4):
                    if jj % 2 == 0:
                        ps, pk = self.pf()
                        pS.append((ps, pk))
                    fw.mm(ps[:, (jj % 2) * 256:(jj % 2 + 1) * 256], B.qT[o:o + 64, jj, :], KTr[o:o + 64, :, :].rearrange("p s t -> p (s t)"),
                          True, True, r=["qT" + B.s, "KTr0", "KTr1"], w=[pk])
                pSg.append(pS)
            gates_part(128, hcur, hkk)

            def softmax(g):
                sc, st, pbf = scg[g], stg[g], pbfg[g]
                sck = ["sc%d_0" % g, "sc%d_1" % g]
                for half, (ps, pk) in enumerate(pSg[g]):
                    self.V(lambda e, ps=ps, half=half, msk=msk, sc=sc: e.scalar_tensor_tensor(
                        sc[:, half * 2:(half + 1) * 2, :], ps[:, :].rearrange("p (j c) -> p j c", j=2), 0.125,
                        msk[:, 0:2, :], ALU.mult, ALU.add), r=[pk, "amask"], w=[sck[half]])
                k0, k2, k3 = "st%d" % g, "st%d_2" % g, "st%d_3" % g
                self.V(lambda e: e.tensor_reduce(st[:, 0:4], sc[:, :, :], AX.X, ALU.max), r=sck, w=[k0])
                self.V(lambda e: e.tensor_tensor(st[:, 0:4], st[:, 0:4], sinks[:, g * 4:(g + 1) * 4], ALU.max), r=[k0, "sinks"], w=[k0])
                self.V(lambda e: e.tensor_tensor(sc[:, :, :], sc[:, :, :], bc3(st[:, 0:4], 256), ALU.subtract), r=sck + [k0], w=sck)
                fw.act(sc[:, :, :], sc[:, :, :], AF.Exp, r=sck, w=sck)
                self.V(lambda e: e.tensor_reduce(st[:, 4:8], sc[:, :, :], AX.X, ALU.add), r=sck, w=[k2])
                self.V(lambda e: e.tensor_tensor(st[:, 8:12], sinks[:, g * 4:(g + 1) * 4], st[:, 0:4], ALU.subtract), r=[k0, "sinks"], w=[k3])
                fw.act(st[:, 8:12], st[:, 8:12], AF.Exp, r=[k3], w=[k3])
                self.V(lambda e: e.tensor_tensor(st[:, 4:8], st[:, 4:8], st[:, 8:12], ALU.add), r=[k2, k3], w=[k2])
                self.V(lambda e: e.reciprocal(st[:, 4:8], st[:, 4:8]), r=[k2], w=[k2])
                self.V(lambda e: e.tensor_tensor(pbf[:, :, :], sc[:, :, :], bc3(st[:, 4:8], 256), ALU.mult), r=sck + [k2], w=["pbf%d" % g])

            def p_transposes(g):
                pbf, pT = pbfg[g], pTg[g]
                pbk, pk = self.pb()
                for jj in range(4):
                    for s_ in range(2):
                        fw.tr(pbk[:, (jj * 2 + s_) * 128:(jj * 2 + s_ + 1) * 128], pbf[:, jj, s_ * 128:(s_ + 1) * 128], identb[:, :], r=["pbf%d" % g, "identb"], w=[pk])
                fw.act(pT[:, :, :, :], pbk[:, :].rearrange("p (j s t) -> p j s t", j=4, s=2), AF.Copy, r=[pk], w=["pT%d" % g])

            def pv(g, pO, pok):
                pT = pTg[g]
                for c2 in range(2):
                    cc = g * 2 + c2
                    n = 0
                    for par in range(2):
                        jj = c2 * 2 + par
                        for s_ in range(2):
                            fw.mm(pO[:, cc * 128:(cc + 1) * 128], Vp[:, s_, g, par, :], pT[:, jj, s_, :], n == 0, n == 3,
                                  r=["Vp0", "Vp1", "pT%d" % g], w=[pok])
                            n += 1

            softmax(0)
            p_transposes(0)
            pO, pok = self.pf()
            pv(0, pO, pok)
            softmax(1)
            if i + 1 < NT:
                pre(i + 1)
            p_transposes(1)
            pv(1, pO, pok)
            fw.act(oT[:, :, :], pO[:, :].rearrange("p (c t) -> p c t", c=4), AF.Copy, r=[pok], w=["oT"])
            xo_, xok = xo[i % 2], "xo%d" % (i % 2)
            gate_out(128, hcur, hkk, oT, "oT", mr, mrk, xt, xk, xo_, xok, do_gates=False)
            fw.dma(self.xbuf[i * 128:(i + 1) * 128, :], xo_[:, :], r=[xok], w=[("xb", i)], key=xok)
        if NSEG > 1:
            self.gather_select(xo_[:, :], [xok], D, self.agX_in, self.agX_out, "agX")
            fw.dma(self.xh_dram, xo_[:, :], r=[xok], w=["xh_dram"], key="xhst")

        i = NT
        xt, xk = self.xt[i % 2], "xt%d" % (i % 2)
        src, _ = self.xsrc(l, i)
        fw.dma(xt[0:MS, :], src, r=[("xb", i)], w=[xk], key=xk)
        mr, mrk = mrl[i % 2], "mrl%d" % (i % 2)
        fw.dma(mr[:, :, 0:MS], self.mrbuf[NT].rearrange("p (c t) -> p c t", c=8)[:, :, 0:MS], r=[("mr", NT)], w=[mrk], key=mrk)
        ck_ = "cs%d" % (i % 2)
        fw.dma(cs[i % 2][0:MS, 0:32], I["c_coss"], w=[ck_], key=ck_)
        fw.dma(cs[i % 2][0:MS, 32:64], I["c_sins"], w=[ck_], key=ck_)
        self.norm_hT(xt, xk, MS, hT[:, :, 0:MS], "hT1", identb)
        hcur = lambda c: hT[:, c, 0:MS]
        proj_rope(B0, MS, hcur, "hT1", cs[i % 2][0:MS, 0:32], cs[i % 2][0:MS, 32:64], ck_)
        for (cin, cout, srcap, srck, dkey) in [("ck", "s_k", rot[:, 512:640], "rot", "sk"), ("cv", "s_v", qkv[:, 640:768], "qkv1", "sv")]:
            fw.dma(O[cout][l, :, 0:124, :], I[cin][l, :, 4:128, :], w=[dkey], key=dkey + "c")
            for t in range(4):
                fw.dma(O[cout][l, :, 124 + t, :], srcap[t * 16:(t + 1) * 16, :], r=[srck], w=[dkey], key=dkey + "n")
        fw.dma(KA[:, :, :], O["s_k"][l].rearrange("q p c -> p q c"), r=["sk"], w=["KA"], key="KA")
        fw.dma(VA[:, :, :], O["s_v"][l].rearrange("q p c -> p q c"), r=["sv"], w=["VA"], key="VA")
        fw.dma(KB[:, :, :], I["ck"][l, :, 0:4, :].rearrange("q p c -> p q c"), w=["KB"], key="KB")
        fw.dma(VBt[:, :, :], I["cv"][l, :, 0:4, :].rearrange("q p c -> p q c"), w=["VB"], key="VB")
        self.P(lambda e: e.tensor_copy(VAb[:, :, :], VA[:, :, :]), r=["VA"], w=["VAb"])
        self.P(lambda e: e.tensor_copy(VBb[:, :, :], VBt[:, :, :]), r=["VB"], w=["VBb"])
        for q4 in range(4):
            ps, pk = self.pf()
            for qq in range(4):
                q = q4 * 4 + qq
                fw.tr(ps[:, qq * 128:(qq + 1) * 128], KA[:, q, :], identf[:, :], r=["KA", "identf"], w=[pk])
            fw.act(KAT[:, q4 * 4:(q4 + 1) * 4, :], ps[:, :].rearrange("p (q t) -> p q t", q=4), AF.Copy, r=[pk], w=["KAT"])
        ps, pk = self.pf()
        for q in range(NS):
            fw.tr(ps[:, q * 4:(q + 1) * 4], KB[0:4, q, :], identf[0:4, 0:4], r=["KB", "identf"], w=[pk])
        fw.act(KBT[:, :, :], ps[:, 0:64].rearrange("p (q t) -> p q t", q=NS), AF.Copy, r=[pk], w=["KBT"])
        q_transposes(B0, MS, qT[:, :, 0:MS], "qT")
        for g in range(2):
            for jj in range(4):
                o = g * 64
                dst = qbd[o:o + 64, :, g * 16 + jj * 4:g * 16 + (jj + 1) * 4]
                srcq = qT[o:o + 64, jj, 0:MS].rearrange("p (t q) -> p q t", t=4)
                self.V(lambda e, dst=dst, srcq=srcq: e.tensor_copy(dst, srcq), r=["qT"], w=["qbd"])
        pSA = []
        for q4 in range(4):
            ps, pk = self.pf()
            pSA.append((ps, pk))
            for qq in range(4):
                q = q4 * 4 + qq
                fw.mm(ps[0:32, qq * 128:(qq + 1) * 128], qbd[:, q, :], KAT[:, q, :], True, True, r=["qbd", "KAT"], w=[pk])
        psB, pkB = self.pf()
        for q in range(NS):
            fw.mm(psB[0:32, q * 4:(q + 1) * 4], qbd[:, q, :], KBT[:, q, :], True, True, r=["qbd", "KBT"], w=[pkB])
        for q4, (ps, pk) in enumerate(pSA):
            self.V(lambda e, q4=q4, ps=ps: e.scalar_tensor_tensor(
                ssc[:, q4 * 4:(q4 + 1) * 4, 0:128], ps[0:32, :].rearrange("p (q c) -> p q c", q=4), 0.125,
                smask[:, 0:128].unsqueeze(1).to_broadcast([32, 4, 128]), ALU.mult, ALU.add), r=[pk, "smask"], w=["ssc"])
        self.V(lambda e: e.scalar_tensor_tensor(
            ssc[:, :, 128:132], psB[0:32, 0:64].rearrange("p (q c) -> p q c", q=NS), 0.125,
            smask[:, 128:132].unsqueeze(1).to_broadcast([32, NS, 4]), ALU.mult, ALU.add), r=[pkB, "smask"], w=["ssc"])
        sinkc = sbl("sinkc", [32, 1])
        for g in range(2):
            for jj in range(4):
                p0 = g * 16 + jj * 4
                fw.dma(sinkc[p0:p0 + 4, :], I["attn_sinks"][l, g * 4 + jj:g * 4 + jj + 1].partition_broadcast(4), w=["sinkc"], key="sinkc")
        self.V(lambda e: e.tensor_reduce(sst[:, 0:NS], ssc[:, :, :], AX.X, ALU.max), r=["ssc"], w=["sst"])
        self.V(lambda e: e.tensor_scalar(sst[:, 0:NS], sst[:, 0:NS], sinkc[:, 0:1], None, ALU.max), r=["sst", "sinkc"], w=["sst"])
        self.V(lambda e: e.tensor_tensor(ssc[:, :, :], ssc[:, :, :], bc3(sst[:, 0:NS], 132), ALU.subtract), r=["ssc", "sst"], w=["ssc"])
        fw.act(ssc[:, :, :], ssc[:, :, :], AF.Exp, r=["ssc"], w=["ssc"])
        self.V(lambda e: e.tensor_reduce(sst[:, NS:2 * NS], ssc[:, :, :], AX.X, ALU.add), r=["ssc"], w=["sst2"])
        self.V(lambda e: e.tensor_scalar(sst[:, 2 * NS:3 * NS], sst[:, 0:NS], sinkc[:, 0:1], None, ALU.subtract), r=["sst", "sinkc"], w=["sst3"])
        fw.act(sst[:, 2 * NS:3 * NS], sst[:, 2 * NS:3 * NS], AF.Exp, r=["sst3"], w=["sst3"], scale=-1.0)
        self.V(lambda e: e.tensor_tensor(sst[:, NS:2 * NS], sst[:, NS:2 * NS], sst[:, 2 * NS:3 * NS], ALU.add), r=["sst2", "sst3"], w=["sst2"])
        self.V(lambda e: e.reciprocal(sst[:, NS:2 * NS], sst[:, NS:2 * NS]), r=["sst2"], w=["sst2"])
        self.V(lambda e: e.tensor_tensor(spb[:, :, :], ssc[:, :, :], bc3(sst[:, NS:2 * NS], 132), ALU.mult), r=["ssc", "sst2"], w=["spb"])
        identb32 = identb[0:32, 0:32]
        for q8 in range(2):
            pbk, pk = self.pb()
            for qq in range(8):
                q = q8 * 8 + qq
                fw.tr(pbk[:, qq * 32:(qq + 1) * 32], spb[:, q, 0:128], identb32, r=["spb", "identb"], w=[pk])
            fw.act(spT[:, q8 * 8:(q8 + 1) * 8, :], pbk[:, 0:256].rearrange("p (q c) -> p q c", q=8), AF.Copy, r=[pk], w=["spT"])
        pbk, pk = self.pb()
        for q in range(NS):
            fw.tr(pbk[0:4, q * 32:(q + 1) * 32], spb[:, q, 128:132], identb32, r=["spb", "identb"], w=[pk])
        fw.act(spTB[:, :, :], pbk[0:4, 0:512].rearrange("p (q c) -> p q c", q=NS), AF.Copy, r=[pk], w=["spTB"])
        pO, pok = self.pf()
        for q in range(NS):
            fw.mm(pO[:, q * 32:(q + 1) * 32], VAb[:, q, :], spT[:, q, :], True, False, r=["VAb", "spT"], w=[pok])
            fw.mm(pO[:, q * 32:(q + 1) * 32], VBb[0:4, q, :], spTB[0:4, q, :], False, True, r=["VBb", "spTB"], w=[pok])
        oraw = sbl("oraw", [128, 32, NS], BF)
        fw.act(oraw.rearrange("p c q -> p q c"), pO[:, :].rearrange("p (q c) -> p q c", q=NS), AF.Copy, r=[pok], w=["oraw"])
        for g in range(2):
            for jj in range(4):
                cc, par = g * 2 + jj // 2, jj % 2
                c0 = g * 16 + jj * 4
                srco = oraw[g * 64:(g + 1) * 64, c0:c0 + 4, :].rearrange("p t q -> p (t q)")
                fw.dma(oTs[par * 64:(par + 1) * 64, cc, :], srco, r=["oraw"], w=["oTs"], key="oTs")
        xo_, xok = xo[i % 2], "xo%d" % (i % 2)
        gate_out(MS, hcur, "hT1", oTs, "oTs", mr, mrk, xt, xk, xo_, xok)
        fw.dma(self.xsbuf, xo_[0:MS, :], r=[xok], w=[("xb", NT)], key=xok)

    def pass_ffn(self, l, es2):
        fw, I, O, NT = self.fw, self.I, self.O, self.NT
        sbl = lambda n, s, dt=F32: self.sbl(es2, "f%d_" % l + n, s, dt)
        identb, identf = self.identb, self.identf
        Wc = sbl("Wc", [128, 8, DFF], BF)
        Wu = sbl("Wu", [128, 8, DFF], BF)
        Wd = sbl("Wd", [128, NFC, D], BF)
        self.col_load(self.gcol[:], "gcol", I["norm_ffn_g"][l], 8)
        cw = sbl("cw", [128, 4, NFC])
        for j in range(3):
            self.col_load(cw[:, j, :], "cw", I["ffn_conv_w"][l, j], NFC)
        self.col_load(cw[:, 3, :], "cw", I["ffn_conv_b"][l], NFC)
        m0 = self.aoff
        self.wstage = [sbl("wst%d" % i_, [128, 2048]) for i_ in range(4)]
        wi = I["ffn_w_in"][l]
        gsc = lambda c: self.gcol[:, c:c + 1]
        self.prep_w(8, DFF, lambda c, s0, n: wi[c * 128:(c + 1) * 128, s0:s0 + n],
                    lambda c, s0, n: Wc[:, c, s0:s0 + n], lambda c: "Wc_%d" % c, "col", gsc)
        self.prep_w(8, DFF, lambda c, s0, n: wi[c * 128:(c + 1) * 128, DFF + s0:DFF + s0 + n],
                    lambda c, s0, n: Wu[:, c, s0:s0 + n], lambda c: "Wu_%d" % c, "col", gsc)
        wd = I["ffn_w_down"][l]
        self.prep_w(NFC, D, lambda c, s0, n: wd[c * 128:(c + 1) * 128, s0:s0 + n],
                    lambda c, s0, n: Wd[:, c, s0:s0 + n], lambda c: "Wd_%d" % c, "plain")
        self.release(m0)
        last = (l == 1)
        if last:
            gf = sbl("gf", [128, D])
            self.bcast_load(gf[:], "gf", I["norm_final_g"])
        hTd = [sbl("hT%d" % i_, [128, 8, 128], BF) for i_ in range(2)]
        hT = hTd[0]
        cxf = sbl("cx", [128, NFC * 130])
        cx1 = cxf.rearrange("p (f t) -> p f t", f=NFC)
        cxs = cxf[:, 0:NFC * NS * 6].rearrange("p (f q j) -> p f q j", f=NFC, q=NS)
        acc = [sbl("acc%d" % i_, [128, 4, 128]) for i_ in range(2)]
        aTd = [sbl("aT%d" % i_, [128, NFC, 128], BF) for i_ in range(2)]
        xo = [sbl("xo%d" % i_, [128, D]) for i_ in range(2)]
        ctok = sbl("ctok", [128, DFF])
        cst = ctok
        jk = self.xn

        def finish(M, xt, xk, xo_, xok, dst_final, dst_x, dkey, aT, aTk):
            for grp in range(2):
                px, pxk = self.pf()
                for fc in range(NFC):
                    fw.mm(px[0:M, :], aT[:, fc, 0:M], Wd[:, fc, grp * 512:(grp + 1) * 512], fc == 0, fc == NFC - 1, r=[aTk, "Wd_%d" % fc], w=[pxk])
                self.V(lambda e, grp=grp, px=px: e.tensor_tensor(xo_[0:M, grp * 512:(grp + 1) * 512], xt[0:M, grp * 512:(grp + 1) * 512], px[0:M, :], ALU.add),
                       r=[xk, pxk], w=[xok])
            if not last:
                fw.dma(dst_x, xo_[0:M, :], r=[xok], w=[dkey], key=xok)
                return
            ss, t1 = self.ss, self.t1
            fw.act(jk[0:M, :], xo_[0:M, :], AF.Square, r=[xok], w=["xn", "ss"], accum_out=ss[0:M, :])
            self.V(lambda e: e.tensor_scalar(t1[0:M, :], ss[0:M, :], 1.0 / D, 1e-6, ALU.mult, ALU.add), r=["ss"], w=["t1"])
            fw.act(t1[0:M, :], t1[0:M, :], AF.Sqrt, r=["t1"], w=["t1"])
            self.V(lambda e: e.reciprocal(t1[0:M, :], t1[0:M, :]), r=["t1"], w=["t1"])
            self.V(lambda e: e.scalar_tensor_tensor(xo_[0:M, :], xo_[0:M, :], t1[0:M, 0:1], gf[0:M, :], ALU.mult, ALU.mult),
                   r=[xok, "t1", "gf"], w=[xok])
            fw.dma(dst_final, xo_[0:M, :], r=[xok], key=xok)

        def ffn_core(M, hcur, hk, cview, ckey, sample, aT, aTk, mid=None, groups=None):
            for b0 in (groups if groups is not None else range(0, NFC, 4)):
                nb = min(4, NFC - b0)
                pc, pck = self.pf()
                for q in range(nb):
                    fc = b0 + q
                    for c in range(8):
                        fw.mm(pc[:, q * M:(q + 1) * M], Wc[:, c, fc * 128:(fc + 1) * 128], hcur(c), c == 0, c == 7, r=[hk, "Wc_%d" % c], w=[pck])
                pu, puk = self.pf()
                for q in range(nb):
                    fc = b0 + q
                    for c in range(8):
                        fw.mm(pu[:, q * M:(q + 1) * M], Wu[:, c, fc * 128:(fc + 1) * 128], hcur(c), c == 0, c == 7, r=[hk, "Wu_%d" % c], w=[puk])
                if sample:
                    fw.act(cview[:, b0:b0 + nb, :, 2:6], pc[:, 0:nb * M].rearrange("p (f t q) -> p f q t", f=nb, t=4), AF.Copy, r=[pck], w=[ckey])
                else:
                    fw.act(cview[:, b0:b0 + nb, 2:130], pc[:, 0:nb * M].rearrange("p (f t) -> p f t", f=nb), AF.Copy, r=[pck], w=[ckey])
                a_ = acc[(b0 // 4) % 2]
                ak = "acc%d" % ((b0 // 4) % 2)
                views = []
                for q in range(nb):
                    fc = b0 + q
                    if sample:
                        c0, c1, c2 = (cview[:, fc, :, s_:s_ + 4] for s_ in range(3))
                        av = a_[:, q, 0:M].rearrange("p (t q) -> p q t", t=4)
                    else:
                        c0, c1, c2 = (cview[:, fc, s_:s_ + 128] for s_ in range(3))
                        av = a_[:, q, :]
                    views.append((fc, av, c0, c1, c2))
                akq = [ak + "_%d" % q for q in range(nb)]
                for q, (fc, av, c0, c1, c2) in enumerate(views):
                    self.P(lambda e, av=av, c0=c0, fc=fc: e.tensor_scalar(av, c0, cw[:, 0, fc:fc + 1], cw[:, 3, fc:fc + 1], ALU.mult, ALU.add),
                           r=[ckey, "cw", ak], w=([akq[q], ak] if q == 0 else [akq[q]]))
                for q, (fc, av, c0, c1, c2) in enumerate(views):
                    self.V(lambda e, av=av, c1=c1, fc=fc: e.scalar_tensor_tensor(av, c1, cw[:, 1, fc:fc + 1], av, ALU.mult, ALU.add),
                           r=[ckey, "cw", akq[q]], w=[akq[q]])
                for q, (fc, av, c0, c1, c2) in enumerate(views):
                    self.V(lambda e, av=av, c2=c2, fc=fc: e.scalar_tensor_tensor(av, c2, cw[:, 2, fc:fc + 1], av, ALU.mult, ALU.add),
                           r=[ckey, "cw", akq[q]], w=[akq[q]])
                fw.act(a_[:, 0:nb, 0:M], a_[:, 0:nb, 0:M], AF.Gelu, r=akq, w=[ak])
                self.V(lambda e, a_=a_, pu=pu, nb=nb, b0=b0, aT=aT: e.tensor_tensor(aT[:, b0:b0 + nb, 0:M], a_[:, 0:nb, 0:M],
                                                                              pu[:, 0:nb * M].rearrange("p (f t) -> p f t", f=nb), ALU.mult),
                       r=[ak, puk], w=[aTk])
                if mid is not None and b0 == 8:
                    mid()

        def c_token_major(M, hcur, hk, rows, dsts):
            for g0 in range(0, DFF, 512):
                n = min(512, DFF - g0)
                ps, pk = self.pf()
                for c in range(8):
                    fw.mm(ps[0:M, 0:n], hcur(c), Wc[:, c, g0:g0 + n], c == 0, c == 7, r=[hk, "Wc_%d" % c], w=[pk])
                fw.act(ctok[0:M, g0:g0 + n], ps[0:M, 0:n], AF.Copy, r=[pk], w=["ctok"])
            for (r0, r1), dst in zip(rows, dsts):
                fw.dma(dst, ctok[r0:r1, :], r=["ctok"], key="ctok")

        xt, xk = self.xt[1], "xt1"
        fw.dma(xt[:], (I["xh0"] if NSEG == 1 else self.xh_dram), r=["xh_dram"], w=[xk], key=xk)
        self.norm_hT(xt, xk, 128, hT[:, :, :], "hT0", identb)
        pc, pck = self.pf()
        for fc in range(NFC):
            for c in range(8):
                fw.mm(pc[:, fc * 2:(fc + 1) * 2], Wc[:, c, fc * 128:(fc + 1) * 128], hT[:, c, 126:128], c == 0, c == 7, r=["hT0", "Wc_%d" % c], w=[pck])
        fw.act(cx1[:, :, 0:2], pc[:, 0:2 * NFC].rearrange("p (f t) -> p f t", f=NFC), AF.Copy, r=[pck], w=["cx"])
        def pre(i):
            xt, xk = self.xt[i % 2], "xt%d" % (i % 2)
            fw.dma(xt[:], self.xbuf[i * 128:(i + 1) * 128, :], r=[("xb", i)], w=[xk], key=xk)
            self.norm_hT(xt, xk, 128, hTd[i % 2][:, :, :], "hT%d" % (i % 2), identb)

        def head(i):
            if i > 0:
                self.P(lambda e: e.tensor_copy(acc[0][:, 0, 0:2 * NFC].rearrange("p (f t) -> p f t", f=NFC), cx1[:, :, 128:130]), r=["cx"], w=["acc0"])
                self.P(lambda e: e.tensor_copy(cx1[:, :, 0:2], acc[0][:, 0, 0:2 * NFC].rearrange("p (f t) -> p f t", f=NFC)), r=["acc0"], w=["cx"])
            ffn_core(128, lambda c, i=i: hTd[i % 2][:, c, :], "hT%d" % (i % 2), cx1, "cx", False, aTd[i % 2], "aT%d" % (i % 2), groups=[0])

        pre(0)
        head(0)
        for i in range(NT):
            xt, xk = self.xt[i % 2], "xt%d" % (i % 2)
            hcur = lambda c, i=i: hTd[i % 2][:, c, :]
            hkk = "hT%d" % (i % 2)
            mid = (lambda i=i: pre(i + 1)) if i + 1 < NT else None
            ffn_core(128, hcur, hkk, cx1, "cx", False, aTd[i % 2], "aT%d" % (i % 2), mid, groups=list(range(4, NFC, 4)))
            if i == NT - 1:
                c_token_major(128, hcur, hkk, [(126, 128)], [O["p_conv"][l]])
            if i + 1 < NT:
                head(i + 1)
            xo_, xok = xo[i % 2], "xo%d" % (i % 2)
            finish(128, xt, xk, xo_, xok, O["yp"][i * 128:(i + 1) * 128, :], self.xbuf[i * 128:(i + 1) * 128, :], ("xb", i),
                   aTd[i % 2], "aT%d" % (i % 2))
        if not last and NSEG > 1:
            self.gather_select(xo_[:, :], [xok], D, self.agX_in, self.agX_out, "agX")
            fw.dma(self.xh_dram, xo_[:, :], r=[xok], w=["xh_dram"], key="xhst")

        i = NT
        xt, xk = self.xt[i % 2], "xt%d" % (i % 2)
        fw.dma(xt[0:MS, :], self.xsbuf, r=[("xb", i)], w=[xk], key=xk)
        self.norm_hT(xt, xk, MS, hT[:, :, 0:MS], "hT0", identb)
        hcur = lambda c: hT[:, c, 0:MS]
        fw.dma(cst[0:32, :], I["st_conv"][l], w=["ctok"], key="cst")
        for b0 in range(0, NFC, 4):
            nb = min(4, NFC - b0)
            ps, pk = self.pf()
            for q in range(nb):
                fc = b0 + q
                fw.tr(ps[:, q * 32:(q + 1) * 32], cst[0:32, fc * 128:(fc + 1) * 128], identf[0:32, 0:32], r=["ctok", "identf"], w=[pk])
            fw.act(cxs[:, b0:b0 + nb, :, 0:2], ps[:, 0:nb * 32].rearrange("p (f q j) -> p f q j", f=nb, j=2), AF.Copy, r=[pk], w=["cx"])
        ffn_core(MS, hcur, "hT0", cxs, "cx", True, aTd[0], "aT0")
        sc_ = O["s_conv"][l].rearrange("(q j) f -> j q f", j=2)
        c_token_major(MS, hcur, "hT0", [(32, 48), (48, 64)], [sc_[0], sc_[1]])
        xo_, xok = xo[i % 2], "xo%d" % (i % 2)
        finish(MS, xt, xk, xo_, xok, O["ys"], self.xsbuf, ("xb", NT), aTd[0], "aT0")


NSEG = 1


def _consts_shared():
    c = {}
    c["c_ident"] = np.eye(128, dtype=np.float32)
    inv = (10000.0 ** (-np.arange(0, HD, 2, dtype=np.float32) / HD)).astype(np.float32)
    pos_s = (PAST + np.repeat(np.arange(4), NS)).astype(np.float32)
    ang_s = pos_s[:, None] * inv[None, :]
    c["c_coss"] = np.cos(ang_s).astype(np.float32)
    c["c_sins"] = np.sin(ang_s).astype(np.float32)
    s = np.arange(128)[:, None]
    t = np.arange(128)[None, :]
    incl = (s <= t).astype(np.float32)
    strict = (s < t).astype(np.float32)
    c["c_tri"] = np.concatenate([incl * CDEC, strict * CDEC], 1).astype(np.float32)
    c["c_mask2"] = np.concatenate([incl, strict], 1).astype(np.float32)
    c["c_maskL"] = (s > t).astype(np.float32)
    i_ = np.arange(128)[:, None]
    j_ = np.arange(128)[None, :]
    cur = np.where(j_ <= i_, 0.0, NEG)
    prev = np.where(j_ > i_, 0.0, NEG)
    dead = np.full((128, 128), NEG)
    c["c_amask"] = np.concatenate([cur, prev, prev, cur, cur, dead], 1).astype(np.float32)
    c["_am_first"] = np.concatenate([cur, dead], 1).astype(np.float32)
    c["_am_mid"] = np.concatenate([cur, prev], 1).astype(np.float32)
    tt = (np.arange(32) % 4)[:, None]
    ia = np.arange(128)[None, :]
    ma = np.where(ia <= 124 + tt, 0.0, NEG)
    rb = np.arange(4)[None, :]
    mb = np.where(rb > tt, 0.0, NEG)
    c["c_smask"] = np.concatenate([ma, mb], 1).astype(np.float32)
    last = np.zeros((128, 1), np.float32)
    last[127, 0] = 1.0
    c["c_last"] = last
    c["_inv"] = inv
    return c


def _rope_tab(pos, inv):
    ang = pos.astype(np.float32)[:, None] * inv[None, :]
    return np.cos(ang).astype(np.float32), np.sin(ang).astype(np.float32)


_CACHE = {}
TAPS = False
TAP_OUT = {}


def kernel(**inp):
    inp = {k: np.asarray(v) for k, v in inp.items()}
    xp_all = inp["x_prompt"].astype(np.float32)
    B, SEQ_, _ = xp_all.shape
    TPC = SEQ_ // NSEG
    if TPC not in _CACHE:
        b_ = Builder(TPC, taps=TAPS)
        _CACHE[TPC] = (b_.build(), b_.tapnames)
    nc, tapnames = _CACHE[TPC]
    consts = _consts_shared()
    inv = consts.pop("_inv")
    am_first, am_mid = consts.pop("_am_first"), consts.pop("_am_mid")
    wnames = ["norm_mix_g", "w_in", "rwkv_mu", "rwkv_w0", "rwkv_w2", "rwkv_a0", "rwkv_a2", "rwkv_g2", "rwkv_k_k",
              "rwkv_k_a", "rwkv_ln_g", "rwkv_ln_b", "attn_sinks", "w_br_rwkv", "w_br_attn", "w_out", "norm_ffn_g",
              "ffn_w_in", "ffn_conv_w", "ffn_conv_b", "ffn_w_down", "norm_final_g"]
    shared = {n: np.ascontiguousarray(inp[n], dtype=np.float32) for n in wnames}
    shared["rwkv_r_k"] = np.ascontiguousarray(inp["rwkv_r_k"], dtype=np.float32).reshape(2, RD)
    shared.update(consts)
    in_maps = []
    ncores = 8
    for c in range(ncores):
        b, seg = (c // NSEG) % B, c % NSEG
        sl = slice(c * NS, (c + 1) * NS)
        m = dict(shared)
        t0 = seg * TPC
        m["xp"] = np.ascontiguousarray(xp_all[b, t0:t0 + TPC])
        m["xh0"] = np.ascontiguousarray(xp_all[b, t0 - 128:t0]) if seg > 0 else np.zeros((128, D), np.float32)
        m["c_cosp"], m["c_sinp"] = _rope_tab(t0 + np.arange(TPC), inv)
        m["c_cosh"], m["c_sinh"] = _rope_tab(np.maximum(t0 - 128 + np.arange(128), 0), inv)
        m["c_amask0"] = am_mid if seg > 0 else am_first
        sel = np.zeros((128, 8), np.float32)
        if seg > 0:
            sel[:, c - 1] = 1.0
        m["c_sel"] = sel
        m["xs"] = np.ascontiguousarray(inp["x_sample"][sl].transpose(1, 0, 2).reshape(MS, D))
        m["st_shift"] = np.ascontiguousarray(inp["state_rwkv_shift"][:, sl])
        m["st_wkv"] = np.ascontiguousarray(inp["state_rwkv_wkv"][:, sl]).reshape(2, 128, 4096)
        m["ck"] = np.ascontiguousarray(inp["cache_swa_k"][:, sl]).reshape(2, NS, 128, 128)
        m["cv"] = np.ascontiguousarray(inp["cache_swa_v"][:, sl]).reshape(2, NS, 128, 128)
        m["st_conv"] = np.ascontiguousarray(inp["state_ffn_conv"][:, sl]).reshape(2, 2 * NS, DFF)
        in_maps.append(m)
    res = run_bass_kernel_spmd(nc, in_maps, core_ids=list(range(ncores)))
    R = res.results
    for tn in tapnames:
        TAP_OUT[tn] = [np.asarray(R[c][tn]) for c in range(ncores)]
    f = np.float32
    lastc = [b * NSEG + NSEG - 1 for b in range(B)]
    y_prompt = np.stack([np.concatenate([R[b * NSEG + sg]["yp"] for sg in range(NSEG)], 0) for b in range(B)]).astype(f)
    y_sample = np.concatenate([R[c]["ys"].reshape(4, NS, D).transpose(1, 0, 2) for c in range(ncores)], 0).astype(f)
    p_shift = np.stack([R[c]["p_shift"] for c in lastc], 1).astype(f)
    p_wkv = np.stack([R[c]["p_wkv"] for c in lastc], 1).astype(f)
    p_k = np.stack([R[c]["p_k"] for c in lastc], 1).reshape(2, B, 128, 2, 64).astype(f)
    p_v = np.stack([R[c]["p_v"] for c in lastc], 1).reshape(2, B, 128, 2, 64).astype(f)
    p_conv = np.stack([R[c]["p_conv"] for c in lastc], 1).astype(f)
    s_shift = np.concatenate([R[c]["s_shift"] for c in range(ncores)], 1).astype(f)
    s_wkv = np.concatenate([R[c]["s_wkv"].reshape(2, NS, NH, 64, 64) for c in range(ncores)], 1).astype(f)
    s_k = np.concatenate([R[c]["s_k"].reshape(2, NS, 128, 2, 64) for c in range(ncores)], 1).astype(f)
    s_v = np.concatenate([R[c]["s_v"].reshape(2, NS, 128, 2, 64) for c in range(ncores)], 1).astype(f)
    s_conv = np.concatenate([R[c]["s_conv"].reshape(2, NS, 2, DFF) for c in range(ncores)], 1).astype(f)
    return (y_prompt, y_sample, p_shift, p_wkv, p_k, p_v, p_conv, s_shift, s_wkv, s_k, s_v, s_conv)
```

```python
import math
from contextlib import ExitStack

import numpy as np
import concourse.bass as bass
import concourse.mybir as mybir
from concourse.bass_utils import run_bass_kernel_spmd

F32 = mybir.dt.float32
BF = mybir.dt.bfloat16
AF = mybir.ActivationFunctionType
ALU = mybir.AluOpType
AX = mybir.AxisListType

ENGS = ["sp", "pe", "act", "dve", "pool"]
DEBUG_WHERE = True

D = 1024
HD = 64
NH = 8
RD = 512
RP = 1792
INP = 4608
DFF = 2816
NFC = 22
NS = 16
MS = 64
PAST = 16384
CDEC = -math.exp(-0.5)
NEG = -30000.0


class FW:
    def __init__(self, nc, es):
        self.nc = nc
        self.es = es
        self.ops = {e: [] for e in ENGS}
        self.lastw = {}
        self.readers = {}
        self.dma_count = {}
        self.inc = {}

    def sb(self, name, shape, dt=F32):
        return self.es.enter_context(self.nc.sbuf_tensor(name, list(shape), dt))

    def ps(self, name, shape, dt=F32):
        return self.es.enter_context(self.nc.psum_tensor(name, list(shape), dt))

    def capture(self, f):
        self.cap = []
        f()
        log, self.cap = self.cap, None
        return log

    def replay(self, logs, chunk=2):
        logs = [list(lg) for lg in logs if lg]
        if not logs:
            return
        mn = min(len(lg) for lg in logs)
        per = [max(1, int(round(chunk * len(lg) / mn))) for lg in logs]
        pos = [0] * len(logs)
        while any(p < len(lg) for p, lg in zip(pos, logs)):
            for k, lg in enumerate(logs):
                for _ in range(per[k]):
                    if pos[k] < len(lg):
                        self.op(*lg[pos[k]])
                        pos[k] += 1

    def op(self, eng, fn, r=(), w=(), dma=None):
        if getattr(self, "cap", None) is not None:
            self.cap.append((eng, fn, tuple(r), tuple(w), dma))
            return
        ops = self.ops[eng]
        idx = len(ops)
        deps = set()
        pr = [k for k in r if isinstance(k, str) and k[:2] in ("ps", "pb") and k[2:].isdigit()]
        if pr:
            r = [k for k in r if k not in pr]
            w = list(w) + pr
        for k in r:
            t = self.lastw.get(k)
            if t is not None:
                deps.add(t)
        for k in w:
            t = self.lastw.get(k)
            if t is not None:
                deps.add(t)
            for t2 in self.readers.get(k, {}).values():
                deps.add(t2)
        if dma is not None:
            c = self.dma_count.get(dma, 0) + 1
            self.dma_count[dma] = c
            tok = ("d", dma, c)
        else:
            tok = ("c", eng, idx)
        if eng == "pe":
            deps = {d for d in deps if not (d[0] == "c" and d[1] == "pe")}
        deps.discard(tok)
        rec = dict(fn=fn, deps=deps, tok=tok, signal=False)
        if DEBUG_WHERE:
            import sys as _s
            f_ = _s._getframe(1)
            wh = []
            while f_ is not None and len(wh) < 4:
                wh.append(f_.f_lineno)
                f_ = f_.f_back
            rec["where"] = wh
        ops.append(rec)
        for d in deps:
            if d[0] == "c":
                self.ops[d[1]][d[2]]["signal"] = True
        for k in w:
            self.lastw[k] = tok
            self.readers[k] = {}
        for k in r:
            rk = ("d", tok[1]) if tok[0] == "d" else tok[1]
            self.readers.setdefault(k, {})[rk] = tok
        return tok

    def fence(self):
        toks = set()
        for e in ENGS:
            for rec in reversed(self.ops[e]):
                if rec["tok"][0] == "c" and rec["fn"] is not None:
                    toks.add(rec["tok"])
                    rec["signal"] = True
                    break
        for k, c in self.dma_count.items():
            toks.add(("d", k, c))
        for e in ENGS:
            self.ops[e].append(dict(fn=None, deps=set(toks), tok=("c", e, len(self.ops[e])), signal=False))

    def dma(self, out, in_, r=(), w=(), key=None, eng="sp", **kw):
        self.op(eng, lambda e: e.dma_start(out=out, in_=in_, **kw), r=r, w=w, dma=key)

    def mm(self, out, lhsT, rhs, start, stop, r=(), w=()):
        self.op("pe", lambda e: e.matmul(out, lhsT, rhs, start=start, stop=stop), r=r, w=w)

    def tr(self, out, in_, ident, r=(), w=()):
        self.op("pe", lambda e: e.transpose(out, in_, ident), r=r, w=w)

    def act(self, out, in_, func, r=(), w=(), **kw):
        self.op("act", lambda e: e.activation(out, in_, func, **kw), r=r, w=w)

    def emit(self):
        nc = self.nc
        sems = {e: self.es.enter_context(nc.semaphore("s_" + e)) for e in ENGS}
        dsems = {}
        for i, k in enumerate(self.dma_count):
            dsems[k] = self.es.enter_context(nc.semaphore("d%d" % i))
        for e in ENGS:
            c = 0
            for rec in self.ops[e]:
                if rec["signal"] and rec["tok"][0] == "c":
                    c += 1
                rec["sigval"] = c
        final_counts = dict(self.dma_count)

        def run(engname, eng):
            waited = {}
            for rec in self.ops[engname]:
                need = {}
                for d in rec["deps"]:
                    if d[0] == "c":
                        s = ("c", d[1])
                        v = self.ops[d[1]][d[2]]["sigval"]
                    else:
                        s = ("d", d[1])
                        v = self.inc.get(d[1], 16) * d[2]
                    if need.get(s, 0) < v:
                        need[s] = v
                for s, v in need.items():
                    if waited.get(s, 0) >= v:
                        continue
                    waited[s] = v
                    eng.wait_ge(sems[s[1]] if s[0] == "c" else dsems[s[1]], v)
                if rec["fn"] is None:
                    continue
                try:
                    ins = rec["fn"](eng)
                except Exception:
                    print("EMIT FAILURE at lines", rec.get("where"), "engine", engname)
                    raise
                if rec["tok"][0] == "d":
                    ins.then_inc(dsems[rec["tok"][1]], self.inc.get(rec["tok"][1], 16))
                elif rec["signal"]:
                    ins.then_inc(sems[engname], 1)
            if engname == "sp":
                for k, c in final_counts.items():
                    v = self.inc.get(k, 16) * c
                    if waited.get(("d", k), 0) < v:
                        eng.wait_ge(dsems[k], v)

        with nc.Block() as block:
            @block.sync
            def _(e):
                run("sp", e)

            @block.tensor
            def _(e):
                run("pe", e)

            @block.scalar
            def _(e):
                run("act", e)

            @block.vector
            def _(e):
                run("dve", e)

            @block.gpsimd
            def _(e):
                run("pool", e)


def bc3(ap2, n):
    s = list(ap2.shape)
    return ap2.unsqueeze(2).to_broadcast([s[0], s[1], n])


def h3(ap2, h=NH):
    return ap2.rearrange("p (h d) -> p h d", h=h)


class Builder:
    def __init__(self, TP, taps=False):
        self.TP = TP
        self.NT = TP // 128
        self.taps = taps
        self.nc = bass.Bass("TRN2", target_bir_lowering=False)
        self.I = {}
        self.O = {}
        self.psi = 0
        self.pbi = 0
        self.tapnames = []
        self.pool = None
        self.pcnt = {}

    def din(self, n, s):
        self.I[n] = self.nc.dram_tensor(n, list(s), F32, kind="ExternalInput").ap()

    def dout(self, n, s):
        self.O[n] = self.nc.dram_tensor(n, list(s), F32, kind="ExternalOutput").ap()

    def declare(self):
        TP = self.TP
        for n, s in [("xp", (TP, D)), ("xs", (MS, D)), ("st_shift", (2, NS, RP)), ("st_wkv", (2, 128, 4096)),
                     ("ck", (2, NS, 128, 128)), ("cv", (2, NS, 128, 128)), ("st_conv", (2, 2 * NS, DFF)),
                     ("norm_mix_g", (2, D)), ("w_in", (2, D, INP)), ("rwkv_mu", (2, RP)), ("rwkv_w0", (2, RD)),
                     ("rwkv_w2", (2, 64, RD)), ("rwkv_a0", (2, RD)), ("rwkv_a2", (2, 64, RD)),
                     ("rwkv_g2", (2, 128, RD)), ("rwkv_k_k", (2, RD)), ("rwkv_k_a", (2, RD)),
                     ("rwkv_r_k", (2, RD)), ("rwkv_ln_g", (2, RD)), ("rwkv_ln_b", (2, RD)),
                     ("attn_sinks", (2, NH)), ("w_br_rwkv", (2, RD, D)), ("w_br_attn", (2, RD, D)),
                     ("w_out", (2, D, D)), ("norm_ffn_g", (2, D)), ("ffn_w_in", (2, D, 2 * DFF)),
                     ("ffn_conv_w", (2, 3, DFF)), ("ffn_conv_b", (2, DFF)), ("ffn_w_down", (2, DFF, D)),
                     ("norm_final_g", (D,)),
                     ("c_ident", (128, 128)), ("c_cosp", (TP, 32)), ("c_sinp", (TP, 32)),
                     ("c_coss", (MS, 32)), ("c_sins", (MS, 32)), ("c_tri", (128, 256)),
                     ("c_mask2", (128, 256)), ("c_maskL", (128, 128)), ("c_amask", (128, 768)),
                     ("c_smask", (32, 132)), ("c_last", (128, 1)),
                     ("xh0", (128, D)), ("c_cosh", (128, 32)), ("c_sinh", (128, 32)), ("c_amask0", (128, 256)), ("c_sel", (128, 8))]:
            self.din(n, s)
        for n, s in [("yp", (TP, D)), ("ys", (MS, D)), ("p_shift", (2, RP)), ("p_wkv", (2, NH, 64, 64)),
                     ("p_k", (2, 128, 128)), ("p_v", (2, 128, 128)), ("p_conv", (2, 2, DFF)),
                     ("s_shift", (2, NS, RP)), ("s_wkv", (2, 128, 4096)), ("s_k", (2, NS, 128, 128)),
                     ("s_v", (2, NS, 128, 128)), ("s_conv", (2, 2 * NS, DFF))]:
            self.dout(n, s)
        nc = self.nc
        self.xbuf = nc.dram_tensor("xbuf", [TP, D], F32).ap()
        self.xsbuf = nc.dram_tensor("xsbuf", [MS, D], F32).ap()
        self.mrbuf = nc.dram_tensor("mrbuf", [self.NT + 1, 128, 1024], BF).ap()
        self.xh_dram = nc.dram_tensor("xh_dram", [128, D], F32).ap()
        self.sq = nc.dram_tensor("sq", [6, MS, RD], F32).ap()
        self.sy = nc.dram_tensor("sy", [MS, RD], F32).ap()

    def alloc(self, name, shape, dt=F32):
        shape = list(shape)
        n = 1
        for d_ in shape[1:]:
            n *= d_
        nbytes = n * (4 if dt == F32 else 2)
        nw = (nbytes + 31) // 32 * 8
        off = self.aoff
        self.aoff += nw
        self.apeak = max(self.apeak, self.aoff)
        assert self.aoff <= self.ASZ, "SBUF arena overflow: %s needs %d words (limit %d)" % (name, self.aoff, self.ASZ)
        ap = self.arena[0:shape[0], off:off + nw]
        if dt != F32:
            ap = ap.bitcast(dt)
        ap = ap[:, 0:n]
        if len(shape) > 2:
            names = ["d%d" % i for i in range(len(shape) - 1)]
            pat = "p (%s) -> p %s" % (" ".join(names), " ".join(names))
            ap = ap.rearrange(pat, **{names[i]: shape[i + 1] for i in range(len(names))})
        return ap

    def release(self, mark):
        self.fw.fence()
        self.aoff = mark

    def pf(self):
        ids = {None: [0, 1, 2, 3, 4, 5], 0: [0, 1, 2], 1: [3, 4, 5]}[self.pool]
        c = self.pcnt.setdefault(("f", self.pool), 0)
        self.pcnt[("f", self.pool)] = c + 1
        k = ids[c % len(ids)]
        return self.PS[k], "ps%d" % k

    def pb(self):
        ids = {None: [0, 1], 0: [0], 1: [1]}[self.pool]
        c = self.pcnt.setdefault(("b", self.pool), 0)
        self.pcnt[("b", self.pool)] = c + 1
        k = ids[c % len(ids)]
        return self.PBK[k], "pb%d" % k

    def tap(self, name, ap, rkeys, dt=F32):
        if not self.taps:
            return
        shp = list(ap.shape)
        t = self.nc.dram_tensor("tap_" + name, shp, dt, kind="ExternalOutput").ap()
        self.tapnames.append("tap_" + name)
        self.fw.dma(t, ap, r=rkeys, key="tap_" + name)

    def V(self, fn, r=(), w=()):
        self.fw.op("dve", fn, r, w)

    def P(self, fn, r=(), w=()):
        self.fw.op("pool", fn, r, w)

    def col_load(self, dst, dkey, vec, n):
        fw = self.fw
        st = self.cstage
        fw.dma(st[0:n, :], vec.rearrange("(c p) -> c p", p=128), w=["cstage"], key="cstage")
        ps, pk = self.pf()
        fw.tr(ps[:, 0:n], st[0:n, :], self.identf[0:n, 0:n], r=["cstage", "identf"], w=[pk])
        fw.act(dst, ps[:, 0:n], AF.Copy, r=[pk], w=[dkey])

    def gather_select(self, src_ap, src_keys, n, ag_in, ag_out, name):
        fw = self.fw
        fw.dma(ag_in, src_ap, r=src_keys, w=[name + "_in"], key=name + "_st")
        self.gi = getattr(self, "gi", 0)
        ck = name + "_cc"
        fw.inc[ck] = 1
        fw.op("pool", lambda e: e.collective_compute("AllGather", ALU.bypass, replica_groups=[list(range(8))], ins=[ag_in], outs=[ag_out]),
              r=[name + "_in"], w=[name + "_out"], dma=ck)
        for r_ in range(8):
            st, sk = self.xt[r_ % 2], "xt%d" % (r_ % 2)
            fw.dma(st[:, 0:n], ag_out[r_ * 128:(r_ + 1) * 128, :], r=[name + "_out"], w=[sk], key=sk)
            if r_ == 0:
                self.V(lambda e, st=st: e.tensor_scalar(src_ap, st[:, 0:n], self.sel[:, 0:1], None, ALU.mult), r=[sk, "sel"], w=src_keys)
            else:
                self.V(lambda e, st=st, r_=r_: e.scalar_tensor_tensor(src_ap, st[:, 0:n], self.sel[:, r_:r_ + 1], src_ap, ALU.mult, ALU.add),
                       r=[sk, "sel"] + list(src_keys), w=src_keys)

    def bcast_load(self, dst, dkey, vec):
        self.fw.dma(dst, vec.partition_broadcast(dst.shape[0]), w=[dkey], key=dkey)

    def prep_w(self, nchunks, ncols, src, dst, dkey, mode, scale=None, mul=None, mulkey=None, sview=None):
        fw = self.fw
        for c in range(nchunks):
            for s0 in range(0, ncols, 2048):
                n = min(2048, ncols - s0)
                k = self.wst_i % 4
                self.wst_i += 1
                st = self.wstage[k]
                sk = "wst%d" % k
                fw.dma(st[:, 0:n], src(c, s0, n), w=[sk], key=sk)
                o = dst(c, s0, n)
                dk = dkey(c)
                if sview is not None:
                    sv_ = sview(st[:, 0:n])
                    sc = scale(c)
                    self.V(lambda eg, o=o, sv_=sv_, sc=sc: eg.tensor_scalar(o, sv_, sc, None, ALU.mult), r=[sk, "gcol"], w=[dk])
                    continue
                if mode == "plain":
                    e = ["dve", "pool", "act"][self.wst_i % 3]
                    if e == "act":
                        fw.act(o, st[:, 0:n], AF.Copy, r=[sk], w=[dk])
                    else:
                        fw.op(e, lambda eg, o=o, st=st, n=n: eg.tensor_copy(o, st[:, 0:n]), r=[sk], w=[dk])
                elif mode == "col":
                    sc = scale(c)
                    e = ["dve", "pool"][self.wst_i % 2]
                    fw.op(e, lambda eg, o=o, st=st, n=n, sc=sc: eg.tensor_scalar(o, st[:, 0:n], sc, None, ALU.mult),
                          r=[sk, "gcol"], w=[dk])
                else:
                    sc = scale(c)
                    m = mul(s0, n)
                    self.V(lambda eg, o=o, st=st, n=n, sc=sc, m=m: eg.scalar_tensor_tensor(
                        o, st[:, 0:n], sc, m, ALU.mult, ALU.mult), r=[sk, "gcol", mulkey], w=[dk])

    def norm_hT(self, xt, xk, M, hdst, hkey, identb):
        self.norm_a(xt, xk, M)
        self.norm_b(M, hdst, hkey, identb)

    def norm_a(self, xt, xk, M):
        fw = self.fw
        xn, ss, t1 = self.xn, self.ss, self.t1
        fw.act(xn[0:M, :], xt[0:M, :], AF.Square, r=[xk], w=["xn", "ss"], accum_out=ss[0:M, :])
        self.V(lambda e: e.tensor_scalar(t1[0:M, :], ss[0:M, :], 1.0 / D, 1e-6, ALU.mult, ALU.add), r=["ss"], w=["t1"])
        fw.act(t1[0:M, :], t1[0:M, :], AF.Sqrt, r=["t1"], w=["t1"])
        self.V(lambda e: e.reciprocal(t1[0:M, :], t1[0:M, :]), r=["t1"], w=["t1"])
        self.V(lambda e: e.tensor_scalar(xn[0:M, :], xt[0:M, :], t1[0:M, 0:1], None, ALU.mult), r=[xk, "t1"], w=["xn"])

    def norm_b(self, M, hdst, hkey, identb):
        fw = self.fw
        xn = self.xn
        pbk, pk = self.pb()
        for c in range(8):
            fw.tr(pbk[:, c * M:(c + 1) * M], xn[0:M, c * 128:(c + 1) * 128], identb[0:M, 0:M], r=["xn", "identb"], w=[pk])
        fw.act(hdst, pbk[:, 0:8 * M].rearrange("p (c t) -> p c t", c=8), AF.Copy, r=[pk], w=[hkey])

    def build(self):
        self.declare()
        nc = self.nc
        with ExitStack() as es:
            self.fw = fw = FW(nc, es)
            self.PS = [fw.ps("ps%d" % i, [128, 512], F32) for i in range(6)]
            self.PBK = [fw.ps("pb%d" % i, [128, 1024], BF) for i in range(2)]
            self.ASZ = 52224
            self.arena = fw.sb("arena", [128, self.ASZ])
            self.aoff = 0
            self.apeak = 0
            self.identf = self.alloc("identf", [128, 128])
            self.identb = self.alloc("identb", [128, 128], BF)
            self.cstage = self.alloc("cstage", [32, 128])
            self.wst_i = 0
            self.xn = self.alloc("xn", [128, D], BF)
            self.ss = self.alloc("ss", [128, 1])
            self.t1 = self.alloc("t1", [128, 1])
            self.gcol = self.alloc("gcol", [128, 8])
            self.xt = [self.alloc("xt%d" % i, [128, D]) for i in range(2)]
            self.sel = self.alloc("sel", [128, 8])
            fw.dma(self.sel[:], self.I["c_sel"], w=["sel"], key="sel")
            fw.dma(self.identf[:], self.I["c_ident"], w=["identf"], key="identf")
            self.V(lambda e: e.tensor_copy(self.identb[:], self.identf[:]), r=["identf"], w=["identb"])
            for l in range(2):
                for p_ in (self.pass_rwkv, self.pass_attn, self.pass_ffn):
                    mk_ = self.aoff
                    p_(l, None)
                    self.release(mk_)
            print("arena peak words", self.apeak, "of", self.ASZ)
            fw.emit()
        return nc

    def sbl(self, es2, name, shape, dt=F32):
        return self.alloc(name, shape, dt)

    def xsrc(self, l, i):
        if i < self.NT:
            src = self.I["xp"] if l == 0 else self.xbuf
            return src[i * 128:(i + 1) * 128, :], ("xb", i)
        src = self.I["xs"] if l == 0 else self.xsbuf
        return src, ("xb", i)

    def pass_rwkv(self, l, es2):
        fw, I, O, NT = self.fw, self.I, self.O, self.NT
        sbl = lambda n, s, dt=F32: self.sbl(es2, "r%d_" % l + n, s, dt)
        identb, identf = self.identb, self.identf
        W1 = sbl("W1", [128, 8, RP], BF)
        W2 = sbl("W2", [128, 8, RP], BF)
        Wg = sbl("Wg", [128, 8, D], BF)
        Wr = sbl("Wr", [128, 4, D], BF)
        lw2 = sbl("lw2", [128, RD], BF)
        lg2 = sbl("lg2", [128, RD], BF)
        bcs = {}
        for n in ["rwkv_w0", "rwkv_a0", "rwkv_k_k", "rwkv_k_a", "rwkv_r_k", "rwkv_ln_g", "rwkv_ln_b"]:
            bcs[n] = sbl(n, [128, RD])
            self.bcast_load(bcs[n][:], n + "_bc", I[n][l])
        mucol = sbl("mucol", [128, 2])
        tri = sbl("tri", [128, 256])
        mask2 = sbl("mask2", [128, 256])
        maskL = sbl("maskL", [128, 128])
        clast = sbl("clast", [128, 1])
        fw.dma(tri[:], I["c_tri"], w=["tri"], key="tri")
        fw.dma(mask2[:], I["c_mask2"], w=["mask2"], key="mask2")
        fw.dma(maskL[:], I["c_maskL"], w=["maskL"], key="maskL")
        fw.dma(clast[:], I["c_last"], w=["clast"], key="clast")
        self.col_load(self.gcol[:], "gcol", I["norm_mix_g"][l], 8)
        self.col_load(mucol[:], "mucol", I["rwkv_mu"][l, 1536:1792], 2)
        m0 = self.aoff
        self.wstage = [sbl("wst%d" % i_, [128, 2048]) for i_ in range(4)]
        mu_bc = sbl("mu_bc", [128, RP])
        omm_bc = sbl("omm_bc", [128, RP])
        self.bcast_load(mu_bc[:], "mu_bc", I["rwkv_mu"][l])
        self.V(lambda e: e.tensor_scalar(omm_bc[:], mu_bc[:], -1.0, 1.0, ALU.mult, ALU.add), r=["mu_bc"], w=["omm_bc"])
        win = I["w_in"][l]
        gsc = lambda c: self.gcol[:, c:c + 1]
        self.prep_w(8, RP, lambda c, s0, n: win[c * 128:(c + 1) * 128, s0:s0 + n],
                    lambda c, s0, n: W1[:, c, s0:s0 + n], lambda c: "W1_%d" % c, "colmul", gsc,
                    lambda s0, n: omm_bc[:, s0:s0 + n], "omm_bc")
        self.prep_w(8, RP, lambda c, s0, n: win[c * 128:(c + 1) * 128, s0:s0 + n],
                    lambda c, s0, n: W2[:, c, s0:s0 + n], lambda c: "W2_%d" % c, "colmul", gsc,
                    lambda s0, n: mu_bc[:, s0:s0 + n], "mu_bc")
        self.prep_w(8, D, lambda c, s0, n: win[c * 128:(c + 1) * 128, 2560 + s0:2560 + s0 + n],
                    lambda c, s0, n: Wg[:, c, s0:s0 + n], lambda c: "Wg_%d" % c, "col", gsc)
        wbr = I["w_br_rwkv"][l]
        self.prep_w(4, D, lambda c, s0, n: wbr[c * 128:(c + 1) * 128, s0:s0 + n],
                    lambda c, s0, n: Wr[:, c, s0:s0 + n], lambda c: "Wr_%d" % c, "plain")
        for (nm, p0, dk_) in [("rwkv_w2", 0, "lw2a"), ("rwkv_a2", 64, "lw2b")]:
            k = self.wst_i % 4
            self.wst_i += 1
            wsk = self.wstage[k]
            fw.dma(wsk[p0:p0 + 64, 0:RD], I[nm][l], w=["wst%d" % k], key="wst%d" % k)
            self.P(lambda e, wsk=wsk, p0=p0: e.tensor_copy(lw2[p0:p0 + 64, :], wsk[p0:p0 + 64, 0:RD]), r=["wst%d" % k], w=[dk_])
        self.prep_w(1, RD, lambda c, s0, n: I["rwkv_g2"][l], lambda c, s0, n: lg2[:, :], lambda c: "lg2", "plain")
        WK1 = ["W1_%d" % c for c in range(8)]
        WK2 = ["W2_%d" % c for c in range(8)]
        self.release(m0)
        class NSP:
            pass
        zr, zk = sbl("zr", [128, RD]), sbl("zk", [128, RD])
        lact = sbl("lact", [128, 128], BF)
        T = [sbl("tmp%d" % i_, [128, RD]) for i_ in range(8)]
        sm = sbl("sm", [128, 64])
        orT = sbl("orT", [128, 4, 128], BF)
        sgr = sbl("sgr", [128, 8, 128], BF)
        mrT0_ = sbl("mrT0", [128, 8, 128], BF)
        mrT = [mrT0_, mrT0_]
        TP_ = [sbl("tpost%d" % i_, [128, RD]) for i_ in range(2)]
        m1 = self.aoff
        NRB = 9864

        def mkrec(k):
            R = NSP()
            rb = sbl("RB%d" % k, [128, NRB], BF)
            rf = sbl("RF%d" % k, [128, 528])
            R.rb, R.rf, R.k = rb, rf, k
            R.RKT = rb[:, 0:1024].rearrange("p (j a t) -> p j a t", j=4, a=2)
            R.G4 = [rb[:, 1024 + j * 1280:1024 + (j + 1) * 1280].rearrange("p (h c) -> p h c", h=2) for j in range(4)]
            R.ZF = [rb[:, 6144 + j * 256:6144 + (j + 1) * 256].rearrange("p (h c) -> p h c", h=2) for j in range(4)]
            R.vb, R.ktt, R.bnt = rb[:, 7168:7680], rb[:, 7680:8192], rb[:, 8192:8704]
            R.sgT = rb[:, 8704:8832]
            R.hT = rb[:, 8832:9864].rearrange("p (c t) -> p c t", c=8)
            R.zv, R.WC, R.bon = rf[:, 0:512], rf[:, 512:516], rf[:, 516:524]
            R.K = (lambda k_: (lambda n: "%s#%d" % (n, k_)))(k)
            return R
        R0 = mkrec(0)
        U0b = [sbl("U0b%d" % j, [128, 2, 64], BF) for j in range(4)]
        Ub = sbl("Ub", [128, RD], BF)
        Nst = sbl("Nst", [128, 4, 128])
        Nb = sbl("Nb", [128, 4, 128], BF)
        self.V(lambda e: e.memset(Nst[:], 0.0), w=["Nst"])
        self.V(lambda e: e.memset(Nb[:], 0.0), w=["Nb"])
        m2 = self.aoff
        rt, kat = sbl("rt", [128, RD], BF), sbl("kat", [128, RD], BF)
        KT = sbl("KT", [128, 4, 128], BF)
        BT = sbl("BT", [128, 4, 128], BF)
        for j in range(4):
            self.P(lambda e, j=j: e.tensor_copy(R0.G4[j][:, :, 512:640], identb[:, :].unsqueeze(1).to_broadcast([128, 2, 128])),
                   r=["identb"], w=["G4_%d" % j])
        EZ = [[sbl("EZ%d_%d" % (j, a), [128, 2, 2, 128], BF) for a in range(2)] for j in range(4)]
        FFa = [sbl("FFa%d" % a, [128, 4, 2, 128], BF) for a in range(2)]
        FF = [[FFa[a][:, j] for a in range(2)] for j in range(4)]

        def tok_proj(M, hcur, hprev, hk, g0, dstkey):
            ps, pk = self.pf()
            n = 0
            for c in range(8):
                fw.mm(ps[0:M, :], hcur(c), W1[:, c, g0:g0 + 512], n == 0, False, r=[hk, WK1[c]], w=[pk])
                n += 1
            for c in range(8):
                fw.mm(ps[0:M, :], hprev(c), W2[:, c, g0:g0 + 512], False, c == 7, r=[hk, WK2[c]], w=[pk])
            return ps, pk

        def feat_proj(M, hcur, hprev, hk, g0):
            ps, pk = self.pf()
            for c in range(8):
                fw.mm(ps[:, 0:M], W1[:, c, g0:g0 + 128], hcur(c), c == 0, False, r=[hk, WK1[c]], w=[pk])
            for c in range(8):
                fw.mm(ps[:, 0:M], W2[:, c, g0:g0 + 128], hprev(c), False, c == 7, r=[hk, WK2[c]], w=[pk])
            return ps, pk

        def raw_last(hl, hk, M, dst):
            for gi, g0 in enumerate(range(0, RP, 512)):
                n = min(512, RP - g0)
                ps, pk = self.pf()
                for c in range(8):
                    fw.mm(ps[0:M, 0:n], hl(c), W1[:, c, g0:g0 + n], c == 0, False, r=[hk, WK1[c]], w=[pk])
                for c in range(8):
                    fw.mm(ps[0:M, 0:n], hl(c), W2[:, c, g0:g0 + n], False, c == 7, r=[hk, WK2[c]], w=[pk])
                fw.act(T[gi][0:M, 0:n], ps[0:M, 0:n], AF.Copy, r=[pk], w=["T%d" % gi])
                fw.dma(dst[:, g0:g0 + n], T[gi][0:M, 0:n], r=["T%d" % gi], key="zl%d" % gi)

        def prep(M, sample, R):
            K = R.K
            w0, a0 = bcs["rwkv_w0"], bcs["rwkv_a0"]
            kkb, kab, rkb = bcs["rwkv_k_k"], bcs["rwkv_k_a"], bcs["rwkv_r_k"]
            pw, pwk = self.pf()
            fw.mm(pw[0:M, :], lact[0:64, 0:M], lw2[0:64, :], True, True, r=["lact", "lw2a"], w=[pwk])
            pa, pak = self.pf()
            fw.mm(pa[0:M, :], lact[64:128, 0:M], lw2[64:128, :], True, True, r=["lact", "lw2b"], w=[pak])
            sg, a_, kk, t3, kf, be = T[0], T[1], T[2], T[3], T[4], T[5]
            self.V(lambda e: e.tensor_tensor(sg[0:M, :], pw[0:M, :], w0[0:M, :], ALU.add), r=[pwk, "rwkv_w0_bc"], w=["T0"])
            fw.act(sg[0:M, :], sg[0:M, :], AF.Sigmoid, r=["T0"], w=["T0"])
            self.V(lambda e: e.tensor_tensor(a_[0:M, :], pa[0:M, :], a0[0:M, :], ALU.add), r=[pak, "rwkv_a0_bc"], w=["T1"])
            fw.act(a_[0:M, :], a_[0:M, :], AF.Sigmoid, r=["T1"], w=["T1"])
            self.P(lambda e: e.tensor_tensor(kk[0:M, :], zk[0:M, :], kkb[0:M, :], ALU.mult), r=["zk", "rwkv_k_k_bc"], w=["T2"])
            self.P(lambda e: e.tensor_tensor(t3[0:M, :], kk[0:M, :], kk[0:M, :], ALU.mult), r=["T2"], w=["T3"])
            self.V(lambda e: e.tensor_reduce(sm[0:M, 0:8], h3(t3[0:M, :]), AX.X, ALU.add), r=["T3"], w=["sm0"])
            fw.act(sm[0:M, 0:8], sm[0:M, 0:8], AF.Sqrt, r=["sm0"], w=["sm0"])
            self.V(lambda e: e.tensor_scalar(sm[0:M, 0:8], sm[0:M, 0:8], 1e-12, None, ALU.max), r=["sm0"], w=["sm0"])
            self.V(lambda e: e.reciprocal(sm[0:M, 0:8], sm[0:M, 0:8]), r=["sm0"], w=["sm0"])
            self.V(lambda e: e.tensor_tensor(h3(kk[0:M, :]), h3(kk[0:M, :]), bc3(sm[0:M, 0:8], 64), ALU.mult),
                   r=["T2", "sm0"], w=["T2"])
            self.V(lambda e: e.scalar_tensor_tensor(t3[0:M, :], a_[0:M, :], -1.0, kab[0:M, :], ALU.add, ALU.mult),
                   r=["T1", "rwkv_k_a_bc"], w=["T3"])
            self.V(lambda e: e.scalar_tensor_tensor(kf[0:M, :], t3[0:M, :], 1.0, zk[0:M, :], ALU.add, ALU.mult),
                   r=["T3", "zk"], w=["T4"])
            self.P(lambda e: e.tensor_tensor(be[0:M, :], kk[0:M, :], a_[0:M, :], ALU.mult), r=["T2", "T1"], w=["T5"])
            self.P(lambda e: e.tensor_tensor(t3[0:M, :], zr[0:M, :], kf[0:M, :], ALU.mult), r=["zr", "T4"], w=["T3"])
            self.P(lambda e: e.tensor_tensor(t3[0:M, :], t3[0:M, :], rkb[0:M, :], ALU.mult), r=["T3", "rwkv_r_k_bc"], w=["T3"])
            self.V(lambda e, R=R: e.tensor_reduce(R.bon[0:M, :], h3(t3[0:M, :]), AX.X, ALU.add), r=["T3"], w=[K("bon")])
            if sample:
                fw.act(T[6][0:M, :], sg[0:M, :], AF.Exp, r=["T0"], w=["T6"], scale=CDEC)
                for x, (tl, tk) in enumerate([(zr, "zr"), (T[6], "T6"), (kf, "T4"), (R.zv, K("zv")), (kk, "T2"), (be, "T5")]):
                    fw.dma(self.sq[x], tl[0:M, :], r=[tk], w=[("sq", x)], key="sqw%d" % x)
                return
            pli, plik = self.pf()
            fw.mm(pli[:, :], tri[:, 0:128], sg[:, :], True, True, r=["tri", "T0"], w=[plik])
            ple, plek = self.pf()
            fw.mm(ple[:, :], tri[:, 128:256], sg[:, :], True, True, r=["tri", "T0"], w=[plek])
            eL, eLm, enL = T[6], T[7], T[3]
            fw.act(eL[:, :], pli[:, :], AF.Exp, r=[plik], w=["T6"])
            fw.act(eLm[:, :], ple[:, :], AF.Exp, r=[plek], w=["T7"])
            fw.act(enL[:, :], pli[:, :], AF.Exp, r=[plik], w=["T3"], scale=-1.0)
            self.V(lambda e: e.tensor_tensor(rt[:, :], zr[:, :], eL[:, :], ALU.mult), r=["zr", "T6"], w=["rt"])
            self.V(lambda e: e.tensor_tensor(kat[:, :], kk[:, :], eLm[:, :], ALU.mult), r=["T2", "T7"], w=["kat"])
            self.P(lambda e, R=R: e.tensor_tensor(R.ktt[:, :], kf[:, :], enL[:, :], ALU.mult), r=["T4", "T3"], w=[K("ktt")])
            self.V(lambda e, R=R: e.scalar_tensor_tensor(R.bnt[:, :], be[:, :], -1.0, enL[:, :], ALU.mult, ALU.mult),
                   r=["T5", "T3"], w=[K("bnt")])
            fw.act(R.vb[:, :], R.zv[:, :], AF.Copy, r=[K("zv")], w=[K("vb")])
            pwc, pwck = self.pf()
            for j in range(4):
                fw.mm(pwc[:, j:j + 1], eL[:, j * 128:(j + 1) * 128], clast[:, :], True, True, r=["T6", "clast"], w=[pwck])
            fw.act(R.WC[:, :], pwc[:, 0:4], AF.Copy, r=[pwck], w=[K("WC")])
            for (src, skey, dstf, dk) in [(rt, "rt", None, "RKT"), (kat, "kat", None, "RKT"),
                                          (R.ktt, K("ktt"), None, "KT"), (R.bnt, K("bnt"), None, "BT")]:
                pbk, pk = self.pb()
                for j in range(4):
                    fw.tr(pbk[:, j * 128:(j + 1) * 128], src[:, j * 128:(j + 1) * 128], identb[:, :], r=[skey, "identb"], w=[pk])
                if dk == "RKT":
                    which = 0 if skey == "rt" else 1
                    fw.act(R.RKT[:, :, which, :], pbk[:, 0:512].rearrange("p (j t) -> p j t", j=4), AF.Copy, r=[pk], w=["RKT%d" % which])
                else:
                    dst = KT if dk == "KT" else BT
                    self.V(lambda e, dst=dst, pbk=pbk: e.tensor_copy(dst[:, :, :], pbk[:, 0:512].rearrange("p (j t) -> p j t", j=4)),
                           r=[pk], w=[dk])

        def stageAB(R):
            K = R.K
            RK = [K("RKT0"), K("RKT1")]
            RKT, G4, ZF = R.RKT, R.G4, R.ZF
            zb = [self.pf(), self.pf()]
            for j in range(4):
                for hh in range(2):
                    o = hh * 64
                    pZ, pzk = zb[hh]
                    fw.mm(pZ[:, j * 128:(j + 1) * 128], RKT[o:o + 64, j, 1, :], BT[o:o + 64, j, :], True, True, r=["BT", K("RKT1")], w=[pzk])
            mlb = maskL[:, :].unsqueeze(1).to_broadcast([128, 4, 128])
            for hh in range(2):
                pZ, pzk = zb[hh]
                self.V(lambda e, pZ=pZ, hh=hh: e.tensor_tensor(FFa[0][:, :, hh, :], pZ[:, :].rearrange("p (j c) -> p j c", j=4), mlb, ALU.mult),
                       r=[pzk, "maskL"], w=["FF%d_0" % j for j in range(4)])
            for j in range(4):
                bk = [self.pf(), self.pf()]
                for hh in range(2):
                    o = hh * 64
                    ps, pk = bk[hh]
                    rhs = RKT[o:o + 64, j, :, :].rearrange("p a t -> p (a t)")
                    fw.mm(ps[:, 0:256], KT[o:o + 64, j, :], rhs, True, True, r=["KT"] + RK, w=[pk])
                    fw.mm(ps[:, 256:512], BT[o:o + 64, j, :], rhs, True, True, r=["BT"] + RK, w=[pk])
                for hh in range(2):
                    ps, pk = bk[hh]
                    self.V(lambda e, j=j, hh=hh, ps=ps, G4=G4: e.tensor_tensor(
                        G4[j][:, hh, 0:512].rearrange("p (a c) -> p a c", a=2), ps[:, :].rearrange("p (a c) -> p a c", a=2),
                        mask2[:, :].unsqueeze(1).to_broadcast([128, 2, 256]), ALU.mult), r=[pk, "mask2"], w=[K("G4_%d" % j)])
            for lev in range(7):
                a, b = lev % 2, (lev + 1) % 2
                for j in range(4):
                    fk, fn_ = "FF%d_%d" % (j, a), "FF%d_%d" % (j, b)
                    ezn = "EZ%d_%d" % (j, b)
                    if lev == 0:
                        ezk = K("G4_%d" % j)
                        EZs = lambda hh, j=j, G4=G4: G4[j][:, hh, 384:640]
                        Es = lambda hh, j=j, G4=G4: G4[j][:, hh, 384:512]
                        Zs = lambda j=j, G4=G4: G4[j][:, :, 512:640]
                    else:
                        ezk = "EZ%d_%d" % (j, a)
                        EZs = lambda hh, j=j, a=a: EZ[j][a][:, hh, :, :].rearrange("p a t -> p (a t)")
                        Es = lambda hh, j=j, a=a: EZ[j][a][:, hh, 0, :]
                        Zs = lambda j=j, a=a: EZ[j][a][:, :, 1, :]
                    if lev < 6:
                        pL, plk = self.pf()
                        for hh in range(2):
                            fw.mm(pL[:, hh * 256:(hh + 1) * 256], FF[j][a][:, hh, :], EZs(hh), True, True, r=[ezk, fk], w=[plk])
                        pF, pfk = self.pf()
                        for hh in range(2):
                            fw.mm(pF[:, hh * 128:(hh + 1) * 128], Es(hh), FF[j][a][:, hh, :], True, True, r=[ezk, fk], w=[pfk])
                        l3 = pL[:, :].rearrange("p (h c) -> p h c", h=2)
                        fw.act(EZ[j][b][:, :, 0, :], l3[:, :, 0:128], AF.Copy, r=[plk], w=[ezn])
                        self.V(lambda e, j=j, b=b, l3=l3, Zs=Zs: e.tensor_tensor(EZ[j][b][:, :, 1, :], l3[:, :, 128:256], Zs(), ALU.add),
                               r=[plk, ezk], w=[ezn])
                        fw.act(FF[j][b][:, :, :], pF[:, 0:256].rearrange("p (h c) -> p h c", h=2), AF.Copy, r=[pfk], w=[fn_])
                    else:
                        pL, plk = self.pf()
                        for hh in range(2):
                            fw.mm(pL[:, hh * 128:(hh + 1) * 128], FF[j][a][:, hh, :], EZ[j][a][:, hh, 1, :], True, True, r=[ezk, fk], w=[plk])
                        self.V(lambda e, j=j, a=a, pL=pL, ZF=ZF: e.tensor_tensor(ZF[j][:, :, :], pL[:, 0:256].rearrange("p (h c) -> p h c", h=2),
                                                                      EZ[j][a][:, :, 1, :], ALU.add), r=[plk, ezk], w=[K("ZF%d" % j)])

        def stageC(R):
            K = R.K
            RKT, G4, ZF, vb = R.RKT, R.G4, R.ZF, R.vb
            for j in range(4):
                pU, puk = self.pf()
                for hh in range(2):
                    o, h = hh * 64, 2 * j + hh
                    fw.mm(pU[:, hh * 64:(hh + 1) * 64], RKT[o:o + 64, j, 1, :], Nb[o:o + 64, j, o:o + 64], True, False, r=[K("RKT1"), "Nb"], w=[puk])
                    fw.mm(pU[:, hh * 64:(hh + 1) * 64], G4[j][:, hh, 128:256], vb[:, h * 64:(h + 1) * 64], False, True, r=[K("G4_%d" % j), K("vb")], w=[puk])
                fw.act(U0b[j][:, :, :], pU[:, 0:128].rearrange("p (h c) -> p h c", h=2), AF.Copy, r=[puk], w=["U0b%d" % j])
            for j in range(4):
                pU, puk = self.pf()
                for hh in range(2):
                    fw.mm(pU[:, hh * 64:(hh + 1) * 64], ZF[j][:, hh, :], U0b[j][:, hh, :], True, True, r=[K("ZF%d" % j), "U0b%d" % j], w=[puk])
                fw.act(Ub[:, j * 128:(j + 1) * 128], pU[:, 0:128], AF.Copy, r=[puk], w=["Ub%d" % j])

        def stageD(R):
            K = R.K
            RKT, G4, vb = R.RKT, R.G4, R.vb
            psY, pyk = self.pf()
            for j in range(4):
                for hh in range(2):
                    o, h = hh * 64, 2 * j + hh
                    fw.mm(psY[:, h * 64:(h + 1) * 64], RKT[o:o + 64, j, 0, :], Nb[o:o + 64, j, o:o + 64], True, False, r=[K("RKT0"), "Nb"], w=[pyk])
                    fw.mm(psY[:, h * 64:(h + 1) * 64], G4[j][:, hh, 0:128], vb[:, h * 64:(h + 1) * 64], False, False, r=[K("G4_%d" % j), K("vb")], w=[pyk])
                    fw.mm(psY[:, h * 64:(h + 1) * 64], G4[j][:, hh, 256:384], Ub[:, h * 64:(h + 1) * 64], False, True, r=[K("G4_%d" % j), "Ub%d" % j], w=[pyk])
            return psY, pyk

        def n_update(R):
            K = R.K
            ktt, bnt, vb, WC = R.ktt, R.bnt, R.vb, R.WC
            pN, pnk = self.pf()
            for j in range(4):
                fw.mm(pN[:, j * 128:(j + 1) * 128], ktt[:, j * 128:(j + 1) * 128], vb[:, j * 128:(j + 1) * 128], True, False, r=[K("ktt"), K("vb")], w=[pnk])
                fw.mm(pN[:, j * 128:(j + 1) * 128], bnt[:, j * 128:(j + 1) * 128], Ub[:, j * 128:(j + 1) * 128], False, True, r=[K("bnt"), "Ub%d" % j], w=[pnk])
            n2 = Nst[:, :, :].rearrange("p j c -> p (j c)")
            self.V(lambda e: e.tensor_tensor(n2, pN[:, :], n2, ALU.add), r=[pnk, "Nst"], w=["Nst"])
            self.V(lambda e, WC=WC: e.tensor_tensor(Nst[:, :, :], Nst[:, :, :], bc3(WC[:, :], 128), ALU.mult), r=["Nst", K("WC")], w=["Nst"])
            fw.act(Nb[:, :, :], Nst[:, :, :], AF.Copy, r=["Nst"], w=["Nb"])


        def post(M, yap, ykeys, pg, pgk, R):
            K = R.K
            lng, lnb = bcs["rwkv_ln_g"], bcs["rwkv_ln_b"]
            y2, yc = TP_[0], TP_[1]
            ob = TP_[0].bitcast(BF)[:, 0:RD]
            self.V(lambda e: e.tensor_reduce(sm[0:M, 16:24], h3(yap), AX.X, ALU.add), r=ykeys, w=["sm2"])
            fw.act(y2[0:M, :], yap, AF.Square, r=ykeys, w=["TP0"])
            self.V(lambda e: e.tensor_reduce(sm[0:M, 24:32], h3(y2[0:M, :]), AX.X, ALU.add), r=["TP0"], w=["sm3"])
            mean, var = sm[0:M, 16:24], sm[0:M, 24:32]
            self.V(lambda e: e.tensor_scalar(mean, mean, 1.0 / 64, None, ALU.mult), r=["sm2"], w=["sm2"])
            self.V(lambda e: e.tensor_tensor(sm[0:M, 32:40], mean, mean, ALU.mult), r=["sm2"], w=["sm4"])
            self.V(lambda e: e.scalar_tensor_tensor(var, var, 1.0 / 64, sm[0:M, 32:40], ALU.mult, ALU.subtract), r=["sm3", "sm4"], w=["sm3"])
            self.V(lambda e: e.tensor_scalar(var, var, 64e-5, None, ALU.add), r=["sm3"], w=["sm3"])
            fw.act(var, var, AF.Sqrt, r=["sm3"], w=["sm3"])
            self.V(lambda e: e.reciprocal(var, var), r=["sm3"], w=["sm3"])
            self.V(lambda e: e.tensor_tensor(h3(yc[0:M, :]), h3(yap), bc3(mean, 64), ALU.subtract), r=list(ykeys) + ["sm2"], w=["TP1"])
            self.V(lambda e: e.tensor_tensor(h3(yc[0:M, :]), h3(yc[0:M, :]), bc3(var, 64), ALU.mult), r=["TP1", "sm3"], w=["TP1"])
            self.P(lambda e: e.tensor_tensor(yc[0:M, :], yc[0:M, :], lng[0:M, :], ALU.mult), r=["TP1", "rwkv_ln_g_bc"], w=["TP1"])
            self.P(lambda e: e.tensor_tensor(yc[0:M, :], yc[0:M, :], lnb[0:M, :], ALU.add), r=["TP1", "rwkv_ln_b_bc"], w=["TP1"])
            self.P(lambda e, R=R: e.tensor_tensor(h3(y2[0:M, :]), h3(R.zv[0:M, :]), bc3(R.bon[0:M, :], 64), ALU.mult), r=[K("zv"), K("bon")], w=["TP0"])
            self.V(lambda e: e.tensor_tensor(yc[0:M, :], yc[0:M, :], y2[0:M, :], ALU.add), r=["TP1", "TP0"], w=["TP1"])
            self.V(lambda e: e.tensor_tensor(ob[0:M, :], yc[0:M, :], pg[0:M, :], ALU.mult), r=["TP1", pgk], w=["TP0"])
            pbk, pk = self.pb()
            for j in range(4):
                fw.tr(pbk[:, j * M:(j + 1) * M], ob[0:M, j * 128:(j + 1) * 128], identb[0:M, 0:M], r=["TP0", "identb"], w=[pk])
            fw.act(orT[:, :, 0:M], pbk[:, 0:4 * M].rearrange("p (j t) -> p j t", j=4), AF.Copy, r=[pk], w=["orT"])

        def gate_branch(M, hcur, hk, mdst, mkey):
            for half in range(2):
                pg, pgk = self.pf()
                for q in range(4):
                    dc = half * 4 + q
                    for c in range(8):
                        fw.mm(pg[:, q * M:(q + 1) * M], Wg[:, c, dc * 128:(dc + 1) * 128], hcur(c), c == 0, c == 7, r=[hk, "Wg_%d" % c], w=[pgk])
                fw.act(sgr[:, half * 4:(half + 1) * 4, 0:M], pg[:, 0:4 * M].rearrange("p (q t) -> p q t", q=4), AF.Sigmoid, r=[pgk], w=["sgr%d" % half])
                pbr, pbk_ = self.pf()
                for q in range(4):
                    dc = half * 4 + q
                    for j in range(4):
                        fw.mm(pbr[:, q * M:(q + 1) * M], Wr[:, j, dc * 128:(dc + 1) * 128], orT[:, j, 0:M], j == 0, j == 3, r=["orT", "Wr_%d" % j], w=[pbk_])
                self.V(lambda e, half=half, pbr=pbr: e.tensor_tensor(mdst[:, half * 4:(half + 1) * 4, 0:M], sgr[:, half * 4:(half + 1) * 4, 0:M],
                                                                 pbr[:, 0:4 * M].rearrange("p (q t) -> p q t", q=4), ALU.mult),
                       r=["sgr%d" % half, pbk_], w=[mkey])

        R1 = mkrec(1)
        for j in range(4):
            self.P(lambda e, j=j: e.tensor_copy(R1.G4[j][:, :, 512:640], identb[:, :].unsqueeze(1).to_broadcast([128, 2, 128])),
                   r=["identb"], w=[R1.K("G4_%d" % j)])
        RR = [R0, R1]

        def H1a(i):
            R, Rp = RR[i % 2], RR[(i + 1) % 2]
            hT = R.hT
            xt, xk = self.xt[i % 2], "xt%d" % (i % 2)
            src, _ = self.xsrc(l, i)
            fw.dma(xt[:], src, r=[("xb", i)], w=[xk], key=xk)
            hk = R.K("hTr")
            if i == 0:
                self.V(lambda e, hT=hT: e.memset(hT[:, :, 0:1], 0.0), w=[hk])
            else:
                self.P(lambda e, hT=hT, hp=Rp.hT: e.tensor_copy(hT[:, :, 0:1], hp[:, :, 128:129]), r=[Rp.K("hTr")], w=[hk])
            self.norm_a(xt, xk, 128)

        def H1b(i):
            R = RR[i % 2]
            K = R.K
            hT = R.hT
            hk = K("hTr")
            self.norm_b(128, hT[:, :, 1:129], hk, identb)
            hcur = lambda c, hT=hT: hT[:, c, 1:129]
            hprev = lambda c, hT=hT: hT[:, c, 0:128]
            for g0, dst, dk in [(0, zr, "zr"), (512, zk, "zk"), (1024, R.zv, K("zv"))]:
                ps, pk = tok_proj(128, hcur, hprev, hk, g0, dk)
                fw.act(dst[:, :], ps[:, :], AF.Copy, r=[pk], w=[dk])
            ps, pk = feat_proj(128, hcur, hprev, hk, 1536)
            fw.act(lact[0:64, :], ps[0:64, 0:128], AF.Tanh, r=[pk], w=["lact"])
            fw.act(lact[64:128, :], ps[64:128, 0:128], AF.Copy, r=[pk], w=["lact"])
            ps, pk = feat_proj(128, hcur, hprev, hk, 1664)
            fw.act(R.sgT[:, :], ps[:, 0:128], AF.Sigmoid, r=[pk], w=[K("sgT")])
            if i == NT - 1:
                raw_last(lambda c, hT=hT: hT[:, c, 128:129], hk, 1, O["p_shift"][l:l + 1, :])

        def H1c(i):
            prep(128, False, RR[i % 2])

        def H1d(i):
            stageAB(RR[i % 2])

        H2st = {}

        def H2a(i):
            R = RR[i % 2]
            stageC(R)
            psY, pyk = stageD(R)
            n_update(R)
            pg, pgk = self.pf()
            fw.mm(pg[:, :], R.sgT[:, :], lg2[:, :], True, True, r=[R.K("sgT"), "lg2"], w=[pgk])
            H2st[i] = (psY, pyk, pg, pgk)

        def H2b(i):
            psY, pyk, pg, pgk = H2st.pop(i)
            post(128, psY[:, :], [pyk], pg, pgk, RR[i % 2])

        def H2c(i):
            R = RR[i % 2]
            m, mk = mrT[0], "mrT0"
            gate_branch(128, lambda c, R=R: R.hT[:, c, 1:129], R.K("hTr"), m, mk)
            fw.dma(self.mrbuf[i].rearrange("p (c t) -> p c t", c=8), m[:, :, :], r=[mk], w=[("mr", i)], key=mk)

        def cap(pool, f, i):
            self.pool = pool
            return fw.capture(lambda: f(i))

        for f in (H1a, H1b, H1c):
            fw.replay([cap(0, f, 0)])
        fw.replay([cap(None, H1d, 0)])
        for i in range(NT):
            nx = i + 1 < NT
            if nx:
                fw.replay([cap(0, H1a, i + 1)])
            fw.replay([cap(1, H2a, i)])
            fw.replay(([cap(0, H1b, i + 1)] if nx else []) + [cap(1, H2b, i)])
            fw.replay(([cap(0, H1c, i + 1)] if nx else []) + [cap(1, H2c, i)])
            if nx:
                fw.replay([cap(None, H1d, i + 1)])
        self.pool = None
        for j in range(4):
            ps, pk = self.pf()
            fw.tr(ps[:, 0:128], Nst[:, j, :], identf[:, :], r=["Nst", "identf"], w=[pk])
            fw.act(T[0][:, j * 128:(j + 1) * 128], ps[:, 0:128], AF.Copy, r=[pk], w=["T0"])
        for h_ in range(8):
            j, o = h_ // 2, (h_ % 2) * 64
            fw.dma(O["p_wkv"][l, h_], T[0][o:o + 64, j * 128 + o:j * 128 + o + 64], r=["T0"], key="T0")

        self.release(m1)
        RS = NSP()
        RS.zv = sbl("zv_s", [128, RD])
        RS.sgT = sbl("sgT_s", [128, 128], BF)
        RS.bon = sbl("bon_s", [128, 8])
        RS.K = lambda n: n + "#s"
        hTs = sbl("hTs", [128, 8, 80], BF)
        sadd = sbl("sadd", [16, RP])
        stT = sbl("stT", [128, 2, 16])
        zf = sbl("zf", [128, 2, 64])
        QH = sbl("QH", [128, 6, 4, 64])
        Sst = sbl("Sst", [128, 64, 64])
        Stmp = sbl("Stmp", [128, 64, 64])
        sk = sbl("sk", [128, 64])
        yh = sbl("yh", [128, 4, 64])
        ytm = T[7]
        self.V(lambda e: e.memset(hTs[:], 0.0), w=["hTs"])
        i = NT
        xt, xk = self.xt[i % 2], "xt%d" % (i % 2)
        src, _ = self.xsrc(l, i)
        fw.dma(xt[0:MS, :], src, r=[("xb", i)], w=[xk], key=xk)
        self.norm_hT(xt, xk, MS, hTs[:, :, 16:80], "hTs", identb)
        hcur = lambda c: hTs[:, c, 16:80]
        hprev = lambda c: hTs[:, c, 0:64]
        fw.dma(sadd[:, :], I["st_shift"][l], w=["sadd"], key="sadd")
        for q in range(2):
            ps, pk = self.pf()
            fw.tr(ps[:, 0:16], sadd[0:16, 1536 + q * 128:1536 + (q + 1) * 128], identf[0:16, 0:16], r=["sadd", "identf"], w=[pk])
            self.V(lambda e, q=q, ps=ps: e.tensor_scalar(stT[:, q, :], ps[:, 0:16], mucol[:, q:q + 1], None, ALU.mult), r=[pk, "mucol"], w=["stT"])
        for gi, g0 in enumerate(range(0, RP, 512)):
            n = min(512, RP - g0)
            self.bcast_load(T[4 + gi][0:16, 0:n], "T%d" % (4 + gi), I["rwkv_mu"][l, g0:g0 + n])
            self.V(lambda e, gi=gi, g0=g0, n=n: e.tensor_tensor(sadd[:, g0:g0 + n], sadd[:, g0:g0 + n], T[4 + gi][0:16, 0:n], ALU.mult),
                   r=["sadd", "T%d" % (4 + gi)], w=["sadd"])
        zv, sgT = RS.zv, RS.sgT
        for g0, dst, dk in [(0, zr, "zr"), (512, zk, "zk"), (1024, zv, RS.K("zv"))]:
            ps, pk = tok_proj(MS, hcur, hprev, "hTs", g0, dk)
            fw.act(dst[0:MS, :], ps[0:MS, :], AF.Copy, r=[pk], w=[dk])
            self.V(lambda e, dst=dst, g0=g0: e.tensor_tensor(dst[0:16, :], dst[0:16, :], sadd[0:16, g0:g0 + 512], ALU.add), r=[dk, "sadd"], w=[dk])
        for q, g0 in enumerate([1536, 1664]):
            ps, pk = feat_proj(MS, hcur, hprev, "hTs", g0)
            fw.act(zf[:, q, :], ps[:, 0:MS], AF.Copy, r=[pk], w=["zf"])
            self.V(lambda e, q=q: e.tensor_tensor(zf[:, q, 0:16], zf[:, q, 0:16], stT[:, q, :], ALU.add), r=["zf", "stT"], w=["zf"])
        fw.act(lact[0:64, 0:MS], zf[0:64, 0, :], AF.Tanh, r=["zf"], w=["lact"])
        fw.act(lact[64:128, 0:MS], zf[64:128, 0, :], AF.Copy, r=["zf"], w=["lact"])
        fw.act(sgT[:, 0:MS], zf[:, 1, :], AF.Sigmoid, r=["zf"], w=[RS.K("sgT")])
        prep(MS, True, RS)
        if l == 0:
            for nm, ap, k in [("s_zr", zr, "zr"), ("s_zk", zk, "zk"), ("s_zv", zv, "zv"), ("s_dec", T[6], "T6"), ("s_kk", T[2], "T2"),
                              ("s_kf", T[4], "T4"), ("s_be", T[5], "T5"), ("s_a", T[1], "T1")]:
                self.tap(nm, ap[0:MS, :], [k])
        sqv = self.sq.rearrange("x (t q) (h d) -> (q h) x t d", t=4, h=NH)
        for x in range(6):
            fw.dma(QH[:, x, :, :], sqv[:, x, :, :], r=[("sq", x)], w=["QH"], key="QH")
        fw.dma(Sst[:, :, :].rearrange("p v k -> p (v k)"), I["st_wkv"][l], w=["Sst"], key="Sst")
        for t in range(4):
            r_, w_, k_, v_, kk_, b_ = (QH[:, x, t, :] for x in range(6))
            rowb = lambda a: a.unsqueeze(1).to_broadcast([128, 64, 64])
            colb = lambda a: a.unsqueeze(2).to_broadcast([128, 64, 64])
            self.V(lambda e, kk_=kk_: e.tensor_tensor(Stmp[:, :, :], Sst[:, :, :], rowb(kk_), ALU.mult), r=["Sst", "QH"], w=["Stmp"])
            self.V(lambda e: e.tensor_reduce(sk[:, :], Stmp[:, :, :], AX.X, ALU.add), r=["Stmp"], w=["sk"])
            self.P(lambda e, w_=w_: e.tensor_tensor(Sst[:, :, :], Sst[:, :, :], rowb(w_), ALU.mult), r=["Sst", "QH", "Stmp"], w=["Sst"])
            self.V(lambda e, b_=b_: e.tensor_tensor(Stmp[:, :, :], colb(sk[:, :]), rowb(b_), ALU.mult), r=["sk", "QH"], w=["Stmp"])
            self.V(lambda e: e.tensor_tensor(Sst[:, :, :], Sst[:, :, :], Stmp[:, :, :], ALU.subtract), r=["Sst", "Stmp"], w=["Sst"])
            self.P(lambda e, v_=v_, k_=k_: e.tensor_tensor(Stmp[:, :, :], colb(v_), rowb(k_), ALU.mult), r=["QH", "Sst"], w=["Stmp"])
            self.V(lambda e: e.tensor_tensor(Sst[:, :, :], Sst[:, :, :], Stmp[:, :, :], ALU.add), r=["Sst", "Stmp"], w=["Sst"])
            self.P(lambda e, r_=r_: e.tensor_tensor(Stmp[:, :, :], Sst[:, :, :], rowb(r_), ALU.mult), r=["Sst", "QH"], w=["Stmp"])
            self.V(lambda e, t=t: e.tensor_reduce(yh[:, t, :], Stmp[:, :, :], AX.X, ALU.add), r=["Stmp"], w=["yh"])
        fw.dma(O["s_wkv"][l], Sst[:, :, :].rearrange("p v k -> p (v k)"), r=["Sst"], key="Sst")
        if l == 0:
            self.tap("s_QH", QH, ["QH"])
            self.tap("s_yh", yh, ["yh"])
        fw.dma(self.sy.rearrange("(t q) (h d) -> (q h) t d", t=4, h=NH), yh[:, :, :], r=["yh"], w=["sy"], key="yh")
        fw.dma(ytm[0:MS, :], self.sy, r=["sy"], w=["T7"], key="ytm")
        pg, pgk = self.pf()
        fw.mm(pg[0:MS, :], sgT[:, 0:MS], lg2[:, :], True, True, r=[RS.K("sgT"), "lg2"], w=[pgk])
        post(MS, ytm[0:MS, :], ["T7"], pg, pgk, RS)
        m, mk = mrT[0], "mrT0"
        gate_branch(MS, hcur, "hTs", m, mk)
        fw.dma(self.mrbuf[NT].rearrange("p (c t) -> p c t", c=8)[:, :, 0:MS], m[:, :, 0:MS], r=[mk], w=[("mr", NT)], key=mk)
        raw_last(lambda c: hTs[:, c, 64:80], "hTs", 16, O["s_shift"][l])

    def pass_attn(self, l, es2):
        fw, I, O, NT = self.fw, self.I, self.O, self.NT
        sbl = lambda n, s, dt=F32: self.sbl(es2, "a%d_" % l + n, s, dt)
        identb, identf = self.identb, self.identf
        Wq = sbl("Wq", [128, 8, 768], BF)
        Wg = sbl("Wg", [128, 8, D], BF)
        Wa = sbl("Wa", [128, 4, D], BF)
        Wo = sbl("Wo", [128, 8, D], BF)
        self.col_load(self.gcol[:], "gcol", I["norm_mix_g"][l], 8)
        m0 = self.aoff
        self.wstage = [sbl("wst%d" % i_, [128, 2048]) for i_ in range(4)]
        win = I["w_in"][l]
        gsc = lambda c: self.gcol[:, c:c + 1]
        self.prep_w(8, 512, lambda c, s0, n: win[c * 128:(c + 1) * 128, RP:RP + 512],
                    lambda c, s0, n: Wq[:, c, 0:512].rearrange("p (j g d) -> p g j d", j=4, g=2), lambda c: "Wq_%d" % c, "col", gsc,
                    sview=lambda a: a.rearrange("p (g j d) -> p g j d", g=2, j=4))
        self.prep_w(8, 256, lambda c, s0, n: win[c * 128:(c + 1) * 128, RP + 512:RP + 768],
                    lambda c, s0, n: Wq[:, c, 512:768], lambda c: "Wq_%d" % c, "col", gsc)
        self.prep_w(8, D, lambda c, s0, n: win[c * 128:(c + 1) * 128, 3584 + s0:3584 + s0 + n],
                    lambda c, s0, n: Wg[:, c, s0:s0 + n], lambda c: "Wga_%d" % c, "col", gsc)
        wbr = I["w_br_attn"][l]
        self.prep_w(4, D, lambda c, s0, n: wbr[c * 128:(c + 1) * 128, s0:s0 + n],
                    lambda c, s0, n: Wa[:, c, s0:s0 + n], lambda c: "Wa_%d" % c, "plain")
        wo = I["w_out"][l]
        self.prep_w(8, D, lambda c, s0, n: wo[c * 128:(c + 1) * 128, s0:s0 + n],
                    lambda c, s0, n: Wo[:, c, s0:s0 + n], lambda c: "Wo_%d" % c, "plain")
        self.release(m0)
        amask = sbl("amask", [128, 1024])
        fw.dma(amask[:, 0:768], I["c_amask"], w=["amask"], key="amask")
        fw.dma(amask[:, 768:1024], I["c_amask0"], w=["amask"], key="amask")
        smask = sbl("smask", [32, 132])
        fw.dma(smask[:], I["c_smask"], w=["smask"], key="smask")
        sinks = sbl("sinks", [128, NH])
        self.bcast_load(sinks[:], "sinks", I["attn_sinks"][l])
        hTd = [sbl("hT%d" % i_, [128, 8, 128], BF) for i_ in range(2)]
        hT = hTd[1]
        qkv = sbl("qkv", [128, 768])
        rot = sbl("rot", [128, 640])
        rtmp = [sbl("rtmp%d" % i, [128, 320]) for i in range(2)]
        rotb = sbl("rotb", [128, 640], BF)
        cs = [sbl("cs%d" % i, [128, 64]) for i in range(2)]
        qT = sbl("qT", [128, 4, 128], BF)

        class NSB:
            pass
        B0, B1 = NSB(), NSB()
        B0.qkv, B0.rot, B0.rotb, B0.qT, B0.s = qkv, rot, rotb, qT, ""
        B1.qkv, B1.rot, B1.rotb, B1.qT, B1.s = (sbl("qkvb", [128, 768]), sbl("rotbb", [128, 640]), sbl("rotbbb", [128, 640], BF),
                                                sbl("qTb", [128, 4, 128], BF), "b")
        Bs = [B0, B1]
        KTr = sbl("KTr", [128, 2, 128], BF)
        Vp = sbl("Vp", [128, 2, 2, 2, 128], BF)
        scg = [sbl("sc%d" % g_, [128, 4, 256]) for g_ in range(2)]
        stg = [sbl("st%d" % g_, [128, 16]) for g_ in range(2)]
        pbfg = [sbl("pbf%d" % g_, [128, 4, 256], BF) for g_ in range(2)]
        pTg = [sbl("pT%d" % g_, [128, 4, 2, 128], BF) for g_ in range(2)]
        oT = sbl("oT", [128, 4, 128], BF)
        sga = sbl("sga", [128, 8, 128])
        mrl = [sbl("mrl%d" % i, [128, 8, 128], BF) for i in range(2)]
        mg = sbl("mg", [128, 8, 128], BF)
        xo = [sbl("xo%d" % i, [128, D]) for i in range(2)]
        KA = sbl("KA", [128, NS, 128])
        VA = sbl("VA", [128, NS, 128])
        VAb = sbl("VAb", [128, NS, 128], BF)
        KB = sbl("KB", [4, NS, 128])
        VBt = sbl("VB", [4, NS, 128])
        VBb = sbl("VBb", [4, NS, 128], BF)
        KAT = sbl("KAT", [128, NS, 128], BF)
        KBT = sbl("KBT", [128, NS, 4], BF)
        qbd = sbl("qbd", [128, NS, 32], BF)
        ssc = sbl("ssc", [32, NS, 132])
        sst = sbl("sst", [32, 4 * NS])
        spb = sbl("spb", [32, NS, 132], BF)
        spT = sbl("spT", [128, NS, 32], BF)
        spTB = sbl("spTB", [4, NS, 32], BF)
        oTs = sbl("oTs", [128, 4, MS], BF)

        self.V(lambda e: e.memset(Vp[:], 0.0), w=["Vp0", "Vp1"])
        self.V(lambda e: e.memset(KTr[:], 0.0), w=["KTr0", "KTr1"])
        self.V(lambda e: e.memset(qbd[:], 0.0), w=["qbd"])

        def proj_rope(B, M, hcur, hk, cosap, sinap, cskey):
            for g0, n in [(0, 512), (512, 256)]:
                ps, pk = self.pf()
                for c in range(8):
                    fw.mm(ps[0:M, 0:n], hcur(c), Wq[:, c, g0:g0 + n], c == 0, c == 7, r=[hk, "Wq_%d" % c], w=[pk])
                fw.act(B.qkv[0:M, g0:g0 + n], ps[0:M, 0:n], AF.Copy, r=[pk], w=["qkv%d" % (g0 // 512) + B.s])
            qk3 = B.qkv[0:M, 0:640].rearrange("p (h d) -> p h d", h=10)
            r3 = B.rot[0:M, :].rearrange("p (h d) -> p h d", h=10)
            x1, x2 = qk3[:, :, 0:32], qk3[:, :, 32:64]
            cb = cosap.unsqueeze(1).to_broadcast([M, 10, 32])
            sb_ = sinap.unsqueeze(1).to_broadcast([M, 10, 32])
            ta = rtmp[0][0:M, :].rearrange("p (h d) -> p h d", h=10)
            tb = rtmp[1][0:M, :].rearrange("p (h d) -> p h d", h=10)
            rk = ["qkv0" + B.s, "qkv1" + B.s, cskey]
            rotk = "rot" + B.s
            self.V(lambda e: e.tensor_tensor(ta, x1, cb, ALU.mult), r=rk, w=["rtmp0"])
            self.P(lambda e: e.tensor_tensor(tb, x2, sb_, ALU.mult), r=rk, w=["rtmp1"])
            self.V(lambda e: e.tensor_tensor(r3[:, :, 0:32], ta, tb, ALU.subtract), r=["rtmp0", "rtmp1"], w=[rotk])
            self.V(lambda e: e.tensor_tensor(ta, x2, cb, ALU.mult), r=rk + [rotk], w=["rtmp0"])
            self.P(lambda e: e.tensor_tensor(tb, x1, sb_, ALU.mult), r=rk + [rotk], w=["rtmp1"])
            self.V(lambda e: e.tensor_tensor(r3[:, :, 32:64], ta, tb, ALU.add), r=["rtmp0", "rtmp1"], w=[rotk])
            fw.act(B.rotb[0:M, :], B.rot[0:M, :], AF.Copy, r=[rotk], w=["rotb" + B.s])

        def q_transposes(B, M, dst, dkey):
            pbk, pk = self.pb()
            for jj in range(4):
                fw.tr(pbk[:, jj * M:(jj + 1) * M], B.rotb[0:M, jj * 128:(jj + 1) * 128], identb[0:M, 0:M], r=["rotb" + B.s, "identb"], w=[pk])
            fw.act(dst, pbk[:, 0:4 * M].rearrange("p (j t) -> p j t", j=4), AF.Copy, r=[pk], w=[dkey])

        def gates_part(M, hcur, hk):
            for half in range(2):
                pg, pgk = self.pf()
                for q in range(4):
                    dc = half * 4 + q
                    for c in range(8):
                        fw.mm(pg[:, q * M:(q + 1) * M], Wg[:, c, dc * 128:(dc + 1) * 128], hcur(c), c == 0, c == 7, r=[hk, "Wga_%d" % c], w=[pgk])
                fw.act(sga[:, half * 4:(half + 1) * 4, 0:M], pg[:, 0:4 * M].rearrange("p (q t) -> p q t", q=4), AF.Sigmoid, r=[pgk], w=["sga%d" % half])

        def gate_out(M, hcur, hk, oTt, okey, mr, mrk, xt, xk, xo_, xok, do_gates=True):
            if do_gates:
                gates_part(M, hcur, hk)
            for half in range(2):
                pbr, pbk_ = self.pf()
                for q in range(4):
                    dc = half * 4 + q
                    for cc in range(4):
                        fw.mm(pbr[:, q * M:(q + 1) * M], Wa[:, cc, dc * 128:(dc + 1) * 128], oTt[:, cc, 0:M], cc == 0, cc == 3, r=[okey, "Wa_%d" % cc], w=[pbk_])
                hs = slice(half * 4, (half + 1) * 4)
                self.V(lambda e, hs=hs, pbr=pbr: e.tensor_tensor(sga[:, hs, 0:M], sga[:, hs, 0:M], pbr[:, 0:4 * M].rearrange("p (q t) -> p q t", q=4), ALU.mult),
                       r=["sga%d" % half, pbk_], w=["sga%d" % half])
                self.V(lambda e, hs=hs: e.tensor_tensor(mg[:, hs, 0:M], sga[:, hs, 0:M], mr[:, hs, 0:M], ALU.add), r=["sga%d" % half, mrk], w=["mg%d" % half])
            for grp in range(2):
                px, pxk = self.pf()
                for dc in range(8):
                    fw.mm(px[0:M, :], mg[:, dc, 0:M], Wo[:, dc, grp * 512:(grp + 1) * 512], dc == 0, dc == 7, r=["mg%d" % (dc // 4), "Wo_%d" % dc], w=[pxk])
                self.V(lambda e, grp=grp, px=px: e.tensor_tensor(xo_[0:M, grp * 512:(grp + 1) * 512], xt[0:M, grp * 512:(grp + 1) * 512], px[0:M, :], ALU.add),
                       r=[xk, pxk], w=[xok])

        def put_kv(B, slot):
            pbk, pk = self.pb()
            fw.tr(pbk[:, 0:128], B.rotb[:, 512:640], identb[:, :], r=["rotb" + B.s, "identb"], w=[pk])
            self.V(lambda e, pbk=pbk, slot=slot: e.tensor_copy(KTr[:, slot, :], pbk[:, 0:128]), r=[pk], w=["KTr%d" % slot])
            for g in range(2):
                vsrc = B.qkv[:, 640 + g * 64:640 + (g + 1) * 64]
                fw.act(Vp[:, slot, g, 0, 0:64], vsrc, AF.Copy, r=["qkv1" + B.s], w=["Vp%d" % slot])
                self.P(lambda e, g=g, vsrc=vsrc, slot=slot: e.tensor_copy(Vp[:, slot, g, 1, 64:128], vsrc), r=["qkv1" + B.s], w=["Vp%d" % slot])

        xt, xk = self.xt[1], "xt1"
        fw.dma(xt[:], (I["xh0"] if (l == 0 or NSEG == 1) else self.xh_dram), r=["xh_dram"], w=[xk], key=xk)
        fw.dma(cs[1][:, 0:32], I["c_cosh"], w=["cs1"], key="cs1")
        fw.dma(cs[1][:, 32:64], I["c_sinh"], w=["cs1"], key="cs1")
        self.norm_hT(xt, xk, 128, hT[:, :, :], "hT1", identb)
        proj_rope(B1, 128, lambda c: hT[:, c, :], "hT1", cs[1][:, 0:32], cs[1][:, 32:64], "cs1")
        put_kv(B1, 1)
        def pre(i):
            xt, xk = self.xt[i % 2], "xt%d" % (i % 2)
            src, _ = self.xsrc(l, i)
            fw.dma(xt[:], src, r=[("xb", i)], w=[xk], key=xk)
            mr, mrk = mrl[i % 2], "mrl%d" % (i % 2)
            fw.dma(mr[:, :, :], self.mrbuf[i].rearrange("p (c t) -> p c t", c=8), r=[("mr", i)], w=[mrk], key=mrk)
            ck_ = "cs%d" % (i % 2)
            fw.dma(cs[i % 2][:, 0:32], I["c_cosp"][i * 128:(i + 1) * 128, :], w=[ck_], key=ck_)
            fw.dma(cs[i % 2][:, 32:64], I["c_sinp"][i * 128:(i + 1) * 128, :], w=[ck_], key=ck_)
            self.norm_hT(xt, xk, 128, hTd[i % 2][:, :, :], "hT%d" % (i % 2), identb)
            B = Bs[i % 2]
            proj_rope(B, 128, lambda c, i=i: hTd[i % 2][:, c, :], "hT%d" % (i % 2), cs[i % 2][:, 0:32], cs[i % 2][:, 32:64], ck_)
            q_transposes(B, 128, B.qT[:, :, :], "qT" + B.s)

        pre(0)
        for i in range(NT):
            xt, xk = self.xt[i % 2], "xt%d" % (i % 2)
            mr, mrk = mrl[i % 2], "mrl%d" % (i % 2)
            ck_ = "cs%d" % (i % 2)
            hkk = "hT%d" % (i % 2)
            hcur = lambda c, i=i: hTd[i % 2][:, c, :]
            B = Bs[i % 2]
            slot = i % 2
            if i == NT - 1:
                fw.dma(O["p_k"][l], B.rot[:, 512:640], r=["rot" + B.s], key="rot")
                fw.dma(O["p_v"][l], B.qkv[:, 640:768], r=["qkv1" + B.s], key="qkv1")
            put_kv(B, slot)
            mvar = 3 if i == 0 else slot
            msk = amask[:, mvar * 256:(mvar + 1) * 256].unsqueeze(1).to_broadcast([128, 4, 256])
            pSg = []
            for g in range(2):
                o = g * 64
                pS = []
                for jj in range(4):
                    if jj % 2 == 0:
                        ps, pk = self.pf()
                        pS.append((ps, pk))
                    fw.mm(ps[:, (jj % 2) * 256:(jj % 2 + 1) * 256], B.qT[o:o + 64, jj, :], KTr[o:o + 64, :, :].rearrange("p s t -> p (s t)"),
                          True, True, r=["qT" + B.s, "KTr0", "KTr1"], w=[pk])
                pSg.append(pS)
            gates_part(128, hcur, hkk)

            def softmax(g):
                sc, st, pbf = scg[g], stg[g], pbfg[g]
                sck = ["sc%d_0" % g, "sc%d_1" % g]
                for half, (ps, pk) in enumerate(pSg[g]):
                    self.V(lambda e, ps=ps, half=half, msk=msk, sc=sc: e.scalar_tensor_tensor(
                        sc[:, half * 2:(half + 1) * 2, :], ps[:, :].rearrange("p (j c) -> p j c", j=2), 0.125,
                        msk[:, 0:2, :], ALU.mult, ALU.add), r=[pk, "amask"], w=[sck[half]])
                k0, k2, k3 = "st%d" % g, "st%d_2" % g, "st%d_3" % g
                self.V(lambda e: e.tensor_reduce(st[:, 0:4], sc[:, :, :], AX.X, ALU.max), r=sck, w=[k0])
                self.V(lambda e: e.tensor_tensor(st[:, 0:4], st[:, 0:4], sinks[:, g * 4:(g + 1) * 4], ALU.max), r=[k0, "sinks"], w=[k0])
                self.V(lambda e: e.tensor_tensor(sc[:, :, :], sc[:, :, :], bc3(st[:, 0:4], 256), ALU.subtract), r=sck + [k0], w=sck)
                fw.act(sc[:, :, :], sc[:, :, :], AF.Exp, r=sck, w=sck)
                self.V(lambda e: e.tensor_reduce(st[:, 4:8], sc[:, :, :], AX.X, ALU.add), r=sck, w=[k2])
                self.V(lambda e: e.tensor_tensor(st[:, 8:12], sinks[:, g * 4:(g + 1) * 4], st[:, 0:4], ALU.subtract), r=[k0, "sinks"], w=[k3])
                fw.act(st[:, 8:12], st[:, 8:12], AF.Exp, r=[k3], w=[k3])
                self.V(lambda e: e.tensor_tensor(st[:, 4:8], st[:, 4:8], st[:, 8:12], ALU.add), r=[k2, k3], w=[k2])
                self.V(lambda e: e.reciprocal(st[:, 4:8], st[:, 4:8]), r=[k2], w=[k2])
                self.V(lambda e: e.tensor_tensor(pbf[:, :, :], sc[:, :, :], bc3(st[:, 4:8], 256), ALU.mult), r=sck + [k2], w=["pbf%d" % g])

            def p_transposes(g):
                pbf, pT = pbfg[g], pTg[g]
                pbk, pk = self.pb()
                for jj in range(4):
                    for s_ in range(2):
                        fw.tr(pbk[:, (jj * 2 + s_) * 128:(jj * 2 + s_ + 1) * 128], pbf[:, jj, s_ * 128:(s_ + 1) * 128], identb[:, :], r=["pbf%d" % g, "identb"], w=[pk])
                fw.act(pT[:, :, :, :], pbk[:, :].rearrange("p (j s t) -> p j s t", j=4, s=2), AF.Copy, r=[pk], w=["pT%d" % g])

            def pv(g, pO, pok):
                pT = pTg[g]
                for c2 in range(2):
                    cc = g * 2 + c2
                    n = 0
                    for par in range(2):
                        jj = c2 * 2 + par
                        for s_ in range(2):
                            fw.mm(pO[:, cc * 128:(cc + 1) * 128], Vp[:, s_, g, par, :], pT[:, jj, s_, :], n == 0, n == 3,
                                  r=["Vp0", "Vp1", "pT%d" % g], w=[pok])
                            n += 1

            fw.replay([fw.capture(lambda: softmax(0)), fw.capture(lambda: softmax(1))], chunk=1)
            p_transposes(0)
            pO, pok = self.pf()
            pv(0, pO, pok)
            if i + 1 < NT:
                pre(i + 1)
            p_transposes(1)
            pv(1, pO, pok)
            fw.act(oT[:, :, :], pO[:, :].rearrange("p (c t) -> p c t", c=4), AF.Copy, r=[pok], w=["oT"])
            xo_, xok = xo[i % 2], "xo%d" % (i % 2)
            gate_out(128, hcur, hkk, oT, "oT", mr, mrk, xt, xk, xo_, xok, do_gates=False)
            fw.dma(self.xbuf[i * 128:(i + 1) * 128, :], xo_[:, :], r=[xok], w=[("xb", i)], key=xok)
        if NSEG > 1:
            self.gather_select(xo_[:, :], [xok], D, self.agX_in, self.agX_out, "agX")
            fw.dma(self.xh_dram, xo_[:, :], r=[xok], w=["xh_dram"], key="xhst")

        i = NT
        xt, xk = self.xt[i % 2], "xt%d" % (i % 2)
        src, _ = self.xsrc(l, i)
        fw.dma(xt[0:MS, :], src, r=[("xb", i)], w=[xk], key=xk)
        mr, mrk = mrl[i % 2], "mrl%d" % (i % 2)
        fw.dma(mr[:, :, 0:MS], self.mrbuf[NT].rearrange("p (c t) -> p c t", c=8)[:, :, 0:MS], r=[("mr", NT)], w=[mrk], key=mrk)
        ck_ = "cs%d" % (i % 2)
        fw.dma(cs[i % 2][0:MS, 0:32], I["c_coss"], w=[ck_], key=ck_)
        fw.dma(cs[i % 2][0:MS, 32:64], I["c_sins"], w=[ck_], key=ck_)
        self.norm_hT(xt, xk, MS, hT[:, :, 0:MS], "hT1", identb)
        hcur = lambda c: hT[:, c, 0:MS]
        proj_rope(B0, MS, hcur, "hT1", cs[i % 2][0:MS, 0:32], cs[i % 2][0:MS, 32:64], ck_)
        for (cin, cout, srcap, srck, dkey) in [("ck", "s_k", rot[:, 512:640], "rot", "sk"), ("cv", "s_v", qkv[:, 640:768], "qkv1", "sv")]:
            fw.dma(O[cout][l, :, 0:124, :], I[cin][l, :, 4:128, :], w=[dkey], key=dkey + "c")
            for t in range(4):
                fw.dma(O[cout][l, :, 124 + t, :], srcap[t * 16:(t + 1) * 16, :], r=[srck], w=[dkey], key=dkey + "n")
        fw.dma(KA[:, :, :], O["s_k"][l].rearrange("q p c -> p q c"), r=["sk"], w=["KA"], key="KA")
        fw.dma(VA[:, :, :], O["s_v"][l].rearrange("q p c -> p q c"), r=["sv"], w=["VA"], key="VA")
        fw.dma(KB[:, :, :], I["ck"][l, :, 0:4, :].rearrange("q p c -> p q c"), w=["KB"], key="KB")
        fw.dma(VBt[:, :, :], I["cv"][l, :, 0:4, :].rearrange("q p c -> p q c"), w=["VB"], key="VB")
        self.P(lambda e: e.tensor_copy(VAb[:, :, :], VA[:, :, :]), r=["VA"], w=["VAb"])
        self.P(lambda e: e.tensor_copy(VBb[:, :, :], VBt[:, :, :]), r=["VB"], w=["VBb"])
        for q4 in range(4):
            ps, pk = self.pf()
            for qq in range(4):
                q = q4 * 4 + qq
                fw.tr(ps[:, qq * 128:(qq + 1) * 128], KA[:, q, :], identf[:, :], r=["KA", "identf"], w=[pk])
            fw.act(KAT[:, q4 * 4:(q4 + 1) * 4, :], ps[:, :].rearrange("p (q t) -> p q t", q=4), AF.Copy, r=[pk], w=["KAT"])
        ps, pk = self.pf()
        for q in range(NS):
            fw.tr(ps[:, q * 4:(q + 1) * 4], KB[0:4, q, :], identf[0:4, 0:4], r=["KB", "identf"], w=[pk])
        fw.act(KBT[:, :, :], ps[:, 0:64].rearrange("p (q t) -> p q t", q=NS), AF.Copy, r=[pk], w=["KBT"])
        q_transposes(B0, MS, qT[:, :, 0:MS], "qT")
        for g in range(2):
            for jj in range(4):
                o = g * 64
                dst = qbd[o:o + 64, :, g * 16 + jj * 4:g * 16 + (jj + 1) * 4]
                srcq = qT[o:o + 64, jj, 0:MS].rearrange("p (t q) -> p q t", t=4)
                self.V(lambda e, dst=dst, srcq=srcq: e.tensor_copy(dst, srcq), r=["qT"], w=["qbd"])
        pSA = []
        for q4 in range(4):
            ps, pk = self.pf()
            pSA.append((ps, pk))
            for qq in range(4):
                q = q4 * 4 + qq
                fw.mm(ps[0:32, qq * 128:(qq + 1) * 128], qbd[:, q, :], KAT[:, q, :], True, True, r=["qbd", "KAT"], w=[pk])
        psB, pkB = self.pf()
        for q in range(NS):
            fw.mm(psB[0:32, q * 4:(q + 1) * 4], qbd[:, q, :], KBT[:, q, :], True, True, r=["qbd", "KBT"], w=[pkB])
        for q4, (ps, pk) in enumerate(pSA):
            self.V(lambda e, q4=q4, ps=ps: e.scalar_tensor_tensor(
                ssc[:, q4 * 4:(q4 + 1) * 4, 0:128], ps[0:32, :].rearrange("p (q c) -> p q c", q=4), 0.125,
                smask[:, 0:128].unsqueeze(1).to_broadcast([32, 4, 128]), ALU.mult, ALU.add), r=[pk, "smask"], w=["ssc"])
        self.V(lambda e: e.scalar_tensor_tensor(
            ssc[:, :, 128:132], psB[0:32, 0:64].rearrange("p (q c) -> p q c", q=NS), 0.125,
            smask[:, 128:132].unsqueeze(1).to_broadcast([32, NS, 4]), ALU.mult, ALU.add), r=[pkB, "smask"], w=["ssc"])
        sinkc = sbl("sinkc", [32, 1])
        for g in range(2):
            for jj in range(4):
                p0 = g * 16 + jj * 4
                fw.dma(sinkc[p0:p0 + 4, :], I["attn_sinks"][l, g * 4 + jj:g * 4 + jj + 1].partition_broadcast(4), w=["sinkc"], key="sinkc")
        self.V(lambda e: e.tensor_reduce(sst[:, 0:NS], ssc[:, :, :], AX.X, ALU.max), r=["ssc"], w=["sst"])
        self.V(lambda e: e.tensor_scalar(sst[:, 0:NS], sst[:, 0:NS], sinkc[:, 0:1], None, ALU.max), r=["sst", "sinkc"], w=["sst"])
        self.V(lambda e: e.tensor_tensor(ssc[:, :, :], ssc[:, :, :], bc3(sst[:, 0:NS], 132), ALU.subtract), r=["ssc", "sst"], w=["ssc"])
        fw.act(ssc[:, :, :], ssc[:, :, :], AF.Exp, r=["ssc"], w=["ssc"])
        self.V(lambda e: e.tensor_reduce(sst[:, NS:2 * NS], ssc[:, :, :], AX.X, ALU.add), r=["ssc"], w=["sst2"])
        self.V(lambda e: e.tensor_scalar(sst[:, 2 * NS:3 * NS], sst[:, 0:NS], sinkc[:, 0:1], None, ALU.subtract), r=["sst", "sinkc"], w=["sst3"])
        fw.act(sst[:, 2 * NS:3 * NS], sst[:, 2 * NS:3 * NS], AF.Exp, r=["sst3"], w=["sst3"], scale=-1.0)
        self.V(lambda e: e.tensor_tensor(sst[:, NS:2 * NS], sst[:, NS:2 * NS], sst[:, 2 * NS:3 * NS], ALU.add), r=["sst2", "sst3"], w=["sst2"])
        self.V(lambda e: e.reciprocal(sst[:, NS:2 * NS], sst[:, NS:2 * NS]), r=["sst2"], w=["sst2"])
        self.V(lambda e: e.tensor_tensor(spb[:, :, :], ssc[:, :, :], bc3(sst[:, NS:2 * NS], 132), ALU.mult), r=["ssc", "sst2"], w=["spb"])
        identb32 = identb[0:32, 0:32]
        for q8 in range(2):
            pbk, pk = self.pb()
            for qq in range(8):
                q = q8 * 8 + qq
                fw.tr(pbk[:, qq * 32:(qq + 1) * 32], spb[:, q, 0:128], identb32, r=["spb", "identb"], w=[pk])
            fw.act(spT[:, q8 * 8:(q8 + 1) * 8, :], pbk[:, 0:256].rearrange("p (q c) -> p q c", q=8), AF.Copy, r=[pk], w=["spT"])
        pbk, pk = self.pb()
        for q in range(NS):
            fw.tr(pbk[0:4, q * 32:(q + 1) * 32], spb[:, q, 128:132], identb32, r=["spb", "identb"], w=[pk])
        fw.act(spTB[:, :, :], pbk[0:4, 0:512].rearrange("p (q c) -> p q c", q=NS), AF.Copy, r=[pk], w=["spTB"])
        pO, pok = self.pf()
        for q in range(NS):
            fw.mm(pO[:, q * 32:(q + 1) * 32], VAb[:, q, :], spT[:, q, :], True, False, r=["VAb", "spT"], w=[pok])
            fw.mm(pO[:, q * 32:(q + 1) * 32], VBb[0:4, q, :], spTB[0:4, q, :], False, True, r=["VBb", "spTB"], w=[pok])
        oraw = sbl("oraw", [128, 32, NS], BF)
        fw.act(oraw.rearrange("p c q -> p q c"), pO[:, :].rearrange("p (q c) -> p q c", q=NS), AF.Copy, r=[pok], w=["oraw"])
        for g in range(2):
            for jj in range(4):
                cc, par = g * 2 + jj // 2, jj % 2
                c0 = g * 16 + jj * 4
                srco = oraw[g * 64:(g + 1) * 64, c0:c0 + 4, :].rearrange("p t q -> p (t q)")
                fw.dma(oTs[par * 64:(par + 1) * 64, cc, :], srco, r=["oraw"], w=["oTs"], key="oTs")
        xo_, xok = xo[i % 2], "xo%d" % (i % 2)
        gate_out(MS, hcur, "hT1", oTs, "oTs", mr, mrk, xt, xk, xo_, xok)
        fw.dma(self.xsbuf, xo_[0:MS, :], r=[xok], w=[("xb", NT)], key=xok)

    def pass_ffn(self, l, es2):
        fw, I, O, NT = self.fw, self.I, self.O, self.NT
        sbl = lambda n, s, dt=F32: self.sbl(es2, "f%d_" % l + n, s, dt)
        identb, identf = self.identb, self.identf
        Wc = sbl("Wc", [128, 8, DFF], BF)
        Wu = sbl("Wu", [128, 8, DFF], BF)
        Wd = sbl("Wd", [128, NFC, D], BF)
        self.col_load(self.gcol[:], "gcol", I["norm_ffn_g"][l], 8)
        cw = sbl("cw", [128, 4, NFC])
        for j in range(3):
            self.col_load(cw[:, j, :], "cw", I["ffn_conv_w"][l, j], NFC)
        self.col_load(cw[:, 3, :], "cw", I["ffn_conv_b"][l], NFC)
        m0 = self.aoff
        self.wstage = [sbl("wst%d" % i_, [128, 2048]) for i_ in range(4)]
        wi = I["ffn_w_in"][l]
        gsc = lambda c: self.gcol[:, c:c + 1]
        self.prep_w(8, DFF, lambda c, s0, n: wi[c * 128:(c + 1) * 128, s0:s0 + n],
                    lambda c, s0, n: Wc[:, c, s0:s0 + n], lambda c: "Wc_%d" % c, "col", gsc)
        self.prep_w(8, DFF, lambda c, s0, n: wi[c * 128:(c + 1) * 128, DFF + s0:DFF + s0 + n],
                    lambda c, s0, n: Wu[:, c, s0:s0 + n], lambda c: "Wu_%d" % c, "col", gsc)
        wd = I["ffn_w_down"][l]
        self.prep_w(NFC, D, lambda c, s0, n: wd[c * 128:(c + 1) * 128, s0:s0 + n],
                    lambda c, s0, n: Wd[:, c, s0:s0 + n], lambda c: "Wd_%d" % c, "plain")
        self.release(m0)
        last = (l == 1)
        if last:
            gf = sbl("gf", [128, D])
            self.bcast_load(gf[:], "gf", I["norm_final_g"])
        hTd = [sbl("hT%d" % i_, [128, 8, 128], BF) for i_ in range(2)]
        hT = hTd[0]
        cxf = sbl("cx", [128, NFC * 130])
        cx1 = cxf.rearrange("p (f t) -> p f t", f=NFC)
        cxs = cxf[:, 0:NFC * NS * 6].rearrange("p (f q j) -> p f q j", f=NFC, q=NS)
        acc = [sbl("acc%d" % i_, [128, 4, 128]) for i_ in range(2)]
        aTd = [sbl("aT%d" % i_, [128, NFC, 128], BF) for i_ in range(2)]
        xo = [sbl("xo%d" % i_, [128, D]) for i_ in range(2)]
        ctok = sbl("ctok", [128, DFF])
        cst = ctok
        jk = self.xn

        def finish(M, xt, xk, xo_, xok, dst_final, dst_x, dkey, aT, aTk):
            for grp in range(2):
                px, pxk = self.pf()
                for fc in range(NFC):
                    fw.mm(px[0:M, :], aT[:, fc, 0:M], Wd[:, fc, grp * 512:(grp + 1) * 512], fc == 0, fc == NFC - 1, r=[aTk, "Wd_%d" % fc], w=[pxk])
                self.V(lambda e, grp=grp, px=px: e.tensor_tensor(xo_[0:M, grp * 512:(grp + 1) * 512], xt[0:M, grp * 512:(grp + 1) * 512], px[0:M, :], ALU.add),
                       r=[xk, pxk], w=[xok])
            if not last:
                fw.dma(dst_x, xo_[0:M, :], r=[xok], w=[dkey], key=xok)
                return
            ss, t1 = self.ss, self.t1
            fw.act(jk[0:M, :], xo_[0:M, :], AF.Square, r=[xok], w=["xn", "ss"], accum_out=ss[0:M, :])
            self.V(lambda e: e.tensor_scalar(t1[0:M, :], ss[0:M, :], 1.0 / D, 1e-6, ALU.mult, ALU.add), r=["ss"], w=["t1"])
            fw.act(t1[0:M, :], t1[0:M, :], AF.Sqrt, r=["t1"], w=["t1"])
            self.V(lambda e: e.reciprocal(t1[0:M, :], t1[0:M, :]), r=["t1"], w=["t1"])
            self.V(lambda e: e.scalar_tensor_tensor(xo_[0:M, :], xo_[0:M, :], t1[0:M, 0:1], gf[0:M, :], ALU.mult, ALU.mult),
                   r=[xok, "t1", "gf"], w=[xok])
            fw.dma(dst_final, xo_[0:M, :], r=[xok], key=xok)

        def ffn_core(M, hcur, hk, cview, ckey, sample, aT, aTk, mid=None, groups=None):
            for b0 in (groups if groups is not None else range(0, NFC, 4)):
                nb = min(4, NFC - b0)
                pc, pck = self.pf()
                for q in range(nb):
                    fc = b0 + q
                    for c in range(8):
                        fw.mm(pc[:, q * M:(q + 1) * M], Wc[:, c, fc * 128:(fc + 1) * 128], hcur(c), c == 0, c == 7, r=[hk, "Wc_%d" % c], w=[pck])
                pu, puk = self.pf()
                for q in range(nb):
                    fc = b0 + q
                    for c in range(8):
                        fw.mm(pu[:, q * M:(q + 1) * M], Wu[:, c, fc * 128:(fc + 1) * 128], hcur(c), c == 0, c == 7, r=[hk, "Wu_%d" % c], w=[puk])
                if sample:
                    fw.act(cview[:, b0:b0 + nb, :, 2:6], pc[:, 0:nb * M].rearrange("p (f t q) -> p f q t", f=nb, t=4), AF.Copy, r=[pck], w=[ckey])
                else:
                    fw.act(cview[:, b0:b0 + nb, 2:130], pc[:, 0:nb * M].rearrange("p (f t) -> p f t", f=nb), AF.Copy, r=[pck], w=[ckey])
                a_ = acc[(b0 // 4) % 2]
                ak = "acc%d" % ((b0 // 4) % 2)
                views = []
                for q in range(nb):
                    fc = b0 + q
                    if sample:
                        c0, c1, c2 = (cview[:, fc, :, s_:s_ + 4] for s_ in range(3))
                        av = a_[:, q, 0:M].rearrange("p (t q) -> p q t", t=4)
                    else:
                        c0, c1, c2 = (cview[:, fc, s_:s_ + 128] for s_ in range(3))
                        av = a_[:, q, :]
                    views.append((fc, av, c0, c1, c2))
                akq = [ak + "_%d" % q for q in range(nb)]
                for q, (fc, av, c0, c1, c2) in enumerate(views):
                    self.P(lambda e, av=av, c0=c0, fc=fc: e.tensor_scalar(av, c0, cw[:, 0, fc:fc + 1], cw[:, 3, fc:fc + 1], ALU.mult, ALU.add),
                           r=[ckey, "cw", ak], w=([akq[q], ak] if q == 0 else [akq[q]]))
                for q, (fc, av, c0, c1, c2) in enumerate(views):
                    self.V(lambda e, av=av, c1=c1, fc=fc: e.scalar_tensor_tensor(av, c1, cw[:, 1, fc:fc + 1], av, ALU.mult, ALU.add),
                           r=[ckey, "cw", akq[q]], w=[akq[q]])
                for q, (fc, av, c0, c1, c2) in enumerate(views):
                    self.V(lambda e, av=av, c2=c2, fc=fc: e.scalar_tensor_tensor(av, c2, cw[:, 2, fc:fc + 1], av, ALU.mult, ALU.add),
                           r=[ckey, "cw", akq[q]], w=[akq[q]])
                fw.act(a_[:, 0:nb, 0:M], a_[:, 0:nb, 0:M], AF.Gelu, r=akq, w=[ak])
                self.V(lambda e, a_=a_, pu=pu, nb=nb, b0=b0, aT=aT: e.tensor_tensor(aT[:, b0:b0 + nb, 0:M], a_[:, 0:nb, 0:M],
                                                                              pu[:, 0:nb * M].rearrange("p (f t) -> p f t", f=nb), ALU.mult),
                       r=[ak, puk], w=[aTk])
                if mid is not None and b0 == 8:
                    mid()

        def c_token_major(M, hcur, hk, rows, dsts):
            for g0 in range(0, DFF, 512):
                n = min(512, DFF - g0)
                ps, pk = self.pf()
                for c in range(8):
                    fw.mm(ps[0:M, 0:n], hcur(c), Wc[:, c, g0:g0 + n], c == 0, c == 7, r=[hk, "Wc_%d" % c], w=[pk])
                fw.act(ctok[0:M, g0:g0 + n], ps[0:M, 0:n], AF.Copy, r=[pk], w=["ctok"])
            for (r0, r1), dst in zip(rows, dsts):
                fw.dma(dst, ctok[r0:r1, :], r=["ctok"], key="ctok")

        xt, xk = self.xt[1], "xt1"
        fw.dma(xt[:], (I["xh0"] if NSEG == 1 else self.xh_dram), r=["xh_dram"], w=[xk], key=xk)
        self.norm_hT(xt, xk, 128, hT[:, :, :], "hT0", identb)
        pc, pck = self.pf()
        for fc in range(NFC):
            for c in range(8):
                fw.mm(pc[:, fc * 2:(fc + 1) * 2], Wc[:, c, fc * 128:(fc + 1) * 128], hT[:, c, 126:128], c == 0, c == 7, r=["hT0", "Wc_%d" % c], w=[pck])
        fw.act(cx1[:, :, 0:2], pc[:, 0:2 * NFC].rearrange("p (f t) -> p f t", f=NFC), AF.Copy, r=[pck], w=["cx"])
        def pre(i):
            xt, xk = self.xt[i % 2], "xt%d" % (i % 2)
            fw.dma(xt[:], self.xbuf[i * 128:(i + 1) * 128, :], r=[("xb", i)], w=[xk], key=xk)
            self.norm_hT(xt, xk, 128, hTd[i % 2][:, :, :], "hT%d" % (i % 2), identb)

        def head(i):
            if i > 0:
                self.P(lambda e: e.tensor_copy(acc[0][:, 0, 0:2 * NFC].rearrange("p (f t) -> p f t", f=NFC), cx1[:, :, 128:130]), r=["cx"], w=["acc0"])
                self.P(lambda e: e.tensor_copy(cx1[:, :, 0:2], acc[0][:, 0, 0:2 * NFC].rearrange("p (f t) -> p f t", f=NFC)), r=["acc0"], w=["cx"])
            ffn_core(128, lambda c, i=i: hTd[i % 2][:, c, :], "hT%d" % (i % 2), cx1, "cx", False, aTd[i % 2], "aT%d" % (i % 2), groups=[0])

        pre(0)
        head(0)
        for i in range(NT):
            xt, xk = self.xt[i % 2], "xt%d" % (i % 2)
            hcur = lambda c, i=i: hTd[i % 2][:, c, :]
            hkk = "hT%d" % (i % 2)
            mid = (lambda i=i: pre(i + 1)) if i + 1 < NT else None
            ffn_core(128, hcur, hkk, cx1, "cx", False, aTd[i % 2], "aT%d" % (i % 2), mid, groups=list(range(4, NFC, 4)))
            if i == NT - 1:
                c_token_major(128, hcur, hkk, [(126, 128)], [O["p_conv"][l]])
            if i + 1 < NT:
                head(i + 1)
            xo_, xok = xo[i % 2], "xo%d" % (i % 2)
            finish(128, xt, xk, xo_, xok, O["yp"][i * 128:(i + 1) * 128, :], self.xbuf[i * 128:(i + 1) * 128, :], ("xb", i),
                   aTd[i % 2], "aT%d" % (i % 2))
        if not last and NSEG > 1:
            self.gather_select(xo_[:, :], [xok], D, self.agX_in, self.agX_out, "agX")
            fw.dma(self.xh_dram, xo_[:, :], r=[xok], w=["xh_dram"], key="xhst")

        i = NT
        xt, xk = self.xt[i % 2], "xt%d" % (i % 2)
        fw.dma(xt[0:MS, :], self.xsbuf, r=[("xb", i)], w=[xk], key=xk)
        self.norm_hT(xt, xk, MS, hT[:, :, 0:MS], "hT0", identb)
        hcur = lambda c: hT[:, c, 0:MS]
        fw.dma(cst[0:32, :], I["st_conv"][l], w=["ctok"], key="cst")
        for b0 in range(0, NFC, 4):
            nb = min(4, NFC - b0)
            ps, pk = self.pf()
            for q in range(nb):
                fc = b0 + q
                fw.tr(ps[:, q * 32:(q + 1) * 32], cst[0:32, fc * 128:(fc + 1) * 128], identf[0:32, 0:32], r=["ctok", "identf"], w=[pk])
            fw.act(cxs[:, b0:b0 + nb, :, 0:2], ps[:, 0:nb * 32].rearrange("p (f q j) -> p f q j", f=nb, j=2), AF.Copy, r=[pk], w=["cx"])
        ffn_core(MS, hcur, "hT0", cxs, "cx", True, aTd[0], "aT0")
        sc_ = O["s_conv"][l].rearrange("(q j) f -> j q f", j=2)
        c_token_major(MS, hcur, "hT0", [(32, 48), (48, 64)], [sc_[0], sc_[1]])
        xo_, xok = xo[i % 2], "xo%d" % (i % 2)
        finish(MS, xt, xk, xo_, xok, O["ys"], self.xsbuf, ("xb", NT), aTd[0], "aT0")


NSEG = 1


def _consts_shared():
    c = {}
    c["c_ident"] = np.eye(128, dtype=np.float32)
    inv = (10000.0 ** (-np.arange(0, HD, 2, dtype=np.float32) / HD)).astype(np.float32)
    pos_s = (PAST + np.repeat(np.arange(4), NS)).astype(np.float32)
    ang_s = pos_s[:, None] * inv[None, :]
    c["c_coss"] = np.cos(ang_s).astype(np.float32)
    c["c_sins"] = np.sin(ang_s).astype(np.float32)
    s = np.arange(128)[:, None]
    t = np.arange(128)[None, :]
    incl = (s <= t).astype(np.float32)
    strict = (s < t).astype(np.float32)
    c["c_tri"] = np.concatenate([incl * CDEC, strict * CDEC], 1).astype(np.float32)
    c["c_mask2"] = np.concatenate([incl, strict], 1).astype(np.float32)
    c["c_maskL"] = (s > t).astype(np.float32)
    i_ = np.arange(128)[:, None]
    j_ = np.arange(128)[None, :]
    cur = np.where(j_ <= i_, 0.0, NEG)
    prev = np.where(j_ > i_, 0.0, NEG)
    dead = np.full((128, 128), NEG)
    c["c_amask"] = np.concatenate([cur, prev, prev, cur, cur, dead], 1).astype(np.float32)
    c["_am_first"] = np.concatenate([cur, dead], 1).astype(np.float32)
    c["_am_mid"] = np.concatenate([cur, prev], 1).astype(np.float32)
    tt = (np.arange(32) % 4)[:, None]
    ia = np.arange(128)[None, :]
    ma = np.where(ia <= 124 + tt, 0.0, NEG)
    rb = np.arange(4)[None, :]
    mb = np.where(rb > tt, 0.0, NEG)
    c["c_smask"] = np.concatenate([ma, mb], 1).astype(np.float32)
    last = np.zeros((128, 1), np.float32)
    last[127, 0] = 1.0
    c["c_last"] = last
    c["_inv"] = inv
    return c


def _rope_tab(pos, inv):
    ang = pos.astype(np.float32)[:, None] * inv[None, :]
    return np.cos(ang).astype(np.float32), np.sin(ang).astype(np.float32)


_CACHE = {}
TAPS = False
TAP_OUT = {}


def kernel(**inp):
    inp = {k: np.asarray(v) for k, v in inp.items()}
    xp_all = inp["x_prompt"].astype(np.float32)
    B, SEQ_, _ = xp_all.shape
    TPC = SEQ_ // NSEG
    if TPC not in _CACHE:
        b_ = Builder(TPC, taps=TAPS)
        _CACHE[TPC] = (b_.build(), b_.tapnames)
    nc, tapnames = _CACHE[TPC]
    consts = _consts_shared()
    inv = consts.pop("_inv")
    am_first, am_mid = consts.pop("_am_first"), consts.pop("_am_mid")
    wnames = ["norm_mix_g", "w_in", "rwkv_mu", "rwkv_w0", "rwkv_w2", "rwkv_a0", "rwkv_a2", "rwkv_g2", "rwkv_k_k",
              "rwkv_k_a", "rwkv_ln_g", "rwkv_ln_b", "attn_sinks", "w_br_rwkv", "w_br_attn", "w_out", "norm_ffn_g",
              "ffn_w_in", "ffn_conv_w", "ffn_conv_b", "ffn_w_down", "norm_final_g"]
    shared = {n: np.ascontiguousarray(inp[n], dtype=np.float32) for n in wnames}
    shared["rwkv_r_k"] = np.ascontiguousarray(inp["rwkv_r_k"], dtype=np.float32).reshape(2, RD)
    shared.update(consts)
    in_maps = []
    ncores = 8
    for c in range(ncores):
        b, seg = (c // NSEG) % B, c % NSEG
        sl = slice(c * NS, (c + 1) * NS)
        m = dict(shared)
        t0 = seg * TPC
        m["xp"] = np.ascontiguousarray(xp_all[b, t0:t0 + TPC])
        m["xh0"] = np.ascontiguousarray(xp_all[b, t0 - 128:t0]) if seg > 0 else np.zeros((128, D), np.float32)
        m["c_cosp"], m["c_sinp"] = _rope_tab(t0 + np.arange(TPC), inv)
        m["c_cosh"], m["c_sinh"] = _rope_tab(np.maximum(t0 - 128 + np.arange(128), 0), inv)
        m["c_amask0"] = am_mid if seg > 0 else am_first
        sel = np.zeros((128, 8), np.float32)
        if seg > 0:
            sel[:, c - 1] = 1.0
        m["c_sel"] = sel
        m["xs"] = np.ascontiguousarray(inp["x_sample"][sl].transpose(1, 0, 2).reshape(MS, D))
        m["st_shift"] = np.ascontiguousarray(inp["state_rwkv_shift"][:, sl])
        m["st_wkv"] = np.ascontiguousarray(inp["state_rwkv_wkv"][:, sl]).reshape(2, 128, 4096)
        m["ck"] = np.ascontiguousarray(inp["cache_swa_k"][:, sl]).reshape(2, NS, 128, 128)
        m["cv"] = np.ascontiguousarray(inp["cache_swa_v"][:, sl]).reshape(2, NS, 128, 128)
        m["st_conv"] = np.ascontiguousarray(inp["state_ffn_conv"][:, sl]).reshape(2, 2 * NS, DFF)
        in_maps.append(m)
    res = run_bass_kernel_spmd(nc, in_maps, core_ids=list(range(ncores)))
    R = res.results
    for tn in tapnames:
        TAP_OUT[tn] = [np.asarray(R[c][tn]) for c in range(ncores)]
    f = np.float32
    lastc = [b * NSEG + NSEG - 1 for b in range(B)]
    y_prompt = np.stack([np.concatenate([R[b * NSEG + sg]["yp"] for sg in range(NSEG)], 0) for b in range(B)]).astype(f)
    y_sample = np.concatenate([R[c]["ys"].reshape(4, NS, D).transpose(1, 0, 2) for c in range(ncores)], 0).astype(f)
    p_shift = np.stack([R[c]["p_shift"] for c in lastc], 1).astype(f)
    p_wkv = np.stack([R[c]["p_wkv"] for c in lastc], 1).astype(f)
    p_k = np.stack([R[c]["p_k"] for c in lastc], 1).reshape(2, B, 128, 2, 64).astype(f)
    p_v = np.stack([R[c]["p_v"] for c in lastc], 1).reshape(2, B, 128, 2, 64).astype(f)
    p_conv = np.stack([R[c]["p_conv"] for c in lastc], 1).astype(f)
    s_shift = np.concatenate([R[c]["s_shift"] for c in range(ncores)], 1).astype(f)
    s_wkv = np.concatenate([R[c]["s_wkv"].reshape(2, NS, NH, 64, 64) for c in range(ncores)], 1).astype(f)
    s_k = np.concatenate([R[c]["s_k"].reshape(2, NS, 128, 2, 64) for c in range(ncores)], 1).astype(f)
    s_v = np.concatenate([R[c]["s_v"].reshape(2, NS, 128, 2, 64) for c in range(ncores)], 1).astype(f)
    s_conv = np.concatenate([R[c]["s_conv"].reshape(2, NS, 2, DFF) for c in range(ncores)], 1).astype(f)
    return (y_prompt, y_sample, p_shift, p_wkv, p_k, p_v, p_conv, s_shift, s_wkv, s_k, s_v, s_conv)
```

```python
import math
from contextlib import ExitStack

import numpy as np
import concourse.bass as bass
import concourse.mybir as mybir
from concourse.bass_utils import run_bass_kernel_spmd

F32 = mybir.dt.float32
BF = mybir.dt.bfloat16
AF = mybir.ActivationFunctionType
ALU = mybir.AluOpType
AX = mybir.AxisListType

ENGS = ["sp", "pe", "act", "dve", "pool"]
DEBUG_WHERE = True

D = 1024
HD = 64
NH = 8
RD = 512
RP = 1792
INP = 4608
DFF = 2816
NFC = 22
NS = 16
MS = 64
PAST = 16384
CDEC = -math.exp(-0.5)
NEG = -30000.0


class FW:
    def __init__(self, nc, es):
        self.nc = nc
        self.es = es
        self.ops = {e: [] for e in ENGS}
        self.lastw = {}
        self.readers = {}
        self.dma_count = {}
        self.inc = {}

    def sb(self, name, shape, dt=F32):
        return self.es.enter_context(self.nc.sbuf_tensor(name, list(shape), dt))

    def ps(self, name, shape, dt=F32):
        return self.es.enter_context(self.nc.psum_tensor(name, list(shape), dt))

    def capture(self, f):
        self.cap = []
        f()
        log, self.cap = self.cap, None
        return log

    def replay(self, logs, chunk=2):
        logs = [list(lg) for lg in logs if lg]
        if not logs:
            return
        mn = min(len(lg) for lg in logs)
        per = [max(1, int(round(chunk * len(lg) / mn))) for lg in logs]
        pos = [0] * len(logs)
        while any(p < len(lg) for p, lg in zip(pos, logs)):
            for k, lg in enumerate(logs):
                for _ in range(per[k]):
                    if pos[k] < len(lg):
                        self.op(*lg[pos[k]])
                        pos[k] += 1

    def op(self, eng, fn, r=(), w=(), dma=None):
        if getattr(self, "cap", None) is not None:
            self.cap.append((eng, fn, tuple(r), tuple(w), dma))
            return
        ops = self.ops[eng]
        idx = len(ops)
        deps = set()
        pr = [k for k in r if isinstance(k, str) and k[:2] in ("ps", "pb") and k[2:].isdigit()]
        if pr:
            r = [k for k in r if k not in pr]
            w = list(w) + pr
        for k in r:
            t = self.lastw.get(k)
            if t is not None:
                deps.add(t)
        for k in w:
            t = self.lastw.get(k)
            if t is not None:
                deps.add(t)
            for t2 in self.readers.get(k, {}).values():
                deps.add(t2)
        if dma is not None:
            c = self.dma_count.get(dma, 0) + 1
            self.dma_count[dma] = c
            tok = ("d", dma, c)
        else:
            tok = ("c", eng, idx)
        if eng == "pe":
            deps = {d for d in deps if not (d[0] == "c" and d[1] == "pe")}
        deps.discard(tok)
        rec = dict(fn=fn, deps=deps, tok=tok, signal=False)
        if DEBUG_WHERE:
            import sys as _s
            f_ = _s._getframe(1)
            wh = []
            while f_ is not None and len(wh) < 4:
                wh.append(f_.f_lineno)
                f_ = f_.f_back
            rec["where"] = wh
        ops.append(rec)
        for d in deps:
            if d[0] == "c":
                self.ops[d[1]][d[2]]["signal"] = True
        for k in w:
            self.lastw[k] = tok
            self.readers[k] = {}
        for k in r:
            rk = ("d", tok[1]) if tok[0] == "d" else tok[1]
            self.readers.setdefault(k, {})[rk] = tok
        return tok

    def fence(self):
        toks = set()
        for e in ENGS:
            for rec in reversed(self.ops[e]):
                if rec["tok"][0] == "c" and rec["fn"] is not None:
                    toks.add(rec["tok"])
                    rec["signal"] = True
                    break
        for k, c in self.dma_count.items():
            toks.add(("d", k, c))
        for e in ENGS:
            self.ops[e].append(dict(fn=None, deps=set(toks), tok=("c", e, len(self.ops[e])), signal=False))

    def dma(self, out, in_, r=(), w=(), key=None, eng="sp", **kw):
        self.op(eng, lambda e: e.dma_start(out=out, in_=in_, **kw), r=r, w=w, dma=key)

    def mm(self, out, lhsT, rhs, start, stop, r=(), w=()):
        self.op("pe", lambda e: e.matmul(out, lhsT, rhs, start=start, stop=stop), r=r, w=w)

    def tr(self, out, in_, ident, r=(), w=()):
        self.op("pe", lambda e: e.transpose(out, in_, ident), r=r, w=w)

    def act(self, out, in_, func, r=(), w=(), **kw):
        self.op("act", lambda e: e.activation(out, in_, func, **kw), r=r, w=w)

    def emit(self):
        nc = self.nc
        sems = {e: self.es.enter_context(nc.semaphore("s_" + e)) for e in ENGS}
        dsems = {}
        for i, k in enumerate(self.dma_count):
            dsems[k] = self.es.enter_context(nc.semaphore("d%d" % i))
        for e in ENGS:
            c = 0
            for rec in self.ops[e]:
                if rec["signal"] and rec["tok"][0] == "c":
                    c += 1
                rec["sigval"] = c
        final_counts = dict(self.dma_count)

        def run(engname, eng):
            waited = {}
            for rec in self.ops[engname]:
                need = {}
                for d in rec["deps"]:
                    if d[0] == "c":
                        s = ("c", d[1])
                        v = self.ops[d[1]][d[2]]["sigval"]
                    else:
                        s = ("d", d[1])
                        v = self.inc.get(d[1], 16) * d[2]
                    if need.get(s, 0) < v:
                        need[s] = v
                for s, v in need.items():
                    if waited.get(s, 0) >= v:
                        continue
                    waited[s] = v
                    eng.wait_ge(sems[s[1]] if s[0] == "c" else dsems[s[1]], v)
                if rec["fn"] is None:
                    continue
                try:
                    ins = rec["fn"](eng)
                except Exception:
                    print("EMIT FAILURE at lines", rec.get("where"), "engine", engname)
                    raise
                if rec["tok"][0] == "d":
                    ins.then_inc(dsems[rec["tok"][1]], self.inc.get(rec["tok"][1], 16))
                elif rec["signal"]:
                    ins.then_inc(sems[engname], 1)
            if engname == "sp":
                for k, c in final_counts.items():
                    v = self.inc.get(k, 16) * c
                    if waited.get(("d", k), 0) < v:
                        eng.wait_ge(dsems[k], v)

        with nc.Block() as block:
            @block.sync
            def _(e):
                run("sp", e)

            @block.tensor
            def _(e):
                run("pe", e)

            @block.scalar
            def _(e):
                run("act", e)

            @block.vector
            def _(e):
                run("dve", e)

            @block.gpsimd
            def _(e):
                run("pool", e)


def bc3(ap2, n):
    s = list(ap2.shape)
    return ap2.unsqueeze(2).to_broadcast([s[0], s[1], n])


def h3(ap2, h=NH):
    return ap2.rearrange("p (h d) -> p h d", h=h)


class Builder:
    def __init__(self, TP, taps=False):
        self.TP = TP
        self.NT = TP // 128
        self.taps = taps
        self.nc = bass.Bass("TRN2", target_bir_lowering=False)
        self.I = {}
        self.O = {}
        self.psi = 0
        self.pbi = 0
        self.tapnames = []
        self.pool = None
        self.pcnt = {}

    def din(self, n, s):
        self.I[n] = self.nc.dram_tensor(n, list(s), F32, kind="ExternalInput").ap()

    def dout(self, n, s):
        self.O[n] = self.nc.dram_tensor(n, list(s), F32, kind="ExternalOutput").ap()

    def declare(self):
        TP = self.TP
        for n, s in [("xp", (TP, D)), ("xs", (MS, D)), ("st_shift", (2, NS, RP)), ("st_wkv", (2, 128, 4096)),
                     ("ck", (2, NS, 128, 128)), ("cv", (2, NS, 128, 128)), ("st_conv", (2, 2 * NS, DFF)),
                     ("norm_mix_g", (2, D)), ("w_in", (2, D, INP)), ("rwkv_mu", (2, RP)), ("rwkv_w0", (2, RD)),
                     ("rwkv_w2", (2, 64, RD)), ("rwkv_a0", (2, RD)), ("rwkv_a2", (2, 64, RD)),
                     ("rwkv_g2", (2, 128, RD)), ("rwkv_k_k", (2, RD)), ("rwkv_k_a", (2, RD)),
                     ("rwkv_r_k", (2, RD)), ("rwkv_ln_g", (2, RD)), ("rwkv_ln_b", (2, RD)),
                     ("attn_sinks", (2, NH)), ("w_br_rwkv", (2, RD, D)), ("w_br_attn", (2, RD, D)),
                     ("w_out", (2, D, D)), ("norm_ffn_g", (2, D)), ("ffn_w_in", (2, D, 2 * DFF)),
                     ("ffn_conv_w", (2, 3, DFF)), ("ffn_conv_b", (2, DFF)), ("ffn_w_down", (2, DFF, D)),
                     ("norm_final_g", (D,)),
                     ("c_ident", (128, 128)), ("c_cosp", (TP, 32)), ("c_sinp", (TP, 32)),
                     ("c_coss", (MS, 32)), ("c_sins", (MS, 32)), ("c_tri", (128, 256)),
                     ("c_mask2", (128, 256)), ("c_maskL", (128, 128)), ("c_amask", (128, 768)),
                     ("c_smask", (32, 132)), ("c_last", (128, 1)),
                     ("xh0", (128, D)), ("c_cosh", (128, 32)), ("c_sinh", (128, 32)), ("c_amask0", (128, 256)), ("c_sel", (128, 8))]:
            self.din(n, s)
        for n, s in [("yp", (TP, D)), ("ys", (MS, D)), ("p_shift", (2, RP)), ("p_wkv", (2, NH, 64, 64)),
                     ("p_k", (2, 128, 128)), ("p_v", (2, 128, 128)), ("p_conv", (2, 2, DFF)),
                     ("s_shift", (2, NS, RP)), ("s_wkv", (2, 128, 4096)), ("s_k", (2, NS, 128, 128)),
                     ("s_v", (2, NS, 128, 128)), ("s_conv", (2, 2 * NS, DFF))]:
            self.dout(n, s)
        nc = self.nc
        self.xbuf = nc.dram_tensor("xbuf", [TP, D], F32).ap()
        self.xsbuf = nc.dram_tensor("xsbuf", [MS, D], F32).ap()
        self.mrbuf = nc.dram_tensor("mrbuf", [self.NT + 1, 128, 1024], BF).ap()
        self.xh_dram = nc.dram_tensor("xh_dram", [128, D], F32).ap()
        self.sq = nc.dram_tensor("sq", [6, MS, RD], F32).ap()
        self.sy = nc.dram_tensor("sy", [MS, RD], F32).ap()

    def alloc(self, name, shape, dt=F32):
        shape = list(shape)
        n = 1
        for d_ in shape[1:]:
            n *= d_
        nbytes = n * (4 if dt == F32 else 2)
        nw = (nbytes + 31) // 32 * 8
        off = self.aoff
        self.aoff += nw
        self.apeak = max(self.apeak, self.aoff)
        assert self.aoff <= self.ASZ, "SBUF arena overflow: %s needs %d words (limit %d)" % (name, self.aoff, self.ASZ)
        ap = self.arena[0:shape[0], off:off + nw]
        if dt != F32:
            ap = ap.bitcast(dt)
        ap = ap[:, 0:n]
        if len(shape) > 2:
            names = ["d%d" % i for i in range(len(shape) - 1)]
            pat = "p (%s) -> p %s" % (" ".join(names), " ".join(names))
            ap = ap.rearrange(pat, **{names[i]: shape[i + 1] for i in range(len(names))})
        return ap

    def release(self, mark):
        self.fw.fence()
        self.aoff = mark

    def pf(self):
        ids = {None: [0, 1, 2, 3, 4, 5], 0: [0, 1, 2], 1: [3, 4, 5]}[self.pool]
        c = self.pcnt.setdefault(("f", self.pool), 0)
        self.pcnt[("f", self.pool)] = c + 1
        k = ids[c % len(ids)]
        return self.PS[k], "ps%d" % k

    def pb(self):
        ids = {None: [0, 1], 0: [0], 1: [1]}[self.pool]
        c = self.pcnt.setdefault(("b", self.pool), 0)
        self.pcnt[("b", self.pool)] = c + 1
        k = ids[c % len(ids)]
        return self.PBK[k], "pb%d" % k

    def tap(self, name, ap, rkeys, dt=F32):
        if not self.taps:
            return
        shp = list(ap.shape)
        t = self.nc.dram_tensor("tap_" + name, shp, dt, kind="ExternalOutput").ap()
        self.tapnames.append("tap_" + name)
        self.fw.dma(t, ap, r=rkeys, key="tap_" + name)

    def V(self, fn, r=(), w=()):
        self.fw.op("dve", fn, r, w)

    def P(self, fn, r=(), w=()):
        self.fw.op("pool", fn, r, w)

    def col_load(self, dst, dkey, vec, n):
        fw = self.fw
        st = self.cstage
        fw.dma(st[0:n, :], vec.rearrange("(c p) -> c p", p=128), w=["cstage"], key="cstage")
        ps, pk = self.pf()
        fw.tr(ps[:, 0:n], st[0:n, :], self.identf[0:n, 0:n], r=["cstage", "identf"], w=[pk])
        fw.act(dst, ps[:, 0:n], AF.Copy, r=[pk], w=[dkey])

    def gather_select(self, src_ap, src_keys, n, ag_in, ag_out, name):
        fw = self.fw
        fw.dma(ag_in, src_ap, r=src_keys, w=[name + "_in"], key=name + "_st")
        self.gi = getattr(self, "gi", 0)
        ck = name + "_cc"
        fw.inc[ck] = 1
        fw.op("pool", lambda e: e.collective_compute("AllGather", ALU.bypass, replica_groups=[list(range(8))], ins=[ag_in], outs=[ag_out]),
              r=[name + "_in"], w=[name + "_out"], dma=ck)
        for r_ in range(8):
            st, sk = self.xt[r_ % 2], "xt%d" % (r_ % 2)
            fw.dma(st[:, 0:n], ag_out[r_ * 128:(r_ + 1) * 128, :], r=[name + "_out"], w=[sk], key=sk)
            if r_ == 0:
                self.V(lambda e, st=st: e.tensor_scalar(src_ap, st[:, 0:n], self.sel[:, 0:1], None, ALU.mult), r=[sk, "sel"], w=src_keys)
            else:
                self.V(lambda e, st=st, r_=r_: e.scalar_tensor_tensor(src_ap, st[:, 0:n], self.sel[:, r_:r_ + 1], src_ap, ALU.mult, ALU.add),
                       r=[sk, "sel"] + list(src_keys), w=src_keys)

    def bcast_load(self, dst, dkey, vec):
        self.fw.dma(dst, vec.partition_broadcast(dst.shape[0]), w=[dkey], key=dkey)

    def prep_w(self, nchunks, ncols, src, dst, dkey, mode, scale=None, mul=None, mulkey=None, sview=None):
        fw = self.fw
        for c in range(nchunks):
            for s0 in range(0, ncols, 2048):
                n = min(2048, ncols - s0)
                k = self.wst_i % 4
                self.wst_i += 1
                st = self.wstage[k]
                sk = "wst%d" % k
                fw.dma(st[:, 0:n], src(c, s0, n), w=[sk], key=sk)
                o = dst(c, s0, n)
                dk = dkey(c)
                if sview is not None:
                    sv_ = sview(st[:, 0:n])
                    sc = scale(c)
                    self.V(lambda eg, o=o, sv_=sv_, sc=sc: eg.tensor_scalar(o, sv_, sc, None, ALU.mult), r=[sk, "gcol"], w=[dk])
                    continue
                if mode == "plain":
                    e = ["dve", "pool", "act"][self.wst_i % 3]
                    if e == "act":
                        fw.act(o, st[:, 0:n], AF.Copy, r=[sk], w=[dk])
                    else:
                        fw.op(e, lambda eg, o=o, st=st, n=n: eg.tensor_copy(o, st[:, 0:n]), r=[sk], w=[dk])
                elif mode == "col":
                    sc = scale(c)
                    e = ["dve", "pool"][self.wst_i % 2]
                    fw.op(e, lambda eg, o=o, st=st, n=n, sc=sc: eg.tensor_scalar(o, st[:, 0:n], sc, None, ALU.mult),
                          r=[sk, "gcol"], w=[dk])
                else:
                    sc = scale(c)
                    m = mul(s0, n)
                    self.V(lambda eg, o=o, st=st, n=n, sc=sc, m=m: eg.scalar_tensor_tensor(
                        o, st[:, 0:n], sc, m, ALU.mult, ALU.mult), r=[sk, "gcol", mulkey], w=[dk])

    def norm_hT(self, xt, xk, M, hdst, hkey, identb):
        self.norm_a(xt, xk, M)
        self.norm_b(M, hdst, hkey, identb)

    def norm_a(self, xt, xk, M):
        fw = self.fw
        xn, ss, t1 = self.xn, self.ss, self.t1
        fw.act(xn[0:M, :], xt[0:M, :], AF.Square, r=[xk], w=["xn", "ss"], accum_out=ss[0:M, :])
        self.V(lambda e: e.tensor_scalar(t1[0:M, :], ss[0:M, :], 1.0 / D, 1e-6, ALU.mult, ALU.add), r=["ss"], w=["t1"])
        fw.act(t1[0:M, :], t1[0:M, :], AF.Sqrt, r=["t1"], w=["t1"])
        self.V(lambda e: e.reciprocal(t1[0:M, :], t1[0:M, :]), r=["t1"], w=["t1"])
        self.V(lambda e: e.tensor_scalar(xn[0:M, :], xt[0:M, :], t1[0:M, 0:1], None, ALU.mult), r=[xk, "t1"], w=["xn"])

    def norm_b(self, M, hdst, hkey, identb):
        fw = self.fw
        xn = self.xn
        pbk, pk = self.pb()
        for c in range(8):
            fw.tr(pbk[:, c * M:(c + 1) * M], xn[0:M, c * 128:(c + 1) * 128], identb[0:M, 0:M], r=["xn", "identb"], w=[pk])
        fw.act(hdst, pbk[:, 0:8 * M].rearrange("p (c t) -> p c t", c=8), AF.Copy, r=[pk], w=[hkey])

    def build(self):
        self.declare()
        nc = self.nc
        with ExitStack() as es:
            self.fw = fw = FW(nc, es)
            self.PS = [fw.ps("ps%d" % i, [128, 512], F32) for i in range(6)]
            self.PBK = [fw.ps("pb%d" % i, [128, 1024], BF) for i in range(2)]
            self.ASZ = 52224
            self.arena = fw.sb("arena", [128, self.ASZ])
            self.aoff = 0
            self.apeak = 0
            self.identf = self.alloc("identf", [128, 128])
            self.identb = self.alloc("identb", [128, 128], BF)
            self.cstage = self.alloc("cstage", [32, 128])
            self.wst_i = 0
            self.xn = self.alloc("xn", [128, D], BF)
            self.ss = self.alloc("ss", [128, 1])
            self.t1 = self.alloc("t1", [128, 1])
            self.gcol = self.alloc("gcol", [128, 8])
            self.xt = [self.alloc("xt%d" % i, [128, D]) for i in range(2)]
            self.sel = self.alloc("sel", [128, 8])
            fw.dma(self.sel[:], self.I["c_sel"], w=["sel"], key="sel")
            fw.dma(self.identf[:], self.I["c_ident"], w=["identf"], key="identf")
            self.V(lambda e: e.tensor_copy(self.identb[:], self.identf[:]), r=["identf"], w=["identb"])
            for l in range(2):
                for p_ in (self.pass_rwkv, self.pass_attn, self.pass_ffn):
                    mk_ = self.aoff
                    p_(l, None)
                    self.release(mk_)
            print("arena peak words", self.apeak, "of", self.ASZ)
            fw.emit()
        return nc

    def sbl(self, es2, name, shape, dt=F32):
        return self.alloc(name, shape, dt)

    def xsrc(self, l, i):
        if i < self.NT:
            src = self.I["xp"] if l == 0 else self.xbuf
            return src[i * 128:(i + 1) * 128, :], ("xb", i)
        src = self.I["xs"] if l == 0 else self.xsbuf
        return src, ("xb", i)

    def pass_rwkv(self, l, es2):
        fw, I, O, NT = self.fw, self.I, self.O, self.NT
        sbl = lambda n, s, dt=F32: self.sbl(es2, "r%d_" % l + n, s, dt)
        identb, identf = self.identb, self.identf
        W1 = sbl("W1", [128, 8, RP], BF)
        W2 = sbl("W2", [128, 8, RP], BF)
        Wg = sbl("Wg", [128, 8, D], BF)
        Wr = sbl("Wr", [128, 4, D], BF)
        lw2 = sbl("lw2", [128, RD], BF)
        lg2 = sbl("lg2", [128, RD], BF)
        bcs = {}
        for n in ["rwkv_w0", "rwkv_a0", "rwkv_k_k", "rwkv_k_a", "rwkv_r_k", "rwkv_ln_g", "rwkv_ln_b"]:
            bcs[n] = sbl(n, [128, RD])
            self.bcast_load(bcs[n][:], n + "_bc", I[n][l])
        mucol = sbl("mucol", [128, 2])
        tri = sbl("tri", [128, 256])
        mask2 = sbl("mask2", [128, 256])
        maskL = sbl("maskL", [128, 128])
        clast = sbl("clast", [128, 1])
        fw.dma(tri[:], I["c_tri"], w=["tri"], key="tri")
        fw.dma(mask2[:], I["c_mask2"], w=["mask2"], key="mask2")
        fw.dma(maskL[:], I["c_maskL"], w=["maskL"], key="maskL")
        fw.dma(clast[:], I["c_last"], w=["clast"], key="clast")
        self.col_load(self.gcol[:], "gcol", I["norm_mix_g"][l], 8)
        self.col_load(mucol[:], "mucol", I["rwkv_mu"][l, 1536:1792], 2)
        m0 = self.aoff
        self.wstage = [sbl("wst%d" % i_, [128, 2048]) for i_ in range(4)]
        mu_bc = sbl("mu_bc", [128, RP])
        omm_bc = sbl("omm_bc", [128, RP])
        self.bcast_load(mu_bc[:], "mu_bc", I["rwkv_mu"][l])
        self.V(lambda e: e.tensor_scalar(omm_bc[:], mu_bc[:], -1.0, 1.0, ALU.mult, ALU.add), r=["mu_bc"], w=["omm_bc"])
        win = I["w_in"][l]
        gsc = lambda c: self.gcol[:, c:c + 1]
        self.prep_w(8, RP, lambda c, s0, n: win[c * 128:(c + 1) * 128, s0:s0 + n],
                    lambda c, s0, n: W1[:, c, s0:s0 + n], lambda c: "W1_%d" % c, "colmul", gsc,
                    lambda s0, n: omm_bc[:, s0:s0 + n], "omm_bc")
        self.prep_w(8, RP, lambda c, s0, n: win[c * 128:(c + 1) * 128, s0:s0 + n],
                    lambda c, s0, n: W2[:, c, s0:s0 + n], lambda c: "W2_%d" % c, "colmul", gsc,
                    lambda s0, n: mu_bc[:, s0:s0 + n], "mu_bc")
        self.prep_w(8, D, lambda c, s0, n: win[c * 128:(c + 1) * 128, 2560 + s0:2560 + s0 + n],
                    lambda c, s0, n: Wg[:, c, s0:s0 + n], lambda c: "Wg_%d" % c, "col", gsc)
        wbr = I["w_br_rwkv"][l]
        self.prep_w(4, D, lambda c, s0, n: wbr[c * 128:(c + 1) * 128, s0:s0 + n],
                    lambda c, s0, n: Wr[:, c, s0:s0 + n], lambda c: "Wr_%d" % c, "plain")
        for (nm, p0, dk_) in [("rwkv_w2", 0, "lw2a"), ("rwkv_a2", 64, "lw2b")]:
            k = self.wst_i % 4
            self.wst_i += 1
            wsk = self.wstage[k]
            fw.dma(wsk[p0:p0 + 64, 0:RD], I[nm][l], w=["wst%d" % k], key="wst%d" % k)
            self.P(lambda e, wsk=wsk, p0=p0: e.tensor_copy(lw2[p0:p0 + 64, :], wsk[p0:p0 + 64, 0:RD]), r=["wst%d" % k], w=[dk_])
        self.prep_w(1, RD, lambda c, s0, n: I["rwkv_g2"][l], lambda c, s0, n: lg2[:, :], lambda c: "lg2", "plain")
        WK1 = ["W1_%d" % c for c in range(8)]
        WK2 = ["W2_%d" % c for c in range(8)]
        self.release(m0)
        class NSP:
            pass
        zr, zk = sbl("zr", [128, RD]), sbl("zk", [128, RD])
        lact = sbl("lact", [128, 128], BF)
        T = [sbl("tmp%d" % i_, [128, RD]) for i_ in range(8)]
        sm = sbl("sm", [128, 64])
        orT = sbl("orT", [128, 4, 128], BF)
        sgr = sbl("sgr", [128, 8, 128], BF)
        mrT0_ = sbl("mrT0", [128, 8, 128], BF)
        mrT = [mrT0_, mrT0_]
        TP_ = [sbl("tpost%d" % i_, [128, RD]) for i_ in range(2)]
        m1 = self.aoff
        NRB = 9864

        def mkrec(k):
            R = NSP()
            rb = sbl("RB%d" % k, [128, NRB], BF)
            rf = sbl("RF%d" % k, [128, 528])
            R.rb, R.rf, R.k = rb, rf, k
            R.RKT = rb[:, 0:1024].rearrange("p (j a t) -> p j a t", j=4, a=2)
            R.G4 = [rb[:, 1024 + j * 1280:1024 + (j + 1) * 1280].rearrange("p (h c) -> p h c", h=2) for j in range(4)]
            R.ZF = [rb[:, 6144 + j * 256:6144 + (j + 1) * 256].rearrange("p (h c) -> p h c", h=2) for j in range(4)]
            R.vb, R.ktt, R.bnt = rb[:, 7168:7680], rb[:, 7680:8192], rb[:, 8192:8704]
            R.sgT = rb[:, 8704:8832]
            R.hT = rb[:, 8832:9864].rearrange("p (c t) -> p c t", c=8)
            R.zv, R.WC, R.bon = rf[:, 0:512], rf[:, 512:516], rf[:, 516:524]
            R.K = (lambda k_: (lambda n: "%s#%d" % (n, k_)))(k)
            return R
        R0 = mkrec(0)
        U0b = [sbl("U0b%d" % j, [128, 2, 64], BF) for j in range(4)]
        Ub = sbl("Ub", [128, RD], BF)
        Nst = sbl("Nst", [128, 4, 128])
        Nb = sbl("Nb", [128, 4, 128], BF)
        self.V(lambda e: e.memset(Nst[:], 0.0), w=["Nst"])
        self.V(lambda e: e.memset(Nb[:], 0.0), w=["Nb"])
        m2 = self.aoff
        rt, kat = sbl("rt", [128, RD], BF), sbl("kat", [128, RD], BF)
        KT = sbl("KT", [128, 4, 128], BF)
        BT = sbl("BT", [128, 4, 128], BF)
        for j in range(4):
            self.P(lambda e, j=j: e.tensor_copy(R0.G4[j][:, :, 512:640], identb[:, :].unsqueeze(1).to_broadcast([128, 2, 128])),
                   r=["identb"], w=["G4_%d" % j])
        EZ = [[sbl("EZ%d_%d" % (j, a), [128, 2, 2, 128], BF) for a in range(2)] for j in range(4)]
        FFa = [sbl("FFa%d" % a, [128, 4, 2, 128], BF) for a in range(2)]
        FF = [[FFa[a][:, j] for a in range(2)] for j in range(4)]

        def tok_proj(M, hcur, hprev, hk, g0, dstkey):
            ps, pk = self.pf()
            n = 0
            for c in range(8):
                fw.mm(ps[0:M, :], hcur(c), W1[:, c, g0:g0 + 512], n == 0, False, r=[hk, WK1[c]], w=[pk])
                n += 1
            for c in range(8):
                fw.mm(ps[0:M, :], hprev(c), W2[:, c, g0:g0 + 512], False, c == 7, r=[hk, WK2[c]], w=[pk])
            return ps, pk

        def feat_proj(M, hcur, hprev, hk, g0):
            ps, pk = self.pf()
            for c in range(8):
                fw.mm(ps[:, 0:M], W1[:, c, g0:g0 + 128], hcur(c), c == 0, False, r=[hk, WK1[c]], w=[pk])
            for c in range(8):
                fw.mm(ps[:, 0:M], W2[:, c, g0:g0 + 128], hprev(c), False, c == 7, r=[hk, WK2[c]], w=[pk])
            return ps, pk

        def raw_last(hl, hk, M, dst):
            for gi, g0 in enumerate(range(0, RP, 512)):
                n = min(512, RP - g0)
                ps, pk = self.pf()
                for c in range(8):
                    fw.mm(ps[0:M, 0:n], hl(c), W1[:, c, g0:g0 + n], c == 0, False, r=[hk, WK1[c]], w=[pk])
                for c in range(8):
                    fw.mm(ps[0:M, 0:n], hl(c), W2[:, c, g0:g0 + n], False, c == 7, r=[hk, WK2[c]], w=[pk])
                fw.act(T[gi][0:M, 0:n], ps[0:M, 0:n], AF.Copy, r=[pk], w=["T%d" % gi])
                fw.dma(dst[:, g0:g0 + n], T[gi][0:M, 0:n], r=["T%d" % gi], key="zl%d" % gi)

        def prep(M, sample, R):
            K = R.K
            w0, a0 = bcs["rwkv_w0"], bcs["rwkv_a0"]
            kkb, kab, rkb = bcs["rwkv_k_k"], bcs["rwkv_k_a"], bcs["rwkv_r_k"]
            pw, pwk = self.pf()
            fw.mm(pw[0:M, :], lact[0:64, 0:M], lw2[0:64, :], True, True, r=["lact", "lw2a"], w=[pwk])
            pa, pak = self.pf()
            fw.mm(pa[0:M, :], lact[64:128, 0:M], lw2[64:128, :], True, True, r=["lact", "lw2b"], w=[pak])
            sg, a_, kk, t3, kf, be = T[0], T[1], T[2], T[3], T[4], T[5]
            self.V(lambda e: e.tensor_tensor(sg[0:M, :], pw[0:M, :], w0[0:M, :], ALU.add), r=[pwk, "rwkv_w0_bc"], w=["T0"])
            fw.act(sg[0:M, :], sg[0:M, :], AF.Sigmoid, r=["T0"], w=["T0"])
            self.V(lambda e: e.tensor_tensor(a_[0:M, :], pa[0:M, :], a0[0:M, :], ALU.add), r=[pak, "rwkv_a0_bc"], w=["T1"])
            fw.act(a_[0:M, :], a_[0:M, :], AF.Sigmoid, r=["T1"], w=["T1"])
            self.P(lambda e: e.tensor_tensor(kk[0:M, :], zk[0:M, :], kkb[0:M, :], ALU.mult), r=["zk", "rwkv_k_k_bc"], w=["T2"])
            self.P(lambda e: e.tensor_tensor(t3[0:M, :], kk[0:M, :], kk[0:M, :], ALU.mult), r=["T2"], w=["T3"])
            self.V(lambda e: e.tensor_reduce(sm[0:M, 0:8], h3(t3[0:M, :]), AX.X, ALU.add), r=["T3"], w=["sm0"])
            fw.act(sm[0:M, 0:8], sm[0:M, 0:8], AF.Sqrt, r=["sm0"], w=["sm0"])
            self.V(lambda e: e.tensor_scalar(sm[0:M, 0:8], sm[0:M, 0:8], 1e-12, None, ALU.max), r=["sm0"], w=["sm0"])
            self.V(lambda e: e.reciprocal(sm[0:M, 0:8], sm[0:M, 0:8]), r=["sm0"], w=["sm0"])
            self.V(lambda e: e.tensor_tensor(h3(kk[0:M, :]), h3(kk[0:M, :]), bc3(sm[0:M, 0:8], 64), ALU.mult),
                   r=["T2", "sm0"], w=["T2"])
            self.V(lambda e: e.scalar_tensor_tensor(t3[0:M, :], a_[0:M, :], -1.0, kab[0:M, :], ALU.add, ALU.mult),
                   r=["T1", "rwkv_k_a_bc"], w=["T3"])
            self.V(lambda e: e.scalar_tensor_tensor(kf[0:M, :], t3[0:M, :], 1.0, zk[0:M, :], ALU.add, ALU.mult),
                   r=["T3", "zk"], w=["T4"])
            self.P(lambda e: e.tensor_tensor(be[0:M, :], kk[0:M, :], a_[0:M, :], ALU.mult), r=["T2", "T1"], w=["T5"])
            self.P(lambda e: e.tensor_tensor(t3[0:M, :], zr[0:M, :], kf[0:M, :], ALU.mult), r=["zr", "T4"], w=["T3"])
            self.P(lambda e: e.tensor_tensor(t3[0:M, :], t3[0:M, :], rkb[0:M, :], ALU.mult), r=["T3", "rwkv_r_k_bc"], w=["T3"])
            self.V(lambda e, R=R: e.tensor_reduce(R.bon[0:M, :], h3(t3[0:M, :]), AX.X, ALU.add), r=["T3"], w=[K("bon")])
            if sample:
                fw.act(T[6][0:M, :], sg[0:M, :], AF.Exp, r=["T0"], w=["T6"], scale=CDEC)
                for x, (tl, tk) in enumerate([(zr, "zr"), (T[6], "T6"), (kf, "T4"), (R.zv, K("zv")), (kk, "T2"), (be, "T5")]):
                    fw.dma(self.sq[x], tl[0:M, :], r=[tk], w=[("sq", x)], key="sqw%d" % x)
                return
            pli, plik = self.pf()
            fw.mm(pli[:, :], tri[:, 0:128], sg[:, :], True, True, r=["tri", "T0"], w=[plik])
            ple, plek = self.pf()
            fw.mm(ple[:, :], tri[:, 128:256], sg[:, :], True, True, r=["tri", "T0"], w=[plek])
            eL, eLm, enL = T[6], T[7], T[3]
            fw.act(eL[:, :], pli[:, :], AF.Exp, r=[plik], w=["T6"])
            fw.act(eLm[:, :], ple[:, :], AF.Exp, r=[plek], w=["T7"])
            fw.act(enL[:, :], pli[:, :], AF.Exp, r=[plik], w=["T3"], scale=-1.0)
            self.V(lambda e: e.tensor_tensor(rt[:, :], zr[:, :], eL[:, :], ALU.mult), r=["zr", "T6"], w=["rt"])
            self.V(lambda e: e.tensor_tensor(kat[:, :], kk[:, :], eLm[:, :], ALU.mult), r=["T2", "T7"], w=["kat"])
            self.P(lambda e, R=R: e.tensor_tensor(R.ktt[:, :], kf[:, :], enL[:, :], ALU.mult), r=["T4", "T3"], w=[K("ktt")])
            self.V(lambda e, R=R: e.scalar_tensor_tensor(R.bnt[:, :], be[:, :], -1.0, enL[:, :], ALU.mult, ALU.mult),
                   r=["T5", "T3"], w=[K("bnt")])
            fw.act(R.vb[:, :], R.zv[:, :], AF.Copy, r=[K("zv")], w=[K("vb")])
            pwc, pwck = self.pf()
            for j in range(4):
                fw.mm(pwc[:, j:j + 1], eL[:, j * 128:(j + 1) * 128], clast[:, :], True, True, r=["T6", "clast"], w=[pwck])
            fw.act(R.WC[:, :], pwc[:, 0:4], AF.Copy, r=[pwck], w=[K("WC")])
            for (src, skey, dstf, dk) in [(rt, "rt", None, "RKT"), (kat, "kat", None, "RKT"),
                                          (R.ktt, K("ktt"), None, "KT"), (R.bnt, K("bnt"), None, "BT")]:
                pbk, pk = self.pb()
                for j in range(4):
                    fw.tr(pbk[:, j * 128:(j + 1) * 128], src[:, j * 128:(j + 1) * 128], identb[:, :], r=[skey, "identb"], w=[pk])
                if dk == "RKT":
                    which = 0 if skey == "rt" else 1
                    fw.act(R.RKT[:, :, which, :], pbk[:, 0:512].rearrange("p (j t) -> p j t", j=4), AF.Copy, r=[pk], w=["RKT%d" % which])
                else:
                    dst = KT if dk == "KT" else BT
                    self.V(lambda e, dst=dst, pbk=pbk: e.tensor_copy(dst[:, :, :], pbk[:, 0:512].rearrange("p (j t) -> p j t", j=4)),
                           r=[pk], w=[dk])

        def stageAB(R):
            K = R.K
            RK = [K("RKT0"), K("RKT1")]
            RKT, G4, ZF = R.RKT, R.G4, R.ZF
            zb = [self.pf(), self.pf()]
            for j in range(4):
                for hh in range(2):
                    o = hh * 64
                    pZ, pzk = zb[hh]
                    fw.mm(pZ[:, j * 128:(j + 1) * 128], RKT[o:o + 64, j, 1, :], BT[o:o + 64, j, :], True, True, r=["BT", K("RKT1")], w=[pzk])
            mlb = maskL[:, :].unsqueeze(1).to_broadcast([128, 4, 128])
            for hh in range(2):
                pZ, pzk = zb[hh]
                self.V(lambda e, pZ=pZ, hh=hh: e.tensor_tensor(FFa[0][:, :, hh, :], pZ[:, :].rearrange("p (j c) -> p j c", j=4), mlb, ALU.mult),
                       r=[pzk, "maskL"], w=["FF%d_0" % j for j in range(4)])
            for j in range(4):
                bk = [self.pf(), self.pf()]
                for hh in range(2):
                    o = hh * 64
                    ps, pk = bk[hh]
                    rhs = RKT[o:o + 64, j, :, :].rearrange("p a t -> p (a t)")
                    fw.mm(ps[:, 0:256], KT[o:o + 64, j, :], rhs, True, True, r=["KT"] + RK, w=[pk])
                    fw.mm(ps[:, 256:512], BT[o:o + 64, j, :], rhs, True, True, r=["BT"] + RK, w=[pk])
                for hh in range(2):
                    ps, pk = bk[hh]
                    self.V(lambda e, j=j, hh=hh, ps=ps, G4=G4: e.tensor_tensor(
                        G4[j][:, hh, 0:512].rearrange("p (a c) -> p a c", a=2), ps[:, :].rearrange("p (a c) -> p a c", a=2),
                        mask2[:, :].unsqueeze(1).to_broadcast([128, 2, 256]), ALU.mult), r=[pk, "mask2"], w=[K("G4_%d" % j)])
            for lev in range(7):
                a, b = lev % 2, (lev + 1) % 2
                for j in range(4):
                    fk, fn_ = "FF%d_%d" % (j, a), "FF%d_%d" % (j, b)
                    ezn = "EZ%d_%d" % (j, b)
                    if lev == 0:
                        ezk = K("G4_%d" % j)
                        EZs = lambda hh, j=j, G4=G4: G4[j][:, hh, 384:640]
                        Es = lambda hh, j=j, G4=G4: G4[j][:, hh, 384:512]
                        Zs = lambda j=j, G4=G4: G4[j][:, :, 512:640]
                    else:
                        ezk = "EZ%d_%d" % (j, a)
                        EZs = lambda hh, j=j, a=a: EZ[j][a][:, hh, :, :].rearrange("p a t -> p (a t)")
                        Es = lambda hh, j=j, a=a: EZ[j][a][:, hh, 0, :]
                        Zs = lambda j=j, a=a: EZ[j][a][:, :, 1, :]
                    if lev < 6:
                        pL, plk = self.pf()
                        for hh in range(2):
                            fw.mm(pL[:, hh * 256:(hh + 1) * 256], FF[j][a][:, hh, :], EZs(hh), True, True, r=[ezk, fk], w=[plk])
                        pF, pfk = self.pf()
                        for hh in range(2):
                            fw.mm(pF[:, hh * 128:(hh + 1) * 128], Es(hh), FF[j][a][:, hh, :], True, True, r=[ezk, fk], w=[pfk])
                        l3 = pL[:, :].rearrange("p (h c) -> p h c", h=2)
                        fw.act(EZ[j][b][:, :, 0, :], l3[:, :, 0:128], AF.Copy, r=[plk], w=[ezn])
                        self.V(lambda e, j=j, b=b, l3=l3, Zs=Zs: e.tensor_tensor(EZ[j][b][:, :, 1, :], l3[:, :, 128:256], Zs(), ALU.add),
                               r=[plk, ezk], w=[ezn])
                        fw.act(FF[j][b][:, :, :], pF[:, 0:256].rearrange("p (h c) -> p h c", h=2), AF.Copy, r=[pfk], w=[fn_])
                    else:
                        pL, plk = self.pf()
                        for hh in range(2):
                            fw.mm(pL[:, hh * 128:(hh + 1) * 128], FF[j][a][:, hh, :], EZ[j][a][:, hh, 1, :], True, True, r=[ezk, fk], w=[plk])
                        self.V(lambda e, j=j, a=a, pL=pL, ZF=ZF: e.tensor_tensor(ZF[j][:, :, :], pL[:, 0:256].rearrange("p (h c) -> p h c", h=2),
                                                                      EZ[j][a][:, :, 1, :], ALU.add), r=[plk, ezk], w=[K("ZF%d" % j)])

        def stageC(R):
            K = R.K
            RKT, G4, ZF, vb = R.RKT, R.G4, R.ZF, R.vb
            for j in range(4):
                pU, puk = self.pf()
                for hh in range(2):
                    o, h = hh * 64, 2 * j + hh
                    fw.mm(pU[:, hh * 64:(hh + 1) * 64], RKT[o:o + 64, j, 1, :], Nb[o:o + 64, j, o:o + 64], True, False, r=[K("RKT1"), "Nb"], w=[puk])
                    fw.mm(pU[:, hh * 64:(hh + 1) * 64], G4[j][:, hh, 128:256], vb[:, h * 64:(h + 1) * 64], False, True, r=[K("G4_%d" % j), K("vb")], w=[puk])
                fw.act(U0b[j][:, :, :], pU[:, 0:128].rearrange("p (h c) -> p h c", h=2), AF.Copy, r=[puk], w=["U0b%d" % j])
            for j in range(4):
                pU, puk = self.pf()
                for hh in range(2):
                    fw.mm(pU[:, hh * 64:(hh + 1) * 64], ZF[j][:, hh, :], U0b[j][:, hh, :], True, True, r=[K("ZF%d" % j), "U0b%d" % j], w=[puk])
                fw.act(Ub[:, j * 128:(j + 1) * 128], pU[:, 0:128], AF.Copy, r=[puk], w=["Ub%d" % j])

        def stageD(R):
            K = R.K
            RKT, G4, vb = R.RKT, R.G4, R.vb
            psY, pyk = self.pf()
            for j in range(4):
                for hh in range(2):
                    o, h = hh * 64, 2 * j + hh
                    fw.mm(psY[:, h * 64:(h + 1) * 64], RKT[o:o + 64, j, 0, :], Nb[o:o + 64, j, o:o + 64], True, False, r=[K("RKT0"), "Nb"], w=[pyk])
                    fw.mm(psY[:, h * 64:(h + 1) * 64], G4[j][:, hh, 0:128], vb[:, h * 64:(h + 1) * 64], False, False, r=[K("G4_%d" % j), K("vb")], w=[pyk])
                    fw.mm(psY[:, h * 64:(h + 1) * 64], G4[j][:, hh, 256:384], Ub[:, h * 64:(h + 1) * 64], False, True, r=[K("G4_%d" % j), "Ub%d" % j], w=[pyk])
            return psY, pyk

        def n_update(R):
            K = R.K
            ktt, bnt, vb, WC = R.ktt, R.bnt, R.vb, R.WC
            pN, pnk = self.pf()
            for j in range(4):
                fw.mm(pN[:, j * 128:(j + 1) * 128], ktt[:, j * 128:(j + 1) * 128], vb[:, j * 128:(j + 1) * 128], True, False, r=[K("ktt"), K("vb")], w=[pnk])
                fw.mm(pN[:, j * 128:(j + 1) * 128], bnt[:, j * 128:(j + 1) * 128], Ub[:, j * 128:(j + 1) * 128], False, True, r=[K("bnt"), "Ub%d" % j], w=[pnk])
            n2 = Nst[:, :, :].rearrange("p j c -> p (j c)")
            self.V(lambda e: e.tensor_tensor(n2, pN[:, :], n2, ALU.add), r=[pnk, "Nst"], w=["Nst"])
            self.V(lambda e, WC=WC: e.tensor_tensor(Nst[:, :, :], Nst[:, :, :], bc3(WC[:, :], 128), ALU.mult), r=["Nst", K("WC")], w=["Nst"])
            fw.act(Nb[:, :, :], Nst[:, :, :], AF.Copy, r=["Nst"], w=["Nb"])


        def post(M, yap, ykeys, pg, pgk, R):
            K = R.K
            lng, lnb = bcs["rwkv_ln_g"], bcs["rwkv_ln_b"]
            y2, yc = TP_[0], TP_[1]
            ob = TP_[0].bitcast(BF)[:, 0:RD]
            self.V(lambda e: e.tensor_reduce(sm[0:M, 16:24], h3(yap), AX.X, ALU.add), r=ykeys, w=["sm2"])
            fw.act(y2[0:M, :], yap, AF.Square, r=ykeys, w=["TP0"])
            self.V(lambda e: e.tensor_reduce(sm[0:M, 24:32], h3(y2[0:M, :]), AX.X, ALU.add), r=["TP0"], w=["sm3"])
            mean, var = sm[0:M, 16:24], sm[0:M, 24:32]
            self.V(lambda e: e.tensor_scalar(mean, mean, 1.0 / 64, None, ALU.mult), r=["sm2"], w=["sm2"])
            self.V(lambda e: e.tensor_tensor(sm[0:M, 32:40], mean, mean, ALU.mult), r=["sm2"], w=["sm4"])
            self.V(lambda e: e.scalar_tensor_tensor(var, var, 1.0 / 64, sm[0:M, 32:40], ALU.mult, ALU.subtract), r=["sm3", "sm4"], w=["sm3"])
            self.V(lambda e: e.tensor_scalar(var, var, 64e-5, None, ALU.add), r=["sm3"], w=["sm3"])
            fw.act(var, var, AF.Sqrt, r=["sm3"], w=["sm3"])
            self.V(lambda e: e.reciprocal(var, var), r=["sm3"], w=["sm3"])
            self.V(lambda e: e.tensor_tensor(h3(yc[0:M, :]), h3(yap), bc3(mean, 64), ALU.subtract), r=list(ykeys) + ["sm2"], w=["TP1"])
            self.V(lambda e: e.tensor_tensor(h3(yc[0:M, :]), h3(yc[0:M, :]), bc3(var, 64), ALU.mult), r=["TP1", "sm3"], w=["TP1"])
            self.P(lambda e: e.tensor_tensor(yc[0:M, :], yc[0:M, :], lng[0:M, :], ALU.mult), r=["TP1", "rwkv_ln_g_bc"], w=["TP1"])
            self.P(lambda e: e.tensor_tensor(yc[0:M, :], yc[0:M, :], lnb[0:M, :], ALU.add), r=["TP1", "rwkv_ln_b_bc"], w=["TP1"])
            self.P(lambda e, R=R: e.tensor_tensor(h3(y2[0:M, :]), h3(R.zv[0:M, :]), bc3(R.bon[0:M, :], 64), ALU.mult), r=[K("zv"), K("bon")], w=["TP0"])
            self.V(lambda e: e.tensor_tensor(yc[0:M, :], yc[0:M, :], y2[0:M, :], ALU.add), r=["TP1", "TP0"], w=["TP1"])
            self.V(lambda e: e.tensor_tensor(ob[0:M, :], yc[0:M, :], pg[0:M, :], ALU.mult), r=["TP1", pgk], w=["TP0"])
            pbk, pk = self.pb()
            for j in range(4):
                fw.tr(pbk[:, j * M:(j + 1) * M], ob[0:M, j * 128:(j + 1) * 128], identb[0:M, 0:M], r=["TP0", "identb"], w=[pk])
            fw.act(orT[:, :, 0:M], pbk[:, 0:4 * M].rearrange("p (j t) -> p j t", j=4), AF.Copy, r=[pk], w=["orT"])

        def gate_branch(M, hcur, hk, mdst, mkey):
            for half in range(2):
                pg, pgk = self.pf()
                for q in range(4):
                    dc = half * 4 + q
                    for c in range(8):
                        fw.mm(pg[:, q * M:(q + 1) * M], Wg[:, c, dc * 128:(dc + 1) * 128], hcur(c), c == 0, c == 7, r=[hk, "Wg_%d" % c], w=[pgk])
                fw.act(sgr[:, half * 4:(half + 1) * 4, 0:M], pg[:, 0:4 * M].rearrange("p (q t) -> p q t", q=4), AF.Sigmoid, r=[pgk], w=["sgr%d" % half])
                pbr, pbk_ = self.pf()
                for q in range(4):
                    dc = half * 4 + q
                    for j in range(4):
                        fw.mm(pbr[:, q * M:(q + 1) * M], Wr[:, j, dc * 128:(dc + 1) * 128], orT[:, j, 0:M], j == 0, j == 3, r=["orT", "Wr_%d" % j], w=[pbk_])
                self.V(lambda e, half=half, pbr=pbr: e.tensor_tensor(mdst[:, half * 4:(half + 1) * 4, 0:M], sgr[:, half * 4:(half + 1) * 4, 0:M],
                                                                 pbr[:, 0:4 * M].rearrange("p (q t) -> p q t", q=4), ALU.mult),
                       r=["sgr%d" % half, pbk_], w=[mkey])

        R1 = mkrec(1)
        for j in range(4):
            self.P(lambda e, j=j: e.tensor_copy(R1.G4[j][:, :, 512:640], identb[:, :].unsqueeze(1).to_broadcast([128, 2, 128])),
                   r=["identb"], w=[R1.K("G4_%d" % j)])
        RR = [R0, R1]

        def H1a(i):
            R, Rp = RR[i % 2], RR[(i + 1) % 2]
            hT = R.hT
            xt, xk = self.xt[i % 2], "xt%d" % (i % 2)
            src, _ = self.xsrc(l, i)
            fw.dma(xt[:], src, r=[("xb", i)], w=[xk], key=xk)
            hk = R.K("hTr")
            if i == 0:
                self.V(lambda e, hT=hT: e.memset(hT[:, :, 0:1], 0.0), w=[hk])
            else:
                self.P(lambda e, hT=hT, hp=Rp.hT: e.tensor_copy(hT[:, :, 0:1], hp[:, :, 128:129]), r=[Rp.K("hTr")], w=[hk])
            self.norm_a(xt, xk, 128)

        def H1b(i):
            R = RR[i % 2]
            K = R.K
            hT = R.hT
            hk = K("hTr")
            self.norm_b(128, hT[:, :, 1:129], hk, identb)
            hcur = lambda c, hT=hT: hT[:, c, 1:129]
            hprev = lambda c, hT=hT: hT[:, c, 0:128]
            for g0, dst, dk in [(0, zr, "zr"), (512, zk, "zk"), (1024, R.zv, K("zv"))]:
                ps, pk = tok_proj(128, hcur, hprev, hk, g0, dk)
                fw.act(dst[:, :], ps[:, :], AF.Copy, r=[pk], w=[dk])
            ps, pk = feat_proj(128, hcur, hprev, hk, 1536)
            fw.act(lact[0:64, :], ps[0:64, 0:128], AF.Tanh, r=[pk], w=["lact"])
            fw.act(lact[64:128, :], ps[64:128, 0:128], AF.Copy, r=[pk], w=["lact"])
            ps, pk = feat_proj(128, hcur, hprev, hk, 1664)
            fw.act(R.sgT[:, :], ps[:, 0:128], AF.Sigmoid, r=[pk], w=[K("sgT")])
            if i == NT - 1:
                raw_last(lambda c, hT=hT: hT[:, c, 128:129], hk, 1, O["p_shift"][l:l + 1, :])

        def H1c(i):
            prep(128, False, RR[i % 2])

        def H1d(i):
            stageAB(RR[i % 2])

        H2st = {}

        def H2a(i):
            R = RR[i % 2]
            stageC(R)
            psY, pyk = stageD(R)
            n_update(R)
            pg, pgk = self.pf()
            fw.mm(pg[:, :], R.sgT[:, :], lg2[:, :], True, True, r=[R.K("sgT"), "lg2"], w=[pgk])
            H2st[i] = (psY, pyk, pg, pgk)

        def H2b(i):
            psY, pyk, pg, pgk = H2st.pop(i)
            post(128, psY[:, :], [pyk], pg, pgk, RR[i % 2])

        def H2c(i):
            R = RR[i % 2]
            m, mk = mrT[0], "mrT0"
            gate_branch(128, lambda c, R=R: R.hT[:, c, 1:129], R.K("hTr"), m, mk)
            fw.dma(self.mrbuf[i].rearrange("p (c t) -> p c t", c=8), m[:, :, :], r=[mk], w=[("mr", i)], key=mk)

        def cap(pool, f, i):
            self.pool = pool
            return fw.capture(lambda: f(i))

        for f in (H1a, H1b, H1c):
            fw.replay([cap(0, f, 0)])
        fw.replay([cap(None, H1d, 0)])
        for i in range(NT):
            nx = i + 1 < NT
            fw.replay(([cap(0, H1a, i + 1)] if nx else []) + [cap(1, H2a, i)], chunk=1)
            fw.replay(([cap(0, H1b, i + 1)] if nx else []) + [cap(1, H2b, i)], chunk=1)
            fw.replay(([cap(0, H1c, i + 1)] if nx else []) + [cap(1, H2c, i)], chunk=1)
            if nx:
                fw.replay([cap(None, H1d, i + 1)])
        self.pool = None
        for j in range(4):
            ps, pk = self.pf()
            fw.tr(ps[:, 0:128], Nst[:, j, :], identf[:, :], r=["Nst", "identf"], w=[pk])
            fw.act(T[0][:, j * 128:(j + 1) * 128], ps[:, 0:128], AF.Copy, r=[pk], w=["T0"])
        for h_ in range(8):
            j, o = h_ // 2, (h_ % 2) * 64
            fw.dma(O["p_wkv"][l, h_], T[0][o:o + 64, j * 128 + o:j * 128 + o + 64], r=["T0"], key="T0")

        self.release(m1)
        RS = NSP()
        RS.zv = sbl("zv_s", [128, RD])
        RS.sgT = sbl("sgT_s", [128, 128], BF)
        RS.bon = sbl("bon_s", [128, 8])
        RS.K = lambda n: n + "#s"
        hTs = sbl("hTs", [128, 8, 80], BF)
        sadd = sbl("sadd", [16, RP])
        stT = sbl("stT", [128, 2, 16])
        zf = sbl("zf", [128, 2, 64])
        QH = sbl("QH", [128, 6, 4, 64])
        Sst = sbl("Sst", [128, 64, 64])
        Stmp = sbl("Stmp", [128, 64, 64])
        sk = sbl("sk", [128, 64])
        yh = sbl("yh", [128, 4, 64])
        ytm = T[7]
        self.V(lambda e: e.memset(hTs[:], 0.0), w=["hTs"])
        i = NT
        xt, xk = self.xt[i % 2], "xt%d" % (i % 2)
        src, _ = self.xsrc(l, i)
        fw.dma(xt[0:MS, :], src, r=[("xb", i)], w=[xk], key=xk)
        self.norm_hT(xt, xk, MS, hTs[:, :, 16:80], "hTs", identb)
        hcur = lambda c: hTs[:, c, 16:80]
        hprev = lambda c: hTs[:, c, 0:64]
        fw.dma(sadd[:, :], I["st_shift"][l], w=["sadd"], key="sadd")
        for q in range(2):
            ps, pk = self.pf()
            fw.tr(ps[:, 0:16], sadd[0:16, 1536 + q * 128:1536 + (q + 1) * 128], identf[0:16, 0:16], r=["sadd", "identf"], w=[pk])
            self.V(lambda e, q=q, ps=ps: e.tensor_scalar(stT[:, q, :], ps[:, 0:16], mucol[:, q:q + 1], None, ALU.mult), r=[pk, "mucol"], w=["stT"])
        for gi, g0 in enumerate(range(0, RP, 512)):
            n = min(512, RP - g0)
            self.bcast_load(T[4 + gi][0:16, 0:n], "T%d" % (4 + gi), I["rwkv_mu"][l, g0:g0 + n])
            self.V(lambda e, gi=gi, g0=g0, n=n: e.tensor_tensor(sadd[:, g0:g0 + n], sadd[:, g0:g0 + n], T[4 + gi][0:16, 0:n], ALU.mult),
                   r=["sadd", "T%d" % (4 + gi)], w=["sadd"])
        zv, sgT = RS.zv, RS.sgT
        for g0, dst, dk in [(0, zr, "zr"), (512, zk, "zk"), (1024, zv, RS.K("zv"))]:
            ps, pk = tok_proj(MS, hcur, hprev, "hTs", g0, dk)
            fw.act(dst[0:MS, :], ps[0:MS, :], AF.Copy, r=[pk], w=[dk])
            self.V(lambda e, dst=dst, g0=g0: e.tensor_tensor(dst[0:16, :], dst[0:16, :], sadd[0:16, g0:g0 + 512], ALU.add), r=[dk, "sadd"], w=[dk])
        for q, g0 in enumerate([1536, 1664]):
            ps, pk = feat_proj(MS, hcur, hprev, "hTs", g0)
            fw.act(zf[:, q, :], ps[:, 0:MS], AF.Copy, r=[pk], w=["zf"])
            self.V(lambda e, q=q: e.tensor_tensor(zf[:, q, 0:16], zf[:, q, 0:16], stT[:, q, :], ALU.add), r=["zf", "stT"], w=["zf"])
        fw.act(lact[0:64, 0:MS], zf[0:64, 0, :], AF.Tanh, r=["zf"], w=["lact"])
        fw.act(lact[64:128, 0:MS], zf[64:128, 0, :], AF.Copy, r=["zf"], w=["lact"])
        fw.act(sgT[:, 0:MS], zf[:, 1, :], AF.Sigmoid, r=["zf"], w=[RS.K("sgT")])
        prep(MS, True, RS)
        if l == 0:
            for nm, ap, k in [("s_zr", zr, "zr"), ("s_zk", zk, "zk"), ("s_zv", zv, "zv"), ("s_dec", T[6], "T6"), ("s_kk", T[2], "T2"),
                              ("s_kf", T[4], "T4"), ("s_be", T[5], "T5"), ("s_a", T[1], "T1")]:
                self.tap(nm, ap[0:MS, :], [k])
        sqv = self.sq.rearrange("x (t q) (h d) -> (q h) x t d", t=4, h=NH)
        for x in range(6):
            fw.dma(QH[:, x, :, :], sqv[:, x, :, :], r=[("sq", x)], w=["QH"], key="QH")
        fw.dma(Sst[:, :, :].rearrange("p v k -> p (v k)"), I["st_wkv"][l], w=["Sst"], key="Sst")
        for t in range(4):
            r_, w_, k_, v_, kk_, b_ = (QH[:, x, t, :] for x in range(6))
            rowb = lambda a: a.unsqueeze(1).to_broadcast([128, 64, 64])
            colb = lambda a: a.unsqueeze(2).to_broadcast([128, 64, 64])
            self.V(lambda e, kk_=kk_: e.tensor_tensor(Stmp[:, :, :], Sst[:, :, :], rowb(kk_), ALU.mult), r=["Sst", "QH"], w=["Stmp"])
            self.V(lambda e: e.tensor_reduce(sk[:, :], Stmp[:, :, :], AX.X, ALU.add), r=["Stmp"], w=["sk"])
            self.P(lambda e, w_=w_: e.tensor_tensor(Sst[:, :, :], Sst[:, :, :], rowb(w_), ALU.mult), r=["Sst", "QH", "Stmp"], w=["Sst"])
            self.V(lambda e, b_=b_: e.tensor_tensor(Stmp[:, :, :], colb(sk[:, :]), rowb(b_), ALU.mult), r=["sk", "QH"], w=["Stmp"])
            self.V(lambda e: e.tensor_tensor(Sst[:, :, :], Sst[:, :, :], Stmp[:, :, :], ALU.subtract), r=["Sst", "Stmp"], w=["Sst"])
            self.P(lambda e, v_=v_, k_=k_: e.tensor_tensor(Stmp[:, :, :], colb(v_), rowb(k_), ALU.mult), r=["QH", "Sst"], w=["Stmp"])
            self.V(lambda e: e.tensor_tensor(Sst[:, :, :], Sst[:, :, :], Stmp[:, :, :], ALU.add), r=["Sst", "Stmp"], w=["Sst"])
            self.P(lambda e, r_=r_: e.tensor_tensor(Stmp[:, :, :], Sst[:, :, :], rowb(r_), ALU.mult), r=["Sst", "QH"], w=["Stmp"])
            self.V(lambda e, t=t: e.tensor_reduce(yh[:, t, :], Stmp[:, :, :], AX.X, ALU.add), r=["Stmp"], w=["yh"])
        fw.dma(O["s_wkv"][l], Sst[:, :, :].rearrange("p v k -> p (v k)"), r=["Sst"], key="Sst")
        if l == 0:
            self.tap("s_QH", QH, ["QH"])
            self.tap("s_yh", yh, ["yh"])
        fw.dma(self.sy.rearrange("(t q) (h d) -> (q h) t d", t=4, h=NH), yh[:, :, :], r=["yh"], w=["sy"], key="yh")
        fw.dma(ytm[0:MS, :], self.sy, r=["sy"], w=["T7"], key="ytm")
        pg, pgk = self.pf()
        fw.mm(pg[0:MS, :], sgT[:, 0:MS], lg2[:, :], True, True, r=[RS.K("sgT"), "lg2"], w=[pgk])
        post(MS, ytm[0:MS, :], ["T7"], pg, pgk, RS)
        m, mk = mrT[0], "mrT0"
        gate_branch(MS, hcur, "hTs", m, mk)
        fw.dma(self.mrbuf[NT].rearrange("p (c t) -> p c t", c=8)[:, :, 0:MS], m[:, :, 0:MS], r=[mk], w=[("mr", NT)], key=mk)
        raw_last(lambda c: hTs[:, c, 64:80], "hTs", 16, O["s_shift"][l])

    def pass_attn(self, l, es2):
        fw, I, O, NT = self.fw, self.I, self.O, self.NT
        sbl = lambda n, s, dt=F32: self.sbl(es2, "a%d_" % l + n, s, dt)
        identb, identf = self.identb, self.identf
        Wq = sbl("Wq", [128, 8, 768], BF)
        Wg = sbl("Wg", [128, 8, D], BF)
        Wa = sbl("Wa", [128, 4, D], BF)
        Wo = sbl("Wo", [128, 8, D], BF)
        self.col_load(self.gcol[:], "gcol", I["norm_mix_g"][l], 8)
        m0 = self.aoff
        self.wstage = [sbl("wst%d" % i_, [128, 2048]) for i_ in range(4)]
        win = I["w_in"][l]
        gsc = lambda c: self.gcol[:, c:c + 1]
        self.prep_w(8, 512, lambda c, s0, n: win[c * 128:(c + 1) * 128, RP:RP + 512],
                    lambda c, s0, n: Wq[:, c, 0:512].rearrange("p (j g d) -> p g j d", j=4, g=2), lambda c: "Wq_%d" % c, "col", gsc,
                    sview=lambda a: a.rearrange("p (g j d) -> p g j d", g=2, j=4))
        self.prep_w(8, 256, lambda c, s0, n: win[c * 128:(c + 1) * 128, RP + 512:RP + 768],
                    lambda c, s0, n: Wq[:, c, 512:768], lambda c: "Wq_%d" % c, "col", gsc)
        self.prep_w(8, D, lambda c, s0, n: win[c * 128:(c + 1) * 128, 3584 + s0:3584 + s0 + n],
                    lambda c, s0, n: Wg[:, c, s0:s0 + n], lambda c: "Wga_%d" % c, "col", gsc)
        wbr = I["w_br_attn"][l]
        self.prep_w(4, D, lambda c, s0, n: wbr[c * 128:(c + 1) * 128, s0:s0 + n],
                    lambda c, s0, n: Wa[:, c, s0:s0 + n], lambda c: "Wa_%d" % c, "plain")
        wo = I["w_out"][l]
        self.prep_w(8, D, lambda c, s0, n: wo[c * 128:(c + 1) * 128, s0:s0 + n],
                    lambda c, s0, n: Wo[:, c, s0:s0 + n], lambda c: "Wo_%d" % c, "plain")
        self.release(m0)
        amask = sbl("amask", [128, 1024])
        fw.dma(amask[:, 0:768], I["c_amask"], w=["amask"], key="amask")
        fw.dma(amask[:, 768:1024], I["c_amask0"], w=["amask"], key="amask")
        smask = sbl("smask", [32, 132])
        fw.dma(smask[:], I["c_smask"], w=["smask"], key="smask")
        sinks = sbl("sinks", [128, NH])
        self.bcast_load(sinks[:], "sinks", I["attn_sinks"][l])
        hTd = [sbl("hT%d" % i_, [128, 8, 128], BF) for i_ in range(2)]
        hT = hTd[1]
        qkv = sbl("qkv", [128, 768])
        rot = sbl("rot", [128, 640])
        rtmp = [sbl("rtmp%d" % i, [128, 320]) for i in range(2)]
        rotb = sbl("rotb", [128, 640], BF)
        cs = [sbl("cs%d" % i, [128, 64]) for i in range(2)]
        qT = sbl("qT", [128, 4, 128], BF)

        class NSB:
            pass
        B0, B1 = NSB(), NSB()
        B0.qkv, B0.rot, B0.rotb, B0.qT, B0.s = qkv, rot, rotb, qT, ""
        B1.qkv, B1.rot, B1.rotb, B1.qT, B1.s = (sbl("qkvb", [128, 768]), sbl("rotbb", [128, 640]), sbl("rotbbb", [128, 640], BF),
                                                sbl("qTb", [128, 4, 128], BF), "b")
        Bs = [B0, B1]
        KTr = sbl("KTr", [128, 2, 128], BF)
        Vp = sbl("Vp", [128, 2, 2, 2, 128], BF)
        scg = [sbl("sc%d" % g_, [128, 4, 256]) for g_ in range(2)]
        stg = [sbl("st%d" % g_, [128, 16]) for g_ in range(2)]
        pbfg = [sbl("pbf%d" % g_, [128, 4, 256], BF) for g_ in range(2)]
        pTg = [sbl("pT%d" % g_, [128, 4, 2, 128], BF) for g_ in range(2)]
        oT = sbl("oT", [128, 4, 128], BF)
        sga = sbl("sga", [128, 8, 128])
        mrl = [sbl("mrl%d" % i, [128, 8, 128], BF) for i in range(2)]
        mg = sbl("mg", [128, 8, 128], BF)
        xo = [sbl("xo%d" % i, [128, D]) for i in range(2)]
        KA = sbl("KA", [128, NS, 128])
        VA = sbl("VA", [128, NS, 128])
        VAb = sbl("VAb", [128, NS, 128], BF)
        KB = sbl("KB", [4, NS, 128])
        VBt = sbl("VB", [4, NS, 128])
        VBb = sbl("VBb", [4, NS, 128], BF)
        KAT = sbl("KAT", [128, NS, 128], BF)
        KBT = sbl("KBT", [128, NS, 4], BF)
        qbd = sbl("qbd", [128, NS, 32], BF)
        ssc = sbl("ssc", [32, NS, 132])
        sst = sbl("sst", [32, 4 * NS])
        spb = sbl("spb", [32, NS, 132], BF)
        spT = sbl("spT", [128, NS, 32], BF)
        spTB = sbl("spTB", [4, NS, 32], BF)
        oTs = sbl("oTs", [128, 4, MS], BF)

        self.V(lambda e: e.memset(Vp[:], 0.0), w=["Vp0", "Vp1"])
        self.V(lambda e: e.memset(KTr[:], 0.0), w=["KTr0", "KTr1"])
        self.V(lambda e: e.memset(qbd[:], 0.0), w=["qbd"])

        def proj_rope(B, M, hcur, hk, cosap, sinap, cskey):
            for g0, n in [(0, 512), (512, 256)]:
                ps, pk = self.pf()
                for c in range(8):
                    fw.mm(ps[0:M, 0:n], hcur(c), Wq[:, c, g0:g0 + n], c == 0, c == 7, r=[hk, "Wq_%d" % c], w=[pk])
                fw.act(B.qkv[0:M, g0:g0 + n], ps[0:M, 0:n], AF.Copy, r=[pk], w=["qkv%d" % (g0 // 512) + B.s])
            qk3 = B.qkv[0:M, 0:640].rearrange("p (h d) -> p h d", h=10)
            r3 = B.rot[0:M, :].rearrange("p (h d) -> p h d", h=10)
            x1, x2 = qk3[:, :, 0:32], qk3[:, :, 32:64]
            cb = cosap.unsqueeze(1).to_broadcast([M, 10, 32])
            sb_ = sinap.unsqueeze(1).to_broadcast([M, 10, 32])
            ta = rtmp[0][0:M, :].rearrange("p (h d) -> p h d", h=10)
            tb = rtmp[1][0:M, :].rearrange("p (h d) -> p h d", h=10)
            rk = ["qkv0" + B.s, "qkv1" + B.s, cskey]
            rotk = "rot" + B.s
            self.V(lambda e: e.tensor_tensor(ta, x1, cb, ALU.mult), r=rk, w=["rtmp0"])
            self.P(lambda e: e.tensor_tensor(tb, x2, sb_, ALU.mult), r=rk, w=["rtmp1"])
            self.V(lambda e: e.tensor_tensor(r3[:, :, 0:32], ta, tb, ALU.subtract), r=["rtmp0", "rtmp1"], w=[rotk])
            self.V(lambda e: e.tensor_tensor(ta, x2, cb, ALU.mult), r=rk + [rotk], w=["rtmp0"])
            self.P(lambda e: e.tensor_tensor(tb, x1, sb_, ALU.mult), r=rk + [rotk], w=["rtmp1"])
            self.V(lambda e: e.tensor_tensor(r3[:, :, 32:64], ta, tb, ALU.add), r=["rtmp0", "rtmp1"], w=[rotk])
            fw.act(B.rotb[0:M, :], B.rot[0:M, :], AF.Copy, r=[rotk], w=["rotb" + B.s])

        def q_transposes(B, M, dst, dkey):
            pbk, pk = self.pb()
            for jj in range(4):
                fw.tr(pbk[:, jj * M:(jj + 1) * M], B.rotb[0:M, jj * 128:(jj + 1) * 128], identb[0:M, 0:M], r=["rotb" + B.s, "identb"], w=[pk])
            fw.act(dst, pbk[:, 0:4 * M].rearrange("p (j t) -> p j t", j=4), AF.Copy, r=[pk], w=[dkey])

        def gates_part(M, hcur, hk):
            for half in range(2):
                pg, pgk = self.pf()
                for q in range(4):
                    dc = half * 4 + q
                    for c in range(8):
                        fw.mm(pg[:, q * M:(q + 1) * M], Wg[:, c, dc * 128:(dc + 1) * 128], hcur(c), c == 0, c == 7, r=[hk, "Wga_%d" % c], w=[pgk])
                fw.act(sga[:, half * 4:(half + 1) * 4, 0:M], pg[:, 0:4 * M].rearrange("p (q t) -> p q t", q=4), AF.Sigmoid, r=[pgk], w=["sga%d" % half])

        def gate_out(M, hcur, hk, oTt, okey, mr, mrk, xt, xk, xo_, xok, do_gates=True):
            if do_gates:
                gates_part(M, hcur, hk)
            for half in range(2):
                pbr, pbk_ = self.pf()
                for q in range(4):
                    dc = half * 4 + q
                    for cc in range(4):
                        fw.mm(pbr[:, q * M:(q + 1) * M], Wa[:, cc, dc * 128:(dc + 1) * 128], oTt[:, cc, 0:M], cc == 0, cc == 3, r=[okey, "Wa_%d" % cc], w=[pbk_])
                hs = slice(half * 4, (half + 1) * 4)
                self.V(lambda e, hs=hs, pbr=pbr: e.tensor_tensor(sga[:, hs, 0:M], sga[:, hs, 0:M], pbr[:, 0:4 * M].rearrange("p (q t) -> p q t", q=4), ALU.mult),
                       r=["sga%d" % half, pbk_], w=["sga%d" % half])
                self.V(lambda e, hs=hs: e.tensor_tensor(mg[:, hs, 0:M], sga[:, hs, 0:M], mr[:, hs, 0:M], ALU.add), r=["sga%d" % half, mrk], w=["mg%d" % half])
            for grp in range(2):
                px, pxk = self.pf()
                for dc in range(8):
                    fw.mm(px[0:M, :], mg[:, dc, 0:M], Wo[:, dc, grp * 512:(grp + 1) * 512], dc == 0, dc == 7, r=["mg%d" % (dc // 4), "Wo_%d" % dc], w=[pxk])
                self.V(lambda e, grp=grp, px=px: e.tensor_tensor(xo_[0:M, grp * 512:(grp + 1) * 512], xt[0:M, grp * 512:(grp + 1) * 512], px[0:M, :], ALU.add),
                       r=[xk, pxk], w=[xok])

        def put_kv(B, slot):
            pbk, pk = self.pb()
            fw.tr(pbk[:, 0:128], B.rotb[:, 512:640], identb[:, :], r=["rotb" + B.s, "identb"], w=[pk])
            self.V(lambda e, pbk=pbk, slot=slot: e.tensor_copy(KTr[:, slot, :], pbk[:, 0:128]), r=[pk], w=["KTr%d" % slot])
            for g in range(2):
                vsrc = B.qkv[:, 640 + g * 64:640 + (g + 1) * 64]
                fw.act(Vp[:, slot, g, 0, 0:64], vsrc, AF.Copy, r=["qkv1" + B.s], w=["Vp%d" % slot])
                self.P(lambda e, g=g, vsrc=vsrc, slot=slot: e.tensor_copy(Vp[:, slot, g, 1, 64:128], vsrc), r=["qkv1" + B.s], w=["Vp%d" % slot])

        xt, xk = self.xt[1], "xt1"
        fw.dma(xt[:], (I["xh0"] if (l == 0 or NSEG == 1) else self.xh_dram), r=["xh_dram"], w=[xk], key=xk)
        fw.dma(cs[1][:, 0:32], I["c_cosh"], w=["cs1"], key="cs1")
        fw.dma(cs[1][:, 32:64], I["c_sinh"], w=["cs1"], key="cs1")
        self.norm_hT(xt, xk, 128, hT[:, :, :], "hT1", identb)
        proj_rope(B1, 128, lambda c: hT[:, c, :], "hT1", cs[1][:, 0:32], cs[1][:, 32:64], "cs1")
        put_kv(B1, 1)
        def pre(i):
            xt, xk = self.xt[i % 2], "xt%d" % (i % 2)
            src, _ = self.xsrc(l, i)
            fw.dma(xt[:], src, r=[("xb", i)], w=[xk], key=xk)
            mr, mrk = mrl[i % 2], "mrl%d" % (i % 2)
            fw.dma(mr[:, :, :], self.mrbuf[i].rearrange("p (c t) -> p c t", c=8), r=[("mr", i)], w=[mrk], key=mrk)
            ck_ = "cs%d" % (i % 2)
            fw.dma(cs[i % 2][:, 0:32], I["c_cosp"][i * 128:(i + 1) * 128, :], w=[ck_], key=ck_)
            fw.dma(cs[i % 2][:, 32:64], I["c_sinp"][i * 128:(i + 1) * 128, :], w=[ck_], key=ck_)
            self.norm_hT(xt, xk, 128, hTd[i % 2][:, :, :], "hT%d" % (i % 2), identb)
            B = Bs[i % 2]
            proj_rope(B, 128, lambda c, i=i: hTd[i % 2][:, c, :], "hT%d" % (i % 2), cs[i % 2][:, 0:32], cs[i % 2][:, 32:64], ck_)
            q_transposes(B, 128, B.qT[:, :, :], "qT" + B.s)

        pre(0)
        for i in range(NT):
            xt, xk = self.xt[i % 2], "xt%d" % (i % 2)
            mr, mrk = mrl[i % 2], "mrl%d" % (i % 2)
            ck_ = "cs%d" % (i % 2)
            hkk = "hT%d" % (i % 2)
            hcur = lambda c, i=i: hTd[i % 2][:, c, :]
            B = Bs[i % 2]
            slot = i % 2
            if i == NT - 1:
                fw.dma(O["p_k"][l], B.rot[:, 512:640], r=["rot" + B.s], key="rot")
                fw.dma(O["p_v"][l], B.qkv[:, 640:768], r=["qkv1" + B.s], key="qkv1")
            put_kv(B, slot)
            mvar = 3 if i == 0 else slot
            msk = amask[:, mvar * 256:(mvar + 1) * 256].unsqueeze(1).to_broadcast([128, 4, 256])
            pSg = []
            for g in range(2):
                o = g * 64
                pS = []
                for jj in range(4):
                    if jj % 2 == 0:
                        ps, pk = self.pf()
                        pS.append((ps, pk))
                    fw.mm(ps[:, (jj % 2) * 256:(jj % 2 + 1) * 256], B.qT[o:o + 64, jj, :], KTr[o:o + 64, :, :].rearrange("p s t -> p (s t)"),
                          True, True, r=["qT" + B.s, "KTr0", "KTr1"], w=[pk])
                pSg.append(pS)
            gates_part(128, hcur, hkk)

            def softmax(g):
                sc, st, pbf = scg[g], stg[g], pbfg[g]
                sck = ["sc%d_0" % g, "sc%d_1" % g]
                for half, (ps, pk) in enumerate(pSg[g]):
                    self.V(lambda e, ps=ps, half=half, msk=msk, sc=sc: e.scalar_tensor_tensor(
                        sc[:, half * 2:(half + 1) * 2, :], ps[:, :].rearrange("p (j c) -> p j c", j=2), 0.125,
                        msk[:, 0:2, :], ALU.mult, ALU.add), r=[pk, "amask"], w=[sck[half]])
                k0, k2, k3 = "st%d" % g, "st%d_2" % g, "st%d_3" % g
                self.V(lambda e: e.tensor_reduce(st[:, 0:4], sc[:, :, :], AX.X, ALU.max), r=sck, w=[k0])
                self.V(lambda e: e.tensor_tensor(st[:, 0:4], st[:, 0:4], sinks[:, g * 4:(g + 1) * 4], ALU.max), r=[k0, "sinks"], w=[k0])
                self.V(lambda e: e.tensor_tensor(sc[:, :, :], sc[:, :, :], bc3(st[:, 0:4], 256), ALU.subtract), r=sck + [k0], w=sck)
                fw.act(sc[:, :, :], sc[:, :, :], AF.Exp, r=sck, w=sck)
                self.V(lambda e: e.tensor_reduce(st[:, 4:8], sc[:, :, :], AX.X, ALU.add), r=sck, w=[k2])
                self.V(lambda e: e.tensor_tensor(st[:, 8:12], sinks[:, g * 4:(g + 1) * 4], st[:, 0:4], ALU.subtract), r=[k0, "sinks"], w=[k3])
                fw.act(st[:, 8:12], st[:, 8:12], AF.Exp, r=[k3], w=[k3])
                self.V(lambda e: e.tensor_tensor(st[:, 4:8], st[:, 4:8], st[:, 8:12], ALU.add), r=[k2, k3], w=[k2])
                self.V(lambda e: e.reciprocal(st[:, 4:8], st[:, 4:8]), r=[k2], w=[k2])
                self.V(lambda e: e.tensor_tensor(pbf[:, :, :], sc[:, :, :], bc3(st[:, 4:8], 256), ALU.mult), r=sck + [k2], w=["pbf%d" % g])

            def p_transposes(g):
                pbf, pT = pbfg[g], pTg[g]
                pbk, pk = self.pb()
                for jj in range(4):
                    for s_ in range(2):
                        fw.tr(pbk[:, (jj * 2 + s_) * 128:(jj * 2 + s_ + 1) * 128], pbf[:, jj, s_ * 128:(s_ + 1) * 128], identb[:, :], r=["pbf%d" % g, "identb"], w=[pk])
                fw.act(pT[:, :, :, :], pbk[:, :].rearrange("p (j s t) -> p j s t", j=4, s=2), AF.Copy, r=[pk], w=["pT%d" % g])

            def pv(g, pO, pok):
                pT = pTg[g]
                for c2 in range(2):
                    cc = g * 2 + c2
                    n = 0
                    for par in range(2):
                        jj = c2 * 2 + par
                        for s_ in range(2):
                            fw.mm(pO[:, cc * 128:(cc + 1) * 128], Vp[:, s_, g, par, :], pT[:, jj, s_, :], n == 0, n == 3,
                                  r=["Vp0", "Vp1", "pT%d" % g], w=[pok])
                            n += 1

            fw.replay([fw.capture(lambda: softmax(0)), fw.capture(lambda: softmax(1))], chunk=1)
            p_transposes(0)
            pO, pok = self.pf()
            pv(0, pO, pok)
            if i + 1 < NT:
                pre(i + 1)
            p_transposes(1)
            pv(1, pO, pok)
            fw.act(oT[:, :, :], pO[:, :].rearrange("p (c t) -> p c t", c=4), AF.Copy, r=[pok], w=["oT"])
            xo_, xok = xo[i % 2], "xo%d" % (i % 2)
            gate_out(128, hcur, hkk, oT, "oT", mr, mrk, xt, xk, xo_, xok, do_gates=False)
            fw.dma(self.xbuf[i * 128:(i + 1) * 128, :], xo_[:, :], r=[xok], w=[("xb", i)], key=xok)
        if NSEG > 1:
            self.gather_select(xo_[:, :], [xok], D, self.agX_in, self.agX_out, "agX")
            fw.dma(self.xh_dram, xo_[:, :], r=[xok], w=["xh_dram"], key="xhst")

        i = NT
        xt, xk = self.xt[i % 2], "xt%d" % (i % 2)
        src, _ = self.xsrc(l, i)
        fw.dma(xt[0:MS, :], src, r=[("xb", i)], w=[xk], key=xk)
        mr, mrk = mrl[i % 2], "mrl%d" % (i % 2)
        fw.dma(mr[:, :, 0:MS], self.mrbuf[NT].rearrange("p (c t) -> p c t", c=8)[:, :, 0:MS], r=[("mr", NT)], w=[mrk], key=mrk)
        ck_ = "cs%d" % (i % 2)
        fw.dma(cs[i % 2][0:MS, 0:32], I["c_coss"], w=[ck_], key=ck_)
        fw.dma(cs[i % 2][0:MS, 32:64], I["c_sins"], w=[ck_], key=ck_)
        self.norm_hT(xt, xk, MS, hT[:, :, 0:MS], "hT1", identb)
        hcur = lambda c: hT[:, c, 0:MS]
        proj_rope(B0, MS, hcur, "hT1", cs[i % 2][0:MS, 0:32], cs[i % 2][0:MS, 32:64], ck_)
        for (cin, cout, srcap, srck, dkey) in [("ck", "s_k", rot[:, 512:640], "rot", "sk"), ("cv", "s_v", qkv[:, 640:768], "qkv1", "sv")]:
            fw.dma(O[cout][l, :, 0:124, :], I[cin][l, :, 4:128, :], w=[dkey], key=dkey + "c")
            for t in range(4):
                fw.dma(O[cout][l, :, 124 + t, :], srcap[t * 16:(t + 1) * 16, :], r=[srck], w=[dkey], key=dkey + "n")
        fw.dma(KA[:, :, :], O["s_k"][l].rearrange("q p c -> p q c"), r=["sk"], w=["KA"], key="KA")
        fw.dma(VA[:, :, :], O["s_v"][l].rearrange("q p c -> p q c"), r=["sv"], w=["VA"], key="VA")
        fw.dma(KB[:, :, :], I["ck"][l, :, 0:4, :].rearrange("q p c -> p q c"), w=["KB"], key="KB")
        fw.dma(VBt[:, :, :], I["cv"][l, :, 0:4, :].rearrange("q p c -> p q c"), w=["VB"], key="VB")
        self.P(lambda e: e.tensor_copy(VAb[:, :, :], VA[:, :, :]), r=["VA"], w=["VAb"])
        self.P(lambda e: e.tensor_copy(VBb[:, :, :], VBt[:, :, :]), r=["VB"], w=["VBb"])
        for q4 in range(4):
            ps, pk = self.pf()
            for qq in range(4):
                q = q4 * 4 + qq
                fw.tr(ps[:, qq * 128:(qq + 1) * 128], KA[:, q, :], identf[:, :], r=["KA", "identf"], w=[pk])
            fw.act(KAT[:, q4 * 4:(q4 + 1) * 4, :], ps[:, :].rearrange("p (q t) -> p q t", q=4), AF.Copy, r=[pk], w=["KAT"])
        ps, pk = self.pf()
        for q in range(NS):
            fw.tr(ps[:, q * 4:(q + 1) * 4], KB[0:4, q, :], identf[0:4, 0:4], r=["KB", "identf"], w=[pk])
        fw.act(KBT[:, :, :], ps[:, 0:64].rearrange("p (q t) -> p q t", q=NS), AF.Copy, r=[pk], w=["KBT"])
        q_transposes(B0, MS, qT[:, :, 0:MS], "qT")
        for g in range(2):
            for jj in range(4):
                o = g * 64
                dst = qbd[o:o + 64, :, g * 16 + jj * 4:g * 16 + (jj + 1) * 4]
                srcq = qT[o:o + 64, jj, 0:MS].rearrange("p (t q) -> p q t", t=4)
                self.V(lambda e, dst=dst, srcq=srcq: e.tensor_copy(dst, srcq), r=["qT"], w=["qbd"])
        pSA = []
        for q4 in range(4):
            ps, pk = self.pf()
            pSA.append((ps, pk))
            for qq in range(4):
                q = q4 * 4 + qq
                fw.mm(ps[0:32, qq * 128:(qq + 1) * 128], qbd[:, q, :], KAT[:, q, :], True, True, r=["qbd", "KAT"], w=[pk])
        psB, pkB = self.pf()
        for q in range(NS):
            fw.mm(psB[0:32, q * 4:(q + 1) * 4], qbd[:, q, :], KBT[:, q, :], True, True, r=["qbd", "KBT"], w=[pkB])
        for q4, (ps, pk) in enumerate(pSA):
            self.V(lambda e, q4=q4, ps=ps: e.scalar_tensor_tensor(
                ssc[:, q4 * 4:(q4 + 1) * 4, 0:128], ps[0:32, :].rearrange("p (q c) -> p q c", q=4), 0.125,
                smask[:, 0:128].unsqueeze(1).to_broadcast([32, 4, 128]), ALU.mult, ALU.add), r=[pk, "smask"], w=["ssc"])
        self.V(lambda e: e.scalar_tensor_tensor(
            ssc[:, :, 128:132], psB[0:32, 0:64].rearrange("p (q c) -> p q c", q=NS), 0.125,
            smask[:, 128:132].unsqueeze(1).to_broadcast([32, NS, 4]), ALU.mult, ALU.add), r=[pkB, "smask"], w=["ssc"])
        sinkc = sbl("sinkc", [32, 1])
        for g in range(2):
            for jj in range(4):
                p0 = g * 16 + jj * 4
                fw.dma(sinkc[p0:p0 + 4, :], I["attn_sinks"][l, g * 4 + jj:g * 4 + jj + 1].partition_broadcast(4), w=["sinkc"], key="sinkc")
        self.V(lambda e: e.tensor_reduce(sst[:, 0:NS], ssc[:, :, :], AX.X, ALU.max), r=["ssc"], w=["sst"])
        self.V(lambda e: e.tensor_scalar(sst[:, 0:NS], sst[:, 0:NS], sinkc[:, 0:1], None, ALU.max), r=["sst", "sinkc"], w=["sst"])
        self.V(lambda e: e.tensor_tensor(ssc[:, :, :], ssc[:, :, :], bc3(sst[:, 0:NS], 132), ALU.subtract), r=["ssc", "sst"], w=["ssc"])
        fw.act(ssc[:, :, :], ssc[:, :, :], AF.Exp, r=["ssc"], w=["ssc"])
        self.V(lambda e: e.tensor_reduce(sst[:, NS:2 * NS], ssc[:, :, :], AX.X, ALU.add), r=["ssc"], w=["sst2"])
        self.V(lambda e: e.tensor_scalar(sst[:, 2 * NS:3 * NS], sst[:, 0:NS], sinkc[:, 0:1], None, ALU.subtract), r=["sst", "sinkc"], w=["sst3"])
        fw.act(sst[:, 2 * NS:3 * NS], sst[:, 2 * NS:3 * NS], AF.Exp, r=["sst3"], w=["sst3"], scale=-1.0)
        self.V(lambda e: e.tensor_tensor(sst[:, NS:2 * NS], sst[:, NS:2 * NS], sst[:, 2 * NS:3 * NS], ALU.add), r=["sst2", "sst3"], w=["sst2"])
        self.V(lambda e: e.reciprocal(sst[:, NS:2 * NS], sst[:, NS:2 * NS]), r=["sst2"], w=["sst2"])
        self.V(lambda e: e.tensor_tensor(spb[:, :, :], ssc[:, :, :], bc3(sst[:, NS:2 * NS], 132), ALU.mult), r=["ssc", "sst2"], w=["spb"])
        identb32 = identb[0:32, 0:32]
        for q8 in range(2):
            pbk, pk = self.pb()
            for qq in range(8):
                q = q8 * 8 + qq
                fw.tr(pbk[:, qq * 32:(qq + 1) * 32], spb[:, q, 0:128], identb32, r=["spb", "identb"], w=[pk])
            fw.act(spT[:, q8 * 8:(q8 + 1) * 8, :], pbk[:, 0:256].rearrange("p (q c) -> p q c", q=8), AF.Copy, r=[pk], w=["spT"])
        pbk, pk = self.pb()
        for q in range(NS):
            fw.tr(pbk[0:4, q * 32:(q + 1) * 32], spb[:, q, 128:132], identb32, r=["spb", "identb"], w=[pk])
        fw.act(spTB[:, :, :], pbk[0:4, 0:512].rearrange("p (q c) -> p q c", q=NS), AF.Copy, r=[pk], w=["spTB"])
        pO, pok = self.pf()
        for q in range(NS):
            fw.mm(pO[:, q * 32:(q + 1) * 32], VAb[:, q, :], spT[:, q, :], True, False, r=["VAb", "spT"], w=[pok])
            fw.mm(pO[:, q * 32:(q + 1) * 32], VBb[0:4, q, :], spTB[0:4, q, :], False, True, r=["VBb", "spTB"], w=[pok])
        oraw = sbl("oraw", [128, 32, NS], BF)
        fw.act(oraw.rearrange("p c q -> p q c"), pO[:, :].rearrange("p (q c) -> p q c", q=NS), AF.Copy, r=[pok], w=["oraw"])
        for g in range(2):
            for jj in range(4):
                cc, par = g * 2 + jj // 2, jj % 2
                c0 = g * 16 + jj * 4
                srco = oraw[g * 64:(g + 1) * 64, c0:c0 + 4, :].rearrange("p t q -> p (t q)")
                fw.dma(oTs[par * 64:(par + 1) * 64, cc, :], srco, r=["oraw"], w=["oTs"], key="oTs")
        xo_, xok = xo[i % 2], "xo%d" % (i % 2)
        gate_out(MS, hcur, "hT1", oTs, "oTs", mr, mrk, xt, xk, xo_, xok)
        fw.dma(self.xsbuf, xo_[0:MS, :], r=[xok], w=[("xb", NT)], key=xok)

    def pass_ffn(self, l, es2):
        fw, I, O, NT = self.fw, self.I, self.O, self.NT
        sbl = lambda n, s, dt=F32: self.sbl(es2, "f%d_" % l + n, s, dt)
        identb, identf = self.identb, self.identf
        Wc = sbl("Wc", [128, 8, DFF], BF)
        Wu = sbl("Wu", [128, 8, DFF], BF)
        Wd = sbl("Wd", [128, NFC, D], BF)
        self.col_load(self.gcol[:], "gcol", I["norm_ffn_g"][l], 8)
        cw = sbl("cw", [128, 4, NFC])
        for j in range(3):
            self.col_load(cw[:, j, :], "cw", I["ffn_conv_w"][l, j], NFC)
        self.col_load(cw[:, 3, :], "cw", I["ffn_conv_b"][l], NFC)
        m0 = self.aoff
        self.wstage = [sbl("wst%d" % i_, [128, 2048]) for i_ in range(4)]
        wi = I["ffn_w_in"][l]
        gsc = lambda c: self.gcol[:, c:c + 1]
        self.prep_w(8, DFF, lambda c, s0, n: wi[c * 128:(c + 1) * 128, s0:s0 + n],
                    lambda c, s0, n: Wc[:, c, s0:s0 + n], lambda c: "Wc_%d" % c, "col", gsc)
        self.prep_w(8, DFF, lambda c, s0, n: wi[c * 128:(c + 1) * 128, DFF + s0:DFF + s0 + n],
                    lambda c, s0, n: Wu[:, c, s0:s0 + n], lambda c: "Wu_%d" % c, "col", gsc)
        wd = I["ffn_w_down"][l]
        self.prep_w(NFC, D, lambda c, s0, n: wd[c * 128:(c + 1) * 128, s0:s0 + n],
                    lambda c, s0, n: Wd[:, c, s0:s0 + n], lambda c: "Wd_%d" % c, "plain")
        self.release(m0)
        last = (l == 1)
        if last:
            gf = sbl("gf", [128, D])
            self.bcast_load(gf[:], "gf", I["norm_final_g"])
        hTd = [sbl("hT%d" % i_, [128, 8, 128], BF) for i_ in range(2)]
        hT = hTd[0]
        cxf = sbl("cx", [128, NFC * 130])
        cx1 = cxf.rearrange("p (f t) -> p f t", f=NFC)
        cxs = cxf[:, 0:NFC * NS * 6].rearrange("p (f q j) -> p f q j", f=NFC, q=NS)
        acc = [sbl("acc%d" % i_, [128, 4, 128]) for i_ in range(2)]
        aTd = [sbl("aT%d" % i_, [128, NFC, 128], BF) for i_ in range(2)]
        xo = [sbl("xo%d" % i_, [128, D]) for i_ in range(2)]
        ctok = sbl("ctok", [128, DFF])
        cst = ctok
        jk = self.xn

        def finish(M, xt, xk, xo_, xok, dst_final, dst_x, dkey, aT, aTk):
            for grp in range(2):
                px, pxk = self.pf()
                for fc in range(NFC):
                    fw.mm(px[0:M, :], aT[:, fc, 0:M], Wd[:, fc, grp * 512:(grp + 1) * 512], fc == 0, fc == NFC - 1, r=[aTk, "Wd_%d" % fc], w=[pxk])
                self.V(lambda e, grp=grp, px=px: e.tensor_tensor(xo_[0:M, grp * 512:(grp + 1) * 512], xt[0:M, grp * 512:(grp + 1) * 512], px[0:M, :], ALU.add),
                       r=[xk, pxk], w=[xok])
            if not last:
                fw.dma(dst_x, xo_[0:M, :], r=[xok], w=[dkey], key=xok)
                return
            ss, t1 = self.ss, self.t1
            fw.act(jk[0:M, :], xo_[0:M, :], AF.Square, r=[xok], w=["xn", "ss"], accum_out=ss[0:M, :])
            self.V(lambda e: e.tensor_scalar(t1[0:M, :], ss[0:M, :], 1.0 / D, 1e-6, ALU.mult, ALU.add), r=["ss"], w=["t1"])
            fw.act(t1[0:M, :], t1[0:M, :], AF.Sqrt, r=["t1"], w=["t1"])
            self.V(lambda e: e.reciprocal(t1[0:M, :], t1[0:M, :]), r=["t1"], w=["t1"])
            self.V(lambda e: e.scalar_tensor_tensor(xo_[0:M, :], xo_[0:M, :], t1[0:M, 0:1], gf[0:M, :], ALU.mult, ALU.mult),
                   r=[xok, "t1", "gf"], w=[xok])
            fw.dma(dst_final, xo_[0:M, :], r=[xok], key=xok)

        def ffn_core(M, hcur, hk, cview, ckey, sample, aT, aTk, mid=None, groups=None):
            for b0 in (groups if groups is not None else range(0, NFC, 4)):
                nb = min(4, NFC - b0)
                pc, pck = self.pf()
                for q in range(nb):
                    fc = b0 + q
                    for c in range(8):
                        fw.mm(pc[:, q * M:(q + 1) * M], Wc[:, c, fc * 128:(fc + 1) * 128], hcur(c), c == 0, c == 7, r=[hk, "Wc_%d" % c], w=[pck])
                pu, puk = self.pf()
                for q in range(nb):
                    fc = b0 + q
                    for c in range(8):
                        fw.mm(pu[:, q * M:(q + 1) * M], Wu[:, c, fc * 128:(fc + 1) * 128], hcur(c), c == 0, c == 7, r=[hk, "Wu_%d" % c], w=[puk])
                if sample:
                    fw.act(cview[:, b0:b0 + nb, :, 2:6], pc[:, 0:nb * M].rearrange("p (f t q) -> p f q t", f=nb, t=4), AF.Copy, r=[pck], w=[ckey])
                else:
                    fw.act(cview[:, b0:b0 + nb, 2:130], pc[:, 0:nb * M].rearrange("p (f t) -> p f t", f=nb), AF.Copy, r=[pck], w=[ckey])
                a_ = acc[(b0 // 4) % 2]
                ak = "acc%d" % ((b0 // 4) % 2)
                views = []
                for q in range(nb):
                    fc = b0 + q
                    if sample:
                        c0, c1, c2 = (cview[:, fc, :, s_:s_ + 4] for s_ in range(3))
                        av = a_[:, q, 0:M].rearrange("p (t q) -> p q t", t=4)
                    else:
                        c0, c1, c2 = (cview[:, fc, s_:s_ + 128] for s_ in range(3))
                        av = a_[:, q, :]
                    views.append((fc, av, c0, c1, c2))
                akq = [ak + "_%d" % q for q in range(nb)]
                for q, (fc, av, c0, c1, c2) in enumerate(views):
                    self.P(lambda e, av=av, c0=c0, fc=fc: e.tensor_scalar(av, c0, cw[:, 0, fc:fc + 1], cw[:, 3, fc:fc + 1], ALU.mult, ALU.add),
                           r=[ckey, "cw", ak], w=([akq[q], ak] if q == 0 else [akq[q]]))
                for q, (fc, av, c0, c1, c2) in enumerate(views):
                    self.V(lambda e, av=av, c1=c1, fc=fc: e.scalar_tensor_tensor(av, c1, cw[:, 1, fc:fc + 1], av, ALU.mult, ALU.add),
                           r=[ckey, "cw", akq[q]], w=[akq[q]])
                for q, (fc, av, c0, c1, c2) in enumerate(views):
                    self.V(lambda e, av=av, c2=c2, fc=fc: e.scalar_tensor_tensor(av, c2, cw[:, 2, fc:fc + 1], av, ALU.mult, ALU.add),
                           r=[ckey, "cw", akq[q]], w=[akq[q]])
                fw.act(a_[:, 0:nb, 0:M], a_[:, 0:nb, 0:M], AF.Gelu, r=akq, w=[ak])
                self.V(lambda e, a_=a_, pu=pu, nb=nb, b0=b0, aT=aT: e.tensor_tensor(aT[:, b0:b0 + nb, 0:M], a_[:, 0:nb, 0:M],
                                                                              pu[:, 0:nb * M].rearrange("p (f t) -> p f t", f=nb), ALU.mult),
                       r=[ak, puk], w=[aTk])
                if mid is not None and b0 == 8:
                    mid()

        def c_token_major(M, hcur, hk, rows, dsts):
            for g0 in range(0, DFF, 512):
                n = min(512, DFF - g0)
                ps, pk = self.pf()
                for c in range(8):
                    fw.mm(ps[0:M, 0:n], hcur(c), Wc[:, c, g0:g0 + n], c == 0, c == 7, r=[hk, "Wc_%d" % c], w=[pk])
                fw.act(ctok[0:M, g0:g0 + n], ps[0:M, 0:n], AF.Copy, r=[pk], w=["ctok"])
            for (r0, r1), dst in zip(rows, dsts):
                fw.dma(dst, ctok[r0:r1, :], r=["ctok"], key="ctok")

        xt, xk = self.xt[1], "xt1"
        fw.dma(xt[:], (I["xh0"] if NSEG == 1 else self.xh_dram), r=["xh_dram"], w=[xk], key=xk)
        self.norm_hT(xt, xk, 128, hT[:, :, :], "hT0", identb)
        pc, pck = self.pf()
        for fc in range(NFC):
            for c in range(8):
                fw.mm(pc[:, fc * 2:(fc + 1) * 2], Wc[:, c, fc * 128:(fc + 1) * 128], hT[:, c, 126:128], c == 0, c == 7, r=["hT0", "Wc_%d" % c], w=[pck])
        fw.act(cx1[:, :, 0:2], pc[:, 0:2 * NFC].rearrange("p (f t) -> p f t", f=NFC), AF.Copy, r=[pck], w=["cx"])
        def pre(i):
            xt, xk = self.xt[i % 2], "xt%d" % (i % 2)
            fw.dma(xt[:], self.xbuf[i * 128:(i + 1) * 128, :], r=[("xb", i)], w=[xk], key=xk)
            self.norm_hT(xt, xk, 128, hTd[i % 2][:, :, :], "hT%d" % (i % 2), identb)

        def head(i):
            if i > 0:
                self.P(lambda e: e.tensor_copy(acc[0][:, 0, 0:2 * NFC].rearrange("p (f t) -> p f t", f=NFC), cx1[:, :, 128:130]), r=["cx"], w=["acc0"])
                self.P(lambda e: e.tensor_copy(cx1[:, :, 0:2], acc[0][:, 0, 0:2 * NFC].rearrange("p (f t) -> p f t", f=NFC)), r=["acc0"], w=["cx"])
            ffn_core(128, lambda c, i=i: hTd[i % 2][:, c, :], "hT%d" % (i % 2), cx1, "cx", False, aTd[i % 2], "aT%d" % (i % 2), groups=[0])

        pre(0)
        head(0)
        for i in range(NT):
            xt, xk = self.xt[i % 2], "xt%d" % (i % 2)
            hcur = lambda c, i=i: hTd[i % 2][:, c, :]
            hkk = "hT%d" % (i % 2)
            mid = (lambda i=i: pre(i + 1)) if i + 1 < NT else None
            ffn_core(128, hcur, hkk, cx1, "cx", False, aTd[i % 2], "aT%d" % (i % 2), mid, groups=list(range(4, NFC, 4)))
            if i == NT - 1:
                c_token_major(128, hcur, hkk, [(126, 128)], [O["p_conv"][l]])
            if i + 1 < NT:
                head(i + 1)
            xo_, xok = xo[i % 2], "xo%d" % (i % 2)
            finish(128, xt, xk, xo_, xok, O["yp"][i * 128:(i + 1) * 128, :], self.xbuf[i * 128:(i + 1) * 128, :], ("xb", i),
                   aTd[i % 2], "aT%d" % (i % 2))
        if not last and NSEG > 1:
            self.gather_select(xo_[:, :], [xok], D, self.agX_in, self.agX_out, "agX")
            fw.dma(self.xh_dram, xo_[:, :], r=[xok], w=["xh_dram"], key="xhst")

        i = NT
        xt, xk = self.xt[i % 2], "xt%d" % (i % 2)
        fw.dma(xt[0:MS, :], self.xsbuf, r=[("xb", i)], w=[xk], key=xk)
        self.norm_hT(xt, xk, MS, hT[:, :, 0:MS], "hT0", identb)
        hcur = lambda c: hT[:, c, 0:MS]
        fw.dma(cst[0:32, :], I["st_conv"][l], w=["ctok"], key="cst")
        for b0 in range(0, NFC, 4):
            nb = min(4, NFC - b0)
            ps, pk = self.pf()
            for q in range(nb):
                fc = b0 + q
                fw.tr(ps[:, q * 32:(q + 1) * 32], cst[0:32, fc * 128:(fc + 1) * 128], identf[0:32, 0:32], r=["ctok", "identf"], w=[pk])
            fw.act(cxs[:, b0:b0 + nb, :, 0:2], ps[:, 0:nb * 32].rearrange("p (f q j) -> p f q j", f=nb, j=2), AF.Copy, r=[pk], w=["cx"])
        ffn_core(MS, hcur, "hT0", cxs, "cx", True, aTd[0], "aT0")
        sc_ = O["s_conv"][l].rearrange("(q j) f -> j q f", j=2)
        c_token_major(MS, hcur, "hT0", [(32, 48), (48, 64)], [sc_[0], sc_[1]])
        xo_, xok = xo[i % 2], "xo%d" % (i % 2)
        finish(MS, xt, xk, xo_, xok, O["ys"], self.xsbuf, ("xb", NT), aTd[0], "aT0")


NSEG = 1


def _consts_shared():
    c = {}
    c["c_ident"] = np.eye(128, dtype=np.float32)
    inv = (10000.0 ** (-np.arange(0, HD, 2, dtype=np.float32) / HD)).astype(np.float32)
    pos_s = (PAST + np.repeat(np.arange(4), NS)).astype(np.float32)
    ang_s = pos_s[:, None] * inv[None, :]
    c["c_coss"] = np.cos(ang_s).astype(np.float32)
    c["c_sins"] = np.sin(ang_s).astype(np.float32)
    s = np.arange(128)[:, None]
    t = np.arange(128)[None, :]
    incl = (s <= t).astype(np.float32)
    strict = (s < t).astype(np.float32)
    c["c_tri"] = np.concatenate([incl * CDEC, strict * CDEC], 1).astype(np.float32)
    c["c_mask2"] = np.concatenate([incl, strict], 1).astype(np.float32)
    c["c_maskL"] = (s > t).astype(np.float32)
    i_ = np.arange(128)[:, None]
    j_ = np.arange(128)[None, :]
    cur = np.where(j_ <= i_, 0.0, NEG)
    prev = np.where(j_ > i_, 0.0, NEG)
    dead = np.full((128, 128), NEG)
    c["c_amask"] = np.concatenate([cur, prev, prev, cur, cur, dead], 1).astype(np.float32)
    c["_am_first"] = np.concatenate([cur, dead], 1).astype(np.float32)
    c["_am_mid"] = np.concatenate([cur, prev], 1).astype(np.float32)
    tt = (np.arange(32) % 4)[:, None]
    ia = np.arange(128)[None, :]
    ma = np.where(ia <= 124 + tt, 0.0, NEG)
    rb = np.arange(4)[None, :]
    mb = np.where(rb > tt, 0.0, NEG)
    c["c_smask"] = np.concatenate([ma, mb], 1).astype(np.float32)
    last = np.zeros((128, 1), np.float32)
    last[127, 0] = 1.0
    c["c_last"] = last
    c["_inv"] = inv
    return c


def _rope_tab(pos, inv):
    ang = pos.astype(np.float32)[:, None] * inv[None, :]
    return np.cos(ang).astype(np.float32), np.sin(ang).astype(np.float32)


_CACHE = {}
TAPS = False
TAP_OUT = {}


def kernel(**inp):
    inp = {k: np.asarray(v) for k, v in inp.items()}
    xp_all = inp["x_prompt"].astype(np.float32)
    B, SEQ_, _ = xp_all.shape
    TPC = SEQ_ // NSEG
    if TPC not in _CACHE:
        b_ = Builder(TPC, taps=TAPS)
        _CACHE[TPC] = (b_.build(), b_.tapnames)
    nc, tapnames = _CACHE[TPC]
    consts = _consts_shared()
    inv = consts.pop("_inv")
    am_first, am_mid = consts.pop("_am_first"), consts.pop("_am_mid")
    wnames = ["norm_mix_g", "w_in", "rwkv_mu", "rwkv_w0", "rwkv_w2", "rwkv_a0", "rwkv_a2", "rwkv_g2", "rwkv_k_k",
              "rwkv_k_a", "rwkv_ln_g", "rwkv_ln_b", "attn_sinks", "w_br_rwkv", "w_br_attn", "w_out", "norm_ffn_g",
              "ffn_w_in", "ffn_conv_w", "ffn_conv_b", "ffn_w_down", "norm_final_g"]
    shared = {n: np.ascontiguousarray(inp[n], dtype=np.float32) for n in wnames}
    shared["rwkv_r_k"] = np.ascontiguousarray(inp["rwkv_r_k"], dtype=np.float32).reshape(2, RD)
    shared.update(consts)
    in_maps = []
    ncores = 8
    for c in range(ncores):
        b, seg = (c // NSEG) % B, c % NSEG
        sl = slice(c * NS, (c + 1) * NS)
        m = dict(shared)
        t0 = seg * TPC
        m["xp"] = np.ascontiguousarray(xp_all[b, t0:t0 + TPC])
        m["xh0"] = np.ascontiguousarray(xp_all[b, t0 - 128:t0]) if seg > 0 else np.zeros((128, D), np.float32)
        m["c_cosp"], m["c_sinp"] = _rope_tab(t0 + np.arange(TPC), inv)
        m["c_cosh"], m["c_sinh"] = _rope_tab(np.maximum(t0 - 128 + np.arange(128), 0), inv)
        m["c_amask0"] = am_mid if seg > 0 else am_first
        sel = np.zeros((128, 8), np.float32)
        if seg > 0:
            sel[:, c - 1] = 1.0
        m["c_sel"] = sel
        m["xs"] = np.ascontiguousarray(inp["x_sample"][sl].transpose(1, 0, 2).reshape(MS, D))
        m["st_shift"] = np.ascontiguousarray(inp["state_rwkv_shift"][:, sl])
        m["st_wkv"] = np.ascontiguousarray(inp["state_rwkv_wkv"][:, sl]).reshape(2, 128, 4096)
        m["ck"] = np.ascontiguousarray(inp["cache_swa_k"][:, sl]).reshape(2, NS, 128, 128)
        m["cv"] = np.ascontiguousarray(inp["cache_swa_v"][:, sl]).reshape(2, NS, 128, 128)
        m["st_conv"] = np.ascontiguousarray(inp["state_ffn_conv"][:, sl]).reshape(2, 2 * NS, DFF)
        in_maps.append(m)
    res = run_bass_kernel_spmd(nc, in_maps, core_ids=list(range(ncores)))
    R = res.results
    for tn in tapnames:
        TAP_OUT[tn] = [np.asarray(R[c][tn]) for c in range(ncores)]
    f = np.float32
    lastc = [b * NSEG + NSEG - 1 for b in range(B)]
    y_prompt = np.stack([np.concatenate([R[b * NSEG + sg]["yp"] for sg in range(NSEG)], 0) for b in range(B)]).astype(f)
    y_sample = np.concatenate([R[c]["ys"].reshape(4, NS, D).transpose(1, 0, 2) for c in range(ncores)], 0).astype(f)
    p_shift = np.stack([R[c]["p_shift"] for c in lastc], 1).astype(f)
    p_wkv = np.stack([R[c]["p_wkv"] for c in lastc], 1).astype(f)
    p_k = np.stack([R[c]["p_k"] for c in lastc], 1).reshape(2, B, 128, 2, 64).astype(f)
    p_v = np.stack([R[c]["p_v"] for c in lastc], 1).reshape(2, B, 128, 2, 64).astype(f)
    p_conv = np.stack([R[c]["p_conv"] for c in lastc], 1).astype(f)
    s_shift = np.concatenate([R[c]["s_shift"] for c in range(ncores)], 1).astype(f)
    s_wkv = np.concatenate([R[c]["s_wkv"].reshape(2, NS, NH, 64, 64) for c in range(ncores)], 1).astype(f)
    s_k = np.concatenate([R[c]["s_k"].reshape(2, NS, 128, 2, 64) for c in range(ncores)], 1).astype(f)
    s_v = np.concatenate([R[c]["s_v"].reshape(2, NS, 128, 2, 64) for c in range(ncores)], 1).astype(f)
    s_conv = np.concatenate([R[c]["s_conv"].reshape(2, NS, 2, DFF) for c in range(ncores)], 1).astype(f)
    return (y_prompt, y_sample, p_shift, p_wkv, p_k, p_v, p_conv, s_shift, s_wkv, s_k, s_v, s_conv)
```

```python
import math
from contextlib import ExitStack

import numpy as np
import concourse.bass as bass
import concourse.mybir as mybir
from concourse.bass_utils import run_bass_kernel_spmd

F32 = mybir.dt.float32
BF = mybir.dt.bfloat16
AF = mybir.ActivationFunctionType
ALU = mybir.AluOpType
AX = mybir.AxisListType

ENGS = ["sp", "pe", "act", "dve", "pool"]
DEBUG_WHERE = True

D = 1024
HD = 64
NH = 8
RD = 512
RP = 1792
INP = 4608
DFF = 2816
NFC = 22
NS = 16
MS = 64
PAST = 16384
CDEC = -math.exp(-0.5)
NEG = -30000.0


class FW:
    def __init__(self, nc, es):
        self.nc = nc
        self.es = es
        self.ops = {e: [] for e in ENGS}
        self.lastw = {}
        self.readers = {}
        self.dma_count = {}
        self.inc = {}

    def sb(self, name, shape, dt=F32):
        return self.es.enter_context(self.nc.sbuf_tensor(name, list(shape), dt))

    def ps(self, name, shape, dt=F32):
        return self.es.enter_context(self.nc.psum_tensor(name, list(shape), dt))

    def capture(self, f):
        self.cap = []
        f()
        log, self.cap = self.cap, None
        return log

    def replay(self, logs, chunk=2):
        logs = [list(lg) for lg in logs if lg]
        if not logs:
            return
        mn = min(len(lg) for lg in logs)
        per = [max(1, int(round(chunk * len(lg) / mn))) for lg in logs]
        pos = [0] * len(logs)
        while any(p < len(lg) for p, lg in zip(pos, logs)):
            for k, lg in enumerate(logs):
                for _ in range(per[k]):
                    if pos[k] < len(lg):
                        self.op(*lg[pos[k]])
                        pos[k] += 1

    def op(self, eng, fn, r=(), w=(), dma=None):
        if getattr(self, "cap", None) is not None:
            self.cap.append((eng, fn, tuple(r), tuple(w), dma))
            return
        ops = self.ops[eng]
        idx = len(ops)
        deps = set()
        pr = [k for k in r if isinstance(k, str) and k[:2] in ("ps", "pb") and k[2:].isdigit()]
        if pr:
            r = [k for k in r if k not in pr]
            w = list(w) + pr
        for k in r:
            t = self.lastw.get(k)
            if t is not None:
                deps.add(t)
        for k in w:
            t = self.lastw.get(k)
            if t is not None:
                deps.add(t)
            for t2 in self.readers.get(k, {}).values():
                deps.add(t2)
        if dma is not None:
            c = self.dma_count.get(dma, 0) + 1
            self.dma_count[dma] = c
            tok = ("d", dma, c)
        else:
            tok = ("c", eng, idx)
        if eng == "pe":
            deps = {d for d in deps if not (d[0] == "c" and d[1] == "pe")}
        deps.discard(tok)
        rec = dict(fn=fn, deps=deps, tok=tok, signal=False)
        if DEBUG_WHERE:
            import sys as _s
            f_ = _s._getframe(1)
            wh = []
            while f_ is not None and len(wh) < 4:
                wh.append(f_.f_lineno)
                f_ = f_.f_back
            rec["where"] = wh
        ops.append(rec)
        for d in deps:
            if d[0] == "c":
                self.ops[d[1]][d[2]]["signal"] = True
        for k in w:
            self.lastw[k] = tok
            self.readers[k] = {}
        for k in r:
            rk = ("d", tok[1]) if tok[0] == "d" else tok[1]
            self.readers.setdefault(k, {})[rk] = tok
        return tok

    def fence(self):
        toks = set()
        for e in ENGS:
            for rec in reversed(self.ops[e]):
                if rec["tok"][0] == "c" and rec["fn"] is not None:
                    toks.add(rec["tok"])
                    rec["signal"] = True
                    break
        for k, c in self.dma_count.items():
            toks.add(("d", k, c))
        for e in ENGS:
            self.ops[e].append(dict(fn=None, deps=set(toks), tok=("c", e, len(self.ops[e])), signal=False))

    def dma(self, out, in_, r=(), w=(), key=None, eng="sp", **kw):
        self.op(eng, lambda e: e.dma_start(out=out, in_=in_, **kw), r=r, w=w, dma=key)

    def mm(self, out, lhsT, rhs, start, stop, r=(), w=()):
        self.op("pe", lambda e: e.matmul(out, lhsT, rhs, start=start, stop=stop), r=r, w=w)

    def tr(self, out, in_, ident, r=(), w=()):
        self.op("pe", lambda e: e.transpose(out, in_, ident), r=r, w=w)

    def act(self, out, in_, func, r=(), w=(), **kw):
        self.op("act", lambda e: e.activation(out, in_, func, **kw), r=r, w=w)

    def emit(self):
        nc = self.nc
        sems = {e: self.es.enter_context(nc.semaphore("s_" + e)) for e in ENGS}
        dsems = {}
        for i, k in enumerate(self.dma_count):
            dsems[k] = self.es.enter_context(nc.semaphore("d%d" % i))
        for e in ENGS:
            c = 0
            for rec in self.ops[e]:
                if rec["signal"] and rec["tok"][0] == "c":
                    c += 1
                rec["sigval"] = c
        final_counts = dict(self.dma_count)

        def run(engname, eng):
            waited = {}
            for rec in self.ops[engname]:
                need = {}
                for d in rec["deps"]:
                    if d[0] == "c":
                        s = ("c", d[1])
                        v = self.ops[d[1]][d[2]]["sigval"]
                    else:
                        s = ("d", d[1])
                        v = self.inc.get(d[1], 16) * d[2]
                    if need.get(s, 0) < v:
                        need[s] = v
                for s, v in need.items():
                    if waited.get(s, 0) >= v:
                        continue
                    waited[s] = v
                    eng.wait_ge(sems[s[1]] if s[0] == "c" else dsems[s[1]], v)
                if rec["fn"] is None:
                    continue
                try:
                    ins = rec["fn"](eng)
                except Exception:
                    print("EMIT FAILURE at lines", rec.get("where"), "engine", engname)
                    raise
                if rec["tok"][0] == "d":
                    ins.then_inc(dsems[rec["tok"][1]], self.inc.get(rec["tok"][1], 16))
                elif rec["signal"]:
                    ins.then_inc(sems[engname], 1)
            if engname == "sp":
                for k, c in final_counts.items():
                    v = self.inc.get(k, 16) * c
                    if waited.get(("d", k), 0) < v:
                        eng.wait_ge(dsems[k], v)

        with nc.Block() as block:
            @block.sync
            def _(e):
                run("sp", e)

            @block.tensor
            def _(e):
                run("pe", e)

            @block.scalar
            def _(e):
                run("act", e)

            @block.vector
            def _(e):
                run("dve", e)

            @block.gpsimd
            def _(e):
                run("pool", e)


def bc3(ap2, n):
    s = list(ap2.shape)
    return ap2.unsqueeze(2).to_broadcast([s[0], s[1], n])


def h3(ap2, h=NH):
    return ap2.rearrange("p (h d) -> p h d", h=h)


class Builder:
    def __init__(self, TP, taps=False):
        self.TP = TP
        self.NT = TP // 128
        self.taps = taps
        self.nc = bass.Bass("TRN2", target_bir_lowering=False)
        self.I = {}
        self.O = {}
        self.psi = 0
        self.pbi = 0
        self.tapnames = []
        self.pool = None
        self.pcnt = {}

    def din(self, n, s):
        self.I[n] = self.nc.dram_tensor(n, list(s), F32, kind="ExternalInput").ap()

    def dout(self, n, s):
        self.O[n] = self.nc.dram_tensor(n, list(s), F32, kind="ExternalOutput").ap()

    def declare(self):
        TP = self.TP
        for n, s in [("xp", (TP, D)), ("xs", (MS, D)), ("st_shift", (2, NS, RP)), ("st_wkv", (2, 128, 4096)),
                     ("ck", (2, NS, 128, 128)), ("cv", (2, NS, 128, 128)), ("st_conv", (2, 2 * NS, DFF)),
                     ("norm_mix_g", (2, D)), ("w_in", (2, D, INP)), ("rwkv_mu", (2, RP)), ("rwkv_w0", (2, RD)),
                     ("rwkv_w2", (2, 64, RD)), ("rwkv_a0", (2, RD)), ("rwkv_a2", (2, 64, RD)),
                     ("rwkv_g2", (2, 128, RD)), ("rwkv_k_k", (2, RD)), ("rwkv_k_a", (2, RD)),
                     ("rwkv_r_k", (2, RD)), ("rwkv_ln_g", (2, RD)), ("rwkv_ln_b", (2, RD)),
                     ("attn_sinks", (2, NH)), ("w_br_rwkv", (2, RD, D)), ("w_br_attn", (2, RD, D)),
                     ("w_out", (2, D, D)), ("norm_ffn_g", (2, D)), ("ffn_w_in", (2, D, 2 * DFF)),
                     ("ffn_conv_w", (2, 3, DFF)), ("ffn_conv_b", (2, DFF)), ("ffn_w_down", (2, DFF, D)),
                     ("norm_final_g", (D,)),
                     ("c_ident", (128, 128)), ("c_cosp", (TP, 32)), ("c_sinp", (TP, 32)),
                     ("c_coss", (MS, 32)), ("c_sins", (MS, 32)), ("c_tri", (128, 256)),
                     ("c_mask2", (128, 256)), ("c_maskL", (128, 128)), ("c_amask", (128, 768)),
                     ("c_smask", (32, 132)), ("c_last", (128, 1)),
                     ("xh0", (128, D)), ("c_cosh", (128, 32)), ("c_sinh", (128, 32)), ("c_amask0", (128, 256)), ("c_sel", (128, 8))]:
            self.din(n, s)
        for n, s in [("yp", (TP, D)), ("ys", (MS, D)), ("p_shift", (2, RP)), ("p_wkv", (2, NH, 64, 64)),
                     ("p_k", (2, 128, 128)), ("p_v", (2, 128, 128)), ("p_conv", (2, 2, DFF)),
                     ("s_shift", (2, NS, RP)), ("s_wkv", (2, 128, 4096)), ("s_k", (2, NS, 128, 128)),
                     ("s_v", (2, NS, 128, 128)), ("s_conv", (2, 2 * NS, DFF))]:
            self.dout(n, s)
        nc = self.nc
        self.xbuf = nc.dram_tensor("xbuf", [TP, D], F32).ap()
        self.xsbuf = nc.dram_tensor("xsbuf", [MS, D], F32).ap()
        self.mrbuf = nc.dram_tensor("mrbuf", [self.NT + 1, 128, 1024], BF).ap()
        self.xh_dram = nc.dram_tensor("xh_dram", [128, D], F32).ap()
        self.sq = nc.dram_tensor("sq", [6, MS, RD], F32).ap()
        self.sy = nc.dram_tensor("sy", [MS, RD], F32).ap()

    def alloc(self, name, shape, dt=F32):
        shape = list(shape)
        n = 1
        for d_ in shape[1:]:
            n *= d_
        nbytes = n * (4 if dt == F32 else 2)
        nw = (nbytes + 31) // 32 * 8
        off = self.aoff
        self.aoff += nw
        self.apeak = max(self.apeak, self.aoff)
        assert self.aoff <= self.ASZ, "SBUF arena overflow: %s needs %d words (limit %d)" % (name, self.aoff, self.ASZ)
        ap = self.arena[0:shape[0], off:off + nw]
        if dt != F32:
            ap = ap.bitcast(dt)
        ap = ap[:, 0:n]
        if len(shape) > 2:
            names = ["d%d" % i for i in range(len(shape) - 1)]
            pat = "p (%s) -> p %s" % (" ".join(names), " ".join(names))
            ap = ap.rearrange(pat, **{names[i]: shape[i + 1] for i in range(len(names))})
        return ap

    def release(self, mark):
        self.fw.fence()
        self.aoff = mark

    def pf(self):
        ids = {None: [0, 1, 2, 3, 4, 5], 0: [0, 1, 2], 1: [3, 4, 5]}[self.pool]
        c = self.pcnt.setdefault(("f", self.pool), 0)
        self.pcnt[("f", self.pool)] = c + 1
        k = ids[c % len(ids)]
        return self.PS[k], "ps%d" % k

    def pb(self):
        ids = {None: [0, 1], 0: [0], 1: [1]}[self.pool]
        c = self.pcnt.setdefault(("b", self.pool), 0)
        self.pcnt[("b", self.pool)] = c + 1
        k = ids[c % len(ids)]
        return self.PBK[k], "pb%d" % k

    def tap(self, name, ap, rkeys, dt=F32):
        if not self.taps:
            return
        shp = list(ap.shape)
        t = self.nc.dram_tensor("tap_" + name, shp, dt, kind="ExternalOutput").ap()
        self.tapnames.append("tap_" + name)
        self.fw.dma(t, ap, r=rkeys, key="tap_" + name)

    def V(self, fn, r=(), w=()):
        self.fw.op("dve", fn, r, w)

    def P(self, fn, r=(), w=()):
        self.fw.op("pool", fn, r, w)

    def col_load(self, dst, dkey, vec, n):
        fw = self.fw
        st = self.cstage
        fw.dma(st[0:n, :], vec.rearrange("(c p) -> c p", p=128), w=["cstage"], key="cstage")
        ps, pk = self.pf()
        fw.tr(ps[:, 0:n], st[0:n, :], self.identf[0:n, 0:n], r=["cstage", "identf"], w=[pk])
        fw.act(dst, ps[:, 0:n], AF.Copy, r=[pk], w=[dkey])

    def gather_select(self, src_ap, src_keys, n, ag_in, ag_out, name):
        fw = self.fw
        fw.dma(ag_in, src_ap, r=src_keys, w=[name + "_in"], key=name + "_st")
        self.gi = getattr(self, "gi", 0)
        ck = name + "_cc"
        fw.inc[ck] = 1
        fw.op("pool", lambda e: e.collective_compute("AllGather", ALU.bypass, replica_groups=[list(range(8))], ins=[ag_in], outs=[ag_out]),
              r=[name + "_in"], w=[name + "_out"], dma=ck)
        for r_ in range(8):
            st, sk = self.xt[r_ % 2], "xt%d" % (r_ % 2)
            fw.dma(st[:, 0:n], ag_out[r_ * 128:(r_ + 1) * 128, :], r=[name + "_out"], w=[sk], key=sk)
            if r_ == 0:
                self.V(lambda e, st=st: e.tensor_scalar(src_ap, st[:, 0:n], self.sel[:, 0:1], None, ALU.mult), r=[sk, "sel"], w=src_keys)
            else:
                self.V(lambda e, st=st, r_=r_: e.scalar_tensor_tensor(src_ap, st[:, 0:n], self.sel[:, r_:r_ + 1], src_ap, ALU.mult, ALU.add),
                       r=[sk, "sel"] + list(src_keys), w=src_keys)

    def bcast_load(self, dst, dkey, vec):
        self.fw.dma(dst, vec.partition_broadcast(dst.shape[0]), w=[dkey], key=dkey)

    def prep_w(self, nchunks, ncols, src, dst, dkey, mode, scale=None, mul=None, mulkey=None, sview=None):
        fw = self.fw
        for c in range(nchunks):
            for s0 in range(0, ncols, 2048):
                n = min(2048, ncols - s0)
                k = self.wst_i % 4
                self.wst_i += 1
                st = self.wstage[k]
                sk = "wst%d" % k
                fw.dma(st[:, 0:n], src(c, s0, n), w=[sk], key=sk)
                o = dst(c, s0, n)
                dk = dkey(c)
                if sview is not None:
                    sv_ = sview(st[:, 0:n])
                    sc = scale(c)
                    self.V(lambda eg, o=o, sv_=sv_, sc=sc: eg.tensor_scalar(o, sv_, sc, None, ALU.mult), r=[sk, "gcol"], w=[dk])
                    continue
                if mode == "plain":
                    e = ["dve", "pool", "act"][self.wst_i % 3]
                    if e == "act":
                        fw.act(o, st[:, 0:n], AF.Copy, r=[sk], w=[dk])
                    else:
                        fw.op(e, lambda eg, o=o, st=st, n=n: eg.tensor_copy(o, st[:, 0:n]), r=[sk], w=[dk])
                elif mode == "col":
                    sc = scale(c)
                    e = ["dve", "pool"][self.wst_i % 2]
                    fw.op(e, lambda eg, o=o, st=st, n=n, sc=sc: eg.tensor_scalar(o, st[:, 0:n], sc, None, ALU.mult),
                          r=[sk, "gcol"], w=[dk])
                else:
                    sc = scale(c)
                    m = mul(s0, n)
                    self.V(lambda eg, o=o, st=st, n=n, sc=sc, m=m: eg.scalar_tensor_tensor(
                        o, st[:, 0:n], sc, m, ALU.mult, ALU.mult), r=[sk, "gcol", mulkey], w=[dk])

    def norm_hT(self, xt, xk, M, hdst, hkey, identb):
        self.norm_a(xt, xk, M)
        self.norm_b(M, hdst, hkey, identb)

    def norm_a(self, xt, xk, M):
        fw = self.fw
        xn, ss, t1 = self.xn, self.ss, self.t1
        fw.act(xn[0:M, :], xt[0:M, :], AF.Square, r=[xk], w=["xn", "ss"], accum_out=ss[0:M, :])
        self.V(lambda e: e.tensor_scalar(t1[0:M, :], ss[0:M, :], 1.0 / D, 1e-6, ALU.mult, ALU.add), r=["ss"], w=["t1"])
        fw.act(t1[0:M, :], t1[0:M, :], AF.Sqrt, r=["t1"], w=["t1"])
        self.V(lambda e: e.reciprocal(t1[0:M, :], t1[0:M, :]), r=["t1"], w=["t1"])
        self.V(lambda e: e.tensor_scalar(xn[0:M, :], xt[0:M, :], t1[0:M, 0:1], None, ALU.mult), r=[xk, "t1"], w=["xn"])

    def norm_b(self, M, hdst, hkey, identb):
        fw = self.fw
        xn = self.xn
        pbk, pk = self.pb()
        for c in range(8):
            fw.tr(pbk[:, c * M:(c + 1) * M], xn[0:M, c * 128:(c + 1) * 128], identb[0:M, 0:M], r=["xn", "identb"], w=[pk])
        fw.act(hdst, pbk[:, 0:8 * M].rearrange("p (c t) -> p c t", c=8), AF.Copy, r=[pk], w=[hkey])

    def build(self):
        self.declare()
        nc = self.nc
        with ExitStack() as es:
            self.fw = fw = FW(nc, es)
            self.PS = [fw.ps("ps%d" % i, [128, 512], F32) for i in range(6)]
            self.PBK = [fw.ps("pb%d" % i, [128, 1024], BF) for i in range(2)]
            self.ASZ = 52224
            self.arena = fw.sb("arena", [128, self.ASZ])
            self.aoff = 0
            self.apeak = 0
            self.identf = self.alloc("identf", [128, 128])
            self.identb = self.alloc("identb", [128, 128], BF)
            self.cstage = self.alloc("cstage", [32, 128])
            self.wst_i = 0
            self.xn = self.alloc("xn", [128, D], BF)
            self.ss = self.alloc("ss", [128, 1])
            self.t1 = self.alloc("t1", [128, 1])
            self.gcol = self.alloc("gcol", [128, 8])
            self.xt = [self.alloc("xt%d" % i, [128, D]) for i in range(2)]
            self.sel = self.alloc("sel", [128, 8])
            fw.dma(self.sel[:], self.I["c_sel"], w=["sel"], key="sel")
            fw.dma(self.identf[:], self.I["c_ident"], w=["identf"], key="identf")
            self.V(lambda e: e.tensor_copy(self.identb[:], self.identf[:]), r=["identf"], w=["identb"])
            for l in range(2):
                for p_ in (self.pass_rwkv, self.pass_attn, self.pass_ffn):
                    mk_ = self.aoff
                    p_(l, None)
                    self.release(mk_)
            print("arena peak words", self.apeak, "of", self.ASZ)
            fw.emit()
        return nc

    def sbl(self, es2, name, shape, dt=F32):
        return self.alloc(name, shape, dt)

    def xsrc(self, l, i):
        if i < self.NT:
            src = self.I["xp"] if l == 0 else self.xbuf
            return src[i * 128:(i + 1) * 128, :], ("xb", i)
        src = self.I["xs"] if l == 0 else self.xsbuf
        return src, ("xb", i)

    def pass_rwkv(self, l, es2):
        fw, I, O, NT = self.fw, self.I, self.O, self.NT
        sbl = lambda n, s, dt=F32: self.sbl(es2, "r%d_" % l + n, s, dt)
        identb, identf = self.identb, self.identf
        W1 = sbl("W1", [128, 8, RP], BF)
        W2 = sbl("W2", [128, 8, RP], BF)
        Wg = sbl("Wg", [128, 8, D], BF)
        Wr = sbl("Wr", [128, 4, D], BF)
        lw2 = sbl("lw2", [128, RD], BF)
        lg2 = sbl("lg2", [128, RD], BF)
        bcs = {}
        for n in ["rwkv_w0", "rwkv_a0", "rwkv_k_k", "rwkv_k_a", "rwkv_r_k", "rwkv_ln_g", "rwkv_ln_b"]:
            bcs[n] = sbl(n, [128, RD])
            self.bcast_load(bcs[n][:], n + "_bc", I[n][l])
        mucol = sbl("mucol", [128, 2])
        tri = sbl("tri", [128, 256])
        mask2 = sbl("mask2", [128, 256])
        maskL = sbl("maskL", [128, 128])
        clast = sbl("clast", [128, 1])
        fw.dma(tri[:], I["c_tri"], w=["tri"], key="tri")
        fw.dma(mask2[:], I["c_mask2"], w=["mask2"], key="mask2")
        fw.dma(maskL[:], I["c_maskL"], w=["maskL"], key="maskL")
        fw.dma(clast[:], I["c_last"], w=["clast"], key="clast")
        self.col_load(self.gcol[:], "gcol", I["norm_mix_g"][l], 8)
        self.col_load(mucol[:], "mucol", I["rwkv_mu"][l, 1536:1792], 2)
        m0 = self.aoff
        self.wstage = [sbl("wst%d" % i_, [128, 2048]) for i_ in range(4)]
        mu_bc = sbl("mu_bc", [128, RP])
        omm_bc = sbl("omm_bc", [128, RP])
        self.bcast_load(mu_bc[:], "mu_bc", I["rwkv_mu"][l])
        self.V(lambda e: e.tensor_scalar(omm_bc[:], mu_bc[:], -1.0, 1.0, ALU.mult, ALU.add), r=["mu_bc"], w=["omm_bc"])
        win = I["w_in"][l]
        gsc = lambda c: self.gcol[:, c:c + 1]
        self.prep_w(8, RP, lambda c, s0, n: win[c * 128:(c + 1) * 128, s0:s0 + n],
                    lambda c, s0, n: W1[:, c, s0:s0 + n], lambda c: "W1_%d" % c, "colmul", gsc,
                    lambda s0, n: omm_bc[:, s0:s0 + n], "omm_bc")
        self.prep_w(8, RP, lambda c, s0, n: win[c * 128:(c + 1) * 128, s0:s0 + n],
                    lambda c, s0, n: W2[:, c, s0:s0 + n], lambda c: "W2_%d" % c, "colmul", gsc,
                    lambda s0, n: mu_bc[:, s0:s0 + n], "mu_bc")
        self.prep_w(8, D, lambda c, s0, n: win[c * 128:(c + 1) * 128, 2560 + s0:2560 + s0 + n],
                    lambda c, s0, n: Wg[:, c, s0:s0 + n], lambda c: "Wg_%d" % c, "col", gsc)
        wbr = I["w_br_rwkv"][l]
        self.prep_w(4, D, lambda c, s0, n: wbr[c * 128:(c + 1) * 128, s0:s0 + n],
                    lambda c, s0, n: Wr[:, c, s0:s0 + n], lambda c: "Wr_%d" % c, "plain")
        for (nm, p0, dk_) in [("rwkv_w2", 0, "lw2a"), ("rwkv_a2", 64, "lw2b")]:
            k = self.wst_i % 4
            self.wst_i += 1
            wsk = self.wstage[k]
            fw.dma(wsk[p0:p0 + 64, 0:RD], I[nm][l], w=["wst%d" % k], key="wst%d" % k)
            self.P(lambda e, wsk=wsk, p0=p0: e.tensor_copy(lw2[p0:p0 + 64, :], wsk[p0:p0 + 64, 0:RD]), r=["wst%d" % k], w=[dk_])
        self.prep_w(1, RD, lambda c, s0, n: I["rwkv_g2"][l], lambda c, s0, n: lg2[:, :], lambda c: "lg2", "plain")
        WK1 = ["W1_%d" % c for c in range(8)]
        WK2 = ["W2_%d" % c for c in range(8)]
        self.release(m0)
        class NSP:
            pass
        zr, zk = sbl("zr", [128, RD]), sbl("zk", [128, RD])
        lact = sbl("lact", [128, 128], BF)
        T = [sbl("tmp%d" % i_, [128, RD]) for i_ in range(8)]
        sm = sbl("sm", [128, 64])
        orT = sbl("orT", [128, 4, 128], BF)
        sgr = sbl("sgr", [128, 8, 128], BF)
        mrT0_ = sbl("mrT0", [128, 8, 128], BF)
        mrT = [mrT0_, mrT0_]
        TP_ = [sbl("tpost%d" % i_, [128, RD]) for i_ in range(2)]
        m1 = self.aoff
        NRB = 9864

        def mkrec(k):
            R = NSP()
            rb = sbl("RB%d" % k, [128, NRB], BF)
            rf = sbl("RF%d" % k, [128, 528])
            R.rb, R.rf, R.k = rb, rf, k
            R.RKT = rb[:, 0:1024].rearrange("p (j a t) -> p j a t", j=4, a=2)
            R.G4 = [rb[:, 1024 + j * 1280:1024 + (j + 1) * 1280].rearrange("p (h c) -> p h c", h=2) for j in range(4)]
            R.ZF = [rb[:, 6144 + j * 256:6144 + (j + 1) * 256].rearrange("p (h c) -> p h c", h=2) for j in range(4)]
            R.vb, R.ktt, R.bnt = rb[:, 7168:7680], rb[:, 7680:8192], rb[:, 8192:8704]
            R.sgT = rb[:, 8704:8832]
            R.hT = rb[:, 8832:9864].rearrange("p (c t) -> p c t", c=8)
            R.zv, R.WC, R.bon = rf[:, 0:512], rf[:, 512:516], rf[:, 516:524]
            R.K = (lambda k_: (lambda n: "%s#%d" % (n, k_)))(k)
            return R
        R0 = mkrec(0)
        U0b = [sbl("U0b%d" % j, [128, 2, 64], BF) for j in range(4)]
        Ub = sbl("Ub", [128, RD], BF)
        Nst = sbl("Nst", [128, 4, 128])
        Nb = sbl("Nb", [128, 4, 128], BF)
        self.V(lambda e: e.memset(Nst[:], 0.0), w=["Nst"])
        self.V(lambda e: e.memset(Nb[:], 0.0), w=["Nb"])
        m2 = self.aoff
        rt, kat = sbl("rt", [128, RD], BF), sbl("kat", [128, RD], BF)
        KT = sbl("KT", [128, 4, 128], BF)
        BT = sbl("BT", [128, 4, 128], BF)
        for j in range(4):
            self.P(lambda e, j=j: e.tensor_copy(R0.G4[j][:, :, 512:640], identb[:, :].unsqueeze(1).to_broadcast([128, 2, 128])),
                   r=["identb"], w=["G4_%d" % j])
        EZ = [[sbl("EZ%d_%d" % (j, a), [128, 2, 2, 128], BF) for a in range(2)] for j in range(4)]
        FFa = [sbl("FFa%d" % a, [128, 4, 2, 128], BF) for a in range(2)]
        FF = [[FFa[a][:, j] for a in range(2)] for j in range(4)]

        def tok_proj(M, hcur, hprev, hk, g0, dstkey):
            ps, pk = self.pf()
            n = 0
            for c in range(8):
                fw.mm(ps[0:M, :], hcur(c), W1[:, c, g0:g0 + 512], n == 0, False, r=[hk, WK1[c]], w=[pk])
                n += 1
            for c in range(8):
                fw.mm(ps[0:M, :], hprev(c), W2[:, c, g0:g0 + 512], False, c == 7, r=[hk, WK2[c]], w=[pk])
            return ps, pk

        def feat_proj(M, hcur, hprev, hk, g0):
            ps, pk = self.pf()
            for c in range(8):
                fw.mm(ps[:, 0:M], W1[:, c, g0:g0 + 128], hcur(c), c == 0, False, r=[hk, WK1[c]], w=[pk])
            for c in range(8):
                fw.mm(ps[:, 0:M], W2[:, c, g0:g0 + 128], hprev(c), False, c == 7, r=[hk, WK2[c]], w=[pk])
            return ps, pk

        def raw_last(hl, hk, M, dst):
            for gi, g0 in enumerate(range(0, RP, 512)):
                n = min(512, RP - g0)
                ps, pk = self.pf()
                for c in range(8):
                    fw.mm(ps[0:M, 0:n], hl(c), W1[:, c, g0:g0 + n], c == 0, False, r=[hk, WK1[c]], w=[pk])
                for c in range(8):
                    fw.mm(ps[0:M, 0:n], hl(c), W2[:, c, g0:g0 + n], False, c == 7, r=[hk, WK2[c]], w=[pk])
                fw.act(T[gi][0:M, 0:n], ps[0:M, 0:n], AF.Copy, r=[pk], w=["T%d" % gi])
                fw.dma(dst[:, g0:g0 + n], T[gi][0:M, 0:n], r=["T%d" % gi], key="zl%d" % gi)

        def prep(M, sample, R):
            K = R.K
            w0, a0 = bcs["rwkv_w0"], bcs["rwkv_a0"]
            kkb, kab, rkb = bcs["rwkv_k_k"], bcs["rwkv_k_a"], bcs["rwkv_r_k"]
            pw, pwk = self.pf()
            fw.mm(pw[0:M, :], lact[0:64, 0:M], lw2[0:64, :], True, True, r=["lact", "lw2a"], w=[pwk])
            pa, pak = self.pf()
            fw.mm(pa[0:M, :], lact[64:128, 0:M], lw2[64:128, :], True, True, r=["lact", "lw2b"], w=[pak])
            sg, a_, kk, t3, kf, be = T[0], T[1], T[2], T[3], T[4], T[5]
            self.V(lambda e: e.tensor_tensor(sg[0:M, :], pw[0:M, :], w0[0:M, :], ALU.add), r=[pwk, "rwkv_w0_bc"], w=["T0"])
            fw.act(sg[0:M, :], sg[0:M, :], AF.Sigmoid, r=["T0"], w=["T0"])
            self.V(lambda e: e.tensor_tensor(a_[0:M, :], pa[0:M, :], a0[0:M, :], ALU.add), r=[pak, "rwkv_a0_bc"], w=["T1"])
            fw.act(a_[0:M, :], a_[0:M, :], AF.Sigmoid, r=["T1"], w=["T1"])
            self.P(lambda e: e.tensor_tensor(kk[0:M, :], zk[0:M, :], kkb[0:M, :], ALU.mult), r=["zk", "rwkv_k_k_bc"], w=["T2"])
            self.P(lambda e: e.tensor_tensor(t3[0:M, :], kk[0:M, :], kk[0:M, :], ALU.mult), r=["T2"], w=["T3"])
            self.V(lambda e: e.tensor_reduce(sm[0:M, 0:8], h3(t3[0:M, :]), AX.X, ALU.add), r=["T3"], w=["sm0"])
            fw.act(sm[0:M, 0:8], sm[0:M, 0:8], AF.Sqrt, r=["sm0"], w=["sm0"])
            self.V(lambda e: e.tensor_scalar(sm[0:M, 0:8], sm[0:M, 0:8], 1e-12, None, ALU.max), r=["sm0"], w=["sm0"])
            self.V(lambda e: e.reciprocal(sm[0:M, 0:8], sm[0:M, 0:8]), r=["sm0"], w=["sm0"])
            self.V(lambda e: e.tensor_tensor(h3(kk[0:M, :]), h3(kk[0:M, :]), bc3(sm[0:M, 0:8], 64), ALU.mult),
                   r=["T2", "sm0"], w=["T2"])
            self.V(lambda e: e.scalar_tensor_tensor(t3[0:M, :], a_[0:M, :], -1.0, kab[0:M, :], ALU.add, ALU.mult),
                   r=["T1", "rwkv_k_a_bc"], w=["T3"])
            self.V(lambda e: e.scalar_tensor_tensor(kf[0:M, :], t3[0:M, :], 1.0, zk[0:M, :], ALU.add, ALU.mult),
                   r=["T3", "zk"], w=["T4"])
            self.P(lambda e: e.tensor_tensor(be[0:M, :], kk[0:M, :], a_[0:M, :], ALU.mult), r=["T2", "T1"], w=["T5"])
            self.P(lambda e: e.tensor_tensor(t3[0:M, :], zr[0:M, :], kf[0:M, :], ALU.mult), r=["zr", "T4"], w=["T3"])
            self.P(lambda e: e.tensor_tensor(t3[0:M, :], t3[0:M, :], rkb[0:M, :], ALU.mult), r=["T3", "rwkv_r_k_bc"], w=["T3"])
            self.V(lambda e, R=R: e.tensor_reduce(R.bon[0:M, :], h3(t3[0:M, :]), AX.X, ALU.add), r=["T3"], w=[K("bon")])
            if sample:
                fw.act(T[6][0:M, :], sg[0:M, :], AF.Exp, r=["T0"], w=["T6"], scale=CDEC)
                for x, (tl, tk) in enumerate([(zr, "zr"), (T[6], "T6"), (kf, "T4"), (R.zv, K("zv")), (kk, "T2"), (be, "T5")]):
                    fw.dma(self.sq[x], tl[0:M, :], r=[tk], w=[("sq", x)], key="sqw%d" % x)
                return
            pli, plik = self.pf()
            fw.mm(pli[:, :], tri[:, 0:128], sg[:, :], True, True, r=["tri", "T0"], w=[plik])
            ple, plek = self.pf()
            fw.mm(ple[:, :], tri[:, 128:256], sg[:, :], True, True, r=["tri", "T0"], w=[plek])
            eL, eLm, enL = T[6], T[7], T[3]
            fw.act(eL[:, :], pli[:, :], AF.Exp, r=[plik], w=["T6"])
            fw.act(eLm[:, :], ple[:, :], AF.Exp, r=[plek], w=["T7"])
            fw.act(enL[:, :], pli[:, :], AF.Exp, r=[plik], w=["T3"], scale=-1.0)
            self.V(lambda e: e.tensor_tensor(rt[:, :], zr[:, :], eL[:, :], ALU.mult), r=["zr", "T6"], w=["rt"])
            self.V(lambda e: e.tensor_tensor(kat[:, :], kk[:, :], eLm[:, :], ALU.mult), r=["T2", "T7"], w=["kat"])
            self.P(lambda e, R=R: e.tensor_tensor(R.ktt[:, :], kf[:, :], enL[:, :], ALU.mult), r=["T4", "T3"], w=[K("ktt")])
            self.V(lambda e, R=R: e.scalar_tensor_tensor(R.bnt[:, :], be[:, :], -1.0, enL[:, :], ALU.mult, ALU.mult),
                   r=["T5", "T3"], w=[K("bnt")])
            fw.act(R.vb[:, :], R.zv[:, :], AF.Copy, r=[K("zv")], w=[K("vb")])
            pwc, pwck = self.pf()
            for j in range(4):
                fw.mm(pwc[:, j:j + 1], eL[:, j * 128:(j + 1) * 128], clast[:, :], True, True, r=["T6", "clast"], w=[pwck])
            fw.act(R.WC[:, :], pwc[:, 0:4], AF.Copy, r=[pwck], w=[K("WC")])
            for (src, skey, dstf, dk) in [(rt, "rt", None, "RKT"), (kat, "kat", None, "RKT"),
                                          (R.ktt, K("ktt"), None, "KT"), (R.bnt, K("bnt"), None, "BT")]:
                pbk, pk = self.pb()
                for j in range(4):
                    fw.tr(pbk[:, j * 128:(j + 1) * 128], src[:, j * 128:(j + 1) * 128], identb[:, :], r=[skey, "identb"], w=[pk])
                if dk == "RKT":
                    which = 0 if skey == "rt" else 1
                    fw.act(R.RKT[:, :, which, :], pbk[:, 0:512].rearrange("p (j t) -> p j t", j=4), AF.Copy, r=[pk], w=["RKT%d" % which])
                else:
                    dst = KT if dk == "KT" else BT
                    self.V(lambda e, dst=dst, pbk=pbk: e.tensor_copy(dst[:, :, :], pbk[:, 0:512].rearrange("p (j t) -> p j t", j=4)),
                           r=[pk], w=[dk])

        def stageAB(R):
            K = R.K
            RK = [K("RKT0"), K("RKT1")]
            RKT, G4, ZF = R.RKT, R.G4, R.ZF
            zb = [self.pf(), self.pf()]
            for j in range(4):
                for hh in range(2):
                    o = hh * 64
                    pZ, pzk = zb[hh]
                    fw.mm(pZ[:, j * 128:(j + 1) * 128], RKT[o:o + 64, j, 1, :], BT[o:o + 64, j, :], True, True, r=["BT", K("RKT1")], w=[pzk])
            mlb = maskL[:, :].unsqueeze(1).to_broadcast([128, 4, 128])
            for hh in range(2):
                pZ, pzk = zb[hh]
                self.V(lambda e, pZ=pZ, hh=hh: e.tensor_tensor(FFa[0][:, :, hh, :], pZ[:, :].rearrange("p (j c) -> p j c", j=4), mlb, ALU.mult),
                       r=[pzk, "maskL"], w=["FF%d_0" % j for j in range(4)])
            for j in range(4):
                bk = [self.pf(), self.pf()]
                for hh in range(2):
                    o = hh * 64
                    ps, pk = bk[hh]
                    rhs = RKT[o:o + 64, j, :, :].rearrange("p a t -> p (a t)")
                    fw.mm(ps[:, 0:256], KT[o:o + 64, j, :], rhs, True, True, r=["KT"] + RK, w=[pk])
                    fw.mm(ps[:, 256:512], BT[o:o + 64, j, :], rhs, True, True, r=["BT"] + RK, w=[pk])
                for hh in range(2):
                    ps, pk = bk[hh]
                    self.V(lambda e, j=j, hh=hh, ps=ps, G4=G4: e.tensor_tensor(
                        G4[j][:, hh, 0:512].rearrange("p (a c) -> p a c", a=2), ps[:, :].rearrange("p (a c) -> p a c", a=2),
                        mask2[:, :].unsqueeze(1).to_broadcast([128, 2, 256]), ALU.mult), r=[pk, "mask2"], w=[K("G4_%d" % j)])
            for lev in range(7):
                a, b = lev % 2, (lev + 1) % 2
                for j in range(4):
                    fk, fn_ = "FF%d_%d" % (j, a), "FF%d_%d" % (j, b)
                    ezn = "EZ%d_%d" % (j, b)
                    if lev == 0:
                        ezk = K("G4_%d" % j)
                        EZs = lambda hh, j=j, G4=G4: G4[j][:, hh, 384:640]
                        Es = lambda hh, j=j, G4=G4: G4[j][:, hh, 384:512]
                        Zs = lambda j=j, G4=G4: G4[j][:, :, 512:640]
                    else:
                        ezk = "EZ%d_%d" % (j, a)
                        EZs = lambda hh, j=j, a=a: EZ[j][a][:, hh, :, :].rearrange("p a t -> p (a t)")
                        Es = lambda hh, j=j, a=a: EZ[j][a][:, hh, 0, :]
                        Zs = lambda j=j, a=a: EZ[j][a][:, :, 1, :]
                    if lev < 6:
                        pL, plk = self.pf()
                        for hh in range(2):
                            fw.mm(pL[:, hh * 256:(hh + 1) * 256], FF[j][a][:, hh, :], EZs(hh), True, True, r=[ezk, fk], w=[plk])
                        pF, pfk = self.pf()
                        for hh in range(2):
                            fw.mm(pF[:, hh * 128:(hh + 1) * 128], Es(hh), FF[j][a][:, hh, :], True, True, r=[ezk, fk], w=[pfk])
                        l3 = pL[:, :].rearrange("p (h c) -> p h c", h=2)
                        fw.act(EZ[j][b][:, :, 0, :], l3[:, :, 0:128], AF.Copy, r=[plk], w=[ezn])
                        self.V(lambda e, j=j, b=b, l3=l3, Zs=Zs: e.tensor_tensor(EZ[j][b][:, :, 1, :], l3[:, :, 128:256], Zs(), ALU.add),
                               r=[plk, ezk], w=[ezn])
                        fw.act(FF[j][b][:, :, :], pF[:, 0:256].rearrange("p (h c) -> p h c", h=2), AF.Copy, r=[pfk], w=[fn_])
                    else:
                        pL, plk = self.pf()
                        for hh in range(2):
                            fw.mm(pL[:, hh * 128:(hh + 1) * 128], FF[j][a][:, hh, :], EZ[j][a][:, hh, 1, :], True, True, r=[ezk, fk], w=[plk])
                        self.V(lambda e, j=j, a=a, pL=pL, ZF=ZF: e.tensor_tensor(ZF[j][:, :, :], pL[:, 0:256].rearrange("p (h c) -> p h c", h=2),
                                                                      EZ[j][a][:, :, 1, :], ALU.add), r=[plk, ezk], w=[K("ZF%d" % j)])

        def stageC(R):
            K = R.K
            RKT, G4, ZF, vb = R.RKT, R.G4, R.ZF, R.vb
            for j in range(4):
                pU, puk = self.pf()
                for hh in range(2):
                    o, h = hh * 64, 2 * j + hh
                    fw.mm(pU[:, hh * 64:(hh + 1) * 64], RKT[o:o + 64, j, 1, :], Nb[o:o + 64, j, o:o + 64], True, False, r=[K("RKT1"), "Nb"], w=[puk])
                    fw.mm(pU[:, hh * 64:(hh + 1) * 64], G4[j][:, hh, 128:256], vb[:, h * 64:(h + 1) * 64], False, True, r=[K("G4_%d" % j), K("vb")], w=[puk])
                fw.act(U0b[j][:, :, :], pU[:, 0:128].rearrange("p (h c) -> p h c", h=2), AF.Copy, r=[puk], w=["U0b%d" % j])
            for j in range(4):
                pU, puk = self.pf()
                for hh in range(2):
                    fw.mm(pU[:, hh * 64:(hh + 1) * 64], ZF[j][:, hh, :], U0b[j][:, hh, :], True, True, r=[K("ZF%d" % j), "U0b%d" % j], w=[puk])
                fw.act(Ub[:, j * 128:(j + 1) * 128], pU[:, 0:128], AF.Copy, r=[puk], w=["Ub%d" % j])

        def stageD(R):
            K = R.K
            RKT, G4, vb = R.RKT, R.G4, R.vb
            psY, pyk = self.pf()
            for j in range(4):
                for hh in range(2):
                    o, h = hh * 64, 2 * j + hh
                    fw.mm(psY[:, h * 64:(h + 1) * 64], RKT[o:o + 64, j, 0, :], Nb[o:o + 64, j, o:o + 64], True, False, r=[K("RKT0"), "Nb"], w=[pyk])
                    fw.mm(psY[:, h * 64:(h + 1) * 64], G4[j][:, hh, 0:128], vb[:, h * 64:(h + 1) * 64], False, False, r=[K("G4_%d" % j), K("vb")], w=[pyk])
                    fw.mm(psY[:, h * 64:(h + 1) * 64], G4[j][:, hh, 256:384], Ub[:, h * 64:(h + 1) * 64], False, True, r=[K("G4_%d" % j), "Ub%d" % j], w=[pyk])
            return psY, pyk

        def n_update(R):
            K = R.K
            ktt, bnt, vb, WC = R.ktt, R.bnt, R.vb, R.WC
            pN, pnk = self.pf()
            for j in range(4):
                fw.mm(pN[:, j * 128:(j + 1) * 128], ktt[:, j * 128:(j + 1) * 128], vb[:, j * 128:(j + 1) * 128], True, False, r=[K("ktt"), K("vb")], w=[pnk])
                fw.mm(pN[:, j * 128:(j + 1) * 128], bnt[:, j * 128:(j + 1) * 128], Ub[:, j * 128:(j + 1) * 128], False, True, r=[K("bnt"), "Ub%d" % j], w=[pnk])
            n2 = Nst[:, :, :].rearrange("p j c -> p (j c)")
            self.V(lambda e: e.tensor_tensor(n2, pN[:, :], n2, ALU.add), r=[pnk, "Nst"], w=["Nst"])
            self.V(lambda e, WC=WC: e.tensor_tensor(Nst[:, :, :], Nst[:, :, :], bc3(WC[:, :], 128), ALU.mult), r=["Nst", K("WC")], w=["Nst"])
            fw.act(Nb[:, :, :], Nst[:, :, :], AF.Copy, r=["Nst"], w=["Nb"])


        def post(M, yap, ykeys, pg, pgk, R):
            K = R.K
            lng, lnb = bcs["rwkv_ln_g"], bcs["rwkv_ln_b"]
            y2, yc = TP_[0], TP_[1]
            ob = TP_[0].bitcast(BF)[:, 0:RD]
            self.V(lambda e: e.tensor_reduce(sm[0:M, 16:24], h3(yap), AX.X, ALU.add), r=ykeys, w=["sm2"])
            fw.act(y2[0:M, :], yap, AF.Square, r=ykeys, w=["TP0"])
            self.V(lambda e: e.tensor_reduce(sm[0:M, 24:32], h3(y2[0:M, :]), AX.X, ALU.add), r=["TP0"], w=["sm3"])
            mean, var = sm[0:M, 16:24], sm[0:M, 24:32]
            self.V(lambda e: e.tensor_scalar(mean, mean, 1.0 / 64, None, ALU.mult), r=["sm2"], w=["sm2"])
            self.V(lambda e: e.tensor_tensor(sm[0:M, 32:40], mean, mean, ALU.mult), r=["sm2"], w=["sm4"])
            self.V(lambda e: e.scalar_tensor_tensor(var, var, 1.0 / 64, sm[0:M, 32:40], ALU.mult, ALU.subtract), r=["sm3", "sm4"], w=["sm3"])
            self.V(lambda e: e.tensor_scalar(var, var, 64e-5, None, ALU.add), r=["sm3"], w=["sm3"])
            fw.act(var, var, AF.Sqrt, r=["sm3"], w=["sm3"])
            self.V(lambda e: e.reciprocal(var, var), r=["sm3"], w=["sm3"])
            self.V(lambda e: e.tensor_tensor(h3(yc[0:M, :]), h3(yap), bc3(mean, 64), ALU.subtract), r=list(ykeys) + ["sm2"], w=["TP1"])
            self.V(lambda e: e.tensor_tensor(h3(yc[0:M, :]), h3(yc[0:M, :]), bc3(var, 64), ALU.mult), r=["TP1", "sm3"], w=["TP1"])
            self.P(lambda e: e.tensor_tensor(yc[0:M, :], yc[0:M, :], lng[0:M, :], ALU.mult), r=["TP1", "rwkv_ln_g_bc"], w=["TP1"])
            self.P(lambda e: e.tensor_tensor(yc[0:M, :], yc[0:M, :], lnb[0:M, :], ALU.add), r=["TP1", "rwkv_ln_b_bc"], w=["TP1"])
            self.P(lambda e, R=R: e.tensor_tensor(h3(y2[0:M, :]), h3(R.zv[0:M, :]), bc3(R.bon[0:M, :], 64), ALU.mult), r=[K("zv"), K("bon")], w=["TP0"])
            self.V(lambda e: e.tensor_tensor(yc[0:M, :], yc[0:M, :], y2[0:M, :], ALU.add), r=["TP1", "TP0"], w=["TP1"])
            self.V(lambda e: e.tensor_tensor(ob[0:M, :], yc[0:M, :], pg[0:M, :], ALU.mult), r=["TP1", pgk], w=["TP0"])
            pbk, pk = self.pb()
            for j in range(4):
                fw.tr(pbk[:, j * M:(j + 1) * M], ob[0:M, j * 128:(j + 1) * 128], identb[0:M, 0:M], r=["TP0", "identb"], w=[pk])
            fw.act(orT[:, :, 0:M], pbk[:, 0:4 * M].rearrange("p (j t) -> p j t", j=4), AF.Copy, r=[pk], w=["orT"])

        def gate_branch(M, hcur, hk, mdst, mkey):
            for half in range(2):
                pg, pgk = self.pf()
                for q in range(4):
                    dc = half * 4 + q
                    for c in range(8):
                        fw.mm(pg[:, q * M:(q + 1) * M], Wg[:, c, dc * 128:(dc + 1) * 128], hcur(c), c == 0, c == 7, r=[hk, "Wg_%d" % c], w=[pgk])
                fw.act(sgr[:, half * 4:(half + 1) * 4, 0:M], pg[:, 0:4 * M].rearrange("p (q t) -> p q t", q=4), AF.Sigmoid, r=[pgk], w=["sgr%d" % half])
                pbr, pbk_ = self.pf()
                for q in range(4):
                    dc = half * 4 + q
                    for j in range(4):
                        fw.mm(pbr[:, q * M:(q + 1) * M], Wr[:, j, dc * 128:(dc + 1) * 128], orT[:, j, 0:M], j == 0, j == 3, r=["orT", "Wr_%d" % j], w=[pbk_])
                self.V(lambda e, half=half, pbr=pbr: e.tensor_tensor(mdst[:, half * 4:(half + 1) * 4, 0:M], sgr[:, half * 4:(half + 1) * 4, 0:M],
                                                                 pbr[:, 0:4 * M].rearrange("p (q t) -> p q t", q=4), ALU.mult),
                       r=["sgr%d" % half, pbk_], w=[mkey])

        R1 = mkrec(1)
        for j in range(4):
            self.P(lambda e, j=j: e.tensor_copy(R1.G4[j][:, :, 512:640], identb[:, :].unsqueeze(1).to_broadcast([128, 2, 128])),
                   r=["identb"], w=[R1.K("G4_%d" % j)])
        RR = [R0, R1]

        def H1a(i):
            R, Rp = RR[i % 2], RR[(i + 1) % 2]
            hT = R.hT
            xt, xk = self.xt[i % 2], "xt%d" % (i % 2)
            src, _ = self.xsrc(l, i)
            fw.dma(xt[:], src, r=[("xb", i)], w=[xk], key=xk)
            hk = R.K("hTr")
            if i == 0:
                self.V(lambda e, hT=hT: e.memset(hT[:, :, 0:1], 0.0), w=[hk])
            else:
                self.P(lambda e, hT=hT, hp=Rp.hT: e.tensor_copy(hT[:, :, 0:1], hp[:, :, 128:129]), r=[Rp.K("hTr")], w=[hk])
            self.norm_a(xt, xk, 128)

        def H1b(i):
            R = RR[i % 2]
            K = R.K
            hT = R.hT
            hk = K("hTr")
            self.norm_b(128, hT[:, :, 1:129], hk, identb)
            hcur = lambda c, hT=hT: hT[:, c, 1:129]
            hprev = lambda c, hT=hT: hT[:, c, 0:128]
            for g0, dst, dk in [(0, zr, "zr"), (512, zk, "zk"), (1024, R.zv, K("zv"))]:
                ps, pk = tok_proj(128, hcur, hprev, hk, g0, dk)
                fw.act(dst[:, :], ps[:, :], AF.Copy, r=[pk], w=[dk])
            ps, pk = feat_proj(128, hcur, hprev, hk, 1536)
            fw.act(lact[0:64, :], ps[0:64, 0:128], AF.Tanh, r=[pk], w=["lact"])
            fw.act(lact[64:128, :], ps[64:128, 0:128], AF.Copy, r=[pk], w=["lact"])
            ps, pk = feat_proj(128, hcur, hprev, hk, 1664)
            fw.act(R.sgT[:, :], ps[:, 0:128], AF.Sigmoid, r=[pk], w=[K("sgT")])
            if i == NT - 1:
                raw_last(lambda c, hT=hT: hT[:, c, 128:129], hk, 1, O["p_shift"][l:l + 1, :])

        def H1c(i):
            prep(128, False, RR[i % 2])

        def H1d(i):
            stageAB(RR[i % 2])

        H2st = {}

        def H2a(i):
            R = RR[i % 2]
            stageC(R)
            psY, pyk = stageD(R)
            n_update(R)
            pg, pgk = self.pf()
            fw.mm(pg[:, :], R.sgT[:, :], lg2[:, :], True, True, r=[R.K("sgT"), "lg2"], w=[pgk])
            H2st[i] = (psY, pyk, pg, pgk)

        def H2b(i):
            psY, pyk, pg, pgk = H2st.pop(i)
            post(128, psY[:, :], [pyk], pg, pgk, RR[i % 2])

        def H2c(i):
            R = RR[i % 2]
            m, mk = mrT[0], "mrT0"
            gate_branch(128, lambda c, R=R: R.hT[:, c, 1:129], R.K("hTr"), m, mk)
            fw.dma(self.mrbuf[i].rearrange("p (c t) -> p c t", c=8), m[:, :, :], r=[mk], w=[("mr", i)], key=mk)

        def cap(pool, f, i):
            self.pool = pool
            return fw.capture(lambda: f(i))

        for f in (H1a, H1b, H1c):
            fw.replay([cap(0, f, 0)])
        fw.replay([cap(None, H1d, 0)])
        for i in range(NT):
            nx = i + 1 < NT
            if nx:
                fw.replay([cap(0, H1a, i + 1)])
            fw.replay([cap(1, H2a, i)])
            fw.replay(([cap(0, H1b, i + 1)] if nx else []) + [cap(1, H2b, i)])
            fw.replay(([cap(0, H1c, i + 1)] if nx else []) + [cap(1, H2c, i)])
            if nx:
                fw.replay([cap(None, H1d, i + 1)])
        self.pool = None
        for j in range(4):
            ps, pk = self.pf()
            fw.tr(ps[:, 0:128], Nst[:, j, :], identf[:, :], r=["Nst", "identf"], w=[pk])
            fw.act(T[0][:, j * 128:(j + 1) * 128], ps[:, 0:128], AF.Copy, r=[pk], w=["T0"])
        for h_ in range(8):
            j, o = h_ // 2, (h_ % 2) * 64
            fw.dma(O["p_wkv"][l, h_], T[0][o:o + 64, j * 128 + o:j * 128 + o + 64], r=["T0"], key="T0")

        self.release(m1)
        RS = NSP()
        RS.zv = sbl("zv_s", [128, RD])
        RS.sgT = sbl("sgT_s", [128, 128], BF)
        RS.bon = sbl("bon_s", [128, 8])
        RS.K = lambda n: n + "#s"
        hTs = sbl("hTs", [128, 8, 80], BF)
        sadd = sbl("sadd", [16, RP])
        stT = sbl("stT", [128, 2, 16])
        zf = sbl("zf", [128, 2, 64])
        QH = sbl("QH", [128, 6, 4, 64])
        Sst = sbl("Sst", [128, 64, 64])
        Stmp = sbl("Stmp", [128, 64, 64])
        sk = sbl("sk", [128, 64])
        yh = sbl("yh", [128, 4, 64])
        ytm = T[7]
        self.V(lambda e: e.memset(hTs[:], 0.0), w=["hTs"])
        i = NT
        xt, xk = self.xt[i % 2], "xt%d" % (i % 2)
        src, _ = self.xsrc(l, i)
        fw.dma(xt[0:MS, :], src, r=[("xb", i)], w=[xk], key=xk)
        self.norm_hT(xt, xk, MS, hTs[:, :, 16:80], "hTs", identb)
        hcur = lambda c: hTs[:, c, 16:80]
        hprev = lambda c: hTs[:, c, 0:64]
        fw.dma(sadd[:, :], I["st_shift"][l], w=["sadd"], key="sadd")
        for q in range(2):
            ps, pk = self.pf()
            fw.tr(ps[:, 0:16], sadd[0:16, 1536 + q * 128:1536 + (q + 1) * 128], identf[0:16, 0:16], r=["sadd", "identf"], w=[pk])
            self.V(lambda e, q=q, ps=ps: e.tensor_scalar(stT[:, q, :], ps[:, 0:16], mucol[:, q:q + 1], None, ALU.mult), r=[pk, "mucol"], w=["stT"])
        for gi, g0 in enumerate(range(0, RP, 512)):
            n = min(512, RP - g0)
            self.bcast_load(T[4 + gi][0:16, 0:n], "T%d" % (4 + gi), I["rwkv_mu"][l, g0:g0 + n])
            self.V(lambda e, gi=gi, g0=g0, n=n: e.tensor_tensor(sadd[:, g0:g0 + n], sadd[:, g0:g0 + n], T[4 + gi][0:16, 0:n], ALU.mult),
                   r=["sadd", "T%d" % (4 + gi)], w=["sadd"])
        zv, sgT = RS.zv, RS.sgT
        for g0, dst, dk in [(0, zr, "zr"), (512, zk, "zk"), (1024, zv, RS.K("zv"))]:
            ps, pk = tok_proj(MS, hcur, hprev, "hTs", g0, dk)
            fw.act(dst[0:MS, :], ps[0:MS, :], AF.Copy, r=[pk], w=[dk])
            self.V(lambda e, dst=dst, g0=g0: e.tensor_tensor(dst[0:16, :], dst[0:16, :], sadd[0:16, g0:g0 + 512], ALU.add), r=[dk, "sadd"], w=[dk])
        for q, g0 in enumerate([1536, 1664]):
            ps, pk = feat_proj(MS, hcur, hprev, "hTs", g0)
            fw.act(zf[:, q, :], ps[:, 0:MS], AF.Copy, r=[pk], w=["zf"])
            self.V(lambda e, q=q: e.tensor_tensor(zf[:, q, 0:16], zf[:, q, 0:16], stT[:, q, :], ALU.add), r=["zf", "stT"], w=["zf"])
        fw.act(lact[0:64, 0:MS], zf[0:64, 0, :], AF.Tanh, r=["zf"], w=["lact"])
        fw.act(lact[64:128, 0:MS], zf[64:128, 0, :], AF.Copy, r=["zf"], w=["lact"])
        fw.act(sgT[:, 0:MS], zf[:, 1, :], AF.Sigmoid, r=["zf"], w=[RS.K("sgT")])
        prep(MS, True, RS)
        if l == 0:
            for nm, ap, k in [("s_zr", zr, "zr"), ("s_zk", zk, "zk"), ("s_zv", zv, "zv"), ("s_dec", T[6], "T6"), ("s_kk", T[2], "T2"),
                              ("s_kf", T[4], "T4"), ("s_be", T[5], "T5"), ("s_a", T[1], "T1")]:
                self.tap(nm, ap[0:MS, :], [k])
        sqv = self.sq.rearrange("x (t q) (h d) -> (q h) x t d", t=4, h=NH)
        for x in range(6):
            fw.dma(QH[:, x, :, :], sqv[:, x, :, :], r=[("sq", x)], w=["QH"], key="QH")
        fw.dma(Sst[:, :, :].rearrange("p v k -> p (v k)"), I["st_wkv"][l], w=["Sst"], key="Sst")
        for t in range(4):
            r_, w_, k_, v_, kk_, b_ = (QH[:, x, t, :] for x in range(6))
            rowb = lambda a: a.unsqueeze(1).to_broadcast([128, 64, 64])
            colb = lambda a: a.unsqueeze(2).to_broadcast([128, 64, 64])
            self.V(lambda e, kk_=kk_: e.tensor_tensor(Stmp[:, :, :], Sst[:, :, :], rowb(kk_), ALU.mult), r=["Sst", "QH"], w=["Stmp"])
            self.V(lambda e: e.tensor_reduce(sk[:, :], Stmp[:, :, :], AX.X, ALU.add), r=["Stmp"], w=["sk"])
            self.P(lambda e, w_=w_: e.tensor_tensor(Sst[:, :, :], Sst[:, :, :], rowb(w_), ALU.mult), r=["Sst", "QH", "Stmp"], w=["Sst"])
            self.V(lambda e, b_=b_: e.tensor_tensor(Stmp[:, :, :], colb(sk[:, :]), rowb(b_), ALU.mult), r=["sk", "QH"], w=["Stmp"])
            self.V(lambda e: e.tensor_tensor(Sst[:, :, :], Sst[:, :, :], Stmp[:, :, :], ALU.subtract), r=["Sst", "Stmp"], w=["Sst"])
            self.P(lambda e, v_=v_, k_=k_: e.tensor_tensor(Stmp[:, :, :], colb(v_), rowb(k_), ALU.mult), r=["QH", "Sst"], w=["Stmp"])
            self.V(lambda e: e.tensor_tensor(Sst[:, :, :], Sst[:, :, :], Stmp[:, :, :], ALU.add), r=["Sst", "Stmp"], w=["Sst"])
            self.P(lambda e, r_=r_: e.tensor_tensor(Stmp[:, :, :], Sst[:, :, :], rowb(r_), ALU.mult), r=["Sst", "QH"], w=["Stmp"])
            self.V(lambda e, t=t: e.tensor_reduce(yh[:, t, :], Stmp[:, :, :], AX.X, ALU.add), r=["Stmp"], w=["yh"])
        fw.dma(O["s_wkv"][l], Sst[:, :, :].rearrange("p v k -> p (v k)"), r=["Sst"], key="Sst")
        if l == 0:
            self.tap("s_QH", QH, ["QH"])
            self.tap("s_yh", yh, ["yh"])
        fw.dma(self.sy.rearrange("(t q) (h d) -> (q h) t d", t=4, h=NH), yh[:, :, :], r=["yh"], w=["sy"], key="yh")
        fw.dma(ytm[0:MS, :], self.sy, r=["sy"], w=["T7"], key="ytm")
        pg, pgk = self.pf()
        fw.mm(pg[0:MS, :], sgT[:, 0:MS], lg2[:, :], True, True, r=[RS.K("sgT"), "lg2"], w=[pgk])
        post(MS, ytm[0:MS, :], ["T7"], pg, pgk, RS)
        m, mk = mrT[0], "mrT0"
        gate_branch(MS, hcur, "hTs", m, mk)
        fw.dma(self.mrbuf[NT].rearrange("p (c t) -> p c t", c=8)[:, :, 0:MS], m[:, :, 0:MS], r=[mk], w=[("mr", NT)], key=mk)
        raw_last(lambda c: hTs[:, c, 64:80], "hTs", 16, O["s_shift"][l])

    def pass_attn(self, l, es2):
        fw, I, O, NT = self.fw, self.I, self.O, self.NT
        sbl = lambda n, s, dt=F32: self.sbl(es2, "a%d_" % l + n, s, dt)
        identb, identf = self.identb, self.identf
        Wq = sbl("Wq", [128, 8, 768], BF)
        Wg = sbl("Wg", [128, 8, D], BF)
        Wa = sbl("Wa", [128, 4, D], BF)
        Wo = sbl("Wo", [128, 8, D], BF)
        self.col_load(self.gcol[:], "gcol", I["norm_mix_g"][l], 8)
        m0 = self.aoff
        self.wstage = [sbl("wst%d" % i_, [128, 2048]) for i_ in range(4)]
        win = I["w_in"][l]
        gsc = lambda c: self.gcol[:, c:c + 1]
        self.prep_w(8, 512, lambda c, s0, n: win[c * 128:(c + 1) * 128, RP:RP + 512],
                    lambda c, s0, n: Wq[:, c, 0:512].rearrange("p (j g d) -> p g j d", j=4, g=2), lambda c: "Wq_%d" % c, "col", gsc,
                    sview=lambda a: a.rearrange("p (g j d) -> p g j d", g=2, j=4))
        self.prep_w(8, 256, lambda c, s0, n: win[c * 128:(c + 1) * 128, RP + 512:RP + 768],
                    lambda c, s0, n: Wq[:, c, 512:768], lambda c: "Wq_%d" % c, "col", gsc)
        self.prep_w(8, D, lambda c, s0, n: win[c * 128:(c + 1) * 128, 3584 + s0:3584 + s0 + n],
                    lambda c, s0, n: Wg[:, c, s0:s0 + n], lambda c: "Wga_%d" % c, "col", gsc)
        wbr = I["w_br_attn"][l]
        self.prep_w(4, D, lambda c, s0, n: wbr[c * 128:(c + 1) * 128, s0:s0 + n],
                    lambda c, s0, n: Wa[:, c, s0:s0 + n], lambda c: "Wa_%d" % c, "plain")
        wo = I["w_out"][l]
        self.prep_w(8, D, lambda c, s0, n: wo[c * 128:(c + 1) * 128, s0:s0 + n],
                    lambda c, s0, n: Wo[:, c, s0:s0 + n], lambda c: "Wo_%d" % c, "plain")
        self.release(m0)
        amask = sbl("amask", [128, 1024])
        fw.dma(amask[:, 0:768], I["c_amask"], w=["amask"], key="amask")
        fw.dma(amask[:, 768:1024], I["c_amask0"], w=["amask"], key="amask")
        smask = sbl("smask", [32, 132])
        fw.dma(smask[:], I["c_smask"], w=["smask"], key="smask")
        sinks = sbl("sinks", [128, NH])
        self.bcast_load(sinks[:], "sinks", I["attn_sinks"][l])
        hTd = [sbl("hT%d" % i_, [128, 8, 128], BF) for i_ in range(2)]
        hT = hTd[1]
        qkv = sbl("qkv", [128, 768])
        rot = sbl("rot", [128, 640])
        rtmp = [sbl("rtmp%d" % i, [128, 320]) for i in range(2)]
        rotb = sbl("rotb", [128, 640], BF)
        cs = [sbl("cs%d" % i, [128, 64]) for i in range(2)]
        qT = sbl("qT", [128, 4, 128], BF)

        class NSB:
            pass
        B0, B1 = NSB(), NSB()
        B0.qkv, B0.rot, B0.rotb, B0.qT, B0.s = qkv, rot, rotb, qT, ""
        B1.qkv, B1.rot, B1.rotb, B1.qT, B1.s = (sbl("qkvb", [128, 768]), sbl("rotbb", [128, 640]), sbl("rotbbb", [128, 640], BF),
                                                sbl("qTb", [128, 4, 128], BF), "b")
        Bs = [B0, B1]
        KTr = sbl("KTr", [128, 2, 128], BF)
        Vp = sbl("Vp", [128, 2, 2, 2, 128], BF)
        scg = [sbl("sc%d" % g_, [128, 4, 256]) for g_ in range(2)]
        stg = [sbl("st%d" % g_, [128, 16]) for g_ in range(2)]
        pbfg = [sbl("pbf%d" % g_, [128, 4, 256], BF) for g_ in range(2)]
        pTg = [sbl("pT%d" % g_, [128, 4, 2, 128], BF) for g_ in range(2)]
        oT = sbl("oT", [128, 4, 128], BF)
        sga = sbl("sga", [128, 8, 128])
        mrl = [sbl("mrl%d" % i, [128, 8, 128], BF) for i in range(2)]
        mg = sbl("mg", [128, 8, 128], BF)
        xo = [sbl("xo%d" % i, [128, D]) for i in range(2)]
        KA = sbl("KA", [128, NS, 128])
        VA = sbl("VA", [128, NS, 128])
        VAb = sbl("VAb", [128, NS, 128], BF)
        KB = sbl("KB", [4, NS, 128])
        VBt = sbl("VB", [4, NS, 128])
        VBb = sbl("VBb", [4, NS, 128], BF)
        KAT = sbl("KAT", [128, NS, 128], BF)
        KBT = sbl("KBT", [128, NS, 4], BF)
        qbd = sbl("qbd", [128, NS, 32], BF)
        ssc = sbl("ssc", [32, NS, 132])
        sst = sbl("sst", [32, 4 * NS])
        spb = sbl("spb", [32, NS, 132], BF)
        spT = sbl("spT", [128, NS, 32], BF)
        spTB = sbl("spTB", [4, NS, 32], BF)
        oTs = sbl("oTs", [128, 4, MS], BF)

        self.V(lambda e: e.memset(Vp[:], 0.0), w=["Vp0", "Vp1"])
        self.V(lambda e: e.memset(KTr[:], 0.0), w=["KTr0", "KTr1"])
        self.V(lambda e: e.memset(qbd[:], 0.0), w=["qbd"])

        def proj_rope(B, M, hcur, hk, cosap, sinap, cskey):
            for g0, n in [(0, 512), (512, 256)]:
                ps, pk = self.pf()
                for c in range(8):
                    fw.mm(ps[0:M, 0:n], hcur(c), Wq[:, c, g0:g0 + n], c == 0, c == 7, r=[hk, "Wq_%d" % c], w=[pk])
                fw.act(B.qkv[0:M, g0:g0 + n], ps[0:M, 0:n], AF.Copy, r=[pk], w=["qkv%d" % (g0 // 512) + B.s])
            qk3 = B.qkv[0:M, 0:640].rearrange("p (h d) -> p h d", h=10)
            r3 = B.rot[0:M, :].rearrange("p (h d) -> p h d", h=10)
            x1, x2 = qk3[:, :, 0:32], qk3[:, :, 32:64]
            cb = cosap.unsqueeze(1).to_broadcast([M, 10, 32])
            sb_ = sinap.unsqueeze(1).to_broadcast([M, 10, 32])
            ta = rtmp[0][0:M, :].rearrange("p (h d) -> p h d", h=10)
            tb = rtmp[1][0:M, :].rearrange("p (h d) -> p h d", h=10)
            rk = ["qkv0" + B.s, "qkv1" + B.s, cskey]
            rotk = "rot" + B.s
            self.V(lambda e: e.tensor_tensor(ta, x1, cb, ALU.mult), r=rk, w=["rtmp0"])
            self.P(lambda e: e.tensor_tensor(tb, x2, sb_, ALU.mult), r=rk, w=["rtmp1"])
            self.V(lambda e: e.tensor_tensor(r3[:, :, 0:32], ta, tb, ALU.subtract), r=["rtmp0", "rtmp1"], w=[rotk])
            self.V(lambda e: e.tensor_tensor(ta, x2, cb, ALU.mult), r=rk + [rotk], w=["rtmp0"])
            self.P(lambda e: e.tensor_tensor(tb, x1, sb_, ALU.mult), r=rk + [rotk], w=["rtmp1"])
            self.V(lambda e: e.tensor_tensor(r3[:, :, 32:64], ta, tb, ALU.add), r=["rtmp0", "rtmp1"], w=[rotk])
            fw.act(B.rotb[0:M, :], B.rot[0:M, :], AF.Copy, r=[rotk], w=["rotb" + B.s])

        def q_transposes(B, M, dst, dkey):
            pbk, pk = self.pb()
            for jj in range(4):
                fw.tr(pbk[:, jj * M:(jj + 1) * M], B.rotb[0:M, jj * 128:(jj + 1) * 128], identb[0:M, 0:M], r=["rotb" + B.s, "identb"], w=[pk])
            fw.act(dst, pbk[:, 0:4 * M].rearrange("p (j t) -> p j t", j=4), AF.Copy, r=[pk], w=[dkey])

        def gates_part(M, hcur, hk):
            for half in range(2):
                pg, pgk = self.pf()
                for q in range(4):
                    dc = half * 4 + q
                    for c in range(8):
                        fw.mm(pg[:, q * M:(q + 1) * M], Wg[:, c, dc * 128:(dc + 1) * 128], hcur(c), c == 0, c == 7, r=[hk, "Wga_%d" % c], w=[pgk])
                fw.act(sga[:, half * 4:(half + 1) * 4, 0:M], pg[:, 0:4 * M].rearrange("p (q t) -> p q t", q=4), AF.Sigmoid, r=[pgk], w=["sga%d" % half])

        def gate_out(M, hcur, hk, oTt, okey, mr, mrk, xt, xk, xo_, xok, do_gates=True):
            if do_gates:
                gates_part(M, hcur, hk)
            for half in range(2):
                pbr, pbk_ = self.pf()
                for q in range(4):
                    dc = half * 4 + q
                    for cc in range(4):
                        fw.mm(pbr[:, q * M:(q + 1) * M], Wa[:, cc, dc * 128:(dc + 1) * 128], oTt[:, cc, 0:M], cc == 0, cc == 3, r=[okey, "Wa_%d" % cc], w=[pbk_])
                hs = slice(half * 4, (half + 1) * 4)
                self.V(lambda e, hs=hs, pbr=pbr: e.tensor_tensor(sga[:, hs, 0:M], sga[:, hs, 0:M], pbr[:, 0:4 * M].rearrange("p (q t) -> p q t", q=4), ALU.mult),
                       r=["sga%d" % half, pbk_], w=["sga%d" % half])
                self.V(lambda e, hs=hs: e.tensor_tensor(mg[:, hs, 0:M], sga[:, hs, 0:M], mr[:, hs, 0:M], ALU.add), r=["sga%d" % half, mrk], w=["mg%d" % half])
            for grp in range(2):
                px, pxk = self.pf()
                for dc in range(8):
                    fw.mm(px[0:M, :], mg[:, dc, 0:M], Wo[:, dc, grp * 512:(grp + 1) * 512], dc == 0, dc == 7, r=["mg%d" % (dc // 4), "Wo_%d" % dc], w=[pxk])
                self.V(lambda e, grp=grp, px=px: e.tensor_tensor(xo_[0:M, grp * 512:(grp + 1) * 512], xt[0:M, grp * 512:(grp + 1) * 512], px[0:M, :], ALU.add),
                       r=[xk, pxk], w=[xok])

        def put_kv(B, slot):
            pbk, pk = self.pb()
            fw.tr(pbk[:, 0:128], B.rotb[:, 512:640], identb[:, :], r=["rotb" + B.s, "identb"], w=[pk])
            self.V(lambda e, pbk=pbk, slot=slot: e.tensor_copy(KTr[:, slot, :], pbk[:, 0:128]), r=[pk], w=["KTr%d" % slot])
            for g in range(2):
                vsrc = B.qkv[:, 640 + g * 64:640 + (g + 1) * 64]
                fw.act(Vp[:, slot, g, 0, 0:64], vsrc, AF.Copy, r=["qkv1" + B.s], w=["Vp%d" % slot])
                self.P(lambda e, g=g, vsrc=vsrc, slot=slot: e.tensor_copy(Vp[:, slot, g, 1, 64:128], vsrc), r=["qkv1" + B.s], w=["Vp%d" % slot])

        xt, xk = self.xt[1], "xt1"
        fw.dma(xt[:], (I["xh0"] if (l == 0 or NSEG == 1) else self.xh_dram), r=["xh_dram"], w=[xk], key=xk)
        fw.dma(cs[1][:, 0:32], I["c_cosh"], w=["cs1"], key="cs1")
        fw.dma(cs[1][:, 32:64], I["c_sinh"], w=["cs1"], key="cs1")
        self.norm_hT(xt, xk, 128, hT[:, :, :], "hT1", identb)
        proj_rope(B1, 128, lambda c: hT[:, c, :], "hT1", cs[1][:, 0:32], cs[1][:, 32:64], "cs1")
        put_kv(B1, 1)
        def pre(i):
            xt, xk = self.xt[i % 2], "xt%d" % (i % 2)
            src, _ = self.xsrc(l, i)
            fw.dma(xt[:], src, r=[("xb", i)], w=[xk], key=xk)
            mr, mrk = mrl[i % 2], "mrl%d" % (i % 2)
            fw.dma(mr[:, :, :], self.mrbuf[i].rearrange("p (c t) -> p c t", c=8), r=[("mr", i)], w=[mrk], key=mrk)
            ck_ = "cs%d" % (i % 2)
            fw.dma(cs[i % 2][:, 0:32], I["c_cosp"][i * 128:(i + 1) * 128, :], w=[ck_], key=ck_)
            fw.dma(cs[i % 2][:, 32:64], I["c_sinp"][i * 128:(i + 1) * 128, :], w=[ck_], key=ck_)
            self.norm_hT(xt, xk, 128, hTd[i % 2][:, :, :], "hT%d" % (i % 2), identb)
            B = Bs[i % 2]
            proj_rope(B, 128, lambda c, i=i: hTd[i % 2][:, c, :], "hT%d" % (i % 2), cs[i % 2][:, 0:32], cs[i % 2][:, 32:64], ck_)
            q_transposes(B, 128, B.qT[:, :, :], "qT" + B.s)

        pre(0)
        for i in range(NT):
            xt, xk = self.xt[i % 2], "xt%d" % (i % 2)
            mr, mrk = mrl[i % 2], "mrl%d" % (i % 2)
            ck_ = "cs%d" % (i % 2)
            hkk = "hT%d" % (i % 2)
            hcur = lambda c, i=i: hTd[i % 2][:, c, :]
            B = Bs[i % 2]
            slot = i % 2
            if i == NT - 1:
                fw.dma(O["p_k"][l], B.rot[:, 512:640], r=["rot" + B.s], key="rot")
                fw.dma(O["p_v"][l], B.qkv[:, 640:768], r=["qkv1" + B.s], key="qkv1")
            put_kv(B, slot)
            mvar = 3 if i == 0 else slot
            msk = amask[:, mvar * 256:(mvar + 1) * 256].unsqueeze(1).to_broadcast([128, 4, 256])
            pSg = []
            for g in range(2):
                o = g * 64
                pS = []
                for jj in range(4):
                    if jj % 2 == 0:
                        ps, pk = self.pf()
                        pS.append((ps, pk))
                    fw.mm(ps[:, (jj % 2) * 256:(jj % 2 + 1) * 256], B.qT[o:o + 64, jj, :], KTr[o:o + 64, :, :].rearrange("p s t -> p (s t)"),
                          True, True, r=["qT" + B.s, "KTr0", "KTr1"], w=[pk])
                pSg.append(pS)
            gates_part(128, hcur, hkk)

            def softmax(g):
                sc, st, pbf = scg[g], stg[g], pbfg[g]
                sck = ["sc%d_0" % g, "sc%d_1" % g]
                for half, (ps, pk) in enumerate(pSg[g]):
                    self.V(lambda e, ps=ps, half=half, msk=msk, sc=sc: e.scalar_tensor_tensor(
                        sc[:, half * 2:(half + 1) * 2, :], ps[:, :].rearrange("p (j c) -> p j c", j=2), 0.125,
                        msk[:, 0:2, :], ALU.mult, ALU.add), r=[pk, "amask"], w=[sck[half]])
                k0, k2, k3 = "st%d" % g, "st%d_2" % g, "st%d_3" % g
                self.V(lambda e: e.tensor_reduce(st[:, 0:4], sc[:, :, :], AX.X, ALU.max), r=sck, w=[k0])
                self.V(lambda e: e.tensor_tensor(st[:, 0:4], st[:, 0:4], sinks[:, g * 4:(g + 1) * 4], ALU.max), r=[k0, "sinks"], w=[k0])
                self.V(lambda e: e.tensor_tensor(sc[:, :, :], sc[:, :, :], bc3(st[:, 0:4], 256), ALU.subtract), r=sck + [k0], w=sck)
                fw.act(sc[:, :, :], sc[:, :, :], AF.Exp, r=sck, w=sck)
                self.V(lambda e: e.tensor_reduce(st[:, 4:8], sc[:, :, :], AX.X, ALU.add), r=sck, w=[k2])
                self.V(lambda e: e.tensor_tensor(st[:, 8:12], sinks[:, g * 4:(g + 1) * 4], st[:, 0:4], ALU.subtract), r=[k0, "sinks"], w=[k3])
                fw.act(st[:, 8:12], st[:, 8:12], AF.Exp, r=[k3], w=[k3])
                self.V(lambda e: e.tensor_tensor(st[:, 4:8], st[:, 4:8], st[:, 8:12], ALU.add), r=[k2, k3], w=[k2])
                self.V(lambda e: e.reciprocal(st[:, 4:8], st[:, 4:8]), r=[k2], w=[k2])
                self.V(lambda e: e.tensor_tensor(pbf[:, :, :], sc[:, :, :], bc3(st[:, 4:8], 256), ALU.mult), r=sck + [k2], w=["pbf%d" % g])

            def p_transposes(g):
                pbf, pT = pbfg[g], pTg[g]
                pbk, pk = self.pb()
                for jj in range(4):
                    for s_ in range(2):
                        fw.tr(pbk[:, (jj * 2 + s_) * 128:(jj * 2 + s_ + 1) * 128], pbf[:, jj, s_ * 128:(s_ + 1) * 128], identb[:, :], r=["pbf%d" % g, "identb"], w=[pk])
                fw.act(pT[:, :, :, :], pbk[:, :].rearrange("p (j s t) -> p j s t", j=4, s=2), AF.Copy, r=[pk], w=["pT%d" % g])

            def pv(g, pO, pok):
                pT = pTg[g]
                for c2 in range(2):
                    cc = g * 2 + c2
                    n = 0
                    for par in range(2):
                        jj = c2 * 2 + par
                        for s_ in range(2):
                            fw.mm(pO[:, cc * 128:(cc + 1) * 128], Vp[:, s_, g, par, :], pT[:, jj, s_, :], n == 0, n == 3,
                                  r=["Vp0", "Vp1", "pT%d" % g], w=[pok])
                            n += 1

            fw.replay([fw.capture(lambda: softmax(0)), fw.capture(lambda: softmax(1))], chunk=1)
            p_transposes(0)
            pO, pok = self.pf()
            pv(0, pO, pok)
            if i + 1 < NT:
                pre(i + 1)
            p_transposes(1)
            pv(1, pO, pok)
            fw.act(oT[:, :, :], pO[:, :].rearrange("p (c t) -> p c t", c=4), AF.Copy, r=[pok], w=["oT"])
            xo_, xok = xo[i % 2], "xo%d" % (i % 2)
            gate_out(128, hcur, hkk, oT, "oT", mr, mrk, xt, xk, xo_, xok, do_gates=False)
            fw.dma(self.xbuf[i * 128:(i + 1) * 128, :], xo_[:, :], r=[xok], w=[("xb", i)], key=xok)
        if NSEG > 1:
            self.gather_select(xo_[:, :], [xok], D, self.agX_in, self.agX_out, "agX")
            fw.dma(self.xh_dram, xo_[:, :], r=[xok], w=["xh_dram"], key="xhst")

        i = NT
        xt, xk = self.xt[i % 2], "xt%d" % (i % 2)
        src, _ = self.xsrc(l, i)
        fw.dma(xt[0:MS, :], src, r=[("xb", i)], w=[xk], key=xk)
        mr, mrk = mrl[i % 2], "mrl%d" % (i % 2)
        fw.dma(mr[:, :, 0:MS], self.mrbuf[NT].rearrange("p (c t) -> p c t", c=8)[:, :, 0:MS], r=[("mr", NT)], w=[mrk], key=mrk)
        ck_ = "cs%d" % (i % 2)
        fw.dma(cs[i % 2][0:MS, 0:32], I["c_coss"], w=[ck_], key=ck_)
        fw.dma(cs[i % 2][0:MS, 32:64], I["c_sins"], w=[ck_], key=ck_)
        self.norm_hT(xt, xk, MS, hT[:, :, 0:MS], "hT1", identb)
        hcur = lambda c: hT[:, c, 0:MS]
        proj_rope(B0, MS, hcur, "hT1", cs[i % 2][0:MS, 0:32], cs[i % 2][0:MS, 32:64], ck_)
        for (cin, cout, srcap, srck, dkey) in [("ck", "s_k", rot[:, 512:640], "rot", "sk"), ("cv", "s_v", qkv[:, 640:768], "qkv1", "sv")]:
            fw.dma(O[cout][l, :, 0:124, :], I[cin][l, :, 4:128, :], w=[dkey], key=dkey + "c")
            for t in range(4):
                fw.dma(O[cout][l, :, 124 + t, :], srcap[t * 16:(t + 1) * 16, :], r=[srck], w=[dkey], key=dkey + "n")
        fw.dma(KA[:, :, :], O["s_k"][l].rearrange("q p c -> p q c"), r=["sk"], w=["KA"], key="KA")
        fw.dma(VA[:, :, :], O["s_v"][l].rearrange("q p c -> p q c"), r=["sv"], w=["VA"], key="VA")
        fw.dma(KB[:, :, :], I["ck"][l, :, 0:4, :].rearrange("q p c -> p q c"), w=["KB"], key="KB")
        fw.dma(VBt[:, :, :], I["cv"][l, :, 0:4, :].rearrange("q p c -> p q c"), w=["VB"], key="VB")
        self.P(lambda e: e.tensor_copy(VAb[:, :, :], VA[:, :, :]), r=["VA"], w=["VAb"])
        self.P(lambda e: e.tensor_copy(VBb[:, :, :], VBt[:, :, :]), r=["VB"], w=["VBb"])
        for q4 in range(4):
            ps, pk = self.pf()
            for qq in range(4):
                q = q4 * 4 + qq
                fw.tr(ps[:, qq * 128:(qq + 1) * 128], KA[:, q, :], identf[:, :], r=["KA", "identf"], w=[pk])
            fw.act(KAT[:, q4 * 4:(q4 + 1) * 4, :], ps[:, :].rearrange("p (q t) -> p q t", q=4), AF.Copy, r=[pk], w=["KAT"])
        ps, pk = self.pf()
        for q in range(NS):
            fw.tr(ps[:, q * 4:(q + 1) * 4], KB[0:4, q, :], identf[0:4, 0:4], r=["KB", "identf"], w=[pk])
        fw.act(KBT[:, :, :], ps[:, 0:64].rearrange("p (q t) -> p q t", q=NS), AF.Copy, r=[pk], w=["KBT"])
        q_transposes(B0, MS, qT[:, :, 0:MS], "qT")
        for g in range(2):
            for jj in range(4):
                o = g * 64
                dst = qbd[o:o + 64, :, g * 16 + jj * 4:g * 16 + (jj + 1) * 4]
                srcq = qT[o:o + 64, jj, 0:MS].rearrange("p (t q) -> p q t", t=4)
                self.V(lambda e, dst=dst, srcq=srcq: e.tensor_copy(dst, srcq), r=["qT"], w=["qbd"])
        pSA = []
        for q4 in range(4):
            ps, pk = self.pf()
            pSA.append((ps, pk))
            for qq in range(4):
                q = q4 * 4 + qq
                fw.mm(ps[0:32, qq * 128:(qq + 1) * 128], qbd[:, q, :], KAT[:, q, :], True, True, r=["qbd", "KAT"], w=[pk])
        psB, pkB = self.pf()
        for q in range(NS):
            fw.mm(psB[0:32, q * 4:(q + 1) * 4], qbd[:, q, :], KBT[:, q, :], True, True, r=["qbd", "KBT"], w=[pkB])
        for q4, (ps, pk) in enumerate(pSA):
            self.V(lambda e, q4=q4, ps=ps: e.scalar_tensor_tensor(
                ssc[:, q4 * 4:(q4 + 1) * 4, 0:128], ps[0:32, :].rearrange("p (q c) -> p q c", q=4), 0.125,
                smask[:, 0:128].unsqueeze(1).to_broadcast([32, 4, 128]), ALU.mult, ALU.add), r=[pk, "smask"], w=["ssc"])
        self.V(lambda e: e.scalar_tensor_tensor(
            ssc[:, :, 128:132], psB[0:32, 0:64].rearrange("p (q c) -> p q c", q=NS), 0.125,
            smask[:, 128:132].unsqueeze(1).to_broadcast([32, NS, 4]), ALU.mult, ALU.add), r=[pkB, "smask"], w=["ssc"])
        sinkc = sbl("sinkc", [32, 1])
        for g in range(2):
            for jj in range(4):
                p0 = g * 16 + jj * 4
                fw.dma(sinkc[p0:p0 + 4, :], I["attn_sinks"][l, g * 4 + jj:g * 4 + jj + 1].partition_broadcast(4), w=["sinkc"], key="sinkc")
        self.V(lambda e: e.tensor_reduce(sst[:, 0:NS], ssc[:, :, :], AX.X, ALU.max), r=["ssc"], w=["sst"])
        self.V(lambda e: e.tensor_scalar(sst[:, 0:NS], sst[:, 0:NS], sinkc[:, 0:1], None, ALU.max), r=["sst", "sinkc"], w=["sst"])
        self.V(lambda e: e.tensor_tensor(ssc[:, :, :], ssc[:, :, :], bc3(sst[:, 0:NS], 132), ALU.subtract), r=["ssc", "sst"], w=["ssc"])
        fw.act(ssc[:, :, :], ssc[:, :, :], AF.Exp, r=["ssc"], w=["ssc"])
        self.V(lambda e: e.tensor_reduce(sst[:, NS:2 * NS], ssc[:, :, :], AX.X, ALU.add), r=["ssc"], w=["sst2"])
        self.V(lambda e: e.tensor_scalar(sst[:, 2 * NS:3 * NS], sst[:, 0:NS], sinkc[:, 0:1], None, ALU.subtract), r=["sst", "sinkc"], w=["sst3"])
        fw.act(sst[:, 2 * NS:3 * NS], sst[:, 2 * NS:3 * NS], AF.Exp, r=["sst3"], w=["sst3"], scale=-1.0)
        self.V(lambda e: e.tensor_tensor(sst[:, NS:2 * NS], sst[:, NS:2 * NS], sst[:, 2 * NS:3 * NS], ALU.add), r=["sst2", "sst3"], w=["sst2"])
        self.V(lambda e: e.reciprocal(sst[:, NS:2 * NS], sst[:, NS:2 * NS]), r=["sst2"], w=["sst2"])
        self.V(lambda e: e.tensor_tensor(spb[:, :, :], ssc[:, :, :], bc3(sst[:, NS:2 * NS], 132), ALU.mult), r=["ssc", "sst2"], w=["spb"])
        identb32 = identb[0:32, 0:32]
        for q8 in range(2):
            pbk, pk = self.pb()
            for qq in range(8):
                q = q8 * 8 + qq
                fw.tr(pbk[:, qq * 32:(qq + 1) * 32], spb[:, q, 0:128], identb32, r=["spb", "identb"], w=[pk])
            fw.act(spT[:, q8 * 8:(q8 + 1) * 8, :], pbk[:, 0:256].rearrange("p (q c) -> p q c", q=8), AF.Copy, r=[pk], w=["spT"])
        pbk, pk = self.pb()
        for q in range(NS):
            fw.tr(pbk[0:4, q * 32:(q + 1) * 32], spb[:, q, 128:132], identb32, r=["spb", "identb"], w=[pk])
        fw.act(spTB[:, :, :], pbk[0:4, 0:512].rearrange("p (q c) -> p q c", q=NS), AF.Copy, r=[pk], w=["spTB"])
        pO, pok = self.pf()
        for q in range(NS):
            fw.mm(pO[:, q * 32:(q + 1) * 32], VAb[:, q, :], spT[:, q, :], True, False, r=["VAb", "spT"], w=[pok])
            fw.mm(pO[:, q * 32:(q + 1) * 32], VBb[0:4, q, :], spTB[0:4, q, :], False, True, r=["VBb", "spTB"], w=[pok])
        oraw = sbl("oraw", [128, 32, NS], BF)
        fw.act(oraw.rearrange("p c q -> p q c"), pO[:, :].rearrange("p (q c) -> p q c", q=NS), AF.Copy, r=[pok], w=["oraw"])
        for g in range(2):
            for jj in range(4):
                cc, par = g * 2 + jj // 2, jj % 2
                c0 = g * 16 + jj * 4
                srco = oraw[g * 64:(g + 1) * 64, c0:c0 + 4, :].rearrange("p t q -> p (t q)")
                fw.dma(oTs[par * 64:(par + 1) * 64, cc, :], srco, r=["oraw"], w=["oTs"], key="oTs")
        xo_, xok = xo[i % 2], "xo%d" % (i % 2)
        gate_out(MS, hcur, "hT1", oTs, "oTs", mr, mrk, xt, xk, xo_, xok)
        fw.dma(self.xsbuf, xo_[0:MS, :], r=[xok], w=[("xb", NT)], key=xok)

    def pass_ffn(self, l, es2):
        fw, I, O, NT = self.fw, self.I, self.O, self.NT
        sbl = lambda n, s, dt=F32: self.sbl(es2, "f%d_" % l + n, s, dt)
        identb, identf = self.identb, self.identf
        Wc = sbl("Wc", [128, 8, DFF], BF)
        Wu = sbl("Wu", [128, 8, DFF], BF)
        Wd = sbl("Wd", [128, NFC, D], BF)
        self.col_load(self.gcol[:], "gcol", I["norm_ffn_g"][l], 8)
        cw = sbl("cw", [128, 4, NFC])
        for j in range(3):
            self.col_load(cw[:, j, :], "cw", I["ffn_conv_w"][l, j], NFC)
        self.col_load(cw[:, 3, :], "cw", I["ffn_conv_b"][l], NFC)
        m0 = self.aoff
        self.wstage = [sbl("wst%d" % i_, [128, 2048]) for i_ in range(4)]
        wi = I["ffn_w_in"][l]
        gsc = lambda c: self.gcol[:, c:c + 1]
        self.prep_w(8, DFF, lambda c, s0, n: wi[c * 128:(c + 1) * 128, s0:s0 + n],
                    lambda c, s0, n: Wc[:, c, s0:s0 + n], lambda c: "Wc_%d" % c, "col", gsc)
        self.prep_w(8, DFF, lambda c, s0, n: wi[c * 128:(c + 1) * 128, DFF + s0:DFF + s0 + n],
                    lambda c, s0, n: Wu[:, c, s0:s0 + n], lambda c: "Wu_%d" % c, "col", gsc)
        wd = I["ffn_w_down"][l]
        self.prep_w(NFC, D, lambda c, s0, n: wd[c * 128:(c + 1) * 128, s0:s0 + n],
                    lambda c, s0, n: Wd[:, c, s0:s0 + n], lambda c: "Wd_%d" % c, "plain")
        self.release(m0)
        last = (l == 1)
        if last:
            gf = sbl("gf", [128, D])
            self.bcast_load(gf[:], "gf", I["norm_final_g"])
        hTd = [sbl("hT%d" % i_, [128, 8, 128], BF) for i_ in range(2)]
        hT = hTd[0]
        cxf = sbl("cx", [128, NFC * 130])
        cx1 = cxf.rearrange("p (f t) -> p f t", f=NFC)
        cxs = cxf[:, 0:NFC * NS * 6].rearrange("p (f q j) -> p f q j", f=NFC, q=NS)
        acc = [sbl("acc%d" % i_, [128, 4, 128]) for i_ in range(2)]
        aTd = [sbl("aT%d" % i_, [128, NFC, 128], BF) for i_ in range(2)]
        xo = [sbl("xo%d" % i_, [128, D]) for i_ in range(2)]
        ctok = sbl("ctok", [128, DFF])
        cst = ctok
        jk = self.xn

        def finish(M, xt, xk, xo_, xok, dst_final, dst_x, dkey, aT, aTk):
            for grp in range(2):
                px, pxk = self.pf()
                for fc in range(NFC):
                    fw.mm(px[0:M, :], aT[:, fc, 0:M], Wd[:, fc, grp * 512:(grp + 1) * 512], fc == 0, fc == NFC - 1, r=[aTk, "Wd_%d" % fc], w=[pxk])
                self.V(lambda e, grp=grp, px=px: e.tensor_tensor(xo_[0:M, grp * 512:(grp + 1) * 512], xt[0:M, grp * 512:(grp + 1) * 512], px[0:M, :], ALU.add),
                       r=[xk, pxk], w=[xok])
            if not last:
                fw.dma(dst_x, xo_[0:M, :], r=[xok], w=[dkey], key=xok)
                return
            ss, t1 = self.ss, self.t1
            fw.act(jk[0:M, :], xo_[0:M, :], AF.Square, r=[xok], w=["xn", "ss"], accum_out=ss[0:M, :])
            self.V(lambda e: e.tensor_scalar(t1[0:M, :], ss[0:M, :], 1.0 / D, 1e-6, ALU.mult, ALU.add), r=["ss"], w=["t1"])
            fw.act(t1[0:M, :], t1[0:M, :], AF.Sqrt, r=["t1"], w=["t1"])
            self.V(lambda e: e.reciprocal(t1[0:M, :], t1[0:M, :]), r=["t1"], w=["t1"])
            self.V(lambda e: e.scalar_tensor_tensor(xo_[0:M, :], xo_[0:M, :], t1[0:M, 0:1], gf[0:M, :], ALU.mult, ALU.mult),
                   r=[xok, "t1", "gf"], w=[xok])
            fw.dma(dst_final, xo_[0:M, :], r=[xok], key=xok)

        def ffn_core(M, hcur, hk, cview, ckey, sample, aT, aTk, mid=None, groups=None):
            def partA(b0):
                nb = min(4, NFC - b0)
                pc, pck = self.pf()
                for q in range(nb):
                    fc = b0 + q
                    for c in range(8):
                        fw.mm(pc[:, q * M:(q + 1) * M], Wc[:, c, fc * 128:(fc + 1) * 128], hcur(c), c == 0, c == 7, r=[hk, "Wc_%d" % c], w=[pck])
                pu, puk = self.pf()
                for q in range(nb):
                    fc = b0 + q
                    for c in range(8):
                        fw.mm(pu[:, q * M:(q + 1) * M], Wu[:, c, fc * 128:(fc + 1) * 128], hcur(c), c == 0, c == 7, r=[hk, "Wu_%d" % c], w=[puk])
                if sample:
                    fw.act(cview[:, b0:b0 + nb, :, 2:6], pc[:, 0:nb * M].rearrange("p (f t q) -> p f q t", f=nb, t=4), AF.Copy, r=[pck], w=[ckey])
                else:
                    fw.act(cview[:, b0:b0 + nb, 2:130], pc[:, 0:nb * M].rearrange("p (f t) -> p f t", f=nb), AF.Copy, r=[pck], w=[ckey])
                a_ = acc[(b0 // 4) % 2]
                ak = "acc%d" % ((b0 // 4) % 2)
                views = []
                for q in range(nb):
                    fc = b0 + q
                    if sample:
                        c0, c1, c2 = (cview[:, fc, :, s_:s_ + 4] for s_ in range(3))
                        av = a_[:, q, 0:M].rearrange("p (t q) -> p q t", t=4)
                    else:
                        c0, c1, c2 = (cview[:, fc, s_:s_ + 128] for s_ in range(3))
                        av = a_[:, q, :]
                    views.append((fc, av, c0, c1, c2))
                akq = [ak + "_%d" % q for q in range(nb)]
                for q, (fc, av, c0, c1, c2) in enumerate(views):
                    self.P(lambda e, av=av, c0=c0, fc=fc: e.tensor_scalar(av, c0, cw[:, 0, fc:fc + 1], cw[:, 3, fc:fc + 1], ALU.mult, ALU.add),
                           r=[ckey, "cw", ak], w=([akq[q], ak] if q == 0 else [akq[q]]))
                return (b0, nb, pu, puk, a_, ak, akq, views)

            def partB(st):
                b0, nb, pu, puk, a_, ak, akq, views = st
                for q, (fc, av, c0, c1, c2) in enumerate(views):
                    self.V(lambda e, av=av, c1=c1, fc=fc: e.scalar_tensor_tensor(av, c1, cw[:, 1, fc:fc + 1], av, ALU.mult, ALU.add),
                           r=[ckey, "cw", akq[q]], w=[akq[q]])
                for q, (fc, av, c0, c1, c2) in enumerate(views):
                    self.V(lambda e, av=av, c2=c2, fc=fc: e.scalar_tensor_tensor(av, c2, cw[:, 2, fc:fc + 1], av, ALU.mult, ALU.add),
                           r=[ckey, "cw", akq[q]], w=[akq[q]])
                fw.act(a_[:, 0:nb, 0:M], a_[:, 0:nb, 0:M], AF.Gelu, r=akq, w=[ak])
                self.V(lambda e, a_=a_, pu=pu, nb=nb, b0=b0, aT=aT: e.tensor_tensor(aT[:, b0:b0 + nb, 0:M], a_[:, 0:nb, 0:M],
                                                                              pu[:, 0:nb * M].rearrange("p (f t) -> p f t", f=nb), ALU.mult),
                       r=[ak, puk], w=[aTk])

            prev = None
            for b0 in (groups if groups is not None else range(0, NFC, 4)):
                cur = partA(b0)
                if prev is not None:
                    partB(prev)
                prev = cur
                if mid is not None and b0 == 8:
                    mid()
            partB(prev)

        def c_token_major(M, hcur, hk, rows, dsts):
            for g0 in range(0, DFF, 512):
                n = min(512, DFF - g0)
                ps, pk = self.pf()
                for c in range(8):
                    fw.mm(ps[0:M, 0:n], hcur(c), Wc[:, c, g0:g0 + n], c == 0, c == 7, r=[hk, "Wc_%d" % c], w=[pk])
                fw.act(ctok[0:M, g0:g0 + n], ps[0:M, 0:n], AF.Copy, r=[pk], w=["ctok"])
            for (r0, r1), dst in zip(rows, dsts):
                fw.dma(dst, ctok[r0:r1, :], r=["ctok"], key="ctok")

        xt, xk = self.xt[1], "xt1"
        fw.dma(xt[:], (I["xh0"] if NSEG == 1 else self.xh_dram), r=["xh_dram"], w=[xk], key=xk)
        self.norm_hT(xt, xk, 128, hT[:, :, :], "hT0", identb)
        pc, pck = self.pf()
        for fc in range(NFC):
            for c in range(8):
                fw.mm(pc[:, fc * 2:(fc + 1) * 2], Wc[:, c, fc * 128:(fc + 1) * 128], hT[:, c, 126:128], c == 0, c == 7, r=["hT0", "Wc_%d" % c], w=[pck])
        fw.act(cx1[:, :, 0:2], pc[:, 0:2 * NFC].rearrange("p (f t) -> p f t", f=NFC), AF.Copy, r=[pck], w=["cx"])
        def pre(i):
            xt, xk = self.xt[i % 2], "xt%d" % (i % 2)
            fw.dma(xt[:], self.xbuf[i * 128:(i + 1) * 128, :], r=[("xb", i)], w=[xk], key=xk)
            self.norm_hT(xt, xk, 128, hTd[i % 2][:, :, :], "hT%d" % (i % 2), identb)

        def head(i):
            if i > 0:
                self.P(lambda e: e.tensor_copy(acc[0][:, 0, 0:2 * NFC].rearrange("p (f t) -> p f t", f=NFC), cx1[:, :, 128:130]), r=["cx"], w=["acc0"])
                self.P(lambda e: e.tensor_copy(cx1[:, :, 0:2], acc[0][:, 0, 0:2 * NFC].rearrange("p (f t) -> p f t", f=NFC)), r=["acc0"], w=["cx"])
            ffn_core(128, lambda c, i=i: hTd[i % 2][:, c, :], "hT%d" % (i % 2), cx1, "cx", False, aTd[i % 2], "aT%d" % (i % 2), groups=[0])

        pre(0)
        head(0)
        for i in range(NT):
            xt, xk = self.xt[i % 2], "xt%d" % (i % 2)
            hcur = lambda c, i=i: hTd[i % 2][:, c, :]
            hkk = "hT%d" % (i % 2)
            mid = (lambda i=i: pre(i + 1)) if i + 1 < NT else None
            ffn_core(128, hcur, hkk, cx1, "cx", False, aTd[i % 2], "aT%d" % (i % 2), mid, groups=list(range(4, NFC, 4)))
            if i == NT - 1:
                c_token_major(128, hcur, hkk, [(126, 128)], [O["p_conv"][l]])
            if i + 1 < NT:
                head(i + 1)
            xo_, xok = xo[i % 2], "xo%d" % (i % 2)
            finish(128, xt, xk, xo_, xok, O["yp"][i * 128:(i + 1) * 128, :], self.xbuf[i * 128:(i + 1) * 128, :], ("xb", i),
                   aTd[i % 2], "aT%d" % (i % 2))
        if not last and NSEG > 1:
            self.gather_select(xo_[:, :], [xok], D, self.agX_in, self.agX_out, "agX")
            fw.dma(self.xh_dram, xo_[:, :], r=[xok], w=["xh_dram"], key="xhst")

        i = NT
        xt, xk = self.xt[i % 2], "xt%d" % (i % 2)
        fw.dma(xt[0:MS, :], self.xsbuf, r=[("xb", i)], w=[xk], key=xk)
        self.norm_hT(xt, xk, MS, hT[:, :, 0:MS], "hT0", identb)
        hcur = lambda c: hT[:, c, 0:MS]
        fw.dma(cst[0:32, :], I["st_conv"][l], w=["ctok"], key="cst")
        for b0 in range(0, NFC, 4):
            nb = min(4, NFC - b0)
            ps, pk = self.pf()
            for q in range(nb):
                fc = b0 + q
                fw.tr(ps[:, q * 32:(q + 1) * 32], cst[0:32, fc * 128:(fc + 1) * 128], identf[0:32, 0:32], r=["ctok", "identf"], w=[pk])
            fw.act(cxs[:, b0:b0 + nb, :, 0:2], ps[:, 0:nb * 32].rearrange("p (f q j) -> p f q j", f=nb, j=2), AF.Copy, r=[pk], w=["cx"])
        ffn_core(MS, hcur, "hT0", cxs, "cx", True, aTd[0], "aT0")
        sc_ = O["s_conv"][l].rearrange("(q j) f -> j q f", j=2)
        c_token_major(MS, hcur, "hT0", [(32, 48), (48, 64)], [sc_[0], sc_[1]])
        xo_, xok = xo[i % 2], "xo%d" % (i % 2)
        finish(MS, xt, xk, xo_, xok, O["ys"], self.xsbuf, ("xb", NT), aTd[0], "aT0")


NSEG = 1


def _consts_shared():
    c = {}
    c["c_ident"] = np.eye(128, dtype=np.float32)
    inv = (10000.0 ** (-np.arange(0, HD, 2, dtype=np.float32) / HD)).astype(np.float32)
    pos_s = (PAST + np.repeat(np.arange(4), NS)).astype(np.float32)
    ang_s = pos_s[:, None] * inv[None, :]
    c["c_coss"] = np.cos(ang_s).astype(np.float32)
    c["c_sins"] = np.sin(ang_s).astype(np.float32)
    s = np.arange(128)[:, None]
    t = np.arange(128)[None, :]
    incl = (s <= t).astype(np.float32)
    strict = (s < t).astype(np.float32)
    c["c_tri"] = np.concatenate([incl * CDEC, strict * CDEC], 1).astype(np.float32)
    c["c_mask2"] = np.concatenate([incl, strict], 1).astype(np.float32)
    c["c_maskL"] = (s > t).astype(np.float32)
    i_ = np.arange(128)[:, None]
    j_ = np.arange(128)[None, :]
    cur = np.where(j_ <= i_, 0.0, NEG)
    prev = np.where(j_ > i_, 0.0, NEG)
    dead = np.full((128, 128), NEG)
    c["c_amask"] = np.concatenate([cur, prev, prev, cur, cur, dead], 1).astype(np.float32)
    c["_am_first"] = np.concatenate([cur, dead], 1).astype(np.float32)
    c["_am_mid"] = np.concatenate([cur, prev], 1).astype(np.float32)
    tt = (np.arange(32) % 4)[:, None]
    ia = np.arange(128)[None, :]
    ma = np.where(ia <= 124 + tt, 0.0, NEG)
    rb = np.arange(4)[None, :]
    mb = np.where(rb > tt, 0.0, NEG)
    c["c_smask"] = np.concatenate([ma, mb], 1).astype(np.float32)
    last = np.zeros((128, 1), np.float32)
    last[127, 0] = 1.0
    c["c_last"] = last
    c["_inv"] = inv
    return c


def _rope_tab(pos, inv):
    ang = pos.astype(np.float32)[:, None] * inv[None, :]
    return np.cos(ang).astype(np.float32), np.sin(ang).astype(np.float32)


_CACHE = {}
TAPS = False
TAP_OUT = {}


def kernel(**inp):
    inp = {k: np.asarray(v) for k, v in inp.items()}
    xp_all = inp["x_prompt"].astype(np.float32)
    B, SEQ_, _ = xp_all.shape
    TPC = SEQ_ // NSEG
    if TPC not in _CACHE:
        b_ = Builder(TPC, taps=TAPS)
        _CACHE[TPC] = (b_.build(), b_.tapnames)
    nc, tapnames = _CACHE[TPC]
    consts = _consts_shared()
    inv = consts.pop("_inv")
    am_first, am_mid = consts.pop("_am_first"), consts.pop("_am_mid")
    wnames = ["norm_mix_g", "w_in", "rwkv_mu", "rwkv_w0", "rwkv_w2", "rwkv_a0", "rwkv_a2", "rwkv_g2", "rwkv_k_k",
              "rwkv_k_a", "rwkv_ln_g", "rwkv_ln_b", "attn_sinks", "w_br_rwkv", "w_br_attn", "w_out", "norm_ffn_g",
              "ffn_w_in", "ffn_conv_w", "ffn_conv_b", "ffn_w_down", "norm_final_g"]
    shared = {n: np.ascontiguousarray(inp[n], dtype=np.float32) for n in wnames}
    shared["rwkv_r_k"] = np.ascontiguousarray(inp["rwkv_r_k"], dtype=np.float32).reshape(2, RD)
    shared.update(consts)
    in_maps = []
    ncores = 8
    for c in range(ncores):
        b, seg = (c // NSEG) % B, c % NSEG
        sl = slice(c * NS, (c + 1) * NS)
        m = dict(shared)
        t0 = seg * TPC
        m["xp"] = np.ascontiguousarray(xp_all[b, t0:t0 + TPC])
        m["xh0"] = np.ascontiguousarray(xp_all[b, t0 - 128:t0]) if seg > 0 else np.zeros((128, D), np.float32)
        m["c_cosp"], m["c_sinp"] = _rope_tab(t0 + np.arange(TPC), inv)
        m["c_cosh"], m["c_sinh"] = _rope_tab(np.maximum(t0 - 128 + np.arange(128), 0), inv)
        m["c_amask0"] = am_mid if seg > 0 else am_first
        sel = np.zeros((128, 8), np.float32)
        if seg > 0:
            sel[:, c - 1] = 1.0
        m["c_sel"] = sel
        m["xs"] = np.ascontiguousarray(inp["x_sample"][sl].transpose(1, 0, 2).reshape(MS, D))
        m["st_shift"] = np.ascontiguousarray(inp["state_rwkv_shift"][:, sl])
        m["st_wkv"] = np.ascontiguousarray(inp["state_rwkv_wkv"][:, sl]).reshape(2, 128, 4096)
        m["ck"] = np.ascontiguousarray(inp["cache_swa_k"][:, sl]).reshape(2, NS, 128, 128)
        m["cv"] = np.ascontiguousarray(inp["cache_swa_v"][:, sl]).reshape(2, NS, 128, 128)
        m["st_conv"] = np.ascontiguousarray(inp["state_ffn_conv"][:, sl]).reshape(2, 2 * NS, DFF)
        in_maps.append(m)
    res = run_bass_kernel_spmd(nc, in_maps, core_ids=list(range(ncores)))
    R = res.results
    for tn in tapnames:
        TAP_OUT[tn] = [np.asarray(R[c][tn]) for c in range(ncores)]
    f = np.float32
    lastc = [b * NSEG + NSEG - 1 for b in range(B)]
    y_prompt = np.stack([np.concatenate([R[b * NSEG + sg]["yp"] for sg in range(NSEG)], 0) for b in range(B)]).astype(f)
    y_sample = np.concatenate([R[c]["ys"].reshape(4, NS, D).transpose(1, 0, 2) for c in range(ncores)], 0).astype(f)
    p_shift = np.stack([R[c]["p_shift"] for c in lastc], 1).astype(f)
    p_wkv = np.stack([R[c]["p_wkv"] for c in lastc], 1).astype(f)
    p_k = np.stack([R[c]["p_k"] for c in lastc], 1).reshape(2, B, 128, 2, 64).astype(f)
    p_v = np.stack([R[c]["p_v"] for c in lastc], 1).reshape(2, B, 128, 2, 64).astype(f)
    p_conv = np.stack([R[c]["p_conv"] for c in lastc], 1).astype(f)
    s_shift = np.concatenate([R[c]["s_shift"] for c in range(ncores)], 1).astype(f)
    s_wkv = np.concatenate([R[c]["s_wkv"].reshape(2, NS, NH, 64, 64) for c in range(ncores)], 1).astype(f)
    s_k = np.concatenate([R[c]["s_k"].reshape(2, NS, 128, 2, 64) for c in range(ncores)], 1).astype(f)
    s_v = np.concatenate([R[c]["s_v"].reshape(2, NS, 128, 2, 64) for c in range(ncores)], 1).astype(f)
    s_conv = np.concatenate([R[c]["s_conv"].reshape(2, NS, 2, DFF) for c in range(ncores)], 1).astype(f)
    return (y_prompt, y_sample, p_shift, p_wkv, p_k, p_v, p_conv, s_shift, s_wkv, s_k, s_v, s_conv)
```

```python
import math
from contextlib import ExitStack

import numpy as np
import concourse.bass as bass
import concourse.mybir as mybir
from concourse.bass_utils import run_bass_kernel_spmd

F32 = mybir.dt.float32
BF = mybir.dt.bfloat16
AF = mybir.ActivationFunctionType
ALU = mybir.AluOpType
AX = mybir.AxisListType

ENGS = ["sp", "pe", "act", "dve", "pool"]
DEBUG_WHERE = True

D = 1024
HD = 64
NH = 8
RD = 512
RP = 1792
INP = 4608
DFF = 2816
NFC = 22
NS = 16
MS = 64
PAST = 16384
CDEC = -math.exp(-0.5)
NEG = -30000.0


class FW:
    def __init__(self, nc, es):
        self.nc = nc
        self.es = es
        self.ops = {e: [] for e in ENGS}
        self.lastw = {}
        self.readers = {}
        self.dma_count = {}
        self.inc = {}

    def sb(self, name, shape, dt=F32):
        return self.es.enter_context(self.nc.sbuf_tensor(name, list(shape), dt))

    def ps(self, name, shape, dt=F32):
        return self.es.enter_context(self.nc.psum_tensor(name, list(shape), dt))

    def capture(self, f):
        self.cap = []
        f()
        log, self.cap = self.cap, None
        return log

    def replay(self, logs, chunk=2):
        logs = [list(lg) for lg in logs if lg]
        if not logs:
            return
        mn = min(len(lg) for lg in logs)
        per = [max(1, int(round(chunk * len(lg) / mn))) for lg in logs]
        pos = [0] * len(logs)
        while any(p < len(lg) for p, lg in zip(pos, logs)):
            for k, lg in enumerate(logs):
                for _ in range(per[k]):
                    if pos[k] < len(lg):
                        self.op(*lg[pos[k]])
                        pos[k] += 1

    def op(self, eng, fn, r=(), w=(), dma=None):
        if getattr(self, "cap", None) is not None:
            self.cap.append((eng, fn, tuple(r), tuple(w), dma))
            return
        ops = self.ops[eng]
        idx = len(ops)
        deps = set()
        pr = [k for k in r if isinstance(k, str) and k[:2] in ("ps", "pb") and k[2:].isdigit()]
        if pr:
            r = [k for k in r if k not in pr]
            w = list(w) + pr
        for k in r:
            t = self.lastw.get(k)
            if t is not None:
                deps.add(t)
        for k in w:
            t = self.lastw.get(k)
            if t is not None:
                deps.add(t)
            for t2 in self.readers.get(k, {}).values():
                deps.add(t2)
        if dma is not None:
            c = self.dma_count.get(dma, 0) + 1
            self.dma_count[dma] = c
            tok = ("d", dma, c)
        else:
            tok = ("c", eng, idx)
        if eng == "pe":
            deps = {d for d in deps if not (d[0] == "c" and d[1] == "pe")}
        deps.discard(tok)
        rec = dict(fn=fn, deps=deps, tok=tok, signal=False)
        if DEBUG_WHERE:
            import sys as _s
            f_ = _s._getframe(1)
            wh = []
            while f_ is not None and len(wh) < 4:
                wh.append(f_.f_lineno)
                f_ = f_.f_back
            rec["where"] = wh
        ops.append(rec)
        for d in deps:
            if d[0] == "c":
                self.ops[d[1]][d[2]]["signal"] = True
        for k in w:
            self.lastw[k] = tok
            self.readers[k] = {}
        for k in r:
            rk = ("d", tok[1]) if tok[0] == "d" else tok[1]
            self.readers.setdefault(k, {})[rk] = tok
        return tok

    def fence(self):
        toks = set()
        for e in ENGS:
            for rec in reversed(self.ops[e]):
                if rec["tok"][0] == "c" and rec["fn"] is not None:
                    toks.add(rec["tok"])
                    rec["signal"] = True
                    break
        for k, c in self.dma_count.items():
            toks.add(("d", k, c))
        for e in ENGS:
            self.ops[e].append(dict(fn=None, deps=set(toks), tok=("c", e, len(self.ops[e])), signal=False))

    def dma(self, out, in_, r=(), w=(), key=None, eng="sp", **kw):
        self.op(eng, lambda e: e.dma_start(out=out, in_=in_, **kw), r=r, w=w, dma=key)

    def mm(self, out, lhsT, rhs, start, stop, r=(), w=()):
        self.op("pe", lambda e: e.matmul(out, lhsT, rhs, start=start, stop=stop), r=r, w=w)

    def tr(self, out, in_, ident, r=(), w=()):
        self.op("pe", lambda e: e.transpose(out, in_, ident), r=r, w=w)

    def act(self, out, in_, func, r=(), w=(), **kw):
        self.op("act", lambda e: e.activation(out, in_, func, **kw), r=r, w=w)

    def emit(self):
        nc = self.nc
        sems = {e: self.es.enter_context(nc.semaphore("s_" + e)) for e in ENGS}
        dsems = {}
        for i, k in enumerate(self.dma_count):
            dsems[k] = self.es.enter_context(nc.semaphore("d%d" % i))
        for e in ENGS:
            c = 0
            for rec in self.ops[e]:
                if rec["signal"] and rec["tok"][0] == "c":
                    c += 1
                rec["sigval"] = c
        final_counts = dict(self.dma_count)

        def run(engname, eng):
            waited = {}
            for rec in self.ops[engname]:
                need = {}
                for d in rec["deps"]:
                    if d[0] == "c":
                        s = ("c", d[1])
                        v = self.ops[d[1]][d[2]]["sigval"]
                    else:
                        s = ("d", d[1])
                        v = self.inc.get(d[1], 16) * d[2]
                    if need.get(s, 0) < v:
                        need[s] = v
                for s, v in need.items():
                    if waited.get(s, 0) >= v:
                        continue
                    waited[s] = v
                    eng.wait_ge(sems[s[1]] if s[0] == "c" else dsems[s[1]], v)
                if rec["fn"] is None:
                    continue
                try:
                    ins = rec["fn"](eng)
                except Exception:
                    print("EMIT FAILURE at lines", rec.get("where"), "engine", engname)
                    raise
                if rec["tok"][0] == "d":
                    ins.then_inc(dsems[rec["tok"][1]], self.inc.get(rec["tok"][1], 16))
                elif rec["signal"]:
                    ins.then_inc(sems[engname], 1)
            if engname == "sp":
                for k, c in final_counts.items():
                    v = self.inc.get(k, 16) * c
                    if waited.get(("d", k), 0) < v:
                        eng.wait_ge(dsems[k], v)

        with nc.Block() as block:
            @block.sync
            def _(e):
                run("sp", e)

            @block.tensor
            def _(e):
                run("pe", e)

            @block.scalar
            def _(e):
                run("act", e)

            @block.vector
            def _(e):
                run("dve", e)

            @block.gpsimd
            def _(e):
                run("pool", e)


def bc3(ap2, n):
    s = list(ap2.shape)
    return ap2.unsqueeze(2).to_broadcast([s[0], s[1], n])


def h3(ap2, h=NH):
    return ap2.rearrange("p (h d) -> p h d", h=h)


class Builder:
    def __init__(self, TP, taps=False):
        self.TP = TP
        self.NT = TP // 128
        self.taps = taps
        self.nc = bass.Bass("TRN2", target_bir_lowering=False)
        self.I = {}
        self.O = {}
        self.psi = 0
        self.pbi = 0
        self.tapnames = []
        self.pool = None
        self.pcnt = {}

    def din(self, n, s):
        self.I[n] = self.nc.dram_tensor(n, list(s), F32, kind="ExternalInput").ap()

    def dout(self, n, s):
        self.O[n] = self.nc.dram_tensor(n, list(s), F32, kind="ExternalOutput").ap()

    def declare(self):
        TP = self.TP
        for n, s in [("xp", (TP, D)), ("xs", (MS, D)), ("st_shift", (2, NS, RP)), ("st_wkv", (2, 128, 4096)),
                     ("ck", (2, NS, 128, 128)), ("cv", (2, NS, 128, 128)), ("st_conv", (2, 2 * NS, DFF)),
                     ("norm_mix_g", (2, D)), ("w_in", (2, D, INP)), ("rwkv_mu", (2, RP)), ("rwkv_w0", (2, RD)),
                     ("rwkv_w2", (2, 64, RD)), ("rwkv_a0", (2, RD)), ("rwkv_a2", (2, 64, RD)),
                     ("rwkv_g2", (2, 128, RD)), ("rwkv_k_k", (2, RD)), ("rwkv_k_a", (2, RD)),
                     ("rwkv_r_k", (2, RD)), ("rwkv_ln_g", (2, RD)), ("rwkv_ln_b", (2, RD)),
                     ("attn_sinks", (2, NH)), ("w_br_rwkv", (2, RD, D)), ("w_br_attn", (2, RD, D)),
                     ("w_out", (2, D, D)), ("norm_ffn_g", (2, D)), ("ffn_w_in", (2, D, 2 * DFF)),
                     ("ffn_conv_w", (2, 3, DFF)), ("ffn_conv_b", (2, DFF)), ("ffn_w_down", (2, DFF, D)),
                     ("norm_final_g", (D,)),
                     ("c_ident", (128, 128)), ("c_cosp", (TP, 32)), ("c_sinp", (TP, 32)),
                     ("c_coss", (MS, 32)), ("c_sins", (MS, 32)), ("c_tri", (128, 256)),
                     ("c_mask2", (128, 256)), ("c_maskL", (128, 128)), ("c_amask", (128, 768)),
                     ("c_smask", (32, 132)), ("c_last", (128, 1)),
                     ("xh0", (128, D)), ("c_cosh", (128, 32)), ("c_sinh", (128, 32)), ("c_amask0", (128, 256)), ("c_sel", (128, 8))]:
            self.din(n, s)
        for n, s in [("yp", (TP, D)), ("ys", (MS, D)), ("p_shift", (2, RP)), ("p_wkv", (2, NH, 64, 64)),
                     ("p_k", (2, 128, 128)), ("p_v", (2, 128, 128)), ("p_conv", (2, 2, DFF)),
                     ("s_shift", (2, NS, RP)), ("s_wkv", (2, 128, 4096)), ("s_k", (2, NS, 128, 128)),
                     ("s_v", (2, NS, 128, 128)), ("s_conv", (2, 2 * NS, DFF))]:
            self.dout(n, s)
        nc = self.nc
        self.xbuf = nc.dram_tensor("xbuf", [TP, D], F32).ap()
        self.xsbuf = nc.dram_tensor("xsbuf", [MS, D], F32).ap()
        self.mrbuf = nc.dram_tensor("mrbuf", [self.NT + 1, 128, 1024], BF).ap()
        self.xh_dram = nc.dram_tensor("xh_dram", [128, D], F32).ap()
        self.sq = nc.dram_tensor("sq", [6, MS, RD], F32).ap()
        self.sy = nc.dram_tensor("sy", [MS, RD], F32).ap()

    def alloc(self, name, shape, dt=F32):
        shape = list(shape)
        n = 1
        for d_ in shape[1:]:
            n *= d_
        nbytes = n * (4 if dt == F32 else 2)
        nw = (nbytes + 31) // 32 * 8
        off = self.aoff
        self.aoff += nw
        self.apeak = max(self.apeak, self.aoff)
        assert self.aoff <= self.ASZ, "SBUF arena overflow: %s needs %d words (limit %d)" % (name, self.aoff, self.ASZ)
        ap = self.arena[0:shape[0], off:off + nw]
        if dt != F32:
            ap = ap.bitcast(dt)
        ap = ap[:, 0:n]
        if len(shape) > 2:
            names = ["d%d" % i for i in range(len(shape) - 1)]
            pat = "p (%s) -> p %s" % (" ".join(names), " ".join(names))
            ap = ap.rearrange(pat, **{names[i]: shape[i + 1] for i in range(len(names))})
        return ap

    def release(self, mark):
        self.fw.fence()
        self.aoff = mark

    def pf(self):
        ids = {None: [0, 1, 2, 3, 4, 5], 0: [0, 1, 2], 1: [3, 4, 5]}[self.pool]
        c = self.pcnt.setdefault(("f", self.pool), 0)
        self.pcnt[("f", self.pool)] = c + 1
        k = ids[c % len(ids)]
        return self.PS[k], "ps%d" % k

    def pb(self):
        ids = {None: [0, 1], 0: [0], 1: [1]}[self.pool]
        c = self.pcnt.setdefault(("b", self.pool), 0)
        self.pcnt[("b", self.pool)] = c + 1
        k = ids[c % len(ids)]
        return self.PBK[k], "pb%d" % k

    def tap(self, name, ap, rkeys, dt=F32):
        if not self.taps:
            return
        shp = list(ap.shape)
        t = self.nc.dram_tensor("tap_" + name, shp, dt, kind="ExternalOutput").ap()
        self.tapnames.append("tap_" + name)
        self.fw.dma(t, ap, r=rkeys, key="tap_" + name)

    def V(self, fn, r=(), w=()):
        self.fw.op("dve", fn, r, w)

    def P(self, fn, r=(), w=()):
        self.fw.op("pool", fn, r, w)

    def col_load(self, dst, dkey, vec, n):
        fw = self.fw
        st = self.cstage
        fw.dma(st[0:n, :], vec.rearrange("(c p) -> c p", p=128), w=["cstage"], key="cstage")
        ps, pk = self.pf()
        fw.tr(ps[:, 0:n], st[0:n, :], self.identf[0:n, 0:n], r=["cstage", "identf"], w=[pk])
        fw.act(dst, ps[:, 0:n], AF.Copy, r=[pk], w=[dkey])

    def gather_select(self, src_ap, src_keys, n, ag_in, ag_out, name):
        fw = self.fw
        fw.dma(ag_in, src_ap, r=src_keys, w=[name + "_in"], key=name + "_st")
        self.gi = getattr(self, "gi", 0)
        ck = name + "_cc"
        fw.inc[ck] = 1
        fw.op("pool", lambda e: e.collective_compute("AllGather", ALU.bypass, replica_groups=[list(range(8))], ins=[ag_in], outs=[ag_out]),
              r=[name + "_in"], w=[name + "_out"], dma=ck)
        for r_ in range(8):
            st, sk = self.xt[r_ % 2], "xt%d" % (r_ % 2)
            fw.dma(st[:, 0:n], ag_out[r_ * 128:(r_ + 1) * 128, :], r=[name + "_out"], w=[sk], key=sk)
            if r_ == 0:
                self.V(lambda e, st=st: e.tensor_scalar(src_ap, st[:, 0:n], self.sel[:, 0:1], None, ALU.mult), r=[sk, "sel"], w=src_keys)
            else:
                self.V(lambda e, st=st, r_=r_: e.scalar_tensor_tensor(src_ap, st[:, 0:n], self.sel[:, r_:r_ + 1], src_ap, ALU.mult, ALU.add),
                       r=[sk, "sel"] + list(src_keys), w=src_keys)

    def bcast_load(self, dst, dkey, vec):
        self.fw.dma(dst, vec.partition_broadcast(dst.shape[0]), w=[dkey], key=dkey)

    def prep_w(self, nchunks, ncols, src, dst, dkey, mode, scale=None, mul=None, mulkey=None, sview=None):
        fw = self.fw
        for c in range(nchunks):
            for s0 in range(0, ncols, 2048):
                n = min(2048, ncols - s0)
                k = self.wst_i % 4
                self.wst_i += 1
                st = self.wstage[k]
                sk = "wst%d" % k
                fw.dma(st[:, 0:n], src(c, s0, n), w=[sk], key=sk)
                o = dst(c, s0, n)
                dk = dkey(c)
                if sview is not None:
                    sv_ = sview(st[:, 0:n])
                    sc = scale(c)
                    self.V(lambda eg, o=o, sv_=sv_, sc=sc: eg.tensor_scalar(o, sv_, sc, None, ALU.mult), r=[sk, "gcol"], w=[dk])
                    continue
                if mode == "plain":
                    e = ["dve", "pool", "act"][self.wst_i % 3]
                    if e == "act":
                        fw.act(o, st[:, 0:n], AF.Copy, r=[sk], w=[dk])
                    else:
                        fw.op(e, lambda eg, o=o, st=st, n=n: eg.tensor_copy(o, st[:, 0:n]), r=[sk], w=[dk])
                elif mode == "col":
                    sc = scale(c)
                    e = ["dve", "pool"][self.wst_i % 2]
                    fw.op(e, lambda eg, o=o, st=st, n=n, sc=sc: eg.tensor_scalar(o, st[:, 0:n], sc, None, ALU.mult),
                          r=[sk, "gcol"], w=[dk])
                else:
                    sc = scale(c)
                    m = mul(s0, n)
                    self.V(lambda eg, o=o, st=st, n=n, sc=sc, m=m: eg.scalar_tensor_tensor(
                        o, st[:, 0:n], sc, m, ALU.mult, ALU.mult), r=[sk, "gcol", mulkey], w=[dk])

    def norm_hT(self, xt, xk, M, hdst, hkey, identb):
        self.norm_a(xt, xk, M)
        self.norm_b(M, hdst, hkey, identb)

    def norm_a(self, xt, xk, M):
        fw = self.fw
        xn, ss, t1 = self.xn, self.ss, self.t1
        fw.act(xn[0:M, :], xt[0:M, :], AF.Square, r=[xk], w=["xn", "ss"], accum_out=ss[0:M, :])
        self.V(lambda e: e.tensor_scalar(t1[0:M, :], ss[0:M, :], 1.0 / D, 1e-6, ALU.mult, ALU.add), r=["ss"], w=["t1"])
        self.P(lambda e: e.tensor_tensor(t1[0:M, :], t1[0:M, :], self.mhalf[0:M, 0:1], ALU.pow), r=["t1", "mhalf"], w=["t1"])
        self.V(lambda e: e.tensor_scalar(xn[0:M, :], xt[0:M, :], t1[0:M, 0:1], None, ALU.mult), r=[xk, "t1"], w=["xn"])

    def norm_b(self, M, hdst, hkey, identb):
        fw = self.fw
        xn = self.xn
        pbk, pk = self.pb()
        for c in range(8):
            fw.tr(pbk[:, c * M:(c + 1) * M], xn[0:M, c * 128:(c + 1) * 128], identb[0:M, 0:M], r=["xn", "identb"], w=[pk])
        fw.act(hdst, pbk[:, 0:8 * M].rearrange("p (c t) -> p c t", c=8), AF.Copy, r=[pk], w=[hkey])

    def build(self):
        self.declare()
        nc = self.nc
        with ExitStack() as es:
            self.fw = fw = FW(nc, es)
            self.PS = [fw.ps("ps%d" % i, [128, 512], F32) for i in range(6)]
            self.PBK = [fw.ps("pb%d" % i, [128, 1024], BF) for i in range(2)]
            self.ASZ = 52224
            self.arena = fw.sb("arena", [128, self.ASZ])
            self.aoff = 0
            self.apeak = 0
            self.identf = self.alloc("identf", [128, 128])
            self.identb = self.alloc("identb", [128, 128], BF)
            self.cstage = self.alloc("cstage", [32, 128])
            self.wst_i = 0
            self.xn = self.alloc("xn", [128, D], BF)
            self.ss = self.alloc("ss", [128, 1])
            self.t1 = self.alloc("t1", [128, 1])
            self.gcol = self.alloc("gcol", [128, 8])
            self.xt = [self.alloc("xt%d" % i, [128, D]) for i in range(2)]
            self.mhalf = self.alloc("mhalf", [128, 8])
            self.V(lambda e: e.memset(self.mhalf[:], -0.5), w=["mhalf"])
            self.sel = self.alloc("sel", [128, 8])
            fw.dma(self.sel[:], self.I["c_sel"], w=["sel"], key="sel")
            fw.dma(self.identf[:], self.I["c_ident"], w=["identf"], key="identf")
            self.V(lambda e: e.tensor_copy(self.identb[:], self.identf[:]), r=["identf"], w=["identb"])
            for l in range(2):
                for p_ in (self.pass_rwkv, self.pass_attn, self.pass_ffn):
                    mk_ = self.aoff
                    p_(l, None)
                    self.release(mk_)
            print("arena peak words", self.apeak, "of", self.ASZ)
            fw.emit()
        return nc

    def sbl(self, es2, name, shape, dt=F32):
        return self.alloc(name, shape, dt)

    def xsrc(self, l, i):
        if i < self.NT:
            src = self.I["xp"] if l == 0 else self.xbuf
            return src[i * 128:(i + 1) * 128, :], ("xb", i)
        src = self.I["xs"] if l == 0 else self.xsbuf
        return src, ("xb", i)

    def pass_rwkv(self, l, es2):
        fw, I, O, NT = self.fw, self.I, self.O, self.NT
        sbl = lambda n, s, dt=F32: self.sbl(es2, "r%d_" % l + n, s, dt)
        identb, identf = self.identb, self.identf
        W1 = sbl("W1", [128, 8, RP], BF)
        W2 = sbl("W2", [128, 8, RP], BF)
        Wg = sbl("Wg", [128, 8, D], BF)
        Wr = sbl("Wr", [128, 4, D], BF)
        lw2 = sbl("lw2", [128, RD], BF)
        lg2 = sbl("lg2", [128, RD], BF)
        bcs = {}
        for n in ["rwkv_w0", "rwkv_a0", "rwkv_k_k", "rwkv_k_a", "rwkv_r_k", "rwkv_ln_g", "rwkv_ln_b"]:
            bcs[n] = sbl(n, [128, RD])
            self.bcast_load(bcs[n][:], n + "_bc", I[n][l])
        mucol = sbl("mucol", [128, 2])
        tri = sbl("tri", [128, 256])
        mask2 = sbl("mask2", [128, 256])
        maskL = sbl("maskL", [128, 128])
        clast = sbl("clast", [128, 1])
        fw.dma(tri[:], I["c_tri"], w=["tri"], key="tri")
        fw.dma(mask2[:], I["c_mask2"], w=["mask2"], key="mask2")
        fw.dma(maskL[:], I["c_maskL"], w=["maskL"], key="maskL")
        fw.dma(clast[:], I["c_last"], w=["clast"], key="clast")
        self.col_load(self.gcol[:], "gcol", I["norm_mix_g"][l], 8)
        self.col_load(mucol[:], "mucol", I["rwkv_mu"][l, 1536:1792], 2)
        m0 = self.aoff
        self.wstage = [sbl("wst%d" % i_, [128, 2048]) for i_ in range(4)]
        mu_bc = sbl("mu_bc", [128, RP])
        omm_bc = sbl("omm_bc", [128, RP])
        self.bcast_load(mu_bc[:], "mu_bc", I["rwkv_mu"][l])
        self.V(lambda e: e.tensor_scalar(omm_bc[:], mu_bc[:], -1.0, 1.0, ALU.mult, ALU.add), r=["mu_bc"], w=["omm_bc"])
        win = I["w_in"][l]
        gsc = lambda c: self.gcol[:, c:c + 1]
        self.prep_w(8, RP, lambda c, s0, n: win[c * 128:(c + 1) * 128, s0:s0 + n],
                    lambda c, s0, n: W1[:, c, s0:s0 + n], lambda c: "W1_%d" % c, "colmul", gsc,
                    lambda s0, n: omm_bc[:, s0:s0 + n], "omm_bc")
        self.prep_w(8, RP, lambda c, s0, n: win[c * 128:(c + 1) * 128, s0:s0 + n],
                    lambda c, s0, n: W2[:, c, s0:s0 + n], lambda c: "W2_%d" % c, "colmul", gsc,
                    lambda s0, n: mu_bc[:, s0:s0 + n], "mu_bc")
        self.prep_w(8, D, lambda c, s0, n: win[c * 128:(c + 1) * 128, 2560 + s0:2560 + s0 + n],
                    lambda c, s0, n: Wg[:, c, s0:s0 + n], lambda c: "Wg_%d" % c, "col", gsc)
        wbr = I["w_br_rwkv"][l]
        self.prep_w(4, D, lambda c, s0, n: wbr[c * 128:(c + 1) * 128, s0:s0 + n],
                    lambda c, s0, n: Wr[:, c, s0:s0 + n], lambda c: "Wr_%d" % c, "plain")
        for (nm, p0, dk_) in [("rwkv_w2", 0, "lw2a"), ("rwkv_a2", 64, "lw2b")]:
            k = self.wst_i % 4
            self.wst_i += 1
            wsk = self.wstage[k]
            fw.dma(wsk[p0:p0 + 64, 0:RD], I[nm][l], w=["wst%d" % k], key="wst%d" % k)
            self.P(lambda e, wsk=wsk, p0=p0: e.tensor_copy(lw2[p0:p0 + 64, :], wsk[p0:p0 + 64, 0:RD]), r=["wst%d" % k], w=[dk_])
        self.prep_w(1, RD, lambda c, s0, n: I["rwkv_g2"][l], lambda c, s0, n: lg2[:, :], lambda c: "lg2", "plain")
        WK1 = ["W1_%d" % c for c in range(8)]
        WK2 = ["W2_%d" % c for c in range(8)]
        self.release(m0)
        class NSP:
            pass
        zr, zk = sbl("zr", [128, RD]), sbl("zk", [128, RD])
        lact = sbl("lact", [128, 128], BF)
        T = [sbl("tmp%d" % i_, [128, RD]) for i_ in range(8)]
        sm = sbl("sm", [128, 64])
        orT = sbl("orT", [128, 4, 128], BF)
        sgr = sbl("sgr", [128, 8, 128], BF)
        mrT0_ = sbl("mrT0", [128, 8, 128], BF)
        mrT = [mrT0_, mrT0_]
        TP_ = [sbl("tpost%d" % i_, [128, RD]) for i_ in range(2)]
        m1 = self.aoff
        NRB = 9864

        def mkrec(k):
            R = NSP()
            rb = sbl("RB%d" % k, [128, NRB], BF)
            rf = sbl("RF%d" % k, [128, 528])
            R.rb, R.rf, R.k = rb, rf, k
            R.RKT = rb[:, 0:1024].rearrange("p (j a t) -> p j a t", j=4, a=2)
            R.G4 = [rb[:, 1024 + j * 1280:1024 + (j + 1) * 1280].rearrange("p (h c) -> p h c", h=2) for j in range(4)]
            R.ZF = [rb[:, 6144 + j * 256:6144 + (j + 1) * 256].rearrange("p (h c) -> p h c", h=2) for j in range(4)]
            R.vb, R.ktt, R.bnt = rb[:, 7168:7680], rb[:, 7680:8192], rb[:, 8192:8704]
            R.sgT = rb[:, 8704:8832]
            R.hT = rb[:, 8832:9864].rearrange("p (c t) -> p c t", c=8)
            R.zv, R.WC, R.bon = rf[:, 0:512], rf[:, 512:516], rf[:, 516:524]
            R.K = (lambda k_: (lambda n: "%s#%d" % (n, k_)))(k)
            return R
        R0 = mkrec(0)
        U0b = [sbl("U0b%d" % j, [128, 2, 64], BF) for j in range(4)]
        Ub = sbl("Ub", [128, RD], BF)
        Nst = sbl("Nst", [128, 4, 128])
        Nb = sbl("Nb", [128, 4, 128], BF)
        self.V(lambda e: e.memset(Nst[:], 0.0), w=["Nst"])
        self.V(lambda e: e.memset(Nb[:], 0.0), w=["Nb"])
        m2 = self.aoff
        rt, kat = sbl("rt", [128, RD], BF), sbl("kat", [128, RD], BF)
        KT = sbl("KT", [128, 4, 128], BF)
        BT = sbl("BT", [128, 4, 128], BF)
        for j in range(4):
            self.P(lambda e, j=j: e.tensor_copy(R0.G4[j][:, :, 512:640], identb[:, :].unsqueeze(1).to_broadcast([128, 2, 128])),
                   r=["identb"], w=["G4_%d" % j])
        EZ = [[sbl("EZ%d_%d" % (j, a), [128, 2, 2, 128], BF) for a in range(2)] for j in range(4)]
        FFa = [sbl("FFa%d" % a, [128, 4, 2, 128], BF) for a in range(2)]
        FF = [[FFa[a][:, j] for a in range(2)] for j in range(4)]

        def tok_proj(M, hcur, hprev, hk, g0, dstkey):
            ps, pk = self.pf()
            n = 0
            for c in range(8):
                fw.mm(ps[0:M, :], hcur(c), W1[:, c, g0:g0 + 512], n == 0, False, r=[hk, WK1[c]], w=[pk])
                n += 1
            for c in range(8):
                fw.mm(ps[0:M, :], hprev(c), W2[:, c, g0:g0 + 512], False, c == 7, r=[hk, WK2[c]], w=[pk])
            return ps, pk

        def feat_proj(M, hcur, hprev, hk, g0):
            ps, pk = self.pf()
            for c in range(8):
                fw.mm(ps[:, 0:M], W1[:, c, g0:g0 + 128], hcur(c), c == 0, False, r=[hk, WK1[c]], w=[pk])
            for c in range(8):
                fw.mm(ps[:, 0:M], W2[:, c, g0:g0 + 128], hprev(c), False, c == 7, r=[hk, WK2[c]], w=[pk])
            return ps, pk

        def raw_last(hl, hk, M, dst):
            for gi, g0 in enumerate(range(0, RP, 512)):
                n = min(512, RP - g0)
                ps, pk = self.pf()
                for c in range(8):
                    fw.mm(ps[0:M, 0:n], hl(c), W1[:, c, g0:g0 + n], c == 0, False, r=[hk, WK1[c]], w=[pk])
                for c in range(8):
                    fw.mm(ps[0:M, 0:n], hl(c), W2[:, c, g0:g0 + n], False, c == 7, r=[hk, WK2[c]], w=[pk])
                fw.act(T[gi][0:M, 0:n], ps[0:M, 0:n], AF.Copy, r=[pk], w=["T%d" % gi])
                fw.dma(dst[:, g0:g0 + n], T[gi][0:M, 0:n], r=["T%d" % gi], key="zl%d" % gi)

        def prep(M, sample, R):
            K = R.K
            w0, a0 = bcs["rwkv_w0"], bcs["rwkv_a0"]
            kkb, kab, rkb = bcs["rwkv_k_k"], bcs["rwkv_k_a"], bcs["rwkv_r_k"]
            pw, pwk = self.pf()
            fw.mm(pw[0:M, :], lact[0:64, 0:M], lw2[0:64, :], True, True, r=["lact", "lw2a"], w=[pwk])
            pa, pak = self.pf()
            fw.mm(pa[0:M, :], lact[64:128, 0:M], lw2[64:128, :], True, True, r=["lact", "lw2b"], w=[pak])
            sg, a_, kk, t3, kf, be = T[0], T[1], T[2], T[3], T[4], T[5]
            self.V(lambda e: e.tensor_tensor(sg[0:M, :], pw[0:M, :], w0[0:M, :], ALU.add), r=[pwk, "rwkv_w0_bc"], w=["T0"])
            fw.act(sg[0:M, :], sg[0:M, :], AF.Sigmoid, r=["T0"], w=["T0"])
            self.V(lambda e: e.tensor_tensor(a_[0:M, :], pa[0:M, :], a0[0:M, :], ALU.add), r=[pak, "rwkv_a0_bc"], w=["T1"])
            fw.act(a_[0:M, :], a_[0:M, :], AF.Sigmoid, r=["T1"], w=["T1"])
            self.P(lambda e: e.tensor_tensor(kk[0:M, :], zk[0:M, :], kkb[0:M, :], ALU.mult), r=["zk", "rwkv_k_k_bc"], w=["T2"])
            self.P(lambda e: e.tensor_tensor(t3[0:M, :], kk[0:M, :], kk[0:M, :], ALU.mult), r=["T2"], w=["T3"])
            self.V(lambda e: e.tensor_reduce(sm[0:M, 0:8], h3(t3[0:M, :]), AX.X, ALU.add), r=["T3"], w=["sm0"])
            self.V(lambda e: e.tensor_scalar(sm[0:M, 0:8], sm[0:M, 0:8], 1e-24, None, ALU.max), r=["sm0"], w=["sm0"])
            self.P(lambda e: e.tensor_tensor(sm[0:M, 0:8], sm[0:M, 0:8], self.mhalf[0:M, 0:8], ALU.pow), r=["sm0", "mhalf"], w=["sm0"])
            self.V(lambda e: e.tensor_tensor(h3(kk[0:M, :]), h3(kk[0:M, :]), bc3(sm[0:M, 0:8], 64), ALU.mult),
                   r=["T2", "sm0"], w=["T2"])
            self.V(lambda e: e.scalar_tensor_tensor(t3[0:M, :], a_[0:M, :], -1.0, kab[0:M, :], ALU.add, ALU.mult),
                   r=["T1", "rwkv_k_a_bc"], w=["T3"])
            self.V(lambda e: e.scalar_tensor_tensor(kf[0:M, :], t3[0:M, :], 1.0, zk[0:M, :], ALU.add, ALU.mult),
                   r=["T3", "zk"], w=["T4"])
            self.P(lambda e: e.tensor_tensor(be[0:M, :], kk[0:M, :], a_[0:M, :], ALU.mult), r=["T2", "T1"], w=["T5"])
            self.P(lambda e: e.tensor_tensor(t3[0:M, :], zr[0:M, :], kf[0:M, :], ALU.mult), r=["zr", "T4"], w=["T3"])
            self.P(lambda e: e.tensor_tensor(t3[0:M, :], t3[0:M, :], rkb[0:M, :], ALU.mult), r=["T3", "rwkv_r_k_bc"], w=["T3"])
            self.V(lambda e, R=R: e.tensor_reduce(R.bon[0:M, :], h3(t3[0:M, :]), AX.X, ALU.add), r=["T3"], w=[K("bon")])
            if sample:
                fw.act(T[6][0:M, :], sg[0:M, :], AF.Exp, r=["T0"], w=["T6"], scale=CDEC)
                for x, (tl, tk) in enumerate([(zr, "zr"), (T[6], "T6"), (kf, "T4"), (R.zv, K("zv")), (kk, "T2"), (be, "T5")]):
                    fw.dma(self.sq[x], tl[0:M, :], r=[tk], w=[("sq", x)], key="sqw%d" % x)
                return
            pli, plik = self.pf()
            fw.mm(pli[:, :], tri[:, 0:128], sg[:, :], True, True, r=["tri", "T0"], w=[plik])
            ple, plek = self.pf()
            fw.mm(ple[:, :], tri[:, 128:256], sg[:, :], True, True, r=["tri", "T0"], w=[plek])
            eL, eLm, enL = T[6], T[7], T[3]
            fw.act(eL[:, :], pli[:, :], AF.Exp, r=[plik], w=["T6"])
            fw.act(eLm[:, :], ple[:, :], AF.Exp, r=[plek], w=["T7"])
            fw.act(enL[:, :], pli[:, :], AF.Exp, r=[plik], w=["T3"], scale=-1.0)
            self.V(lambda e: e.tensor_tensor(rt[:, :], zr[:, :], eL[:, :], ALU.mult), r=["zr", "T6"], w=["rt"])
            self.V(lambda e: e.tensor_tensor(kat[:, :], kk[:, :], eLm[:, :], ALU.mult), r=["T2", "T7"], w=["kat"])
            self.P(lambda e, R=R: e.tensor_tensor(R.ktt[:, :], kf[:, :], enL[:, :], ALU.mult), r=["T4", "T3"], w=[K("ktt")])
            self.V(lambda e, R=R: e.scalar_tensor_tensor(R.bnt[:, :], be[:, :], -1.0, enL[:, :], ALU.mult, ALU.mult),
                   r=["T5", "T3"], w=[K("bnt")])
            fw.act(R.vb[:, :], R.zv[:, :], AF.Copy, r=[K("zv")], w=[K("vb")])
            pwc, pwck = self.pf()
            for j in range(4):
                fw.mm(pwc[:, j:j + 1], eL[:, j * 128:(j + 1) * 128], clast[:, :], True, True, r=["T6", "clast"], w=[pwck])
            fw.act(R.WC[:, :], pwc[:, 0:4], AF.Copy, r=[pwck], w=[K("WC")])
            for (src, skey, dstf, dk) in [(rt, "rt", None, "RKT"), (kat, "kat", None, "RKT"),
                                          (R.ktt, K("ktt"), None, "KT"), (R.bnt, K("bnt"), None, "BT")]:
                pbk, pk = self.pb()
                for j in range(4):
                    fw.tr(pbk[:, j * 128:(j + 1) * 128], src[:, j * 128:(j + 1) * 128], identb[:, :], r=[skey, "identb"], w=[pk])
                if dk == "RKT":
                    which = 0 if skey == "rt" else 1
                    fw.act(R.RKT[:, :, which, :], pbk[:, 0:512].rearrange("p (j t) -> p j t", j=4), AF.Copy, r=[pk], w=["RKT%d" % which])
                else:
                    dst = KT if dk == "KT" else BT
                    self.V(lambda e, dst=dst, pbk=pbk: e.tensor_copy(dst[:, :, :], pbk[:, 0:512].rearrange("p (j t) -> p j t", j=4)),
                           r=[pk], w=[dk])

        def stageAB(R):
            K = R.K
            RK = [K("RKT0"), K("RKT1")]
            RKT, G4, ZF = R.RKT, R.G4, R.ZF
            zb = [self.pf(), self.pf()]
            for j in range(4):
                for hh in range(2):
                    o = hh * 64
                    pZ, pzk = zb[hh]
                    fw.mm(pZ[:, j * 128:(j + 1) * 128], RKT[o:o + 64, j, 1, :], BT[o:o + 64, j, :], True, True, r=["BT", K("RKT1")], w=[pzk])
            mlb = maskL[:, :].unsqueeze(1).to_broadcast([128, 4, 128])
            for hh in range(2):
                pZ, pzk = zb[hh]
                self.V(lambda e, pZ=pZ, hh=hh: e.tensor_tensor(FFa[0][:, :, hh, :], pZ[:, :].rearrange("p (j c) -> p j c", j=4), mlb, ALU.mult),
                       r=[pzk, "maskL"], w=["FF%d_0" % j for j in range(4)])
            for j in range(4):
                bk = [self.pf(), self.pf()]
                for hh in range(2):
                    o = hh * 64
                    ps, pk = bk[hh]
                    rhs = RKT[o:o + 64, j, :, :].rearrange("p a t -> p (a t)")
                    fw.mm(ps[:, 0:256], KT[o:o + 64, j, :], rhs, True, True, r=["KT"] + RK, w=[pk])
                    fw.mm(ps[:, 256:512], BT[o:o + 64, j, :], rhs, True, True, r=["BT"] + RK, w=[pk])
                for hh in range(2):
                    ps, pk = bk[hh]
                    self.V(lambda e, j=j, hh=hh, ps=ps, G4=G4: e.tensor_tensor(
                        G4[j][:, hh, 0:512].rearrange("p (a c) -> p a c", a=2), ps[:, :].rearrange("p (a c) -> p a c", a=2),
                        mask2[:, :].unsqueeze(1).to_broadcast([128, 2, 256]), ALU.mult), r=[pk, "mask2"], w=[K("G4_%d" % j)])
            for lev in range(7):
                a, b = lev % 2, (lev + 1) % 2
                for j in range(4):
                    fk, fn_ = "FF%d_%d" % (j, a), "FF%d_%d" % (j, b)
                    ezn = "EZ%d_%d" % (j, b)
                    if lev == 0:
                        ezk = K("G4_%d" % j)
                        EZs = lambda hh, j=j, G4=G4: G4[j][:, hh, 384:640]
                        Es = lambda hh, j=j, G4=G4: G4[j][:, hh, 384:512]
                        Zs = lambda j=j, G4=G4: G4[j][:, :, 512:640]
                    else:
                        ezk = "EZ%d_%d" % (j, a)
                        EZs = lambda hh, j=j, a=a: EZ[j][a][:, hh, :, :].rearrange("p a t -> p (a t)")
                        Es = lambda hh, j=j, a=a: EZ[j][a][:, hh, 0, :]
                        Zs = lambda j=j, a=a: EZ[j][a][:, :, 1, :]
                    if lev < 6:
                        pL, plk = self.pf()
                        for hh in range(2):
                            fw.mm(pL[:, hh * 256:(hh + 1) * 256], FF[j][a][:, hh, :], EZs(hh), True, True, r=[ezk, fk], w=[plk])
                        pF, pfk = self.pf()
                        for hh in range(2):
                            fw.mm(pF[:, hh * 128:(hh + 1) * 128], Es(hh), FF[j][a][:, hh, :], True, True, r=[ezk, fk], w=[pfk])
                        l3 = pL[:, :].rearrange("p (h c) -> p h c", h=2)
                        fw.act(EZ[j][b][:, :, 0, :], l3[:, :, 0:128], AF.Copy, r=[plk], w=[ezn])
                        self.V(lambda e, j=j, b=b, l3=l3, Zs=Zs: e.tensor_tensor(EZ[j][b][:, :, 1, :], l3[:, :, 128:256], Zs(), ALU.add),
                               r=[plk, ezk], w=[ezn])
                        fw.act(FF[j][b][:, :, :], pF[:, 0:256].rearrange("p (h c) -> p h c", h=2), AF.Copy, r=[pfk], w=[fn_])
                    else:
                        pL, plk = self.pf()
                        for hh in range(2):
                            fw.mm(pL[:, hh * 128:(hh + 1) * 128], FF[j][a][:, hh, :], EZ[j][a][:, hh, 1, :], True, True, r=[ezk, fk], w=[plk])
                        self.V(lambda e, j=j, a=a, pL=pL, ZF=ZF: e.tensor_tensor(ZF[j][:, :, :], pL[:, 0:256].rearrange("p (h c) -> p h c", h=2),
                                                                      EZ[j][a][:, :, 1, :], ALU.add), r=[plk, ezk], w=[K("ZF%d" % j)])

        def stageC(R):
            K = R.K
            RKT, G4, ZF, vb = R.RKT, R.G4, R.ZF, R.vb
            for j in range(4):
                pU, puk = self.pf()
                for hh in range(2):
                    o, h = hh * 64, 2 * j + hh
                    fw.mm(pU[:, hh * 64:(hh + 1) * 64], RKT[o:o + 64, j, 1, :], Nb[o:o + 64, j, o:o + 64], True, False, r=[K("RKT1"), "Nb"], w=[puk])
                    fw.mm(pU[:, hh * 64:(hh + 1) * 64], G4[j][:, hh, 128:256], vb[:, h * 64:(h + 1) * 64], False, True, r=[K("G4_%d" % j), K("vb")], w=[puk])
                fw.act(U0b[j][:, :, :], pU[:, 0:128].rearrange("p (h c) -> p h c", h=2), AF.Copy, r=[puk], w=["U0b%d" % j])
            for j in range(4):
                pU, puk = self.pf()
                for hh in range(2):
                    fw.mm(pU[:, hh * 64:(hh + 1) * 64], ZF[j][:, hh, :], U0b[j][:, hh, :], True, True, r=[K("ZF%d" % j), "U0b%d" % j], w=[puk])
                fw.act(Ub[:, j * 128:(j + 1) * 128], pU[:, 0:128], AF.Copy, r=[puk], w=["Ub%d" % j])

        def stageD(R):
            K = R.K
            RKT, G4, vb = R.RKT, R.G4, R.vb
            psY, pyk = self.pf()
            for j in range(4):
                for hh in range(2):
                    o, h = hh * 64, 2 * j + hh
                    fw.mm(psY[:, h * 64:(h + 1) * 64], RKT[o:o + 64, j, 0, :], Nb[o:o + 64, j, o:o + 64], True, False, r=[K("RKT0"), "Nb"], w=[pyk])
                    fw.mm(psY[:, h * 64:(h + 1) * 64], G4[j][:, hh, 0:128], vb[:, h * 64:(h + 1) * 64], False, False, r=[K("G4_%d" % j), K("vb")], w=[pyk])
                    fw.mm(psY[:, h * 64:(h + 1) * 64], G4[j][:, hh, 256:384], Ub[:, h * 64:(h + 1) * 64], False, True, r=[K("G4_%d" % j), "Ub%d" % j], w=[pyk])
            return psY, pyk

        def n_update(R):
            K = R.K
            ktt, bnt, vb, WC = R.ktt, R.bnt, R.vb, R.WC
            pN, pnk = self.pf()
            for j in range(4):
                fw.mm(pN[:, j * 128:(j + 1) * 128], ktt[:, j * 128:(j + 1) * 128], vb[:, j * 128:(j + 1) * 128], True, False, r=[K("ktt"), K("vb")], w=[pnk])
                fw.mm(pN[:, j * 128:(j + 1) * 128], bnt[:, j * 128:(j + 1) * 128], Ub[:, j * 128:(j + 1) * 128], False, True, r=[K("bnt"), "Ub%d" % j], w=[pnk])
            n2 = Nst[:, :, :].rearrange("p j c -> p (j c)")
            self.V(lambda e: e.tensor_tensor(n2, pN[:, :], n2, ALU.add), r=[pnk, "Nst"], w=["Nst"])
            self.V(lambda e, WC=WC: e.tensor_tensor(Nst[:, :, :], Nst[:, :, :], bc3(WC[:, :], 128), ALU.mult), r=["Nst", K("WC")], w=["Nst"])
            fw.act(Nb[:, :, :], Nst[:, :, :], AF.Copy, r=["Nst"], w=["Nb"])


        def post(M, yap, ykeys, pg, pgk, R):
            K = R.K
            lng, lnb = bcs["rwkv_ln_g"], bcs["rwkv_ln_b"]
            y2, yc = TP_[0], TP_[1]
            ob = TP_[0].bitcast(BF)[:, 0:RD]
            self.V(lambda e: e.tensor_reduce(sm[0:M, 16:24], h3(yap), AX.X, ALU.add), r=ykeys, w=["sm2"])
            fw.act(y2[0:M, :], yap, AF.Square, r=ykeys, w=["TP0"])
            self.V(lambda e: e.tensor_reduce(sm[0:M, 24:32], h3(y2[0:M, :]), AX.X, ALU.add), r=["TP0"], w=["sm3"])
            mean, var = sm[0:M, 16:24], sm[0:M, 24:32]
            self.V(lambda e: e.tensor_scalar(mean, mean, 1.0 / 64, None, ALU.mult), r=["sm2"], w=["sm2"])
            self.V(lambda e: e.tensor_tensor(sm[0:M, 32:40], mean, mean, ALU.mult), r=["sm2"], w=["sm4"])
            self.V(lambda e: e.scalar_tensor_tensor(var, var, 1.0 / 64, sm[0:M, 32:40], ALU.mult, ALU.subtract), r=["sm3", "sm4"], w=["sm3"])
            self.V(lambda e: e.tensor_scalar(var, var, 64e-5, None, ALU.add), r=["sm3"], w=["sm3"])
            self.P(lambda e: e.tensor_tensor(var, var, self.mhalf[0:M, 0:8], ALU.pow), r=["sm3", "mhalf"], w=["sm3"])
            self.V(lambda e: e.tensor_tensor(h3(yc[0:M, :]), h3(yap), bc3(mean, 64), ALU.subtract), r=list(ykeys) + ["sm2"], w=["TP1"])
            self.V(lambda e: e.tensor_tensor(h3(yc[0:M, :]), h3(yc[0:M, :]), bc3(var, 64), ALU.mult), r=["TP1", "sm3"], w=["TP1"])
            self.P(lambda e: e.tensor_tensor(yc[0:M, :], yc[0:M, :], lng[0:M, :], ALU.mult), r=["TP1", "rwkv_ln_g_bc"], w=["TP1"])
            self.P(lambda e: e.tensor_tensor(yc[0:M, :], yc[0:M, :], lnb[0:M, :], ALU.add), r=["TP1", "rwkv_ln_b_bc"], w=["TP1"])
            self.P(lambda e, R=R: e.tensor_tensor(h3(y2[0:M, :]), h3(R.zv[0:M, :]), bc3(R.bon[0:M, :], 64), ALU.mult), r=[K("zv"), K("bon")], w=["TP0"])
            self.V(lambda e: e.tensor_tensor(yc[0:M, :], yc[0:M, :], y2[0:M, :], ALU.add), r=["TP1", "TP0"], w=["TP1"])
            self.V(lambda e: e.tensor_tensor(ob[0:M, :], yc[0:M, :], pg[0:M, :], ALU.mult), r=["TP1", pgk], w=["TP0"])
            pbk, pk = self.pb()
            for j in range(4):
                fw.tr(pbk[:, j * M:(j + 1) * M], ob[0:M, j * 128:(j + 1) * 128], identb[0:M, 0:M], r=["TP0", "identb"], w=[pk])
            fw.act(orT[:, :, 0:M], pbk[:, 0:4 * M].rearrange("p (j t) -> p j t", j=4), AF.Copy, r=[pk], w=["orT"])

        def gate_branch(M, hcur, hk, mdst, mkey):
            for half in range(2):
                pg, pgk = self.pf()
                for q in range(4):
                    dc = half * 4 + q
                    for c in range(8):
                        fw.mm(pg[:, q * M:(q + 1) * M], Wg[:, c, dc * 128:(dc + 1) * 128], hcur(c), c == 0, c == 7, r=[hk, "Wg_%d" % c], w=[pgk])
                fw.act(sgr[:, half * 4:(half + 1) * 4, 0:M], pg[:, 0:4 * M].rearrange("p (q t) -> p q t", q=4), AF.Sigmoid, r=[pgk], w=["sgr%d" % half])
                pbr, pbk_ = self.pf()
                for q in range(4):
                    dc = half * 4 + q
                    for j in range(4):
                        fw.mm(pbr[:, q * M:(q + 1) * M], Wr[:, j, dc * 128:(dc + 1) * 128], orT[:, j, 0:M], j == 0, j == 3, r=["orT", "Wr_%d" % j], w=[pbk_])
                self.V(lambda e, half=half, pbr=pbr: e.tensor_tensor(mdst[:, half * 4:(half + 1) * 4, 0:M], sgr[:, half * 4:(half + 1) * 4, 0:M],
                                                                 pbr[:, 0:4 * M].rearrange("p (q t) -> p q t", q=4), ALU.mult),
                       r=["sgr%d" % half, pbk_], w=[mkey])

        R1 = mkrec(1)
        for j in range(4):
            self.P(lambda e, j=j: e.tensor_copy(R1.G4[j][:, :, 512:640], identb[:, :].unsqueeze(1).to_broadcast([128, 2, 128])),
                   r=["identb"], w=[R1.K("G4_%d" % j)])
        RR = [R0, R1]

        def H1a(i):
            R, Rp = RR[i % 2], RR[(i + 1) % 2]
            hT = R.hT
            xt, xk = self.xt[i % 2], "xt%d" % (i % 2)
            src, _ = self.xsrc(l, i)
            fw.dma(xt[:], src, r=[("xb", i)], w=[xk], key=xk)
            hk = R.K("hTr")
            if i == 0:
                self.V(lambda e, hT=hT: e.memset(hT[:, :, 0:1], 0.0), w=[hk])
            else:
                self.P(lambda e, hT=hT, hp=Rp.hT: e.tensor_copy(hT[:, :, 0:1], hp[:, :, 128:129]), r=[Rp.K("hTr")], w=[hk])
            self.norm_a(xt, xk, 128)

        def H1b(i):
            R = RR[i % 2]
            K = R.K
            hT = R.hT
            hk = K("hTr")
            self.norm_b(128, hT[:, :, 1:129], hk, identb)
            hcur = lambda c, hT=hT: hT[:, c, 1:129]
            hprev = lambda c, hT=hT: hT[:, c, 0:128]
            for g0, dst, dk in [(0, zr, "zr"), (512, zk, "zk"), (1024, R.zv, K("zv"))]:
                ps, pk = tok_proj(128, hcur, hprev, hk, g0, dk)
                fw.act(dst[:, :], ps[:, :], AF.Copy, r=[pk], w=[dk])
            ps, pk = feat_proj(128, hcur, hprev, hk, 1536)
            fw.act(lact[0:64, :], ps[0:64, 0:128], AF.Tanh, r=[pk], w=["lact"])
            fw.act(lact[64:128, :], ps[64:128, 0:128], AF.Copy, r=[pk], w=["lact"])
            ps, pk = feat_proj(128, hcur, hprev, hk, 1664)
            fw.act(R.sgT[:, :], ps[:, 0:128], AF.Sigmoid, r=[pk], w=[K("sgT")])
            if i == NT - 1:
                raw_last(lambda c, hT=hT: hT[:, c, 128:129], hk, 1, O["p_shift"][l:l + 1, :])

        def H1c(i):
            prep(128, False, RR[i % 2])

        def H1d(i):
            stageAB(RR[i % 2])

        H2st = {}

        def H2a(i):
            R = RR[i % 2]
            stageC(R)
            psY, pyk = stageD(R)
            n_update(R)
            pg, pgk = self.pf()
            fw.mm(pg[:, :], R.sgT[:, :], lg2[:, :], True, True, r=[R.K("sgT"), "lg2"], w=[pgk])
            H2st[i] = (psY, pyk, pg, pgk)

        def H2b(i):
            psY, pyk, pg, pgk = H2st.pop(i)
            post(128, psY[:, :], [pyk], pg, pgk, RR[i % 2])

        def H2c(i):
            R = RR[i % 2]
            m, mk = mrT[0], "mrT0"
            gate_branch(128, lambda c, R=R: R.hT[:, c, 1:129], R.K("hTr"), m, mk)
            fw.dma(self.mrbuf[i].rearrange("p (c t) -> p c t", c=8), m[:, :, :], r=[mk], w=[("mr", i)], key=mk)

        def cap(pool, f, i):
            self.pool = pool
            return fw.capture(lambda: f(i))

        for f in (H1a, H1b, H1c):
            fw.replay([cap(0, f, 0)])
        fw.replay([cap(None, H1d, 0)])
        for i in range(NT):
            nx = i + 1 < NT
            if nx:
                fw.replay([cap(0, H1a, i + 1)])
            fw.replay([cap(1, H2a, i)])
            fw.replay(([cap(0, H1b, i + 1)] if nx else []) + [cap(1, H2b, i)])
            fw.replay(([cap(0, H1c, i + 1)] if nx else []) + [cap(1, H2c, i)])
            if nx:
                fw.replay([cap(None, H1d, i + 1)])
        self.pool = None
        for j in range(4):
            ps, pk = self.pf()
            fw.tr(ps[:, 0:128], Nst[:, j, :], identf[:, :], r=["Nst", "identf"], w=[pk])
            fw.act(T[0][:, j * 128:(j + 1) * 128], ps[:, 0:128], AF.Copy, r=[pk], w=["T0"])
        for h_ in range(8):
            j, o = h_ // 2, (h_ % 2) * 64
            fw.dma(O["p_wkv"][l, h_], T[0][o:o + 64, j * 128 + o:j * 128 + o + 64], r=["T0"], key="T0")

        self.release(m1)
        RS = NSP()
        RS.zv = sbl("zv_s", [128, RD])
        RS.sgT = sbl("sgT_s", [128, 128], BF)
        RS.bon = sbl("bon_s", [128, 8])
        RS.K = lambda n: n + "#s"
        hTs = sbl("hTs", [128, 8, 80], BF)
        sadd = sbl("sadd", [16, RP])
        stT = sbl("stT", [128, 2, 16])
        zf = sbl("zf", [128, 2, 64])
        QH = sbl("QH", [128, 6, 4, 64])
        Sst = sbl("Sst", [128, 64, 64])
        Stmp = sbl("Stmp", [128, 64, 64])
        sk = sbl("sk", [128, 64])
        yh = sbl("yh", [128, 4, 64])
        ytm = T[7]
        self.V(lambda e: e.memset(hTs[:], 0.0), w=["hTs"])
        i = NT
        xt, xk = self.xt[i % 2], "xt%d" % (i % 2)
        src, _ = self.xsrc(l, i)
        fw.dma(xt[0:MS, :], src, r=[("xb", i)], w=[xk], key=xk)
        self.norm_hT(xt, xk, MS, hTs[:, :, 16:80], "hTs", identb)
        hcur = lambda c: hTs[:, c, 16:80]
        hprev = lambda c: hTs[:, c, 0:64]
        fw.dma(sadd[:, :], I["st_shift"][l], w=["sadd"], key="sadd")
        for q in range(2):
            ps, pk = self.pf()
            fw.tr(ps[:, 0:16], sadd[0:16, 1536 + q * 128:1536 + (q + 1) * 128], identf[0:16, 0:16], r=["sadd", "identf"], w=[pk])
            self.V(lambda e, q=q, ps=ps: e.tensor_scalar(stT[:, q, :], ps[:, 0:16], mucol[:, q:q + 1], None, ALU.mult), r=[pk, "mucol"], w=["stT"])
        for gi, g0 in enumerate(range(0, RP, 512)):
            n = min(512, RP - g0)
            self.bcast_load(T[4 + gi][0:16, 0:n], "T%d" % (4 + gi), I["rwkv_mu"][l, g0:g0 + n])
            self.V(lambda e, gi=gi, g0=g0, n=n: e.tensor_tensor(sadd[:, g0:g0 + n], sadd[:, g0:g0 + n], T[4 + gi][0:16, 0:n], ALU.mult),
                   r=["sadd", "T%d" % (4 + gi)], w=["sadd"])
        zv, sgT = RS.zv, RS.sgT
        for g0, dst, dk in [(0, zr, "zr"), (512, zk, "zk"), (1024, zv, RS.K("zv"))]:
            ps, pk = tok_proj(MS, hcur, hprev, "hTs", g0, dk)
            fw.act(dst[0:MS, :], ps[0:MS, :], AF.Copy, r=[pk], w=[dk])
            self.V(lambda e, dst=dst, g0=g0: e.tensor_tensor(dst[0:16, :], dst[0:16, :], sadd[0:16, g0:g0 + 512], ALU.add), r=[dk, "sadd"], w=[dk])
        for q, g0 in enumerate([1536, 1664]):
            ps, pk = feat_proj(MS, hcur, hprev, "hTs", g0)
            fw.act(zf[:, q, :], ps[:, 0:MS], AF.Copy, r=[pk], w=["zf"])
            self.V(lambda e, q=q: e.tensor_tensor(zf[:, q, 0:16], zf[:, q, 0:16], stT[:, q, :], ALU.add), r=["zf", "stT"], w=["zf"])
        fw.act(lact[0:64, 0:MS], zf[0:64, 0, :], AF.Tanh, r=["zf"], w=["lact"])
        fw.act(lact[64:128, 0:MS], zf[64:128, 0, :], AF.Copy, r=["zf"], w=["lact"])
        fw.act(sgT[:, 0:MS], zf[:, 1, :], AF.Sigmoid, r=["zf"], w=[RS.K("sgT")])
        prep(MS, True, RS)
        if l == 0:
            for nm, ap, k in [("s_zr", zr, "zr"), ("s_zk", zk, "zk"), ("s_zv", zv, "zv"), ("s_dec", T[6], "T6"), ("s_kk", T[2], "T2"),
                              ("s_kf", T[4], "T4"), ("s_be", T[5], "T5"), ("s_a", T[1], "T1")]:
                self.tap(nm, ap[0:MS, :], [k])
        sqv = self.sq.rearrange("x (t q) (h d) -> (q h) x t d", t=4, h=NH)
        for x in range(6):
            fw.dma(QH[:, x, :, :], sqv[:, x, :, :], r=[("sq", x)], w=["QH"], key="QH")
        fw.dma(Sst[:, :, :].rearrange("p v k -> p (v k)"), I["st_wkv"][l], w=["Sst"], key="Sst")
        for t in range(4):
            r_, w_, k_, v_, kk_, b_ = (QH[:, x, t, :] for x in range(6))
            rowb = lambda a: a.unsqueeze(1).to_broadcast([128, 64, 64])
            colb = lambda a: a.unsqueeze(2).to_broadcast([128, 64, 64])
            self.V(lambda e, kk_=kk_: e.tensor_tensor(Stmp[:, :, :], Sst[:, :, :], rowb(kk_), ALU.mult), r=["Sst", "QH"], w=["Stmp"])
            self.V(lambda e: e.tensor_reduce(sk[:, :], Stmp[:, :, :], AX.X, ALU.add), r=["Stmp"], w=["sk"])
            self.P(lambda e, w_=w_: e.tensor_tensor(Sst[:, :, :], Sst[:, :, :], rowb(w_), ALU.mult), r=["Sst", "QH", "Stmp"], w=["Sst"])
            self.V(lambda e, b_=b_: e.tensor_tensor(Stmp[:, :, :], colb(sk[:, :]), rowb(b_), ALU.mult), r=["sk", "QH"], w=["Stmp"])
            self.V(lambda e: e.tensor_tensor(Sst[:, :, :], Sst[:, :, :], Stmp[:, :, :], ALU.subtract), r=["Sst", "Stmp"], w=["Sst"])
            self.P(lambda e, v_=v_, k_=k_: e.tensor_tensor(Stmp[:, :, :], colb(v_), rowb(k_), ALU.mult), r=["QH", "Sst"], w=["Stmp"])
            self.V(lambda e: e.tensor_tensor(Sst[:, :, :], Sst[:, :, :], Stmp[:, :, :], ALU.add), r=["Sst", "Stmp"], w=["Sst"])
            self.P(lambda e, r_=r_: e.tensor_tensor(Stmp[:, :, :], Sst[:, :, :], rowb(r_), ALU.mult), r=["Sst", "QH"], w=["Stmp"])
            self.V(lambda e, t=t: e.tensor_reduce(yh[:, t, :], Stmp[:, :, :], AX.X, ALU.add), r=["Stmp"], w=["yh"])
        fw.dma(O["s_wkv"][l], Sst[:, :, :].rearrange("p v k -> p (v k)"), r=["Sst"], key="Sst")
        if l == 0:
            self.tap("s_QH", QH, ["QH"])
            self.tap("s_yh", yh, ["yh"])
        fw.dma(self.sy.rearrange("(t q) (h d) -> (q h) t d", t=4, h=NH), yh[:, :, :], r=["yh"], w=["sy"], key="yh")
        fw.dma(ytm[0:MS, :], self.sy, r=["sy"], w=["T7"], key="ytm")
        pg, pgk = self.pf()
        fw.mm(pg[0:MS, :], sgT[:, 0:MS], lg2[:, :], True, True, r=[RS.K("sgT"), "lg2"], w=[pgk])
        post(MS, ytm[0:MS, :], ["T7"], pg, pgk, RS)
        m, mk = mrT[0], "mrT0"
        gate_branch(MS, hcur, "hTs", m, mk)
        fw.dma(self.mrbuf[NT].rearrange("p (c t) -> p c t", c=8)[:, :, 0:MS], m[:, :, 0:MS], r=[mk], w=[("mr", NT)], key=mk)
        raw_last(lambda c: hTs[:, c, 64:80], "hTs", 16, O["s_shift"][l])

    def pass_attn(self, l, es2):
        fw, I, O, NT = self.fw, self.I, self.O, self.NT
        sbl = lambda n, s, dt=F32: self.sbl(es2, "a%d_" % l + n, s, dt)
        identb, identf = self.identb, self.identf
        Wq = sbl("Wq", [128, 8, 768], BF)
        Wg = sbl("Wg", [128, 8, D], BF)
        Wa = sbl("Wa", [128, 4, D], BF)
        Wo = sbl("Wo", [128, 8, D], BF)
        self.col_load(self.gcol[:], "gcol", I["norm_mix_g"][l], 8)
        m0 = self.aoff
        self.wstage = [sbl("wst%d" % i_, [128, 2048]) for i_ in range(4)]
        win = I["w_in"][l]
        gsc = lambda c: self.gcol[:, c:c + 1]
        self.prep_w(8, 512, lambda c, s0, n: win[c * 128:(c + 1) * 128, RP:RP + 512],
                    lambda c, s0, n: Wq[:, c, 0:512].rearrange("p (j g d) -> p g j d", j=4, g=2), lambda c: "Wq_%d" % c, "col", gsc,
                    sview=lambda a: a.rearrange("p (g j d) -> p g j d", g=2, j=4))
        self.prep_w(8, 256, lambda c, s0, n: win[c * 128:(c + 1) * 128, RP + 512:RP + 768],
                    lambda c, s0, n: Wq[:, c, 512:768], lambda c: "Wq_%d" % c, "col", gsc)
        self.prep_w(8, D, lambda c, s0, n: win[c * 128:(c + 1) * 128, 3584 + s0:3584 + s0 + n],
                    lambda c, s0, n: Wg[:, c, s0:s0 + n], lambda c: "Wga_%d" % c, "col", gsc)
        wbr = I["w_br_attn"][l]
        self.prep_w(4, D, lambda c, s0, n: wbr[c * 128:(c + 1) * 128, s0:s0 + n],
                    lambda c, s0, n: Wa[:, c, s0:s0 + n], lambda c: "Wa_%d" % c, "plain")
        wo = I["w_out"][l]
        self.prep_w(8, D, lambda c, s0, n: wo[c * 128:(c + 1) * 128, s0:s0 + n],
                    lambda c, s0, n: Wo[:, c, s0:s0 + n], lambda c: "Wo_%d" % c, "plain")
        self.release(m0)
        amask = sbl("amask", [128, 1024])
        fw.dma(amask[:, 0:768], I["c_amask"], w=["amask"], key="amask")
        fw.dma(amask[:, 768:1024], I["c_amask0"], w=["amask"], key="amask")
        smask = sbl("smask", [32, 132])
        fw.dma(smask[:], I["c_smask"], w=["smask"], key="smask")
        sinks = sbl("sinks", [128, NH])
        self.bcast_load(sinks[:], "sinks", I["attn_sinks"][l])
        hTd = [sbl("hT%d" % i_, [128, 8, 128], BF) for i_ in range(2)]
        hT = hTd[1]
        qkv = sbl("qkv", [128, 768])
        rot = sbl("rot", [128, 640])
        rtmp = [sbl("rtmp%d" % i, [128, 320]) for i in range(2)]
        rotb = sbl("rotb", [128, 640], BF)
        cs = [sbl("cs%d" % i, [128, 64]) for i in range(2)]
        qT = sbl("qT", [128, 4, 128], BF)

        class NSB:
            pass
        B0, B1 = NSB(), NSB()
        B0.qkv, B0.rot, B0.rotb, B0.qT, B0.s = qkv, rot, rotb, qT, ""
        B1.qkv, B1.rot, B1.rotb, B1.qT, B1.s = (sbl("qkvb", [128, 768]), sbl("rotbb", [128, 640]), sbl("rotbbb", [128, 640], BF),
                                                sbl("qTb", [128, 4, 128], BF), "b")
        Bs = [B0, B1]
        KTr = sbl("KTr", [128, 2, 128], BF)
        Vp = sbl("Vp", [128, 2, 2, 2, 128], BF)
        scg = [sbl("sc%d" % g_, [128, 4, 256]) for g_ in range(2)]
        stg = [sbl("st%d" % g_, [128, 16]) for g_ in range(2)]
        pbfg = [sbl("pbf%d" % g_, [128, 4, 256], BF) for g_ in range(2)]
        pTg = [sbl("pT%d" % g_, [128, 4, 2, 128], BF) for g_ in range(2)]
        oT = sbl("oT", [128, 4, 128], BF)
        sga = sbl("sga", [128, 8, 128])
        mrl = [sbl("mrl%d" % i, [128, 8, 128], BF) for i in range(2)]
        mg = sbl("mg", [128, 8, 128], BF)
        xo = [sbl("xo%d" % i, [128, D]) for i in range(2)]
        KA = sbl("KA", [128, NS, 128])
        VA = sbl("VA", [128, NS, 128])
        VAb = sbl("VAb", [128, NS, 128], BF)
        KB = sbl("KB", [4, NS, 128])
        VBt = sbl("VB", [4, NS, 128])
        VBb = sbl("VBb", [4, NS, 128], BF)
        KAT = sbl("KAT", [128, NS, 128], BF)
        KBT = sbl("KBT", [128, NS, 4], BF)
        qbd = sbl("qbd", [128, NS, 32], BF)
        ssc = sbl("ssc", [32, NS, 132])
        sst = sbl("sst", [32, 4 * NS])
        spb = sbl("spb", [32, NS, 132], BF)
        spT = sbl("spT", [128, NS, 32], BF)
        spTB = sbl("spTB", [4, NS, 32], BF)
        oTs = sbl("oTs", [128, 4, MS], BF)

        self.V(lambda e: e.memset(Vp[:], 0.0), w=["Vp0", "Vp1"])
        self.V(lambda e: e.memset(KTr[:], 0.0), w=["KTr0", "KTr1"])
        self.V(lambda e: e.memset(qbd[:], 0.0), w=["qbd"])

        def proj_rope(B, M, hcur, hk, cosap, sinap, cskey):
            for g0, n in [(0, 512), (512, 256)]:
                ps, pk = self.pf()
                for c in range(8):
                    fw.mm(ps[0:M, 0:n], hcur(c), Wq[:, c, g0:g0 + n], c == 0, c == 7, r=[hk, "Wq_%d" % c], w=[pk])
                fw.act(B.qkv[0:M, g0:g0 + n], ps[0:M, 0:n], AF.Copy, r=[pk], w=["qkv%d" % (g0 // 512) + B.s])
            qk3 = B.qkv[0:M, 0:640].rearrange("p (h d) -> p h d", h=10)
            r3 = B.rot[0:M, :].rearrange("p (h d) -> p h d", h=10)
            x1, x2 = qk3[:, :, 0:32], qk3[:, :, 32:64]
            cb = cosap.unsqueeze(1).to_broadcast([M, 10, 32])
            sb_ = sinap.unsqueeze(1).to_broadcast([M, 10, 32])
            ta = rtmp[0][0:M, :].rearrange("p (h d) -> p h d", h=10)
            tb = rtmp[1][0:M, :].rearrange("p (h d) -> p h d", h=10)
            rk = ["qkv0" + B.s, "qkv1" + B.s, cskey]
            rotk = "rot" + B.s
            self.V(lambda e: e.tensor_tensor(ta, x1, cb, ALU.mult), r=rk, w=["rtmp0"])
            self.P(lambda e: e.tensor_tensor(tb, x2, sb_, ALU.mult), r=rk, w=["rtmp1"])
            self.V(lambda e: e.tensor_tensor(r3[:, :, 0:32], ta, tb, ALU.subtract), r=["rtmp0", "rtmp1"], w=[rotk])
            self.V(lambda e: e.tensor_tensor(ta, x2, cb, ALU.mult), r=rk + [rotk], w=["rtmp0"])
            self.P(lambda e: e.tensor_tensor(tb, x1, sb_, ALU.mult), r=rk + [rotk], w=["rtmp1"])
            self.V(lambda e: e.tensor_tensor(r3[:, :, 32:64], ta, tb, ALU.add), r=["rtmp0", "rtmp1"], w=[rotk])
            fw.act(B.rotb[0:M, :], B.rot[0:M, :], AF.Copy, r=[rotk], w=["rotb" + B.s])

        def q_transposes(B, M, dst, dkey):
            pbk, pk = self.pb()
            for jj in range(4):
                fw.tr(pbk[:, jj * M:(jj + 1) * M], B.rotb[0:M, jj * 128:(jj + 1) * 128], identb[0:M, 0:M], r=["rotb" + B.s, "identb"], w=[pk])
            fw.act(dst, pbk[:, 0:4 * M].rearrange("p (j t) -> p j t", j=4), AF.Copy, r=[pk], w=[dkey])

        def gates_part(M, hcur, hk):
            for half in range(2):
                pg, pgk = self.pf()
                for q in range(4):
                    dc = half * 4 + q
                    for c in range(8):
                        fw.mm(pg[:, q * M:(q + 1) * M], Wg[:, c, dc * 128:(dc + 1) * 128], hcur(c), c == 0, c == 7, r=[hk, "Wga_%d" % c], w=[pgk])
                fw.act(sga[:, half * 4:(half + 1) * 4, 0:M], pg[:, 0:4 * M].rearrange("p (q t) -> p q t", q=4), AF.Sigmoid, r=[pgk], w=["sga%d" % half])

        def gate_out(M, hcur, hk, oTt, okey, mr, mrk, xt, xk, xo_, xok, do_gates=True):
            if do_gates:
                gates_part(M, hcur, hk)
            for half in range(2):
                pbr, pbk_ = self.pf()
                for q in range(4):
                    dc = half * 4 + q
                    for cc in range(4):
                        fw.mm(pbr[:, q * M:(q + 1) * M], Wa[:, cc, dc * 128:(dc + 1) * 128], oTt[:, cc, 0:M], cc == 0, cc == 3, r=[okey, "Wa_%d" % cc], w=[pbk_])
                hs = slice(half * 4, (half + 1) * 4)
                self.V(lambda e, hs=hs, pbr=pbr: e.tensor_tensor(sga[:, hs, 0:M], sga[:, hs, 0:M], pbr[:, 0:4 * M].rearrange("p (q t) -> p q t", q=4), ALU.mult),
                       r=["sga%d" % half, pbk_], w=["sga%d" % half])
                self.V(lambda e, hs=hs: e.tensor_tensor(mg[:, hs, 0:M], sga[:, hs, 0:M], mr[:, hs, 0:M], ALU.add), r=["sga%d" % half, mrk], w=["mg%d" % half])
            for grp in range(2):
                px, pxk = self.pf()
                for dc in range(8):
                    fw.mm(px[0:M, :], mg[:, dc, 0:M], Wo[:, dc, grp * 512:(grp + 1) * 512], dc == 0, dc == 7, r=["mg%d" % (dc // 4), "Wo_%d" % dc], w=[pxk])
                self.V(lambda e, grp=grp, px=px: e.tensor_tensor(xo_[0:M, grp * 512:(grp + 1) * 512], xt[0:M, grp * 512:(grp + 1) * 512], px[0:M, :], ALU.add),
                       r=[xk, pxk], w=[xok])

        def put_kv(B, slot):
            pbk, pk = self.pb()
            fw.tr(pbk[:, 0:128], B.rotb[:, 512:640], identb[:, :], r=["rotb" + B.s, "identb"], w=[pk])
            self.V(lambda e, pbk=pbk, slot=slot: e.tensor_copy(KTr[:, slot, :], pbk[:, 0:128]), r=[pk], w=["KTr%d" % slot])
            for g in range(2):
                vsrc = B.qkv[:, 640 + g * 64:640 + (g + 1) * 64]
                fw.act(Vp[:, slot, g, 0, 0:64], vsrc, AF.Copy, r=["qkv1" + B.s], w=["Vp%d" % slot])
                self.P(lambda e, g=g, vsrc=vsrc, slot=slot: e.tensor_copy(Vp[:, slot, g, 1, 64:128], vsrc), r=["qkv1" + B.s], w=["Vp%d" % slot])

        xt, xk = self.xt[1], "xt1"
        fw.dma(xt[:], (I["xh0"] if (l == 0 or NSEG == 1) else self.xh_dram), r=["xh_dram"], w=[xk], key=xk)
        fw.dma(cs[1][:, 0:32], I["c_cosh"], w=["cs1"], key="cs1")
        fw.dma(cs[1][:, 32:64], I["c_sinh"], w=["cs1"], key="cs1")
        self.norm_hT(xt, xk, 128, hT[:, :, :], "hT1", identb)
        proj_rope(B1, 128, lambda c: hT[:, c, :], "hT1", cs[1][:, 0:32], cs[1][:, 32:64], "cs1")
        put_kv(B1, 1)
        def pre(i):
            xt, xk = self.xt[i % 2], "xt%d" % (i % 2)
            src, _ = self.xsrc(l, i)
            fw.dma(xt[:], src, r=[("xb", i)], w=[xk], key=xk)
            mr, mrk = mrl[i % 2], "mrl%d" % (i % 2)
            fw.dma(mr[:, :, :], self.mrbuf[i].rearrange("p (c t) -> p c t", c=8), r=[("mr", i)], w=[mrk], key=mrk)
            ck_ = "cs%d" % (i % 2)
            fw.dma(cs[i % 2][:, 0:32], I["c_cosp"][i * 128:(i + 1) * 128, :], w=[ck_], key=ck_)
            fw.dma(cs[i % 2][:, 32:64], I["c_sinp"][i * 128:(i + 1) * 128, :], w=[ck_], key=ck_)
            self.norm_hT(xt, xk, 128, hTd[i % 2][:, :, :], "hT%d" % (i % 2), identb)
            B = Bs[i % 2]
            proj_rope(B, 128, lambda c, i=i: hTd[i % 2][:, c, :], "hT%d" % (i % 2), cs[i % 2][:, 0:32], cs[i % 2][:, 32:64], ck_)
            q_transposes(B, 128, B.qT[:, :, :], "qT" + B.s)

        pre(0)
        for i in range(NT):
            xt, xk = self.xt[i % 2], "xt%d" % (i % 2)
            mr, mrk = mrl[i % 2], "mrl%d" % (i % 2)
            ck_ = "cs%d" % (i % 2)
            hkk = "hT%d" % (i % 2)
            hcur = lambda c, i=i: hTd[i % 2][:, c, :]
            B = Bs[i % 2]
            slot = i % 2
            if i == NT - 1:
                fw.dma(O["p_k"][l], B.rot[:, 512:640], r=["rot" + B.s], key="rot")
                fw.dma(O["p_v"][l], B.qkv[:, 640:768], r=["qkv1" + B.s], key="qkv1")
            put_kv(B, slot)
            mvar = 3 if i == 0 else slot
            msk = amask[:, mvar * 256:(mvar + 1) * 256].unsqueeze(1).to_broadcast([128, 4, 256])
            pSg = []
            for g in range(2):
                o = g * 64
                pS = []
                for jj in range(4):
                    if jj % 2 == 0:
                        ps, pk = self.pf()
                        pS.append((ps, pk))
                    fw.mm(ps[:, (jj % 2) * 256:(jj % 2 + 1) * 256], B.qT[o:o + 64, jj, :], KTr[o:o + 64, :, :].rearrange("p s t -> p (s t)"),
                          True, True, r=["qT" + B.s, "KTr0", "KTr1"], w=[pk])
                pSg.append(pS)
            gates_part(128, hcur, hkk)

            def softmax(g):
                sc, st, pbf = scg[g], stg[g], pbfg[g]
                sck = ["sc%d_0" % g, "sc%d_1" % g]
                for half, (ps, pk) in enumerate(pSg[g]):
                    self.V(lambda e, ps=ps, half=half, msk=msk, sc=sc: e.scalar_tensor_tensor(
                        sc[:, half * 2:(half + 1) * 2, :], ps[:, :].rearrange("p (j c) -> p j c", j=2), 0.125,
                        msk[:, 0:2, :], ALU.mult, ALU.add), r=[pk, "amask"], w=[sck[half]])
                k0, k2, k3 = "st%d" % g, "st%d_2" % g, "st%d_3" % g
                self.V(lambda e: e.tensor_reduce(st[:, 0:4], sc[:, :, :], AX.X, ALU.max), r=sck, w=[k0])
                self.V(lambda e: e.tensor_tensor(st[:, 0:4], st[:, 0:4], sinks[:, g * 4:(g + 1) * 4], ALU.max), r=[k0, "sinks"], w=[k0])
                self.V(lambda e: e.tensor_tensor(sc[:, :, :], sc[:, :, :], bc3(st[:, 0:4], 256), ALU.subtract), r=sck + [k0], w=sck)
                fw.act(sc[:, :, :], sc[:, :, :], AF.Exp, r=sck, w=sck)
                self.V(lambda e: e.tensor_reduce(st[:, 4:8], sc[:, :, :], AX.X, ALU.add), r=sck, w=[k2])
                self.V(lambda e: e.tensor_tensor(st[:, 8:12], sinks[:, g * 4:(g + 1) * 4], st[:, 0:4], ALU.subtract), r=[k0, "sinks"], w=[k3])
                fw.act(st[:, 8:12], st[:, 8:12], AF.Exp, r=[k3], w=[k3])
                self.V(lambda e: e.tensor_tensor(st[:, 4:8], st[:, 4:8], st[:, 8:12], ALU.add), r=[k2, k3], w=[k2])
                self.V(lambda e: e.reciprocal(st[:, 4:8], st[:, 4:8]), r=[k2], w=[k2])
                self.V(lambda e: e.tensor_tensor(pbf[:, :, :], sc[:, :, :], bc3(st[:, 4:8], 256), ALU.mult), r=sck + [k2], w=["pbf%d" % g])

            def p_transposes(g):
                pbf, pT = pbfg[g], pTg[g]
                pbk, pk = self.pb()
                for jj in range(4):
                    for s_ in range(2):
                        fw.tr(pbk[:, (jj * 2 + s_) * 128:(jj * 2 + s_ + 1) * 128], pbf[:, jj, s_ * 128:(s_ + 1) * 128], identb[:, :], r=["pbf%d" % g, "identb"], w=[pk])
                fw.act(pT[:, :, :, :], pbk[:, :].rearrange("p (j s t) -> p j s t", j=4, s=2), AF.Copy, r=[pk], w=["pT%d" % g])

            def pv(g, pO, pok):
                pT = pTg[g]
                for c2 in range(2):
                    cc = g * 2 + c2
                    n = 0
                    for par in range(2):
                        jj = c2 * 2 + par
                        for s_ in range(2):
                            fw.mm(pO[:, cc * 128:(cc + 1) * 128], Vp[:, s_, g, par, :], pT[:, jj, s_, :], n == 0, n == 3,
                                  r=["Vp0", "Vp1", "pT%d" % g], w=[pok])
                            n += 1

            fw.replay([fw.capture(lambda: softmax(0)), fw.capture(lambda: softmax(1))], chunk=1)
            p_transposes(0)
            pO, pok = self.pf()
            pv(0, pO, pok)
            if i + 1 < NT:
                pre(i + 1)
            p_transposes(1)
            pv(1, pO, pok)
            fw.act(oT[:, :, :], pO[:, :].rearrange("p (c t) -> p c t", c=4), AF.Copy, r=[pok], w=["oT"])
            xo_, xok = xo[i % 2], "xo%d" % (i % 2)
            gate_out(128, hcur, hkk, oT, "oT", mr, mrk, xt, xk, xo_, xok, do_gates=False)
            fw.dma(self.xbuf[i * 128:(i + 1) * 128, :], xo_[:, :], r=[xok], w=[("xb", i)], key=xok)
        if NSEG > 1:
            self.gather_select(xo_[:, :], [xok], D, self.agX_in, self.agX_out, "agX")
            fw.dma(self.xh_dram, xo_[:, :], r=[xok], w=["xh_dram"], key="xhst")

        i = NT
        xt, xk = self.xt[i % 2], "xt%d" % (i % 2)
        src, _ = self.xsrc(l, i)
        fw.dma(xt[0:MS, :], src, r=[("xb", i)], w=[xk], key=xk)
        mr, mrk = mrl[i % 2], "mrl%d" % (i % 2)
        fw.dma(mr[:, :, 0:MS], self.mrbuf[NT].rearrange("p (c t) -> p c t", c=8)[:, :, 0:MS], r=[("mr", NT)], w=[mrk], key=mrk)
        ck_ = "cs%d" % (i % 2)
        fw.dma(cs[i % 2][0:MS, 0:32], I["c_coss"], w=[ck_], key=ck_)
        fw.dma(cs[i % 2][0:MS, 32:64], I["c_sins"], w=[ck_], key=ck_)
        self.norm_hT(xt, xk, MS, hT[:, :, 0:MS], "hT1", identb)
        hcur = lambda c: hT[:, c, 0:MS]
        proj_rope(B0, MS, hcur, "hT1", cs[i % 2][0:MS, 0:32], cs[i % 2][0:MS, 32:64], ck_)
        for (cin, cout, srcap, srck, dkey) in [("ck", "s_k", rot[:, 512:640], "rot", "sk"), ("cv", "s_v", qkv[:, 640:768], "qkv1", "sv")]:
            fw.dma(O[cout][l, :, 0:124, :], I[cin][l, :, 4:128, :], w=[dkey], key=dkey + "c")
            for t in range(4):
                fw.dma(O[cout][l, :, 124 + t, :], srcap[t * 16:(t + 1) * 16, :], r=[srck], w=[dkey], key=dkey + "n")
        fw.dma(KA[:, :, :], O["s_k"][l].rearrange("q p c -> p q c"), r=["sk"], w=["KA"], key="KA")
        fw.dma(VA[:, :, :], O["s_v"][l].rearrange("q p c -> p q c"), r=["sv"], w=["VA"], key="VA")
        fw.dma(KB[:, :, :], I["ck"][l, :, 0:4, :].rearrange("q p c -> p q c"), w=["KB"], key="KB")
        fw.dma(VBt[:, :, :], I["cv"][l, :, 0:4, :].rearrange("q p c -> p q c"), w=["VB"], key="VB")
        self.P(lambda e: e.tensor_copy(VAb[:, :, :], VA[:, :, :]), r=["VA"], w=["VAb"])
        self.P(lambda e: e.tensor_copy(VBb[:, :, :], VBt[:, :, :]), r=["VB"], w=["VBb"])
        for q4 in range(4):
            ps, pk = self.pf()
            for qq in range(4):
                q = q4 * 4 + qq
                fw.tr(ps[:, qq * 128:(qq + 1) * 128], KA[:, q, :], identf[:, :], r=["KA", "identf"], w=[pk])
            fw.act(KAT[:, q4 * 4:(q4 + 1) * 4, :], ps[:, :].rearrange("p (q t) -> p q t", q=4), AF.Copy, r=[pk], w=["KAT"])
        ps, pk = self.pf()
        for q in range(NS):
            fw.tr(ps[:, q * 4:(q + 1) * 4], KB[0:4, q, :], identf[0:4, 0:4], r=["KB", "identf"], w=[pk])
        fw.act(KBT[:, :, :], ps[:, 0:64].rearrange("p (q t) -> p q t", q=NS), AF.Copy, r=[pk], w=["KBT"])
        q_transposes(B0, MS, qT[:, :, 0:MS], "qT")
        for g in range(2):
            for jj in range(4):
                o = g * 64
                dst = qbd[o:o + 64, :, g * 16 + jj * 4:g * 16 + (jj + 1) * 4]
                srcq = qT[o:o + 64, jj, 0:MS].rearrange("p (t q) -> p q t", t=4)
                self.V(lambda e, dst=dst, srcq=srcq: e.tensor_copy(dst, srcq), r=["qT"], w=["qbd"])
        pSA = []
        for q4 in range(4):
            ps, pk = self.pf()
            pSA.append((ps, pk))
            for qq in range(4):
                q = q4 * 4 + qq
                fw.mm(ps[0:32, qq * 128:(qq + 1) * 128], qbd[:, q, :], KAT[:, q, :], True, True, r=["qbd", "KAT"], w=[pk])
        psB, pkB = self.pf()
        for q in range(NS):
            fw.mm(psB[0:32, q * 4:(q + 1) * 4], qbd[:, q, :], KBT[:, q, :], True, True, r=["qbd", "KBT"], w=[pkB])
        for q4, (ps, pk) in enumerate(pSA):
            self.V(lambda e, q4=q4, ps=ps: e.scalar_tensor_tensor(
                ssc[:, q4 * 4:(q4 + 1) * 4, 0:128], ps[0:32, :].rearrange("p (q c) -> p q c", q=4), 0.125,
                smask[:, 0:128].unsqueeze(1).to_broadcast([32, 4, 128]), ALU.mult, ALU.add), r=[pk, "smask"], w=["ssc"])
        self.V(lambda e: e.scalar_tensor_tensor(
            ssc[:, :, 128:132], psB[0:32, 0:64].rearrange("p (q c) -> p q c", q=NS), 0.125,
            smask[:, 128:132].unsqueeze(1).to_broadcast([32, NS, 4]), ALU.mult, ALU.add), r=[pkB, "smask"], w=["ssc"])
        sinkc = sbl("sinkc", [32, 1])
        for g in range(2):
            for jj in range(4):
                p0 = g * 16 + jj * 4
                fw.dma(sinkc[p0:p0 + 4, :], I["attn_sinks"][l, g * 4 + jj:g * 4 + jj + 1].partition_broadcast(4), w=["sinkc"], key="sinkc")
        self.V(lambda e: e.tensor_reduce(sst[:, 0:NS], ssc[:, :, :], AX.X, ALU.max), r=["ssc"], w=["sst"])
        self.V(lambda e: e.tensor_scalar(sst[:, 0:NS], sst[:, 0:NS], sinkc[:, 0:1], None, ALU.max), r=["sst", "sinkc"], w=["sst"])
        self.V(lambda e: e.tensor_tensor(ssc[:, :, :], ssc[:, :, :], bc3(sst[:, 0:NS], 132), ALU.subtract), r=["ssc", "sst"], w=["ssc"])
        fw.act(ssc[:, :, :], ssc[:, :, :], AF.Exp, r=["ssc"], w=["ssc"])
        self.V(lambda e: e.tensor_reduce(sst[:, NS:2 * NS], ssc[:, :, :], AX.X, ALU.add), r=["ssc"], w=["sst2"])
        self.V(lambda e: e.tensor_scalar(sst[:, 2 * NS:3 * NS], sst[:, 0:NS], sinkc[:, 0:1], None, ALU.subtract), r=["sst", "sinkc"], w=["sst3"])
        fw.act(sst[:, 2 * NS:3 * NS], sst[:, 2 * NS:3 * NS], AF.Exp, r=["sst3"], w=["sst3"], scale=-1.0)
        self.V(lambda e: e.tensor_tensor(sst[:, NS:2 * NS], sst[:, NS:2 * NS], sst[:, 2 * NS:3 * NS], ALU.add), r=["sst2", "sst3"], w=["sst2"])
        self.V(lambda e: e.reciprocal(sst[:, NS:2 * NS], sst[:, NS:2 * NS]), r=["sst2"], w=["sst2"])
        self.V(lambda e: e.tensor_tensor(spb[:, :, :], ssc[:, :, :], bc3(sst[:, NS:2 * NS], 132), ALU.mult), r=["ssc", "sst2"], w=["spb"])
        identb32 = identb[0:32, 0:32]
        for q8 in range(2):
            pbk, pk = self.pb()
            for qq in range(8):
                q = q8 * 8 + qq
                fw.tr(pbk[:, qq * 32:(qq + 1) * 32], spb[:, q, 0:128], identb32, r=["spb", "identb"], w=[pk])
            fw.act(spT[:, q8 * 8:(q8 + 1) * 8, :], pbk[:, 0:256].rearrange("p (q c) -> p q c", q=8), AF.Copy, r=[pk], w=["spT"])
        pbk, pk = self.pb()
        for q in range(NS):
            fw.tr(pbk[0:4, q * 32:(q + 1) * 32], spb[:, q, 128:132], identb32, r=["spb", "identb"], w=[pk])
        fw.act(spTB[:, :, :], pbk[0:4, 0:512].rearrange("p (q c) -> p q c", q=NS), AF.Copy, r=[pk], w=["spTB"])
        pO, pok = self.pf()
        for q in range(NS):
            fw.mm(pO[:, q * 32:(q + 1) * 32], VAb[:, q, :], spT[:, q, :], True, False, r=["VAb", "spT"], w=[pok])
            fw.mm(pO[:, q * 32:(q + 1) * 32], VBb[0:4, q, :], spTB[0:4, q, :], False, True, r=["VBb", "spTB"], w=[pok])
        oraw = sbl("oraw", [128, 32, NS], BF)
        fw.act(oraw.rearrange("p c q -> p q c"), pO[:, :].rearrange("p (q c) -> p q c", q=NS), AF.Copy, r=[pok], w=["oraw"])
        for g in range(2):
            for jj in range(4):
                cc, par = g * 2 + jj // 2, jj % 2
                c0 = g * 16 + jj * 4
                srco = oraw[g * 64:(g + 1) * 64, c0:c0 + 4, :].rearrange("p t q -> p (t q)")
                fw.dma(oTs[par * 64:(par + 1) * 64, cc, :], srco, r=["oraw"], w=["oTs"], key="oTs")
        xo_, xok = xo[i % 2], "xo%d" % (i % 2)
        gate_out(MS, hcur, "hT1", oTs, "oTs", mr, mrk, xt, xk, xo_, xok)
        fw.dma(self.xsbuf, xo_[0:MS, :], r=[xok], w=[("xb", NT)], key=xok)

    def pass_ffn(self, l, es2):
        fw, I, O, NT = self.fw, self.I, self.O, self.NT
        sbl = lambda n, s, dt=F32: self.sbl(es2, "f%d_" % l + n, s, dt)
        identb, identf = self.identb, self.identf
        Wc = sbl("Wc", [128, 8, DFF], BF)
        Wu = sbl("Wu", [128, 8, DFF], BF)
        Wd = sbl("Wd", [128, NFC, D], BF)
        self.col_load(self.gcol[:], "gcol", I["norm_ffn_g"][l], 8)
        cw = sbl("cw", [128, 4, NFC])
        for j in range(3):
            self.col_load(cw[:, j, :], "cw", I["ffn_conv_w"][l, j], NFC)
        self.col_load(cw[:, 3, :], "cw", I["ffn_conv_b"][l], NFC)
        m0 = self.aoff
        self.wstage = [sbl("wst%d" % i_, [128, 2048]) for i_ in range(4)]
        wi = I["ffn_w_in"][l]
        gsc = lambda c: self.gcol[:, c:c + 1]
        self.prep_w(8, DFF, lambda c, s0, n: wi[c * 128:(c + 1) * 128, s0:s0 + n],
                    lambda c, s0, n: Wc[:, c, s0:s0 + n], lambda c: "Wc_%d" % c, "col", gsc)
        self.prep_w(8, DFF, lambda c, s0, n: wi[c * 128:(c + 1) * 128, DFF + s0:DFF + s0 + n],
                    lambda c, s0, n: Wu[:, c, s0:s0 + n], lambda c: "Wu_%d" % c, "col", gsc)
        wd = I["ffn_w_down"][l]
        self.prep_w(NFC, D, lambda c, s0, n: wd[c * 128:(c + 1) * 128, s0:s0 + n],
                    lambda c, s0, n: Wd[:, c, s0:s0 + n], lambda c: "Wd_%d" % c, "plain")
        self.release(m0)
        last = (l == 1)
        if last:
            gf = sbl("gf", [128, D])
            self.bcast_load(gf[:], "gf", I["norm_final_g"])
        hTd = [sbl("hT%d" % i_, [128, 8, 128], BF) for i_ in range(2)]
        hT = hTd[0]
        cxf = sbl("cx", [128, NFC * 130])
        cx1 = cxf.rearrange("p (f t) -> p f t", f=NFC)
        cxs = cxf[:, 0:NFC * NS * 6].rearrange("p (f q j) -> p f q j", f=NFC, q=NS)
        acc = [sbl("acc%d" % i_, [128, 4, 128]) for i_ in range(2)]
        aTd = [sbl("aT%d" % i_, [128, NFC, 128], BF) for i_ in range(2)]
        xo = [sbl("xo%d" % i_, [128, D]) for i_ in range(2)]
        ctok = sbl("ctok", [128, DFF])
        cst = ctok
        jk = self.xn

        def finish(M, xt, xk, xo_, xok, dst_final, dst_x, dkey, aT, aTk):
            for grp in range(2):
                px, pxk = self.pf()
                for fc in range(NFC):
                    fw.mm(px[0:M, :], aT[:, fc, 0:M], Wd[:, fc, grp * 512:(grp + 1) * 512], fc == 0, fc == NFC - 1, r=[aTk, "Wd_%d" % fc], w=[pxk])
                self.V(lambda e, grp=grp, px=px: e.tensor_tensor(xo_[0:M, grp * 512:(grp + 1) * 512], xt[0:M, grp * 512:(grp + 1) * 512], px[0:M, :], ALU.add),
                       r=[xk, pxk], w=[xok])
            if not last:
                fw.dma(dst_x, xo_[0:M, :], r=[xok], w=[dkey], key=xok)
                return
            ss, t1 = self.ss, self.t1
            fw.act(jk[0:M, :], xo_[0:M, :], AF.Square, r=[xok], w=["xn", "ss"], accum_out=ss[0:M, :])
            self.V(lambda e: e.tensor_scalar(t1[0:M, :], ss[0:M, :], 1.0 / D, 1e-6, ALU.mult, ALU.add), r=["ss"], w=["t1"])
            self.P(lambda e: e.tensor_tensor(t1[0:M, :], t1[0:M, :], self.mhalf[0:M, 0:1], ALU.pow), r=["t1", "mhalf"], w=["t1"])
            self.V(lambda e: e.scalar_tensor_tensor(xo_[0:M, :], xo_[0:M, :], t1[0:M, 0:1], gf[0:M, :], ALU.mult, ALU.mult),
                   r=[xok, "t1", "gf"], w=[xok])
            fw.dma(dst_final, xo_[0:M, :], r=[xok], key=xok)

        def ffn_core(M, hcur, hk, cview, ckey, sample, aT, aTk, mid=None, groups=None):
            def partA(b0):
                nb = min(4, NFC - b0)
                pc, pck = self.pf()
                for q in range(nb):
                    fc = b0 + q
                    for c in range(8):
                        fw.mm(pc[:, q * M:(q + 1) * M], Wc[:, c, fc * 128:(fc + 1) * 128], hcur(c), c == 0, c == 7, r=[hk, "Wc_%d" % c], w=[pck])
                pu, puk = self.pf()
                for q in range(nb):
                    fc = b0 + q
                    for c in range(8):
                        fw.mm(pu[:, q * M:(q + 1) * M], Wu[:, c, fc * 128:(fc + 1) * 128], hcur(c), c == 0, c == 7, r=[hk, "Wu_%d" % c], w=[puk])
                if sample:
                    fw.act(cview[:, b0:b0 + nb, :, 2:6], pc[:, 0:nb * M].rearrange("p (f t q) -> p f q t", f=nb, t=4), AF.Copy, r=[pck], w=[ckey])
                else:
                    fw.act(cview[:, b0:b0 + nb, 2:130], pc[:, 0:nb * M].rearrange("p (f t) -> p f t", f=nb), AF.Copy, r=[pck], w=[ckey])
                a_ = acc[(b0 // 4) % 2]
                ak = "acc%d" % ((b0 // 4) % 2)
                views = []
                for q in range(nb):
                    fc = b0 + q
                    if sample:
                        c0, c1, c2 = (cview[:, fc, :, s_:s_ + 4] for s_ in range(3))
                        av = a_[:, q, 0:M].rearrange("p (t q) -> p q t", t=4)
                    else:
                        c0, c1, c2 = (cview[:, fc, s_:s_ + 128] for s_ in range(3))
                        av = a_[:, q, :]
                    views.append((fc, av, c0, c1, c2))
                akq = [ak + "_%d" % q for q in range(nb)]
                for q, (fc, av, c0, c1, c2) in enumerate(views):
                    self.P(lambda e, av=av, c0=c0, fc=fc: e.tensor_scalar(av, c0, cw[:, 0, fc:fc + 1], cw[:, 3, fc:fc + 1], ALU.mult, ALU.add),
                           r=[ckey, "cw", ak], w=([akq[q], ak] if q == 0 else [akq[q]]))
                return (b0, nb, pu, puk, a_, ak, akq, views)

            def partB(st):
                b0, nb, pu, puk, a_, ak, akq, views = st
                for q, (fc, av, c0, c1, c2) in enumerate(views):
                    self.V(lambda e, av=av, c1=c1, fc=fc: e.scalar_tensor_tensor(av, c1, cw[:, 1, fc:fc + 1], av, ALU.mult, ALU.add),
                           r=[ckey, "cw", akq[q]], w=[akq[q]])
                for q, (fc, av, c0, c1, c2) in enumerate(views):
                    self.V(lambda e, av=av, c2=c2, fc=fc: e.scalar_tensor_tensor(av, c2, cw[:, 2, fc:fc + 1], av, ALU.mult, ALU.add),
                           r=[ckey, "cw", akq[q]], w=[akq[q]])
                fw.act(a_[:, 0:nb, 0:M], a_[:, 0:nb, 0:M], AF.Gelu, r=akq, w=[ak])
                self.V(lambda e, a_=a_, pu=pu, nb=nb, b0=b0, aT=aT: e.tensor_tensor(aT[:, b0:b0 + nb, 0:M], a_[:, 0:nb, 0:M],
                                                                              pu[:, 0:nb * M].rearrange("p (f t) -> p f t", f=nb), ALU.mult),
                       r=[ak, puk], w=[aTk])

            prev = None
            for b0 in (groups if groups is not None else range(0, NFC, 4)):
                cur = partA(b0)
                if prev is not None:
                    partB(prev)
                prev = cur
                if mid is not None and b0 == 8:
                    mid()
            partB(prev)

        def c_token_major(M, hcur, hk, rows, dsts):
            for g0 in range(0, DFF, 512):
                n = min(512, DFF - g0)
                ps, pk = self.pf()
                for c in range(8):
                    fw.mm(ps[0:M, 0:n], hcur(c), Wc[:, c, g0:g0 + n], c == 0, c == 7, r=[hk, "Wc_%d" % c], w=[pk])
                fw.act(ctok[0:M, g0:g0 + n], ps[0:M, 0:n], AF.Copy, r=[pk], w=["ctok"])
            for (r0, r1), dst in zip(rows, dsts):
                fw.dma(dst, ctok[r0:r1, :], r=["ctok"], key="ctok")

        xt, xk = self.xt[1], "xt1"
        fw.dma(xt[:], (I["xh0"] if NSEG == 1 else self.xh_dram), r=["xh_dram"], w=[xk], key=xk)
        self.norm_hT(xt, xk, 128, hT[:, :, :], "hT0", identb)
        pc, pck = self.pf()
        for fc in range(NFC):
            for c in range(8):
                fw.mm(pc[:, fc * 2:(fc + 1) * 2], Wc[:, c, fc * 128:(fc + 1) * 128], hT[:, c, 126:128], c == 0, c == 7, r=["hT0", "Wc_%d" % c], w=[pck])
        fw.act(cx1[:, :, 0:2], pc[:, 0:2 * NFC].rearrange("p (f t) -> p f t", f=NFC), AF.Copy, r=[pck], w=["cx"])
        def pre(i):
            xt, xk = self.xt[i % 2], "xt%d" % (i % 2)
            fw.dma(xt[:], self.xbuf[i * 128:(i + 1) * 128, :], r=[("xb", i)], w=[xk], key=xk)
            self.norm_hT(xt, xk, 128, hTd[i % 2][:, :, :], "hT%d" % (i % 2), identb)

        def head(i):
            if i > 0:
                self.P(lambda e: e.tensor_copy(acc[0][:, 0, 0:2 * NFC].rearrange("p (f t) -> p f t", f=NFC), cx1[:, :, 128:130]), r=["cx"], w=["acc0"])
                self.P(lambda e: e.tensor_copy(cx1[:, :, 0:2], acc[0][:, 0, 0:2 * NFC].rearrange("p (f t) -> p f t", f=NFC)), r=["acc0"], w=["cx"])
            ffn_core(128, lambda c, i=i: hTd[i % 2][:, c, :], "hT%d" % (i % 2), cx1, "cx", False, aTd[i % 2], "aT%d" % (i % 2), groups=[0])

        pre(0)
        head(0)
        for i in range(NT):
            xt, xk = self.xt[i % 2], "xt%d" % (i % 2)
            hcur = lambda c, i=i: hTd[i % 2][:, c, :]
            hkk = "hT%d" % (i % 2)
            mid = (lambda i=i: pre(i + 1)) if i + 1 < NT else None
            ffn_core(128, hcur, hkk, cx1, "cx", False, aTd[i % 2], "aT%d" % (i % 2), mid, groups=list(range(4, NFC, 4)))
            if i == NT - 1:
                c_token_major(128, hcur, hkk, [(126, 128)], [O["p_conv"][l]])
            if i + 1 < NT:
                head(i + 1)
            xo_, xok = xo[i % 2], "xo%d" % (i % 2)
            finish(128, xt, xk, xo_, xok, O["yp"][i * 128:(i + 1) * 128, :], self.xbuf[i * 128:(i + 1) * 128, :], ("xb", i),
                   aTd[i % 2], "aT%d" % (i % 2))
        if not last and NSEG > 1:
            self.gather_select(xo_[:, :], [xok], D, self.agX_in, self.agX_out, "agX")
            fw.dma(self.xh_dram, xo_[:, :], r=[xok], w=["xh_dram"], key="xhst")

        i = NT
        xt, xk = self.xt[i % 2], "xt%d" % (i % 2)
        fw.dma(xt[0:MS, :], self.xsbuf, r=[("xb", i)], w=[xk], key=xk)
        self.norm_hT(xt, xk, MS, hT[:, :, 0:MS], "hT0", identb)
        hcur = lambda c: hT[:, c, 0:MS]
        fw.dma(cst[0:32, :], I["st_conv"][l], w=["ctok"], key="cst")
        for b0 in range(0, NFC, 4):
            nb = min(4, NFC - b0)
            ps, pk = self.pf()
            for q in range(nb):
                fc = b0 + q
                fw.tr(ps[:, q * 32:(q + 1) * 32], cst[0:32, fc * 128:(fc + 1) * 128], identf[0:32, 0:32], r=["ctok", "identf"], w=[pk])
            fw.act(cxs[:, b0:b0 + nb, :, 0:2], ps[:, 0:nb * 32].rearrange("p (f q j) -> p f q j", f=nb, j=2), AF.Copy, r=[pk], w=["cx"])
        ffn_core(MS, hcur, "hT0", cxs, "cx", True, aTd[0], "aT0")
        sc_ = O["s_conv"][l].rearrange("(q j) f -> j q f", j=2)
        c_token_major(MS, hcur, "hT0", [(32, 48), (48, 64)], [sc_[0], sc_[1]])
        xo_, xok = xo[i % 2], "xo%d" % (i % 2)
        finish(MS, xt, xk, xo_, xok, O["ys"], self.xsbuf, ("xb", NT), aTd[0], "aT0")


NSEG = 1


def _consts_shared():
    c = {}
    c["c_ident"] = np.eye(128, dtype=np.float32)
    inv = (10000.0 ** (-np.arange(0, HD, 2, dtype=np.float32) / HD)).astype(np.float32)
    pos_s = (PAST + np.repeat(np.arange(4), NS)).astype(np.float32)
    ang_s = pos_s[:, None] * inv[None, :]
    c["c_coss"] = np.cos(ang_s).astype(np.float32)
    c["c_sins"] = np.sin(ang_s).astype(np.float32)
    s = np.arange(128)[:, None]
    t = np.arange(128)[None, :]
    incl = (s <= t).astype(np.float32)
    strict = (s < t).astype(np.float32)
    c["c_tri"] = np.concatenate([incl * CDEC, strict * CDEC], 1).astype(np.float32)
    c["c_mask2"] = np.concatenate([incl, strict], 1).astype(np.float32)
    c["c_maskL"] = (s > t).astype(np.float32)
    i_ = np.arange(128)[:, None]
    j_ = np.arange(128)[None, :]
    cur = np.where(j_ <= i_, 0.0, NEG)
    prev = np.where(j_ > i_, 0.0, NEG)
    dead = np.full((128, 128), NEG)
    c["c_amask"] = np.concatenate([cur, prev, prev, cur, cur, dead], 1).astype(np.float32)
    c["_am_first"] = np.concatenate([cur, dead], 1).astype(np.float32)
    c["_am_mid"] = np.concatenate([cur, prev], 1).astype(np.float32)
    tt = (np.arange(32) % 4)[:, None]
    ia = np.arange(128)[None, :]
    ma = np.where(ia <= 124 + tt, 0.0, NEG)
    rb = np.arange(4)[None, :]
    mb = np.where(rb > tt, 0.0, NEG)
    c["c_smask"] = np.concatenate([ma, mb], 1).astype(np.float32)
    last = np.zeros((128, 1), np.float32)
    last[127, 0] = 1.0
    c["c_last"] = last
    c["_inv"] = inv
    return c


def _rope_tab(pos, inv):
    ang = pos.astype(np.float32)[:, None] * inv[None, :]
    return np.cos(ang).astype(np.float32), np.sin(ang).astype(np.float32)


_CACHE = {}
TAPS = False
TAP_OUT = {}


def kernel(**inp):
    inp = {k: np.asarray(v) for k, v in inp.items()}
    xp_all = inp["x_prompt"].astype(np.float32)
    B, SEQ_, _ = xp_all.shape
    TPC = SEQ_ // NSEG
    if TPC not in _CACHE:
        b_ = Builder(TPC, taps=TAPS)
        _CACHE[TPC] = (b_.build(), b_.tapnames)
    nc, tapnames = _CACHE[TPC]
    consts = _consts_shared()
    inv = consts.pop("_inv")
    am_first, am_mid = consts.pop("_am_first"), consts.pop("_am_mid")
    wnames = ["norm_mix_g", "w_in", "rwkv_mu", "rwkv_w0", "rwkv_w2", "rwkv_a0", "rwkv_a2", "rwkv_g2", "rwkv_k_k",
              "rwkv_k_a", "rwkv_ln_g", "rwkv_ln_b", "attn_sinks", "w_br_rwkv", "w_br_attn", "w_out", "norm_ffn_g",
              "ffn_w_in", "ffn_conv_w", "ffn_conv_b", "ffn_w_down", "norm_final_g"]
    shared = {n: np.ascontiguousarray(inp[n], dtype=np.float32) for n in wnames}
    shared["rwkv_r_k"] = np.ascontiguousarray(inp["rwkv_r_k"], dtype=np.float32).reshape(2, RD)
    shared.update(consts)
    in_maps = []
    ncores = 8
    for c in range(ncores):
        b, seg = (c // NSEG) % B, c % NSEG
        sl = slice(c * NS, (c + 1) * NS)
        m = dict(shared)
        t0 = seg * TPC
        m["xp"] = np.ascontiguousarray(xp_all[b, t0:t0 + TPC])
        m["xh0"] = np.ascontiguousarray(xp_all[b, t0 - 128:t0]) if seg > 0 else np.zeros((128, D), np.float32)
        m["c_cosp"], m["c_sinp"] = _rope_tab(t0 + np.arange(TPC), inv)
        m["c_cosh"], m["c_sinh"] = _rope_tab(np.maximum(t0 - 128 + np.arange(128), 0), inv)
        m["c_amask0"] = am_mid if seg > 0 else am_first
        sel = np.zeros((128, 8), np.float32)
        if seg > 0:
            sel[:, c - 1] = 1.0
        m["c_sel"] = sel
        m["xs"] = np.ascontiguousarray(inp["x_sample"][sl].transpose(1, 0, 2).reshape(MS, D))
        m["st_shift"] = np.ascontiguousarray(inp["state_rwkv_shift"][:, sl])
        m["st_wkv"] = np.ascontiguousarray(inp["state_rwkv_wkv"][:, sl]).reshape(2, 128, 4096)
        m["ck"] = np.ascontiguousarray(inp["cache_swa_k"][:, sl]).reshape(2, NS, 128, 128)
        m["cv"] = np.ascontiguousarray(inp["cache_swa_v"][:, sl]).reshape(2, NS, 128, 128)
        m["st_conv"] = np.ascontiguousarray(inp["state_ffn_conv"][:, sl]).reshape(2, 2 * NS, DFF)
        in_maps.append(m)
    res = run_bass_kernel_spmd(nc, in_maps, core_ids=list(range(ncores)))
    R = res.results
    for tn in tapnames:
        TAP_OUT[tn] = [np.asarray(R[c][tn]) for c in range(ncores)]
    f = np.float32
    lastc = [b * NSEG + NSEG - 1 for b in range(B)]
    y_prompt = np.stack([np.concatenate([R[b * NSEG + sg]["yp"] for sg in range(NSEG)], 0) for b in range(B)]).astype(f)
    y_sample = np.concatenate([R[c]["ys"].reshape(4, NS, D).transpose(1, 0, 2) for c in range(ncores)], 0).astype(f)
    p_shift = np.stack([R[c]["p_shift"] for c in lastc], 1).astype(f)
    p_wkv = np.stack([R[c]["p_wkv"] for c in lastc], 1).astype(f)
    p_k = np.stack([R[c]["p_k"] for c in lastc], 1).reshape(2, B, 128, 2, 64).astype(f)
    p_v = np.stack([R[c]["p_v"] for c in lastc], 1).reshape(2, B, 128, 2, 64).astype(f)
    p_conv = np.stack([R[c]["p_conv"] for c in lastc], 1).astype(f)
    s_shift = np.concatenate([R[c]["s_shift"] for c in range(ncores)], 1).astype(f)
    s_wkv = np.concatenate([R[c]["s_wkv"].reshape(2, NS, NH, 64, 64) for c in range(ncores)], 1).astype(f)
    s_k = np.concatenate([R[c]["s_k"].reshape(2, NS, 128, 2, 64) for c in range(ncores)], 1).astype(f)
    s_v = np.concatenate([R[c]["s_v"].reshape(2, NS, 128, 2, 64) for c in range(ncores)], 1).astype(f)
    s_conv = np.concatenate([R[c]["s_conv"].reshape(2, NS, 2, DFF) for c in range(ncores)], 1).astype(f)
    return (y_prompt, y_sample, p_shift, p_wkv, p_k, p_v, p_conv, s_shift, s_wkv, s_k, s_v, s_conv)
```
